# Optimizing a Trainium2 kernel written in Bass

```python
import math
import jax, jax.numpy as jnp
from jax import lax
import numpy as np

D_MODEL = 1024
BATCH = 32
SEQ = 256
DEPTH = 4
DEC_BATCH = 8
DEC_SEQ = 2048
PAST_LEN = 256

GRID_W = 64
N_MIXERS = 3
N_A = (DEPTH + 2) // 3
N_B = (DEPTH + 1) // 3
N_C = DEPTH // 3
EPS = 1e-6
F32 = jnp.float32

H_A = 8
DK_A = D_MODEL // H_A
DV_A = D_MODEL // H_A
CONV_W = 5
CHUNK_A = 64
QK_A = H_A * DK_A
V_A = H_A * DV_A
GDN_IN = 2 * QK_A + 2 * V_A + 4 * H_A

H_B = 8
DK_B = D_MODEL // (2 * H_B)
DV_B = D_MODEL // H_B
CHUNK_B = 64
QK_B = H_B * DK_B
V_B = H_B * DV_B
MLSTM_IN = 2 * QK_B + 2 * V_B + 4 * H_B

H_C = 8
DK_C = D_MODEL // (2 * H_C)
DV_C = D_MODEL // H_C
QK_C = H_C * 2 * DK_C
V_C = H_C * DV_C
DIFF_IN = 2 * QK_C + V_C
Q_BLOCK = 128
ROPE_BASE = 10000.0
ROPE_PAIRS = DK_C // 4

D_FF = -(-(8 * D_MODEL) // (3 * 256)) * 256

kernel_name = 'hybrid_diffusion_gdn_mlstm_diffattn_step'


def rms_norm(x, g):
    xf = x.astype(F32)
    y = xf * lax.rsqrt(jnp.mean(xf * xf, axis=-1, keepdims=True) + EPS)
    return (y * g.astype(F32)).astype(x.dtype)


def l2_norm(x):
    xf = x.astype(F32)
    return xf * lax.rsqrt(jnp.sum(xf * xf, axis=-1, keepdims=True) + EPS)


def heads(x, n, d):
    return x.reshape(x.shape[0], x.shape[1], n, d).transpose(0, 2, 1, 3)


def centred_dwconv(x, w):
    pad = CONV_W // 2
    return lax.conv_general_dilated(x, w[:, None, :].astype(x.dtype), window_strides=(1,), padding=[(pad, pad)], dimension_numbers=('NWC', 'WIO', 'NWC'), feature_group_count=x.shape[-1])


def modulation(cond, w, b):
    m = (jax.nn.silu(cond) @ w + b)[:, None, :]
    return jnp.split(m, 6, axis=-1)


def swiglu(h, w_in, w_out):
    g, u = jnp.split(h @ w_in, 2, axis=-1)
    return (jax.nn.silu(g) * u) @ w_out


def gated_delta_scan(q, k, v, log_a, beta, s0):
    b_, h_, t_, dk = k.shape
    dv = v.shape[-1]
    c_ = CHUNK_A
    n_ = t_ // c_
    q = q.astype(F32).reshape(b_, h_, n_, c_, dk)
    k = k.astype(F32).reshape(b_, h_, n_, c_, dk)
    v = v.astype(F32).reshape(b_, h_, n_, c_, dv)
    beta = beta.astype(F32).reshape(b_, h_, n_, c_)
    g = jnp.cumsum(log_a.astype(F32).reshape(b_, h_, n_, c_), axis=-1)
    causal = jnp.tril(jnp.ones((c_, c_), bool))
    strict = jnp.tril(jnp.ones((c_, c_), bool), -1)
    decay = jnp.exp(jnp.where(causal, g[..., :, None] - g[..., None, :], -jnp.inf))
    kb = k * beta[..., None]
    lower = jnp.where(strict, jnp.einsum('bhntd,bhnsd->bhnts', kb, k) * decay, 0.0)
    eye = jnp.eye(c_, dtype=F32)
    rhs = jnp.concatenate([v * beta[..., None], kb * jnp.exp(g)[..., None]], axis=-1)
    uw = lax.linalg.triangular_solve(eye + lower, rhs, left_side=True, lower=True)
    u, w = uw[..., :dv], uw[..., dv:]
    qk = jnp.where(causal, jnp.einsum('bhntd,bhnsd->bhnts', q, k) * decay, 0.0)
    q_dec = q * jnp.exp(g)[..., None]
    k_dec = k * jnp.exp(g[..., -1:] - g)[..., None]
    g_last = jnp.exp(g[..., -1])
    xs = tuple(jnp.moveaxis(a, 2, 0) for a in (q_dec, k_dec, u, w, qk, g_last))

    def step(s, inp):
        qd, kd, uc, wc, qkc, gl = inp
        v_new = uc - jnp.einsum('bhtd,bhde->bhte', wc, s)
        o = jnp.einsum('bhtd,bhde->bhte', qd, s) + jnp.einsum('bhts,bhse->bhte', qkc, v_new)
        s = s * gl[..., None, None] + jnp.einsum('bhtd,bhte->bhde', kd, v_new)
        return s, o

    s_fin, o = lax.scan(step, s0.astype(F32), xs)
    return jnp.moveaxis(o, 0, 2).reshape(b_, h_, t_, dv), s_fin


def gdn_mixer(h, w_in, conv_w, a_log, dt_bias, norm_g, w_out, s0):
    b_, t_, _ = h.shape
    proj = h @ w_in
    qkv = jax.nn.silu(centred_dwconv(proj[..., :2 * QK_A + V_A], conv_w))
    z = proj[..., 2 * QK_A + V_A:2 * QK_A + 2 * V_A]
    ab = proj[..., 2 * QK_A + 2 * V_A:].astype(F32).reshape(b_, t_, 2, 2, H_A)
    log_a = -jnp.exp(a_log.astype(F32)) * jax.nn.softplus(ab[:, :, 0] + dt_bias.astype(F32))
    beta = jax.nn.sigmoid(ab[:, :, 1])
    q = l2_norm(heads(qkv[..., :QK_A], H_A, DK_A)) * DK_A ** -0.5
    k = l2_norm(heads(qkv[..., QK_A:2 * QK_A], H_A, DK_A))
    v = heads(qkv[..., 2 * QK_A:], H_A, DV_A)
    outs, states = [], []
    for d in range(2):
        seqs = (q, k, v, log_a[:, :, d].transpose(0, 2, 1), beta[:, :, d].transpose(0, 2, 1))
        if d == 1:
            seqs = tuple(jnp.flip(a, axis=2) for a in seqs)
        o, s = gated_delta_scan(*seqs, s0[:, d])
        outs.append(jnp.flip(o, axis=2) if d == 1 else o)
        states.append(s)
    o = (outs[0] + outs[1]).transpose(0, 2, 1, 3)
    o = rms_norm(o, norm_g) * jax.nn.silu(z.reshape(b_, t_, H_A, DV_A).astype(F32))
    return o.reshape(b_, t_, V_A).astype(h.dtype) @ w_out, jnp.stack(states, axis=1)


def mlstm_scan(q, k, v, i_pre, log_f, c0, n0, m0):
    b_, h_, t_, dk = q.shape
    dv = v.shape[-1]
    l_ = CHUNK_B
    n_ = t_ // l_
    q = q.astype(F32).reshape(b_, h_, n_, l_, dk)
    k = k.astype(F32).reshape(b_, h_, n_, l_, dk)
    v = v.astype(F32).reshape(b_, h_, n_, l_, dv)
    i_pre = i_pre.astype(F32).reshape(b_, h_, n_, l_)
    b = jnp.cumsum(log_f.astype(F32).reshape(b_, h_, n_, l_), axis=-1)
    causal = jnp.tril(jnp.ones((l_, l_), bool))
    dmat = jnp.where(causal, b[..., :, None] - b[..., None, :] + i_pre[..., None, :], -jnp.inf)
    dmax = jnp.max(dmat, axis=-1)
    qk = jnp.einsum('bhntd,bhnsd->bhnts', q, k)
    g_end = b[..., -1:] - b + i_pre
    xs = tuple(jnp.moveaxis(a, 2, 0) for a in (q, k, v, b, dmat, dmax, qk, g_end))

    def step(carry, inp):
        cs, ns, m = carry
        qc, kc, vc, bc, dc, dmc, qkc, gec = inp
        m_t = jnp.maximum(bc + m[..., None], dmc)
        inter = jnp.exp(bc + m[..., None] - m_t)
        wts = jnp.exp(dc - m_t[..., None]) * qkc
        num = inter[..., None] * jnp.einsum('bhtd,bhde->bhte', qc, cs) + jnp.einsum('bhts,bhse->bhte', wts, vc)
        den = inter * jnp.einsum('bhtd,bhd->bht', qc, ns) + jnp.sum(wts, axis=-1)
        hout = num / jnp.maximum(jnp.abs(den), jnp.exp(-m_t))[..., None]
        b_last = bc[..., -1]
        m_new = jnp.maximum(b_last + m, jnp.max(gec, axis=-1))
        dec = jnp.exp(b_last + m - m_new)
        wk = jnp.exp(gec - m_new[..., None])
        cs = dec[..., None, None] * cs + jnp.einsum('bhs,bhsd,bhse->bhde', wk, kc, vc)
        ns = dec[..., None] * ns + jnp.einsum('bhs,bhsd->bhd', wk, kc)
        return (cs, ns, m_new), hout

    (c_fin, n_fin, m_fin), hs = lax.scan(step, (c0.astype(F32), n0.astype(F32), m0.astype(F32)), xs)
    return jnp.moveaxis(hs, 0, 2).reshape(b_, h_, t_, dv), c_fin, n_fin, m_fin


def mlstm_mixer(h, w_in, gate_b, norm_g, w_out, c0, n0, m0):
    b_, t_, _ = h.shape
    proj = h @ w_in
    q = heads(proj[..., :QK_B], H_B, DK_B) * DK_B ** -0.5
    k = heads(proj[..., QK_B:2 * QK_B], H_B, DK_B)
    v = heads(proj[..., 2 * QK_B:2 * QK_B + V_B], H_B, DV_B)
    o_gate = jax.nn.sigmoid(proj[..., 2 * QK_B + V_B:2 * QK_B + 2 * V_B].astype(F32)).reshape(b_, t_, H_B, DV_B)
    gates = proj[..., 2 * QK_B + 2 * V_B:].astype(F32).reshape(b_, t_, 2, 2, H_B) + gate_b.astype(F32)
    i_pre = gates[:, :, 0]
    log_f = jax.nn.log_sigmoid(gates[:, :, 1])
    outs, cs, ns, ms = [], [], [], []
    for d in range(2):
        seqs = (q, k, v, i_pre[:, :, d].transpose(0, 2, 1), log_f[:, :, d].transpose(0, 2, 1))
        if d == 1:
            seqs = tuple(jnp.flip(a, axis=2) for a in seqs)
        hd, cd, nd, md = mlstm_scan(*seqs, c0[:, d], n0[:, d], m0[:, d])
        outs.append(jnp.flip(hd, axis=2) if d == 1 else hd)
        cs.append(cd)
        ns.append(nd)
        ms.append(md)
    hsum = (outs[0] + outs[1]).transpose(0, 2, 1, 3)
    y = rms_norm(hsum, norm_g) * o_gate
    return y.reshape(b_, t_, V_B).astype(h.dtype) @ w_out, jnp.stack(cs, axis=1), jnp.stack(ns, axis=1), jnp.stack(ms, axis=1)


def axial_rope_tables(t_):
    rows = t_ // GRID_W
    row = jnp.broadcast_to(jnp.arange(rows)[:, None], (rows, GRID_W)).reshape(-1)
    col = jnp.broadcast_to(jnp.arange(GRID_W)[None, :], (rows, GRID_W)).reshape(-1)
    inv = ROPE_BASE ** (-jnp.arange(ROPE_PAIRS, dtype=F32) / ROPE_PAIRS)
    ang = jnp.stack([row, col], axis=-1).astype(F32)[:, :, None] * inv
    return jnp.cos(ang), jnp.sin(ang)


def apply_axial_rope(x, cos, sin):
    xs = x.astype(F32).reshape(*x.shape[:-1], 2, 2, ROPE_PAIRS)
    a, b = xs[..., 0, :], xs[..., 1, :]
    return jnp.stack([a * cos - b * sin, a * sin + b * cos], axis=-2).reshape(x.shape)


def diff_block(qb, k, v, lam):
    s = jnp.einsum('bhmqd,bhmkd->bhmqk', qb, k) * DK_C ** -0.5
    p = jax.nn.softmax(s, axis=-1)
    return jnp.einsum('bhqk,bhkd->bhqd', p[:, :, 0] - lam * p[:, :, 1], v)


def diff_attn_mixer(h, layer_idx, w_in, qn_g, kn_g, lam_p, subln_g, w_out, rope=None, ctx_k=None, ctx_v=None):
    b_, t_, _ = h.shape
    proj = h @ w_in
    q = rms_norm(proj[..., :QK_C].reshape(b_, t_, H_C, 2, DK_C), qn_g).transpose(0, 2, 3, 1, 4)
    k = rms_norm(proj[..., QK_C:2 * QK_C].reshape(b_, t_, H_C, 2, DK_C), kn_g).transpose(0, 2, 3, 1, 4)
    v = heads(proj[..., 2 * QK_C:], H_C, DV_C)
    if rope is None:
        q_use = q.astype(F32)
        k_all = k.astype(F32)
        v_all = v.astype(F32)
    else:
        q_use = apply_axial_rope(q, *rope)
        k_all = jnp.concatenate([apply_axial_rope(k, *rope), ctx_k.astype(F32)], axis=3)
        v_all = jnp.concatenate([v.astype(F32), ctx_v.astype(F32)], axis=2)
    lam_init = 0.8 - 0.6 * math.exp(-0.3 * layer_idx)
    lq1, lk1, lq2, lk2 = lam_p.astype(F32)
    lam = jnp.exp(jnp.sum(lq1 * lk1)) - jnp.exp(jnp.sum(lq2 * lk2)) + lam_init
    nb = t_ // Q_BLOCK
    qb = jnp.moveaxis(q_use.reshape(b_, H_C, 2, nb, Q_BLOCK, DK_C), 3, 0)
    ob = lax.map(lambda blk: diff_block(blk, k_all, v_all, lam), qb)
    o = jnp.moveaxis(ob, 0, 2).reshape(b_, H_C, t_, DV_C).transpose(0, 2, 1, 3)
    o = rms_norm(o, subln_g) * (1.0 - lam_init)
    return o.reshape(b_, t_, V_C).astype(h.dtype) @ w_out, k, v


def setup_inputs(seed: int = 0) -> dict:
    key = jax.random.key(seed)
    kit = iter(jax.random.split(key, 40))

    def nrm(shape, scale=1.0):
        return scale * jax.random.normal(next(kit), shape, F32)

    dt = jnp.exp(jax.random.uniform(next(kit), (N_A, 2, H_A), F32, math.log(1e-3), math.log(1e-1)))
    i_b = -2.0 + nrm((N_B, 2, H_B), 0.1)
    f_b = jnp.linspace(3.0, 6.0, H_B, dtype=F32) + nrm((N_B, 2, H_B), 0.1)
    return {
        'x_prompt': nrm((BATCH, SEQ, D_MODEL)),
        'x_sample': nrm((DEC_BATCH, DEC_SEQ, D_MODEL)),
        'c': nrm((DEC_BATCH, D_MODEL)),
        'state_delta': nrm((DEC_BATCH, N_A, 2, H_A, DK_A, DV_A), 0.1),
        'state_mlstm_C': nrm((DEC_BATCH, N_B, 2, H_B, DK_B, DV_B), 0.5),
        'state_mlstm_n': nrm((DEC_BATCH, N_B, 2, H_B, DK_B), 0.5),
        'state_mlstm_m': nrm((DEC_BATCH, N_B, 2, H_B)),
        'cache_diff_k': nrm((DEC_BATCH, N_C, H_C, 2, PAST_LEN, DK_C)),
        'cache_diff_v': nrm((DEC_BATCH, N_C, H_C, PAST_LEN, DV_C)),
        'c_ctx': nrm((D_MODEL,)),
        'ada_w': nrm((DEPTH, D_MODEL, 6 * D_MODEL), 0.5 * D_MODEL ** -0.5),
        'ada_b': nrm((DEPTH, 6 * D_MODEL), 0.01),
        'norm_g': 1.0 + nrm((DEPTH, 2, D_MODEL), 0.05),
        'gdn_w_in': nrm((N_A, D_MODEL, GDN_IN), D_MODEL ** -0.5),
        'gdn_conv_w': nrm((N_A, CONV_W, 2 * QK_A + V_A), CONV_W ** -0.5),
        'gdn_a_log': jnp.log(jax.random.uniform(next(kit), (N_A, 2, H_A), F32, 1.0, 16.0)),
        'gdn_dt_bias': dt + jnp.log(-jnp.expm1(-dt)),
        'gdn_norm_g': 1.0 + nrm((N_A, DV_A), 0.05),
        'gdn_w_out': nrm((N_A, V_A, D_MODEL), V_A ** -0.5),
        'mlstm_w_in': nrm((N_B, D_MODEL, MLSTM_IN), D_MODEL ** -0.5),
        'mlstm_gate_b': jnp.stack([i_b, f_b], axis=1),
        'mlstm_norm_g': 1.0 + nrm((N_B, DV_B), 0.05),
        'mlstm_w_out': nrm((N_B, V_B, D_MODEL), V_B ** -0.5),
        'diff_w_in': nrm((N_C, D_MODEL, DIFF_IN), D_MODEL ** -0.5),
        'diff_q_norm_g': 1.0 + nrm((N_C, DK_C), 0.05),
        'diff_k_norm_g': 1.0 + nrm((N_C, DK_C), 0.05),
        'diff_lambda': nrm((N_C, 4, DK_C), 0.1),
        'diff_subln_g': 1.0 + nrm((N_C, DV_C), 0.05),
        'diff_w_out': nrm((N_C, V_C, D_MODEL), V_C ** -0.5),
        'ffn_w_in': nrm((DEPTH, D_MODEL, 2 * D_FF), D_MODEL ** -0.5),
        'ffn_w_out': nrm((DEPTH, D_FF, D_MODEL), D_FF ** -0.5),
    }


def reference(x_prompt, x_sample, c, state_delta, state_mlstm_C, state_mlstm_n, state_mlstm_m, cache_diff_k, cache_diff_v, c_ctx, ada_w, ada_b, norm_g, gdn_w_in, gdn_conv_w, gdn_a_log, gdn_dt_bias, gdn_norm_g, gdn_w_out, mlstm_w_in, mlstm_gate_b, mlstm_norm_g, mlstm_w_out, diff_w_in, diff_q_norm_g, diff_k_norm_g, diff_lambda, diff_subln_g, diff_w_out, ffn_w_in, ffn_w_out):
    xp, xs = x_prompt, x_sample
    bp = xp.shape[0]
    rope = axial_rope_tables(xs.shape[1])
    new_delta, new_c, new_n, new_m, new_k, new_v = [], [], [], [], [], []
    for i in range(DEPTH):
        kind, j = i % N_MIXERS, i // N_MIXERS
        mp = modulation(c_ctx[None, :], ada_w[i], ada_b[i])
        ms = modulation(c, ada_w[i], ada_b[i])
        hp = rms_norm(xp, norm_g[i, 0]) * (1.0 + mp[1]) + mp[0]
        hs = rms_norm(xs, norm_g[i, 0]) * (1.0 + ms[1]) + ms[0]
        if kind == 0:
            wts = (gdn_w_in[j], gdn_conv_w[j], gdn_a_log[j], gdn_dt_bias[j], gdn_norm_g[j], gdn_w_out[j])
            op, sp = gdn_mixer(hp, *wts, jnp.zeros((bp, 2, H_A, DK_A, DV_A), F32))
            os_, _ = gdn_mixer(hs, *wts, state_delta[:, j])
            new_delta.append(sp.astype(xp.dtype))
        elif kind == 1:
            wts = (mlstm_w_in[j], mlstm_gate_b[j], mlstm_norm_g[j], mlstm_w_out[j])
            op, cp, nvec, mval = mlstm_mixer(hp, *wts, jnp.zeros((bp, 2, H_B, DK_B, DV_B), F32), jnp.zeros((bp, 2, H_B, DK_B), F32), jnp.zeros((bp, 2, H_B), F32))
            os_, _, _, _ = mlstm_mixer(hs, *wts, state_mlstm_C[:, j], state_mlstm_n[:, j], state_mlstm_m[:, j])
            new_c.append(cp.astype(xp.dtype))
            new_n.append(nvec.astype(xp.dtype))
            new_m.append(mval.astype(xp.dtype))
        else:
            wts = (diff_w_in[j], diff_q_norm_g[j], diff_k_norm_g[j], diff_lambda[j], diff_subln_g[j], diff_w_out[j])
            op, kp, vp = diff_attn_mixer(hp, i, *wts)
            os_, _, _ = diff_attn_mixer(hs, i, *wts, rope=rope, ctx_k=cache_diff_k[:, j], ctx_v=cache_diff_v[:, j])
            new_k.append(kp.astype(xp.dtype))
            new_v.append(vp.astype(xp.dtype))
        xp = xp + mp[2] * op
        xs = xs + ms[2] * os_
        hp = rms_norm(xp, norm_g[i, 1]) * (1.0 + mp[4]) + mp[3]
        hs = rms_norm(xs, norm_g[i, 1]) * (1.0 + ms[4]) + ms[3]
        xp = xp + mp[5] * swiglu(hp, ffn_w_in[i], ffn_w_out[i])
        xs = xs + ms[5] * swiglu(hs, ffn_w_in[i], ffn_w_out[i])
    return (xp, xs, jnp.stack(new_delta, axis=1), jnp.stack(new_c, axis=1), jnp.stack(new_n, axis=1), jnp.stack(new_m, axis=1), jnp.stack(new_k, axis=1), jnp.stack(new_v, axis=1))
```

```python
import numpy as np
from contextlib import ExitStack
import concourse.bass as bass
import concourse.mybir as mybir
from concourse.bass_utils import run_bass_kernel_spmd

F32 = mybir.dt.float32
AF = mybir.ActivationFunctionType
ALU = mybir.AluOpType
AX = mybir.AxisListType

ENGS = ('pe', 'act', 'dve', 'pool', 'sp')
NDS = 40

D = 1024
NCORE = 8
NP = 4
TP = 256
TS = 2048
TT = NP * TP + TS
DFF = 2816
EPS = 1e-6


class Tk:
    __slots__ = ('ap', 'lw', 'rd', 'rp', 'name')

    def __init__(self, ap, name=''):
        self.ap = ap
        self.lw = {}
        self.rd = {}
        self.rp = {}
        self.name = name

    def __getitem__(self, idx):
        return self.ap[idx]


class Sched:
    def __init__(self, nc, es):
        self.nc = nc
        self.es = es
        self.q = {e: [] for e in ENGS}
        self.sem = {e: es.enter_context(nc.semaphore("s_" + e)) for e in ENGS}
        self.cnt = {e: 0 for e in ENGS}
        self.seen = {e: {} for e in ENGS}
        self.dsem = [es.enter_context(nc.semaphore("d%d" % i)) for i in range(NDS)]
        self.dcnt = [0] * NDS
        self.dnext = 0
        self.nins = 0
        self.uid = 0

    def sb(self, shape, dt=F32, name=None):
        self.uid += 1
        name = name or "t%d" % self.uid
        t = self.es.enter_context(self.nc.sbuf_tensor(name, list(shape), dt))
        return Tk(t, name)

    def ps(self, shape, dt=F32, name=None):
        self.uid += 1
        name = name or "p%d" % self.uid
        t = self.es.enter_context(self.nc.psum_tensor(name, list(shape), dt))
        return Tk(t, name)

    def _wait(self, eng, d):
        k = d[0]
        if eng == 'pe' and k == ('e', 'pe'):
            return
        seen = self.seen[eng]
        if seen.get(k, 0) >= d[2]:
            return
        seen[k] = d[2]
        self.q[eng].append(lambda E, d=d: E.wait_ge(d[1], d[2]))
        self.nins += 1

    def _deps(self, eng, reads, writes, pw):
        deps = {}

        def add(d):
            k = d[0]
            if k not in deps or deps[k][2] < d[2]:
                deps[k] = d
        for t in reads:
            for d in t.lw.values():
                add(d)
        for t in writes:
            for d in t.lw.values():
                add(d)
            for d in t.rd.values():
                add(d)
        for t in pw:
            for d in t.rd.values():
                add(d)
            for d in t.rp.values():
                add(d)
        for d in deps.values():
            self._wait(eng, d)

    def _mark(self, me, reads, writes, pw):
        for t in reads:
            t.rd[me[0]] = me
        for t in writes:
            t.lw = {me[0]: me}
            t.rp = t.rd
            t.rd = {}
        for t in pw:
            t.lw[me[0]] = me

    def op(self, eng, fn, reads=(), writes=(), pw=()):
        self._deps(eng, reads, writes, pw)
        self.cnt[eng] += 1
        sem = self.sem[eng]
        me = (('e', eng), sem, self.cnt[eng])
        self.q[eng].append(lambda E: fn(E).then_inc(sem, 1))
        self.nins += 1
        self._mark(me, reads, writes, pw)

    def dma(self, eng, out_ap, in_ap, reads=(), writes=(), pw=()):
        slot = self.dnext
        self.dnext = (slot + 1) % NDS
        ds = self.dsem[slot]
        self._deps(eng, reads, writes, pw)
        if self.dcnt[slot] > 0:
            self._wait(eng, (('d', slot), ds, 16 * self.dcnt[slot]))
        self.dcnt[slot] += 1
        me = (('d', slot), ds, 16 * self.dcnt[slot])
        self.q[eng].append(lambda E: E.dma_start(out=out_ap, in_=in_ap).then_inc(ds, 16))
        self.nins += 1
        self._mark(me, reads, writes, pw)

    def barrier(self):
        for e in ENGS:
            for o in ENGS:
                if o != e and self.cnt[o] > 0:
                    self._wait(e, (('e', o), self.sem[o], self.cnt[o]))
            for i in range(NDS):
                if self.dcnt[i] > 0:
                    self._wait(e, (('d', i), self.dsem[i], 16 * self.dcnt[i]))

    def emit(self):
        self.barrier()
        q = self.q
        with self.nc.Block() as block:
            @block.tensor
            def _(E):
                for f in q['pe']:
                    f(E)

            @block.scalar
            def _(E):
                for f in q['act']:
                    f(E)

            @block.vector
            def _(E):
                for f in q['dve']:
                    f(E)

            @block.gpsimd
            def _(E):
                for f in q['pool']:
                    f(E)

            @block.sync
            def _(E):
                for f in q['sp']:
                    f(E)


def wr(t, first):
    return {'w': [t]} if first else {'pw': [t]}


class Rot:
    def __init__(self, tiles):
        self.t = tiles
        self.i = 0

    def get(self):
        t = self.t[self.i % len(self.t)]
        self.i += 1
        return t


ARENA_COLS = 40960


class Prog:
    def __init__(self, opts):
        self.opts = opts
        self.nc = bass.Bass("TRN2", target_bir_lowering=False)
        self.es = ExitStack()
        self.din = {}
        self.dout = {}

    def inp(self, name, shape):
        t = self.nc.dram_tensor(name, list(shape), F32, kind="ExternalInput").ap()
        self.din[name] = t
        return t

    def outp(self, name, shape):
        t = self.nc.dram_tensor(name, list(shape), F32, kind="ExternalOutput").ap()
        self.dout[name] = t
        return t

    def scratch(self, name, shape):
        return self.nc.dram_tensor(name, list(shape), F32, kind="Internal").ap()

    def areset(self):
        self.S.barrier()
        self.apos = 0

    def take(self, shape, n=None):
        cols = int(np.prod(shape[1:]))
        out = []
        for _ in range(n or 1):
            assert self.apos + cols <= ARENA_COLS, ("arena overflow", self.apos, cols)
            ap = self.arena[0:shape[0], self.apos:self.apos + cols]
            if len(shape) == 3:
                ap = ap.rearrange("p (a b) -> p a b", a=shape[1])
            elif len(shape) == 4:
                ap = ap.rearrange("p (a b c) -> p a b c", a=shape[1], b=shape[2])
            self.apos += cols
            out.append(Tk(ap))
        return out[0] if n is None else Rot(out)

    def pnext(self):
        p = self.psum[self.pi % 8]
        self.pi += 1
        return p

    def pns(self):
        return self.pnext()

    def mm(self, out, lhsT, rhs, start, stop, r=(), w=(), pw=()):
        self.S.op('pe', lambda E: E.matmul(out, lhsT=lhsT, rhs=rhs, start=start, stop=stop), reads=r, writes=w, pw=pw)

    def tr(self, out, in_, r=(), w=(), pw=()):
        ident = self.ident
        n = in_.shape[0]
        self.S.op('pe', lambda E: E.transpose(out, in_, ident[0:n, 0:n]), reads=list(r) + [ident], writes=w, pw=pw)

    def act(self, out, in_, func, r=(), w=(), pw=(), bias=None, scale=None, accum=None):
        kw = {}
        if bias is not None:
            kw['bias'] = bias
        if scale is not None:
            kw['scale'] = scale
        if accum is not None:
            kw['accum_out'] = accum
        self.S.op('act', lambda E: E.activation(out=out, in_=in_, func=func, **kw), reads=r, writes=w, pw=pw)

    def tt(self, eng, out, a, b, op, r=(), w=(), pw=()):
        self.S.op(eng, lambda E: E.tensor_tensor(out=out, in0=a, in1=b, op=op), reads=r, writes=w, pw=pw)

    def ts(self, eng, out, a, s1, s2, op0, op1=None, r=(), w=(), pw=()):
        if op1 is None:
            self.S.op(eng, lambda E: E.tensor_scalar(out=out, in0=a, scalar1=s1, scalar2=None, op0=op0), reads=r, writes=w, pw=pw)
        else:
            self.S.op(eng, lambda E: E.tensor_scalar(out=out, in0=a, scalar1=s1, scalar2=s2, op0=op0, op1=op1), reads=r, writes=w, pw=pw)

    def stt(self, eng, out, a, s, b, op0, op1, r=(), w=(), pw=()):
        self.S.op(eng, lambda E: E.scalar_tensor_tensor(out=out, in0=a, scalar=s, in1=b, op0=op0, op1=op1), reads=r, writes=w, pw=pw)

    def cp(self, eng, out, in_, r=(), w=(), pw=()):
        if eng == 'act':
            self.S.op('act', lambda E: E.copy(out=out, in_=in_), reads=r, writes=w, pw=pw)
        else:
            self.S.op(eng, lambda E: E.tensor_copy(out=out, in_=in_), reads=r, writes=w, pw=pw)

    def recip(self, out, in_, r=(), w=(), pw=()):
        self.S.op('dve', lambda E: E.reciprocal(out=out, in_=in_), reads=r, writes=w, pw=pw)

    def memset(self, eng, ap, val, w=(), pw=()):
        self.S.op(eng, lambda E: E.memset(ap, val), writes=w, pw=pw)

    def ld(self, out, in_, r=(), w=(), pw=(), eng='sp'):
        self.S.dma(eng, out, in_, reads=r, writes=w, pw=pw)

    def build(self):
        nc = self.nc
        o = self.opts
        with self.es:
            S = self.S = Sched(nc, self.es)
            self.arena = self.es.enter_context(nc.sbuf_tensor("arena", [128, ARENA_COLS], F32))
            self.psum = [S.ps([128, 512], name="ps%d" % i) for i in range(8)]
            self.pi = 0
            self.apos = 0
            self.psmall = [Tk(self.psum[i // 2].ap[:, (i % 2) * 256:(i % 2) * 256 + 256]) for i in range(16)]
            self.psi = 0
            self.x_tok = self.inp("x_tok", [TT, D])
            self.y_tok = self.outp("y_tok", [TT, D])
            self.xT = self.scratch("xT", [8, 128, TT])
            self.xT_v = self.xT.rearrange("c p t -> p c t")
            self.xT_tk = [Tk(None, "xT%d" % i) for i in range(TT // 128)]
            self.Xtok = Tk(None)
            self.Ytok = Tk(None)
            self.Win = Tk(None)
            consts = self.inp("consts", [128, 8 * 128])
            condT = self.inp("condT", [128, 8, 2])
            self.ada_w = self.inp("ada_w", [4, D, 6 * D])
            ada_bT = self.inp("ada_bT", [128, 4, 48])
            normgT = self.inp("normgT", [128, 4, 2, 8])
            self.ffn_w_in = self.inp("ffn_w_in", [4, D, 2 * DFF])
            self.ffn_w_out = self.inp("ffn_w_out", [4, DFF, D])
            self.cst = S.sb([128, 8, 128], name="cst")
            self.ld(self.cst[:], consts.rearrange("p (a b) -> p a b", a=8), r=[self.Win], w=[self.cst])
            self.ones = self.cst.ap[:, 1, :]
            self.sc = S.sb([128, 8, 2], name="sc")
            self.ld(self.sc[:], condT, r=[self.Win], w=[self.sc])
            self.act(self.sc[:], self.sc[:], AF.Silu, r=[self.sc], w=[self.sc])
            self.adab = S.sb([128, 4, 48], name="adab")
            self.ld(self.adab[:], ada_bT, r=[self.Win], w=[self.adab])
            self.normg = S.sb([128, 4, 2, 8], name="normg")
            self.ld(self.normg[:], normgT, r=[self.Win], w=[self.normg])
            self.mod = S.sb([128, 48, 2], name="mod")
            self.modA = S.sb([128, 2, 8, 2], name="modA")
            self.epsb = S.sb([128, 1], name="epsb")
            self.memset('pool', self.epsb[:], EPS, w=[self.epsb])

            self.stage_in()
            mixers = o.get('mixers', (0, 1, 2))
            self.setup_mixers(mixers)
            for i in range(o.get('depth', 4)):
                self.stage_mod(i)
                if i % 3 == 0 and 0 in mixers:
                    self.gdn(i)
                if i % 3 == 1 and 1 in mixers:
                    self.mlstm(i)
                if i % 3 == 2 and 2 in mixers:
                    self.diffattn(i)
                if o.get('ffn', True):
                    self.stage_ffn(i)
            self.stage_out()
            S.emit()
        return nc

    def setup_mixers(self, mixers):
        S = self.S
        if 1 in mixers:
            self.ml_w_in = self.inp("mlstm_w_in", [D, 3104])
            self.ml_w_out = self.inp("mlstm_w_out", [D, D])
            ml_gb = self.inp("mlstm_gate_b", [1, 32])
            ml_ng = self.inp("mlstm_norm_g", [1, 128])
            self.st_C = self.inp("st_C", [2, 8, 64, 128])
            self.st_n = self.inp("st_n", [2, 8, 64, 1])
            self.st_m = self.inp("st_m", [1, 16])
            self.newC = self.outp("newC", [NP, 2, 8, 64, 128])
            self.newn = self.outp("newn", [NP, 2, 8, 64, 1])
            self.newm = self.outp("newm", [NP, 2, 8, 1])
            self.ml_gb = S.sb([128, 32], name="ml_gb")
            self.ld(self.ml_gb[:], ml_gb.partition_broadcast(128), r=[self.Win], w=[self.ml_gb])
            self.ml_ng = S.sb([128, 128], name="ml_ng")
            self.ld(self.ml_ng[:], ml_ng.partition_broadcast(128), r=[self.Win], w=[self.ml_ng])
        self.Oout = Tk(None)
        self.qkT = self.scratch("qkT", [24, 128, TT])
        self.qkT_v = self.qkT.rearrange("c p t -> p c t")
        self.qkT_h = self.qkT[0:8].rearrange("c (two p) t -> p (c two) t", two=2)
        self.ktok = self.scratch("ktok", [TT, 2048])
        self.vtok = self.scratch("vtok", [TT, 1024])
        self.otok = self.scratch("otok", [TT, 1024])
        self.gtok = self.scratch("gtok", [TT, 32])
        self.hdir = [self.scratch("hdir%d" % d, [TT, 1024]) for d in range(2)]
        self.proj_tk = [Tk(None) for _ in range(TT // 128)]
        self.prep_tk = [Tk(None) for _ in range(TT // 128)]
        self.hdir_tk = [[Tk(None) for _ in range(TT // 64)] for d in range(2)]
        if 2 in mixers:
            self.df_w_in = self.inp("diff_w_in", [D, 3072])
            self.df_w_out = self.inp("diff_w_out", [D, D])
            df_g = self.inp("diff_qkg", [1, 128])
            df_lam = self.inp("diff_lambda", [1, 256])
            df_sg = self.inp("diff_subln_g", [1, 128])
            self.rope_cs = self.inp("rope_cs", [TS, 64])
            self.ctx_k = self.inp("ctx_k", [8, 2, 256, 64])
            self.ctx_v = self.inp("ctx_v", [8, 256, 128])
            self.newk = self.outp("newk", [NP, 8, 2, TP, 64])
            self.newv = self.outp("newv", [NP, 8, TP, 128])
            self.df_g = S.sb([128, 2, 64], name="df_g")
            self.ld(self.df_g[:], df_g.rearrange("o (a b) -> o a b", a=2).partition_broadcast(128), r=[self.Win], w=[self.df_g])
            self.df_sg = S.sb([128, 128], name="df_sg")
            self.ld(self.df_sg[:], df_sg.partition_broadcast(128), r=[self.Win], w=[self.df_sg])
            lam_init = 0.8 - 0.6 * float(np.exp(-0.3 * 2))
            self.ts('dve', self.df_sg[:], self.df_sg[:], 1.0 - lam_init, None, ALU.mult, r=[self.df_sg], w=[self.df_sg])
            lm = S.sb([128, 4, 64], name="df_lm")
            self.ld(lm[:], df_lam.rearrange("o (a b) -> o a b", a=4).partition_broadcast(128), r=[self.Win], w=[lm])
            l2 = S.sb([128, 2, 64], name="df_l2")
            self.tt('dve', l2[:, 0, :], lm[:, 0, :], lm[:, 1, :], ALU.mult, r=[lm], w=[l2])
            self.tt('dve', l2[:, 1, :], lm[:, 2, :], lm[:, 3, :], ALU.mult, r=[lm], pw=[l2])
            self.nlam = S.sb([128, 4], name="nlam")
            nl = self.nlam
            S.op('dve', lambda E: E.tensor_reduce(out=nl[:, 0:2], in_=l2[:], axis=AX.X, op=ALU.add), reads=[l2], writes=[nl])
            self.act(nl[:, 0:2], nl[:, 0:2], AF.Exp, r=[nl], w=[nl])
            self.tt('dve', nl[:, 2:3], nl[:, 1:2], nl[:, 0:1], ALU.subtract, r=[nl], pw=[nl])
            self.ts('dve', nl[:, 3:4], nl[:, 2:3], -lam_init, None, ALU.add, r=[nl], pw=[nl])
            self.ctxkT = self.scratch("ctxkT", [8, 128, 256])
            self.ctx_tk = Tk(None)
        if 0 in mixers:
            self.gd_w_in = self.inp("gdn_w_in", [2, D, 4128])
            self.gd_w_out = self.inp("gdn_w_out", [2, D, D])
            gd_cw = self.inp("gdn_convT", [128, 2, 24, 5])
            gd_al = self.inp("gdn_a_log", [2, 1, 16])
            gd_dt = self.inp("gdn_dt_bias", [2, 1, 16])
            gd_ng = self.inp("gdn_norm_g", [2, 1, 128])
            self.st_S = self.inp("st_S", [2, 2, 8, 128, 128])
            self.newS = self.outp("newS", [NP, 2, 2, 8, 128, 128])
            self.gd_cw = S.sb([128, 2, 24, 5], name="gd_cw")
            self.ld(self.gd_cw[:], gd_cw, r=[self.Win], w=[self.gd_cw])
            self.gd_nea = S.sb([128, 2, 16], name="gd_nea")
            self.gd_dt = S.sb([128, 2, 16], name="gd_dt")
            self.gd_ng = S.sb([128, 2, 128], name="gd_ng")
            for j in range(2):
                self.ld(self.gd_nea[:, j, :], gd_al[j].partition_broadcast(128), r=[self.Win], **wr(self.gd_nea, j == 0))
                self.ld(self.gd_dt[:, j, :], gd_dt[j].partition_broadcast(128), r=[self.Win], **wr(self.gd_dt, j == 0))
                self.ld(self.gd_ng[:, j, :], gd_ng[j].partition_broadcast(128), r=[self.Win], **wr(self.gd_ng, j == 0))
            self.act(self.gd_nea[:], self.gd_nea[:], AF.Exp, r=[self.gd_nea], w=[self.gd_nea])
            self.ts('dve', self.gd_nea[:], self.gd_nea[:], -1.0, None, ALU.mult, r=[self.gd_nea], w=[self.gd_nea])

    def proj_stage(self, i, w_in, ncol, fm, tm, col_lo=0):
        self.areset()
        NT = 512
        xbs = self.take([128, 8, NT], 1)
        hbs = self.take([128, 8, NT], 1)
        self.rstd = self.take([128, NT], 2)
        W = self.take([128, 8, ncol])
        wv_ = w_in.rearrange("(kc p) n -> p kc n", p=128)
        self.ld(W[:, 0:4, :], wv_[:, 0:4, col_lo:col_lo + ncol], r=[self.Win], w=[W])
        self.ld(W[:, 4:8, :], wv_[:, 4:8, col_lo:col_lo + ncol], r=[self.Win], pw=[W], eng='pool')
        fm = [(a - col_lo, b, c_, d_, e_) for (a, b, c_, d_, e_) in fm]
        tm = [(a - col_lo, b, c_) for (a, b, c_) in tm]
        ofm = self.take([128, NT], 3)
        otm = self.take([128, 512], 3)
        for blk in range(TT // NT):
            c = 0 if blk < (NP * TP) // NT else 1
            tks = self.xT_tk[blk * 4:(blk + 1) * 4]
            ptk = self.proj_tk[blk * 4:(blk + 1) * 4]
            xb = xbs.get()
            self.ld(xb[:], self.xT_v[:, :, blk * NT:(blk + 1) * NT], r=tks, w=[xb], eng='pool')
            hb = hbs.get()
            self.norm_mod(xb, hb, 0, c, NT)
            n = 0
            for (col0, nch, dstv, ch0, scale) in fm:
                for oc in range(nch):
                    p = self.pnext()
                    for kc in range(8):
                        self.mm(p[:, :], W[:, kc, col0 + oc * 128: col0 + (oc + 1) * 128], hb[:, kc, :], kc == 0, kc == 7, r=[W, hb], **wr(p, kc == 0))
                    ot = ofm.get()
                    if n % 2 == 0:
                        self.act(ot[:], p[:, :], AF.Copy, r=[p], w=[ot], scale=scale)
                    else:
                        self.ts('dve', ot[:], p[:, :], scale, None, ALU.mult, r=[p], w=[ot])
                    n += 1
                    self.ld(dstv[:, ch0 + oc, blk * NT:(blk + 1) * NT], ot[:], r=[ot], pw=ptk, eng='pool')
            for q in range(4):
                t0 = blk * NT + q * 128
                for (col0, ncols, dst) in tm:
                    for g0 in range(0, ncols, 512):
                        gw = min(512, ncols - g0)
                        p = self.pnext()
                        for kc in range(8):
                            self.mm(p[:, 0:gw], hb[:, kc, q * 128:(q + 1) * 128], W[:, kc, col0 + g0: col0 + g0 + gw], kc == 0, kc == 7, r=[W, hb], **wr(p, kc == 0))
                        ot = otm.get()
                        if n % 2 == 0:
                            self.cp('act', ot[:, 0:gw], p[:, 0:gw], r=[p], w=[ot])
                        else:
                            self.cp('dve', ot[:, 0:gw], p[:, 0:gw], r=[p], w=[ot])
                        n += 1
                        self.ld(dst[t0:t0 + 128, g0:g0 + gw], ot[:, 0:gw], r=[ot], pw=[ptk[q]], eng='sp')

    def mlstm(self, i):
        o = self.opts
        self.proj_stage(i, self.ml_w_in, 3104,
                        fm=[(0, 4, self.qkT_v, 0, 0.125), (512, 4, self.qkT_v, 4, 1.0)],
                        tm=[(512, 512, self.ktok), (1024, 1024, self.vtok), (2048, 1024, self.otok), (3072, 32, self.gtok)])
        self.mlstm_scan()
        self.mixer_post(i, self.ml_w_out, self.ml_ng, self.ml_ng[:], self.hdir, 'sigmoid')

    def mlstm_scan(self):
        self.areset()
        cst = self.cst
        Tri = [cst.ap[0:64, 2, 0:64], cst.ap[0:64, 3, 0:64]]
        Str = [cst.ap[0:64, 4, 0:64], cst.ap[0:64, 5, 0:64]]
        ones64 = cst.ap[0:64, 1, 0:64]
        Cn = [[self.take([64, 129]) for h in range(8)] for d in range(2)]
        qks = self.take([64, 16, 64], 4)
        kts = self.take([64, 512], 4)
        v1s = self.take([64, 8, 129], 4)
        for v1 in v1s.t:
            self.memset('pool', v1[:, :, 128:129], 1.0, pw=[v1])
        gts = self.take([64, 32], 4)
        gps = self.take([64, 64], 4)
        tls = self.take([64, 64], 6)
        Es = self.take([64, 64], 6)
        WTs = self.take([64, 64], 6)
        ias = self.take([64, 129], 6)
        tots = self.take([64, 129], 6)
        dns = self.take([64, 2], 6)
        kws = self.take([64, 64], 6)
        houts = self.take([64, 8, 128], 4)
        mst = [self.take([8, 1]) for d in range(2)]
        msm = self.take([8, 8], 2)
        emf = self.take([64, 8], 2)
        GBs = self.take([8, 2], 4)
        em0 = self.take([64, 16])
        cos = self.take([64, 129], 4)
        seqs = [(p * TP, TP // 64, p) for p in range(NP)] + [(NP * TP, TS // 64, -1)]
        for (tok0, nch, pidx) in seqs:
            if pidx >= 0:
                for d in range(2):
                    for h in range(8):
                        self.memset('pool', Cn[d][h][:], 0.0, w=[Cn[d][h]])
                    self.memset('pool', mst[d][:], 0.0, w=[mst[d]])
            else:
                self.ld(em0[:], self.st_m.partition_broadcast(64), r=[self.Win], w=[em0])
                self.act(em0[:], em0[:], AF.Exp, r=[em0], w=[em0])
                for d in range(2):
                    for h in range(8):
                        T_ = Cn[d][h]
                        self.ld(T_[:, 0:128], self.st_C[d, h], r=[self.Win], w=[T_])
                        self.ld(T_[:, 128:129], self.st_n[d, h], r=[self.Win], pw=[T_], eng='pool')
                        self.ts('pool', T_[:], T_[:], em0[:, d * 8 + h: d * 8 + h + 1], None, ALU.mult, r=[T_, em0], w=[T_])
            for step in range(nch):
                for d in range(2):
                    c = step if d == 0 else nch - 1 - step
                    t0 = tok0 + c * 64
                    ptk = [self.proj_tk[t0 // 128]]
                    qk = qks.get()
                    self.ld(qk[:], self.qkT_h[:, :, t0:t0 + 64], r=ptk, w=[qk])
                    kt = kts.get()
                    self.ld(kt[:], self.ktok[t0:t0 + 64, 0:512], r=ptk, w=[kt], eng="pool")
                    v1 = v1s.get()
                    self.ld(v1[:, :, 0:128], self.vtok[t0:t0 + 64, :].rearrange("t (h e) -> t h e", h=8), r=ptk, pw=[v1])
                    gt = gts.get()
                    self.ld(gt[:], self.gtok[t0:t0 + 64, :], r=ptk, w=[gt], eng='pool')
                    gp = gps.get()
                    dc = slice(d * 8, d * 8 + 8)
                    self.tt('dve', gp[:, 0:8], gt[:, dc], self.ml_gb[0:64, dc], ALU.add, r=[gt, self.ml_gb], w=[gp])
                    self.tt('dve', gp[:, 16:24], gt[:, 16 + d * 8:24 + d * 8], self.ml_gb[0:64, 16 + d * 8:24 + d * 8], ALU.add, r=[gt, self.ml_gb], pw=[gp])
                    self.act(gp[:, 16:24], gp[:, 16:24], AF.Exp, r=[gp], pw=[gp], scale=-1.0)
                    self.act(gp[:, 16:24], gp[:, 16:24], AF.Ln, r=[gp], pw=[gp], bias=1.0)
                    self.ts('dve', gp[:, 16:24], gp[:, 16:24], -1.0, None, ALU.mult, r=[gp], pw=[gp])
                    lf = gp[:, 16:24]
                    pg = self.pnext()
                    self.mm(pg[0:64, 0:8], Tri[d], lf, True, True, r=[cst, gp], w=[pg])
                    self.mm(pg[0:64, 8:16], Str[d], lf, True, True, r=[cst, gp], pw=[pg])
                    self.mm(pg[0:64, 16:24], ones64, lf, True, True, r=[cst, gp], pw=[pg])
                    self.act(gp[:, 32:40], pg[0:64, 0:8], AF.Exp, r=[pg], pw=[gp])
                    self.tt('dve', gp[:, 56:64], pg[0:64, 8:16], gp[:, 0:8], ALU.add, r=[pg, gp], pw=[gp])
                    self.act(gp[:, 40:48], gp[:, 56:64], AF.Exp, r=[gp], pw=[gp])
                    self.act(gp[:, 48:56], pg[0:64, 16:24], AF.Exp, r=[pg], pw=[gp])
                    if pidx >= 0:
                        pt = self.pnext()
                        self.tr_(pt[0:8, 0:64], gp[:, 56:64], r=[gp], w=[pt])
                        self.tr_(pt[0:8, 64:128], gp[:, 24:32] if False else pg[0:64, 16:24], r=[pg], pw=[pt]) if False else None
                        GB = GBs.get()
                        self.S.op('dve', lambda E, GB=GB, pt=pt: E.tensor_reduce(out=GB[:, 0:1], in_=pt[0:8, 0:64], axis=AX.X, op=ALU.max), reads=[pt], writes=[GB])
                        pb = self.pnext()
                        self.mm(pb[0:8, 0:1], lf, cst.ap[0:64, 1, 0:1], True, True, r=[gp, cst], w=[pb])
                        self.stt('dve', mst[d][:], mst[d][:], pb[0:8, 0:1], GB[:, 0:1], ALU.add, ALU.max, r=[mst[d], pb, GB], w=[mst[d]])
                    ho = houts.get()
                    for h in range(8):
                        qT = qk[:, h, :]
                        kT = qk[:, 8 + h, :]
                        tl = tls.get()
                        self.ts('pool', tl[:], Tri[d], gp[:, 16 + h:17 + h], None, ALU.mult, r=[cst, gp], w=[tl])
                        pD = self.pnext()
                        self.mm(pD[0:64, 0:64], Str[d], tl[:], True, True, r=[cst, tl], w=[pD])
                        E_ = Es.get()
                        self.act(E_[:], pD[0:64, 0:64], AF.Exp, r=[pD, gp], w=[E_], bias=gp[:, h:h + 1])
                        self.tt('pool', E_[:], E_[:], Tri[d], ALU.mult, r=[E_, cst], w=[E_])
                        pK = self.pnext()
                        self.mm(pK[0:64, 0:64], kT, qT, True, True, r=[qk], w=[pK])
                        WT = WTs.get()
                        self.tt('dve', WT[:], E_[:], pK[0:64, 0:64], ALU.mult, r=[E_, pK], w=[WT])
                        pI = self.pnext()
                        self.mm(pI[0:64, 0:129], WT[:], v1[:, h, :], True, True, r=[WT, v1], w=[pI])
                        pN = self.pnext()
                        C_ = Cn[d][h]
                        self.mm(pN[0:64, 0:129], qT, C_[:], True, True, r=[qk, C_], w=[pN])
                        ia = ias.get()
                        self.cp('act', ia[:], pI[0:64, 0:129], r=[pI], w=[ia])
                        tot = tots.get()
                        self.stt('dve', tot[:], pN[0:64, 0:129], gp[:, 32 + h:33 + h], ia[:], ALU.mult, ALU.add, r=[pN, gp, ia], w=[tot])
                        dn = dns.get()
                        self.ts('dve', dn[:, 0:1], tot[:, 128:129], -1.0, None, ALU.mult, r=[tot], w=[dn])
                        self.tt('dve', dn[:, 0:1], dn[:, 0:1], tot[:, 128:129], ALU.max, r=[dn, tot], w=[dn])
                        self.ts('dve', dn[:, 0:1], dn[:, 0:1], 1.0, None, ALU.max, r=[dn], w=[dn])
                        self.recip(dn[:, 1:2], dn[:, 0:1], r=[dn], pw=[dn])
                        self.ts('pool', ho[:, h, :], tot[:, 0:128], dn[:, 1:2], None, ALU.mult, r=[tot, dn], **wr(ho, h == 0))
                        kw = kws.get()
                        self.ts('pool', kw[:], kt[:, h * 64:(h + 1) * 64], gp[:, 40 + h:41 + h], None, ALU.mult, r=[kt, gp], w=[kw])
                        pU = self.pnext()
                        self.mm(pU[0:64, 0:129], kw[:], v1[:, h, :], True, True, r=[kw, v1], w=[pU])
                        self.stt('dve', C_[:], C_[:], gp[:, 48 + h:49 + h], pU[0:64, 0:129], ALU.mult, ALU.add, r=[C_, gp, pU], w=[C_])
                    self.ld(self.hdir[d][t0:t0 + 64, :].rearrange("t (h e) -> t h e", h=8), ho[:], r=[ho], w=[self.hdir_tk[d][t0 // 64]], eng='pool')
            if pidx >= 0:
                for d in range(2):
                    dm = msm.get()
                    self.ts('dve', dm[:], cst.ap[0:8, 0, 0:8], mst[d][:, 0:1], None, ALU.mult, r=[cst, mst[d]], w=[dm])
                    pm = self.pnext()
                    self.mm(pm[0:64, 0:8], cst.ap[0:8, 1, 0:64], dm[:], True, True, r=[cst, dm], w=[pm])
                    ef = emf.get()
                    self.act(ef[:], pm[0:64, 0:8], AF.Exp, r=[pm], w=[ef], scale=-1.0)
                    self.ld(self.newm[pidx, d], mst[d][:], r=[mst[d]], w=[self.Oout], eng='pool')
                    for h in range(8):
                        co = cos.get()
                        self.ts('dve' if h % 2 else 'pool', co[:], Cn[d][h][:], ef[:, h:h + 1], None, ALU.mult, r=[Cn[d][h], ef], w=[co])
                        self.ld(self.newC[pidx, d, h], co[:, 0:128], r=[co], pw=[self.Oout], eng='sp')
                        self.ld(self.newn[pidx, d, h], co[:, 128:129], r=[co], pw=[self.Oout], eng='pool')

    def gdn(self, i):
        j = i // 3
        w_in = self.gd_w_in[j]
        self.proj_stage(i, w_in, 2048, fm=[(0, 16, self.qkT_v, 0, 1.0)], tm=[], col_lo=0)
        self.proj_stage(i, w_in, 2080, fm=[(2048, 8, self.qkT_v, 16, 1.0)],
                        tm=[(3072, 1024, self.otok), (4096, 32, self.gtok)], col_lo=2048)
        stop = self.opts.get('gdn_stop', 9)
        if stop >= 2:
            self.gdn_conv(j)
        if stop >= 3:
            self.gdn_scan(j)
        if stop >= 4:
            self.mixer_post(i, self.gd_w_out[j], self.gd_ng, self.gd_ng[:, j, :], self.hdir, 'silu')

    def gdn_conv(self, j):
        self.areset()
        xins = self.take([128, TS + 4], 2)
        accs = self.take([128, TS], 2)
        tmps = self.take([128, TS], 2)
        sqs = self.take([128, 512], 2)
        rss = self.take([128, 512], 2)
        tos = self.take([128, 4, 128], 3)
        seqs = [(p * TP, TP) for p in range(NP)] + [(NP * TP, TS)]
        n = 0
        for (tok0, T) in seqs:
            for ch in range(24):
                eng = 'dve' if n % 2 == 0 else 'pool'
                n += 1
                xin = xins.get()
                self.memset('pool', xin[:, 0:2], 0.0, w=[xin])
                self.memset('pool', xin[:, T + 2:T + 4], 0.0, pw=[xin])
                self.ld(xin[:, 2:T + 2], self.qkT_v[:, ch, tok0:tok0 + T], pw=[xin])
                acc = accs.get()
                cw = self.gd_cw
                if eng == 'dve':
                    self.ts(eng, acc[:, 0:T], xin[:, 0:T], cw[:, j, ch, 0:1], None, ALU.mult, r=[xin, cw], w=[acc])
                    for k in range(1, 5):
                        self.stt(eng, acc[:, 0:T], xin[:, k:k + T], cw[:, j, ch, k:k + 1], acc[:, 0:T], ALU.mult, ALU.add, r=[xin, cw, acc], w=[acc])
                else:
                    self.act(acc[:, 0:T], xin[:, 0:T], AF.Copy, r=[xin, cw], w=[acc], scale=cw[:, j, ch, 0:1])
                    for k in range(1, 5):
                        tm_ = tmps.get()
                        self.act(tm_[:, 0:T], xin[:, k:k + T], AF.Copy, r=[xin, cw], w=[tm_], scale=cw[:, j, ch, k:k + 1])
                        self.tt('pool', acc[:, 0:T], acc[:, 0:T], tm_[:, 0:T], ALU.add, r=[acc, tm_], w=[acc])
                self.act(acc[:, 0:T], acc[:, 0:T], AF.Silu, r=[acc], w=[acc])
                if ch < 16:
                    scale = (128.0 ** -0.5) if ch < 8 else 1.0
                    for b0 in range(0, T, 512):
                        bw = min(512, T - b0)
                        sq = sqs.get()
                        self.tt('pool', sq[:, 0:bw], acc[:, b0:b0 + bw], acc[:, b0:b0 + bw], ALU.mult, r=[acc], w=[sq])
                        p = self.pnext()
                        self.mm(p[:, 0:bw], self.cst.ap[:, 1, :], sq[:, 0:bw], True, True, r=[self.cst, sq], w=[p])
                        rs = rss.get()
                        self.act(rs[:, 0:bw], p[:, 0:bw], AF.Sqrt, r=[p, self.epsb], w=[rs], bias=self.epsb[:, 0:1])
                        self.recip(rs[:, 0:bw], rs[:, 0:bw], r=[rs], w=[rs])
                        self.stt('dve', acc[:, b0:b0 + bw], acc[:, b0:b0 + bw], scale, rs[:, 0:bw], ALU.mult, ALU.mult, r=[acc, rs], w=[acc])
                    self.ld(self.qkT_v[:, ch, tok0:tok0 + T], acc[:, 0:T], r=[acc], pw=[self.Oout], eng='pool')
                if ch >= 8:
                    dst = self.ktok if ch < 16 else self.vtok
                    c0 = (ch - 8) * 128 if ch < 16 else (ch - 16) * 128
                    for g0 in range(0, T, 512):
                        ng = min(4, (T - g0) // 128)
                        p = self.pnext()
                        for k in range(ng):
                            self.tr_(p[:, k * 128:(k + 1) * 128], acc[:, g0 + k * 128:g0 + (k + 1) * 128], r=[acc], **wr(p, k == 0))
                        to = tos.get()
                        self.cp('act', to[:, 0:ng, :], p[:, 0:ng * 128].rearrange("p (a b) -> p a b", a=ng), r=[p], w=[to])
                        self.ld(dst[tok0 + g0:tok0 + g0 + ng * 128, c0:c0 + 128].rearrange("(n p) e -> p n e", p=128), to[:, 0:ng, :], r=[to], pw=[self.Oout], eng='sp')

    def gdn_scan(self, j):
        self.areset()
        cst = self.cst
        Tri = [cst.ap[0:64, 2, 0:64], cst.ap[0:64, 3, 0:64]]
        Str = [cst.ap[0:64, 4, 0:64], cst.ap[0:64, 5, 0:64]]
        Sm = [cst.ap[0:64, 5, 0:64], cst.ap[0:64, 4, 0:64]]
        I64 = cst.ap[0:64, 0, 0:64]
        ones64 = cst.ap[0:64, 1, 0:64]
        ones64w = cst.ap[0:64, 1, 0:128]
        Sst = [[self.take([128, 128]) for h in range(8)] for d in range(2)]
        qks = self.take([128, 16, 64], 4)
        kts = self.take([64, 8, 128], 4)
        vts = self.take([64, 8, 128], 4)
        gts = self.take([64, 32], 4)
        gps = self.take([64, 48], 4)
        gls = self.take([128, 8], 4)
        NB = 18
        tls = self.take([64, 64], 6)
        Ers = self.take([64, 64], 6)
        Eis = self.take([64, 64], 6)
        Ess = self.take([64, 64], 6)
        qkTs = self.take([64, 64], NB)
        Xs = self.take([64, 64], 40)
        XTs = self.take([64, 64], 40)
        Ps = self.take([64, 64], NB)
        Us = self.take([64, 128], NB)
        kegs = self.take([64, 128], 6)
        WTs = self.take([128, 64], NB)
        kdecs = self.take([64, 128], NB)
        vns = self.take([64, 128], NB)
        o2s = self.take([64, 128], 6)
        houts = self.take([64, 8, 128], 4)
        seqs = [(p * TP, TP // 64, p) for p in range(NP)] + [(NP * TP, TS // 64, -1)]
        seqs = seqs[self.opts.get('gdn_seq0', 0):self.opts.get('gdn_seq1', 5)]
        for (tok0, nch, pidx) in seqs:
            for d in range(2):
                for h in range(8):
                    S_ = Sst[d][h]
                    if pidx >= 0:
                        self.memset('pool', S_[:], 0.0, w=[S_])
                    else:
                        self.ld(S_[:], self.st_S[j, d, h], r=[self.Win], w=[S_], eng='sp' if h % 2 else 'pool')
            for step in range(nch):
                ctx = []
                for d in range(2):
                    c = step if d == 0 else nch - 1 - step
                    t0 = tok0 + c * 64
                    qk = qks.get()
                    self.ld(qk[:], self.qkT_v[:, 0:16, t0:t0 + 64], w=[qk])
                    kt = kts.get()
                    self.ld(kt[:], self.ktok[t0:t0 + 64, 0:1024].rearrange("t (h e) -> t h e", h=8), w=[kt], eng='pool')
                    vt = vts.get()
                    self.ld(vt[:], self.vtok[t0:t0 + 64, :].rearrange("t (h e) -> t h e", h=8), w=[vt])
                    gt = gts.get()
                    self.ld(gt[:], self.gtok[t0:t0 + 64, :], w=[gt], eng='pool')
                    gp = gps.get()
                    dc = slice(d * 8, d * 8 + 8)
                    self.tt('dve', gp[:, 0:8], gt[:, dc], self.gd_dt[0:64, j, dc], ALU.add, r=[gt, self.gd_dt], w=[gp])
                    self.act(gp[:, 0:8], gp[:, 0:8], AF.Exp, r=[gp], pw=[gp])
                    self.act(gp[:, 0:8], gp[:, 0:8], AF.Ln, r=[gp], pw=[gp], bias=1.0)
                    self.tt('dve', gp[:, 8:16], gp[:, 0:8], self.gd_nea[0:64, j, dc], ALU.mult, r=[gp, self.gd_nea], pw=[gp])
                    self.act(gp[:, 16:24], gt[:, 16 + d * 8:24 + d * 8], AF.Sigmoid, r=[gt], pw=[gp])
                    self.ts('dve', gp[:, 24:32], gp[:, 16:24], -1.0, None, ALU.mult, r=[gp], pw=[gp])
                    la = gp[:, 8:16]
                    pg = self.pns()
                    self.mm(pg[0:64, 0:8], Tri[d], la, True, True, r=[cst, gp], w=[pg])
                    self.mm(pg[0:64, 8:16], Str[d], la, True, True, r=[cst, gp], pw=[pg])
                    self.mm(pg[0:128, 16:24], ones64w, la, True, True, r=[cst, gp], pw=[pg])
                    self.act(gp[:, 32:40], pg[0:64, 0:8], AF.Exp, r=[pg], pw=[gp])
                    self.act(gp[:, 40:48], pg[0:64, 8:16], AF.Exp, r=[pg], pw=[gp])
                    gl = gls.get()
                    self.act(gl[:], pg[0:128, 16:24], AF.Exp, r=[pg], w=[gl])
                    ctx.append((d, t0, qk, kt, vt, gp, gl, houts.get()))
                units = [(cx, h) for cx in ctx for h in range(8)]
                st = {}
                for (cx, h) in units:
                    d, t0, qk, kt, vt, gp, gl, ho = cx
                    kT = qk[:, 8 + h, :]
                    qT = qk[:, h, :]
                    tl = tls.get()
                    self.ts('pool', tl[:], Tri[d], gp[:, 8 + h:9 + h], None, ALU.mult, r=[cst, gp], w=[tl])
                    pD = self.pns()
                    self.mm(pD[0:64, 0:64], Str[d], tl[:], True, True, r=[cst, tl], w=[pD])
                    Er = Ers.get()
                    self.act(Er[:], pD[0:64, 0:64], AF.Exp, r=[pD], w=[Er])
                    Ei = Eis.get()
                    Es = Ess.get()
                    self.tt('pool', Ei[:], Er[:], Tri[d], ALU.mult, r=[Er, cst], w=[Ei])
                    self.tt('pool', Es[:], Er[:], Sm[d], ALU.mult, r=[Er, cst], w=[Es])
                    pKK = self.pns()
                    self.mm(pKK[0:64, 0:64], kT, kT, True, True, r=[qk], w=[pKK])
                    pKQ = self.pns()
                    self.mm(pKQ[0:64, 0:64], kT, qT, True, True, r=[qk], w=[pKQ])
                    qkT = qkTs.get()
                    self.tt('dve', qkT[:], Ei[:], pKQ[0:64, 0:64], ALU.mult, r=[Ei, pKQ], w=[qkT])
                    X = Xs.get()
                    self.stt('dve', X[:], pKK[0:64, 0:64], gp[:, 24 + h:25 + h], Es[:], ALU.mult, ALU.mult, r=[pKK, gp, Es], w=[X])
                    pT = self.pns()
                    self.tr_(pT[0:64, 0:64], X[:], r=[X], w=[pT])
                    XT = XTs.get()
                    self.cp('act', XT[:], pT[0:64, 0:64], r=[pT], w=[XT])
                    P_ = Ps.get()
                    self.tt('pool', P_[:], X[:], I64, ALU.add, r=[X, cst], w=[P_])
                    st[(d, h)] = [X, XT, P_, qkT]
                for jn in range(1, 6):
                    for (cx, h) in units:
                        d = cx[0]
                        X, XT, P_, qkT = st[(d, h)]
                        Xn = None
                        if jn < 5:
                            pX = self.pns()
                            self.mm(pX[0:64, 0:64], XT[:], X[:], True, True, r=[XT, X], w=[pX])
                            Xn = Xs.get()
                            self.cp('dve', Xn[:], pX[0:64, 0:64], r=[pX], w=[Xn])
                        pXT = self.pns()
                        self.mm(pXT[0:64, 0:64], X[:], XT[:], True, True, r=[XT, X], w=[pXT])
                        XnT = XTs.get()
                        self.cp('act', XnT[:], pXT[0:64, 0:64], r=[pXT], w=[XnT])
                        pP = self.pns()
                        self.mm(pP[0:64, 0:64], XnT[:], P_[:], True, True, r=[XnT, P_], w=[pP])
                        self.tt('dve', P_[:], P_[:], pP[0:64, 0:64], ALU.add, r=[P_, pP], w=[P_])
                        st[(d, h)] = [Xn, XnT, P_, qkT]
                for (cx, h) in units:
                    d, t0, qk, kt, vt, gp, gl, ho = cx
                    X, XT, P_, qkT = st[(d, h)]
                    pU = self.pns()
                    self.mm(pU[0:64, 0:128], P_[:], vt[:, h, :], True, True, r=[P_, vt], w=[pU])
                    U = Us.get()
                    self.ts('pool' if False else 'dve', U[:], pU[0:64, 0:128], gp[:, 16 + h:17 + h], None, ALU.mult, r=[pU, gp], w=[U])
                    keg = kegs.get()
                    self.ts('pool', keg[:], kt[:, h, :], gp[:, 32 + h:33 + h], None, ALU.mult, r=[kt, gp], w=[keg])
                    pW = self.pns()
                    self.mm(pW[0:128, 0:64], keg[:], P_[:], True, True, r=[keg, P_], w=[pW])
                    WT = WTs.get()
                    self.cp('act', WT[:], pW[0:128, 0:64], r=[pW], w=[WT])
                    kdec = kdecs.get()
                    self.ts('pool', kdec[:], kt[:, h, :], gp[:, 40 + h:41 + h], None, ALU.mult, r=[kt, gp], w=[kdec])
                    st[(d, h)] = [U, WT, kdec, qkT]
                pas = {}
                for (cx, h) in units:
                    d = cx[0]
                    U, WT, kdec, qkT = st[(d, h)]
                    pa = self.pns()
                    self.mm(pa[0:64, 0:128], WT[:], Sst[d][h][:], True, True, r=[WT, Sst[d][h]], w=[pa])
                    vn = vns.get()
                    self.stt('dve', vn[:], pa[0:64, 0:128], cx[5][:, 24 + h:25 + h], U[:], ALU.mult, ALU.add, r=[pa, cx[5], U], w=[vn])
                    pas[(d, h)] = vn
                for (cx, h) in units:
                    d, t0, qk, kt, vt, gp, gl, ho = cx
                    U, WT, kdec, qkT = st[(d, h)]
                    vn = pas[(d, h)]
                    S_ = Sst[d][h]
                    po = self.pns()
                    self.mm(po[0:64, 0:128], qk[:, h, :], S_[:], True, True, r=[qk, S_], w=[po])
                    po2 = self.pns()
                    self.mm(po2[0:64, 0:128], qkT[:], vn[:], True, True, r=[qkT, vn], w=[po2])
                    pS = self.pns()
                    self.mm(pS[0:128, 0:128], kdec[:], vn[:], True, True, r=[kdec, vn], w=[pS])
                    o2 = o2s.get()
                    self.cp('act', o2[:], po2[0:64, 0:128], r=[po2], w=[o2])
                    self.stt('dve', ho[:, h, :], po[0:64, 0:128], gp[:, 32 + h:33 + h], o2[:], ALU.mult, ALU.add, r=[po, gp, o2], **wr(ho, h == 0))
                    self.stt('dve', S_[:], S_[:], gl[:, h:h + 1], pS[0:128, 0:128], ALU.mult, ALU.add, r=[S_, gl, pS], w=[S_])
                for cx in ctx:
                    d, t0, qk, kt, vt, gp, gl, ho = cx
                    self.ld(self.hdir[d][t0:t0 + 64, :].rearrange("t (h e) -> t h e", h=8), ho[:], r=[ho], pw=[self.Oout], eng='pool')
            if pidx >= 0:
                for d in range(2):
                    for h in range(8):
                        self.ld(self.newS[pidx, j, d, h], Sst[d][h][:], r=[Sst[d][h]], pw=[self.Oout], eng='sp' if h % 2 else 'pool')

    def diffattn(self, i):
        self.proj_stage(i, self.df_w_in, 3072, fm=[],
                        tm=[(0, 2048, self.ktok), (2048, 1024, self.vtok)])
        self.attn_prep()
        self.attn_core()
        self.mixer_post(i, self.df_w_out, self.df_sg, self.df_sg[:], self.hdir, None)

    def attn_prep(self):
        self.areset()
        xs = self.take([128, 32, 64], 2)
        sqs = self.take([128, 32, 64], 1)
        sss = self.take([128, 64], 2)
        css = self.take([128, 64], 2)
        r1 = self.take([128, 32, 2, 16], 1)
        r2 = self.take([128, 32, 2, 16], 1)
        r3 = self.take([128, 32, 2, 16], 1)
        xr = self.take([128, 32, 64], 2)
        vts = self.take([128, 1024], 2)
        xos = self.take([128, 16, 128], 2)
        cks = self.take([128, 16, 64], 2)
        cko = self.take([128, 8, 128], 2)
        gq = self.df_g
        for t in range(TT // 128):
            t0 = t * 128
            x = xs.get()
            self.ld(x[:], self.ktok[t0:t0 + 128, :].rearrange("t (g d) -> t g d", g=32), r=[self.proj_tk[t]], w=[x])
            sq = sqs.get()
            self.tt('pool', sq[:], x[:], x[:], ALU.mult, r=[x], w=[sq])
            ss = sss.get()
            self.S.op('dve', lambda E, ss=ss, sq=sq: E.tensor_reduce(out=ss[:, 0:32], in_=sq[:], axis=AX.X, op=ALU.add), reads=[sq], writes=[ss])
            self.act(ss[:, 0:32], ss[:, 0:32], AF.Sqrt, r=[ss, self.epsb], w=[ss], scale=1.0 / 64, bias=self.epsb[:, 0:1])
            self.recip(ss[:, 32:64], ss[:, 0:32], r=[ss], pw=[ss])
            self.tt('dve', x[:], x[:], ss[:, 32:64].unsqueeze(2).to_broadcast([128, 32, 64]), ALU.mult, r=[x, ss], w=[x])
            self.tt('pool', x[:, 0:16, :], x[:, 0:16, :], gq[:, 0, :].unsqueeze(1).to_broadcast([128, 16, 64]), ALU.mult, r=[x, gq], w=[x])
            self.tt('dve', x[:, 16:32, :], x[:, 16:32, :], gq[:, 1, :].unsqueeze(1).to_broadcast([128, 16, 64]), ALU.mult, r=[x, gq], w=[x])
            if t0 < NP * TP:
                p, tl = t0 // TP, t0 % TP
                self.ld(self.newk[p, :, :, tl:tl + 128, :].rearrange("h m t d -> t (h m) d"), x[:, 16:32, :], r=[x], pw=[self.Oout], eng='pool')
                vt = vts.get()
                self.ld(vt[:], self.vtok[t0:t0 + 128, :], r=[self.proj_tk[t]], w=[vt])
                self.ld(self.newv[p, :, tl:tl + 128, :].rearrange("h t e -> t h e"), vt[:].rearrange("t (h e) -> t h e", h=8), r=[vt], pw=[self.Oout], eng='pool')
                src = x
            else:
                cs = css.get()
                self.ld(cs[:], self.rope_cs[t0 - NP * TP:t0 - NP * TP + 128, :], r=[self.Win], w=[cs])
                X = x[:].rearrange("t g (a f r) -> t g a f r", a=2, f=2)
                xa = X[:, :, :, 0, :]
                xb_ = X[:, :, :, 1, :]
                cosb = cs[:, 0:32].rearrange("t (a r) -> t a r", a=2).unsqueeze(1).to_broadcast([128, 32, 2, 16])
                sinb = cs[:, 32:64].rearrange("t (a r) -> t a r", a=2).unsqueeze(1).to_broadcast([128, 32, 2, 16])
                o_ = xr.get()
                O = o_[:].rearrange("t g (a f r) -> t g a f r", a=2, f=2)
                a1, a2, a3 = r1.get(), r2.get(), r3.get()
                self.tt('dve', a1[:], xa, cosb, ALU.mult, r=[x, cs], w=[a1])
                self.tt('pool', a2[:], xb_, sinb, ALU.mult, r=[x, cs], w=[a2])
                self.tt('dve', O[:, :, :, 0, :], a1[:], a2[:], ALU.subtract, r=[a1, a2], w=[o_])
                self.tt('pool', a3[:], xa, sinb, ALU.mult, r=[x, cs], w=[a3])
                self.tt('dve', a1[:], xb_, cosb, ALU.mult, r=[x, cs], w=[a1])
                self.tt('pool', O[:, :, :, 1, :], a3[:], a1[:], ALU.add, r=[a3, a1], pw=[o_])
                src = o_
            xo = xos.get()
            for g in range(4):
                pp = self.pnext()
                for k in range(4):
                    ch = g * 4 + k
                    self.tr_(pp[:, k * 128:(k + 1) * 128], src[:, 2 * ch:2 * ch + 2, :].rearrange("t a d -> t (a d)"), r=[src], **wr(pp, k == 0))
                self.cp('act' if g % 2 else 'dve', xo[:, g * 4:(g + 1) * 4, :], pp[:, :].rearrange("p (a b) -> p a b", a=4), r=[pp], **wr(xo, g == 0))
            self.ld(self.qkT_v[:, 0:16, t0:t0 + 128], xo[:], r=[xo], w=[self.prep_tk[t]], eng='pool')
        for kt in range(2):
            ck = cks.get()
            self.ld(ck[:], self.ctx_k[:, :, kt * 128:(kt + 1) * 128, :].rearrange("h m t d -> t (h m) d"), r=[self.Win], w=[ck])
            co = cko.get()
            for g in range(2):
                pp = self.pnext()
                for k in range(4):
                    ch = g * 4 + k
                    self.tr_(pp[:, k * 128:(k + 1) * 128], ck[:, 2 * ch:2 * ch + 2, :].rearrange("t a d -> t (a d)"), r=[ck], **wr(pp, k == 0))
                self.cp('act' if g % 2 else 'dve', co[:, g * 4:(g + 1) * 4, :], pp[:, :].rearrange("p (a b) -> p a b", a=4), r=[pp], **wr(co, g == 0))
            self.ld(self.ctxkT.rearrange("c p t -> p c t")[:, :, kt * 128:(kt + 1) * 128], co[:], r=[co], **wr(self.ctx_tk, kt == 0), eng='pool')

    def attn_core(self):
        self.areset()
        NKT = (TS + 256) // 128
        qTs = self.take([128, TS], 2)
        kTs = self.take([128, TS + 256], 2)
        V1s = self.take([128, NKT, 129], 2)
        for V1 in V1s.t:
            self.memset('pool', V1[:, :, 128:129], 1.0, pw=[V1])
        PTs = self.take([128, NKT, 512], 2)
        obs = self.take([128, 4, 128], 2)
        rvs = self.take([128, 2], 4)
        seqs = [(p * TP, TP, False) for p in range(NP)] + [(NP * TP, TS, True)]
        for (tok0, T, is_s) in seqs:
            nk = T + (256 if is_s else 0)
            nkt = nk // 128
            QB = min(512, T)
            tks = self.prep_tk[tok0 // 128:(tok0 + T) // 128]
            ptk = self.proj_tk[tok0 // 128:(tok0 + T) // 128]
            for h in range(8):
                qT = qTs.get()
                kT = kTs.get()
                V1 = V1s.get()
                self.ld(qT[:, 0:T], self.qkT_v[:, h, tok0:tok0 + T], r=tks, w=[qT])
                self.ld(kT[:, 0:T], self.qkT_v[:, 8 + h, tok0:tok0 + T], r=tks, w=[kT], eng='pool')
                self.ld(V1[:, 0:T // 128, 0:128], self.vtok[tok0:tok0 + T, h * 128:(h + 1) * 128].rearrange("(n p) e -> p n e", p=128), r=ptk, pw=[V1])
                if is_s:
                    self.ld(kT[:, T:T + 256], self.ctxkT[h], r=[self.ctx_tk], pw=[kT], eng='pool')
                    self.ld(V1[:, T // 128:nkt, 0:128], self.ctx_v[h].rearrange("(n p) e -> p n e", p=128), r=[self.Win], pw=[V1])
                for qb in range(T // QB):
                    ob = obs.get()
                    for m in range(2):
                        PT = PTs.get()
                        for kt in range(nkt):
                            pS = self.pnext()
                            self.mm(pS[:, 0:QB], kT[m * 64:(m + 1) * 64, kt * 128:(kt + 1) * 128], qT[m * 64:(m + 1) * 64, qb * QB:(qb + 1) * QB],
                                    True, True, r=[kT, qT], w=[pS])
                            self.act(PT[:, kt, 0:QB], pS[:, 0:QB], AF.Exp, r=[pS], **wr(PT, kt == 0), scale=0.125)
                        for qs in range(QB // 128):
                            pO = self.pnext()
                            for kt in range(nkt):
                                self.mm(pO[:, 0:129], PT[:, kt, qs * 128:(qs + 1) * 128], V1[:, kt, :], kt == 0, kt == nkt - 1, r=[PT, V1], **wr(pO, kt == 0))
                            rv = rvs.get()
                            self.recip(rv[:, 0:1], pO[:, 128:129], r=[pO], w=[rv])
                            if m == 0:
                                self.ts('dve', ob[:, qs, :], pO[:, 0:128], rv[:, 0:1], None, ALU.mult, r=[pO, rv], **wr(ob, qs == 0))
                            else:
                                self.tt('dve', rv[:, 1:2], rv[:, 0:1], self.nlam[:, 3:4], ALU.mult, r=[rv, self.nlam], pw=[rv])
                                self.stt('dve', ob[:, qs, :], pO[:, 0:128], rv[:, 1:2], ob[:, qs, :], ALU.mult, ALU.add, r=[pO, rv, ob], pw=[ob])
                    q0 = tok0 + qb * QB
                    nq = QB // 128
                    htk = self.hdir_tk[0][q0 // 64:(q0 + QB) // 64]
                    self.ld(self.hdir[0][q0:q0 + QB, h * 128:(h + 1) * 128].rearrange("(n p) e -> p n e", p=128), ob[:, 0:nq, :], r=[ob], pw=htk, eng='pool')

    def mixer_post(self, i, w_out, ng_tk, ng_bc, hdir, gate):
        self.areset()
        NT = 512
        W = self.take([128, 8, D])
        self.ld(W[:], w_out.rearrange("(kc p) n -> p kc n", p=128), r=[self.Win], w=[W])
        hfs = self.take([128, 8, 128], 2)
        hbs = self.take([128, 8, 128], 2)
        ogs = self.take([128, 8, 128], 2)
        sqs = self.take([128, 8, 128], 2)
        sss = self.take([128, 16], 2)
        yTs = self.take([128, 8, NT], 2)
        xbs = self.take([128, 8, NT], 2)
        for blk in range(TT // NT):
            c = 0 if blk < (NP * TP) // NT else 1
            tks = self.xT_tk[blk * 4:(blk + 1) * 4]
            xb = xbs.get()
            self.ld(xb[:], self.xT_v[:, :, blk * NT:(blk + 1) * NT], r=tks, w=[xb], eng='pool')
            yT = yTs.get()
            for q in range(4):
                t0 = blk * NT + q * 128
                hf = hfs.get()
                hb = hbs.get()
                og = ogs.get()
                self.ld(hf[:], hdir[0][t0:t0 + 128, :].rearrange("t (h e) -> t h e", h=8), r=self.hdir_tk[0][t0 // 64:t0 // 64 + 2], w=[hf])
                if gate is not None:
                    self.ld(hb[:], hdir[1][t0:t0 + 128, :].rearrange("t (h e) -> t h e", h=8), r=self.hdir_tk[1][t0 // 64:t0 // 64 + 2], w=[hb], eng='pool')
                    self.ld(og[:], self.otok[t0:t0 + 128, :].rearrange("t (h e) -> t h e", h=8), r=[self.proj_tk[t0 // 128]], w=[og])
                    self.tt('dve', hf[:], hf[:], hb[:], ALU.add, r=[hf, hb], w=[hf])
                sq = sqs.get()
                self.tt('pool', sq[:], hf[:], hf[:], ALU.mult, r=[hf], w=[sq])
                ss = sss.get()
                self.S.op('dve', lambda E, ss=ss, sq=sq: E.tensor_reduce(out=ss[:, 0:8], in_=sq[:], axis=AX.X, op=ALU.add), reads=[sq], writes=[ss])
                self.act(ss[:, 0:8], ss[:, 0:8], AF.Sqrt, r=[ss, self.epsb], w=[ss], scale=1.0 / 128, bias=self.epsb[:, 0:1])
                self.recip(ss[:, 8:16], ss[:, 0:8], r=[ss], pw=[ss])
                ngb = ng_bc.unsqueeze(1).to_broadcast([128, 8, 128])
                if gate == 'sigmoid':
                    self.act(og[:], og[:], AF.Sigmoid, r=[og], w=[og])
                    self.tt('pool', og[:], og[:], ngb, ALU.mult, r=[og, ng_tk], w=[og])
                elif gate == 'silu':
                    self.act(og[:], og[:], AF.Silu, r=[og], w=[og])
                    self.tt('pool', og[:], og[:], ngb, ALU.mult, r=[og, ng_tk], w=[og])
                else:
                    self.cp('pool', og[:], ngb, r=[ng_tk], w=[og])
                self.tt('dve', hf[:], hf[:], ss[:, 8:16].unsqueeze(2).to_broadcast([128, 8, 128]), ALU.mult, r=[hf, ss], w=[hf])
                self.tt('dve', hf[:], hf[:], og[:], ALU.mult, r=[hf, og], w=[hf])
                for hh in range(2):
                    p = self.pnext()
                    for k in range(4):
                        self.tr_(p[:, k * 128:(k + 1) * 128], hf[:, hh * 4 + k, :], r=[hf], **wr(p, k == 0))
                    dst = yT[:, hh * 4:(hh + 1) * 4, q * 128:(q + 1) * 128]
                    src = p[:, :].rearrange("p (a b) -> p a b", a=4)
                    self.cp('act' if hh else 'dve', dst, src, r=[p], **wr(yT, q == 0 and hh == 0))
            for oc in range(8):
                p = self.pnext()
                for kc in range(8):
                    self.mm(p[:, :], W[:, kc, oc * 128:(oc + 1) * 128], yT[:, kc, :], kc == 0, kc == 7, r=[W, yT], **wr(p, kc == 0))
                self.stt('dve', xb[:, oc, :], p[:, :], self.mod[:, 16 + oc, c:c + 1], xb[:, oc, :], ALU.mult, ALU.add,
                         r=[p, self.mod, xb], pw=[xb])
            for q in range(4):
                self.ld(self.xT_v[:, :, blk * NT + q * 128: blk * NT + (q + 1) * 128], xb[:, :, q * 128:(q + 1) * 128],
                        r=[xb], w=[tks[q]], eng='pool')

    def tr_(self, out, in_, r=(), w=(), pw=()):
        n = in_.shape[0]
        idn = self.cst.ap[0:n, 0, 0:n]
        self.S.op('pe', lambda E: E.transpose(out, in_, idn), reads=list(r) + [self.cst], writes=w, pw=pw)

    def stage_in(self):
        self.areset()
        xin = self.take([128, D], 2)
        xo = self.take([128, 8, 128], 2)
        for t in range(TT // 128):
            a = xin.get()
            self.ld(a[:], self.x_tok[t * 128:(t + 1) * 128, :], r=[self.Xtok], w=[a])
            b = xo.get()
            for h in range(2):
                p = self.pnext()
                for k in range(4):
                    kc = h * 4 + k
                    self.tr_(p[:, k * 128:(k + 1) * 128], a[:, kc * 128:(kc + 1) * 128], r=[a], w=[p] if k == 0 else (), pw=() if k == 0 else [p])
                dst = b[:, h * 4:(h + 1) * 4, :]
                src = p[:, :].rearrange("p (a b) -> p a b", a=4)
                if h == 0:
                    self.cp('dve', dst, src, r=[p], w=[b])
                else:
                    self.cp('act', dst, src, r=[p], pw=[b])
            self.ld(self.xT_v[:, :, t * 128:(t + 1) * 128], b[:], r=[b], w=[self.xT_tk[t]], eng='pool')

    def stage_out(self):
        self.areset()
        xi = self.take([128, 8, 128], 2)
        yo = self.take([128, D], 2)
        for t in range(TT // 128):
            a = xi.get()
            self.ld(a[:], self.xT_v[:, :, t * 128:(t + 1) * 128], r=[self.xT_tk[t]], w=[a])
            b = yo.get()
            for h in range(2):
                p = self.pnext()
                for k in range(4):
                    kc = h * 4 + k
                    self.tr_(p[:, k * 128:(k + 1) * 128], a[:, kc, :], r=[a], w=[p] if k == 0 else (), pw=() if k == 0 else [p])
                if h == 0:
                    self.cp('dve', b[:, 0:512], p[:, :], r=[p], w=[b])
                else:
                    self.cp('act', b[:, 512:1024], p[:, :], r=[p], pw=[b])
            self.ld(self.y_tok[t * 128:(t + 1) * 128, :], b[:], r=[b], w=[self.Ytok], eng='pool')

    def stage_mod(self, i):
        self.areset()
        wt = self.take([128, 8, 512], 2)
        wv = self.ada_w[i].rearrange("(kc p) n -> p kc n", p=128)
        mp = self.pnext()
        for n in range(12):
            w = wt.get()
            self.ld(w[:], wv[:, :, n * 512:(n + 1) * 512], r=[self.Win], w=[w])
            for jj in range(4):
                j = n * 4 + jj
                for kc in range(8):
                    self.mm(mp[:, 2 * j:2 * j + 2], w[:, kc, jj * 128:(jj + 1) * 128], self.sc[:, kc, :], kc == 0, kc == 7,
                            r=[w, self.sc], **wr(mp, j == 0 and kc == 0))
        mpv = mp[:, 0:96].rearrange("p (j c) -> p j c", c=2)
        for c in range(2):
            self.tt('dve', self.mod[:, :, c], mpv[:, :, c], self.adab[:, i, :], ALU.add, r=[mp, self.adab],
                    w=[self.mod] if c == 0 else (), pw=() if c == 0 else [self.mod])
        for wi in range(2):
            sj = 8 + 24 * wi
            for c in range(2):
                first = (wi == 0 and c == 0)
                self.stt('dve', self.modA[:, wi, :, c], self.mod[:, sj:sj + 8, c], 1.0, self.normg[:, i, wi, :], ALU.add, ALU.mult,
                         r=[self.mod, self.normg], w=[self.modA] if first else (), pw=() if first else [self.modA])

    def norm_mod(self, xb, hb, wi, c, nt, sq=None):
        sj = 24 * wi
        self.act(hb[:, :, :], xb[:, :, :], AF.Square, r=[xb], w=[hb])
        p = self.pnext()
        for kc in range(8):
            self.mm(p[:, 0:nt], self.cst.ap[:, 1, :], hb[:, kc, :], kc == 0, kc == 7, r=[self.cst, hb], **wr(p, kc == 0))
        rs = self.rstd.get()
        self.act(rs[:, 0:nt], p[:, 0:nt], AF.Sqrt, r=[p, self.epsb], w=[rs], scale=1.0 / D, bias=self.epsb[:, 0:1])
        self.recip(rs[:, 0:nt], rs[:, 0:nt], r=[rs], w=[rs])
        for kc in range(8):
            self.tt('dve' if kc % 2 == 0 else 'pool', hb[:, kc, :], xb[:, kc, :], rs[:, 0:nt], ALU.mult, r=[xb, rs],
                    w=[hb] if kc == 0 else (), pw=() if kc == 0 else [hb])
        for kc in range(8):
            self.act(hb[:, kc, :], hb[:, kc, :], AF.Identity, r=[hb, self.modA, self.mod], pw=[hb],
                     scale=self.modA[:, wi, kc, c:c + 1], bias=self.mod[:, sj + kc, c:c + 1])

    def stage_ffn(self, i):
        self.areset()
        NT = 512
        xbs = self.take([128, 8, NT], 2)
        hbs = self.take([128, 8, NT], 1)
        self.rstd = self.take([128, NT], 2)
        acts = self.take([128, 22, NT], 1)
        sg = self.take([128, NT], 2)
        wins = self.take([128, 8, 2, 128], 3)
        wouts = self.take([128, 22, 128], 2)
        wiv = self.ffn_w_in[i].rearrange("(kc p) n -> p kc n", p=128)
        wov = self.ffn_w_out[i].rearrange("(kc p) n -> p kc n", p=128)
        for blk in range(TT // NT):
            c = 0 if blk < (NP * TP) // NT else 1
            tks = self.xT_tk[blk * 4:(blk + 1) * 4]
            xb = xbs.get()
            self.ld(xb[:], self.xT_v[:, :, blk * NT:(blk + 1) * NT], r=tks, w=[xb], eng='pool')
            hb = hbs.get()
            self.norm_mod(xb, hb, 1, c, NT)
            at = acts.get()
            for j in range(22):
                w = wins.get()
                self.ld(w[:, :, 0, :], wiv[:, :, j * 128:(j + 1) * 128], r=[self.Win], w=[w])
                self.ld(w[:, :, 1, :], wiv[:, :, DFF + j * 128:DFF + (j + 1) * 128], r=[self.Win], pw=[w])
                pg = self.pnext()
                pu = self.pnext()
                for kc in range(8):
                    self.mm(pg[:, :], w[:, kc, 0, :], hb[:, kc, :], kc == 0, kc == 7, r=[w, hb], **wr(pg, kc == 0))
                for kc in range(8):
                    self.mm(pu[:, :], w[:, kc, 1, :], hb[:, kc, :], kc == 0, kc == 7, r=[w, hb], **wr(pu, kc == 0))
                s = sg.get()
                self.act(s[:], pg[:, :], AF.Silu, r=[pg], w=[s])
                self.tt('dve', at[:, j, :], s[:], pu[:, :], ALU.mult, r=[s, pu], w=[at] if j == 0 else (), pw=() if j == 0 else [at])
            for oc in range(8):
                w = wouts.get()
                self.ld(w[:], wov[:, :, oc * 128:(oc + 1) * 128], r=[self.Win], w=[w])
                p = self.pnext()
                for k2 in range(22):
                    self.mm(p[:, :], w[:, k2, :], at[:, k2, :], k2 == 0, k2 == 21, r=[w, at], **wr(p, k2 == 0))
                self.stt('dve', xb[:, oc, :], p[:, :], self.mod[:, 40 + oc, c:c + 1], xb[:, oc, :], ALU.mult, ALU.add,
                         r=[p, self.mod, xb], pw=[xb])
            for q in range(4):
                self.ld(self.xT_v[:, :, blk * NT + q * 128: blk * NT + (q + 1) * 128], xb[:, :, q * 128:(q + 1) * 128],
                        r=[xb], w=[tks[q]], eng='pool')


def host_consts():
    c = np.zeros((128, 8, 128), np.float32)
    c[:, 0, :] = np.eye(128)
    c[:, 1, :] = 1.0
    k = np.arange(128)[:, None]
    t = np.arange(128)[None, :]
    c[:, 2, :] = (k <= t)
    c[:, 3, :] = (k >= t)
    c[:, 4, :] = (k > t)
    c[:, 5, :] = (k < t)
    return c.reshape(128, 1024)


def rope_tables():
    rows = TS // 64
    row = np.broadcast_to(np.arange(rows)[:, None], (rows, 64)).reshape(-1)
    col = np.broadcast_to(np.arange(64)[None, :], (rows, 64)).reshape(-1)
    inv = (np.float32(10000.0) ** (-np.arange(16, dtype=np.float32) / np.float32(16))).astype(np.float32)
    ang = np.stack([row, col], axis=-1).astype(np.float32)[:, :, None] * inv
    return np.concatenate([np.cos(ang).reshape(TS, 32), np.sin(ang).reshape(TS, 32)], axis=1).astype(np.float32)


_CACHE = {}


def kernel(**inp):
    opts = inp.pop('_opts', {})
    key = repr(sorted(opts.items()))
    if key not in _CACHE:
        P = Prog(opts)
        P.build()
        _CACHE[key] = P
    P = _CACHE[key]
    f = lambda a: np.ascontiguousarray(np.asarray(a, dtype=np.float32))
    xp = f(inp['x_prompt'])
    xs = f(inp['x_sample'])
    c = f(inp['c'])
    c_ctx = f(inp['c_ctx'])
    ada_b = f(inp['ada_b'])
    norm_g = f(inp['norm_g'])
    shared = {
        'consts': host_consts(),
        'ada_w': f(inp['ada_w']),
        'ada_bT': f(ada_b.reshape(4, 48, 128).transpose(2, 0, 1)),
        'normgT': f(norm_g.reshape(4, 2, 8, 128).transpose(3, 0, 1, 2)),
        'ffn_w_in': f(inp['ffn_w_in']),
        'ffn_w_out': f(inp['ffn_w_out']),
        'mlstm_w_in': f(inp['mlstm_w_in'][0]),
        'mlstm_w_out': f(inp['mlstm_w_out'][0]),
        'mlstm_gate_b': f(inp['mlstm_gate_b'].reshape(1, 32)),
        'mlstm_norm_g': f(inp['mlstm_norm_g'].reshape(1, 128)),
    }
    shared.update({
        'diff_w_in': f(inp['diff_w_in'][0]),
        'diff_w_out': f(inp['diff_w_out'][0]),
        'diff_qkg': f(np.concatenate([inp['diff_q_norm_g'][0], inp['diff_k_norm_g'][0]]).reshape(1, 128)),
        'diff_lambda': f(inp['diff_lambda'][0].reshape(1, 256)),
        'diff_subln_g': f(inp['diff_subln_g'][0].reshape(1, 128)),
        'rope_cs': rope_tables(),
    })
    cw = f(inp['gdn_conv_w'])
    shared.update({
        'gdn_w_in': f(inp['gdn_w_in']),
        'gdn_w_out': f(inp['gdn_w_out']),
        'gdn_convT': f(cw.reshape(2, 5, 24, 128).transpose(3, 0, 2, 1)),
        'gdn_a_log': f(inp['gdn_a_log'].reshape(2, 1, 16)),
        'gdn_dt_bias': f(inp['gdn_dt_bias'].reshape(2, 1, 16)),
        'gdn_norm_g': f(inp['gdn_norm_g'].reshape(2, 1, 128)),
    })
    stS = f(inp['state_delta'])
    ck = f(inp['cache_diff_k'])
    cv = f(inp['cache_diff_v'])
    stC = f(inp['state_mlstm_C'])
    stn = f(inp['state_mlstm_n'])
    stm = f(inp['state_mlstm_m'])
    in_maps = []
    for k in range(NCORE):
        m = dict(shared)
        m['x_tok'] = f(np.concatenate([xp[NP * k:NP * (k + 1)].reshape(NP * TP, D), xs[k]], axis=0))
        cond = np.stack([c_ctx, c[k]], axis=-1)
        m['condT'] = f(cond.reshape(8, 128, 2).transpose(1, 0, 2))
        m['st_S'] = f(stS[k])
        m['ctx_k'] = f(ck[k, 0])
        m['ctx_v'] = f(cv[k, 0])
        m['st_C'] = f(stC[k, 0])
        m['st_n'] = f(stn[k, 0].reshape(2, 8, 64, 1))
        m['st_m'] = f(stm[k, 0].reshape(1, 16))
        in_maps.append({n: m[n] for n in P.din})
    res = run_bass_kernel_spmd(P.nc, in_maps, core_ids=list(range(NCORE)))
    R = res.results
    y = np.stack([r['y_tok'] for r in R])
    y_prompt = y[:, :NP * TP].reshape(NCORE * NP, TP, D)
    y_sample = y[:, NP * TP:]
    outs = [y_prompt, y_sample]
    if 'newS' in P.dout:
        outs.append(np.stack([r['newS'] for r in R]).reshape(NCORE * NP, 2, 2, 8, 128, 128))
    if 'newC' in P.dout:
        outs.append(np.stack([r['newC'] for r in R]).reshape(NCORE * NP, 1, 2, 8, 64, 128))
        outs.append(np.stack([r['newn'] for r in R]).reshape(NCORE * NP, 1, 2, 8, 64))
        outs.append(np.stack([r['newm'] for r in R]).reshape(NCORE * NP, 1, 2, 8))
    if 'newk' in P.dout:
        outs.append(np.stack([r['newk'] for r in R]).reshape(NCORE * NP, 1, 8, 2, TP, 64))
        outs.append(np.stack([r['newv'] for r in R]).reshape(NCORE * NP, 1, 8, TP, 128))
    return tuple(outs)
```

```python
import numpy as np
from contextlib import ExitStack
import concourse.bass as bass
import concourse.mybir as mybir
from concourse.bass_utils import run_bass_kernel_spmd

F32 = mybir.dt.float32
BF16 = mybir.dt.bfloat16
AF = mybir.ActivationFunctionType
ALU = mybir.AluOpType
AX = mybir.AxisListType

ENGS = ('pe', 'act', 'dve', 'pool', 'sp')
NDS = 40

D = 1024
NCORE = 8
NP = 4
TP = 256
TS = 2048
TT = NP * TP + TS
DFF = 2816
EPS = 1e-6


class Tk:
    __slots__ = ('ap', 'lw', 'rd', 'rp', 'name')

    def __init__(self, ap, name=''):
        self.ap = ap
        self.lw = {}
        self.rd = {}
        self.rp = {}
        self.name = name

    def __getitem__(self, idx):
        return self.ap[idx]


class Sched:
    def __init__(self, nc, es):
        self.nc = nc
        self.es = es
        self.q = {e: [] for e in ENGS}
        self.sem = {e: es.enter_context(nc.semaphore("s_" + e)) for e in ENGS}
        self.cnt = {e: 0 for e in ENGS}
        self.seen = {e: {} for e in ENGS}
        self.dsem = [es.enter_context(nc.semaphore("d%d" % i)) for i in range(NDS)]
        self.dcnt = [0] * NDS
        self.dnext = 0
        self.nins = 0
        self.uid = 0

    def sb(self, shape, dt=F32, name=None):
        self.uid += 1
        name = name or "t%d" % self.uid
        t = self.es.enter_context(self.nc.sbuf_tensor(name, list(shape), dt))
        return Tk(t, name)

    def ps(self, shape, dt=F32, name=None):
        self.uid += 1
        name = name or "p%d" % self.uid
        t = self.es.enter_context(self.nc.psum_tensor(name, list(shape), dt))
        return Tk(t, name)

    def _wait(self, eng, d):
        k = d[0]
        if eng == 'pe' and k == ('e', 'pe'):
            return
        seen = self.seen[eng]
        if seen.get(k, 0) >= d[2]:
            return
        seen[k] = d[2]
        self.q[eng].append(lambda E, d=d: E.wait_ge(d[1], d[2]))
        self.nins += 1

    def _deps(self, eng, reads, writes, pw):
        deps = {}

        def add(d):
            k = d[0]
            if k not in deps or deps[k][2] < d[2]:
                deps[k] = d
        for t in reads:
            for d in t.lw.values():
                add(d)
        for t in writes:
            for d in t.lw.values():
                add(d)
            for d in t.rd.values():
                add(d)
        for t in pw:
            for d in t.rd.values():
                add(d)
            for d in t.rp.values():
                add(d)
        for d in deps.values():
            self._wait(eng, d)

    def _mark(self, me, reads, writes, pw):
        for t in reads:
            t.rd[me[0]] = me
        for t in writes:
            t.lw = {me[0]: me}
            t.rp = t.rd
            t.rd = {}
        for t in pw:
            t.lw[me[0]] = me

    def op(self, eng, fn, reads=(), writes=(), pw=()):
        self._deps(eng, reads, writes, pw)
        self.cnt[eng] += 1
        sem = self.sem[eng]
        me = (('e', eng), sem, self.cnt[eng])
        self.q[eng].append(lambda E: fn(E).then_inc(sem, 1))
        self.nins += 1
        self._mark(me, reads, writes, pw)

    def dma(self, eng, out_ap, in_ap, reads=(), writes=(), pw=()):
        slot = self.dnext
        self.dnext = (slot + 1) % NDS
        ds = self.dsem[slot]
        self._deps(eng, reads, writes, pw)
        if self.dcnt[slot] > 0:
            self._wait(eng, (('d', slot), ds, 16 * self.dcnt[slot]))
        self.dcnt[slot] += 1
        me = (('d', slot), ds, 16 * self.dcnt[slot])
        self.q[eng].append(lambda E: E.dma_start(out=out_ap, in_=in_ap).then_inc(ds, 16))
        self.nins += 1
        self._mark(me, reads, writes, pw)

    def barrier(self):
        for e in ENGS:
            for o in ENGS:
                if o != e and self.cnt[o] > 0:
                    self._wait(e, (('e', o), self.sem[o], self.cnt[o]))
            for i in range(NDS):
                if self.dcnt[i] > 0:
                    self._wait(e, (('d', i), self.dsem[i], 16 * self.dcnt[i]))

    def emit(self):
        self.barrier()
        q = self.q
        with self.nc.Block() as block:
            @block.tensor
            def _(E):
                for f in q['pe']:
                    f(E)

            @block.scalar
            def _(E):
                for f in q['act']:
                    f(E)

            @block.vector
            def _(E):
                for f in q['dve']:
                    f(E)

            @block.gpsimd
            def _(E):
                for f in q['pool']:
                    f(E)

            @block.sync
            def _(E):
                for f in q['sp']:
                    f(E)


def wr(t, first):
    return {'w': [t]} if first else {'pw': [t]}


class Rot:
    def __init__(self, tiles):
        self.t = tiles
        self.i = 0

    def get(self):
        t = self.t[self.i % len(self.t)]
        self.i += 1
        return t


ARENA_COLS = 50000


class Prog:
    def __init__(self, opts):
        self.opts = opts
        self.nc = bass.Bass("TRN2", target_bir_lowering=False)
        self.es = ExitStack()
        self.din = {}
        self.dout = {}

    def inp(self, name, shape):
        t = self.nc.dram_tensor(name, list(shape), F32, kind="ExternalInput").ap()
        self.din[name] = t
        return t

    def outp(self, name, shape):
        t = self.nc.dram_tensor(name, list(shape), F32, kind="ExternalOutput").ap()
        self.dout[name] = t
        return t

    def scratch(self, name, shape):
        return self.nc.dram_tensor(name, list(shape), F32, kind="Internal").ap()

    def areset(self):
        self.S.barrier()
        self.apos = 0

    def take(self, shape, n=None, dt=F32):
        cols = int(np.prod(shape[1:]))
        c32 = cols if dt == F32 else (cols + 1) // 2
        out = []
        for _ in range(n or 1):
            assert self.apos + c32 <= ARENA_COLS, ("arena overflow", self.apos, c32)
            ap = self.arena[0:shape[0], self.apos:self.apos + c32]
            if dt != F32:
                ap = ap.bitcast(dt)[:, 0:cols]
            if len(shape) == 3:
                ap = ap.rearrange("p (a b) -> p a b", a=shape[1])
            elif len(shape) == 4:
                ap = ap.rearrange("p (a b c) -> p a b c", a=shape[1], b=shape[2])
            self.apos += c32
            out.append(Tk(ap))
        return out[0] if n is None else Rot(out)

    def pnext(self):
        p = self.psum[self.pi % 8]
        self.pi += 1
        return p

    def pns(self):
        return self.pnext()

    def mm(self, out, lhsT, rhs, start, stop, r=(), w=(), pw=()):
        self.S.op('pe', lambda E: E.matmul(out, lhsT=lhsT, rhs=rhs, start=start, stop=stop), reads=r, writes=w, pw=pw)

    def tr(self, out, in_, r=(), w=(), pw=()):
        ident = self.ident
        n = in_.shape[0]
        self.S.op('pe', lambda E: E.transpose(out, in_, ident[0:n, 0:n]), reads=list(r) + [ident], writes=w, pw=pw)

    def act(self, out, in_, func, r=(), w=(), pw=(), bias=None, scale=None, accum=None):
        kw = {}
        if bias is not None:
            kw['bias'] = bias
        if scale is not None:
            kw['scale'] = scale
        if accum is not None:
            kw['accum_out'] = accum
        self.S.op('act', lambda E: E.activation(out=out, in_=in_, func=func, **kw), reads=r, writes=w, pw=pw)

    def tt(self, eng, out, a, b, op, r=(), w=(), pw=()):
        self.S.op(eng, lambda E: E.tensor_tensor(out=out, in0=a, in1=b, op=op), reads=r, writes=w, pw=pw)

    def ts(self, eng, out, a, s1, s2, op0, op1=None, r=(), w=(), pw=()):
        if op1 is None:
            self.S.op(eng, lambda E: E.tensor_scalar(out=out, in0=a, scalar1=s1, scalar2=None, op0=op0), reads=r, writes=w, pw=pw)
        else:
            self.S.op(eng, lambda E: E.tensor_scalar(out=out, in0=a, scalar1=s1, scalar2=s2, op0=op0, op1=op1), reads=r, writes=w, pw=pw)

    def stt(self, eng, out, a, s, b, op0, op1, r=(), w=(), pw=()):
        self.S.op(eng, lambda E: E.scalar_tensor_tensor(out=out, in0=a, scalar=s, in1=b, op0=op0, op1=op1), reads=r, writes=w, pw=pw)

    def cp(self, eng, out, in_, r=(), w=(), pw=()):
        if eng == 'act':
            self.S.op('act', lambda E: E.copy(out=out, in_=in_), reads=r, writes=w, pw=pw)
        else:
            self.S.op(eng, lambda E: E.tensor_copy(out=out, in_=in_), reads=r, writes=w, pw=pw)

    def recip(self, out, in_, r=(), w=(), pw=()):
        self.S.op('dve', lambda E: E.reciprocal(out=out, in_=in_), reads=r, writes=w, pw=pw)

    def memset(self, eng, ap, val, w=(), pw=()):
        self.S.op(eng, lambda E: E.memset(ap, val), writes=w, pw=pw)

    def ld(self, out, in_, r=(), w=(), pw=(), eng='sp'):
        self.S.dma(eng, out, in_, reads=r, writes=w, pw=pw)

    def build(self):
        nc = self.nc
        o = self.opts
        with self.es:
            S = self.S = Sched(nc, self.es)
            self.arena = self.es.enter_context(nc.sbuf_tensor("arena", [128, ARENA_COLS], F32))
            self.psum = [S.ps([128, 512], name="ps%d" % i) for i in range(8)]
            self.pi = 0
            self.apos = 0
            self.psmall = [Tk(self.psum[i // 2].ap[:, (i % 2) * 256:(i % 2) * 256 + 256]) for i in range(16)]
            self.psi = 0
            self.x_tok = self.inp("x_tok", [TT, D])
            self.y_tok = self.outp("y_tok", [TT, D])
            self.xT = self.scratch("xT", [8, 128, TT])
            self.xT_v = self.xT.rearrange("c p t -> p c t")
            self.xT_tk = [Tk(None, "xT%d" % i) for i in range(TT // 128)]
            self.Xtok = Tk(None)
            self.Ytok = Tk(None)
            self.Win = Tk(None)
            consts = self.inp("consts", [128, 8 * 128])
            condT = self.inp("condT", [128, 8, 2])
            self.ada_w = self.inp("ada_w", [4, D, 6 * D])
            ada_bT = self.inp("ada_bT", [128, 4, 48])
            normgT = self.inp("normgT", [128, 4, 2, 8])
            self.ffn_w_in = self.inp("ffn_w_in", [4, D, 2 * DFF])
            self.ffn_w_out = self.inp("ffn_w_out", [4, DFF, D])
            self.cst = S.sb([128, 8, 128], name="cst")
            self.ld(self.cst[:], consts.rearrange("p (a b) -> p a b", a=8), r=[self.Win], w=[self.cst])
            self.ones = self.cst.ap[:, 1, :]
            self.sc = S.sb([128, 8, 2], name="sc")
            self.ld(self.sc[:], condT, r=[self.Win], w=[self.sc])
            self.act(self.sc[:], self.sc[:], AF.Silu, r=[self.sc], w=[self.sc])
            self.adab = S.sb([128, 4, 48], name="adab")
            self.ld(self.adab[:], ada_bT, r=[self.Win], w=[self.adab])
            self.normg = S.sb([128, 4, 2, 8], name="normg")
            self.ld(self.normg[:], normgT, r=[self.Win], w=[self.normg])
            self.mod = S.sb([128, 48, 2], name="mod")
            self.modA = S.sb([128, 2, 8, 2], name="modA")
            self.epsb = S.sb([128, 1], name="epsb")
            self.memset('pool', self.epsb[:], EPS, w=[self.epsb])

            self.stage_in()
            self.bf = o.get('bf16', True)
            mixers = o.get('mixers', (0, 1, 2))
            self.setup_mixers(mixers)
            for i in range(o.get('depth', 4)):
                self.stage_mod(i)
                if i % 3 == 0 and 0 in mixers:
                    self.gdn(i)
                if i % 3 == 1 and 1 in mixers:
                    self.mlstm(i)
                if i % 3 == 2 and 2 in mixers:
                    self.diffattn(i)
                if o.get('ffn', True):
                    if self.bf:
                        self.stage_ffn16(i)
                    else:
                        self.stage_ffn(i)
            self.stage_out()
            S.emit()
        return nc

    def setup_mixers(self, mixers):
        S = self.S
        if 1 in mixers:
            self.ml_w_in = self.inp("mlstm_w_in", [D, 3104])
            self.ml_w_out = self.inp("mlstm_w_out", [D, D])
            ml_gb = self.inp("mlstm_gate_b", [1, 32])
            ml_ng = self.inp("mlstm_norm_g", [1, 128])
            self.st_C = self.inp("st_C", [2, 8, 64, 128])
            self.st_n = self.inp("st_n", [2, 8, 64, 1])
            self.st_m = self.inp("st_m", [1, 16])
            self.newC = self.outp("newC", [NP, 2, 8, 64, 128])
            self.newn = self.outp("newn", [NP, 2, 8, 64, 1])
            self.newm = self.outp("newm", [NP, 2, 8, 1])
            self.ml_gb = S.sb([128, 32], name="ml_gb")
            self.ld(self.ml_gb[:], ml_gb.partition_broadcast(128), r=[self.Win], w=[self.ml_gb])
            self.ml_ng = S.sb([128, 128], name="ml_ng")
            self.ld(self.ml_ng[:], ml_ng.partition_broadcast(128), r=[self.Win], w=[self.ml_ng])
        self.Oout = Tk(None)
        self.qkT = self.scratch("qkT", [24, 128, TT])
        self.qkT_v = self.qkT.rearrange("c p t -> p c t")
        self.qkT_h = self.qkT[0:8].rearrange("c (two p) t -> p (c two) t", two=2)
        self.ktok = self.scratch("ktok", [TT, 2048])
        self.vtok = self.scratch("vtok", [TT, 1024])
        self.otok = self.scratch("otok", [TT, 1024])
        self.gtok = self.scratch("gtok", [TT, 32])
        self.hdir = [self.scratch("hdir%d" % d, [TT, 1024]) for d in range(2)]
        self.proj_tk = [Tk(None) for _ in range(TT // 128)]
        self.prep_tk = [Tk(None) for _ in range(TT // 128)]
        self.hdir_tk = [[Tk(None) for _ in range(TT // 64)] for d in range(2)]
        if 2 in mixers:
            self.df_w_in = self.inp("diff_w_in", [D, 3072])
            self.df_w_out = self.inp("diff_w_out", [D, D])
            df_g = self.inp("diff_qkg", [1, 128])
            df_lam = self.inp("diff_lambda", [1, 256])
            df_sg = self.inp("diff_subln_g", [1, 128])
            self.rope_cs = self.inp("rope_cs", [TS, 64])
            self.ctx_k = self.inp("ctx_k", [8, 2, 256, 64])
            self.ctx_v = self.inp("ctx_v", [8, 256, 128])
            self.newk = self.outp("newk", [NP, 8, 2, TP, 64])
            self.newv = self.outp("newv", [NP, 8, TP, 128])
            self.df_g = S.sb([128, 2, 64], name="df_g")
            self.ld(self.df_g[:], df_g.rearrange("o (a b) -> o a b", a=2).partition_broadcast(128), r=[self.Win], w=[self.df_g])
            self.df_sg = S.sb([128, 128], name="df_sg")
            self.ld(self.df_sg[:], df_sg.partition_broadcast(128), r=[self.Win], w=[self.df_sg])
            lam_init = 0.8 - 0.6 * float(np.exp(-0.3 * 2))
            self.ts('dve', self.df_sg[:], self.df_sg[:], 1.0 - lam_init, None, ALU.mult, r=[self.df_sg], w=[self.df_sg])
            lm = S.sb([128, 4, 64], name="df_lm")
            self.ld(lm[:], df_lam.rearrange("o (a b) -> o a b", a=4).partition_broadcast(128), r=[self.Win], w=[lm])
            l2 = S.sb([128, 2, 64], name="df_l2")
            self.tt('dve', l2[:, 0, :], lm[:, 0, :], lm[:, 1, :], ALU.mult, r=[lm], w=[l2])
            self.tt('dve', l2[:, 1, :], lm[:, 2, :], lm[:, 3, :], ALU.mult, r=[lm], pw=[l2])
            self.nlam = S.sb([128, 4], name="nlam")
            nl = self.nlam
            S.op('dve', lambda E: E.tensor_reduce(out=nl[:, 0:2], in_=l2[:], axis=AX.X, op=ALU.add), reads=[l2], writes=[nl])
            self.act(nl[:, 0:2], nl[:, 0:2], AF.Exp, r=[nl], w=[nl])
            self.tt('dve', nl[:, 2:3], nl[:, 1:2], nl[:, 0:1], ALU.subtract, r=[nl], pw=[nl])
            self.ts('dve', nl[:, 3:4], nl[:, 2:3], -lam_init, None, ALU.add, r=[nl], pw=[nl])
            self.ctxkT = self.scratch("ctxkT", [8, 128, 256])
            self.ctx_tk = Tk(None)
        if 0 in mixers:
            self.gd_w_in = self.inp("gdn_w_in", [2, D, 4128])
            self.gd_w_out = self.inp("gdn_w_out", [2, D, D])
            gd_cw = self.inp("gdn_convT", [128, 2, 24, 5])
            gd_al = self.inp("gdn_a_log", [2, 1, 16])
            gd_dt = self.inp("gdn_dt_bias", [2, 1, 16])
            gd_ng = self.inp("gdn_norm_g", [2, 1, 128])
            self.st_S = self.inp("st_S", [2, 2, 8, 128, 128])
            self.newS = self.outp("newS", [NP, 2, 2, 8, 128, 128])
            self.gd_cw = S.sb([128, 2, 24, 5], name="gd_cw")
            self.ld(self.gd_cw[:], gd_cw, r=[self.Win], w=[self.gd_cw])
            self.gd_nea = S.sb([128, 2, 16], name="gd_nea")
            self.gd_dt = S.sb([128, 2, 16], name="gd_dt")
            self.gd_ng = S.sb([128, 2, 128], name="gd_ng")
            for j in range(2):
                self.ld(self.gd_nea[:, j, :], gd_al[j].partition_broadcast(128), r=[self.Win], **wr(self.gd_nea, j == 0))
                self.ld(self.gd_dt[:, j, :], gd_dt[j].partition_broadcast(128), r=[self.Win], **wr(self.gd_dt, j == 0))
                self.ld(self.gd_ng[:, j, :], gd_ng[j].partition_broadcast(128), r=[self.Win], **wr(self.gd_ng, j == 0))
            self.act(self.gd_nea[:], self.gd_nea[:], AF.Exp, r=[self.gd_nea], w=[self.gd_nea])
            self.ts('dve', self.gd_nea[:], self.gd_nea[:], -1.0, None, ALU.mult, r=[self.gd_nea], w=[self.gd_nea])

    def proj_stage(self, i, w_in, ncol, fm, tm, col_lo=0):
        self.areset()
        NT = 512
        xbs = self.take([128, 8, NT], 1)
        hbs = self.take([128, 8, NT], 1)
        self.rstd = self.take([128, NT], 2)
        W = self.take([128, 8, ncol])
        wv_ = w_in.rearrange("(kc p) n -> p kc n", p=128)
        self.ld(W[:, 0:4, :], wv_[:, 0:4, col_lo:col_lo + ncol], r=[self.Win], w=[W])
        self.ld(W[:, 4:8, :], wv_[:, 4:8, col_lo:col_lo + ncol], r=[self.Win], pw=[W], eng='pool')
        fm = [(a - col_lo, b, c_, d_, e_) for (a, b, c_, d_, e_) in fm]
        tm = [(a - col_lo, b, c_) for (a, b, c_) in tm]
        ofm = self.take([128, NT], 3)
        otm = self.take([128, 512], 3)
        for blk in range(TT // NT):
            c = 0 if blk < (NP * TP) // NT else 1
            tks = self.xT_tk[blk * 4:(blk + 1) * 4]
            ptk = self.proj_tk[blk * 4:(blk + 1) * 4]
            xb = xbs.get()
            self.ld(xb[:], self.xT_v[:, :, blk * NT:(blk + 1) * NT], r=tks, w=[xb], eng='pool')
            hb = hbs.get()
            self.norm_mod(xb, hb, 0, c, NT)
            n = 0
            for (col0, nch, dstv, ch0, scale) in fm:
                for oc in range(nch):
                    p = self.pnext()
                    for kc in range(8):
                        self.mm(p[:, :], W[:, kc, col0 + oc * 128: col0 + (oc + 1) * 128], hb[:, kc, :], kc == 0, kc == 7, r=[W, hb], **wr(p, kc == 0))
                    ot = ofm.get()
                    if n % 2 == 0:
                        self.act(ot[:], p[:, :], AF.Copy, r=[p], w=[ot], scale=scale)
                    else:
                        self.ts('dve', ot[:], p[:, :], scale, None, ALU.mult, r=[p], w=[ot])
                    n += 1
                    self.ld(dstv[:, ch0 + oc, blk * NT:(blk + 1) * NT], ot[:], r=[ot], pw=ptk, eng='pool')
            for q in range(4):
                t0 = blk * NT + q * 128
                for (col0, ncols, dst) in tm:
                    for g0 in range(0, ncols, 512):
                        gw = min(512, ncols - g0)
                        p = self.pnext()
                        for kc in range(8):
                            self.mm(p[:, 0:gw], hb[:, kc, q * 128:(q + 1) * 128], W[:, kc, col0 + g0: col0 + g0 + gw], kc == 0, kc == 7, r=[W, hb], **wr(p, kc == 0))
                        ot = otm.get()
                        if n % 2 == 0:
                            self.cp('act', ot[:, 0:gw], p[:, 0:gw], r=[p], w=[ot])
                        else:
                            self.cp('dve', ot[:, 0:gw], p[:, 0:gw], r=[p], w=[ot])
                        n += 1
                        self.ld(dst[t0:t0 + 128, g0:g0 + gw], ot[:, 0:gw], r=[ot], pw=[ptk[q]], eng='sp')

    def load_w16(self, W16, w_view, ncol, col_lo=0, piece=512, eng2='pool'):
        n = 0
        for c0 in range(0, ncol, piece):
            cw = min(piece, ncol - c0)
            st = self.wstage.get()
            self.ld(st[:, :, 0:cw], w_view[:, :, col_lo + c0:col_lo + c0 + cw], r=[self.Win], w=[st], eng='sp' if n % 2 == 0 else eng2)
            if n % 2 == 0:
                self.cp('pool', W16[:, :, c0:c0 + cw], st[:, :, 0:cw], r=[st], **wr(W16, c0 == 0))
            else:
                self.cp('act', W16[:, :, c0:c0 + cw], st[:, :, 0:cw], r=[st], **wr(W16, c0 == 0))
            n += 1

    def proj_stage16(self, i, w_in, ncol, fm, tm):
        self.areset()
        NT = 512
        xbs = self.take([128, 8, NT], 2)
        hbs = self.take([128, 8, NT], 2, BF16)
        sq = self.take([128, 8, NT])
        self.rstd = self.take([128, NT], 2)
        W = self.take([128, 8, ncol], None, BF16)
        self.wstage = self.take([128, 8, 512], 2)
        self.load_w16(W, w_in.rearrange("(kc p) n -> p kc n", p=128), ncol)
        ofm = self.take([128, NT], 3)
        otm = self.take([128, 512], 3)
        for blk in range(TT // NT):
            c = 0 if blk < (NP * TP) // NT else 1
            tks = self.xT_tk[blk * 4:(blk + 1) * 4]
            xb = xbs.get()
            self.ld(xb[:], self.xT_v[:, :, blk * NT:(blk + 1) * NT], r=tks, w=[xb], eng='pool')
            hb = hbs.get()
            self.norm_mod2(xb, xb[:, :, :], hb, hb[:, :, :], sq, 0, c, NT, True)
            n = 0
            for (col0, nch, dstv, ch0, scale) in fm:
                for oc in range(nch):
                    p = self.pnext()
                    for kc in range(8):
                        self.mm(p[:, :], W[:, kc, col0 + oc * 128: col0 + (oc + 1) * 128], hb[:, kc, :], kc == 0, kc == 7, r=[W, hb], **wr(p, kc == 0))
                    ot = ofm.get()
                    if n % 2 == 0:
                        self.act(ot[:], p[:, :], AF.Copy, r=[p], w=[ot], scale=scale)
                    else:
                        self.ts('dve', ot[:], p[:, :], scale, None, ALU.mult, r=[p], w=[ot])
                    n += 1
                    self.ld(dstv[:, ch0 + oc, blk * NT:(blk + 1) * NT], ot[:], r=[ot], pw=[self.Oout], eng='pool')
            for q in range(4):
                t0 = blk * NT + q * 128
                for (col0, ncols, dst) in tm:
                    for g0 in range(0, ncols, 512):
                        gw = min(512, ncols - g0)
                        p = self.pnext()
                        for kc in range(8):
                            self.mm(p[:, 0:gw], hb[:, kc, q * 128:(q + 1) * 128], W[:, kc, col0 + g0: col0 + g0 + gw], kc == 0, kc == 7, r=[W, hb], **wr(p, kc == 0))
                        ot = otm.get()
                        if n % 2 == 0:
                            self.cp('act', ot[:, 0:gw], p[:, 0:gw], r=[p], w=[ot])
                        else:
                            self.cp('dve', ot[:, 0:gw], p[:, 0:gw], r=[p], w=[ot])
                        n += 1
                        self.ld(dst[t0:t0 + 128, g0:g0 + gw], ot[:, 0:gw], r=[ot], pw=[self.Oout], eng='sp')

    def mlstm(self, i):
        o = self.opts
        (self.proj_stage16 if self.bf else self.proj_stage)(i, self.ml_w_in, 3104,
                        fm=[(0, 4, self.qkT_v, 0, 0.125), (512, 4, self.qkT_v, 4, 1.0)],
                        tm=[(512, 512, self.ktok), (1024, 1024, self.vtok), (2048, 1024, self.otok), (3072, 32, self.gtok)])
        self.mlstm_scan()
        self.mixer_post(i, self.ml_w_out, self.ml_ng, self.ml_ng[:], self.hdir, 'sigmoid')

    def mlstm_scan(self):
        self.areset()
        cst = self.cst
        Tri = [cst.ap[0:64, 2, 0:64], cst.ap[0:64, 3, 0:64]]
        Str = [cst.ap[0:64, 4, 0:64], cst.ap[0:64, 5, 0:64]]
        ones64 = cst.ap[0:64, 1, 0:64]
        Cn = [[self.take([64, 129]) for h in range(8)] for d in range(2)]
        qks = self.take([64, 16, 64], 4)
        kts = self.take([64, 512], 4)
        v1s = self.take([64, 8, 129], 4)
        for v1 in v1s.t:
            self.memset('pool', v1[:, :, 128:129], 1.0, pw=[v1])
        gts = self.take([64, 32], 4)
        gps = self.take([64, 64], 4)
        tls = self.take([64, 64], 6)
        Es = self.take([64, 64], 6)
        WTs = self.take([64, 64], 6)
        ias = self.take([64, 129], 6)
        tots = self.take([64, 129], 6)
        dns = self.take([64, 2], 6)
        kws = self.take([64, 64], 6)
        houts = self.take([64, 8, 128], 4)
        mst = [self.take([8, 1]) for d in range(2)]
        msm = self.take([8, 8], 2)
        emf = self.take([64, 8], 2)
        GBs = self.take([8, 2], 4)
        em0 = self.take([64, 16])
        cos = self.take([64, 129], 4)
        seqs = [(p * TP, TP // 64, p) for p in range(NP)] + [(NP * TP, TS // 64, -1)]
        for (tok0, nch, pidx) in seqs:
            if pidx >= 0:
                for d in range(2):
                    for h in range(8):
                        self.memset('pool', Cn[d][h][:], 0.0, w=[Cn[d][h]])
                    self.memset('pool', mst[d][:], 0.0, w=[mst[d]])
            else:
                self.ld(em0[:], self.st_m.partition_broadcast(64), r=[self.Win], w=[em0])
                self.act(em0[:], em0[:], AF.Exp, r=[em0], w=[em0])
                for d in range(2):
                    for h in range(8):
                        T_ = Cn[d][h]
                        self.ld(T_[:, 0:128], self.st_C[d, h], r=[self.Win], w=[T_])
                        self.ld(T_[:, 128:129], self.st_n[d, h], r=[self.Win], pw=[T_], eng='pool')
                        self.ts('pool', T_[:], T_[:], em0[:, d * 8 + h: d * 8 + h + 1], None, ALU.mult, r=[T_, em0], w=[T_])
            for step in range(nch):
                for d in range(2):
                    c = step if d == 0 else nch - 1 - step
                    t0 = tok0 + c * 64
                    ptk = [self.proj_tk[t0 // 128]]
                    qk = qks.get()
                    self.ld(qk[:], self.qkT_h[:, :, t0:t0 + 64], r=ptk, w=[qk])
                    kt = kts.get()
                    self.ld(kt[:], self.ktok[t0:t0 + 64, 0:512], r=ptk, w=[kt], eng="pool")
                    v1 = v1s.get()
                    self.ld(v1[:, :, 0:128], self.vtok[t0:t0 + 64, :].rearrange("t (h e) -> t h e", h=8), r=ptk, pw=[v1])
                    gt = gts.get()
                    self.ld(gt[:], self.gtok[t0:t0 + 64, :], r=ptk, w=[gt], eng='pool')
                    gp = gps.get()
                    dc = slice(d * 8, d * 8 + 8)
                    self.tt('dve', gp[:, 0:8], gt[:, dc], self.ml_gb[0:64, dc], ALU.add, r=[gt, self.ml_gb], w=[gp])
                    self.tt('dve', gp[:, 16:24], gt[:, 16 + d * 8:24 + d * 8], self.ml_gb[0:64, 16 + d * 8:24 + d * 8], ALU.add, r=[gt, self.ml_gb], pw=[gp])
                    self.act(gp[:, 16:24], gp[:, 16:24], AF.Exp, r=[gp], pw=[gp], scale=-1.0)
                    self.act(gp[:, 16:24], gp[:, 16:24], AF.Ln, r=[gp], pw=[gp], bias=1.0)
                    self.ts('dve', gp[:, 16:24], gp[:, 16:24], -1.0, None, ALU.mult, r=[gp], pw=[gp])
                    lf = gp[:, 16:24]
                    pg = self.pnext()
                    self.mm(pg[0:64, 0:8], Tri[d], lf, True, True, r=[cst, gp], w=[pg])
                    self.mm(pg[0:64, 8:16], Str[d], lf, True, True, r=[cst, gp], pw=[pg])
                    self.mm(pg[0:64, 16:24], ones64, lf, True, True, r=[cst, gp], pw=[pg])
                    self.act(gp[:, 32:40], pg[0:64, 0:8], AF.Exp, r=[pg], pw=[gp])
                    self.tt('dve', gp[:, 56:64], pg[0:64, 8:16], gp[:, 0:8], ALU.add, r=[pg, gp], pw=[gp])
                    self.act(gp[:, 40:48], gp[:, 56:64], AF.Exp, r=[gp], pw=[gp])
                    self.act(gp[:, 48:56], pg[0:64, 16:24], AF.Exp, r=[pg], pw=[gp])
                    if pidx >= 0:
                        pt = self.pnext()
                        self.tr_(pt[0:8, 0:64], gp[:, 56:64], r=[gp], w=[pt])
                        self.tr_(pt[0:8, 64:128], gp[:, 24:32] if False else pg[0:64, 16:24], r=[pg], pw=[pt]) if False else None
                        GB = GBs.get()
                        self.S.op('dve', lambda E, GB=GB, pt=pt: E.tensor_reduce(out=GB[:, 0:1], in_=pt[0:8, 0:64], axis=AX.X, op=ALU.max), reads=[pt], writes=[GB])
                        pb = self.pnext()
                        self.mm(pb[0:8, 0:1], lf, cst.ap[0:64, 1, 0:1], True, True, r=[gp, cst], w=[pb])
                        self.stt('dve', mst[d][:], mst[d][:], pb[0:8, 0:1], GB[:, 0:1], ALU.add, ALU.max, r=[mst[d], pb, GB], w=[mst[d]])
                    ho = houts.get()
                    for h in range(8):
                        qT = qk[:, h, :]
                        kT = qk[:, 8 + h, :]
                        tl = tls.get()
                        self.ts('pool', tl[:], Tri[d], gp[:, 16 + h:17 + h], None, ALU.mult, r=[cst, gp], w=[tl])
                        pD = self.pnext()
                        self.mm(pD[0:64, 0:64], Str[d], tl[:], True, True, r=[cst, tl], w=[pD])
                        E_ = Es.get()
                        self.act(E_[:], pD[0:64, 0:64], AF.Exp, r=[pD, gp], w=[E_], bias=gp[:, h:h + 1])
                        self.tt('pool', E_[:], E_[:], Tri[d], ALU.mult, r=[E_, cst], w=[E_])
                        pK = self.pnext()
                        self.mm(pK[0:64, 0:64], kT, qT, True, True, r=[qk], w=[pK])
                        WT = WTs.get()
                        self.tt('dve', WT[:], E_[:], pK[0:64, 0:64], ALU.mult, r=[E_, pK], w=[WT])
                        pI = self.pnext()
                        self.mm(pI[0:64, 0:129], WT[:], v1[:, h, :], True, True, r=[WT, v1], w=[pI])
                        pN = self.pnext()
                        C_ = Cn[d][h]
                        self.mm(pN[0:64, 0:129], qT, C_[:], True, True, r=[qk, C_], w=[pN])
                        ia = ias.get()
                        self.cp('act', ia[:], pI[0:64, 0:129], r=[pI], w=[ia])
                        tot = tots.get()
                        self.stt('dve', tot[:], pN[0:64, 0:129], gp[:, 32 + h:33 + h], ia[:], ALU.mult, ALU.add, r=[pN, gp, ia], w=[tot])
                        dn = dns.get()
                        self.ts('dve', dn[:, 0:1], tot[:, 128:129], -1.0, None, ALU.mult, r=[tot], w=[dn])
                        self.tt('dve', dn[:, 0:1], dn[:, 0:1], tot[:, 128:129], ALU.max, r=[dn, tot], w=[dn])
                        self.ts('dve', dn[:, 0:1], dn[:, 0:1], 1.0, None, ALU.max, r=[dn], w=[dn])
                        self.recip(dn[:, 1:2], dn[:, 0:1], r=[dn], pw=[dn])
                        self.ts('pool', ho[:, h, :], tot[:, 0:128], dn[:, 1:2], None, ALU.mult, r=[tot, dn], **wr(ho, h == 0))
                        kw = kws.get()
                        self.ts('pool', kw[:], kt[:, h * 64:(h + 1) * 64], gp[:, 40 + h:41 + h], None, ALU.mult, r=[kt, gp], w=[kw])
                        pU = self.pnext()
                        self.mm(pU[0:64, 0:129], kw[:], v1[:, h, :], True, True, r=[kw, v1], w=[pU])
                        self.stt('dve', C_[:], C_[:], gp[:, 48 + h:49 + h], pU[0:64, 0:129], ALU.mult, ALU.add, r=[C_, gp, pU], w=[C_])
                    self.ld(self.hdir[d][t0:t0 + 64, :].rearrange("t (h e) -> t h e", h=8), ho[:], r=[ho], w=[self.hdir_tk[d][t0 // 64]], eng='pool')
            if pidx >= 0:
                for d in range(2):
                    dm = msm.get()
                    self.ts('dve', dm[:], cst.ap[0:8, 0, 0:8], mst[d][:, 0:1], None, ALU.mult, r=[cst, mst[d]], w=[dm])
                    pm = self.pnext()
                    self.mm(pm[0:64, 0:8], cst.ap[0:8, 1, 0:64], dm[:], True, True, r=[cst, dm], w=[pm])
                    ef = emf.get()
                    self.act(ef[:], pm[0:64, 0:8], AF.Exp, r=[pm], w=[ef], scale=-1.0)
                    self.ld(self.newm[pidx, d], mst[d][:], r=[mst[d]], w=[self.Oout], eng='pool')
                    for h in range(8):
                        co = cos.get()
                        self.ts('dve' if h % 2 else 'pool', co[:], Cn[d][h][:], ef[:, h:h + 1], None, ALU.mult, r=[Cn[d][h], ef], w=[co])
                        self.ld(self.newC[pidx, d, h], co[:, 0:128], r=[co], pw=[self.Oout], eng='sp')
                        self.ld(self.newn[pidx, d, h], co[:, 128:129], r=[co], pw=[self.Oout], eng='pool')

    def gdn(self, i):
        j = i // 3
        w_in = self.gd_w_in[j]
        if self.bf:
            self.proj_stage16(i, w_in, 4128, fm=[(0, 24, self.qkT_v, 0, 1.0)],
                              tm=[(3072, 1024, self.otok), (4096, 32, self.gtok)])
        else:
            self.proj_stage(i, w_in, 2048, fm=[(0, 16, self.qkT_v, 0, 1.0)], tm=[], col_lo=0)
            self.proj_stage(i, w_in, 2080, fm=[(2048, 8, self.qkT_v, 16, 1.0)],
                            tm=[(3072, 1024, self.otok), (4096, 32, self.gtok)], col_lo=2048)
        stop = self.opts.get('gdn_stop', 9)
        if stop >= 2:
            self.gdn_conv(j)
        if stop >= 3:
            self.gdn_scan(j)
        if stop >= 4:
            self.mixer_post(i, self.gd_w_out[j], self.gd_ng, self.gd_ng[:, j, :], self.hdir, 'silu')

    def gdn_conv(self, j):
        self.areset()
        xins = self.take([128, TS + 4], 2)
        accs = self.take([128, TS], 2)
        tmps = self.take([128, TS], 2)
        sqs = self.take([128, 512], 2)
        rss = self.take([128, 512], 2)
        tos = self.take([128, 4, 128], 3)
        seqs = [(p * TP, TP) for p in range(NP)] + [(NP * TP, TS)]
        n = 0
        for (tok0, T) in seqs:
            for ch in range(24):
                eng = 'dve' if n % 2 == 0 else 'pool'
                n += 1
                xin = xins.get()
                self.memset('pool', xin[:, 0:2], 0.0, w=[xin])
                self.memset('pool', xin[:, T + 2:T + 4], 0.0, pw=[xin])
                self.ld(xin[:, 2:T + 2], self.qkT_v[:, ch, tok0:tok0 + T], pw=[xin])
                acc = accs.get()
                cw = self.gd_cw
                if eng == 'dve':
                    self.ts(eng, acc[:, 0:T], xin[:, 0:T], cw[:, j, ch, 0:1], None, ALU.mult, r=[xin, cw], w=[acc])
                    for k in range(1, 5):
                        self.stt(eng, acc[:, 0:T], xin[:, k:k + T], cw[:, j, ch, k:k + 1], acc[:, 0:T], ALU.mult, ALU.add, r=[xin, cw, acc], w=[acc])
                else:
                    self.act(acc[:, 0:T], xin[:, 0:T], AF.Copy, r=[xin, cw], w=[acc], scale=cw[:, j, ch, 0:1])
                    for k in range(1, 5):
                        tm_ = tmps.get()
                        self.act(tm_[:, 0:T], xin[:, k:k + T], AF.Copy, r=[xin, cw], w=[tm_], scale=cw[:, j, ch, k:k + 1])
                        self.tt('pool', acc[:, 0:T], acc[:, 0:T], tm_[:, 0:T], ALU.add, r=[acc, tm_], w=[acc])
                self.act(acc[:, 0:T], acc[:, 0:T], AF.Silu, r=[acc], w=[acc])
                if ch < 16:
                    scale = (128.0 ** -0.5) if ch < 8 else 1.0
                    for b0 in range(0, T, 512):
                        bw = min(512, T - b0)
                        sq = sqs.get()
                        self.tt('pool', sq[:, 0:bw], acc[:, b0:b0 + bw], acc[:, b0:b0 + bw], ALU.mult, r=[acc], w=[sq])
                        p = self.pnext()
                        self.mm(p[:, 0:bw], self.cst.ap[:, 1, :], sq[:, 0:bw], True, True, r=[self.cst, sq], w=[p])
                        rs = rss.get()
                        self.act(rs[:, 0:bw], p[:, 0:bw], AF.Sqrt, r=[p, self.epsb], w=[rs], bias=self.epsb[:, 0:1])
                        self.recip(rs[:, 0:bw], rs[:, 0:bw], r=[rs], w=[rs])
                        self.stt('dve', acc[:, b0:b0 + bw], acc[:, b0:b0 + bw], scale, rs[:, 0:bw], ALU.mult, ALU.mult, r=[acc, rs], w=[acc])
                    self.ld(self.qkT_v[:, ch, tok0:tok0 + T], acc[:, 0:T], r=[acc], pw=[self.Oout], eng='pool')
                if ch >= 8:
                    dst = self.ktok if ch < 16 else self.vtok
                    c0 = (ch - 8) * 128 if ch < 16 else (ch - 16) * 128
                    for g0 in range(0, T, 512):
                        ng = min(4, (T - g0) // 128)
                        p = self.pnext()
                        for k in range(ng):
                            self.tr_(p[:, k * 128:(k + 1) * 128], acc[:, g0 + k * 128:g0 + (k + 1) * 128], r=[acc], **wr(p, k == 0))
                        to = tos.get()
                        self.cp('act', to[:, 0:ng, :], p[:, 0:ng * 128].rearrange("p (a b) -> p a b", a=ng), r=[p], w=[to])
                        self.ld(dst[tok0 + g0:tok0 + g0 + ng * 128, c0:c0 + 128].rearrange("(n p) e -> p n e", p=128), to[:, 0:ng, :], r=[to], pw=[self.Oout], eng='sp')

    def gdn_scan(self, j):
        self.areset()
        cst = self.cst
        Tri = [cst.ap[0:64, 2, 0:64], cst.ap[0:64, 3, 0:64]]
        Str = [cst.ap[0:64, 4, 0:64], cst.ap[0:64, 5, 0:64]]
        Sm = [cst.ap[0:64, 5, 0:64], cst.ap[0:64, 4, 0:64]]
        I64 = cst.ap[0:64, 0, 0:64]
        ones64 = cst.ap[0:64, 1, 0:64]
        ones64w = cst.ap[0:64, 1, 0:128]
        Sst = [[self.take([128, 128]) for h in range(8)] for d in range(2)]
        qks = self.take([128, 16, 64], 4)
        kts = self.take([64, 8, 128], 4)
        vts = self.take([64, 8, 128], 4)
        gts = self.take([64, 32], 4)
        gps = self.take([64, 48], 4)
        gls = self.take([128, 8], 4)
        NB = 18
        tls = self.take([64, 64], 6)
        Ers = self.take([64, 64], 6)
        Eis = self.take([64, 64], 6)
        Ess = self.take([64, 64], 6)
        qkTs = self.take([64, 64], NB)
        Xs = self.take([64, 64], 40)
        XTs = self.take([64, 64], 40)
        Ps = self.take([64, 64], NB)
        Us = self.take([64, 128], NB)
        kegs = self.take([64, 128], 6)
        WTs = self.take([128, 64], NB)
        kdecs = self.take([64, 128], NB)
        vns = self.take([64, 128], NB)
        o2s = self.take([64, 128], 6)
        houts = self.take([64, 8, 128], 4)
        seqs = [(p * TP, TP // 64, p) for p in range(NP)] + [(NP * TP, TS // 64, -1)]
        seqs = seqs[self.opts.get('gdn_seq0', 0):self.opts.get('gdn_seq1', 5)]
        for (tok0, nch, pidx) in seqs:
            for d in range(2):
                for h in range(8):
                    S_ = Sst[d][h]
                    if pidx >= 0:
                        self.memset('pool', S_[:], 0.0, w=[S_])
                    else:
                        self.ld(S_[:], self.st_S[j, d, h], r=[self.Win], w=[S_], eng='sp' if h % 2 else 'pool')
            for step in range(nch):
                ctx = []
                for d in range(2):
                    c = step if d == 0 else nch - 1 - step
                    t0 = tok0 + c * 64
                    qk = qks.get()
                    self.ld(qk[:], self.qkT_v[:, 0:16, t0:t0 + 64], w=[qk])
                    kt = kts.get()
                    self.ld(kt[:], self.ktok[t0:t0 + 64, 0:1024].rearrange("t (h e) -> t h e", h=8), w=[kt], eng='pool')
                    vt = vts.get()
                    self.ld(vt[:], self.vtok[t0:t0 + 64, :].rearrange("t (h e) -> t h e", h=8), w=[vt])
                    gt = gts.get()
                    self.ld(gt[:], self.gtok[t0:t0 + 64, :], w=[gt], eng='pool')
                    gp = gps.get()
                    dc = slice(d * 8, d * 8 + 8)
                    self.tt('dve', gp[:, 0:8], gt[:, dc], self.gd_dt[0:64, j, dc], ALU.add, r=[gt, self.gd_dt], w=[gp])
                    self.act(gp[:, 0:8], gp[:, 0:8], AF.Exp, r=[gp], pw=[gp])
                    self.act(gp[:, 0:8], gp[:, 0:8], AF.Ln, r=[gp], pw=[gp], bias=1.0)
                    self.tt('dve', gp[:, 8:16], gp[:, 0:8], self.gd_nea[0:64, j, dc], ALU.mult, r=[gp, self.gd_nea], pw=[gp])
                    self.act(gp[:, 16:24], gt[:, 16 + d * 8:24 + d * 8], AF.Sigmoid, r=[gt], pw=[gp])
                    self.ts('dve', gp[:, 24:32], gp[:, 16:24], -1.0, None, ALU.mult, r=[gp], pw=[gp])
                    la = gp[:, 8:16]
                    pg = self.pns()
                    self.mm(pg[0:64, 0:8], Tri[d], la, True, True, r=[cst, gp], w=[pg])
                    self.mm(pg[0:64, 8:16], Str[d], la, True, True, r=[cst, gp], pw=[pg])
                    self.mm(pg[0:128, 16:24], ones64w, la, True, True, r=[cst, gp], pw=[pg])
                    self.act(gp[:, 32:40], pg[0:64, 0:8], AF.Exp, r=[pg], pw=[gp])
                    self.act(gp[:, 40:48], pg[0:64, 8:16], AF.Exp, r=[pg], pw=[gp])
                    gl = gls.get()
                    self.act(gl[:], pg[0:128, 16:24], AF.Exp, r=[pg], w=[gl])
                    ctx.append((d, t0, qk, kt, vt, gp, gl, houts.get()))
                units = [(cx, h) for cx in ctx for h in range(8)]
                st = {}
                for (cx, h) in units:
                    d, t0, qk, kt, vt, gp, gl, ho = cx
                    kT = qk[:, 8 + h, :]
                    qT = qk[:, h, :]
                    tl = tls.get()
                    self.ts('pool', tl[:], Tri[d], gp[:, 8 + h:9 + h], None, ALU.mult, r=[cst, gp], w=[tl])
                    pD = self.pns()
                    self.mm(pD[0:64, 0:64], Str[d], tl[:], True, True, r=[cst, tl], w=[pD])
                    Er = Ers.get()
                    self.act(Er[:], pD[0:64, 0:64], AF.Exp, r=[pD], w=[Er])
                    Ei = Eis.get()
                    Es = Ess.get()
                    self.tt('pool', Ei[:], Er[:], Tri[d], ALU.mult, r=[Er, cst], w=[Ei])
                    self.tt('pool', Es[:], Er[:], Sm[d], ALU.mult, r=[Er, cst], w=[Es])
                    pKK = self.pns()
                    self.mm(pKK[0:64, 0:64], kT, kT, True, True, r=[qk], w=[pKK])
                    pKQ = self.pns()
                    self.mm(pKQ[0:64, 0:64], kT, qT, True, True, r=[qk], w=[pKQ])
                    qkT = qkTs.get()
                    self.tt('dve', qkT[:], Ei[:], pKQ[0:64, 0:64], ALU.mult, r=[Ei, pKQ], w=[qkT])
                    X = Xs.get()
                    self.stt('dve', X[:], pKK[0:64, 0:64], gp[:, 24 + h:25 + h], Es[:], ALU.mult, ALU.mult, r=[pKK, gp, Es], w=[X])
                    pT = self.pns()
                    self.tr_(pT[0:64, 0:64], X[:], r=[X], w=[pT])
                    XT = XTs.get()
                    self.cp('act', XT[:], pT[0:64, 0:64], r=[pT], w=[XT])
                    P_ = Ps.get()
                    self.tt('pool', P_[:], X[:], I64, ALU.add, r=[X, cst], w=[P_])
                    st[(d, h)] = [X, XT, P_, qkT]
                for jn in range(1, 6):
                    for (cx, h) in units:
                        d = cx[0]
                        X, XT, P_, qkT = st[(d, h)]
                        Xn = None
                        if jn < 5:
                            pX = self.pns()
                            self.mm(pX[0:64, 0:64], XT[:], X[:], True, True, r=[XT, X], w=[pX])
                            Xn = Xs.get()
                            self.cp('dve', Xn[:], pX[0:64, 0:64], r=[pX], w=[Xn])
                        pXT = self.pns()
                        self.mm(pXT[0:64, 0:64], X[:], XT[:], True, True, r=[XT, X], w=[pXT])
                        XnT = XTs.get()
                        self.cp('act', XnT[:], pXT[0:64, 0:64], r=[pXT], w=[XnT])
                        pP = self.pns()
                        self.mm(pP[0:64, 0:64], XnT[:], P_[:], True, True, r=[XnT, P_], w=[pP])
                        self.tt('dve', P_[:], P_[:], pP[0:64, 0:64], ALU.add, r=[P_, pP], w=[P_])
                        st[(d, h)] = [Xn, XnT, P_, qkT]
                for (cx, h) in units:
                    d, t0, qk, kt, vt, gp, gl, ho = cx
                    X, XT, P_, qkT = st[(d, h)]
                    pU = self.pns()
                    self.mm(pU[0:64, 0:128], P_[:], vt[:, h, :], True, True, r=[P_, vt], w=[pU])
                    U = Us.get()
                    self.ts('pool' if False else 'dve', U[:], pU[0:64, 0:128], gp[:, 16 + h:17 + h], None, ALU.mult, r=[pU, gp], w=[U])
                    keg = kegs.get()
                    self.ts('pool', keg[:], kt[:, h, :], gp[:, 32 + h:33 + h], None, ALU.mult, r=[kt, gp], w=[keg])
                    pW = self.pns()
                    self.mm(pW[0:128, 0:64], keg[:], P_[:], True, True, r=[keg, P_], w=[pW])
                    WT = WTs.get()
                    self.cp('act', WT[:], pW[0:128, 0:64], r=[pW], w=[WT])
                    kdec = kdecs.get()
                    self.ts('pool', kdec[:], kt[:, h, :], gp[:, 40 + h:41 + h], None, ALU.mult, r=[kt, gp], w=[kdec])
                    st[(d, h)] = [U, WT, kdec, qkT]
                pas = {}
                for (cx, h) in units:
                    d = cx[0]
                    U, WT, kdec, qkT = st[(d, h)]
                    pa = self.pns()
                    self.mm(pa[0:64, 0:128], WT[:], Sst[d][h][:], True, True, r=[WT, Sst[d][h]], w=[pa])
                    vn = vns.get()
                    self.stt('dve', vn[:], pa[0:64, 0:128], cx[5][:, 24 + h:25 + h], U[:], ALU.mult, ALU.add, r=[pa, cx[5], U], w=[vn])
                    pas[(d, h)] = vn
                for (cx, h) in units:
                    d, t0, qk, kt, vt, gp, gl, ho = cx
                    U, WT, kdec, qkT = st[(d, h)]
                    vn = pas[(d, h)]
                    S_ = Sst[d][h]
                    po = self.pns()
                    self.mm(po[0:64, 0:128], qk[:, h, :], S_[:], True, True, r=[qk, S_], w=[po])
                    po2 = self.pns()
                    self.mm(po2[0:64, 0:128], qkT[:], vn[:], True, True, r=[qkT, vn], w=[po2])
                    pS = self.pns()
                    self.mm(pS[0:128, 0:128], kdec[:], vn[:], True, True, r=[kdec, vn], w=[pS])
                    o2 = o2s.get()
                    self.cp('act', o2[:], po2[0:64, 0:128], r=[po2], w=[o2])
                    self.stt('dve', ho[:, h, :], po[0:64, 0:128], gp[:, 32 + h:33 + h], o2[:], ALU.mult, ALU.add, r=[po, gp, o2], **wr(ho, h == 0))
                    self.stt('dve', S_[:], S_[:], gl[:, h:h + 1], pS[0:128, 0:128], ALU.mult, ALU.add, r=[S_, gl, pS], w=[S_])
                for cx in ctx:
                    d, t0, qk, kt, vt, gp, gl, ho = cx
                    self.ld(self.hdir[d][t0:t0 + 64, :].rearrange("t (h e) -> t h e", h=8), ho[:], r=[ho], pw=[self.Oout], eng='pool')
            if pidx >= 0:
                for d in range(2):
                    for h in range(8):
                        self.ld(self.newS[pidx, j, d, h], Sst[d][h][:], r=[Sst[d][h]], pw=[self.Oout], eng='sp' if h % 2 else 'pool')

    def diffattn(self, i):
        (self.proj_stage16 if self.bf else self.proj_stage)(i, self.df_w_in, 3072, fm=[],
                        tm=[(0, 2048, self.ktok), (2048, 1024, self.vtok)])
        self.attn_prep()
        self.attn_core()
        self.mixer_post(i, self.df_w_out, self.df_sg, self.df_sg[:], self.hdir, None)

    def attn_prep(self):
        self.areset()
        xs = self.take([128, 32, 64], 2)
        sqs = self.take([128, 32, 64], 1)
        sss = self.take([128, 64], 2)
        css = self.take([128, 64], 2)
        r1 = self.take([128, 32, 2, 16], 1)
        r2 = self.take([128, 32, 2, 16], 1)
        r3 = self.take([128, 32, 2, 16], 1)
        xr = self.take([128, 32, 64], 2)
        vts = self.take([128, 1024], 2)
        xos = self.take([128, 16, 128], 2)
        cks = self.take([128, 16, 64], 2)
        cko = self.take([128, 8, 128], 2)
        gq = self.df_g
        for t in range(TT // 128):
            t0 = t * 128
            x = xs.get()
            self.ld(x[:], self.ktok[t0:t0 + 128, :].rearrange("t (g d) -> t g d", g=32), r=[self.proj_tk[t]], w=[x])
            sq = sqs.get()
            self.tt('pool', sq[:], x[:], x[:], ALU.mult, r=[x], w=[sq])
            ss = sss.get()
            self.S.op('dve', lambda E, ss=ss, sq=sq: E.tensor_reduce(out=ss[:, 0:32], in_=sq[:], axis=AX.X, op=ALU.add), reads=[sq], writes=[ss])
            self.act(ss[:, 0:32], ss[:, 0:32], AF.Sqrt, r=[ss, self.epsb], w=[ss], scale=1.0 / 64, bias=self.epsb[:, 0:1])
            self.recip(ss[:, 32:64], ss[:, 0:32], r=[ss], pw=[ss])
            self.tt('dve', x[:], x[:], ss[:, 32:64].unsqueeze(2).to_broadcast([128, 32, 64]), ALU.mult, r=[x, ss], w=[x])
            self.tt('pool', x[:, 0:16, :], x[:, 0:16, :], gq[:, 0, :].unsqueeze(1).to_broadcast([128, 16, 64]), ALU.mult, r=[x, gq], w=[x])
            self.tt('dve', x[:, 16:32, :], x[:, 16:32, :], gq[:, 1, :].unsqueeze(1).to_broadcast([128, 16, 64]), ALU.mult, r=[x, gq], w=[x])
            if t0 < NP * TP:
                p, tl = t0 // TP, t0 % TP
                self.ld(self.newk[p, :, :, tl:tl + 128, :].rearrange("h m t d -> t (h m) d"), x[:, 16:32, :], r=[x], pw=[self.Oout], eng='pool')
                vt = vts.get()
                self.ld(vt[:], self.vtok[t0:t0 + 128, :], r=[self.proj_tk[t]], w=[vt])
                self.ld(self.newv[p, :, tl:tl + 128, :].rearrange("h t e -> t h e"), vt[:].rearrange("t (h e) -> t h e", h=8), r=[vt], pw=[self.Oout], eng='pool')
                src = x
            else:
                cs = css.get()
                self.ld(cs[:], self.rope_cs[t0 - NP * TP:t0 - NP * TP + 128, :], r=[self.Win], w=[cs])
                X = x[:].rearrange("t g (a f r) -> t g a f r", a=2, f=2)
                xa = X[:, :, :, 0, :]
                xb_ = X[:, :, :, 1, :]
                cosb = cs[:, 0:32].rearrange("t (a r) -> t a r", a=2).unsqueeze(1).to_broadcast([128, 32, 2, 16])
                sinb = cs[:, 32:64].rearrange("t (a r) -> t a r", a=2).unsqueeze(1).to_broadcast([128, 32, 2, 16])
                o_ = xr.get()
                O = o_[:].rearrange("t g (a f r) -> t g a f r", a=2, f=2)
                a1, a2, a3 = r1.get(), r2.get(), r3.get()
                self.tt('dve', a1[:], xa, cosb, ALU.mult, r=[x, cs], w=[a1])
                self.tt('pool', a2[:], xb_, sinb, ALU.mult, r=[x, cs], w=[a2])
                self.tt('dve', O[:, :, :, 0, :], a1[:], a2[:], ALU.subtract, r=[a1, a2], w=[o_])
                self.tt('pool', a3[:], xa, sinb, ALU.mult, r=[x, cs], w=[a3])
                self.tt('dve', a1[:], xb_, cosb, ALU.mult, r=[x, cs], w=[a1])
                self.tt('pool', O[:, :, :, 1, :], a3[:], a1[:], ALU.add, r=[a3, a1], pw=[o_])
                src = o_
            xo = xos.get()
            for g in range(4):
                pp = self.pnext()
                for k in range(4):
                    ch = g * 4 + k
                    self.tr_(pp[:, k * 128:(k + 1) * 128], src[:, 2 * ch:2 * ch + 2, :].rearrange("t a d -> t (a d)"), r=[src], **wr(pp, k == 0))
                self.cp('act' if g % 2 else 'dve', xo[:, g * 4:(g + 1) * 4, :], pp[:, :].rearrange("p (a b) -> p a b", a=4), r=[pp], **wr(xo, g == 0))
            self.ld(self.qkT_v[:, 0:16, t0:t0 + 128], xo[:], r=[xo], w=[self.prep_tk[t]], eng='pool')
        for kt in range(2):
            ck = cks.get()
            self.ld(ck[:], self.ctx_k[:, :, kt * 128:(kt + 1) * 128, :].rearrange("h m t d -> t (h m) d"), r=[self.Win], w=[ck])
            co = cko.get()
            for g in range(2):
                pp = self.pnext()
                for k in range(4):
                    ch = g * 4 + k
                    self.tr_(pp[:, k * 128:(k + 1) * 128], ck[:, 2 * ch:2 * ch + 2, :].rearrange("t a d -> t (a d)"), r=[ck], **wr(pp, k == 0))
                self.cp('act' if g % 2 else 'dve', co[:, g * 4:(g + 1) * 4, :], pp[:, :].rearrange("p (a b) -> p a b", a=4), r=[pp], **wr(co, g == 0))
            self.ld(self.ctxkT.rearrange("c p t -> p c t")[:, :, kt * 128:(kt + 1) * 128], co[:], r=[co], **wr(self.ctx_tk, kt == 0), eng='pool')

    def attn_core(self):
        self.areset()
        NKT = (TS + 256) // 128
        bf = self.bf
        MD = BF16 if bf else F32
        qTs = self.take([128, TS], 2, MD)
        kTs = self.take([128, TS + 256], 2, MD)
        V1s = self.take([128, NKT, 129], 2, MD)
        for V1 in V1s.t:
            self.memset('pool', V1[:, :, 128:129], 1.0, pw=[V1])
        PTs = self.take([128, NKT, 512], 2, MD)
        if bf:
            q32 = self.take([128, TS], 1)
            k32 = self.take([128, TS + 256], 1)
            v32 = self.take([128, NKT, 128], 1)
        obs = self.take([128, 4, 128], 2)
        rvs = self.take([128, 2], 4)
        seqs = [(p * TP, TP, False) for p in range(NP)] + [(NP * TP, TS, True)]
        for (tok0, T, is_s) in seqs:
            nk = T + (256 if is_s else 0)
            nkt = nk // 128
            QB = min(512, T)
            tks = self.prep_tk[tok0 // 128:(tok0 + T) // 128]
            ptk = self.proj_tk[tok0 // 128:(tok0 + T) // 128]
            for h in range(8):
                qT = qTs.get()
                kT = kTs.get()
                V1 = V1s.get()
                if bf:
                    qd, kd, vd = q32.get(), k32.get(), v32.get()
                else:
                    qd, kd, vd = qT, kT, V1
                self.ld(qd[:, 0:T], self.qkT_v[:, h, tok0:tok0 + T], r=tks, w=[qd])
                self.ld(kd[:, 0:T], self.qkT_v[:, 8 + h, tok0:tok0 + T], r=tks, w=[kd], eng='pool')
                self.ld(vd[:, 0:T // 128, 0:128], self.vtok[tok0:tok0 + T, h * 128:(h + 1) * 128].rearrange("(n p) e -> p n e", p=128), r=ptk, **wr(vd, bf))
                if is_s:
                    self.ld(kd[:, T:T + 256], self.ctxkT[h], r=[self.ctx_tk], pw=[kd], eng='pool')
                    self.ld(vd[:, T // 128:nkt, 0:128], self.ctx_v[h].rearrange("(n p) e -> p n e", p=128), r=[self.Win], pw=[vd])
                if bf:
                    self.cp('pool', qT[:, 0:T], qd[:, 0:T], r=[qd], w=[qT])
                    self.cp('pool', kT[:, 0:nk], kd[:, 0:nk], r=[kd], w=[kT])
                    self.cp('dve', V1[:, 0:nkt, 0:128], vd[:, 0:nkt, :], r=[vd], pw=[V1])
                for qb in range(T // QB):
                    ob = obs.get()
                    for m in range(2):
                        PT = PTs.get()
                        for kt in range(nkt):
                            pS = self.pnext()
                            self.mm(pS[:, 0:QB], kT[m * 64:(m + 1) * 64, kt * 128:(kt + 1) * 128], qT[m * 64:(m + 1) * 64, qb * QB:(qb + 1) * QB],
                                    True, True, r=[kT, qT], w=[pS])
                            self.act(PT[:, kt, 0:QB], pS[:, 0:QB], AF.Exp, r=[pS], **wr(PT, kt == 0), scale=0.125)
                        for qs in range(QB // 128):
                            pO = self.pnext()
                            for kt in range(nkt):
                                self.mm(pO[:, 0:129], PT[:, kt, qs * 128:(qs + 1) * 128], V1[:, kt, :], kt == 0, kt == nkt - 1, r=[PT, V1], **wr(pO, kt == 0))
                            rv = rvs.get()
                            self.recip(rv[:, 0:1], pO[:, 128:129], r=[pO], w=[rv])
                            if m == 0:
                                self.ts('dve', ob[:, qs, :], pO[:, 0:128], rv[:, 0:1], None, ALU.mult, r=[pO, rv], **wr(ob, qs == 0))
                            else:
                                self.tt('dve', rv[:, 1:2], rv[:, 0:1], self.nlam[:, 3:4], ALU.mult, r=[rv, self.nlam], pw=[rv])
                                self.stt('dve', ob[:, qs, :], pO[:, 0:128], rv[:, 1:2], ob[:, qs, :], ALU.mult, ALU.add, r=[pO, rv, ob], pw=[ob])
                    q0 = tok0 + qb * QB
                    nq = QB // 128
                    htk = self.hdir_tk[0][q0 // 64:(q0 + QB) // 64]
                    self.ld(self.hdir[0][q0:q0 + QB, h * 128:(h + 1) * 128].rearrange("(n p) e -> p n e", p=128), ob[:, 0:nq, :], r=[ob], pw=htk, eng='pool')

    def mixer_post(self, i, w_out, ng_tk, ng_bc, hdir, gate):
        self.areset()
        NT = 512
        if self.bf:
            W = self.take([128, 8, D], None, BF16)
            self.wstage = self.take([128, 8, 512], 2)
            self.load_w16(W, w_out.rearrange("(kc p) n -> p kc n", p=128), D)
        else:
            W = self.take([128, 8, D])
            self.ld(W[:], w_out.rearrange("(kc p) n -> p kc n", p=128), r=[self.Win], w=[W])
        hfs = self.take([128, 8, 128], 2)
        hbs = self.take([128, 8, 128], 2)
        ogs = self.take([128, 8, 128], 2)
        sqs = self.take([128, 8, 128], 2)
        sss = self.take([128, 16], 2)
        yTs = self.take([128, 8, NT], 2, BF16 if self.bf else F32)
        xbs = self.take([128, 8, NT], 2)
        for blk in range(TT // NT):
            c = 0 if blk < (NP * TP) // NT else 1
            tks = self.xT_tk[blk * 4:(blk + 1) * 4]
            xb = xbs.get()
            self.ld(xb[:], self.xT_v[:, :, blk * NT:(blk + 1) * NT], r=tks, w=[xb], eng='pool')
            yT = yTs.get()
            for q in range(4):
                t0 = blk * NT + q * 128
                hf = hfs.get()
                hb = hbs.get()
                og = ogs.get()
                self.ld(hf[:], hdir[0][t0:t0 + 128, :].rearrange("t (h e) -> t h e", h=8), r=self.hdir_tk[0][t0 // 64:t0 // 64 + 2], w=[hf])
                if gate is not None:
                    self.ld(hb[:], hdir[1][t0:t0 + 128, :].rearrange("t (h e) -> t h e", h=8), r=self.hdir_tk[1][t0 // 64:t0 // 64 + 2], w=[hb], eng='pool')
                    self.ld(og[:], self.otok[t0:t0 + 128, :].rearrange("t (h e) -> t h e", h=8), r=[self.proj_tk[t0 // 128]], w=[og])
                    self.tt('dve', hf[:], hf[:], hb[:], ALU.add, r=[hf, hb], w=[hf])
                sq = sqs.get()
                self.tt('pool', sq[:], hf[:], hf[:], ALU.mult, r=[hf], w=[sq])
                ss = sss.get()
                self.S.op('dve', lambda E, ss=ss, sq=sq: E.tensor_reduce(out=ss[:, 0:8], in_=sq[:], axis=AX.X, op=ALU.add), reads=[sq], writes=[ss])
                self.act(ss[:, 0:8], ss[:, 0:8], AF.Sqrt, r=[ss, self.epsb], w=[ss], scale=1.0 / 128, bias=self.epsb[:, 0:1])
                self.recip(ss[:, 8:16], ss[:, 0:8], r=[ss], pw=[ss])
                ngb = ng_bc.unsqueeze(1).to_broadcast([128, 8, 128])
                if gate == 'sigmoid':
                    self.act(og[:], og[:], AF.Sigmoid, r=[og], w=[og])
                    self.tt('pool', og[:], og[:], ngb, ALU.mult, r=[og, ng_tk], w=[og])
                elif gate == 'silu':
                    self.act(og[:], og[:], AF.Silu, r=[og], w=[og])
                    self.tt('pool', og[:], og[:], ngb, ALU.mult, r=[og, ng_tk], w=[og])
                else:
                    self.cp('pool', og[:], ngb, r=[ng_tk], w=[og])
                self.tt('dve', hf[:], hf[:], ss[:, 8:16].unsqueeze(2).to_broadcast([128, 8, 128]), ALU.mult, r=[hf, ss], w=[hf])
                self.tt('dve', hf[:], hf[:], og[:], ALU.mult, r=[hf, og], w=[hf])
                for hh in range(2):
                    p = self.pnext()
                    for k in range(4):
                        self.tr_(p[:, k * 128:(k + 1) * 128], hf[:, hh * 4 + k, :], r=[hf], **wr(p, k == 0))
                    dst = yT[:, hh * 4:(hh + 1) * 4, q * 128:(q + 1) * 128]
                    src = p[:, :].rearrange("p (a b) -> p a b", a=4)
                    self.cp('act' if hh else 'dve', dst, src, r=[p], **wr(yT, q == 0 and hh == 0))
            for oc in range(8):
                p = self.pnext()
                for kc in range(8):
                    self.mm(p[:, :], W[:, kc, oc * 128:(oc + 1) * 128], yT[:, kc, :], kc == 0, kc == 7, r=[W, yT], **wr(p, kc == 0))
                self.stt('dve', xb[:, oc, :], p[:, :], self.mod[:, 16 + oc, c:c + 1], xb[:, oc, :], ALU.mult, ALU.add,
                         r=[p, self.mod, xb], pw=[xb])
            for q in range(4):
                self.ld(self.xT_v[:, :, blk * NT + q * 128: blk * NT + (q + 1) * 128], xb[:, :, q * 128:(q + 1) * 128],
                        r=[xb], w=[tks[q]], eng='pool')

    def tr_(self, out, in_, r=(), w=(), pw=()):
        n = in_.shape[0]
        idn = self.cst.ap[0:n, 0, 0:n]
        self.S.op('pe', lambda E: E.transpose(out, in_, idn), reads=list(r) + [self.cst], writes=w, pw=pw)

    def stage_in(self):
        self.areset()
        xin = self.take([128, D], 2)
        xo = self.take([128, 8, 128], 2)
        for t in range(TT // 128):
            a = xin.get()
            self.ld(a[:], self.x_tok[t * 128:(t + 1) * 128, :], r=[self.Xtok], w=[a])
            b = xo.get()
            for h in range(2):
                p = self.pnext()
                for k in range(4):
                    kc = h * 4 + k
                    self.tr_(p[:, k * 128:(k + 1) * 128], a[:, kc * 128:(kc + 1) * 128], r=[a], w=[p] if k == 0 else (), pw=() if k == 0 else [p])
                dst = b[:, h * 4:(h + 1) * 4, :]
                src = p[:, :].rearrange("p (a b) -> p a b", a=4)
                if h == 0:
                    self.cp('dve', dst, src, r=[p], w=[b])
                else:
                    self.cp('act', dst, src, r=[p], pw=[b])
            self.ld(self.xT_v[:, :, t * 128:(t + 1) * 128], b[:], r=[b], w=[self.xT_tk[t]], eng='pool')

    def stage_out(self):
        self.areset()
        xi = self.take([128, 8, 128], 2)
        yo = self.take([128, D], 2)
        for t in range(TT // 128):
            a = xi.get()
            self.ld(a[:], self.xT_v[:, :, t * 128:(t + 1) * 128], r=[self.xT_tk[t]], w=[a])
            b = yo.get()
            for h in range(2):
                p = self.pnext()
                for k in range(4):
                    kc = h * 4 + k
                    self.tr_(p[:, k * 128:(k + 1) * 128], a[:, kc, :], r=[a], w=[p] if k == 0 else (), pw=() if k == 0 else [p])
                if h == 0:
                    self.cp('dve', b[:, 0:512], p[:, :], r=[p], w=[b])
                else:
                    self.cp('act', b[:, 512:1024], p[:, :], r=[p], pw=[b])
            self.ld(self.y_tok[t * 128:(t + 1) * 128, :], b[:], r=[b], w=[self.Ytok], eng='pool')

    def stage_mod(self, i):
        self.areset()
        wt = self.take([128, 8, 512], 2)
        wv = self.ada_w[i].rearrange("(kc p) n -> p kc n", p=128)
        mp = self.pnext()
        for n in range(12):
            w = wt.get()
            self.ld(w[:], wv[:, :, n * 512:(n + 1) * 512], r=[self.Win], w=[w])
            for jj in range(4):
                j = n * 4 + jj
                for kc in range(8):
                    self.mm(mp[:, 2 * j:2 * j + 2], w[:, kc, jj * 128:(jj + 1) * 128], self.sc[:, kc, :], kc == 0, kc == 7,
                            r=[w, self.sc], **wr(mp, j == 0 and kc == 0))
        mpv = mp[:, 0:96].rearrange("p (j c) -> p j c", c=2)
        for c in range(2):
            self.tt('dve', self.mod[:, :, c], mpv[:, :, c], self.adab[:, i, :], ALU.add, r=[mp, self.adab],
                    w=[self.mod] if c == 0 else (), pw=() if c == 0 else [self.mod])
        for wi in range(2):
            sj = 8 + 24 * wi
            for c in range(2):
                first = (wi == 0 and c == 0)
                self.stt('dve', self.modA[:, wi, :, c], self.mod[:, sj:sj + 8, c], 1.0, self.normg[:, i, wi, :], ALU.add, ALU.mult,
                         r=[self.mod, self.normg], w=[self.modA] if first else (), pw=() if first else [self.modA])

    def norm_mod(self, xb, hb, wi, c, nt, sq=None):
        sj = 24 * wi
        self.act(hb[:, :, :], xb[:, :, :], AF.Square, r=[xb], w=[hb])
        p = self.pnext()
        for kc in range(8):
            self.mm(p[:, 0:nt], self.cst.ap[:, 1, :], hb[:, kc, :], kc == 0, kc == 7, r=[self.cst, hb], **wr(p, kc == 0))
        rs = self.rstd.get()
        self.act(rs[:, 0:nt], p[:, 0:nt], AF.Sqrt, r=[p, self.epsb], w=[rs], scale=1.0 / D, bias=self.epsb[:, 0:1])
        self.recip(rs[:, 0:nt], rs[:, 0:nt], r=[rs], w=[rs])
        for kc in range(8):
            self.tt('dve' if kc % 2 == 0 else 'pool', hb[:, kc, :], xb[:, kc, :], rs[:, 0:nt], ALU.mult, r=[xb, rs],
                    w=[hb] if kc == 0 else (), pw=() if kc == 0 else [hb])
        for kc in range(8):
            self.act(hb[:, kc, :], hb[:, kc, :], AF.Identity, r=[hb, self.modA, self.mod], pw=[hb],
                     scale=self.modA[:, wi, kc, c:c + 1], bias=self.mod[:, sj + kc, c:c + 1])

    def norm_mod2(self, xtk, xap, htk, hap, tmp, wi, c, nt, first):
        sj = 24 * wi
        self.act(tmp[:, :, 0:nt], xap, AF.Square, r=[xtk], w=[tmp])
        p = self.pnext()
        for kc in range(8):
            self.mm(p[:, 0:nt], self.cst.ap[:, 1, :], tmp[:, kc, 0:nt], kc == 0, kc == 7, r=[self.cst, tmp], **wr(p, kc == 0))
        rs = self.rstd.get()
        self.act(rs[:, 0:nt], p[:, 0:nt], AF.Sqrt, r=[p, self.epsb], w=[rs], scale=1.0 / D, bias=self.epsb[:, 0:1])
        self.recip(rs[:, 0:nt], rs[:, 0:nt], r=[rs], w=[rs])
        for kc in range(8):
            self.tt('dve' if kc % 2 == 0 else 'pool', tmp[:, kc, 0:nt], xap[:, kc, :], rs[:, 0:nt], ALU.mult, r=[xtk, rs], **wr(tmp, kc == 0))
        for kc in range(8):
            self.act(hap[:, kc, :], tmp[:, kc, 0:nt], AF.Identity, r=[tmp, self.modA, self.mod], **wr(htk, first and kc == 0),
                     scale=self.modA[:, wi, kc, c:c + 1], bias=self.mod[:, sj + kc, c:c + 1])

    def stage_ffn16(self, i):
        self.areset()
        SB = 1024
        NH = SB // 512
        xbs = self.take([128, 8, SB], 1)
        hbs = self.take([128, 8, SB], 1, BF16)
        sq = self.take([128, 8, 512])
        self.rstd = self.take([128, 512], 2)
        acts = self.take([128, 22, SB], 1, BF16)
        wst = self.take([128, 8, 2, 128], 3)
        w16 = self.take([128, 8, 2, 128], 3, BF16)
        wost = self.take([128, 22, 128], 2)
        wo16 = self.take([128, 22, 128], 2, BF16)
        sg = self.take([128, 512], 2)
        wiv = self.ffn_w_in[i].rearrange("(kc p) n -> p kc n", p=128)
        wov = self.ffn_w_out[i].rearrange("(kc p) n -> p kc n", p=128)
        for sb in range(TT // SB):
            c = 0 if sb * SB < NP * TP else 1
            tks = self.xT_tk[sb * 8:(sb + 1) * 8]
            xb = xbs.get()
            self.ld(xb[:], self.xT_v[:, :, sb * SB:(sb + 1) * SB], r=tks, w=[xb], eng='pool')
            hb = hbs.get()
            for hf in range(NH):
                hs = slice(hf * 512, (hf + 1) * 512)
                self.norm_mod2(xb, xb[:, :, hs], hb, hb[:, :, hs], sq, 1, c, 512, hf == 0)
            at = acts.get()
            for j in range(22):
                ws = wst.get()
                self.ld(ws[:, :, 0, :], wiv[:, :, j * 128:(j + 1) * 128], r=[self.Win], w=[ws])
                self.ld(ws[:, :, 1, :], wiv[:, :, DFF + j * 128:DFF + (j + 1) * 128], r=[self.Win], pw=[ws])
                w = w16.get()
                self.cp('pool', w[:], ws[:], r=[ws], w=[w])
                for hf in range(NH):
                    hs = slice(hf * 512, (hf + 1) * 512)
                    pg = self.pnext()
                    pu = self.pnext()
                    for kc in range(8):
                        self.mm(pg[:, :], w[:, kc, 0, :], hb[:, kc, hs], kc == 0, kc == 7, r=[w, hb], **wr(pg, kc == 0))
                    for kc in range(8):
                        self.mm(pu[:, :], w[:, kc, 1, :], hb[:, kc, hs], kc == 0, kc == 7, r=[w, hb], **wr(pu, kc == 0))
                    s_ = sg.get()
                    self.act(s_[:], pg[:, :], AF.Silu, r=[pg], w=[s_])
                    self.tt('dve', at[:, j, hs], s_[:], pu[:, :], ALU.mult, r=[s_, pu], **wr(at, j == 0 and hf == 0))
            for oc in range(8):
                ws = wost.get()
                self.ld(ws[:], wov[:, :, oc * 128:(oc + 1) * 128], r=[self.Win], w=[ws])
                w = wo16.get()
                self.cp('pool', w[:], ws[:], r=[ws], w=[w])
                for hf in range(NH):
                    hs = slice(hf * 512, (hf + 1) * 512)
                    p = self.pnext()
                    for k2 in range(22):
                        self.mm(p[:, :], w[:, k2, :], at[:, k2, hs], k2 == 0, k2 == 21, r=[w, at], **wr(p, k2 == 0))
                    self.stt('dve', xb[:, oc, hs], p[:, :], self.mod[:, 40 + oc, c:c + 1], xb[:, oc, hs], ALU.mult, ALU.add,
                             r=[p, self.mod, xb], pw=[xb])
            for q in range(SB // 128):
                self.ld(self.xT_v[:, :, sb * SB + q * 128: sb * SB + (q + 1) * 128], xb[:, :, q * 128:(q + 1) * 128],
                        r=[xb], w=[tks[q]], eng='pool')

    def stage_ffn(self, i):
        self.areset()
        NT = 512
        xbs = self.take([128, 8, NT], 2)
        hbs = self.take([128, 8, NT], 1)
        self.rstd = self.take([128, NT], 2)
        acts = self.take([128, 22, NT], 1)
        sg = self.take([128, NT], 2)
        wins = self.take([128, 8, 2, 128], 3)
        wouts = self.take([128, 22, 128], 2)
        wiv = self.ffn_w_in[i].rearrange("(kc p) n -> p kc n", p=128)
        wov = self.ffn_w_out[i].rearrange("(kc p) n -> p kc n", p=128)
        for blk in range(TT // NT):
            c = 0 if blk < (NP * TP) // NT else 1
            tks = self.xT_tk[blk * 4:(blk + 1) * 4]
            xb = xbs.get()
            self.ld(xb[:], self.xT_v[:, :, blk * NT:(blk + 1) * NT], r=tks, w=[xb], eng='pool')
            hb = hbs.get()
            self.norm_mod(xb, hb, 1, c, NT)
            at = acts.get()
            for j in range(22):
                w = wins.get()
                self.ld(w[:, :, 0, :], wiv[:, :, j * 128:(j + 1) * 128], r=[self.Win], w=[w])
                self.ld(w[:, :, 1, :], wiv[:, :, DFF + j * 128:DFF + (j + 1) * 128], r=[self.Win], pw=[w])
                pg = self.pnext()
                pu = self.pnext()
                for kc in range(8):
                    self.mm(pg[:, :], w[:, kc, 0, :], hb[:, kc, :], kc == 0, kc == 7, r=[w, hb], **wr(pg, kc == 0))
                for kc in range(8):
                    self.mm(pu[:, :], w[:, kc, 1, :], hb[:, kc, :], kc == 0, kc == 7, r=[w, hb], **wr(pu, kc == 0))
                s = sg.get()
                self.act(s[:], pg[:, :], AF.Silu, r=[pg], w=[s])
                self.tt('dve', at[:, j, :], s[:], pu[:, :], ALU.mult, r=[s, pu], w=[at] if j == 0 else (), pw=() if j == 0 else [at])
            for oc in range(8):
                w = wouts.get()
                self.ld(w[:], wov[:, :, oc * 128:(oc + 1) * 128], r=[self.Win], w=[w])
                p = self.pnext()
                for k2 in range(22):
                    self.mm(p[:, :], w[:, k2, :], at[:, k2, :], k2 == 0, k2 == 21, r=[w, at], **wr(p, k2 == 0))
                self.stt('dve', xb[:, oc, :], p[:, :], self.mod[:, 40 + oc, c:c + 1], xb[:, oc, :], ALU.mult, ALU.add,
                         r=[p, self.mod, xb], pw=[xb])
            for q in range(4):
                self.ld(self.xT_v[:, :, blk * NT + q * 128: blk * NT + (q + 1) * 128], xb[:, :, q * 128:(q + 1) * 128],
                        r=[xb], w=[tks[q]], eng='pool')


def host_consts():
    c = np.zeros((128, 8, 128), np.float32)
    c[:, 0, :] = np.eye(128)
    c[:, 1, :] = 1.0
    k = np.arange(128)[:, None]
    t = np.arange(128)[None, :]
    c[:, 2, :] = (k <= t)
    c[:, 3, :] = (k >= t)
    c[:, 4, :] = (k > t)
    c[:, 5, :] = (k < t)
    return c.reshape(128, 1024)


def rope_tables():
    rows = TS // 64
    row = np.broadcast_to(np.arange(rows)[:, None], (rows, 64)).reshape(-1)
    col = np.broadcast_to(np.arange(64)[None, :], (rows, 64)).reshape(-1)
    inv = (np.float32(10000.0) ** (-np.arange(16, dtype=np.float32) / np.float32(16))).astype(np.float32)
    ang = np.stack([row, col], axis=-1).astype(np.float32)[:, :, None] * inv
    return np.concatenate([np.cos(ang).reshape(TS, 32), np.sin(ang).reshape(TS, 32)], axis=1).astype(np.float32)


_CACHE = {}


def kernel(**inp):
    opts = inp.pop('_opts', {})
    key = repr(sorted(opts.items()))
    if key not in _CACHE:
        P = Prog(opts)
        P.build()
        _CACHE[key] = P
    P = _CACHE[key]
    f = lambda a: np.ascontiguousarray(np.asarray(a, dtype=np.float32))
    xp = f(inp['x_prompt'])
    xs = f(inp['x_sample'])
    c = f(inp['c'])
    c_ctx = f(inp['c_ctx'])
    ada_b = f(inp['ada_b'])
    norm_g = f(inp['norm_g'])
    shared = {
        'consts': host_consts(),
        'ada_w': f(inp['ada_w']),
        'ada_bT': f(ada_b.reshape(4, 48, 128).transpose(2, 0, 1)),
        'normgT': f(norm_g.reshape(4, 2, 8, 128).transpose(3, 0, 1, 2)),
        'ffn_w_in': f(inp['ffn_w_in']),
        'ffn_w_out': f(inp['ffn_w_out']),
        'mlstm_w_in': f(inp['mlstm_w_in'][0]),
        'mlstm_w_out': f(inp['mlstm_w_out'][0]),
        'mlstm_gate_b': f(inp['mlstm_gate_b'].reshape(1, 32)),
        'mlstm_norm_g': f(inp['mlstm_norm_g'].reshape(1, 128)),
    }
    shared.update({
        'diff_w_in': f(inp['diff_w_in'][0]),
        'diff_w_out': f(inp['diff_w_out'][0]),
        'diff_qkg': f(np.concatenate([inp['diff_q_norm_g'][0], inp['diff_k_norm_g'][0]]).reshape(1, 128)),
        'diff_lambda': f(inp['diff_lambda'][0].reshape(1, 256)),
        'diff_subln_g': f(inp['diff_subln_g'][0].reshape(1, 128)),
        'rope_cs': rope_tables(),
    })
    cw = f(inp['gdn_conv_w'])
    shared.update({
        'gdn_w_in': f(inp['gdn_w_in']),
        'gdn_w_out': f(inp['gdn_w_out']),
        'gdn_convT': f(cw.reshape(2, 5, 24, 128).transpose(3, 0, 2, 1)),
        'gdn_a_log': f(inp['gdn_a_log'].reshape(2, 1, 16)),
        'gdn_dt_bias': f(inp['gdn_dt_bias'].reshape(2, 1, 16)),
        'gdn_norm_g': f(inp['gdn_norm_g'].reshape(2, 1, 128)),
    })
    stS = f(inp['state_delta'])
    ck = f(inp['cache_diff_k'])
    cv = f(inp['cache_diff_v'])
    stC = f(inp['state_mlstm_C'])
    stn = f(inp['state_mlstm_n'])
    stm = f(inp['state_mlstm_m'])
    in_maps = []
    for k in range(NCORE):
        m = dict(shared)
        m['x_tok'] = f(np.concatenate([xp[NP * k:NP * (k + 1)].reshape(NP * TP, D), xs[k]], axis=0))
        cond = np.stack([c_ctx, c[k]], axis=-1)
        m['condT'] = f(cond.reshape(8, 128, 2).transpose(1, 0, 2))
        m['st_S'] = f(stS[k])
        m['ctx_k'] = f(ck[k, 0])
        m['ctx_v'] = f(cv[k, 0])
        m['st_C'] = f(stC[k, 0])
        m['st_n'] = f(stn[k, 0].reshape(2, 8, 64, 1))
        m['st_m'] = f(stm[k, 0].reshape(1, 16))
        in_maps.append({n: m[n] for n in P.din})
    res = run_bass_kernel_spmd(P.nc, in_maps, core_ids=list(range(NCORE)))
    R = res.results
    y = np.stack([r['y_tok'] for r in R])
    y_prompt = y[:, :NP * TP].reshape(NCORE * NP, TP, D)
    y_sample = y[:, NP * TP:]
    outs = [y_prompt, y_sample]
    if 'newS' in P.dout:
        outs.append(np.stack([r['newS'] for r in R]).reshape(NCORE * NP, 2, 2, 8, 128, 128))
    if 'newC' in P.dout:
        outs.append(np.stack([r['newC'] for r in R]).reshape(NCORE * NP, 1, 2, 8, 64, 128))
        outs.append(np.stack([r['newn'] for r in R]).reshape(NCORE * NP, 1, 2, 8, 64))
        outs.append(np.stack([r['newm'] for r in R]).reshape(NCORE * NP, 1, 2, 8))
    if 'newk' in P.dout:
        outs.append(np.stack([r['newk'] for r in R]).reshape(NCORE * NP, 1, 8, 2, TP, 64))
        outs.append(np.stack([r['newv'] for r in R]).reshape(NCORE * NP, 1, 8, TP, 128))
    return tuple(outs)
```

```python
import numpy as np
from contextlib import ExitStack
import concourse.bass as bass
import concourse.mybir as mybir
from concourse.bass_utils import run_bass_kernel_spmd

F32 = mybir.dt.float32
BF16 = mybir.dt.bfloat16
AF = mybir.ActivationFunctionType
ALU = mybir.AluOpType
AX = mybir.AxisListType

ENGS = ('pe', 'act', 'dve', 'pool', 'sp')
NDS = 40

D = 1024
NCORE = 8
NP = 4
TP = 256
TS = 2048
TT = NP * TP + TS
DFF = 2816
EPS = 1e-6


class Tk:
    __slots__ = ('ap', 'lw', 'rd', 'rp', 'name')

    def __init__(self, ap, name=''):
        self.ap = ap
        self.lw = {}
        self.rd = {}
        self.rp = {}
        self.name = name

    def __getitem__(self, idx):
        return self.ap[idx]


class Sched:
    def __init__(self, nc, es):
        self.nc = nc
        self.es = es
        self.q = {e: [] for e in ENGS}
        self.sem = {e: es.enter_context(nc.semaphore("s_" + e)) for e in ENGS}
        self.cnt = {e: 0 for e in ENGS}
        self.seen = {e: {} for e in ENGS}
        self.dsem = [es.enter_context(nc.semaphore("d%d" % i)) for i in range(NDS)]
        self.dcnt = [0] * NDS
        self.dnext = 0
        self.nins = 0
        self.uid = 0

    def sb(self, shape, dt=F32, name=None):
        self.uid += 1
        name = name or "t%d" % self.uid
        t = self.es.enter_context(self.nc.sbuf_tensor(name, list(shape), dt))
        return Tk(t, name)

    def ps(self, shape, dt=F32, name=None):
        self.uid += 1
        name = name or "p%d" % self.uid
        t = self.es.enter_context(self.nc.psum_tensor(name, list(shape), dt))
        return Tk(t, name)

    def _wait(self, eng, d):
        k = d[0]
        if eng == 'pe' and k == ('e', 'pe'):
            return
        seen = self.seen[eng]
        if seen.get(k, 0) >= d[2]:
            return
        seen[k] = d[2]
        self.q[eng].append(lambda E, d=d: E.wait_ge(d[1], d[2]))
        self.nins += 1

    def _deps(self, eng, reads, writes, pw):
        deps = {}

        def add(d):
            k = d[0]
            if k not in deps or deps[k][2] < d[2]:
                deps[k] = d
        for t in reads:
            for d in t.lw.values():
                add(d)
        for t in writes:
            for d in t.lw.values():
                add(d)
            for d in t.rd.values():
                add(d)
        for t in pw:
            for d in t.rd.values():
                add(d)
            for d in t.rp.values():
                add(d)
        for d in deps.values():
            self._wait(eng, d)

    def _mark(self, me, reads, writes, pw):
        for t in reads:
            t.rd[me[0]] = me
        for t in writes:
            t.lw = {me[0]: me}
            t.rp = t.rd
            t.rd = {}
        for t in pw:
            t.lw[me[0]] = me

    def op(self, eng, fn, reads=(), writes=(), pw=()):
        self._deps(eng, reads, writes, pw)
        self.cnt[eng] += 1
        sem = self.sem[eng]
        me = (('e', eng), sem, self.cnt[eng])
        self.q[eng].append(lambda E: fn(E).then_inc(sem, 1))
        self.nins += 1
        self._mark(me, reads, writes, pw)

    def dma(self, eng, out_ap, in_ap, reads=(), writes=(), pw=()):
        slot = self.dnext
        self.dnext = (slot + 1) % NDS
        ds = self.dsem[slot]
        self._deps(eng, reads, writes, pw)
        if self.dcnt[slot] > 0:
            self._wait(eng, (('d', slot), ds, 16 * self.dcnt[slot]))
        self.dcnt[slot] += 1
        me = (('d', slot), ds, 16 * self.dcnt[slot])
        self.q[eng].append(lambda E: E.dma_start(out=out_ap, in_=in_ap).then_inc(ds, 16))
        self.nins += 1
        self._mark(me, reads, writes, pw)

    def barrier(self):
        for e in ENGS:
            for o in ENGS:
                if o != e and self.cnt[o] > 0:
                    self._wait(e, (('e', o), self.sem[o], self.cnt[o]))
            for i in range(NDS):
                if self.dcnt[i] > 0:
                    self._wait(e, (('d', i), self.dsem[i], 16 * self.dcnt[i]))

    def emit(self):
        self.barrier()
        q = self.q
        with self.nc.Block() as block:
            @block.tensor
            def _(E):
                for f in q['pe']:
                    f(E)

            @block.scalar
            def _(E):
                for f in q['act']:
                    f(E)

            @block.vector
            def _(E):
                for f in q['dve']:
                    f(E)

            @block.gpsimd
            def _(E):
                for f in q['pool']:
                    f(E)

            @block.sync
            def _(E):
                for f in q['sp']:
                    f(E)


def wr(t, first):
    return {'w': [t]} if first else {'pw': [t]}


class Rot:
    def __init__(self, tiles):
        self.t = tiles
        self.i = 0

    def get(self):
        t = self.t[self.i % len(self.t)]
        self.i += 1
        return t


ARENA_COLS = 50000


class Prog:
    def __init__(self, opts):
        self.opts = opts
        self.nc = bass.Bass("TRN2", target_bir_lowering=False)
        self.es = ExitStack()
        self.din = {}
        self.dout = {}

    def inp(self, name, shape):
        t = self.nc.dram_tensor(name, list(shape), F32, kind="ExternalInput").ap()
        self.din[name] = t
        return t

    def outp(self, name, shape):
        t = self.nc.dram_tensor(name, list(shape), F32, kind="ExternalOutput").ap()
        self.dout[name] = t
        return t

    def scratch(self, name, shape):
        return self.nc.dram_tensor(name, list(shape), F32, kind="Internal").ap()

    def areset(self):
        self.S.barrier()
        self.apos = 0

    def take(self, shape, n=None, dt=F32):
        cols = int(np.prod(shape[1:]))
        c32 = cols if dt == F32 else (cols + 1) // 2
        out = []
        for _ in range(n or 1):
            assert self.apos + c32 <= ARENA_COLS, ("arena overflow", self.apos, c32)
            ap = self.arena[0:shape[0], self.apos:self.apos + c32]
            if dt != F32:
                ap = ap.bitcast(dt)[:, 0:cols]
            if len(shape) == 3:
                ap = ap.rearrange("p (a b) -> p a b", a=shape[1])
            elif len(shape) == 4:
                ap = ap.rearrange("p (a b c) -> p a b c", a=shape[1], b=shape[2])
            self.apos += c32
            out.append(Tk(ap))
        return out[0] if n is None else Rot(out)

    def pnext(self):
        p = self.psum[self.pi % 8]
        self.pi += 1
        return p

    def pns(self):
        return self.pnext()

    def mm(self, out, lhsT, rhs, start, stop, r=(), w=(), pw=()):
        self.S.op('pe', lambda E: E.matmul(out, lhsT=lhsT, rhs=rhs, start=start, stop=stop), reads=r, writes=w, pw=pw)

    def tr(self, out, in_, r=(), w=(), pw=()):
        ident = self.ident
        n = in_.shape[0]
        self.S.op('pe', lambda E: E.transpose(out, in_, ident[0:n, 0:n]), reads=list(r) + [ident], writes=w, pw=pw)

    def act(self, out, in_, func, r=(), w=(), pw=(), bias=None, scale=None, accum=None):
        kw = {}
        if bias is not None:
            kw['bias'] = bias
        if scale is not None:
            kw['scale'] = scale
        if accum is not None:
            kw['accum_out'] = accum
        self.S.op('act', lambda E: E.activation(out=out, in_=in_, func=func, **kw), reads=r, writes=w, pw=pw)

    def tt(self, eng, out, a, b, op, r=(), w=(), pw=()):
        self.S.op(eng, lambda E: E.tensor_tensor(out=out, in0=a, in1=b, op=op), reads=r, writes=w, pw=pw)

    def ts(self, eng, out, a, s1, s2, op0, op1=None, r=(), w=(), pw=()):
        if op1 is None:
            self.S.op(eng, lambda E: E.tensor_scalar(out=out, in0=a, scalar1=s1, scalar2=None, op0=op0), reads=r, writes=w, pw=pw)
        else:
            self.S.op(eng, lambda E: E.tensor_scalar(out=out, in0=a, scalar1=s1, scalar2=s2, op0=op0, op1=op1), reads=r, writes=w, pw=pw)

    def stt(self, eng, out, a, s, b, op0, op1, r=(), w=(), pw=()):
        self.S.op(eng, lambda E: E.scalar_tensor_tensor(out=out, in0=a, scalar=s, in1=b, op0=op0, op1=op1), reads=r, writes=w, pw=pw)

    def cp(self, eng, out, in_, r=(), w=(), pw=()):
        if eng == 'act':
            self.S.op('act', lambda E: E.copy(out=out, in_=in_), reads=r, writes=w, pw=pw)
        else:
            self.S.op(eng, lambda E: E.tensor_copy(out=out, in_=in_), reads=r, writes=w, pw=pw)

    def recip(self, out, in_, r=(), w=(), pw=()):
        self.S.op('dve', lambda E: E.reciprocal(out=out, in_=in_), reads=r, writes=w, pw=pw)

    def memset(self, eng, ap, val, w=(), pw=()):
        self.S.op(eng, lambda E: E.memset(ap, val), writes=w, pw=pw)

    def ld(self, out, in_, r=(), w=(), pw=(), eng='sp'):
        self.S.dma(eng, out, in_, reads=r, writes=w, pw=pw)

    def build(self):
        nc = self.nc
        o = self.opts
        with self.es:
            S = self.S = Sched(nc, self.es)
            self.arena = self.es.enter_context(nc.sbuf_tensor("arena", [128, ARENA_COLS], F32))
            self.psum = [S.ps([128, 512], name="ps%d" % i) for i in range(8)]
            self.pi = 0
            self.apos = 0
            self.psmall = [Tk(self.psum[i // 2].ap[:, (i % 2) * 256:(i % 2) * 256 + 256]) for i in range(16)]
            self.psi = 0
            self.x_tok = self.inp("x_tok", [TT, D])
            self.y_tok = self.outp("y_tok", [TT, D])
            self.xT = self.scratch("xT", [8, 128, TT])
            self.xT_v = self.xT.rearrange("c p t -> p c t")
            self.xT_tk = [Tk(None, "xT%d" % i) for i in range(TT // 128)]
            self.Xtok = Tk(None)
            self.Ytok = Tk(None)
            self.Win = Tk(None)
            consts = self.inp("consts", [128, 8 * 128])
            condT = self.inp("condT", [128, 8, 2])
            self.ada_w = self.inp("ada_w", [4, D, 6 * D])
            ada_bT = self.inp("ada_bT", [128, 4, 48])
            normgT = self.inp("normgT", [128, 4, 2, 8])
            self.ffn_w_in = self.inp("ffn_w_in", [4, D, 2 * DFF])
            self.ffn_w_out = self.inp("ffn_w_out", [4, DFF, D])
            self.cst = S.sb([128, 8, 128], name="cst")
            self.ld(self.cst[:], consts.rearrange("p (a b) -> p a b", a=8), r=[self.Win], w=[self.cst])
            self.ones = self.cst.ap[:, 1, :]
            self.sc = S.sb([128, 8, 2], name="sc")
            self.ld(self.sc[:], condT, r=[self.Win], w=[self.sc])
            self.act(self.sc[:], self.sc[:], AF.Silu, r=[self.sc], w=[self.sc])
            self.adab = S.sb([128, 4, 48], name="adab")
            self.ld(self.adab[:], ada_bT, r=[self.Win], w=[self.adab])
            self.normg = S.sb([128, 4, 2, 8], name="normg")
            self.ld(self.normg[:], normgT, r=[self.Win], w=[self.normg])
            self.mod = S.sb([128, 48, 2], name="mod")
            self.modA = S.sb([128, 2, 8, 2], name="modA")
            self.epsb = S.sb([128, 1], name="epsb")
            self.memset('pool', self.epsb[:], EPS, w=[self.epsb])

            self.stage_in()
            self.bf = o.get('bf16', True)
            mixers = o.get('mixers', (0, 1, 2))
            self.setup_mixers(mixers)
            for i in range(o.get('depth', 4)):
                self.stage_mod(i)
                if i % 3 == 0 and 0 in mixers:
                    self.gdn(i)
                if i % 3 == 1 and 1 in mixers:
                    self.mlstm(i)
                if i % 3 == 2 and 2 in mixers:
                    self.diffattn(i)
                if o.get('ffn', True):
                    if self.bf:
                        self.stage_ffn16(i)
                    else:
                        self.stage_ffn(i)
            self.stage_out()
            S.emit()
        return nc

    def setup_mixers(self, mixers):
        S = self.S
        if 1 in mixers:
            self.ml_w_in = self.inp("mlstm_w_in", [D, 3104])
            self.ml_w_out = self.inp("mlstm_w_out", [D, D])
            ml_gb = self.inp("mlstm_gate_b", [1, 32])
            ml_ng = self.inp("mlstm_norm_g", [1, 128])
            self.st_C = self.inp("st_C", [2, 8, 64, 128])
            self.st_n = self.inp("st_n", [2, 8, 64, 1])
            self.st_m = self.inp("st_m", [1, 16])
            self.newC = self.outp("newC", [NP, 2, 8, 64, 128])
            self.newn = self.outp("newn", [NP, 2, 8, 64, 1])
            self.newm = self.outp("newm", [NP, 2, 8, 1])
            self.ml_gb = S.sb([128, 32], name="ml_gb")
            self.ld(self.ml_gb[:], ml_gb.partition_broadcast(128), r=[self.Win], w=[self.ml_gb])
            self.ml_ng = S.sb([128, 128], name="ml_ng")
            self.ld(self.ml_ng[:], ml_ng.partition_broadcast(128), r=[self.Win], w=[self.ml_ng])
        self.Oout = Tk(None)
        self.qkT = self.scratch("qkT", [24, 128, TT])
        self.qkT_v = self.qkT.rearrange("c p t -> p c t")
        self.qkT_h = self.qkT[0:8].rearrange("c (two p) t -> p (c two) t", two=2)
        self.ktok = self.scratch("ktok", [TT, 2048])
        self.vtok = self.scratch("vtok", [TT, 1024])
        self.otok = self.scratch("otok", [TT, 1024])
        self.gtok = self.scratch("gtok", [TT, 32])
        self.hdir = [self.scratch("hdir%d" % d, [TT, 1024]) for d in range(2)]
        self.proj_tk = [Tk(None) for _ in range(TT // 128)]
        self.prep_tk = [Tk(None) for _ in range(TT // 128)]
        self.hdir_tk = [[Tk(None) for _ in range(TT // 64)] for d in range(2)]
        if 2 in mixers:
            self.df_w_in = self.inp("diff_w_in", [D, 3072])
            self.df_w_out = self.inp("diff_w_out", [D, D])
            df_g = self.inp("diff_qkg", [1, 128])
            df_lam = self.inp("diff_lambda", [1, 256])
            df_sg = self.inp("diff_subln_g", [1, 128])
            self.rope_cs = self.inp("rope_cs", [TS, 64])
            self.ctx_k = self.inp("ctx_k", [8, 2, 256, 64])
            self.ctx_v = self.inp("ctx_v", [8, 256, 128])
            self.newk = self.outp("newk", [NP, 8, 2, TP, 64])
            self.newv = self.outp("newv", [NP, 8, TP, 128])
            self.df_g = S.sb([128, 2, 64], name="df_g")
            self.ld(self.df_g[:], df_g.rearrange("o (a b) -> o a b", a=2).partition_broadcast(128), r=[self.Win], w=[self.df_g])
            self.df_sg = S.sb([128, 128], name="df_sg")
            self.ld(self.df_sg[:], df_sg.partition_broadcast(128), r=[self.Win], w=[self.df_sg])
            lam_init = 0.8 - 0.6 * float(np.exp(-0.3 * 2))
            self.ts('dve', self.df_sg[:], self.df_sg[:], 1.0 - lam_init, None, ALU.mult, r=[self.df_sg], w=[self.df_sg])
            lm = S.sb([128, 4, 64], name="df_lm")
            self.ld(lm[:], df_lam.rearrange("o (a b) -> o a b", a=4).partition_broadcast(128), r=[self.Win], w=[lm])
            l2 = S.sb([128, 2, 64], name="df_l2")
            self.tt('dve', l2[:, 0, :], lm[:, 0, :], lm[:, 1, :], ALU.mult, r=[lm], w=[l2])
            self.tt('dve', l2[:, 1, :], lm[:, 2, :], lm[:, 3, :], ALU.mult, r=[lm], pw=[l2])
            self.nlam = S.sb([128, 4], name="nlam")
            nl = self.nlam
            S.op('dve', lambda E: E.tensor_reduce(out=nl[:, 0:2], in_=l2[:], axis=AX.X, op=ALU.add), reads=[l2], writes=[nl])
            self.act(nl[:, 0:2], nl[:, 0:2], AF.Exp, r=[nl], w=[nl])
            self.tt('dve', nl[:, 2:3], nl[:, 1:2], nl[:, 0:1], ALU.subtract, r=[nl], pw=[nl])
            self.ts('dve', nl[:, 3:4], nl[:, 2:3], -lam_init, None, ALU.add, r=[nl], pw=[nl])
            self.ctxkT = self.scratch("ctxkT", [8, 128, 256])
            self.ctx_tk = Tk(None)
        if 0 in mixers:
            self.gd_w_in = self.inp("gdn_w_in", [2, D, 4128])
            self.gd_w_out = self.inp("gdn_w_out", [2, D, D])
            gd_cw = self.inp("gdn_convT", [128, 2, 24, 5])
            gd_al = self.inp("gdn_a_log", [2, 1, 16])
            gd_dt = self.inp("gdn_dt_bias", [2, 1, 16])
            gd_ng = self.inp("gdn_norm_g", [2, 1, 128])
            self.st_S = self.inp("st_S", [2, 2, 8, 128, 128])
            self.newS = self.outp("newS", [NP, 2, 2, 8, 128, 128])
            self.gd_cw = S.sb([128, 2, 24, 5], name="gd_cw")
            self.ld(self.gd_cw[:], gd_cw, r=[self.Win], w=[self.gd_cw])
            self.gd_nea = S.sb([128, 2, 16], name="gd_nea")
            self.gd_dt = S.sb([128, 2, 16], name="gd_dt")
            self.gd_ng = S.sb([128, 2, 128], name="gd_ng")
            for j in range(2):
                self.ld(self.gd_nea[:, j, :], gd_al[j].partition_broadcast(128), r=[self.Win], **wr(self.gd_nea, j == 0))
                self.ld(self.gd_dt[:, j, :], gd_dt[j].partition_broadcast(128), r=[self.Win], **wr(self.gd_dt, j == 0))
                self.ld(self.gd_ng[:, j, :], gd_ng[j].partition_broadcast(128), r=[self.Win], **wr(self.gd_ng, j == 0))
            self.act(self.gd_nea[:], self.gd_nea[:], AF.Exp, r=[self.gd_nea], w=[self.gd_nea])
            self.ts('dve', self.gd_nea[:], self.gd_nea[:], -1.0, None, ALU.mult, r=[self.gd_nea], w=[self.gd_nea])

    def proj_stage(self, i, w_in, ncol, fm, tm, col_lo=0):
        self.areset()
        NT = 512
        xbs = self.take([128, 8, NT], 1)
        hbs = self.take([128, 8, NT], 1)
        self.rstd = self.take([128, NT], 2)
        W = self.take([128, 8, ncol])
        wv_ = w_in.rearrange("(kc p) n -> p kc n", p=128)
        self.ld(W[:, 0:4, :], wv_[:, 0:4, col_lo:col_lo + ncol], r=[self.Win], w=[W])
        self.ld(W[:, 4:8, :], wv_[:, 4:8, col_lo:col_lo + ncol], r=[self.Win], pw=[W], eng='pool')
        fm = [(a - col_lo, b, c_, d_, e_) for (a, b, c_, d_, e_) in fm]
        tm = [(a - col_lo, b, c_) for (a, b, c_) in tm]
        ofm = self.take([128, NT], 3)
        otm = self.take([128, 512], 3)
        for blk in range(TT // NT):
            c = 0 if blk < (NP * TP) // NT else 1
            tks = self.xT_tk[blk * 4:(blk + 1) * 4]
            ptk = self.proj_tk[blk * 4:(blk + 1) * 4]
            xb = xbs.get()
            self.ld(xb[:], self.xT_v[:, :, blk * NT:(blk + 1) * NT], r=tks, w=[xb], eng='pool')
            hb = hbs.get()
            self.norm_mod(xb, hb, 0, c, NT)
            n = 0
            for (col0, nch, dstv, ch0, scale) in fm:
                for oc in range(nch):
                    p = self.pnext()
                    for kc in range(8):
                        self.mm(p[:, :], W[:, kc, col0 + oc * 128: col0 + (oc + 1) * 128], hb[:, kc, :], kc == 0, kc == 7, r=[W, hb], **wr(p, kc == 0))
                    ot = ofm.get()
                    if n % 2 == 0:
                        self.act(ot[:], p[:, :], AF.Copy, r=[p], w=[ot], scale=scale)
                    else:
                        self.ts('dve', ot[:], p[:, :], scale, None, ALU.mult, r=[p], w=[ot])
                    n += 1
                    self.ld(dstv[:, ch0 + oc, blk * NT:(blk + 1) * NT], ot[:], r=[ot], pw=ptk, eng='pool')
            for q in range(4):
                t0 = blk * NT + q * 128
                for (col0, ncols, dst) in tm:
                    for g0 in range(0, ncols, 512):
                        gw = min(512, ncols - g0)
                        p = self.pnext()
                        for kc in range(8):
                            self.mm(p[:, 0:gw], hb[:, kc, q * 128:(q + 1) * 128], W[:, kc, col0 + g0: col0 + g0 + gw], kc == 0, kc == 7, r=[W, hb], **wr(p, kc == 0))
                        ot = otm.get()
                        if n % 2 == 0:
                            self.cp('act', ot[:, 0:gw], p[:, 0:gw], r=[p], w=[ot])
                        else:
                            self.cp('dve', ot[:, 0:gw], p[:, 0:gw], r=[p], w=[ot])
                        n += 1
                        self.ld(dst[t0:t0 + 128, g0:g0 + gw], ot[:, 0:gw], r=[ot], pw=[ptk[q]], eng='sp')

    def load_w16(self, W16, w_view, ncol, col_lo=0, piece=512, eng2='pool'):
        n = 0
        for c0 in range(0, ncol, piece):
            cw = min(piece, ncol - c0)
            st = self.wstage.get()
            self.ld(st[:, :, 0:cw], w_view[:, :, col_lo + c0:col_lo + c0 + cw], r=[self.Win], w=[st], eng='sp' if n % 2 == 0 else eng2)
            if n % 2 == 0:
                self.cp('pool', W16[:, :, c0:c0 + cw], st[:, :, 0:cw], r=[st], **wr(W16, c0 == 0))
            else:
                self.cp('act', W16[:, :, c0:c0 + cw], st[:, :, 0:cw], r=[st], **wr(W16, c0 == 0))
            n += 1

    def proj_stage16(self, i, w_in, ncol, fm, tm):
        self.areset()
        NT = 512
        xbs = self.take([128, 8, NT], 2)
        hbs = self.take([128, 8, NT], 2, BF16)
        sq = self.take([128, 8, NT])
        self.rstd = self.take([128, NT], 2)
        W = self.take([128, 8, ncol], None, BF16)
        self.wstage = self.take([128, 8, 512], 2)
        self.load_w16(W, w_in.rearrange("(kc p) n -> p kc n", p=128), ncol)
        ofm = self.take([128, NT], 3)
        otm = self.take([128, 512], 3)
        for blk in range(TT // NT):
            c = 0 if blk < (NP * TP) // NT else 1
            tks = self.xT_tk[blk * 4:(blk + 1) * 4]
            xb = xbs.get()
            self.ld(xb[:], self.xT_v[:, :, blk * NT:(blk + 1) * NT], r=tks, w=[xb], eng='pool')
            hb = hbs.get()
            self.norm_mod2(xb, xb[:, :, :], hb, hb[:, :, :], sq, 0, c, NT, True)
            n = 0
            for (col0, nch, dstv, ch0, scale) in fm:
                for oc in range(nch):
                    p = self.pnext()
                    for kc in range(8):
                        self.mm(p[:, :], W[:, kc, col0 + oc * 128: col0 + (oc + 1) * 128], hb[:, kc, :], kc == 0, kc == 7, r=[W, hb], **wr(p, kc == 0))
                    ot = ofm.get()
                    if n % 2 == 0:
                        self.act(ot[:], p[:, :], AF.Copy, r=[p], w=[ot], scale=scale)
                    else:
                        self.ts('dve', ot[:], p[:, :], scale, None, ALU.mult, r=[p], w=[ot])
                    n += 1
                    self.ld(dstv[:, ch0 + oc, blk * NT:(blk + 1) * NT], ot[:], r=[ot], pw=[self.Oout], eng='pool')
            for q in range(4):
                t0 = blk * NT + q * 128
                for (col0, ncols, dst) in tm:
                    for g0 in range(0, ncols, 512):
                        gw = min(512, ncols - g0)
                        p = self.pnext()
                        for kc in range(8):
                            self.mm(p[:, 0:gw], hb[:, kc, q * 128:(q + 1) * 128], W[:, kc, col0 + g0: col0 + g0 + gw], kc == 0, kc == 7, r=[W, hb], **wr(p, kc == 0))
                        ot = otm.get()
                        if n % 2 == 0:
                            self.cp('act', ot[:, 0:gw], p[:, 0:gw], r=[p], w=[ot])
                        else:
                            self.cp('dve', ot[:, 0:gw], p[:, 0:gw], r=[p], w=[ot])
                        n += 1
                        self.ld(dst[t0:t0 + 128, g0:g0 + gw], ot[:, 0:gw], r=[ot], pw=[self.Oout], eng='sp')

    def mlstm(self, i):
        o = self.opts
        (self.proj_stage16 if self.bf else self.proj_stage)(i, self.ml_w_in, 3104,
                        fm=[(0, 4, self.qkT_v, 0, 0.125), (512, 4, self.qkT_v, 4, 1.0)],
                        tm=[(512, 512, self.ktok), (1024, 1024, self.vtok), (2048, 1024, self.otok), (3072, 32, self.gtok)])
        self.mlstm_scan()
        self.mixer_post(i, self.ml_w_out, self.ml_ng, self.ml_ng[:], self.hdir, 'sigmoid')

    def mlstm_scan(self):
        self.areset()
        cst = self.cst
        Tri = [cst.ap[0:64, 2, 0:64], cst.ap[0:64, 3, 0:64]]
        Str = [cst.ap[0:64, 4, 0:64], cst.ap[0:64, 5, 0:64]]
        ones64 = cst.ap[0:64, 1, 0:64]
        Cn = [[self.take([64, 129]) for h in range(8)] for d in range(2)]
        qks = self.take([64, 16, 64], 4)
        kts = self.take([64, 512], 4)
        v1s = self.take([64, 8, 129], 4)
        for v1 in v1s.t:
            self.memset('pool', v1[:, :, 128:129], 1.0, pw=[v1])
        gts = self.take([64, 32], 4)
        gps = self.take([64, 64], 4)
        tls = self.take([64, 64], 6)
        Es = self.take([64, 64], 20)
        WTs = self.take([64, 64], 20)
        ias = self.take([64, 129], 6)
        tot8s = self.take([64, 8, 129], 4)
        dns = self.take([64, 16], 4)
        kws = self.take([64, 64], 20)
        houts = self.take([64, 8, 128], 4)
        mst = [self.take([8, 1]) for d in range(2)]
        msm = self.take([8, 8], 2)
        emf = self.take([64, 8], 2)
        GBs = self.take([8, 2], 4)
        em0 = self.take([64, 16])
        cos = self.take([64, 129], 4)
        seqs = [(p * TP, TP // 64, p) for p in range(NP)] + [(NP * TP, TS // 64, -1)]
        for (tok0, nch, pidx) in seqs:
            if pidx >= 0:
                for d in range(2):
                    for h in range(8):
                        self.memset('pool', Cn[d][h][:], 0.0, w=[Cn[d][h]])
                    self.memset('pool', mst[d][:], 0.0, w=[mst[d]])
            else:
                self.ld(em0[:], self.st_m.partition_broadcast(64), r=[self.Win], w=[em0])
                self.act(em0[:], em0[:], AF.Exp, r=[em0], w=[em0])
                for d in range(2):
                    for h in range(8):
                        T_ = Cn[d][h]
                        self.ld(T_[:, 0:128], self.st_C[d, h], r=[self.Win], w=[T_])
                        self.ld(T_[:, 128:129], self.st_n[d, h], r=[self.Win], pw=[T_], eng='pool')
                        self.ts('pool', T_[:], T_[:], em0[:, d * 8 + h: d * 8 + h + 1], None, ALU.mult, r=[T_, em0], w=[T_])
            for step in range(nch):
                ctxs = []
                for d in range(2):
                    c = step if d == 0 else nch - 1 - step
                    t0 = tok0 + c * 64
                    ptk = [self.proj_tk[t0 // 128]]
                    qk = qks.get()
                    self.ld(qk[:], self.qkT_h[:, :, t0:t0 + 64], r=ptk, w=[qk])
                    kt = kts.get()
                    self.ld(kt[:], self.ktok[t0:t0 + 64, 0:512], r=ptk, w=[kt], eng="pool")
                    v1 = v1s.get()
                    self.ld(v1[:, :, 0:128], self.vtok[t0:t0 + 64, :].rearrange("t (h e) -> t h e", h=8), r=ptk, pw=[v1])
                    gt = gts.get()
                    self.ld(gt[:], self.gtok[t0:t0 + 64, :], r=ptk, w=[gt], eng='pool')
                    gp = gps.get()
                    dc = slice(d * 8, d * 8 + 8)
                    self.tt('dve', gp[:, 0:8], gt[:, dc], self.ml_gb[0:64, dc], ALU.add, r=[gt, self.ml_gb], w=[gp])
                    self.tt('dve', gp[:, 16:24], gt[:, 16 + d * 8:24 + d * 8], self.ml_gb[0:64, 16 + d * 8:24 + d * 8], ALU.add, r=[gt, self.ml_gb], pw=[gp])
                    self.act(gp[:, 16:24], gp[:, 16:24], AF.Exp, r=[gp], pw=[gp], scale=-1.0)
                    self.act(gp[:, 16:24], gp[:, 16:24], AF.Ln, r=[gp], pw=[gp], bias=1.0)
                    self.ts('dve', gp[:, 16:24], gp[:, 16:24], -1.0, None, ALU.mult, r=[gp], pw=[gp])
                    lf = gp[:, 16:24]
                    pg = self.pnext()
                    self.mm(pg[0:64, 0:8], Tri[d], lf, True, True, r=[cst, gp], w=[pg])
                    self.mm(pg[0:64, 8:16], Str[d], lf, True, True, r=[cst, gp], pw=[pg])
                    self.mm(pg[0:64, 16:24], ones64, lf, True, True, r=[cst, gp], pw=[pg])
                    self.act(gp[:, 32:40], pg[0:64, 0:8], AF.Exp, r=[pg], pw=[gp])
                    self.tt('dve', gp[:, 56:64], pg[0:64, 8:16], gp[:, 0:8], ALU.add, r=[pg, gp], pw=[gp])
                    self.act(gp[:, 40:48], gp[:, 56:64], AF.Exp, r=[gp], pw=[gp])
                    self.act(gp[:, 48:56], pg[0:64, 16:24], AF.Exp, r=[pg], pw=[gp])
                    if pidx >= 0:
                        pt = self.pnext()
                        self.tr_(pt[0:8, 0:64], gp[:, 56:64], r=[gp], w=[pt])
                        self.tr_(pt[0:8, 64:128], gp[:, 24:32] if False else pg[0:64, 16:24], r=[pg], pw=[pt]) if False else None
                        GB = GBs.get()
                        self.S.op('dve', lambda E, GB=GB, pt=pt: E.tensor_reduce(out=GB[:, 0:1], in_=pt[0:8, 0:64], axis=AX.X, op=ALU.max), reads=[pt], writes=[GB])
                        pb = self.pnext()
                        self.mm(pb[0:8, 0:1], lf, cst.ap[0:64, 1, 0:1], True, True, r=[gp, cst], w=[pb])
                        self.stt('dve', mst[d][:], mst[d][:], pb[0:8, 0:1], GB[:, 0:1], ALU.add, ALU.max, r=[mst[d], pb, GB], w=[mst[d]])
                    ctxs.append((d, t0, qk, kt, v1, gp, houts.get(), tot8s.get()))
                units = [(cx, h) for cx in ctxs for h in range(8)]
                stE = {}
                for (cx, h) in units:
                    d, t0, qk, kt, v1, gp, ho, t8 = cx
                    tl = tls.get()
                    self.ts('pool', tl[:], Tri[d], gp[:, 16 + h:17 + h], None, ALU.mult, r=[cst, gp], w=[tl])
                    pD = self.pnext()
                    self.mm(pD[0:64, 0:64], Str[d], tl[:], True, True, r=[cst, tl], w=[pD])
                    E_ = Es.get()
                    self.act(E_[:], pD[0:64, 0:64], AF.Exp, r=[pD, gp], w=[E_], bias=gp[:, h:h + 1])
                    stE[(d, h)] = E_
                stK = {}
                for (cx, h) in units:
                    d, t0, qk, kt, v1, gp, ho, t8 = cx
                    E_ = stE[(d, h)]
                    self.tt('pool', E_[:], E_[:], Tri[d], ALU.mult, r=[E_, cst], w=[E_])
                    kw = kws.get()
                    self.ts('pool', kw[:], kt[:, h * 64:(h + 1) * 64], gp[:, 40 + h:41 + h], None, ALU.mult, r=[kt, gp], w=[kw])
                    stK[(d, h)] = kw
                stW = {}
                for (cx, h) in units:
                    d, t0, qk, kt, v1, gp, ho, t8 = cx
                    pK = self.pnext()
                    self.mm(pK[0:64, 0:64], qk[:, 8 + h, :], qk[:, h, :], True, True, r=[qk], w=[pK])
                    WT = WTs.get()
                    self.tt('dve', WT[:], stE[(d, h)][:], pK[0:64, 0:64], ALU.mult, r=[stE[(d, h)], pK], w=[WT])
                    stW[(d, h)] = WT
                for (cx, h) in units:
                    d, t0, qk, kt, v1, gp, ho, t8 = cx
                    pI = self.pnext()
                    self.mm(pI[0:64, 0:129], stW[(d, h)][:], v1[:, h, :], True, True, r=[stW[(d, h)], v1], w=[pI])
                    pN = self.pnext()
                    C_ = Cn[d][h]
                    self.mm(pN[0:64, 0:129], qk[:, h, :], C_[:], True, True, r=[qk, C_], w=[pN])
                    ia = ias.get()
                    self.cp('act', ia[:], pI[0:64, 0:129], r=[pI], w=[ia])
                    self.stt('dve', t8[:, h, :], pN[0:64, 0:129], gp[:, 32 + h:33 + h], ia[:], ALU.mult, ALU.add, r=[pN, gp, ia], **wr(t8, h == 0))
                for cx in ctxs:
                    d, t0, qk, kt, v1, gp, ho, t8 = cx
                    dn = dns.get()
                    den = t8[:, :, 128]
                    self.ts('dve', dn[:, 0:8], den, -1.0, None, ALU.mult, r=[t8], w=[dn])
                    self.tt('dve', dn[:, 0:8], dn[:, 0:8], den, ALU.max, r=[dn, t8], w=[dn])
                    self.ts('dve', dn[:, 0:8], dn[:, 0:8], 1.0, None, ALU.max, r=[dn], w=[dn])
                    self.recip(dn[:, 8:16], dn[:, 0:8], r=[dn], pw=[dn])
                    self.tt('pool', ho[:], t8[:, :, 0:128], dn[:, 8:16].unsqueeze(2).to_broadcast([64, 8, 128]), ALU.mult, r=[t8, dn], w=[ho])
                    self.ld(self.hdir[d][t0:t0 + 64, :].rearrange("t (h e) -> t h e", h=8), ho[:], r=[ho], w=[self.hdir_tk[d][t0 // 64]], eng='pool')
                for (cx, h) in units:
                    d, t0, qk, kt, v1, gp, ho, t8 = cx
                    C_ = Cn[d][h]
                    pU = self.pnext()
                    self.mm(pU[0:64, 0:129], stK[(d, h)][:], v1[:, h, :], True, True, r=[stK[(d, h)], v1], w=[pU])
                    self.stt('dve', C_[:], C_[:], gp[:, 48 + h:49 + h], pU[0:64, 0:129], ALU.mult, ALU.add, r=[C_, gp, pU], w=[C_])
            if pidx >= 0:
                for d in range(2):
                    dm = msm.get()
                    self.ts('dve', dm[:], cst.ap[0:8, 0, 0:8], mst[d][:, 0:1], None, ALU.mult, r=[cst, mst[d]], w=[dm])
                    pm = self.pnext()
                    self.mm(pm[0:64, 0:8], cst.ap[0:8, 1, 0:64], dm[:], True, True, r=[cst, dm], w=[pm])
                    ef = emf.get()
                    self.act(ef[:], pm[0:64, 0:8], AF.Exp, r=[pm], w=[ef], scale=-1.0)
                    self.ld(self.newm[pidx, d], mst[d][:], r=[mst[d]], w=[self.Oout], eng='pool')
                    for h in range(8):
                        co = cos.get()
                        self.ts('dve' if h % 2 else 'pool', co[:], Cn[d][h][:], ef[:, h:h + 1], None, ALU.mult, r=[Cn[d][h], ef], w=[co])
                        self.ld(self.newC[pidx, d, h], co[:, 0:128], r=[co], pw=[self.Oout], eng='sp')
                        self.ld(self.newn[pidx, d, h], co[:, 128:129], r=[co], pw=[self.Oout], eng='pool')

    def gdn(self, i):
        j = i // 3
        w_in = self.gd_w_in[j]
        if self.bf:
            self.proj_stage16(i, w_in, 4128, fm=[(0, 24, self.qkT_v, 0, 1.0)],
                              tm=[(3072, 1024, self.otok), (4096, 32, self.gtok)])
        else:
            self.proj_stage(i, w_in, 2048, fm=[(0, 16, self.qkT_v, 0, 1.0)], tm=[], col_lo=0)
            self.proj_stage(i, w_in, 2080, fm=[(2048, 8, self.qkT_v, 16, 1.0)],
                            tm=[(3072, 1024, self.otok), (4096, 32, self.gtok)], col_lo=2048)
        stop = self.opts.get('gdn_stop', 9)
        if stop >= 2:
            self.gdn_conv(j)
        if stop >= 3:
            self.gdn_scan(j)
        if stop >= 4:
            self.mixer_post(i, self.gd_w_out[j], self.gd_ng, self.gd_ng[:, j, :], self.hdir, 'silu')

    def gdn_conv(self, j):
        self.areset()
        xins = self.take([128, TS + 4], 2)
        accs = self.take([128, TS], 2)
        tmps = self.take([128, TS], 2)
        sqs = self.take([128, 512], 2)
        rss = self.take([128, 512], 2)
        tos = self.take([128, 4, 128], 3)
        seqs = [(p * TP, TP) for p in range(NP)] + [(NP * TP, TS)]
        n = 0
        for (tok0, T) in seqs:
            for ch in range(24):
                eng = 'dve' if n % 2 == 0 else 'pool'
                n += 1
                xin = xins.get()
                self.memset('pool', xin[:, 0:2], 0.0, w=[xin])
                self.memset('pool', xin[:, T + 2:T + 4], 0.0, pw=[xin])
                self.ld(xin[:, 2:T + 2], self.qkT_v[:, ch, tok0:tok0 + T], pw=[xin])
                acc = accs.get()
                cw = self.gd_cw
                if eng == 'dve':
                    self.ts(eng, acc[:, 0:T], xin[:, 0:T], cw[:, j, ch, 0:1], None, ALU.mult, r=[xin, cw], w=[acc])
                    for k in range(1, 5):
                        self.stt(eng, acc[:, 0:T], xin[:, k:k + T], cw[:, j, ch, k:k + 1], acc[:, 0:T], ALU.mult, ALU.add, r=[xin, cw, acc], w=[acc])
                else:
                    self.act(acc[:, 0:T], xin[:, 0:T], AF.Copy, r=[xin, cw], w=[acc], scale=cw[:, j, ch, 0:1])
                    for k in range(1, 5):
                        tm_ = tmps.get()
                        self.act(tm_[:, 0:T], xin[:, k:k + T], AF.Copy, r=[xin, cw], w=[tm_], scale=cw[:, j, ch, k:k + 1])
                        self.tt('pool', acc[:, 0:T], acc[:, 0:T], tm_[:, 0:T], ALU.add, r=[acc, tm_], w=[acc])
                self.act(acc[:, 0:T], acc[:, 0:T], AF.Silu, r=[acc], w=[acc])
                if ch < 16:
                    scale = (128.0 ** -0.5) if ch < 8 else 1.0
                    for b0 in range(0, T, 512):
                        bw = min(512, T - b0)
                        sq = sqs.get()
                        self.tt('pool', sq[:, 0:bw], acc[:, b0:b0 + bw], acc[:, b0:b0 + bw], ALU.mult, r=[acc], w=[sq])
                        p = self.pnext()
                        self.mm(p[:, 0:bw], self.cst.ap[:, 1, :], sq[:, 0:bw], True, True, r=[self.cst, sq], w=[p])
                        rs = rss.get()
                        self.act(rs[:, 0:bw], p[:, 0:bw], AF.Sqrt, r=[p, self.epsb], w=[rs], bias=self.epsb[:, 0:1])
                        self.recip(rs[:, 0:bw], rs[:, 0:bw], r=[rs], w=[rs])
                        self.stt('dve', acc[:, b0:b0 + bw], acc[:, b0:b0 + bw], scale, rs[:, 0:bw], ALU.mult, ALU.mult, r=[acc, rs], w=[acc])
                    self.ld(self.qkT_v[:, ch, tok0:tok0 + T], acc[:, 0:T], r=[acc], pw=[self.Oout], eng='pool')
                if ch >= 8:
                    dst = self.ktok if ch < 16 else self.vtok
                    c0 = (ch - 8) * 128 if ch < 16 else (ch - 16) * 128
                    for g0 in range(0, T, 512):
                        ng = min(4, (T - g0) // 128)
                        p = self.pnext()
                        for k in range(ng):
                            self.tr_(p[:, k * 128:(k + 1) * 128], acc[:, g0 + k * 128:g0 + (k + 1) * 128], r=[acc], **wr(p, k == 0))
                        to = tos.get()
                        self.cp('act', to[:, 0:ng, :], p[:, 0:ng * 128].rearrange("p (a b) -> p a b", a=ng), r=[p], w=[to])
                        self.ld(dst[tok0 + g0:tok0 + g0 + ng * 128, c0:c0 + 128].rearrange("(n p) e -> p n e", p=128), to[:, 0:ng, :], r=[to], pw=[self.Oout], eng='sp')

    def gdn_scan(self, j):
        self.areset()
        cst = self.cst
        Tri = [cst.ap[0:64, 2, 0:64], cst.ap[0:64, 3, 0:64]]
        Str = [cst.ap[0:64, 4, 0:64], cst.ap[0:64, 5, 0:64]]
        Sm = [cst.ap[0:64, 5, 0:64], cst.ap[0:64, 4, 0:64]]
        I64 = cst.ap[0:64, 0, 0:64]
        ones64 = cst.ap[0:64, 1, 0:64]
        ones64w = cst.ap[0:64, 1, 0:128]
        Sst = [[self.take([128, 128]) for h in range(8)] for d in range(2)]
        qks = self.take([128, 16, 64], 4)
        kts = self.take([64, 8, 128], 4)
        vts = self.take([64, 8, 128], 4)
        gts = self.take([64, 32], 4)
        gps = self.take([64, 48], 4)
        gls = self.take([128, 8], 4)
        NB = 18
        tls = self.take([64, 64], 6)
        Ers = self.take([64, 64], NB)
        Eis = self.take([64, 64], NB)
        Ess = self.take([64, 64], NB)
        qkTs = self.take([64, 64], NB)
        Xs = self.take([64, 64], 36)
        XTs = self.take([64, 64], 36)
        Ps = self.take([64, 64], NB)
        Us = self.take([64, 128], NB)
        kegs = self.take([64, 128], 6)
        WTs = self.take([128, 64], NB)
        kdecs = self.take([64, 128], NB)
        vns = self.take([64, 128], NB)
        o2s = self.take([64, 128], 6)
        houts = self.take([64, 8, 128], 4)
        seqs = [(p * TP, TP // 64, p) for p in range(NP)] + [(NP * TP, TS // 64, -1)]
        seqs = seqs[self.opts.get('gdn_seq0', 0):self.opts.get('gdn_seq1', 5)]
        for (tok0, nch, pidx) in seqs:
            for d in range(2):
                for h in range(8):
                    S_ = Sst[d][h]
                    if pidx >= 0:
                        self.memset('pool', S_[:], 0.0, w=[S_])
                    else:
                        self.ld(S_[:], self.st_S[j, d, h], r=[self.Win], w=[S_], eng='sp' if h % 2 else 'pool')
            for step in range(nch):
                ctx = []
                for d in range(2):
                    c = step if d == 0 else nch - 1 - step
                    t0 = tok0 + c * 64
                    qk = qks.get()
                    self.ld(qk[:], self.qkT_v[:, 0:16, t0:t0 + 64], w=[qk])
                    kt = kts.get()
                    self.ld(kt[:], self.ktok[t0:t0 + 64, 0:1024].rearrange("t (h e) -> t h e", h=8), w=[kt], eng='pool')
                    vt = vts.get()
                    self.ld(vt[:], self.vtok[t0:t0 + 64, :].rearrange("t (h e) -> t h e", h=8), w=[vt])
                    gt = gts.get()
                    self.ld(gt[:], self.gtok[t0:t0 + 64, :], w=[gt], eng='pool')
                    gp = gps.get()
                    dc = slice(d * 8, d * 8 + 8)
                    self.tt('dve', gp[:, 0:8], gt[:, dc], self.gd_dt[0:64, j, dc], ALU.add, r=[gt, self.gd_dt], w=[gp])
                    self.act(gp[:, 0:8], gp[:, 0:8], AF.Exp, r=[gp], pw=[gp])
                    self.act(gp[:, 0:8], gp[:, 0:8], AF.Ln, r=[gp], pw=[gp], bias=1.0)
                    self.tt('dve', gp[:, 8:16], gp[:, 0:8], self.gd_nea[0:64, j, dc], ALU.mult, r=[gp, self.gd_nea], pw=[gp])
                    self.act(gp[:, 16:24], gt[:, 16 + d * 8:24 + d * 8], AF.Sigmoid, r=[gt], pw=[gp])
                    self.ts('dve', gp[:, 24:32], gp[:, 16:24], -1.0, None, ALU.mult, r=[gp], pw=[gp])
                    la = gp[:, 8:16]
                    pg = self.pns()
                    self.mm(pg[0:64, 0:8], Tri[d], la, True, True, r=[cst, gp], w=[pg])
                    self.mm(pg[0:64, 8:16], Str[d], la, True, True, r=[cst, gp], pw=[pg])
                    self.mm(pg[0:128, 16:24], ones64w, la, True, True, r=[cst, gp], pw=[pg])
                    self.act(gp[:, 32:40], pg[0:64, 0:8], AF.Exp, r=[pg], pw=[gp])
                    self.act(gp[:, 40:48], pg[0:64, 8:16], AF.Exp, r=[pg], pw=[gp])
                    gl = gls.get()
                    self.act(gl[:], pg[0:128, 16:24], AF.Exp, r=[pg], w=[gl])
                    ctx.append((d, t0, qk, kt, vt, gp, gl, houts.get()))
                units = [(cx, h) for cx in ctx for h in range(8)]
                st = {}
                stEr = {}
                for (cx, h) in units:
                    d, t0, qk, kt, vt, gp, gl, ho = cx
                    tl = tls.get()
                    self.ts('pool', tl[:], Tri[d], gp[:, 8 + h:9 + h], None, ALU.mult, r=[cst, gp], w=[tl])
                    pD = self.pns()
                    self.mm(pD[0:64, 0:64], Str[d], tl[:], True, True, r=[cst, tl], w=[pD])
                    Er = Ers.get()
                    self.act(Er[:], pD[0:64, 0:64], AF.Exp, r=[pD], w=[Er])
                    stEr[(d, h)] = Er
                stM = {}
                for (cx, h) in units:
                    d = cx[0]
                    Er = stEr[(d, h)]
                    Ei = Eis.get()
                    Es = Ess.get()
                    self.tt('pool', Ei[:], Er[:], Tri[d], ALU.mult, r=[Er, cst], w=[Ei])
                    self.tt('pool', Es[:], Er[:], Sm[d], ALU.mult, r=[Er, cst], w=[Es])
                    stM[(d, h)] = (Ei, Es)
                for (cx, h) in units:
                    d, t0, qk, kt, vt, gp, gl, ho = cx
                    Ei, Es = stM[(d, h)]
                    kT = qk[:, 8 + h, :]
                    qT = qk[:, h, :]
                    pKK = self.pns()
                    self.mm(pKK[0:64, 0:64], kT, kT, True, True, r=[qk], w=[pKK])
                    pKQ = self.pns()
                    self.mm(pKQ[0:64, 0:64], kT, qT, True, True, r=[qk], w=[pKQ])
                    qkT = qkTs.get()
                    self.tt('dve', qkT[:], Ei[:], pKQ[0:64, 0:64], ALU.mult, r=[Ei, pKQ], w=[qkT])
                    X = Xs.get()
                    self.stt('dve', X[:], pKK[0:64, 0:64], gp[:, 24 + h:25 + h], Es[:], ALU.mult, ALU.mult, r=[pKK, gp, Es], w=[X])
                    st[(d, h)] = [X, None, None, qkT]
                for (cx, h) in units:
                    d = cx[0]
                    X = st[(d, h)][0]
                    pT = self.pns()
                    self.tr_(pT[0:64, 0:64], X[:], r=[X], w=[pT])
                    XT = XTs.get()
                    self.cp('act', XT[:], pT[0:64, 0:64], r=[pT], w=[XT])
                    P_ = Ps.get()
                    self.tt('pool', P_[:], X[:], I64, ALU.add, r=[X, cst], w=[P_])
                    st[(d, h)][1] = XT
                    st[(d, h)][2] = P_
                for jn in range(1, 6):
                    for (cx, h) in units:
                        d = cx[0]
                        X, XT, P_, qkT = st[(d, h)]
                        Xn = None
                        if jn < 5:
                            pX = self.pns()
                            self.mm(pX[0:64, 0:64], XT[:], X[:], True, True, r=[XT, X], w=[pX])
                            Xn = Xs.get()
                            self.cp('dve', Xn[:], pX[0:64, 0:64], r=[pX], w=[Xn])
                        pXT = self.pns()
                        self.mm(pXT[0:64, 0:64], X[:], XT[:], True, True, r=[XT, X], w=[pXT])
                        XnT = XTs.get()
                        self.cp('act', XnT[:], pXT[0:64, 0:64], r=[pXT], w=[XnT])
                        st[(d, h)] = [Xn, XnT, P_, qkT]
                    for (cx, h) in units:
                        d = cx[0]
                        X, XT, P_, qkT = st[(d, h)]
                        pP = self.pns()
                        self.mm(pP[0:64, 0:64], XT[:], P_[:], True, True, r=[XT, P_], w=[pP])
                        self.tt('dve', P_[:], P_[:], pP[0:64, 0:64], ALU.add, r=[P_, pP], w=[P_])
                for (cx, h) in units:
                    d, t0, qk, kt, vt, gp, gl, ho = cx
                    X, XT, P_, qkT = st[(d, h)]
                    pU = self.pns()
                    self.mm(pU[0:64, 0:128], P_[:], vt[:, h, :], True, True, r=[P_, vt], w=[pU])
                    U = Us.get()
                    self.ts('pool' if False else 'dve', U[:], pU[0:64, 0:128], gp[:, 16 + h:17 + h], None, ALU.mult, r=[pU, gp], w=[U])
                    keg = kegs.get()
                    self.ts('pool', keg[:], kt[:, h, :], gp[:, 32 + h:33 + h], None, ALU.mult, r=[kt, gp], w=[keg])
                    pW = self.pns()
                    self.mm(pW[0:128, 0:64], keg[:], P_[:], True, True, r=[keg, P_], w=[pW])
                    WT = WTs.get()
                    self.cp('act', WT[:], pW[0:128, 0:64], r=[pW], w=[WT])
                    kdec = kdecs.get()
                    self.ts('pool', kdec[:], kt[:, h, :], gp[:, 40 + h:41 + h], None, ALU.mult, r=[kt, gp], w=[kdec])
                    st[(d, h)] = [U, WT, kdec, qkT]
                pas = {}
                for (cx, h) in units:
                    d = cx[0]
                    U, WT, kdec, qkT = st[(d, h)]
                    pa = self.pns()
                    self.mm(pa[0:64, 0:128], WT[:], Sst[d][h][:], True, True, r=[WT, Sst[d][h]], w=[pa])
                    vn = vns.get()
                    self.stt('dve', vn[:], pa[0:64, 0:128], cx[5][:, 24 + h:25 + h], U[:], ALU.mult, ALU.add, r=[pa, cx[5], U], w=[vn])
                    pas[(d, h)] = vn
                for (cx, h) in units:
                    d, t0, qk, kt, vt, gp, gl, ho = cx
                    U, WT, kdec, qkT = st[(d, h)]
                    vn = pas[(d, h)]
                    S_ = Sst[d][h]
                    po = self.pns()
                    self.mm(po[0:64, 0:128], qk[:, h, :], S_[:], True, True, r=[qk, S_], w=[po])
                    po2 = self.pns()
                    self.mm(po2[0:64, 0:128], qkT[:], vn[:], True, True, r=[qkT, vn], w=[po2])
                    pS = self.pns()
                    self.mm(pS[0:128, 0:128], kdec[:], vn[:], True, True, r=[kdec, vn], w=[pS])
                    o2 = o2s.get()
                    self.cp('act', o2[:], po2[0:64, 0:128], r=[po2], w=[o2])
                    self.stt('dve', ho[:, h, :], po[0:64, 0:128], gp[:, 32 + h:33 + h], o2[:], ALU.mult, ALU.add, r=[po, gp, o2], **wr(ho, h == 0))
                    self.stt('dve', S_[:], S_[:], gl[:, h:h + 1], pS[0:128, 0:128], ALU.mult, ALU.add, r=[S_, gl, pS], w=[S_])
                for cx in ctx:
                    d, t0, qk, kt, vt, gp, gl, ho = cx
                    self.ld(self.hdir[d][t0:t0 + 64, :].rearrange("t (h e) -> t h e", h=8), ho[:], r=[ho], pw=[self.Oout], eng='pool')
            if pidx >= 0:
                for d in range(2):
                    for h in range(8):
                        self.ld(self.newS[pidx, j, d, h], Sst[d][h][:], r=[Sst[d][h]], pw=[self.Oout], eng='sp' if h % 2 else 'pool')

    def diffattn(self, i):
        (self.proj_stage16 if self.bf else self.proj_stage)(i, self.df_w_in, 3072, fm=[],
                        tm=[(0, 2048, self.ktok), (2048, 1024, self.vtok)])
        self.attn_prep()
        self.attn_core()
        self.mixer_post(i, self.df_w_out, self.df_sg, self.df_sg[:], self.hdir, None)

    def attn_prep(self):
        self.areset()
        xs = self.take([128, 32, 64], 2)
        sqs = self.take([128, 32, 64], 1)
        sss = self.take([128, 64], 2)
        css = self.take([128, 64], 2)
        r1 = self.take([128, 32, 2, 16], 1)
        r2 = self.take([128, 32, 2, 16], 1)
        r3 = self.take([128, 32, 2, 16], 1)
        xr = self.take([128, 32, 64], 2)
        vts = self.take([128, 1024], 2)
        xos = self.take([128, 16, 128], 2)
        cks = self.take([128, 16, 64], 2)
        cko = self.take([128, 8, 128], 2)
        gq = self.df_g
        for t in range(TT // 128):
            t0 = t * 128
            x = xs.get()
            self.ld(x[:], self.ktok[t0:t0 + 128, :].rearrange("t (g d) -> t g d", g=32), r=[self.proj_tk[t]], w=[x])
            sq = sqs.get()
            self.tt('pool', sq[:], x[:], x[:], ALU.mult, r=[x], w=[sq])
            ss = sss.get()
            self.S.op('dve', lambda E, ss=ss, sq=sq: E.tensor_reduce(out=ss[:, 0:32], in_=sq[:], axis=AX.X, op=ALU.add), reads=[sq], writes=[ss])
            self.act(ss[:, 0:32], ss[:, 0:32], AF.Sqrt, r=[ss, self.epsb], w=[ss], scale=1.0 / 64, bias=self.epsb[:, 0:1])
            self.recip(ss[:, 32:64], ss[:, 0:32], r=[ss], pw=[ss])
            self.tt('dve', x[:], x[:], ss[:, 32:64].unsqueeze(2).to_broadcast([128, 32, 64]), ALU.mult, r=[x, ss], w=[x])
            self.tt('pool', x[:, 0:16, :], x[:, 0:16, :], gq[:, 0, :].unsqueeze(1).to_broadcast([128, 16, 64]), ALU.mult, r=[x, gq], w=[x])
            self.tt('dve', x[:, 16:32, :], x[:, 16:32, :], gq[:, 1, :].unsqueeze(1).to_broadcast([128, 16, 64]), ALU.mult, r=[x, gq], w=[x])
            if t0 < NP * TP:
                p, tl = t0 // TP, t0 % TP
                self.ld(self.newk[p, :, :, tl:tl + 128, :].rearrange("h m t d -> t (h m) d"), x[:, 16:32, :], r=[x], pw=[self.Oout], eng='pool')
                vt = vts.get()
                self.ld(vt[:], self.vtok[t0:t0 + 128, :], r=[self.proj_tk[t]], w=[vt])
                self.ld(self.newv[p, :, tl:tl + 128, :].rearrange("h t e -> t h e"), vt[:].rearrange("t (h e) -> t h e", h=8), r=[vt], pw=[self.Oout], eng='pool')
                src = x
            else:
                cs = css.get()
                self.ld(cs[:], self.rope_cs[t0 - NP * TP:t0 - NP * TP + 128, :], r=[self.Win], w=[cs])
                X = x[:].rearrange("t g (a f r) -> t g a f r", a=2, f=2)
                xa = X[:, :, :, 0, :]
                xb_ = X[:, :, :, 1, :]
                cosb = cs[:, 0:32].rearrange("t (a r) -> t a r", a=2).unsqueeze(1).to_broadcast([128, 32, 2, 16])
                sinb = cs[:, 32:64].rearrange("t (a r) -> t a r", a=2).unsqueeze(1).to_broadcast([128, 32, 2, 16])
                o_ = xr.get()
                O = o_[:].rearrange("t g (a f r) -> t g a f r", a=2, f=2)
                a1, a2, a3 = r1.get(), r2.get(), r3.get()
                self.tt('dve', a1[:], xa, cosb, ALU.mult, r=[x, cs], w=[a1])
                self.tt('pool', a2[:], xb_, sinb, ALU.mult, r=[x, cs], w=[a2])
                self.tt('dve', O[:, :, :, 0, :], a1[:], a2[:], ALU.subtract, r=[a1, a2], w=[o_])
                self.tt('pool', a3[:], xa, sinb, ALU.mult, r=[x, cs], w=[a3])
                self.tt('dve', a1[:], xb_, cosb, ALU.mult, r=[x, cs], w=[a1])
                self.tt('pool', O[:, :, :, 1, :], a3[:], a1[:], ALU.add, r=[a3, a1], pw=[o_])
                src = o_
            xo = xos.get()
            for g in range(4):
                pp = self.pnext()
                for k in range(4):
                    ch = g * 4 + k
                    self.tr_(pp[:, k * 128:(k + 1) * 128], src[:, 2 * ch:2 * ch + 2, :].rearrange("t a d -> t (a d)"), r=[src], **wr(pp, k == 0))
                self.cp('act' if g % 2 else 'dve', xo[:, g * 4:(g + 1) * 4, :], pp[:, :].rearrange("p (a b) -> p a b", a=4), r=[pp], **wr(xo, g == 0))
            self.ld(self.qkT_v[:, 0:16, t0:t0 + 128], xo[:], r=[xo], w=[self.prep_tk[t]], eng='pool')
        for kt in range(2):
            ck = cks.get()
            self.ld(ck[:], self.ctx_k[:, :, kt * 128:(kt + 1) * 128, :].rearrange("h m t d -> t (h m) d"), r=[self.Win], w=[ck])
            co = cko.get()
            for g in range(2):
                pp = self.pnext()
                for k in range(4):
                    ch = g * 4 + k
                    self.tr_(pp[:, k * 128:(k + 1) * 128], ck[:, 2 * ch:2 * ch + 2, :].rearrange("t a d -> t (a d)"), r=[ck], **wr(pp, k == 0))
                self.cp('act' if g % 2 else 'dve', co[:, g * 4:(g + 1) * 4, :], pp[:, :].rearrange("p (a b) -> p a b", a=4), r=[pp], **wr(co, g == 0))
            self.ld(self.ctxkT.rearrange("c p t -> p c t")[:, :, kt * 128:(kt + 1) * 128], co[:], r=[co], **wr(self.ctx_tk, kt == 0), eng='pool')

    def attn_core(self):
        self.areset()
        NKT = (TS + 256) // 128
        bf = self.bf
        MD = BF16 if bf else F32
        qTs = self.take([128, TS], 2, MD)
        kTs = self.take([128, TS + 256], 2, MD)
        V1s = self.take([128, NKT, 129], 2, MD)
        for V1 in V1s.t:
            self.memset('pool', V1[:, :, 128:129], 1.0, pw=[V1])
        PTs = self.take([128, NKT, 512], 2, MD)
        if bf:
            q32 = self.take([128, TS], 1)
            k32 = self.take([128, TS + 256], 1)
            v32 = self.take([128, NKT, 128], 1)
        obs = self.take([128, 4, 128], 2)
        rvs = self.take([128, 2], 4)
        seqs = [(p * TP, TP, False) for p in range(NP)] + [(NP * TP, TS, True)]
        for (tok0, T, is_s) in seqs:
            nk = T + (256 if is_s else 0)
            nkt = nk // 128
            QB = min(512, T)
            tks = self.prep_tk[tok0 // 128:(tok0 + T) // 128]
            ptk = self.proj_tk[tok0 // 128:(tok0 + T) // 128]
            for h in range(8):
                qT = qTs.get()
                kT = kTs.get()
                V1 = V1s.get()
                if bf:
                    qd, kd, vd = q32.get(), k32.get(), v32.get()
                else:
                    qd, kd, vd = qT, kT, V1
                self.ld(qd[:, 0:T], self.qkT_v[:, h, tok0:tok0 + T], r=tks, w=[qd])
                self.ld(kd[:, 0:T], self.qkT_v[:, 8 + h, tok0:tok0 + T], r=tks, w=[kd], eng='pool')
                self.ld(vd[:, 0:T // 128, 0:128], self.vtok[tok0:tok0 + T, h * 128:(h + 1) * 128].rearrange("(n p) e -> p n e", p=128), r=ptk, **wr(vd, bf))
                if is_s:
                    self.ld(kd[:, T:T + 256], self.ctxkT[h], r=[self.ctx_tk], pw=[kd], eng='pool')
                    self.ld(vd[:, T // 128:nkt, 0:128], self.ctx_v[h].rearrange("(n p) e -> p n e", p=128), r=[self.Win], pw=[vd])
                if bf:
                    self.cp('pool', qT[:, 0:T], qd[:, 0:T], r=[qd], w=[qT])
                    self.cp('pool', kT[:, 0:nk], kd[:, 0:nk], r=[kd], w=[kT])
                    self.cp('dve', V1[:, 0:nkt, 0:128], vd[:, 0:nkt, :], r=[vd], pw=[V1])
                for qb in range(T // QB):
                    ob = obs.get()
                    for m in range(2):
                        PT = PTs.get()
                        for kt in range(nkt):
                            pS = self.pnext()
                            self.mm(pS[:, 0:QB], kT[m * 64:(m + 1) * 64, kt * 128:(kt + 1) * 128], qT[m * 64:(m + 1) * 64, qb * QB:(qb + 1) * QB],
                                    True, True, r=[kT, qT], w=[pS])
                            self.act(PT[:, kt, 0:QB], pS[:, 0:QB], AF.Exp, r=[pS], **wr(PT, kt == 0), scale=0.125)
                        for qs in range(QB // 128):
                            pO = self.pnext()
                            for kt in range(nkt):
                                self.mm(pO[:, 0:129], PT[:, kt, qs * 128:(qs + 1) * 128], V1[:, kt, :], kt == 0, kt == nkt - 1, r=[PT, V1], **wr(pO, kt == 0))
                            rv = rvs.get()
                            self.recip(rv[:, 0:1], pO[:, 128:129], r=[pO], w=[rv])
                            if m == 0:
                                self.ts('dve', ob[:, qs, :], pO[:, 0:128], rv[:, 0:1], None, ALU.mult, r=[pO, rv], **wr(ob, qs == 0))
                            else:
                                self.tt('dve', rv[:, 1:2], rv[:, 0:1], self.nlam[:, 3:4], ALU.mult, r=[rv, self.nlam], pw=[rv])
                                self.stt('dve', ob[:, qs, :], pO[:, 0:128], rv[:, 1:2], ob[:, qs, :], ALU.mult, ALU.add, r=[pO, rv, ob], pw=[ob])
                    q0 = tok0 + qb * QB
                    nq = QB // 128
                    htk = self.hdir_tk[0][q0 // 64:(q0 + QB) // 64]
                    self.ld(self.hdir[0][q0:q0 + QB, h * 128:(h + 1) * 128].rearrange("(n p) e -> p n e", p=128), ob[:, 0:nq, :], r=[ob], pw=htk, eng='pool')

    def mixer_post(self, i, w_out, ng_tk, ng_bc, hdir, gate):
        self.areset()
        NT = 512
        if self.bf:
            W = self.take([128, 8, D], None, BF16)
            self.wstage = self.take([128, 8, 512], 2)
            self.load_w16(W, w_out.rearrange("(kc p) n -> p kc n", p=128), D)
        else:
            W = self.take([128, 8, D])
            self.ld(W[:], w_out.rearrange("(kc p) n -> p kc n", p=128), r=[self.Win], w=[W])
        hfs = self.take([128, 8, 128], 2)
        hbs = self.take([128, 8, 128], 2)
        ogs = self.take([128, 8, 128], 2)
        sqs = self.take([128, 8, 128], 2)
        sss = self.take([128, 16], 2)
        yTs = self.take([128, 8, NT], 2, BF16 if self.bf else F32)
        xbs = self.take([128, 8, NT], 2)
        for blk in range(TT // NT):
            c = 0 if blk < (NP * TP) // NT else 1
            tks = self.xT_tk[blk * 4:(blk + 1) * 4]
            xb = xbs.get()
            self.ld(xb[:], self.xT_v[:, :, blk * NT:(blk + 1) * NT], r=tks, w=[xb], eng='pool')
            yT = yTs.get()
            for q in range(4):
                t0 = blk * NT + q * 128
                hf = hfs.get()
                hb = hbs.get()
                og = ogs.get()
                self.ld(hf[:], hdir[0][t0:t0 + 128, :].rearrange("t (h e) -> t h e", h=8), r=self.hdir_tk[0][t0 // 64:t0 // 64 + 2], w=[hf])
                if gate is not None:
                    self.ld(hb[:], hdir[1][t0:t0 + 128, :].rearrange("t (h e) -> t h e", h=8), r=self.hdir_tk[1][t0 // 64:t0 // 64 + 2], w=[hb], eng='pool')
                    self.ld(og[:], self.otok[t0:t0 + 128, :].rearrange("t (h e) -> t h e", h=8), r=[self.proj_tk[t0 // 128]], w=[og])
                    self.tt('dve', hf[:], hf[:], hb[:], ALU.add, r=[hf, hb], w=[hf])
                sq = sqs.get()
                self.tt('pool', sq[:], hf[:], hf[:], ALU.mult, r=[hf], w=[sq])
                ss = sss.get()
                self.S.op('dve', lambda E, ss=ss, sq=sq: E.tensor_reduce(out=ss[:, 0:8], in_=sq[:], axis=AX.X, op=ALU.add), reads=[sq], writes=[ss])
                self.act(ss[:, 0:8], ss[:, 0:8], AF.Sqrt, r=[ss, self.epsb], w=[ss], scale=1.0 / 128, bias=self.epsb[:, 0:1])
                self.recip(ss[:, 8:16], ss[:, 0:8], r=[ss], pw=[ss])
                ngb = ng_bc.unsqueeze(1).to_broadcast([128, 8, 128])
                if gate == 'sigmoid':
                    self.act(og[:], og[:], AF.Sigmoid, r=[og], w=[og])
                    self.tt('pool', og[:], og[:], ngb, ALU.mult, r=[og, ng_tk], w=[og])
                elif gate == 'silu':
                    self.act(og[:], og[:], AF.Silu, r=[og], w=[og])
                    self.tt('pool', og[:], og[:], ngb, ALU.mult, r=[og, ng_tk], w=[og])
                else:
                    self.cp('pool', og[:], ngb, r=[ng_tk], w=[og])
                self.tt('dve', hf[:], hf[:], ss[:, 8:16].unsqueeze(2).to_broadcast([128, 8, 128]), ALU.mult, r=[hf, ss], w=[hf])
                self.tt('dve', hf[:], hf[:], og[:], ALU.mult, r=[hf, og], w=[hf])
                for hh in range(2):
                    p = self.pnext()
                    for k in range(4):
                        self.tr_(p[:, k * 128:(k + 1) * 128], hf[:, hh * 4 + k, :], r=[hf], **wr(p, k == 0))
                    dst = yT[:, hh * 4:(hh + 1) * 4, q * 128:(q + 1) * 128]
                    src = p[:, :].rearrange("p (a b) -> p a b", a=4)
                    self.cp('act' if hh else 'dve', dst, src, r=[p], **wr(yT, q == 0 and hh == 0))
            for oc in range(8):
                p = self.pnext()
                for kc in range(8):
                    self.mm(p[:, :], W[:, kc, oc * 128:(oc + 1) * 128], yT[:, kc, :], kc == 0, kc == 7, r=[W, yT], **wr(p, kc == 0))
                self.stt('dve', xb[:, oc, :], p[:, :], self.mod[:, 16 + oc, c:c + 1], xb[:, oc, :], ALU.mult, ALU.add,
                         r=[p, self.mod, xb], pw=[xb])
            for q in range(4):
                self.ld(self.xT_v[:, :, blk * NT + q * 128: blk * NT + (q + 1) * 128], xb[:, :, q * 128:(q + 1) * 128],
                        r=[xb], w=[tks[q]], eng='pool')

    def tr_(self, out, in_, r=(), w=(), pw=()):
        n = in_.shape[0]
        idn = self.cst.ap[0:n, 0, 0:n]
        self.S.op('pe', lambda E: E.transpose(out, in_, idn), reads=list(r) + [self.cst], writes=w, pw=pw)

    def stage_in(self):
        self.areset()
        xin = self.take([128, D], 2)
        xo = self.take([128, 8, 128], 2)
        for t in range(TT // 128):
            a = xin.get()
            self.ld(a[:], self.x_tok[t * 128:(t + 1) * 128, :], r=[self.Xtok], w=[a])
            b = xo.get()
            for h in range(2):
                p = self.pnext()
                for k in range(4):
                    kc = h * 4 + k
                    self.tr_(p[:, k * 128:(k + 1) * 128], a[:, kc * 128:(kc + 1) * 128], r=[a], w=[p] if k == 0 else (), pw=() if k == 0 else [p])
                dst = b[:, h * 4:(h + 1) * 4, :]
                src = p[:, :].rearrange("p (a b) -> p a b", a=4)
                if h == 0:
                    self.cp('dve', dst, src, r=[p], w=[b])
                else:
                    self.cp('act', dst, src, r=[p], pw=[b])
            self.ld(self.xT_v[:, :, t * 128:(t + 1) * 128], b[:], r=[b], w=[self.xT_tk[t]], eng='pool')

    def stage_out(self):
        self.areset()
        xi = self.take([128, 8, 128], 2)
        yo = self.take([128, D], 2)
        for t in range(TT // 128):
            a = xi.get()
            self.ld(a[:], self.xT_v[:, :, t * 128:(t + 1) * 128], r=[self.xT_tk[t]], w=[a])
            b = yo.get()
            for h in range(2):
                p = self.pnext()
                for k in range(4):
                    kc = h * 4 + k
                    self.tr_(p[:, k * 128:(k + 1) * 128], a[:, kc, :], r=[a], w=[p] if k == 0 else (), pw=() if k == 0 else [p])
                if h == 0:
                    self.cp('dve', b[:, 0:512], p[:, :], r=[p], w=[b])
                else:
                    self.cp('act', b[:, 512:1024], p[:, :], r=[p], pw=[b])
            self.ld(self.y_tok[t * 128:(t + 1) * 128, :], b[:], r=[b], w=[self.Ytok], eng='pool')

    def stage_mod(self, i):
        self.areset()
        wt = self.take([128, 8, 512], 2)
        wv = self.ada_w[i].rearrange("(kc p) n -> p kc n", p=128)
        mp = self.pnext()
        for n in range(12):
            w = wt.get()
            self.ld(w[:], wv[:, :, n * 512:(n + 1) * 512], r=[self.Win], w=[w])
            for jj in range(4):
                j = n * 4 + jj
                for kc in range(8):
                    self.mm(mp[:, 2 * j:2 * j + 2], w[:, kc, jj * 128:(jj + 1) * 128], self.sc[:, kc, :], kc == 0, kc == 7,
                            r=[w, self.sc], **wr(mp, j == 0 and kc == 0))
        mpv = mp[:, 0:96].rearrange("p (j c) -> p j c", c=2)
        for c in range(2):
            self.tt('dve', self.mod[:, :, c], mpv[:, :, c], self.adab[:, i, :], ALU.add, r=[mp, self.adab],
                    w=[self.mod] if c == 0 else (), pw=() if c == 0 else [self.mod])
        for wi in range(2):
            sj = 8 + 24 * wi
            for c in range(2):
                first = (wi == 0 and c == 0)
                self.stt('dve', self.modA[:, wi, :, c], self.mod[:, sj:sj + 8, c], 1.0, self.normg[:, i, wi, :], ALU.add, ALU.mult,
                         r=[self.mod, self.normg], w=[self.modA] if first else (), pw=() if first else [self.modA])

    def norm_mod(self, xb, hb, wi, c, nt, sq=None):
        sj = 24 * wi
        self.act(hb[:, :, :], xb[:, :, :], AF.Square, r=[xb], w=[hb])
        p = self.pnext()
        for kc in range(8):
            self.mm(p[:, 0:nt], self.cst.ap[:, 1, :], hb[:, kc, :], kc == 0, kc == 7, r=[self.cst, hb], **wr(p, kc == 0))
        rs = self.rstd.get()
        self.act(rs[:, 0:nt], p[:, 0:nt], AF.Sqrt, r=[p, self.epsb], w=[rs], scale=1.0 / D, bias=self.epsb[:, 0:1])
        self.recip(rs[:, 0:nt], rs[:, 0:nt], r=[rs], w=[rs])
        for kc in range(8):
            self.tt('dve' if kc % 2 == 0 else 'pool', hb[:, kc, :], xb[:, kc, :], rs[:, 0:nt], ALU.mult, r=[xb, rs],
                    w=[hb] if kc == 0 else (), pw=() if kc == 0 else [hb])
        for kc in range(8):
            self.act(hb[:, kc, :], hb[:, kc, :], AF.Identity, r=[hb, self.modA, self.mod], pw=[hb],
                     scale=self.modA[:, wi, kc, c:c + 1], bias=self.mod[:, sj + kc, c:c + 1])

    def norm_mod2(self, xtk, xap, htk, hap, tmp, wi, c, nt, first):
        sj = 24 * wi
        self.act(tmp[:, :, 0:nt], xap, AF.Square, r=[xtk], w=[tmp])
        p = self.pnext()
        for kc in range(8):
            self.mm(p[:, 0:nt], self.cst.ap[:, 1, :], tmp[:, kc, 0:nt], kc == 0, kc == 7, r=[self.cst, tmp], **wr(p, kc == 0))
        rs = self.rstd.get()
        self.act(rs[:, 0:nt], p[:, 0:nt], AF.Sqrt, r=[p, self.epsb], w=[rs], scale=1.0 / D, bias=self.epsb[:, 0:1])
        self.recip(rs[:, 0:nt], rs[:, 0:nt], r=[rs], w=[rs])
        for kc in range(8):
            self.tt('dve' if kc % 2 == 0 else 'pool', tmp[:, kc, 0:nt], xap[:, kc, :], rs[:, 0:nt], ALU.mult, r=[xtk, rs], **wr(tmp, kc == 0))
        for kc in range(8):
            self.act(hap[:, kc, :], tmp[:, kc, 0:nt], AF.Identity, r=[tmp, self.modA, self.mod], **wr(htk, first and kc == 0),
                     scale=self.modA[:, wi, kc, c:c + 1], bias=self.mod[:, sj + kc, c:c + 1])

    def stage_ffn16(self, i):
        self.areset()
        SB = 1024
        NH = SB // 512
        xbs = self.take([128, 8, SB], 1)
        hbs = self.take([128, 8, SB], 1, BF16)
        sq = self.take([128, 8, 512])
        self.rstd = self.take([128, 512], 2)
        acts = self.take([128, 22, SB], 1, BF16)
        wst = self.take([128, 8, 2, 128], 3)
        w16 = self.take([128, 8, 2, 128], 3, BF16)
        wost = self.take([128, 22, 128], 2)
        wo16 = self.take([128, 22, 128], 2, BF16)
        sg = self.take([128, 512], 2)
        wiv = self.ffn_w_in[i].rearrange("(kc p) n -> p kc n", p=128)
        wov = self.ffn_w_out[i].rearrange("(kc p) n -> p kc n", p=128)
        for sb in range(TT // SB):
            c = 0 if sb * SB < NP * TP else 1
            tks = self.xT_tk[sb * 8:(sb + 1) * 8]
            xb = xbs.get()
            self.ld(xb[:], self.xT_v[:, :, sb * SB:(sb + 1) * SB], r=tks, w=[xb], eng='pool')
            hb = hbs.get()
            for hf in range(NH):
                hs = slice(hf * 512, (hf + 1) * 512)
                self.norm_mod2(xb, xb[:, :, hs], hb, hb[:, :, hs], sq, 1, c, 512, hf == 0)
            at = acts.get()
            for j in range(22):
                ws = wst.get()
                self.ld(ws[:, :, 0, :], wiv[:, :, j * 128:(j + 1) * 128], r=[self.Win], w=[ws])
                self.ld(ws[:, :, 1, :], wiv[:, :, DFF + j * 128:DFF + (j + 1) * 128], r=[self.Win], pw=[ws])
                w = w16.get()
                self.cp('pool', w[:], ws[:], r=[ws], w=[w])
                for hf in range(NH):
                    hs = slice(hf * 512, (hf + 1) * 512)
                    pg = self.pnext()
                    pu = self.pnext()
                    for kc in range(8):
                        self.mm(pg[:, :], w[:, kc, 0, :], hb[:, kc, hs], kc == 0, kc == 7, r=[w, hb], **wr(pg, kc == 0))
                    for kc in range(8):
                        self.mm(pu[:, :], w[:, kc, 1, :], hb[:, kc, hs], kc == 0, kc == 7, r=[w, hb], **wr(pu, kc == 0))
                    s_ = sg.get()
                    self.act(s_[:], pg[:, :], AF.Silu, r=[pg], w=[s_])
                    self.tt('dve', at[:, j, hs], s_[:], pu[:, :], ALU.mult, r=[s_, pu], **wr(at, j == 0 and hf == 0))
            for oc in range(8):
                ws = wost.get()
                self.ld(ws[:], wov[:, :, oc * 128:(oc + 1) * 128], r=[self.Win], w=[ws])
                w = wo16.get()
                self.cp('pool', w[:], ws[:], r=[ws], w=[w])
                for hf in range(NH):
                    hs = slice(hf * 512, (hf + 1) * 512)
                    p = self.pnext()
                    for k2 in range(22):
                        self.mm(p[:, :], w[:, k2, :], at[:, k2, hs], k2 == 0, k2 == 21, r=[w, at], **wr(p, k2 == 0))
                    self.stt('dve', xb[:, oc, hs], p[:, :], self.mod[:, 40 + oc, c:c + 1], xb[:, oc, hs], ALU.mult, ALU.add,
                             r=[p, self.mod, xb], pw=[xb])
            for q in range(SB // 128):
                self.ld(self.xT_v[:, :, sb * SB + q * 128: sb * SB + (q + 1) * 128], xb[:, :, q * 128:(q + 1) * 128],
                        r=[xb], w=[tks[q]], eng='pool')

    def stage_ffn(self, i):
        self.areset()
        NT = 512
        xbs = self.take([128, 8, NT], 2)
        hbs = self.take([128, 8, NT], 1)
        self.rstd = self.take([128, NT], 2)
        acts = self.take([128, 22, NT], 1)
        sg = self.take([128, NT], 2)
        wins = self.take([128, 8, 2, 128], 3)
        wouts = self.take([128, 22, 128], 2)
        wiv = self.ffn_w_in[i].rearrange("(kc p) n -> p kc n", p=128)
        wov = self.ffn_w_out[i].rearrange("(kc p) n -> p kc n", p=128)
        for blk in range(TT // NT):
            c = 0 if blk < (NP * TP) // NT else 1
            tks = self.xT_tk[blk * 4:(blk + 1) * 4]
            xb = xbs.get()
            self.ld(xb[:], self.xT_v[:, :, blk * NT:(blk + 1) * NT], r=tks, w=[xb], eng='pool')
            hb = hbs.get()
            self.norm_mod(xb, hb, 1, c, NT)
            at = acts.get()
            for j in range(22):
                w = wins.get()
                self.ld(w[:, :, 0, :], wiv[:, :, j * 128:(j + 1) * 128], r=[self.Win], w=[w])
                self.ld(w[:, :, 1, :], wiv[:, :, DFF + j * 128:DFF + (j + 1) * 128], r=[self.Win], pw=[w])
                pg = self.pnext()
                pu = self.pnext()
                for kc in range(8):
                    self.mm(pg[:, :], w[:, kc, 0, :], hb[:, kc, :], kc == 0, kc == 7, r=[w, hb], **wr(pg, kc == 0))
                for kc in range(8):
                    self.mm(pu[:, :], w[:, kc, 1, :], hb[:, kc, :], kc == 0, kc == 7, r=[w, hb], **wr(pu, kc == 0))
                s = sg.get()
                self.act(s[:], pg[:, :], AF.Silu, r=[pg], w=[s])
                self.tt('dve', at[:, j, :], s[:], pu[:, :], ALU.mult, r=[s, pu], w=[at] if j == 0 else (), pw=() if j == 0 else [at])
            for oc in range(8):
                w = wouts.get()
                self.ld(w[:], wov[:, :, oc * 128:(oc + 1) * 128], r=[self.Win], w=[w])
                p = self.pnext()
                for k2 in range(22):
                    self.mm(p[:, :], w[:, k2, :], at[:, k2, :], k2 == 0, k2 == 21, r=[w, at], **wr(p, k2 == 0))
                self.stt('dve', xb[:, oc, :], p[:, :], self.mod[:, 40 + oc, c:c + 1], xb[:, oc, :], ALU.mult, ALU.add,
                         r=[p, self.mod, xb], pw=[xb])
            for q in range(4):
                self.ld(self.xT_v[:, :, blk * NT + q * 128: blk * NT + (q + 1) * 128], xb[:, :, q * 128:(q + 1) * 128],
                        r=[xb], w=[tks[q]], eng='pool')


def host_consts():
    c = np.zeros((128, 8, 128), np.float32)
    c[:, 0, :] = np.eye(128)
    c[:, 1, :] = 1.0
    k = np.arange(128)[:, None]
    t = np.arange(128)[None, :]
    c[:, 2, :] = (k <= t)
    c[:, 3, :] = (k >= t)
    c[:, 4, :] = (k > t)
    c[:, 5, :] = (k < t)
    return c.reshape(128, 1024)


def rope_tables():
    rows = TS // 64
    row = np.broadcast_to(np.arange(rows)[:, None], (rows, 64)).reshape(-1)
    col = np.broadcast_to(np.arange(64)[None, :], (rows, 64)).reshape(-1)
    inv = (np.float32(10000.0) ** (-np.arange(16, dtype=np.float32) / np.float32(16))).astype(np.float32)
    ang = np.stack([row, col], axis=-1).astype(np.float32)[:, :, None] * inv
    return np.concatenate([np.cos(ang).reshape(TS, 32), np.sin(ang).reshape(TS, 32)], axis=1).astype(np.float32)


_CACHE = {}


def kernel(**inp):
    opts = inp.pop('_opts', {})
    key = repr(sorted(opts.items()))
    if key not in _CACHE:
        P = Prog(opts)
        P.build()
        _CACHE[key] = P
    P = _CACHE[key]
    f = lambda a: np.ascontiguousarray(np.asarray(a, dtype=np.float32))
    xp = f(inp['x_prompt'])
    xs = f(inp['x_sample'])
    c = f(inp['c'])
    c_ctx = f(inp['c_ctx'])
    ada_b = f(inp['ada_b'])
    norm_g = f(inp['norm_g'])
    shared = {
        'consts': host_consts(),
        'ada_w': f(inp['ada_w']),
        'ada_bT': f(ada_b.reshape(4, 48, 128).transpose(2, 0, 1)),
        'normgT': f(norm_g.reshape(4, 2, 8, 128).transpose(3, 0, 1, 2)),
        'ffn_w_in': f(inp['ffn_w_in']),
        'ffn_w_out': f(inp['ffn_w_out']),
        'mlstm_w_in': f(inp['mlstm_w_in'][0]),
        'mlstm_w_out': f(inp['mlstm_w_out'][0]),
        'mlstm_gate_b': f(inp['mlstm_gate_b'].reshape(1, 32)),
        'mlstm_norm_g': f(inp['mlstm_norm_g'].reshape(1, 128)),
    }
    shared.update({
        'diff_w_in': f(inp['diff_w_in'][0]),
        'diff_w_out': f(inp['diff_w_out'][0]),
        'diff_qkg': f(np.concatenate([inp['diff_q_norm_g'][0], inp['diff_k_norm_g'][0]]).reshape(1, 128)),
        'diff_lambda': f(inp['diff_lambda'][0].reshape(1, 256)),
        'diff_subln_g': f(inp['diff_subln_g'][0].reshape(1, 128)),
        'rope_cs': rope_tables(),
    })
    cw = f(inp['gdn_conv_w'])
    shared.update({
        'gdn_w_in': f(inp['gdn_w_in']),
        'gdn_w_out': f(inp['gdn_w_out']),
        'gdn_convT': f(cw.reshape(2, 5, 24, 128).transpose(3, 0, 2, 1)),
        'gdn_a_log': f(inp['gdn_a_log'].reshape(2, 1, 16)),
        'gdn_dt_bias': f(inp['gdn_dt_bias'].reshape(2, 1, 16)),
        'gdn_norm_g': f(inp['gdn_norm_g'].reshape(2, 1, 128)),
    })
    stS = f(inp['state_delta'])
    ck = f(inp['cache_diff_k'])
    cv = f(inp['cache_diff_v'])
    stC = f(inp['state_mlstm_C'])
    stn = f(inp['state_mlstm_n'])
    stm = f(inp['state_mlstm_m'])
    in_maps = []
    for k in range(NCORE):
        m = dict(shared)
        m['x_tok'] = f(np.concatenate([xp[NP * k:NP * (k + 1)].reshape(NP * TP, D), xs[k]], axis=0))
        cond = np.stack([c_ctx, c[k]], axis=-1)
        m['condT'] = f(cond.reshape(8, 128, 2).transpose(1, 0, 2))
        m['st_S'] = f(stS[k])
        m['ctx_k'] = f(ck[k, 0])
        m['ctx_v'] = f(cv[k, 0])
        m['st_C'] = f(stC[k, 0])
        m['st_n'] = f(stn[k, 0].reshape(2, 8, 64, 1))
        m['st_m'] = f(stm[k, 0].reshape(1, 16))
        in_maps.append({n: m[n] for n in P.din})
    res = run_bass_kernel_spmd(P.nc, in_maps, core_ids=list(range(NCORE)))
    R = res.results
    y = np.stack([r['y_tok'] for r in R])
    y_prompt = y[:, :NP * TP].reshape(NCORE * NP, TP, D)
    y_sample = y[:, NP * TP:]
    outs = [y_prompt, y_sample]
    if 'newS' in P.dout:
        outs.append(np.stack([r['newS'] for r in R]).reshape(NCORE * NP, 2, 2, 8, 128, 128))
    if 'newC' in P.dout:
        outs.append(np.stack([r['newC'] for r in R]).reshape(NCORE * NP, 1, 2, 8, 64, 128))
        outs.append(np.stack([r['newn'] for r in R]).reshape(NCORE * NP, 1, 2, 8, 64))
        outs.append(np.stack([r['newm'] for r in R]).reshape(NCORE * NP, 1, 2, 8))
    if 'newk' in P.dout:
        outs.append(np.stack([r['newk'] for r in R]).reshape(NCORE * NP, 1, 8, 2, TP, 64))
        outs.append(np.stack([r['newv'] for r in R]).reshape(NCORE * NP, 1, 8, TP, 128))
    return tuple(outs)
```

```python
import numpy as np
from contextlib import ExitStack
import concourse.bass as bass
import concourse.mybir as mybir
from concourse.bass_utils import run_bass_kernel_spmd

F32 = mybir.dt.float32
BF16 = mybir.dt.bfloat16
AF = mybir.ActivationFunctionType
ALU = mybir.AluOpType
AX = mybir.AxisListType

ENGS = ('pe', 'act', 'dve', 'pool', 'sp')
NDS = 40

D = 1024
NCORE = 8
NP = 4
TP = 256
TS = 2048
TT = NP * TP + TS
DFF = 2816
EPS = 1e-6


class Tk:
    __slots__ = ('ap', 'lw', 'rd', 'rp', 'name')

    def __init__(self, ap, name=''):
        self.ap = ap
        self.lw = {}
        self.rd = {}
        self.rp = {}
        self.name = name

    def __getitem__(self, idx):
        return self.ap[idx]


class Sched:
    def __init__(self, nc, es):
        self.nc = nc
        self.es = es
        self.q = {e: [] for e in ENGS}
        self.sem = {e: es.enter_context(nc.semaphore("s_" + e)) for e in ENGS}
        self.cnt = {e: 0 for e in ENGS}
        self.seen = {e: {} for e in ENGS}
        self.dsem = [es.enter_context(nc.semaphore("d%d" % i)) for i in range(NDS)]
        self.dcnt = [0] * NDS
        self.dnext = 0
        self.nins = 0
        self.uid = 0

    def sb(self, shape, dt=F32, name=None):
        self.uid += 1
        name = name or "t%d" % self.uid
        t = self.es.enter_context(self.nc.sbuf_tensor(name, list(shape), dt))
        return Tk(t, name)

    def ps(self, shape, dt=F32, name=None):
        self.uid += 1
        name = name or "p%d" % self.uid
        t = self.es.enter_context(self.nc.psum_tensor(name, list(shape), dt))
        return Tk(t, name)

    def _wait(self, eng, d):
        k = d[0]
        if eng == 'pe' and k == ('e', 'pe'):
            return
        seen = self.seen[eng]
        if seen.get(k, 0) >= d[2]:
            return
        seen[k] = d[2]
        self.q[eng].append(lambda E, d=d: E.wait_ge(d[1], d[2]))
        self.nins += 1

    def _deps(self, eng, reads, writes, pw):
        deps = {}

        def add(d):
            k = d[0]
            if k not in deps or deps[k][2] < d[2]:
                deps[k] = d
        for t in reads:
            for d in t.lw.values():
                add(d)
        for t in writes:
            for d in t.lw.values():
                add(d)
            for d in t.rd.values():
                add(d)
        for t in pw:
            for d in t.rd.values():
                add(d)
            for d in t.rp.values():
                add(d)
        for d in deps.values():
            self._wait(eng, d)

    def _mark(self, me, reads, writes, pw):
        for t in reads:
            t.rd[me[0]] = me
        for t in writes:
            t.lw = {me[0]: me}
            t.rp = t.rd
            t.rd = {}
        for t in pw:
            t.lw[me[0]] = me

    def op(self, eng, fn, reads=(), writes=(), pw=()):
        self._deps(eng, reads, writes, pw)
        self.cnt[eng] += 1
        sem = self.sem[eng]
        me = (('e', eng), sem, self.cnt[eng])
        self.q[eng].append(lambda E: fn(E).then_inc(sem, 1))
        self.nins += 1
        self._mark(me, reads, writes, pw)

    def dma(self, eng, out_ap, in_ap, reads=(), writes=(), pw=()):
        slot = self.dnext
        self.dnext = (slot + 1) % NDS
        ds = self.dsem[slot]
        self._deps(eng, reads, writes, pw)
        if self.dcnt[slot] > 0:
            self._wait(eng, (('d', slot), ds, 16 * self.dcnt[slot]))
        self.dcnt[slot] += 1
        me = (('d', slot), ds, 16 * self.dcnt[slot])
        self.q[eng].append(lambda E: E.dma_start(out=out_ap, in_=in_ap).then_inc(ds, 16))
        self.nins += 1
        self._mark(me, reads, writes, pw)

    def barrier(self):
        for e in ENGS:
            for o in ENGS:
                if o != e and self.cnt[o] > 0:
                    self._wait(e, (('e', o), self.sem[o], self.cnt[o]))
            for i in range(NDS):
                if self.dcnt[i] > 0:
                    self._wait(e, (('d', i), self.dsem[i], 16 * self.dcnt[i]))

    def emit(self):
        self.barrier()
        q = self.q
        with self.nc.Block() as block:
            @block.tensor
            def _(E):
                for f in q['pe']:
                    f(E)

            @block.scalar
            def _(E):
                for f in q['act']:
                    f(E)

            @block.vector
            def _(E):
                for f in q['dve']:
                    f(E)

            @block.gpsimd
            def _(E):
                for f in q['pool']:
                    f(E)

            @block.sync
            def _(E):
                for f in q['sp']:
                    f(E)


def wr(t, first):
    return {'w': [t]} if first else {'pw': [t]}


class Rot:
    def __init__(self, tiles):
        self.t = tiles
        self.i = 0

    def get(self):
        t = self.t[self.i % len(self.t)]
        self.i += 1
        return t


ARENA_COLS = 50000


class Prog:
    def __init__(self, opts):
        self.opts = opts
        self.nc = bass.Bass("TRN2", target_bir_lowering=False)
        self.es = ExitStack()
        self.din = {}
        self.dout = {}

    def inp(self, name, shape):
        t = self.nc.dram_tensor(name, list(shape), F32, kind="ExternalInput").ap()
        self.din[name] = t
        return t

    def outp(self, name, shape):
        t = self.nc.dram_tensor(name, list(shape), F32, kind="ExternalOutput").ap()
        self.dout[name] = t
        return t

    def scratch(self, name, shape):
        return self.nc.dram_tensor(name, list(shape), F32, kind="Internal").ap()

    def areset(self):
        self.S.barrier()
        self.apos = 0

    def take(self, shape, n=None, dt=F32):
        cols = int(np.prod(shape[1:]))
        c32 = cols if dt == F32 else (cols + 1) // 2
        out = []
        for _ in range(n or 1):
            assert self.apos + c32 <= ARENA_COLS, ("arena overflow", self.apos, c32)
            ap = self.arena[0:shape[0], self.apos:self.apos + c32]
            if dt != F32:
                ap = ap.bitcast(dt)[:, 0:cols]
            if len(shape) == 3:
                ap = ap.rearrange("p (a b) -> p a b", a=shape[1])
            elif len(shape) == 4:
                ap = ap.rearrange("p (a b c) -> p a b c", a=shape[1], b=shape[2])
            self.apos += c32
            out.append(Tk(ap))
        return out[0] if n is None else Rot(out)

    def pnext(self):
        p = self.psum[self.pi % 8]
        self.pi += 1
        return p

    def pns(self):
        return self.pnext()

    def mm(self, out, lhsT, rhs, start, stop, r=(), w=(), pw=()):
        self.S.op('pe', lambda E: E.matmul(out, lhsT=lhsT, rhs=rhs, start=start, stop=stop), reads=r, writes=w, pw=pw)

    def tr(self, out, in_, r=(), w=(), pw=()):
        ident = self.ident
        n = in_.shape[0]
        self.S.op('pe', lambda E: E.transpose(out, in_, ident[0:n, 0:n]), reads=list(r) + [ident], writes=w, pw=pw)

    def act(self, out, in_, func, r=(), w=(), pw=(), bias=None, scale=None, accum=None):
        kw = {}
        if bias is not None:
            kw['bias'] = bias
        if scale is not None:
            kw['scale'] = scale
        if accum is not None:
            kw['accum_out'] = accum
        self.S.op('act', lambda E: E.activation(out=out, in_=in_, func=func, **kw), reads=r, writes=w, pw=pw)

    def tt(self, eng, out, a, b, op, r=(), w=(), pw=()):
        self.S.op(eng, lambda E: E.tensor_tensor(out=out, in0=a, in1=b, op=op), reads=r, writes=w, pw=pw)

    def ts(self, eng, out, a, s1, s2, op0, op1=None, r=(), w=(), pw=()):
        if op1 is None:
            self.S.op(eng, lambda E: E.tensor_scalar(out=out, in0=a, scalar1=s1, scalar2=None, op0=op0), reads=r, writes=w, pw=pw)
        else:
            self.S.op(eng, lambda E: E.tensor_scalar(out=out, in0=a, scalar1=s1, scalar2=s2, op0=op0, op1=op1), reads=r, writes=w, pw=pw)

    def stt(self, eng, out, a, s, b, op0, op1, r=(), w=(), pw=()):
        self.S.op(eng, lambda E: E.scalar_tensor_tensor(out=out, in0=a, scalar=s, in1=b, op0=op0, op1=op1), reads=r, writes=w, pw=pw)

    def cp(self, eng, out, in_, r=(), w=(), pw=()):
        if eng == 'act':
            self.S.op('act', lambda E: E.copy(out=out, in_=in_), reads=r, writes=w, pw=pw)
        else:
            self.S.op(eng, lambda E: E.tensor_copy(out=out, in_=in_), reads=r, writes=w, pw=pw)

    def recip(self, out, in_, r=(), w=(), pw=()):
        self.S.op('dve', lambda E: E.reciprocal(out=out, in_=in_), reads=r, writes=w, pw=pw)

    def memset(self, eng, ap, val, w=(), pw=()):
        self.S.op(eng, lambda E: E.memset(ap, val), writes=w, pw=pw)

    def ld(self, out, in_, r=(), w=(), pw=(), eng='sp'):
        self.S.dma(eng, out, in_, reads=r, writes=w, pw=pw)

    def build(self):
        nc = self.nc
        o = self.opts
        with self.es:
            S = self.S = Sched(nc, self.es)
            self.arena = self.es.enter_context(nc.sbuf_tensor("arena", [128, ARENA_COLS], F32))
            self.psum = [S.ps([128, 512], name="ps%d" % i) for i in range(8)]
            self.pi = 0
            self.apos = 0
            self.psmall = [Tk(self.psum[i // 2].ap[:, (i % 2) * 256:(i % 2) * 256 + 256]) for i in range(16)]
            self.psi = 0
            self.x_tok = self.inp("x_tok", [TT, D])
            self.y_tok = self.outp("y_tok", [TT, D])
            self.xT = self.scratch("xT", [8, 128, TT])
            self.xT_v = self.xT.rearrange("c p t -> p c t")
            self.xT_tk = [Tk(None, "xT%d" % i) for i in range(TT // 128)]
            self.Xtok = Tk(None)
            self.Ytok = Tk(None)
            self.Win = Tk(None)
            consts = self.inp("consts", [128, 8 * 128])
            condT = self.inp("condT", [128, 8, 2])
            self.ada_w = self.inp("ada_w", [4, D, 6 * D])
            ada_bT = self.inp("ada_bT", [128, 4, 48])
            normgT = self.inp("normgT", [128, 4, 2, 8])
            self.ffn_w_in = self.inp("ffn_w_in", [4, D, 2 * DFF])
            self.ffn_w_out = self.inp("ffn_w_out", [4, DFF, D])
            self.cst = S.sb([128, 8, 128], name="cst")
            self.ld(self.cst[:], consts.rearrange("p (a b) -> p a b", a=8), r=[self.Win], w=[self.cst])
            self.ones = self.cst.ap[:, 1, :]
            self.sc = S.sb([128, 8, 2], name="sc")
            self.ld(self.sc[:], condT, r=[self.Win], w=[self.sc])
            self.act(self.sc[:], self.sc[:], AF.Silu, r=[self.sc], w=[self.sc])
            self.adab = S.sb([128, 4, 48], name="adab")
            self.ld(self.adab[:], ada_bT, r=[self.Win], w=[self.adab])
            self.normg = S.sb([128, 4, 2, 8], name="normg")
            self.ld(self.normg[:], normgT, r=[self.Win], w=[self.normg])
            self.mod = S.sb([128, 48, 2], name="mod")
            self.modA = S.sb([128, 2, 8, 2], name="modA")
            self.epsb = S.sb([128, 1], name="epsb")
            self.memset('pool', self.epsb[:], EPS, w=[self.epsb])

            self.stage_in()
            self.bf = o.get('bf16', True)
            mixers = o.get('mixers', (0, 1, 2))
            self.setup_mixers(mixers)
            for i in range(o.get('depth', 4)):
                self.stage_mod(i)
                if i % 3 == 0 and 0 in mixers:
                    self.gdn(i)
                if i % 3 == 1 and 1 in mixers:
                    self.mlstm(i)
                if i % 3 == 2 and 2 in mixers:
                    self.diffattn(i)
                if o.get('ffn', True):
                    if self.bf:
                        self.stage_ffn16(i)
                    else:
                        self.stage_ffn(i)
            self.stage_out()
            S.emit()
        return nc

    def setup_mixers(self, mixers):
        S = self.S
        if 1 in mixers:
            self.ml_w_in = self.inp("mlstm_w_in", [D, 3104])
            self.ml_w_out = self.inp("mlstm_w_out", [D, D])
            ml_gb = self.inp("mlstm_gate_b", [1, 32])
            ml_ng = self.inp("mlstm_norm_g", [1, 128])
            self.st_C = self.inp("st_C", [2, 8, 64, 128])
            self.st_n = self.inp("st_n", [2, 8, 64, 1])
            self.st_m = self.inp("st_m", [1, 16])
            self.newC = self.outp("newC", [NP, 2, 8, 64, 128])
            self.newn = self.outp("newn", [NP, 2, 8, 64, 1])
            self.newm = self.outp("newm", [NP, 2, 8, 1])
            self.ml_gb = S.sb([128, 32], name="ml_gb")
            self.ld(self.ml_gb[:], ml_gb.partition_broadcast(128), r=[self.Win], w=[self.ml_gb])
            self.ml_ng = S.sb([128, 128], name="ml_ng")
            self.ld(self.ml_ng[:], ml_ng.partition_broadcast(128), r=[self.Win], w=[self.ml_ng])
        self.Oout = Tk(None)
        self.qkT = self.scratch("qkT", [24, 128, TT])
        self.qkT_v = self.qkT.rearrange("c p t -> p c t")
        self.qkT_h = self.qkT[0:8].rearrange("c (two p) t -> p (c two) t", two=2)
        self.ktok = self.scratch("ktok", [TT, 2048])
        self.vtok = self.scratch("vtok", [TT, 1024])
        self.otok = self.scratch("otok", [TT, 1024])
        self.gtok = self.scratch("gtok", [TT, 32])
        self.hdir = [self.scratch("hdir%d" % d, [TT, 1024]) for d in range(2)]
        self.proj_tk = [Tk(None) for _ in range(TT // 128)]
        self.prep_tk = [Tk(None) for _ in range(TT // 128)]
        self.hdir_tk = [[Tk(None) for _ in range(TT // 64)] for d in range(2)]
        if 2 in mixers:
            self.df_w_in = self.inp("diff_w_in", [D, 3072])
            self.df_w_out = self.inp("diff_w_out", [D, D])
            df_g = self.inp("diff_qkg", [1, 128])
            df_lam = self.inp("diff_lambda", [1, 256])
            df_sg = self.inp("diff_subln_g", [1, 128])
            self.rope_cs = self.inp("rope_cs", [TS, 64])
            self.ctx_k = self.inp("ctx_k", [8, 2, 256, 64])
            self.ctx_v = self.inp("ctx_v", [8, 256, 128])
            self.newk = self.outp("newk", [NP, 8, 2, TP, 64])
            self.newv = self.outp("newv", [NP, 8, TP, 128])
            self.df_g = S.sb([128, 2, 64], name="df_g")
            self.ld(self.df_g[:], df_g.rearrange("o (a b) -> o a b", a=2).partition_broadcast(128), r=[self.Win], w=[self.df_g])
            self.df_sg = S.sb([128, 128], name="df_sg")
            self.ld(self.df_sg[:], df_sg.partition_broadcast(128), r=[self.Win], w=[self.df_sg])
            lam_init = 0.8 - 0.6 * float(np.exp(-0.3 * 2))
            self.ts('dve', self.df_sg[:], self.df_sg[:], 1.0 - lam_init, None, ALU.mult, r=[self.df_sg], w=[self.df_sg])
            lm = S.sb([128, 4, 64], name="df_lm")
            self.ld(lm[:], df_lam.rearrange("o (a b) -> o a b", a=4).partition_broadcast(128), r=[self.Win], w=[lm])
            l2 = S.sb([128, 2, 64], name="df_l2")
            self.tt('dve', l2[:, 0, :], lm[:, 0, :], lm[:, 1, :], ALU.mult, r=[lm], w=[l2])
            self.tt('dve', l2[:, 1, :], lm[:, 2, :], lm[:, 3, :], ALU.mult, r=[lm], pw=[l2])
            self.nlam = S.sb([128, 4], name="nlam")
            nl = self.nlam
            S.op('dve', lambda E: E.tensor_reduce(out=nl[:, 0:2], in_=l2[:], axis=AX.X, op=ALU.add), reads=[l2], writes=[nl])
            self.act(nl[:, 0:2], nl[:, 0:2], AF.Exp, r=[nl], w=[nl])
            self.tt('dve', nl[:, 2:3], nl[:, 1:2], nl[:, 0:1], ALU.subtract, r=[nl], pw=[nl])
            self.ts('dve', nl[:, 3:4], nl[:, 2:3], -lam_init, None, ALU.add, r=[nl], pw=[nl])
            self.ctxkT = self.scratch("ctxkT", [8, 128, 256])
            self.ctx_tk = Tk(None)
        if 0 in mixers:
            self.gd_w_in = self.inp("gdn_w_in", [2, D, 4128])
            self.gd_w_out = self.inp("gdn_w_out", [2, D, D])
            gd_cw = self.inp("gdn_convT", [128, 2, 24, 5])
            gd_al = self.inp("gdn_a_log", [2, 1, 16])
            gd_dt = self.inp("gdn_dt_bias", [2, 1, 16])
            gd_ng = self.inp("gdn_norm_g", [2, 1, 128])
            self.st_S = self.inp("st_S", [2, 2, 8, 128, 128])
            self.newS = self.outp("newS", [NP, 2, 2, 8, 128, 128])
            self.gd_cw = S.sb([128, 2, 24, 5], name="gd_cw")
            self.ld(self.gd_cw[:], gd_cw, r=[self.Win], w=[self.gd_cw])
            self.gd_nea = S.sb([128, 2, 16], name="gd_nea")
            self.gd_dt = S.sb([128, 2, 16], name="gd_dt")
            self.gd_ng = S.sb([128, 2, 128], name="gd_ng")
            for j in range(2):
                self.ld(self.gd_nea[:, j, :], gd_al[j].partition_broadcast(128), r=[self.Win], **wr(self.gd_nea, j == 0))
                self.ld(self.gd_dt[:, j, :], gd_dt[j].partition_broadcast(128), r=[self.Win], **wr(self.gd_dt, j == 0))
                self.ld(self.gd_ng[:, j, :], gd_ng[j].partition_broadcast(128), r=[self.Win], **wr(self.gd_ng, j == 0))
            self.act(self.gd_nea[:], self.gd_nea[:], AF.Exp, r=[self.gd_nea], w=[self.gd_nea])
            self.ts('dve', self.gd_nea[:], self.gd_nea[:], -1.0, None, ALU.mult, r=[self.gd_nea], w=[self.gd_nea])

    def proj_stage(self, i, w_in, ncol, fm, tm, col_lo=0):
        self.areset()
        NT = 512
        xbs = self.take([128, 8, NT], 1)
        hbs = self.take([128, 8, NT], 1)
        self.rstd = self.take([128, NT], 2)
        W = self.take([128, 8, ncol])
        wv_ = w_in.rearrange("(kc p) n -> p kc n", p=128)
        self.ld(W[:, 0:4, :], wv_[:, 0:4, col_lo:col_lo + ncol], r=[self.Win], w=[W])
        self.ld(W[:, 4:8, :], wv_[:, 4:8, col_lo:col_lo + ncol], r=[self.Win], pw=[W], eng='pool')
        fm = [(a - col_lo, b, c_, d_, e_) for (a, b, c_, d_, e_) in fm]
        tm = [(a - col_lo, b, c_) for (a, b, c_) in tm]
        ofm = self.take([128, NT], 3)
        otm = self.take([128, 512], 3)
        for blk in range(TT // NT):
            c = 0 if blk < (NP * TP) // NT else 1
            tks = self.xT_tk[blk * 4:(blk + 1) * 4]
            ptk = self.proj_tk[blk * 4:(blk + 1) * 4]
            xb = xbs.get()
            self.ld(xb[:], self.xT_v[:, :, blk * NT:(blk + 1) * NT], r=tks, w=[xb], eng='pool')
            hb = hbs.get()
            self.norm_mod(xb, hb, 0, c, NT)
            n = 0
            for (col0, nch, dstv, ch0, scale) in fm:
                for oc in range(nch):
                    p = self.pnext()
                    for kc in range(8):
                        self.mm(p[:, :], W[:, kc, col0 + oc * 128: col0 + (oc + 1) * 128], hb[:, kc, :], kc == 0, kc == 7, r=[W, hb], **wr(p, kc == 0))
                    ot = ofm.get()
                    if n % 2 == 0:
                        self.act(ot[:], p[:, :], AF.Copy, r=[p], w=[ot], scale=scale)
                    else:
                        self.ts('dve', ot[:], p[:, :], scale, None, ALU.mult, r=[p], w=[ot])
                    n += 1
                    self.ld(dstv[:, ch0 + oc, blk * NT:(blk + 1) * NT], ot[:], r=[ot], pw=ptk, eng='pool')
            for q in range(4):
                t0 = blk * NT + q * 128
                for (col0, ncols, dst) in tm:
                    for g0 in range(0, ncols, 512):
                        gw = min(512, ncols - g0)
                        p = self.pnext()
                        for kc in range(8):
                            self.mm(p[:, 0:gw], hb[:, kc, q * 128:(q + 1) * 128], W[:, kc, col0 + g0: col0 + g0 + gw], kc == 0, kc == 7, r=[W, hb], **wr(p, kc == 0))
                        ot = otm.get()
                        if n % 2 == 0:
                            self.cp('act', ot[:, 0:gw], p[:, 0:gw], r=[p], w=[ot])
                        else:
                            self.cp('dve', ot[:, 0:gw], p[:, 0:gw], r=[p], w=[ot])
                        n += 1
                        self.ld(dst[t0:t0 + 128, g0:g0 + gw], ot[:, 0:gw], r=[ot], pw=[ptk[q]], eng='sp')

    def load_w16(self, W16, w_view, ncol, col_lo=0, piece=512, eng2='pool'):
        n = 0
        for c0 in range(0, ncol, piece):
            cw = min(piece, ncol - c0)
            st = self.wstage.get()
            self.ld(st[:, :, 0:cw], w_view[:, :, col_lo + c0:col_lo + c0 + cw], r=[self.Win], w=[st], eng='sp' if n % 2 == 0 else eng2)
            if n % 2 == 0:
                self.cp('pool', W16[:, :, c0:c0 + cw], st[:, :, 0:cw], r=[st], **wr(W16, c0 == 0))
            else:
                self.cp('act', W16[:, :, c0:c0 + cw], st[:, :, 0:cw], r=[st], **wr(W16, c0 == 0))
            n += 1

    def proj_stage16(self, i, w_in, ncol, fm, tm):
        self.areset()
        NT = 512
        xbs = self.take([128, 8, NT], 2)
        hbs = self.take([128, 8, NT], 2, BF16)
        sq = self.take([128, 8, NT])
        self.rstd = self.take([128, NT], 2)
        W = self.take([128, 8, ncol], None, BF16)
        self.wstage = self.take([128, 8, 512], 2)
        self.load_w16(W, w_in.rearrange("(kc p) n -> p kc n", p=128), ncol)
        ofm = self.take([128, NT], 3)
        otm = self.take([128, 512], 3)
        for blk in range(TT // NT):
            c = 0 if blk < (NP * TP) // NT else 1
            tks = self.xT_tk[blk * 4:(blk + 1) * 4]
            xb = xbs.get()
            self.ld(xb[:], self.xT_v[:, :, blk * NT:(blk + 1) * NT], r=tks, w=[xb], eng='pool')
            hb = hbs.get()
            self.norm_mod2(xb, xb[:, :, :], hb, hb[:, :, :], sq, 0, c, NT, True)
            n = 0
            for (col0, nch, dstv, ch0, scale) in fm:
                for oc in range(nch):
                    p = self.pnext()
                    for kc in range(8):
                        self.mm(p[:, :], W[:, kc, col0 + oc * 128: col0 + (oc + 1) * 128], hb[:, kc, :], kc == 0, kc == 7, r=[W, hb], **wr(p, kc == 0))
                    ot = ofm.get()
                    if n % 2 == 0:
                        self.act(ot[:], p[:, :], AF.Copy, r=[p], w=[ot], scale=scale)
                    else:
                        self.ts('dve', ot[:], p[:, :], scale, None, ALU.mult, r=[p], w=[ot])
                    n += 1
                    self.ld(dstv[:, ch0 + oc, blk * NT:(blk + 1) * NT], ot[:], r=[ot], pw=[self.Oout], eng='pool')
            for q in range(4):
                t0 = blk * NT + q * 128
                for (col0, ncols, dst) in tm:
                    for g0 in range(0, ncols, 512):
                        gw = min(512, ncols - g0)
                        p = self.pnext()
                        for kc in range(8):
                            self.mm(p[:, 0:gw], hb[:, kc, q * 128:(q + 1) * 128], W[:, kc, col0 + g0: col0 + g0 + gw], kc == 0, kc == 7, r=[W, hb], **wr(p, kc == 0))
                        ot = otm.get()
                        if n % 2 == 0:
                            self.cp('act', ot[:, 0:gw], p[:, 0:gw], r=[p], w=[ot])
                        else:
                            self.cp('dve', ot[:, 0:gw], p[:, 0:gw], r=[p], w=[ot])
                        n += 1
                        self.ld(dst[t0:t0 + 128, g0:g0 + gw], ot[:, 0:gw], r=[ot], pw=[self.Oout], eng='sp')

    def mlstm(self, i):
        o = self.opts
        (self.proj_stage16 if self.bf else self.proj_stage)(i, self.ml_w_in, 3104,
                        fm=[(0, 4, self.qkT_v, 0, 0.125), (512, 4, self.qkT_v, 4, 1.0)],
                        tm=[(512, 512, self.ktok), (1024, 1024, self.vtok), (2048, 1024, self.otok), (3072, 32, self.gtok)])
        self.mlstm_scan()
        self.mixer_post(i, self.ml_w_out, self.ml_ng, self.ml_ng[:], self.hdir, 'sigmoid')

    def mlstm_scan(self):
        self.areset()
        cst = self.cst
        Tri = [cst.ap[0:64, 2, 0:64], cst.ap[0:64, 3, 0:64]]
        Str = [cst.ap[0:64, 4, 0:64], cst.ap[0:64, 5, 0:64]]
        ones64 = cst.ap[0:64, 1, 0:64]
        Cn = [[self.take([64, 129]) for h in range(8)] for d in range(2)]
        qks = self.take([64, 16, 64], 4)
        kts = self.take([64, 512], 4)
        v1s = self.take([64, 8, 129], 4)
        for v1 in v1s.t:
            self.memset('pool', v1[:, :, 128:129], 1.0, pw=[v1])
        gts = self.take([64, 32], 4)
        gps = self.take([64, 64], 4)
        tls = self.take([64, 64], 6)
        Es = self.take([64, 64], 20)
        WTs = self.take([64, 64], 20)
        ias = self.take([64, 129], 6)
        tot8s = self.take([64, 8, 129], 4)
        tl8s = self.take([64, 8, 64], 2)
        E8s = self.take([64, 8, 64], 4)
        kw8s = self.take([64, 8, 64], 4)
        dns = self.take([64, 16], 4)
        kws = self.take([64, 64], 20)
        houts = self.take([64, 8, 128], 4)
        mst = [self.take([8, 1]) for d in range(2)]
        msm = self.take([8, 8], 2)
        emf = self.take([64, 8], 2)
        GBs = self.take([8, 2], 4)
        em0 = self.take([64, 16])
        cos = self.take([64, 129], 4)
        seqs = [(p * TP, TP // 64, p) for p in range(NP)] + [(NP * TP, TS // 64, -1)]
        for (tok0, nch, pidx) in seqs:
            if pidx >= 0:
                for d in range(2):
                    for h in range(8):
                        self.memset('pool', Cn[d][h][:], 0.0, w=[Cn[d][h]])
                    self.memset('pool', mst[d][:], 0.0, w=[mst[d]])
            else:
                self.ld(em0[:], self.st_m.partition_broadcast(64), r=[self.Win], w=[em0])
                self.act(em0[:], em0[:], AF.Exp, r=[em0], w=[em0])
                for d in range(2):
                    for h in range(8):
                        T_ = Cn[d][h]
                        self.ld(T_[:, 0:128], self.st_C[d, h], r=[self.Win], w=[T_])
                        self.ld(T_[:, 128:129], self.st_n[d, h], r=[self.Win], pw=[T_], eng='pool')
                        self.ts('pool', T_[:], T_[:], em0[:, d * 8 + h: d * 8 + h + 1], None, ALU.mult, r=[T_, em0], w=[T_])
            for step in range(nch):
                ctxs = []
                for d in range(2):
                    c = step if d == 0 else nch - 1 - step
                    t0 = tok0 + c * 64
                    ptk = [self.proj_tk[t0 // 128]]
                    qk = qks.get()
                    self.ld(qk[:], self.qkT_h[:, :, t0:t0 + 64], r=ptk, w=[qk])
                    kt = kts.get()
                    self.ld(kt[:], self.ktok[t0:t0 + 64, 0:512], r=ptk, w=[kt], eng="pool")
                    v1 = v1s.get()
                    self.ld(v1[:, :, 0:128], self.vtok[t0:t0 + 64, :].rearrange("t (h e) -> t h e", h=8), r=ptk, pw=[v1])
                    gt = gts.get()
                    self.ld(gt[:], self.gtok[t0:t0 + 64, :], r=ptk, w=[gt], eng='pool')
                    gp = gps.get()
                    dc = slice(d * 8, d * 8 + 8)
                    self.tt('dve', gp[:, 0:8], gt[:, dc], self.ml_gb[0:64, dc], ALU.add, r=[gt, self.ml_gb], w=[gp])
                    self.tt('dve', gp[:, 16:24], gt[:, 16 + d * 8:24 + d * 8], self.ml_gb[0:64, 16 + d * 8:24 + d * 8], ALU.add, r=[gt, self.ml_gb], pw=[gp])
                    self.act(gp[:, 16:24], gp[:, 16:24], AF.Exp, r=[gp], pw=[gp], scale=-1.0)
                    self.act(gp[:, 16:24], gp[:, 16:24], AF.Ln, r=[gp], pw=[gp], bias=1.0)
                    self.ts('dve', gp[:, 16:24], gp[:, 16:24], -1.0, None, ALU.mult, r=[gp], pw=[gp])
                    lf = gp[:, 16:24]
                    pg = self.pnext()
                    self.mm(pg[0:64, 0:8], Tri[d], lf, True, True, r=[cst, gp], w=[pg])
                    self.mm(pg[0:64, 8:16], Str[d], lf, True, True, r=[cst, gp], pw=[pg])
                    self.mm(pg[0:64, 16:24], ones64, lf, True, True, r=[cst, gp], pw=[pg])
                    self.act(gp[:, 32:40], pg[0:64, 0:8], AF.Exp, r=[pg], pw=[gp])
                    self.tt('dve', gp[:, 56:64], pg[0:64, 8:16], gp[:, 0:8], ALU.add, r=[pg, gp], pw=[gp])
                    self.act(gp[:, 40:48], gp[:, 56:64], AF.Exp, r=[gp], pw=[gp])
                    self.act(gp[:, 48:56], pg[0:64, 16:24], AF.Exp, r=[pg], pw=[gp])
                    if pidx >= 0:
                        pt = self.pnext()
                        self.tr_(pt[0:8, 0:64], gp[:, 56:64], r=[gp], w=[pt])
                        self.tr_(pt[0:8, 64:128], gp[:, 24:32] if False else pg[0:64, 16:24], r=[pg], pw=[pt]) if False else None
                        GB = GBs.get()
                        self.S.op('dve', lambda E, GB=GB, pt=pt: E.tensor_reduce(out=GB[:, 0:1], in_=pt[0:8, 0:64], axis=AX.X, op=ALU.max), reads=[pt], writes=[GB])
                        pb = self.pnext()
                        self.mm(pb[0:8, 0:1], lf, cst.ap[0:64, 1, 0:1], True, True, r=[gp, cst], w=[pb])
                        self.stt('dve', mst[d][:], mst[d][:], pb[0:8, 0:1], GB[:, 0:1], ALU.add, ALU.max, r=[mst[d], pb, GB], w=[mst[d]])
                    ctxs.append((d, t0, qk, kt, v1, gp, houts.get(), tot8s.get()))
                units = [(cx, h) for cx in ctxs for h in range(8)]
                stE = {}
                for cx in ctxs:
                    d, t0, qk, kt, v1, gp, ho, t8 = cx
                    tl8 = tl8s.get()
                    self.tt('dve', tl8[:], Tri[d].unsqueeze(1).to_broadcast([64, 8, 64]), gp[:, 16:24].unsqueeze(2).to_broadcast([64, 8, 64]), ALU.mult, r=[cst, gp], w=[tl8])
                    pD = self.pnext()
                    self.mm(pD[0:64, 0:512], Str[d], tl8[:].rearrange("p h t -> p (h t)"), True, True, r=[cst, tl8], w=[pD])
                    E8 = E8s.get()
                    self.act(E8[:].rearrange("p h t -> p (h t)"), pD[0:64, 0:512], AF.Exp, r=[pD], w=[E8])
                    self.act(gp[:, 8:16], gp[:, 0:8], AF.Exp, r=[gp], pw=[gp])
                    stE[d] = E8
                stK = {}
                for cx in ctxs:
                    d, t0, qk, kt, v1, gp, ho, t8 = cx
                    E8 = stE[d]
                    self.tt('pool', E8[:], E8[:], Tri[d].unsqueeze(1).to_broadcast([64, 8, 64]), ALU.mult, r=[E8, cst], w=[E8])
                    self.tt('dve', E8[:], E8[:], gp[:, 8:16].unsqueeze(2).to_broadcast([64, 8, 64]), ALU.mult, r=[E8, gp], w=[E8])
                    kw8 = kw8s.get()
                    self.tt('pool', kw8[:], kt[:].rearrange("t (h e) -> t h e", h=8), gp[:, 40:48].unsqueeze(2).to_broadcast([64, 8, 64]), ALU.mult, r=[kt, gp], w=[kw8])
                    stK[d] = kw8
                stW = {}
                for (cx, h) in units:
                    d, t0, qk, kt, v1, gp, ho, t8 = cx
                    pK = self.pnext()
                    self.mm(pK[0:64, 0:64], qk[:, 8 + h, :], qk[:, h, :], True, True, r=[qk], w=[pK])
                    WT = WTs.get()
                    self.tt('dve', WT[:], stE[d][:, h, :], pK[0:64, 0:64], ALU.mult, r=[stE[d], pK], w=[WT])
                    stW[(d, h)] = WT
                for (cx, h) in units:
                    d, t0, qk, kt, v1, gp, ho, t8 = cx
                    pI = self.pnext()
                    self.mm(pI[0:64, 0:129], stW[(d, h)][:], v1[:, h, :], True, True, r=[stW[(d, h)], v1], w=[pI])
                    pN = self.pnext()
                    C_ = Cn[d][h]
                    self.mm(pN[0:64, 0:129], qk[:, h, :], C_[:], True, True, r=[qk, C_], w=[pN])
                    ia = ias.get()
                    self.cp('act', ia[:], pI[0:64, 0:129], r=[pI], w=[ia])
                    self.stt('dve', t8[:, h, :], pN[0:64, 0:129], gp[:, 32 + h:33 + h], ia[:], ALU.mult, ALU.add, r=[pN, gp, ia], **wr(t8, h == 0))
                for cx in ctxs:
                    d, t0, qk, kt, v1, gp, ho, t8 = cx
                    dn = dns.get()
                    den = t8[:, :, 128]
                    self.ts('dve', dn[:, 0:8], den, -1.0, None, ALU.mult, r=[t8], w=[dn])
                    self.tt('dve', dn[:, 0:8], dn[:, 0:8], den, ALU.max, r=[dn, t8], w=[dn])
                    self.ts('dve', dn[:, 0:8], dn[:, 0:8], 1.0, None, ALU.max, r=[dn], w=[dn])
                    self.recip(dn[:, 8:16], dn[:, 0:8], r=[dn], pw=[dn])
                    self.tt('pool', ho[:], t8[:, :, 0:128], dn[:, 8:16].unsqueeze(2).to_broadcast([64, 8, 128]), ALU.mult, r=[t8, dn], w=[ho])
                    self.ld(self.hdir[d][t0:t0 + 64, :].rearrange("t (h e) -> t h e", h=8), ho[:], r=[ho], w=[self.hdir_tk[d][t0 // 64]], eng='pool')
                for (cx, h) in units:
                    d, t0, qk, kt, v1, gp, ho, t8 = cx
                    C_ = Cn[d][h]
                    pU = self.pnext()
                    self.mm(pU[0:64, 0:129], stK[d][:, h, :], v1[:, h, :], True, True, r=[stK[d], v1], w=[pU])
                    self.stt('dve', C_[:], C_[:], gp[:, 48 + h:49 + h], pU[0:64, 0:129], ALU.mult, ALU.add, r=[C_, gp, pU], w=[C_])
            if pidx >= 0:
                for d in range(2):
                    dm = msm.get()
                    self.ts('dve', dm[:], cst.ap[0:8, 0, 0:8], mst[d][:, 0:1], None, ALU.mult, r=[cst, mst[d]], w=[dm])
                    pm = self.pnext()
                    self.mm(pm[0:64, 0:8], cst.ap[0:8, 1, 0:64], dm[:], True, True, r=[cst, dm], w=[pm])
                    ef = emf.get()
                    self.act(ef[:], pm[0:64, 0:8], AF.Exp, r=[pm], w=[ef], scale=-1.0)
                    self.ld(self.newm[pidx, d], mst[d][:], r=[mst[d]], w=[self.Oout], eng='pool')
                    for h in range(8):
                        co = cos.get()
                        self.ts('dve' if h % 2 else 'pool', co[:], Cn[d][h][:], ef[:, h:h + 1], None, ALU.mult, r=[Cn[d][h], ef], w=[co])
                        self.ld(self.newC[pidx, d, h], co[:, 0:128], r=[co], pw=[self.Oout], eng='sp')
                        self.ld(self.newn[pidx, d, h], co[:, 128:129], r=[co], pw=[self.Oout], eng='pool')

    def gdn(self, i):
        j = i // 3
        w_in = self.gd_w_in[j]
        if self.bf:
            self.proj_stage16(i, w_in, 4128, fm=[(0, 24, self.qkT_v, 0, 1.0)],
                              tm=[(3072, 1024, self.otok), (4096, 32, self.gtok)])
        else:
            self.proj_stage(i, w_in, 2048, fm=[(0, 16, self.qkT_v, 0, 1.0)], tm=[], col_lo=0)
            self.proj_stage(i, w_in, 2080, fm=[(2048, 8, self.qkT_v, 16, 1.0)],
                            tm=[(3072, 1024, self.otok), (4096, 32, self.gtok)], col_lo=2048)
        stop = self.opts.get('gdn_stop', 9)
        if stop >= 2:
            self.gdn_conv(j)
        if stop >= 3:
            self.gdn_scan(j)
        if stop >= 4:
            self.mixer_post(i, self.gd_w_out[j], self.gd_ng, self.gd_ng[:, j, :], self.hdir, 'silu')

    def gdn_conv(self, j):
        self.areset()
        xins = self.take([128, TS + 4], 2)
        accs = self.take([128, TS], 2)
        tmps = self.take([128, TS], 2)
        sqs = self.take([128, 512], 2)
        rss = self.take([128, 512], 2)
        tos = self.take([128, 4, 128], 3)
        seqs = [(p * TP, TP) for p in range(NP)] + [(NP * TP, TS)]
        n = 0
        for (tok0, T) in seqs:
            for ch in range(24):
                eng = 'dve' if n % 2 == 0 else 'pool'
                n += 1
                xin = xins.get()
                self.memset('pool', xin[:, 0:2], 0.0, w=[xin])
                self.memset('pool', xin[:, T + 2:T + 4], 0.0, pw=[xin])
                self.ld(xin[:, 2:T + 2], self.qkT_v[:, ch, tok0:tok0 + T], pw=[xin])
                acc = accs.get()
                cw = self.gd_cw
                if eng == 'dve':
                    self.ts(eng, acc[:, 0:T], xin[:, 0:T], cw[:, j, ch, 0:1], None, ALU.mult, r=[xin, cw], w=[acc])
                    for k in range(1, 5):
                        self.stt(eng, acc[:, 0:T], xin[:, k:k + T], cw[:, j, ch, k:k + 1], acc[:, 0:T], ALU.mult, ALU.add, r=[xin, cw, acc], w=[acc])
                else:
                    self.act(acc[:, 0:T], xin[:, 0:T], AF.Copy, r=[xin, cw], w=[acc], scale=cw[:, j, ch, 0:1])
                    for k in range(1, 5):
                        tm_ = tmps.get()
                        self.act(tm_[:, 0:T], xin[:, k:k + T], AF.Copy, r=[xin, cw], w=[tm_], scale=cw[:, j, ch, k:k + 1])
                        self.tt('pool', acc[:, 0:T], acc[:, 0:T], tm_[:, 0:T], ALU.add, r=[acc, tm_], w=[acc])
                self.act(acc[:, 0:T], acc[:, 0:T], AF.Silu, r=[acc], w=[acc])
                if ch < 16:
                    scale = (128.0 ** -0.5) if ch < 8 else 1.0
                    for b0 in range(0, T, 512):
                        bw = min(512, T - b0)
                        sq = sqs.get()
                        self.tt('pool', sq[:, 0:bw], acc[:, b0:b0 + bw], acc[:, b0:b0 + bw], ALU.mult, r=[acc], w=[sq])
                        p = self.pnext()
                        self.mm(p[:, 0:bw], self.cst.ap[:, 1, :], sq[:, 0:bw], True, True, r=[self.cst, sq], w=[p])
                        rs = rss.get()
                        self.act(rs[:, 0:bw], p[:, 0:bw], AF.Sqrt, r=[p, self.epsb], w=[rs], bias=self.epsb[:, 0:1])
                        self.recip(rs[:, 0:bw], rs[:, 0:bw], r=[rs], w=[rs])
                        self.stt('dve', acc[:, b0:b0 + bw], acc[:, b0:b0 + bw], scale, rs[:, 0:bw], ALU.mult, ALU.mult, r=[acc, rs], w=[acc])
                    self.ld(self.qkT_v[:, ch, tok0:tok0 + T], acc[:, 0:T], r=[acc], pw=[self.Oout], eng='pool')
                if ch >= 8:
                    dst = self.ktok if ch < 16 else self.vtok
                    c0 = (ch - 8) * 128 if ch < 16 else (ch - 16) * 128
                    for g0 in range(0, T, 512):
                        ng = min(4, (T - g0) // 128)
                        p = self.pnext()
                        for k in range(ng):
                            self.tr_(p[:, k * 128:(k + 1) * 128], acc[:, g0 + k * 128:g0 + (k + 1) * 128], r=[acc], **wr(p, k == 0))
                        to = tos.get()
                        self.cp('act', to[:, 0:ng, :], p[:, 0:ng * 128].rearrange("p (a b) -> p a b", a=ng), r=[p], w=[to])
                        self.ld(dst[tok0 + g0:tok0 + g0 + ng * 128, c0:c0 + 128].rearrange("(n p) e -> p n e", p=128), to[:, 0:ng, :], r=[to], pw=[self.Oout], eng='sp')

    def gdn_scan(self, j):
        self.areset()
        cst = self.cst
        Tri = [cst.ap[0:64, 2, 0:64], cst.ap[0:64, 3, 0:64]]
        Str = [cst.ap[0:64, 4, 0:64], cst.ap[0:64, 5, 0:64]]
        Sm = [cst.ap[0:64, 5, 0:64], cst.ap[0:64, 4, 0:64]]
        I64 = cst.ap[0:64, 0, 0:64]
        ones64w = cst.ap[0:64, 1, 0:128]

        def bh(m):
            return m.unsqueeze(1).to_broadcast([64, 8, 64])

        def bt(v, n, np_=64):
            return v.unsqueeze(2).to_broadcast([np_, v.shape[1], n])

        S8 = [self.take([128, 8, 128]) for d in range(2)]
        qks = self.take([128, 8, 2, 64], 3)
        kts = self.take([64, 8, 128], 3)
        vts = self.take([64, 8, 128], 3)
        gts = self.take([64, 32], 3)
        gps = self.take([64, 48], 3)
        gls = self.take([128, 8], 3)
        tl8s = self.take([64, 8, 64], 2)
        Er8s = self.take([64, 8, 64], 2)
        Ei8s = self.take([64, 8, 64], 2)
        Es8s = self.take([64, 8, 64], 2)
        qkT8s = self.take([64, 8, 64], 3)
        P8s = self.take([64, 8, 64], 3)
        X8s = self.take([64, 8, 64], 5)
        XT8s = self.take([64, 8, 64], 5)
        U8s = self.take([64, 8, 128], 3)
        keg8s = self.take([64, 8, 128], 2)
        kdec8s = self.take([64, 8, 128], 3)
        vn8s = self.take([64, 8, 128], 3)
        o8s = self.take([64, 8, 128], 3)
        WT8s = self.take([128, 8, 64], 3)
        seqs = [(p * TP, TP // 64, p) for p in range(NP)] + [(NP * TP, TS // 64, -1)]
        seqs = seqs[self.opts.get('gdn_seq0', 0):self.opts.get('gdn_seq1', 5)]
        for (tok0, nch, pidx) in seqs:
            for d in range(2):
                if pidx >= 0:
                    self.memset('pool', S8[d][:], 0.0, w=[S8[d]])
                else:
                    self.ld(S8[d][:], self.st_S[j, d].rearrange("h k e -> k h e"), r=[self.Win], w=[S8[d]], eng='sp' if d else 'pool')
            for step in range(nch):
                ctx = []
                for d in range(2):
                    c = step if d == 0 else nch - 1 - step
                    t0 = tok0 + c * 64
                    qk = qks.get()
                    self.ld(qk[:, :, 0, :], self.qkT_v[:, 8:16, t0:t0 + 64], w=[qk])
                    self.ld(qk[:, :, 1, :], self.qkT_v[:, 0:8, t0:t0 + 64], pw=[qk], eng='pool')
                    kt = kts.get()
                    self.ld(kt[:], self.ktok[t0:t0 + 64, 0:1024].rearrange("t (h e) -> t h e", h=8), w=[kt], eng='pool')
                    vt = vts.get()
                    self.ld(vt[:], self.vtok[t0:t0 + 64, :].rearrange("t (h e) -> t h e", h=8), w=[vt])
                    gt = gts.get()
                    self.ld(gt[:], self.gtok[t0:t0 + 64, :], w=[gt], eng='pool')
                    gp = gps.get()
                    dc = slice(d * 8, d * 8 + 8)
                    self.tt('dve', gp[:, 0:8], gt[:, dc], self.gd_dt[0:64, j, dc], ALU.add, r=[gt, self.gd_dt], w=[gp])
                    self.act(gp[:, 0:8], gp[:, 0:8], AF.Exp, r=[gp], pw=[gp])
                    self.act(gp[:, 0:8], gp[:, 0:8], AF.Ln, r=[gp], pw=[gp], bias=1.0)
                    self.tt('dve', gp[:, 8:16], gp[:, 0:8], self.gd_nea[0:64, j, dc], ALU.mult, r=[gp, self.gd_nea], pw=[gp])
                    self.act(gp[:, 16:24], gt[:, 16 + d * 8:24 + d * 8], AF.Sigmoid, r=[gt], pw=[gp])
                    self.ts('dve', gp[:, 24:32], gp[:, 16:24], -1.0, None, ALU.mult, r=[gp], pw=[gp])
                    la = gp[:, 8:16]
                    pg = self.pnext()
                    self.mm(pg[0:64, 0:8], Tri[d], la, True, True, r=[cst, gp], w=[pg])
                    self.mm(pg[0:64, 8:16], Str[d], la, True, True, r=[cst, gp], pw=[pg])
                    self.mm(pg[0:128, 16:24], ones64w, la, True, True, r=[cst, gp], pw=[pg])
                    self.act(gp[:, 32:48], pg[0:64, 0:16], AF.Exp, r=[pg], pw=[gp])
                    gl = gls.get()
                    self.act(gl[:], pg[0:128, 16:24], AF.Exp, r=[pg], w=[gl])
                    ctx.append(dict(d=d, t0=t0, qk=qk, kt=kt, vt=vt, gp=gp, gl=gl))
                for cx in ctx:
                    d, gp = cx['d'], cx['gp']
                    tl8 = tl8s.get()
                    self.tt('dve', tl8[:], bh(Tri[d]), bt(gp[:, 8:16], 64), ALU.mult, r=[cst, gp], w=[tl8])
                    pD = self.pnext()
                    self.mm(pD[0:64, 0:512], Str[d], tl8[:].rearrange("p h t -> p (h t)"), True, True, r=[cst, tl8], w=[pD])
                    Er = Er8s.get()
                    self.act(Er[:].rearrange("p h t -> p (h t)"), pD[0:64, 0:512], AF.Exp, r=[pD], w=[Er])
                    cx['Er'] = Er
                for cx in ctx:
                    d, gp, Er = cx['d'], cx['gp'], cx['Er']
                    Ei = Ei8s.get()
                    Es = Es8s.get()
                    self.tt('pool', Ei[:], Er[:], bh(Tri[d]), ALU.mult, r=[Er, cst], w=[Ei])
                    self.tt('pool', Es[:], Er[:], bh(Sm[d]), ALU.mult, r=[Er, cst], w=[Es])
                    self.tt('dve', Es[:], Es[:], bt(gp[:, 24:32], 64), ALU.mult, r=[Es, gp], w=[Es])
                    cx['Ei'], cx['Es'] = Ei, Es
                for cx in ctx:
                    qk = cx['qk']
                    X = X8s.get()
                    qkT = qkT8s.get()
                    for g in range(2):
                        pG = self.pnext()
                        for hh in range(4):
                            h = 4 * g + hh
                            self.mm(pG[0:64, hh * 128:(hh + 1) * 128], qk[:, h, 0, :], qk[:, h, :, :].rearrange("p a t -> p (a t)"), True, True,
                                    r=[qk], **wr(pG, hh == 0))
                        pv = pG[0:64, 0:512].rearrange("p (h a t) -> p h a t", h=4, a=2)
                        self.tt('dve', X[:, 4 * g:4 * g + 4, :], pv[:, :, 0, :], cx['Es'][:, 4 * g:4 * g + 4, :], ALU.mult, r=[pG, cx['Es']], **wr(X, g == 0))
                        self.tt('dve', qkT[:, 4 * g:4 * g + 4, :], pv[:, :, 1, :], cx['Ei'][:, 4 * g:4 * g + 4, :], ALU.mult, r=[pG, cx['Ei']], **wr(qkT, g == 0))
                    cx['X'], cx['qkT'] = X, qkT
                for cx in ctx:
                    X = cx['X']
                    pT = self.pnext()
                    for h in range(8):
                        self.tr_(pT[0:64, h * 64:(h + 1) * 64], X[:, h, :], r=[X], **wr(pT, h == 0))
                    XT = XT8s.get()
                    self.cp('act', XT[:].rearrange("p h t -> p (h t)"), pT[0:64, 0:512], r=[pT], w=[XT])
                    P_ = P8s.get()
                    self.tt('pool', P_[:], X[:], bh(I64), ALU.add, r=[X, cst], w=[P_])
                    cx['XT'], cx['P'] = XT, P_
                for jn in range(1, 6):
                    for cx in ctx:
                        X, XT = cx['X'], cx['XT']
                        Xn = None
                        if jn < 5:
                            pX = self.pnext()
                            for h in range(8):
                                self.mm(pX[0:64, h * 64:(h + 1) * 64], XT[:, h, :], X[:, h, :], True, True, r=[XT, X], **wr(pX, h == 0))
                            Xn = X8s.get()
                            self.cp('dve', Xn[:].rearrange("p h t -> p (h t)"), pX[0:64, 0:512], r=[pX], w=[Xn])
                        pXT = self.pnext()
                        for h in range(8):
                            self.mm(pXT[0:64, h * 64:(h + 1) * 64], X[:, h, :], XT[:, h, :], True, True, r=[XT, X], **wr(pXT, h == 0))
                        XnT = XT8s.get()
                        self.cp('act', XnT[:].rearrange("p h t -> p (h t)"), pXT[0:64, 0:512], r=[pXT], w=[XnT])
                        cx['X'], cx['XT'] = Xn, XnT
                    for cx in ctx:
                        XT, P_ = cx['XT'], cx['P']
                        pP = self.pnext()
                        for h in range(8):
                            self.mm(pP[0:64, h * 64:(h + 1) * 64], XT[:, h, :], P_[:, h, :], True, True, r=[XT, P_], **wr(pP, h == 0))
                        self.tt('dve', P_[:].rearrange("p h t -> p (h t)"), P_[:].rearrange("p h t -> p (h t)"), pP[0:64, 0:512], ALU.add, r=[P_, pP], w=[P_])
                for cx in ctx:
                    gp, kt, vt, P_ = cx['gp'], cx['kt'], cx['vt'], cx['P']
                    keg = keg8s.get()
                    self.tt('pool', keg[:], kt[:], bt(gp[:, 32:40], 128), ALU.mult, r=[kt, gp], w=[keg])
                    kdec = kdec8s.get()
                    self.tt('pool', kdec[:], kt[:], bt(gp[:, 40:48], 128), ALU.mult, r=[kt, gp], w=[kdec])
                    U = U8s.get()
                    for g in range(2):
                        pU = self.pnext()
                        for hh in range(4):
                            h = 4 * g + hh
                            self.mm(pU[0:64, hh * 128:(hh + 1) * 128], P_[:, h, :], vt[:, h, :], True, True, r=[P_, vt], **wr(pU, hh == 0))
                        self.tt('dve', U[:, 4 * g:4 * g + 4, :], pU[0:64, 0:512].rearrange("p (h e) -> p h e", h=4), bt(gp[:, 16 + 4 * g:20 + 4 * g], 128), ALU.mult,
                                r=[pU, gp], **wr(U, g == 0))
                    pW = self.pnext()
                    for h in range(8):
                        self.mm(pW[0:128, h * 64:(h + 1) * 64], keg[:, h, :], P_[:, h, :], True, True, r=[keg, P_], **wr(pW, h == 0))
                    WT = WT8s.get()
                    self.cp('act', WT[:].rearrange("p h t -> p (h t)"), pW[0:128, 0:512], r=[pW], w=[WT])
                    cx['U'], cx['WT'], cx['kdec'] = U, WT, kdec
                for cx in ctx:
                    d, gp = cx['d'], cx['gp']
                    vn = vn8s.get()
                    for g in range(2):
                        pa = self.pnext()
                        for hh in range(4):
                            h = 4 * g + hh
                            self.mm(pa[0:64, hh * 128:(hh + 1) * 128], cx['WT'][:, h, :], S8[d][:, h, :], True, True, r=[cx['WT'], S8[d]], **wr(pa, hh == 0))
                        self.tt('dve', vn[:, 4 * g:4 * g + 4, :], pa[0:64, 0:512].rearrange("p (h e) -> p h e", h=4), bt(gp[:, 24 + 4 * g:28 + 4 * g], 128), ALU.mult,
                                r=[pa, gp], **wr(vn, g == 0))
                    self.tt('pool', vn[:], vn[:], cx['U'][:], ALU.add, r=[vn, cx['U']], w=[vn])
                    cx['vn'] = vn
                for cx in ctx:
                    d, gp, gl, qk, vn = cx['d'], cx['gp'], cx['gl'], cx['qk'], cx['vn']
                    o8 = o8s.get()
                    for g in range(2):
                        po = self.pnext()
                        for hh in range(4):
                            h = 4 * g + hh
                            self.mm(po[0:64, hh * 128:(hh + 1) * 128], qk[:, h, 1, :], S8[d][:, h, :], True, True, r=[qk, S8[d]], **wr(po, hh == 0))
                        self.tt('dve', o8[:, 4 * g:4 * g + 4, :], po[0:64, 0:512].rearrange("p (h e) -> p h e", h=4), bt(gp[:, 32 + 4 * g:36 + 4 * g], 128), ALU.mult,
                                r=[po, gp], **wr(o8, g == 0))
                    for g in range(2):
                        po2 = self.pnext()
                        for hh in range(4):
                            h = 4 * g + hh
                            self.mm(po2[0:64, hh * 128:(hh + 1) * 128], cx['qkT'][:, h, :], vn[:, h, :], True, True, r=[cx['qkT'], vn], **wr(po2, hh == 0))
                        self.tt('dve', o8[:, 4 * g:4 * g + 4, :], o8[:, 4 * g:4 * g + 4, :], po2[0:64, 0:512].rearrange("p (h e) -> p h e", h=4), ALU.add,
                                r=[po2, o8], pw=[o8])
                    self.ld(self.hdir[d][cx['t0']:cx['t0'] + 64, :].rearrange("t (h e) -> t h e", h=8), o8[:], r=[o8], pw=[self.Oout], eng='pool')
                    for g in range(2):
                        pS = self.pnext()
                        for hh in range(4):
                            h = 4 * g + hh
                            self.mm(pS[0:128, hh * 128:(hh + 1) * 128], cx['kdec'][:, h, :], vn[:, h, :], True, True, r=[cx['kdec'], vn], **wr(pS, hh == 0))
                        Sg = S8[d][:, 4 * g:4 * g + 4, :]
                        self.tt('pool', Sg, Sg, bt(gl[:, 4 * g:4 * g + 4], 128, 128), ALU.mult, r=[S8[d], gl], w=[S8[d]])
                        self.tt('dve', Sg, Sg, pS[0:128, 0:512].rearrange("p (h e) -> p h e", h=4), ALU.add, r=[S8[d], pS], w=[S8[d]])
            if pidx >= 0:
                for d in range(2):
                    self.ld(self.newS[pidx, j, d].rearrange("h k e -> k h e"), S8[d][:], r=[S8[d]], pw=[self.Oout], eng='sp' if d else 'pool')

    def diffattn(self, i):
        (self.proj_stage16 if self.bf else self.proj_stage)(i, self.df_w_in, 3072, fm=[],
                        tm=[(0, 2048, self.ktok), (2048, 1024, self.vtok)])
        self.attn_prep()
        self.attn_core()
        self.mixer_post(i, self.df_w_out, self.df_sg, self.df_sg[:], self.hdir, None)

    def attn_prep(self):
        self.areset()
        xs = self.take([128, 32, 64], 2)
        sqs = self.take([128, 32, 64], 1)
        sss = self.take([128, 64], 2)
        css = self.take([128, 64], 2)
        r1 = self.take([128, 32, 2, 16], 1)
        r2 = self.take([128, 32, 2, 16], 1)
        r3 = self.take([128, 32, 2, 16], 1)
        xr = self.take([128, 32, 64], 2)
        vts = self.take([128, 1024], 2)
        xos = self.take([128, 16, 128], 2)
        cks = self.take([128, 16, 64], 2)
        cko = self.take([128, 8, 128], 2)
        gq = self.df_g
        for t in range(TT // 128):
            t0 = t * 128
            x = xs.get()
            self.ld(x[:], self.ktok[t0:t0 + 128, :].rearrange("t (g d) -> t g d", g=32), r=[self.proj_tk[t]], w=[x])
            sq = sqs.get()
            self.tt('pool', sq[:], x[:], x[:], ALU.mult, r=[x], w=[sq])
            ss = sss.get()
            self.S.op('dve', lambda E, ss=ss, sq=sq: E.tensor_reduce(out=ss[:, 0:32], in_=sq[:], axis=AX.X, op=ALU.add), reads=[sq], writes=[ss])
            self.act(ss[:, 0:32], ss[:, 0:32], AF.Sqrt, r=[ss, self.epsb], w=[ss], scale=1.0 / 64, bias=self.epsb[:, 0:1])
            self.recip(ss[:, 32:64], ss[:, 0:32], r=[ss], pw=[ss])
            self.tt('dve', x[:], x[:], ss[:, 32:64].unsqueeze(2).to_broadcast([128, 32, 64]), ALU.mult, r=[x, ss], w=[x])
            self.tt('pool', x[:, 0:16, :], x[:, 0:16, :], gq[:, 0, :].unsqueeze(1).to_broadcast([128, 16, 64]), ALU.mult, r=[x, gq], w=[x])
            self.tt('dve', x[:, 16:32, :], x[:, 16:32, :], gq[:, 1, :].unsqueeze(1).to_broadcast([128, 16, 64]), ALU.mult, r=[x, gq], w=[x])
            if t0 < NP * TP:
                p, tl = t0 // TP, t0 % TP
                self.ld(self.newk[p, :, :, tl:tl + 128, :].rearrange("h m t d -> t (h m) d"), x[:, 16:32, :], r=[x], pw=[self.Oout], eng='pool')
                vt = vts.get()
                self.ld(vt[:], self.vtok[t0:t0 + 128, :], r=[self.proj_tk[t]], w=[vt])
                self.ld(self.newv[p, :, tl:tl + 128, :].rearrange("h t e -> t h e"), vt[:].rearrange("t (h e) -> t h e", h=8), r=[vt], pw=[self.Oout], eng='pool')
                src = x
            else:
                cs = css.get()
                self.ld(cs[:], self.rope_cs[t0 - NP * TP:t0 - NP * TP + 128, :], r=[self.Win], w=[cs])
                X = x[:].rearrange("t g (a f r) -> t g a f r", a=2, f=2)
                xa = X[:, :, :, 0, :]
                xb_ = X[:, :, :, 1, :]
                cosb = cs[:, 0:32].rearrange("t (a r) -> t a r", a=2).unsqueeze(1).to_broadcast([128, 32, 2, 16])
                sinb = cs[:, 32:64].rearrange("t (a r) -> t a r", a=2).unsqueeze(1).to_broadcast([128, 32, 2, 16])
                o_ = xr.get()
                O = o_[:].rearrange("t g (a f r) -> t g a f r", a=2, f=2)
                a1, a2, a3 = r1.get(), r2.get(), r3.get()
                self.tt('dve', a1[:], xa, cosb, ALU.mult, r=[x, cs], w=[a1])
                self.tt('pool', a2[:], xb_, sinb, ALU.mult, r=[x, cs], w=[a2])
                self.tt('dve', O[:, :, :, 0, :], a1[:], a2[:], ALU.subtract, r=[a1, a2], w=[o_])
                self.tt('pool', a3[:], xa, sinb, ALU.mult, r=[x, cs], w=[a3])
                self.tt('dve', a1[:], xb_, cosb, ALU.mult, r=[x, cs], w=[a1])
                self.tt('pool', O[:, :, :, 1, :], a3[:], a1[:], ALU.add, r=[a3, a1], pw=[o_])
                src = o_
            xo = xos.get()
            for g in range(4):
                pp = self.pnext()
                for k in range(4):
                    ch = g * 4 + k
                    self.tr_(pp[:, k * 128:(k + 1) * 128], src[:, 2 * ch:2 * ch + 2, :].rearrange("t a d -> t (a d)"), r=[src], **wr(pp, k == 0))
                self.cp('act' if g % 2 else 'dve', xo[:, g * 4:(g + 1) * 4, :], pp[:, :].rearrange("p (a b) -> p a b", a=4), r=[pp], **wr(xo, g == 0))
            self.ld(self.qkT_v[:, 0:16, t0:t0 + 128], xo[:], r=[xo], w=[self.prep_tk[t]], eng='pool')
        for kt in range(2):
            ck = cks.get()
            self.ld(ck[:], self.ctx_k[:, :, kt * 128:(kt + 1) * 128, :].rearrange("h m t d -> t (h m) d"), r=[self.Win], w=[ck])
            co = cko.get()
            for g in range(2):
                pp = self.pnext()
                for k in range(4):
                    ch = g * 4 + k
                    self.tr_(pp[:, k * 128:(k + 1) * 128], ck[:, 2 * ch:2 * ch + 2, :].rearrange("t a d -> t (a d)"), r=[ck], **wr(pp, k == 0))
                self.cp('act' if g % 2 else 'dve', co[:, g * 4:(g + 1) * 4, :], pp[:, :].rearrange("p (a b) -> p a b", a=4), r=[pp], **wr(co, g == 0))
            self.ld(self.ctxkT.rearrange("c p t -> p c t")[:, :, kt * 128:(kt + 1) * 128], co[:], r=[co], **wr(self.ctx_tk, kt == 0), eng='pool')

    def attn_core(self):
        self.areset()
        NKT = (TS + 256) // 128
        bf = self.bf
        MD = BF16 if bf else F32
        qTs = self.take([128, TS], 2, MD)
        kTs = self.take([128, TS + 256], 2, MD)
        V1s = self.take([128, NKT, 129], 2, MD)
        for V1 in V1s.t:
            self.memset('pool', V1[:, :, 128:129], 1.0, pw=[V1])
        PTs = self.take([128, NKT, 512], 2, MD)
        if bf:
            q32 = self.take([128, TS], 1)
            k32 = self.take([128, TS + 256], 1)
            v32 = self.take([128, NKT, 128], 1)
        obs = self.take([128, 4, 128], 2)
        rvs = self.take([128, 2], 4)
        seqs = [(p * TP, TP, False) for p in range(NP)] + [(NP * TP, TS, True)]
        for (tok0, T, is_s) in seqs:
            nk = T + (256 if is_s else 0)
            nkt = nk // 128
            QB = min(512, T)
            tks = self.prep_tk[tok0 // 128:(tok0 + T) // 128]
            ptk = self.proj_tk[tok0 // 128:(tok0 + T) // 128]
            for h in range(8):
                qT = qTs.get()
                kT = kTs.get()
                V1 = V1s.get()
                if bf:
                    qd, kd, vd = q32.get(), k32.get(), v32.get()
                else:
                    qd, kd, vd = qT, kT, V1
                self.ld(qd[:, 0:T], self.qkT_v[:, h, tok0:tok0 + T], r=tks, w=[qd])
                self.ld(kd[:, 0:T], self.qkT_v[:, 8 + h, tok0:tok0 + T], r=tks, w=[kd], eng='pool')
                self.ld(vd[:, 0:T // 128, 0:128], self.vtok[tok0:tok0 + T, h * 128:(h + 1) * 128].rearrange("(n p) e -> p n e", p=128), r=ptk, **wr(vd, bf))
                if is_s:
                    self.ld(kd[:, T:T + 256], self.ctxkT[h], r=[self.ctx_tk], pw=[kd], eng='pool')
                    self.ld(vd[:, T // 128:nkt, 0:128], self.ctx_v[h].rearrange("(n p) e -> p n e", p=128), r=[self.Win], pw=[vd])
                if bf:
                    self.cp('pool', qT[:, 0:T], qd[:, 0:T], r=[qd], w=[qT])
                    self.cp('pool', kT[:, 0:nk], kd[:, 0:nk], r=[kd], w=[kT])
                    self.cp('dve', V1[:, 0:nkt, 0:128], vd[:, 0:nkt, :], r=[vd], pw=[V1])
                for qb in range(T // QB):
                    ob = obs.get()
                    for m in range(2):
                        PT = PTs.get()
                        for kt in range(nkt):
                            pS = self.pnext()
                            self.mm(pS[:, 0:QB], kT[m * 64:(m + 1) * 64, kt * 128:(kt + 1) * 128], qT[m * 64:(m + 1) * 64, qb * QB:(qb + 1) * QB],
                                    True, True, r=[kT, qT], w=[pS])
                            self.act(PT[:, kt, 0:QB], pS[:, 0:QB], AF.Exp, r=[pS], **wr(PT, kt == 0), scale=0.125)
                        for qs in range(QB // 128):
                            pO = self.pnext()
                            for kt in range(nkt):
                                self.mm(pO[:, 0:129], PT[:, kt, qs * 128:(qs + 1) * 128], V1[:, kt, :], kt == 0, kt == nkt - 1, r=[PT, V1], **wr(pO, kt == 0))
                            rv = rvs.get()
                            self.recip(rv[:, 0:1], pO[:, 128:129], r=[pO], w=[rv])
                            if m == 0:
                                self.ts('dve', ob[:, qs, :], pO[:, 0:128], rv[:, 0:1], None, ALU.mult, r=[pO, rv], **wr(ob, qs == 0))
                            else:
                                self.tt('dve', rv[:, 1:2], rv[:, 0:1], self.nlam[:, 3:4], ALU.mult, r=[rv, self.nlam], pw=[rv])
                                self.stt('dve', ob[:, qs, :], pO[:, 0:128], rv[:, 1:2], ob[:, qs, :], ALU.mult, ALU.add, r=[pO, rv, ob], pw=[ob])
                    q0 = tok0 + qb * QB
                    nq = QB // 128
                    htk = self.hdir_tk[0][q0 // 64:(q0 + QB) // 64]
                    self.ld(self.hdir[0][q0:q0 + QB, h * 128:(h + 1) * 128].rearrange("(n p) e -> p n e", p=128), ob[:, 0:nq, :], r=[ob], pw=htk, eng='pool')

    def mixer_post(self, i, w_out, ng_tk, ng_bc, hdir, gate):
        self.areset()
        NT = 512
        if self.bf:
            W = self.take([128, 8, D], None, BF16)
            self.wstage = self.take([128, 8, 512], 2)
            self.load_w16(W, w_out.rearrange("(kc p) n -> p kc n", p=128), D)
        else:
            W = self.take([128, 8, D])
            self.ld(W[:], w_out.rearrange("(kc p) n -> p kc n", p=128), r=[self.Win], w=[W])
        hfs = self.take([128, 8, 128], 2)
        hbs = self.take([128, 8, 128], 2)
        ogs = self.take([128, 8, 128], 2)
        sqs = self.take([128, 8, 128], 2)
        sss = self.take([128, 16], 2)
        yTs = self.take([128, 8, NT], 2, BF16 if self.bf else F32)
        xbs = self.take([128, 8, NT], 2)
        for blk in range(TT // NT):
            c = 0 if blk < (NP * TP) // NT else 1
            tks = self.xT_tk[blk * 4:(blk + 1) * 4]
            xb = xbs.get()
            self.ld(xb[:], self.xT_v[:, :, blk * NT:(blk + 1) * NT], r=tks, w=[xb], eng='pool')
            yT = yTs.get()
            for q in range(4):
                t0 = blk * NT + q * 128
                hf = hfs.get()
                hb = hbs.get()
                og = ogs.get()
                self.ld(hf[:], hdir[0][t0:t0 + 128, :].rearrange("t (h e) -> t h e", h=8), r=self.hdir_tk[0][t0 // 64:t0 // 64 + 2], w=[hf])
                if gate is not None:
                    self.ld(hb[:], hdir[1][t0:t0 + 128, :].rearrange("t (h e) -> t h e", h=8), r=self.hdir_tk[1][t0 // 64:t0 // 64 + 2], w=[hb], eng='pool')
                    self.ld(og[:], self.otok[t0:t0 + 128, :].rearrange("t (h e) -> t h e", h=8), r=[self.proj_tk[t0 // 128]], w=[og])
                    self.tt('dve', hf[:], hf[:], hb[:], ALU.add, r=[hf, hb], w=[hf])
                sq = sqs.get()
                self.tt('pool', sq[:], hf[:], hf[:], ALU.mult, r=[hf], w=[sq])
                ss = sss.get()
                self.S.op('dve', lambda E, ss=ss, sq=sq: E.tensor_reduce(out=ss[:, 0:8], in_=sq[:], axis=AX.X, op=ALU.add), reads=[sq], writes=[ss])
                self.act(ss[:, 0:8], ss[:, 0:8], AF.Sqrt, r=[ss, self.epsb], w=[ss], scale=1.0 / 128, bias=self.epsb[:, 0:1])
                self.recip(ss[:, 8:16], ss[:, 0:8], r=[ss], pw=[ss])
                ngb = ng_bc.unsqueeze(1).to_broadcast([128, 8, 128])
                if gate == 'sigmoid':
                    self.act(og[:], og[:], AF.Sigmoid, r=[og], w=[og])
                    self.tt('pool', og[:], og[:], ngb, ALU.mult, r=[og, ng_tk], w=[og])
                elif gate == 'silu':
                    self.act(og[:], og[:], AF.Silu, r=[og], w=[og])
                    self.tt('pool', og[:], og[:], ngb, ALU.mult, r=[og, ng_tk], w=[og])
                else:
                    self.cp('pool', og[:], ngb, r=[ng_tk], w=[og])
                self.tt('dve', hf[:], hf[:], ss[:, 8:16].unsqueeze(2).to_broadcast([128, 8, 128]), ALU.mult, r=[hf, ss], w=[hf])
                self.tt('dve', hf[:], hf[:], og[:], ALU.mult, r=[hf, og], w=[hf])
                for hh in range(2):
                    p = self.pnext()
                    for k in range(4):
                        self.tr_(p[:, k * 128:(k + 1) * 128], hf[:, hh * 4 + k, :], r=[hf], **wr(p, k == 0))
                    dst = yT[:, hh * 4:(hh + 1) * 4, q * 128:(q + 1) * 128]
                    src = p[:, :].rearrange("p (a b) -> p a b", a=4)
                    self.cp('act' if hh else 'dve', dst, src, r=[p], **wr(yT, q == 0 and hh == 0))
            for oc in range(8):
                p = self.pnext()
                for kc in range(8):
                    self.mm(p[:, :], W[:, kc, oc * 128:(oc + 1) * 128], yT[:, kc, :], kc == 0, kc == 7, r=[W, yT], **wr(p, kc == 0))
                self.stt('dve', xb[:, oc, :], p[:, :], self.mod[:, 16 + oc, c:c + 1], xb[:, oc, :], ALU.mult, ALU.add,
                         r=[p, self.mod, xb], pw=[xb])
            for q in range(4):
                self.ld(self.xT_v[:, :, blk * NT + q * 128: blk * NT + (q + 1) * 128], xb[:, :, q * 128:(q + 1) * 128],
                        r=[xb], w=[tks[q]], eng='pool')

    def tr_(self, out, in_, r=(), w=(), pw=()):
        n = in_.shape[0]
        idn = self.cst.ap[0:n, 0, 0:n]
        self.S.op('pe', lambda E: E.transpose(out, in_, idn), reads=list(r) + [self.cst], writes=w, pw=pw)

    def stage_in(self):
        self.areset()
        xin = self.take([128, D], 2)
        xo = self.take([128, 8, 128], 2)
        for t in range(TT // 128):
            a = xin.get()
            self.ld(a[:], self.x_tok[t * 128:(t + 1) * 128, :], r=[self.Xtok], w=[a])
            b = xo.get()
            for h in range(2):
                p = self.pnext()
                for k in range(4):
                    kc = h * 4 + k
                    self.tr_(p[:, k * 128:(k + 1) * 128], a[:, kc * 128:(kc + 1) * 128], r=[a], w=[p] if k == 0 else (), pw=() if k == 0 else [p])
                dst = b[:, h * 4:(h + 1) * 4, :]
                src = p[:, :].rearrange("p (a b) -> p a b", a=4)
                if h == 0:
                    self.cp('dve', dst, src, r=[p], w=[b])
                else:
                    self.cp('act', dst, src, r=[p], pw=[b])
            self.ld(self.xT_v[:, :, t * 128:(t + 1) * 128], b[:], r=[b], w=[self.xT_tk[t]], eng='pool')

    def stage_out(self):
        self.areset()
        xi = self.take([128, 8, 128], 2)
        yo = self.take([128, D], 2)
        for t in range(TT // 128):
            a = xi.get()
            self.ld(a[:], self.xT_v[:, :, t * 128:(t + 1) * 128], r=[self.xT_tk[t]], w=[a])
            b = yo.get()
            for h in range(2):
                p = self.pnext()
                for k in range(4):
                    kc = h * 4 + k
                    self.tr_(p[:, k * 128:(k + 1) * 128], a[:, kc, :], r=[a], w=[p] if k == 0 else (), pw=() if k == 0 else [p])
                if h == 0:
                    self.cp('dve', b[:, 0:512], p[:, :], r=[p], w=[b])
                else:
                    self.cp('act', b[:, 512:1024], p[:, :], r=[p], pw=[b])
            self.ld(self.y_tok[t * 128:(t + 1) * 128, :], b[:], r=[b], w=[self.Ytok], eng='pool')

    def stage_mod(self, i):
        self.areset()
        wt = self.take([128, 8, 512], 2)
        wv = self.ada_w[i].rearrange("(kc p) n -> p kc n", p=128)
        mp = self.pnext()
        for n in range(12):
            w = wt.get()
            self.ld(w[:], wv[:, :, n * 512:(n + 1) * 512], r=[self.Win], w=[w])
            for jj in range(4):
                j = n * 4 + jj
                for kc in range(8):
                    self.mm(mp[:, 2 * j:2 * j + 2], w[:, kc, jj * 128:(jj + 1) * 128], self.sc[:, kc, :], kc == 0, kc == 7,
                            r=[w, self.sc], **wr(mp, j == 0 and kc == 0))
        mpv = mp[:, 0:96].rearrange("p (j c) -> p j c", c=2)
        for c in range(2):
            self.tt('dve', self.mod[:, :, c], mpv[:, :, c], self.adab[:, i, :], ALU.add, r=[mp, self.adab],
                    w=[self.mod] if c == 0 else (), pw=() if c == 0 else [self.mod])
        for wi in range(2):
            sj = 8 + 24 * wi
            for c in range(2):
                first = (wi == 0 and c == 0)
                self.stt('dve', self.modA[:, wi, :, c], self.mod[:, sj:sj + 8, c], 1.0, self.normg[:, i, wi, :], ALU.add, ALU.mult,
                         r=[self.mod, self.normg], w=[self.modA] if first else (), pw=() if first else [self.modA])

    def norm_mod(self, xb, hb, wi, c, nt, sq=None):
        sj = 24 * wi
        self.act(hb[:, :, :], xb[:, :, :], AF.Square, r=[xb], w=[hb])
        p = self.pnext()
        for kc in range(8):
            self.mm(p[:, 0:nt], self.cst.ap[:, 1, :], hb[:, kc, :], kc == 0, kc == 7, r=[self.cst, hb], **wr(p, kc == 0))
        rs = self.rstd.get()
        self.act(rs[:, 0:nt], p[:, 0:nt], AF.Sqrt, r=[p, self.epsb], w=[rs], scale=1.0 / D, bias=self.epsb[:, 0:1])
        self.recip(rs[:, 0:nt], rs[:, 0:nt], r=[rs], w=[rs])
        for kc in range(8):
            self.tt('dve' if kc % 2 == 0 else 'pool', hb[:, kc, :], xb[:, kc, :], rs[:, 0:nt], ALU.mult, r=[xb, rs],
                    w=[hb] if kc == 0 else (), pw=() if kc == 0 else [hb])
        for kc in range(8):
            self.act(hb[:, kc, :], hb[:, kc, :], AF.Identity, r=[hb, self.modA, self.mod], pw=[hb],
                     scale=self.modA[:, wi, kc, c:c + 1], bias=self.mod[:, sj + kc, c:c + 1])

    def norm_mod2(self, xtk, xap, htk, hap, tmp, wi, c, nt, first):
        sj = 24 * wi
        self.act(tmp[:, :, 0:nt], xap, AF.Square, r=[xtk], w=[tmp])
        p = self.pnext()
        for kc in range(8):
            self.mm(p[:, 0:nt], self.cst.ap[:, 1, :], tmp[:, kc, 0:nt], kc == 0, kc == 7, r=[self.cst, tmp], **wr(p, kc == 0))
        rs = self.rstd.get()
        self.act(rs[:, 0:nt], p[:, 0:nt], AF.Sqrt, r=[p, self.epsb], w=[rs], scale=1.0 / D, bias=self.epsb[:, 0:1])
        self.recip(rs[:, 0:nt], rs[:, 0:nt], r=[rs], w=[rs])
        for kc in range(8):
            self.tt('dve' if kc % 2 == 0 else 'pool', tmp[:, kc, 0:nt], xap[:, kc, :], rs[:, 0:nt], ALU.mult, r=[xtk, rs], **wr(tmp, kc == 0))
        for kc in range(8):
            self.act(hap[:, kc, :], tmp[:, kc, 0:nt], AF.Identity, r=[tmp, self.modA, self.mod], **wr(htk, first and kc == 0),
                     scale=self.modA[:, wi, kc, c:c + 1], bias=self.mod[:, sj + kc, c:c + 1])

    def stage_ffn16(self, i):
        self.areset()
        SB = 1024
        NH = SB // 512
        xbs = self.take([128, 8, SB], 1)
        hbs = self.take([128, 8, SB], 1, BF16)
        sq = self.take([128, 8, 512])
        self.rstd = self.take([128, 512], 2)
        acts = self.take([128, 22, SB], 1, BF16)
        wst = self.take([128, 8, 2, 128], 3)
        w16 = self.take([128, 8, 2, 128], 3, BF16)
        wost = self.take([128, 22, 128], 2)
        wo16 = self.take([128, 22, 128], 2, BF16)
        sg = self.take([128, 512], 2)
        wiv = self.ffn_w_in[i].rearrange("(kc p) n -> p kc n", p=128)
        wov = self.ffn_w_out[i].rearrange("(kc p) n -> p kc n", p=128)
        for sb in range(TT // SB):
            c = 0 if sb * SB < NP * TP else 1
            tks = self.xT_tk[sb * 8:(sb + 1) * 8]
            xb = xbs.get()
            self.ld(xb[:], self.xT_v[:, :, sb * SB:(sb + 1) * SB], r=tks, w=[xb], eng='pool')
            hb = hbs.get()
            for hf in range(NH):
                hs = slice(hf * 512, (hf + 1) * 512)
                self.norm_mod2(xb, xb[:, :, hs], hb, hb[:, :, hs], sq, 1, c, 512, hf == 0)
            at = acts.get()
            for j in range(22):
                ws = wst.get()
                self.ld(ws[:, :, 0, :], wiv[:, :, j * 128:(j + 1) * 128], r=[self.Win], w=[ws])
                self.ld(ws[:, :, 1, :], wiv[:, :, DFF + j * 128:DFF + (j + 1) * 128], r=[self.Win], pw=[ws])
                w = w16.get()
                self.cp('pool', w[:], ws[:], r=[ws], w=[w])
                for hf in range(NH):
                    hs = slice(hf * 512, (hf + 1) * 512)
                    pg = self.pnext()
                    pu = self.pnext()
                    for kc in range(8):
                        self.mm(pg[:, :], w[:, kc, 0, :], hb[:, kc, hs], kc == 0, kc == 7, r=[w, hb], **wr(pg, kc == 0))
                    for kc in range(8):
                        self.mm(pu[:, :], w[:, kc, 1, :], hb[:, kc, hs], kc == 0, kc == 7, r=[w, hb], **wr(pu, kc == 0))
                    s_ = sg.get()
                    self.act(s_[:], pg[:, :], AF.Silu, r=[pg], w=[s_])
                    self.tt('dve', at[:, j, hs], s_[:], pu[:, :], ALU.mult, r=[s_, pu], **wr(at, j == 0 and hf == 0))
            for oc in range(8):
                ws = wost.get()
                self.ld(ws[:], wov[:, :, oc * 128:(oc + 1) * 128], r=[self.Win], w=[ws])
                w = wo16.get()
                self.cp('pool', w[:], ws[:], r=[ws], w=[w])
                for hf in range(NH):
                    hs = slice(hf * 512, (hf + 1) * 512)
                    p = self.pnext()
                    for k2 in range(22):
                        self.mm(p[:, :], w[:, k2, :], at[:, k2, hs], k2 == 0, k2 == 21, r=[w, at], **wr(p, k2 == 0))
                    self.stt('dve', xb[:, oc, hs], p[:, :], self.mod[:, 40 + oc, c:c + 1], xb[:, oc, hs], ALU.mult, ALU.add,
                             r=[p, self.mod, xb], pw=[xb])
            for q in range(SB // 128):
                self.ld(self.xT_v[:, :, sb * SB + q * 128: sb * SB + (q + 1) * 128], xb[:, :, q * 128:(q + 1) * 128],
                        r=[xb], w=[tks[q]], eng='pool')

    def stage_ffn(self, i):
        self.areset()
        NT = 512
        xbs = self.take([128, 8, NT], 2)
        hbs = self.take([128, 8, NT], 1)
        self.rstd = self.take([128, NT], 2)
        acts = self.take([128, 22, NT], 1)
        sg = self.take([128, NT], 2)
        wins = self.take([128, 8, 2, 128], 3)
        wouts = self.take([128, 22, 128], 2)
        wiv = self.ffn_w_in[i].rearrange("(kc p) n -> p kc n", p=128)
        wov = self.ffn_w_out[i].rearrange("(kc p) n -> p kc n", p=128)
        for blk in range(TT // NT):
            c = 0 if blk < (NP * TP) // NT else 1
            tks = self.xT_tk[blk * 4:(blk + 1) * 4]
            xb = xbs.get()
            self.ld(xb[:], self.xT_v[:, :, blk * NT:(blk + 1) * NT], r=tks, w=[xb], eng='pool')
            hb = hbs.get()
            self.norm_mod(xb, hb, 1, c, NT)
            at = acts.get()
            for j in range(22):
                w = wins.get()
                self.ld(w[:, :, 0, :], wiv[:, :, j * 128:(j + 1) * 128], r=[self.Win], w=[w])
                self.ld(w[:, :, 1, :], wiv[:, :, DFF + j * 128:DFF + (j + 1) * 128], r=[self.Win], pw=[w])
                pg = self.pnext()
                pu = self.pnext()
                for kc in range(8):
                    self.mm(pg[:, :], w[:, kc, 0, :], hb[:, kc, :], kc == 0, kc == 7, r=[w, hb], **wr(pg, kc == 0))
                for kc in range(8):
                    self.mm(pu[:, :], w[:, kc, 1, :], hb[:, kc, :], kc == 0, kc == 7, r=[w, hb], **wr(pu, kc == 0))
                s = sg.get()
                self.act(s[:], pg[:, :], AF.Silu, r=[pg], w=[s])
                self.tt('dve', at[:, j, :], s[:], pu[:, :], ALU.mult, r=[s, pu], w=[at] if j == 0 else (), pw=() if j == 0 else [at])
            for oc in range(8):
                w = wouts.get()
                self.ld(w[:], wov[:, :, oc * 128:(oc + 1) * 128], r=[self.Win], w=[w])
                p = self.pnext()
                for k2 in range(22):
                    self.mm(p[:, :], w[:, k2, :], at[:, k2, :], k2 == 0, k2 == 21, r=[w, at], **wr(p, k2 == 0))
                self.stt('dve', xb[:, oc, :], p[:, :], self.mod[:, 40 + oc, c:c + 1], xb[:, oc, :], ALU.mult, ALU.add,
                         r=[p, self.mod, xb], pw=[xb])
            for q in range(4):
                self.ld(self.xT_v[:, :, blk * NT + q * 128: blk * NT + (q + 1) * 128], xb[:, :, q * 128:(q + 1) * 128],
                        r=[xb], w=[tks[q]], eng='pool')


def host_consts():
    c = np.zeros((128, 8, 128), np.float32)
    c[:, 0, :] = np.eye(128)
    c[:, 1, :] = 1.0
    k = np.arange(128)[:, None]
    t = np.arange(128)[None, :]
    c[:, 2, :] = (k <= t)
    c[:, 3, :] = (k >= t)
    c[:, 4, :] = (k > t)
    c[:, 5, :] = (k < t)
    return c.reshape(128, 1024)


def rope_tables():
    rows = TS // 64
    row = np.broadcast_to(np.arange(rows)[:, None], (rows, 64)).reshape(-1)
    col = np.broadcast_to(np.arange(64)[None, :], (rows, 64)).reshape(-1)
    inv = (np.float32(10000.0) ** (-np.arange(16, dtype=np.float32) / np.float32(16))).astype(np.float32)
    ang = np.stack([row, col], axis=-1).astype(np.float32)[:, :, None] * inv
    return np.concatenate([np.cos(ang).reshape(TS, 32), np.sin(ang).reshape(TS, 32)], axis=1).astype(np.float32)


_CACHE = {}


def kernel(**inp):
    opts = inp.pop('_opts', {})
    key = repr(sorted(opts.items()))
    if key not in _CACHE:
        P = Prog(opts)
        P.build()
        _CACHE[key] = P
    P = _CACHE[key]
    f = lambda a: np.ascontiguousarray(np.asarray(a, dtype=np.float32))
    xp = f(inp['x_prompt'])
    xs = f(inp['x_sample'])
    c = f(inp['c'])
    c_ctx = f(inp['c_ctx'])
    ada_b = f(inp['ada_b'])
    norm_g = f(inp['norm_g'])
    shared = {
        'consts': host_consts(),
        'ada_w': f(inp['ada_w']),
        'ada_bT': f(ada_b.reshape(4, 48, 128).transpose(2, 0, 1)),
        'normgT': f(norm_g.reshape(4, 2, 8, 128).transpose(3, 0, 1, 2)),
        'ffn_w_in': f(inp['ffn_w_in']),
        'ffn_w_out': f(inp['ffn_w_out']),
        'mlstm_w_in': f(inp['mlstm_w_in'][0]),
        'mlstm_w_out': f(inp['mlstm_w_out'][0]),
        'mlstm_gate_b': f(inp['mlstm_gate_b'].reshape(1, 32)),
        'mlstm_norm_g': f(inp['mlstm_norm_g'].reshape(1, 128)),
    }
    shared.update({
        'diff_w_in': f(inp['diff_w_in'][0]),
        'diff_w_out': f(inp['diff_w_out'][0]),
        'diff_qkg': f(np.concatenate([inp['diff_q_norm_g'][0], inp['diff_k_norm_g'][0]]).reshape(1, 128)),
        'diff_lambda': f(inp['diff_lambda'][0].reshape(1, 256)),
        'diff_subln_g': f(inp['diff_subln_g'][0].reshape(1, 128)),
        'rope_cs': rope_tables(),
    })
    cw = f(inp['gdn_conv_w'])
    shared.update({
        'gdn_w_in': f(inp['gdn_w_in']),
        'gdn_w_out': f(inp['gdn_w_out']),
        'gdn_convT': f(cw.reshape(2, 5, 24, 128).transpose(3, 0, 2, 1)),
        'gdn_a_log': f(inp['gdn_a_log'].reshape(2, 1, 16)),
        'gdn_dt_bias': f(inp['gdn_dt_bias'].reshape(2, 1, 16)),
        'gdn_norm_g': f(inp['gdn_norm_g'].reshape(2, 1, 128)),
    })
    stS = f(inp['state_delta'])
    ck = f(inp['cache_diff_k'])
    cv = f(inp['cache_diff_v'])
    stC = f(inp['state_mlstm_C'])
    stn = f(inp['state_mlstm_n'])
    stm = f(inp['state_mlstm_m'])
    in_maps = []
    for k in range(NCORE):
        m = dict(shared)
        m['x_tok'] = f(np.concatenate([xp[NP * k:NP * (k + 1)].reshape(NP * TP, D), xs[k]], axis=0))
        cond = np.stack([c_ctx, c[k]], axis=-1)
        m['condT'] = f(cond.reshape(8, 128, 2).transpose(1, 0, 2))
        m['st_S'] = f(stS[k])
        m['ctx_k'] = f(ck[k, 0])
        m['ctx_v'] = f(cv[k, 0])
        m['st_C'] = f(stC[k, 0])
        m['st_n'] = f(stn[k, 0].reshape(2, 8, 64, 1))
        m['st_m'] = f(stm[k, 0].reshape(1, 16))
        in_maps.append({n: m[n] for n in P.din})
    res = run_bass_kernel_spmd(P.nc, in_maps, core_ids=list(range(NCORE)))
    R = res.results
    y = np.stack([r['y_tok'] for r in R])
    y_prompt = y[:, :NP * TP].reshape(NCORE * NP, TP, D)
    y_sample = y[:, NP * TP:]
    outs = [y_prompt, y_sample]
    if 'newS' in P.dout:
        outs.append(np.stack([r['newS'] for r in R]).reshape(NCORE * NP, 2, 2, 8, 128, 128))
    if 'newC' in P.dout:
        outs.append(np.stack([r['newC'] for r in R]).reshape(NCORE * NP, 1, 2, 8, 64, 128))
        outs.append(np.stack([r['newn'] for r in R]).reshape(NCORE * NP, 1, 2, 8, 64))
        outs.append(np.stack([r['newm'] for r in R]).reshape(NCORE * NP, 1, 2, 8))
    if 'newk' in P.dout:
        outs.append(np.stack([r['newk'] for r in R]).reshape(NCORE * NP, 1, 8, 2, TP, 64))
        outs.append(np.stack([r['newv'] for r in R]).reshape(NCORE * NP, 1, 8, TP, 128))
    return tuple(outs)
```

```python
import numpy as np
from contextlib import ExitStack
import concourse.bass as bass
import concourse.mybir as mybir
from concourse.bass_utils import run_bass_kernel_spmd

F32 = mybir.dt.float32
BF16 = mybir.dt.bfloat16
AF = mybir.ActivationFunctionType
ALU = mybir.AluOpType
AX = mybir.AxisListType

ENGS = ('pe', 'act', 'dve', 'pool', 'sp')
NDS = 40

D = 1024
NCORE = 8
NP = 4
TP = 256
TS = 2048
TT = NP * TP + TS
DFF = 2816
EPS = 1e-6


class Tk:
    __slots__ = ('ap', 'lw', 'rd', 'rp', 'name')

    def __init__(self, ap, name=''):
        self.ap = ap
        self.lw = {}
        self.rd = {}
        self.rp = {}
        self.name = name

    def __getitem__(self, idx):
        return self.ap[idx]


class Sched:
    def __init__(self, nc, es):
        self.nc = nc
        self.es = es
        self.q = {e: [] for e in ENGS}
        self.sem = {e: es.enter_context(nc.semaphore("s_" + e)) for e in ENGS}
        self.cnt = {e: 0 for e in ENGS}
        self.seen = {e: {} for e in ENGS}
        self.dsem = [es.enter_context(nc.semaphore("d%d" % i)) for i in range(NDS)]
        self.dcnt = [0] * NDS
        self.dnext = 0
        self.nins = 0
        self.uid = 0

    def sb(self, shape, dt=F32, name=None):
        self.uid += 1
        name = name or "t%d" % self.uid
        t = self.es.enter_context(self.nc.sbuf_tensor(name, list(shape), dt))
        return Tk(t, name)

    def ps(self, shape, dt=F32, name=None):
        self.uid += 1
        name = name or "p%d" % self.uid
        t = self.es.enter_context(self.nc.psum_tensor(name, list(shape), dt))
        return Tk(t, name)

    def _wait(self, eng, d):
        k = d[0]
        if eng == 'pe' and k == ('e', 'pe'):
            return
        seen = self.seen[eng]
        if seen.get(k, 0) >= d[2]:
            return
        seen[k] = d[2]
        self.q[eng].append(lambda E, d=d: E.wait_ge(d[1], d[2]))
        self.nins += 1

    def _deps(self, eng, reads, writes, pw):
        deps = {}

        def add(d):
            k = d[0]
            if k not in deps or deps[k][2] < d[2]:
                deps[k] = d
        for t in reads:
            for d in t.lw.values():
                add(d)
        for t in writes:
            for d in t.lw.values():
                add(d)
            for d in t.rd.values():
                add(d)
        for t in pw:
            for d in t.rd.values():
                add(d)
            for d in t.rp.values():
                add(d)
        for d in deps.values():
            self._wait(eng, d)

    def _mark(self, me, reads, writes, pw):
        for t in reads:
            t.rd[me[0]] = me
        for t in writes:
            t.lw = {me[0]: me}
            t.rp = t.rd
            t.rd = {}
        for t in pw:
            t.lw[me[0]] = me

    def op(self, eng, fn, reads=(), writes=(), pw=()):
        self._deps(eng, reads, writes, pw)
        self.cnt[eng] += 1
        sem = self.sem[eng]
        me = (('e', eng), sem, self.cnt[eng])
        self.q[eng].append(lambda E: fn(E).then_inc(sem, 1))
        self.nins += 1
        self._mark(me, reads, writes, pw)

    def dma(self, eng, out_ap, in_ap, reads=(), writes=(), pw=()):
        slot = self.dnext
        self.dnext = (slot + 1) % NDS
        ds = self.dsem[slot]
        self._deps(eng, reads, writes, pw)
        if self.dcnt[slot] > 0:
            self._wait(eng, (('d', slot), ds, 16 * self.dcnt[slot]))
        self.dcnt[slot] += 1
        me = (('d', slot), ds, 16 * self.dcnt[slot])
        self.q[eng].append(lambda E: E.dma_start(out=out_ap, in_=in_ap).then_inc(ds, 16))
        self.nins += 1
        self._mark(me, reads, writes, pw)

    def barrier(self):
        for e in ENGS:
            for o in ENGS:
                if o != e and self.cnt[o] > 0:
                    self._wait(e, (('e', o), self.sem[o], self.cnt[o]))
            for i in range(NDS):
                if self.dcnt[i] > 0:
                    self._wait(e, (('d', i), self.dsem[i], 16 * self.dcnt[i]))

    def emit(self):
        self.barrier()
        q = self.q
        with self.nc.Block() as block:
            @block.tensor
            def _(E):
                for f in q['pe']:
                    f(E)

            @block.scalar
            def _(E):
                for f in q['act']:
                    f(E)

            @block.vector
            def _(E):
                for f in q['dve']:
                    f(E)

            @block.gpsimd
            def _(E):
                for f in q['pool']:
                    f(E)

            @block.sync
            def _(E):
                for f in q['sp']:
                    f(E)


def wr(t, first):
    return {'w': [t]} if first else {'pw': [t]}


class Rot:
    def __init__(self, tiles):
        self.t = tiles
        self.i = 0

    def get(self):
        t = self.t[self.i % len(self.t)]
        self.i += 1
        return t


ARENA_COLS = 50000


class Prog:
    def __init__(self, opts):
        self.opts = opts
        self.nc = bass.Bass("TRN2", target_bir_lowering=False)
        self.es = ExitStack()
        self.din = {}
        self.dout = {}

    def inp(self, name, shape):
        t = self.nc.dram_tensor(name, list(shape), F32, kind="ExternalInput").ap()
        self.din[name] = t
        return t

    def outp(self, name, shape):
        t = self.nc.dram_tensor(name, list(shape), F32, kind="ExternalOutput").ap()
        self.dout[name] = t
        return t

    def scratch(self, name, shape):
        return self.nc.dram_tensor(name, list(shape), F32, kind="Internal").ap()

    def areset(self):
        self.S.barrier()
        self.apos = 0

    def take(self, shape, n=None, dt=F32):
        cols = int(np.prod(shape[1:]))
        c32 = cols if dt == F32 else (cols + 1) // 2
        out = []
        for _ in range(n or 1):
            assert self.apos + c32 <= ARENA_COLS, ("arena overflow", self.apos, c32)
            ap = self.arena[0:shape[0], self.apos:self.apos + c32]
            if dt != F32:
                ap = ap.bitcast(dt)[:, 0:cols]
            if len(shape) == 3:
                ap = ap.rearrange("p (a b) -> p a b", a=shape[1])
            elif len(shape) == 4:
                ap = ap.rearrange("p (a b c) -> p a b c", a=shape[1], b=shape[2])
            self.apos += c32
            out.append(Tk(ap))
        return out[0] if n is None else Rot(out)

    def pnext(self):
        p = self.psum[self.pi % 8]
        self.pi += 1
        return p

    def pns(self):
        return self.pnext()

    def mm(self, out, lhsT, rhs, start, stop, r=(), w=(), pw=()):
        self.S.op('pe', lambda E: E.matmul(out, lhsT=lhsT, rhs=rhs, start=start, stop=stop), reads=r, writes=w, pw=pw)

    def tr(self, out, in_, r=(), w=(), pw=()):
        ident = self.ident
        n = in_.shape[0]
        self.S.op('pe', lambda E: E.transpose(out, in_, ident[0:n, 0:n]), reads=list(r) + [ident], writes=w, pw=pw)

    def act(self, out, in_, func, r=(), w=(), pw=(), bias=None, scale=None, accum=None):
        kw = {}
        if bias is not None:
            kw['bias'] = bias
        if scale is not None:
            kw['scale'] = scale
        if accum is not None:
            kw['accum_out'] = accum
        self.S.op('act', lambda E: E.activation(out=out, in_=in_, func=func, **kw), reads=r, writes=w, pw=pw)

    def tt(self, eng, out, a, b, op, r=(), w=(), pw=()):
        self.S.op(eng, lambda E: E.tensor_tensor(out=out, in0=a, in1=b, op=op), reads=r, writes=w, pw=pw)

    def ts(self, eng, out, a, s1, s2, op0, op1=None, r=(), w=(), pw=()):
        if op1 is None:
            self.S.op(eng, lambda E: E.tensor_scalar(out=out, in0=a, scalar1=s1, scalar2=None, op0=op0), reads=r, writes=w, pw=pw)
        else:
            self.S.op(eng, lambda E: E.tensor_scalar(out=out, in0=a, scalar1=s1, scalar2=s2, op0=op0, op1=op1), reads=r, writes=w, pw=pw)

    def stt(self, eng, out, a, s, b, op0, op1, r=(), w=(), pw=()):
        self.S.op(eng, lambda E: E.scalar_tensor_tensor(out=out, in0=a, scalar=s, in1=b, op0=op0, op1=op1), reads=r, writes=w, pw=pw)

    def cp(self, eng, out, in_, r=(), w=(), pw=()):
        if eng == 'act':
            self.S.op('act', lambda E: E.copy(out=out, in_=in_), reads=r, writes=w, pw=pw)
        else:
            self.S.op(eng, lambda E: E.tensor_copy(out=out, in_=in_), reads=r, writes=w, pw=pw)

    def recip(self, out, in_, r=(), w=(), pw=()):
        self.S.op('dve', lambda E: E.reciprocal(out=out, in_=in_), reads=r, writes=w, pw=pw)

    def memset(self, eng, ap, val, w=(), pw=()):
        self.S.op(eng, lambda E: E.memset(ap, val), writes=w, pw=pw)

    def ld(self, out, in_, r=(), w=(), pw=(), eng='sp'):
        self.S.dma(eng, out, in_, reads=r, writes=w, pw=pw)

    def build(self):
        nc = self.nc
        o = self.opts
        with self.es:
            S = self.S = Sched(nc, self.es)
            self.arena = self.es.enter_context(nc.sbuf_tensor("arena", [128, ARENA_COLS], F32))
            self.psum = [S.ps([128, 512], name="ps%d" % i) for i in range(8)]
            self.pi = 0
            self.apos = 0
            self.psmall = [Tk(self.psum[i // 2].ap[:, (i % 2) * 256:(i % 2) * 256 + 256]) for i in range(16)]
            self.psi = 0
            self.x_tok = self.inp("x_tok", [TT, D])
            self.y_tok = self.outp("y_tok", [TT, D])
            self.xT = self.scratch("xT", [8, 128, TT])
            self.xT_v = self.xT.rearrange("c p t -> p c t")
            self.xT_tk = [Tk(None, "xT%d" % i) for i in range(TT // 128)]
            self.Xtok = Tk(None)
            self.Ytok = Tk(None)
            self.Win = Tk(None)
            consts = self.inp("consts", [128, 8 * 128])
            condT = self.inp("condT", [128, 8, 2])
            self.ada_w = self.inp("ada_w", [4, D, 6 * D])
            ada_bT = self.inp("ada_bT", [128, 4, 48])
            normgT = self.inp("normgT", [128, 4, 2, 8])
            self.ffn_w_in = self.inp("ffn_w_in", [4, D, 2 * DFF])
            self.ffn_w_out = self.inp("ffn_w_out", [4, DFF, D])
            self.cst = S.sb([128, 8, 128], name="cst")
            self.ld(self.cst[:], consts.rearrange("p (a b) -> p a b", a=8), r=[self.Win], w=[self.cst])
            self.ones = self.cst.ap[:, 1, :]
            self.sc = S.sb([128, 8, 2], name="sc")
            self.ld(self.sc[:], condT, r=[self.Win], w=[self.sc])
            self.act(self.sc[:], self.sc[:], AF.Silu, r=[self.sc], w=[self.sc])
            self.adab = S.sb([128, 4, 48], name="adab")
            self.ld(self.adab[:], ada_bT, r=[self.Win], w=[self.adab])
            self.normg = S.sb([128, 4, 2, 8], name="normg")
            self.ld(self.normg[:], normgT, r=[self.Win], w=[self.normg])
            self.mod = S.sb([128, 48, 2], name="mod")
            self.modA = S.sb([128, 2, 8, 2], name="modA")
            self.epsb = S.sb([128, 1], name="epsb")
            self.memset('pool', self.epsb[:], EPS, w=[self.epsb])

            self.stage_in()
            self.bf = o.get('bf16', True)
            mixers = o.get('mixers', (0, 1, 2))
            self.setup_mixers(mixers)
            for i in range(o.get('depth', 4)):
                self.stage_mod(i)
                if i % 3 == 0 and 0 in mixers:
                    self.gdn(i)
                if i % 3 == 1 and 1 in mixers:
                    self.mlstm(i)
                if i % 3 == 2 and 2 in mixers:
                    self.diffattn(i)
                if o.get('ffn', True):
                    if self.bf:
                        self.stage_ffn16(i)
                    else:
                        self.stage_ffn(i)
            self.stage_out()
            S.emit()
        return nc

    def setup_mixers(self, mixers):
        S = self.S
        if 1 in mixers:
            self.ml_w_in = self.inp("mlstm_w_in", [D, 3104])
            self.ml_w_out = self.inp("mlstm_w_out", [D, D])
            ml_gb = self.inp("mlstm_gate_b", [1, 32])
            ml_ng = self.inp("mlstm_norm_g", [1, 128])
            self.st_C = self.inp("st_C", [2, 8, 64, 128])
            self.st_n = self.inp("st_n", [2, 8, 64, 1])
            self.st_m = self.inp("st_m", [1, 16])
            self.newC = self.outp("newC", [NP, 2, 8, 64, 128])
            self.newn = self.outp("newn", [NP, 2, 8, 64, 1])
            self.newm = self.outp("newm", [NP, 2, 8, 1])
            self.ml_gb = S.sb([128, 32], name="ml_gb")
            self.ld(self.ml_gb[:], ml_gb.partition_broadcast(128), r=[self.Win], w=[self.ml_gb])
            self.ml_ng = S.sb([128, 128], name="ml_ng")
            self.ld(self.ml_ng[:], ml_ng.partition_broadcast(128), r=[self.Win], w=[self.ml_ng])
        self.Oout = Tk(None)
        self.qkT = self.scratch("qkT", [24, 128, TT])
        self.qkT_v = self.qkT.rearrange("c p t -> p c t")
        self.qkT_h = self.qkT[0:8].rearrange("c (two p) t -> p (c two) t", two=2)
        self.ktok = self.scratch("ktok", [TT, 2048])
        self.vtok = self.scratch("vtok", [TT, 1024])
        self.otok = self.scratch("otok", [TT, 1024])
        self.gtok = self.scratch("gtok", [TT, 32])
        self.hdir = [self.scratch("hdir%d" % d, [TT, 1024]) for d in range(2)]
        self.proj_tk = [Tk(None) for _ in range(TT // 128)]
        self.prep_tk = [Tk(None) for _ in range(TT // 128)]
        self.hdir_tk = [[Tk(None) for _ in range(TT // 64)] for d in range(2)]
        if 2 in mixers:
            self.df_w_in = self.inp("diff_w_in", [D, 3072])
            self.df_w_out = self.inp("diff_w_out", [D, D])
            df_g = self.inp("diff_qkg", [1, 128])
            df_lam = self.inp("diff_lambda", [1, 256])
            df_sg = self.inp("diff_subln_g", [1, 128])
            self.rope_cs = self.inp("rope_cs", [TS, 64])
            self.ctx_k = self.inp("ctx_k", [8, 2, 256, 64])
            self.ctx_v = self.inp("ctx_v", [8, 256, 128])
            self.newk = self.outp("newk", [NP, 8, 2, TP, 64])
            self.newv = self.outp("newv", [NP, 8, TP, 128])
            self.df_g = S.sb([128, 2, 64], name="df_g")
            self.ld(self.df_g[:], df_g.rearrange("o (a b) -> o a b", a=2).partition_broadcast(128), r=[self.Win], w=[self.df_g])
            self.df_sg = S.sb([128, 128], name="df_sg")
            self.ld(self.df_sg[:], df_sg.partition_broadcast(128), r=[self.Win], w=[self.df_sg])
            lam_init = 0.8 - 0.6 * float(np.exp(-0.3 * 2))
            self.ts('dve', self.df_sg[:], self.df_sg[:], 1.0 - lam_init, None, ALU.mult, r=[self.df_sg], w=[self.df_sg])
            lm = S.sb([128, 4, 64], name="df_lm")
            self.ld(lm[:], df_lam.rearrange("o (a b) -> o a b", a=4).partition_broadcast(128), r=[self.Win], w=[lm])
            l2 = S.sb([128, 2, 64], name="df_l2")
            self.tt('dve', l2[:, 0, :], lm[:, 0, :], lm[:, 1, :], ALU.mult, r=[lm], w=[l2])
            self.tt('dve', l2[:, 1, :], lm[:, 2, :], lm[:, 3, :], ALU.mult, r=[lm], pw=[l2])
            self.nlam = S.sb([128, 4], name="nlam")
            nl = self.nlam
            S.op('dve', lambda E: E.tensor_reduce(out=nl[:, 0:2], in_=l2[:], axis=AX.X, op=ALU.add), reads=[l2], writes=[nl])
            self.act(nl[:, 0:2], nl[:, 0:2], AF.Exp, r=[nl], w=[nl])
            self.tt('dve', nl[:, 2:3], nl[:, 1:2], nl[:, 0:1], ALU.subtract, r=[nl], pw=[nl])
            self.ts('dve', nl[:, 3:4], nl[:, 2:3], -lam_init, None, ALU.add, r=[nl], pw=[nl])
            self.ctxkT = self.scratch("ctxkT", [8, 128, 256])
            self.ctx_tk = Tk(None)
        if 0 in mixers:
            self.gd_w_in = self.inp("gdn_w_in", [2, D, 4128])
            self.gd_w_out = self.inp("gdn_w_out", [2, D, D])
            gd_cw = self.inp("gdn_convT", [128, 2, 24, 5])
            gd_al = self.inp("gdn_a_log", [2, 1, 16])
            gd_dt = self.inp("gdn_dt_bias", [2, 1, 16])
            gd_ng = self.inp("gdn_norm_g", [2, 1, 128])
            self.st_S = self.inp("st_S", [2, 2, 8, 128, 128])
            self.newS = self.outp("newS", [NP, 2, 2, 8, 128, 128])
            self.gd_cw = S.sb([128, 2, 24, 5], name="gd_cw")
            self.ld(self.gd_cw[:], gd_cw, r=[self.Win], w=[self.gd_cw])
            self.gd_nea = S.sb([128, 2, 16], name="gd_nea")
            self.gd_dt = S.sb([128, 2, 16], name="gd_dt")
            self.gd_ng = S.sb([128, 2, 128], name="gd_ng")
            for j in range(2):
                self.ld(self.gd_nea[:, j, :], gd_al[j].partition_broadcast(128), r=[self.Win], **wr(self.gd_nea, j == 0))
                self.ld(self.gd_dt[:, j, :], gd_dt[j].partition_broadcast(128), r=[self.Win], **wr(self.gd_dt, j == 0))
                self.ld(self.gd_ng[:, j, :], gd_ng[j].partition_broadcast(128), r=[self.Win], **wr(self.gd_ng, j == 0))
            self.act(self.gd_nea[:], self.gd_nea[:], AF.Exp, r=[self.gd_nea], w=[self.gd_nea])
            self.ts('dve', self.gd_nea[:], self.gd_nea[:], -1.0, None, ALU.mult, r=[self.gd_nea], w=[self.gd_nea])

    def proj_stage(self, i, w_in, ncol, fm, tm, col_lo=0):
        self.areset()
        NT = 512
        xbs = self.take([128, 8, NT], 1)
        hbs = self.take([128, 8, NT], 1)
        self.rstd = self.take([128, NT], 2)
        W = self.take([128, 8, ncol])
        wv_ = w_in.rearrange("(kc p) n -> p kc n", p=128)
        self.ld(W[:, 0:4, :], wv_[:, 0:4, col_lo:col_lo + ncol], r=[self.Win], w=[W])
        self.ld(W[:, 4:8, :], wv_[:, 4:8, col_lo:col_lo + ncol], r=[self.Win], pw=[W], eng='pool')
        fm = [(a - col_lo, b, c_, d_, e_) for (a, b, c_, d_, e_) in fm]
        tm = [(a - col_lo, b, c_) for (a, b, c_) in tm]
        ofm = self.take([128, NT], 3)
        otm = self.take([128, 512], 3)
        for blk in range(TT // NT):
            c = 0 if blk < (NP * TP) // NT else 1
            tks = self.xT_tk[blk * 4:(blk + 1) * 4]
            ptk = self.proj_tk[blk * 4:(blk + 1) * 4]
            xb = xbs.get()
            self.ld(xb[:], self.xT_v[:, :, blk * NT:(blk + 1) * NT], r=tks, w=[xb], eng='pool')
            hb = hbs.get()
            self.norm_mod(xb, hb, 0, c, NT)
            n = 0
            for (col0, nch, dstv, ch0, scale) in fm:
                for oc in range(nch):
                    p = self.pnext()
                    for kc in range(8):
                        self.mm(p[:, :], W[:, kc, col0 + oc * 128: col0 + (oc + 1) * 128], hb[:, kc, :], kc == 0, kc == 7, r=[W, hb], **wr(p, kc == 0))
                    ot = ofm.get()
                    if n % 2 == 0:
                        self.act(ot[:], p[:, :], AF.Copy, r=[p], w=[ot], scale=scale)
                    else:
                        self.ts('dve', ot[:], p[:, :], scale, None, ALU.mult, r=[p], w=[ot])
                    n += 1
                    self.ld(dstv[:, ch0 + oc, blk * NT:(blk + 1) * NT], ot[:], r=[ot], pw=ptk, eng='pool')
            for q in range(4):
                t0 = blk * NT + q * 128
                for (col0, ncols, dst) in tm:
                    for g0 in range(0, ncols, 512):
                        gw = min(512, ncols - g0)
                        p = self.pnext()
                        for kc in range(8):
                            self.mm(p[:, 0:gw], hb[:, kc, q * 128:(q + 1) * 128], W[:, kc, col0 + g0: col0 + g0 + gw], kc == 0, kc == 7, r=[W, hb], **wr(p, kc == 0))
                        ot = otm.get()
                        if n % 2 == 0:
                            self.cp('act', ot[:, 0:gw], p[:, 0:gw], r=[p], w=[ot])
                        else:
                            self.cp('dve', ot[:, 0:gw], p[:, 0:gw], r=[p], w=[ot])
                        n += 1
                        self.ld(dst[t0:t0 + 128, g0:g0 + gw], ot[:, 0:gw], r=[ot], pw=[ptk[q]], eng='sp')

    def load_w16(self, W16, w_view, ncol, col_lo=0, piece=512, eng2='pool'):
        n = 0
        for c0 in range(0, ncol, piece):
            cw = min(piece, ncol - c0)
            st = self.wstage.get()
            self.ld(st[:, :, 0:cw], w_view[:, :, col_lo + c0:col_lo + c0 + cw], r=[self.Win], w=[st], eng='sp' if n % 2 == 0 else eng2)
            if n % 2 == 0:
                self.cp('dve', W16[:, :, c0:c0 + cw], st[:, :, 0:cw], r=[st], **wr(W16, c0 == 0))
            else:
                self.cp('act', W16[:, :, c0:c0 + cw], st[:, :, 0:cw], r=[st], **wr(W16, c0 == 0))
            n += 1

    def proj_stage16(self, i, w_in, ncol, fm, tm):
        self.areset()
        NT = 512
        xbs = self.take([128, 8, NT], 2)
        hbs = self.take([128, 8, NT], 2, BF16)
        sq = self.take([128, 8, NT])
        self.rstd = self.take([128, NT], 2)
        W = self.take([128, 8, ncol], None, BF16)
        self.wstage = self.take([128, 8, 512], 2)
        self.load_w16(W, w_in.rearrange("(kc p) n -> p kc n", p=128), ncol)
        ofm = self.take([128, NT], 3)
        otm = self.take([128, 512], 3)
        for blk in range(TT // NT):
            c = 0 if blk < (NP * TP) // NT else 1
            tks = self.xT_tk[blk * 4:(blk + 1) * 4]
            xb = xbs.get()
            self.ld(xb[:], self.xT_v[:, :, blk * NT:(blk + 1) * NT], r=tks, w=[xb], eng='pool')
            hb = hbs.get()
            self.norm_mod2(xb, xb[:, :, :], hb, hb[:, :, :], sq, 0, c, NT, True)
            n = 0
            for (col0, nch, dstv, ch0, scale) in fm:
                for oc in range(nch):
                    p = self.pnext()
                    for kc in range(8):
                        self.mm(p[:, :], W[:, kc, col0 + oc * 128: col0 + (oc + 1) * 128], hb[:, kc, :], kc == 0, kc == 7, r=[W, hb], **wr(p, kc == 0))
                    ot = ofm.get()
                    if n % 2 == 0:
                        self.act(ot[:], p[:, :], AF.Copy, r=[p], w=[ot], scale=scale)
                    else:
                        self.ts('dve', ot[:], p[:, :], scale, None, ALU.mult, r=[p], w=[ot])
                    n += 1
                    self.ld(dstv[:, ch0 + oc, blk * NT:(blk + 1) * NT], ot[:], r=[ot], pw=[self.Oout], eng='pool')
            for q in range(4):
                t0 = blk * NT + q * 128
                for (col0, ncols, dst) in tm:
                    for g0 in range(0, ncols, 512):
                        gw = min(512, ncols - g0)
                        p = self.pnext()
                        for kc in range(8):
                            self.mm(p[:, 0:gw], hb[:, kc, q * 128:(q + 1) * 128], W[:, kc, col0 + g0: col0 + g0 + gw], kc == 0, kc == 7, r=[W, hb], **wr(p, kc == 0))
                        ot = otm.get()
                        if n % 2 == 0:
                            self.cp('act', ot[:, 0:gw], p[:, 0:gw], r=[p], w=[ot])
                        else:
                            self.cp('dve', ot[:, 0:gw], p[:, 0:gw], r=[p], w=[ot])
                        n += 1
                        self.ld(dst[t0:t0 + 128, g0:g0 + gw], ot[:, 0:gw], r=[ot], pw=[self.Oout], eng='sp')

    def mlstm(self, i):
        o = self.opts
        (self.proj_stage16 if self.bf else self.proj_stage)(i, self.ml_w_in, 3104,
                        fm=[(0, 4, self.qkT_v, 0, 0.125), (512, 4, self.qkT_v, 4, 1.0)],
                        tm=[(512, 512, self.ktok), (1024, 1024, self.vtok), (2048, 1024, self.otok), (3072, 32, self.gtok)])
        self.mlstm_scan()
        self.mixer_post(i, self.ml_w_out, self.ml_ng, self.ml_ng[:], self.hdir, 'sigmoid')

    def mlstm_scan(self):
        self.areset()
        cst = self.cst
        Tri = [cst.ap[0:64, 2, 0:64], cst.ap[0:64, 3, 0:64]]
        Str = [cst.ap[0:64, 4, 0:64], cst.ap[0:64, 5, 0:64]]
        ones64 = cst.ap[0:64, 1, 0:64]
        Cn = [[self.take([64, 129]) for h in range(8)] for d in range(2)]
        qks = self.take([64, 16, 64], 4)
        kts = self.take([64, 512], 4)
        v1s = self.take([64, 8, 129], 4)
        for v1 in v1s.t:
            self.memset('pool', v1[:, :, 128:129], 1.0, pw=[v1])
        gts = self.take([64, 32], 4)
        gps = self.take([64, 64], 4)
        tls = self.take([64, 64], 6)
        Es = self.take([64, 64], 20)
        WTs = self.take([64, 64], 20)
        ias = self.take([64, 129], 6)
        tot8s = self.take([64, 8, 129], 4)
        tl8s = self.take([64, 8, 64], 2)
        E8s = self.take([64, 8, 64], 4)
        kw8s = self.take([64, 8, 64], 4)
        dns = self.take([64, 16], 4)
        kws = self.take([64, 64], 20)
        houts = self.take([64, 8, 128], 4)
        mst = [self.take([8, 1]) for d in range(2)]
        msm = self.take([8, 8], 2)
        emf = self.take([64, 8], 2)
        GBs = self.take([8, 2], 4)
        em0 = self.take([64, 16])
        cos = self.take([64, 129], 4)
        seqs = [(p * TP, TP // 64, p) for p in range(NP)] + [(NP * TP, TS // 64, -1)]
        for (tok0, nch, pidx) in seqs:
            if pidx >= 0:
                for d in range(2):
                    for h in range(8):
                        self.memset('pool', Cn[d][h][:], 0.0, w=[Cn[d][h]])
                    self.memset('pool', mst[d][:], 0.0, w=[mst[d]])
            else:
                self.ld(em0[:], self.st_m.partition_broadcast(64), r=[self.Win], w=[em0])
                self.act(em0[:], em0[:], AF.Exp, r=[em0], w=[em0])
                for d in range(2):
                    for h in range(8):
                        T_ = Cn[d][h]
                        self.ld(T_[:, 0:128], self.st_C[d, h], r=[self.Win], w=[T_])
                        self.ld(T_[:, 128:129], self.st_n[d, h], r=[self.Win], pw=[T_], eng='pool')
                        self.ts('pool', T_[:], T_[:], em0[:, d * 8 + h: d * 8 + h + 1], None, ALU.mult, r=[T_, em0], w=[T_])
            for step in range(nch):
                ctxs = []
                for d in range(2):
                    c = step if d == 0 else nch - 1 - step
                    t0 = tok0 + c * 64
                    ptk = [self.proj_tk[t0 // 128]]
                    qk = qks.get()
                    self.ld(qk[:], self.qkT_h[:, :, t0:t0 + 64], r=ptk, w=[qk])
                    kt = kts.get()
                    self.ld(kt[:], self.ktok[t0:t0 + 64, 0:512], r=ptk, w=[kt], eng="pool")
                    v1 = v1s.get()
                    self.ld(v1[:, :, 0:128], self.vtok[t0:t0 + 64, :].rearrange("t (h e) -> t h e", h=8), r=ptk, pw=[v1])
                    gt = gts.get()
                    self.ld(gt[:], self.gtok[t0:t0 + 64, :], r=ptk, w=[gt], eng='pool')
                    gp = gps.get()
                    dc = slice(d * 8, d * 8 + 8)
                    self.tt('dve', gp[:, 0:8], gt[:, dc], self.ml_gb[0:64, dc], ALU.add, r=[gt, self.ml_gb], w=[gp])
                    self.tt('dve', gp[:, 16:24], gt[:, 16 + d * 8:24 + d * 8], self.ml_gb[0:64, 16 + d * 8:24 + d * 8], ALU.add, r=[gt, self.ml_gb], pw=[gp])
                    self.act(gp[:, 16:24], gp[:, 16:24], AF.Exp, r=[gp], pw=[gp], scale=-1.0)
                    self.act(gp[:, 16:24], gp[:, 16:24], AF.Ln, r=[gp], pw=[gp], bias=1.0)
                    self.ts('dve', gp[:, 16:24], gp[:, 16:24], -1.0, None, ALU.mult, r=[gp], pw=[gp])
                    lf = gp[:, 16:24]
                    pg = self.pnext()
                    self.mm(pg[0:64, 0:8], Tri[d], lf, True, True, r=[cst, gp], w=[pg])
                    self.mm(pg[0:64, 8:16], Str[d], lf, True, True, r=[cst, gp], pw=[pg])
                    self.mm(pg[0:64, 16:24], ones64, lf, True, True, r=[cst, gp], pw=[pg])
                    self.act(gp[:, 32:40], pg[0:64, 0:8], AF.Exp, r=[pg], pw=[gp])
                    self.tt('dve', gp[:, 56:64], pg[0:64, 8:16], gp[:, 0:8], ALU.add, r=[pg, gp], pw=[gp])
                    self.act(gp[:, 40:48], gp[:, 56:64], AF.Exp, r=[gp], pw=[gp])
                    self.act(gp[:, 48:56], pg[0:64, 16:24], AF.Exp, r=[pg], pw=[gp])
                    if pidx >= 0:
                        pt = self.pnext()
                        self.tr_(pt[0:8, 0:64], gp[:, 56:64], r=[gp], w=[pt])
                        self.tr_(pt[0:8, 64:128], gp[:, 24:32] if False else pg[0:64, 16:24], r=[pg], pw=[pt]) if False else None
                        GB = GBs.get()
                        self.S.op('dve', lambda E, GB=GB, pt=pt: E.tensor_reduce(out=GB[:, 0:1], in_=pt[0:8, 0:64], axis=AX.X, op=ALU.max), reads=[pt], writes=[GB])
                        pb = self.pnext()
                        self.mm(pb[0:8, 0:1], lf, cst.ap[0:64, 1, 0:1], True, True, r=[gp, cst], w=[pb])
                        self.stt('dve', mst[d][:], mst[d][:], pb[0:8, 0:1], GB[:, 0:1], ALU.add, ALU.max, r=[mst[d], pb, GB], w=[mst[d]])
                    ctxs.append((d, t0, qk, kt, v1, gp, houts.get(), tot8s.get()))
                units = [(cx, h) for cx in ctxs for h in range(8)]
                stE = {}
                for cx in ctxs:
                    d, t0, qk, kt, v1, gp, ho, t8 = cx
                    tl8 = tl8s.get()
                    self.tt('dve', tl8[:], Tri[d].unsqueeze(1).to_broadcast([64, 8, 64]), gp[:, 16:24].unsqueeze(2).to_broadcast([64, 8, 64]), ALU.mult, r=[cst, gp], w=[tl8])
                    pD = self.pnext()
                    self.mm(pD[0:64, 0:512], Str[d], tl8[:].rearrange("p h t -> p (h t)"), True, True, r=[cst, tl8], w=[pD])
                    E8 = E8s.get()
                    self.act(E8[:].rearrange("p h t -> p (h t)"), pD[0:64, 0:512], AF.Exp, r=[pD], w=[E8])
                    self.act(gp[:, 8:16], gp[:, 0:8], AF.Exp, r=[gp], pw=[gp])
                    stE[d] = E8
                stK = {}
                for cx in ctxs:
                    d, t0, qk, kt, v1, gp, ho, t8 = cx
                    E8 = stE[d]
                    self.tt('pool', E8[:], E8[:], Tri[d].unsqueeze(1).to_broadcast([64, 8, 64]), ALU.mult, r=[E8, cst], w=[E8])
                    self.tt('dve', E8[:], E8[:], gp[:, 8:16].unsqueeze(2).to_broadcast([64, 8, 64]), ALU.mult, r=[E8, gp], w=[E8])
                    kw8 = kw8s.get()
                    self.tt('pool', kw8[:], kt[:].rearrange("t (h e) -> t h e", h=8), gp[:, 40:48].unsqueeze(2).to_broadcast([64, 8, 64]), ALU.mult, r=[kt, gp], w=[kw8])
                    stK[d] = kw8
                stW = {}
                for (cx, h) in units:
                    d, t0, qk, kt, v1, gp, ho, t8 = cx
                    pK = self.pnext()
                    self.mm(pK[0:64, 0:64], qk[:, 8 + h, :], qk[:, h, :], True, True, r=[qk], w=[pK])
                    WT = WTs.get()
                    self.tt('dve', WT[:], stE[d][:, h, :], pK[0:64, 0:64], ALU.mult, r=[stE[d], pK], w=[WT])
                    stW[(d, h)] = WT
                for (cx, h) in units:
                    d, t0, qk, kt, v1, gp, ho, t8 = cx
                    pI = self.pnext()
                    self.mm(pI[0:64, 0:129], stW[(d, h)][:], v1[:, h, :], True, True, r=[stW[(d, h)], v1], w=[pI])
                    pN = self.pnext()
                    C_ = Cn[d][h]
                    self.mm(pN[0:64, 0:129], qk[:, h, :], C_[:], True, True, r=[qk, C_], w=[pN])
                    ia = ias.get()
                    self.cp('act', ia[:], pI[0:64, 0:129], r=[pI], w=[ia])
                    self.stt('dve', t8[:, h, :], pN[0:64, 0:129], gp[:, 32 + h:33 + h], ia[:], ALU.mult, ALU.add, r=[pN, gp, ia], **wr(t8, h == 0))
                for cx in ctxs:
                    d, t0, qk, kt, v1, gp, ho, t8 = cx
                    dn = dns.get()
                    den = t8[:, :, 128]
                    self.ts('dve', dn[:, 0:8], den, -1.0, None, ALU.mult, r=[t8], w=[dn])
                    self.tt('dve', dn[:, 0:8], dn[:, 0:8], den, ALU.max, r=[dn, t8], w=[dn])
                    self.ts('dve', dn[:, 0:8], dn[:, 0:8], 1.0, None, ALU.max, r=[dn], w=[dn])
                    self.recip(dn[:, 8:16], dn[:, 0:8], r=[dn], pw=[dn])
                    self.tt('pool', ho[:], t8[:, :, 0:128], dn[:, 8:16].unsqueeze(2).to_broadcast([64, 8, 128]), ALU.mult, r=[t8, dn], w=[ho])
                    self.ld(self.hdir[d][t0:t0 + 64, :].rearrange("t (h e) -> t h e", h=8), ho[:], r=[ho], w=[self.hdir_tk[d][t0 // 64]], eng='pool')
                for (cx, h) in units:
                    d, t0, qk, kt, v1, gp, ho, t8 = cx
                    C_ = Cn[d][h]
                    pU = self.pnext()
                    self.mm(pU[0:64, 0:129], stK[d][:, h, :], v1[:, h, :], True, True, r=[stK[d], v1], w=[pU])
                    self.stt('dve', C_[:], C_[:], gp[:, 48 + h:49 + h], pU[0:64, 0:129], ALU.mult, ALU.add, r=[C_, gp, pU], w=[C_])
            if pidx >= 0:
                for d in range(2):
                    dm = msm.get()
                    self.ts('dve', dm[:], cst.ap[0:8, 0, 0:8], mst[d][:, 0:1], None, ALU.mult, r=[cst, mst[d]], w=[dm])
                    pm = self.pnext()
                    self.mm(pm[0:64, 0:8], cst.ap[0:8, 1, 0:64], dm[:], True, True, r=[cst, dm], w=[pm])
                    ef = emf.get()
                    self.act(ef[:], pm[0:64, 0:8], AF.Exp, r=[pm], w=[ef], scale=-1.0)
                    self.ld(self.newm[pidx, d], mst[d][:], r=[mst[d]], w=[self.Oout], eng='pool')
                    for h in range(8):
                        co = cos.get()
                        self.ts('dve' if h % 2 else 'pool', co[:], Cn[d][h][:], ef[:, h:h + 1], None, ALU.mult, r=[Cn[d][h], ef], w=[co])
                        self.ld(self.newC[pidx, d, h], co[:, 0:128], r=[co], pw=[self.Oout], eng='sp')
                        self.ld(self.newn[pidx, d, h], co[:, 128:129], r=[co], pw=[self.Oout], eng='pool')

    def gdn(self, i):
        j = i // 3
        w_in = self.gd_w_in[j]
        if self.bf:
            self.proj_stage16(i, w_in, 4128, fm=[(0, 24, self.qkT_v, 0, 1.0)],
                              tm=[(3072, 1024, self.otok), (4096, 32, self.gtok)])
        else:
            self.proj_stage(i, w_in, 2048, fm=[(0, 16, self.qkT_v, 0, 1.0)], tm=[], col_lo=0)
            self.proj_stage(i, w_in, 2080, fm=[(2048, 8, self.qkT_v, 16, 1.0)],
                            tm=[(3072, 1024, self.otok), (4096, 32, self.gtok)], col_lo=2048)
        stop = self.opts.get('gdn_stop', 9)
        if stop >= 2:
            self.gdn_conv(j)
        if stop >= 3:
            self.gdn_scan(j)
        if stop >= 4:
            self.mixer_post(i, self.gd_w_out[j], self.gd_ng, self.gd_ng[:, j, :], self.hdir, 'silu')

    def gdn_conv(self, j):
        self.areset()
        NBUF = 3
        xins = self.take([128, TS + 16], NBUF)
        tmps = self.take([128, TS], 2)
        sqs = self.take([128, 512], 3)
        rss = self.take([128, 512], 3)
        tos = self.take([128, 4, 128], 4)
        accs = []
        for _ in range(NBUF):
            base = self.take([128, TS])
            accs.append((base.ap, [Tk(base.ap[:, b * 512:(b + 1) * 512]) for b in range(4)]))
        cw = self.gd_cw
        n = 0
        items = [(0, NP, TP), (NP * TP, 1, TS)]
        for (tok0, ns, T) in items:
            W = T + 4
            for ch in range(24):
                on_dve = (n % 2 == 0)
                xin = xins.get()
                acc_ap, accb = accs[n % NBUF]
                n += 1
                xv = xin[:, 0:ns * W].rearrange("p (s w) -> p s w", s=ns)
                av = acc_ap[:, 0:ns * T].rearrange("p (s t) -> p s t", s=ns)
                self.memset('pool', xv[:, :, 0:2], 0.0, w=[xin])
                self.memset('pool', xv[:, :, T + 2:T + 4], 0.0, pw=[xin])
                self.ld(xv[:, :, 2:T + 2], self.qkT_v[:, ch, tok0:tok0 + ns * T].rearrange("p (s t) -> p s t", s=ns), pw=[xin])
                if on_dve:
                    self.ts('dve', av, xv[:, :, 0:T], cw[:, j, ch, 0:1], None, ALU.mult, r=[xin, cw], w=accb)
                    for k in range(1, 5):
                        self.stt('dve', av, xv[:, :, k:k + T], cw[:, j, ch, k:k + 1], av, ALU.mult, ALU.add, r=[xin, cw] + accb, w=accb)
                else:
                    self.act(av, xv[:, :, 0:T], AF.Copy, r=[xin, cw], w=accb, scale=cw[:, j, ch, 0:1])
                    for k in range(1, 5):
                        tm_ = tmps.get()
                        tv = tm_[:, 0:ns * T].rearrange("p (s t) -> p s t", s=ns)
                        self.act(tv, xv[:, :, k:k + T], AF.Copy, r=[xin, cw], w=[tm_], scale=cw[:, j, ch, k:k + 1])
                        self.tt('pool', av, av, tv, ALU.add, r=accb + [tm_], w=accb)
                self.act(acc_ap[:, 0:ns * T], acc_ap[:, 0:ns * T], AF.Silu, r=accb, w=accb)
                NTOK = ns * T
                nb = NTOK // 512
                if ch < 16:
                    scale = (128.0 ** -0.5) if ch < 8 else 1.0
                    for b in range(nb):
                        bs = slice(b * 512, (b + 1) * 512)
                        sq = sqs.get()
                        self.tt('pool', sq[:], acc_ap[:, bs], acc_ap[:, bs], ALU.mult, r=[accb[b]], w=[sq])
                        p = self.pnext()
                        self.mm(p[:, :], self.cst.ap[:, 1, :], sq[:], True, True, r=[self.cst, sq], w=[p])
                        rs = rss.get()
                        self.act(rs[:], p[:, :], AF.Sqrt, r=[p, self.epsb], w=[rs], bias=self.epsb[:, 0:1])
                        self.recip(rs[:], rs[:], r=[rs], w=[rs])
                        self.stt('dve', acc_ap[:, bs], acc_ap[:, bs], scale, rs[:], ALU.mult, ALU.mult, r=[accb[b], rs], w=[accb[b]])
                    self.ld(self.qkT_v[:, ch, tok0:tok0 + NTOK], acc_ap[:, 0:NTOK], r=accb[0:nb], pw=[self.Oout], eng='pool')
                if ch >= 8:
                    dst = self.ktok if ch < 16 else self.vtok
                    c0 = (ch - 8) * 128 if ch < 16 else (ch - 16) * 128
                    for b in range(nb):
                        p = self.pnext()
                        for k in range(4):
                            self.tr_(p[:, k * 128:(k + 1) * 128], acc_ap[:, b * 512 + k * 128:b * 512 + (k + 1) * 128], r=[accb[b]], **wr(p, k == 0))
                        to = tos.get()
                        self.cp('act' if b % 2 else 'dve', to[:], p[:, :].rearrange("p (a b) -> p a b", a=4), r=[p], w=[to])
                        self.ld(dst[tok0 + b * 512:tok0 + (b + 1) * 512, c0:c0 + 128].rearrange("(n p) e -> p n e", p=128), to[:], r=[to], pw=[self.Oout], eng='sp')

    def gdn_scan(self, j):
        self.areset()
        cst = self.cst
        Tri = [cst.ap[0:64, 2, 0:64], cst.ap[0:64, 3, 0:64]]
        Str = [cst.ap[0:64, 4, 0:64], cst.ap[0:64, 5, 0:64]]
        Sm = [cst.ap[0:64, 5, 0:64], cst.ap[0:64, 4, 0:64]]
        I64 = cst.ap[0:64, 0, 0:64]
        ones64w = cst.ap[0:64, 1, 0:128]

        def bh(m):
            return m.unsqueeze(1).to_broadcast([64, 8, 64])

        def bt(v, n, np_=64):
            return v.unsqueeze(2).to_broadcast([np_, v.shape[1], n])

        S8 = [self.take([128, 8, 128]) for d in range(2)]
        qks = self.take([128, 8, 2, 64], 3)
        kts = self.take([64, 8, 128], 3)
        vts = self.take([64, 8, 128], 3)
        gts = self.take([64, 32], 3)
        gps = self.take([64, 48], 3)
        gls = self.take([128, 8], 3)
        tl8s = self.take([64, 8, 64], 2)
        Er8s = self.take([64, 8, 64], 2)
        Ei8s = self.take([64, 8, 64], 2)
        Es8s = self.take([64, 8, 64], 2)
        qkT8s = self.take([64, 8, 64], 3)
        P8s = self.take([64, 8, 64], 3)
        X8s = self.take([64, 8, 64], 5)
        XT8s = self.take([64, 8, 64], 5)
        U8s = self.take([64, 8, 128], 3)
        keg8s = self.take([64, 8, 128], 2)
        kdec8s = self.take([64, 8, 128], 3)
        vn8s = self.take([64, 8, 128], 3)
        o8s = self.take([64, 8, 128], 3)
        WT8s = self.take([128, 8, 64], 3)
        seqs = [(p * TP, TP // 64, p) for p in range(NP)] + [(NP * TP, TS // 64, -1)]
        seqs = seqs[self.opts.get('gdn_seq0', 0):self.opts.get('gdn_seq1', 5)]
        for (tok0, nch, pidx) in seqs:
            for d in range(2):
                if pidx >= 0:
                    self.memset('pool', S8[d][:], 0.0, w=[S8[d]])
                else:
                    self.ld(S8[d][:], self.st_S[j, d].rearrange("h k e -> k h e"), r=[self.Win], w=[S8[d]], eng='sp' if d else 'pool')
            for step in range(nch):
                ctx = []
                for d in range(2):
                    c = step if d == 0 else nch - 1 - step
                    t0 = tok0 + c * 64
                    qk = qks.get()
                    self.ld(qk[:, :, 0, :], self.qkT_v[:, 8:16, t0:t0 + 64], w=[qk])
                    self.ld(qk[:, :, 1, :], self.qkT_v[:, 0:8, t0:t0 + 64], pw=[qk], eng='pool')
                    kt = kts.get()
                    self.ld(kt[:], self.ktok[t0:t0 + 64, 0:1024].rearrange("t (h e) -> t h e", h=8), w=[kt], eng='pool')
                    vt = vts.get()
                    self.ld(vt[:], self.vtok[t0:t0 + 64, :].rearrange("t (h e) -> t h e", h=8), w=[vt])
                    gt = gts.get()
                    self.ld(gt[:], self.gtok[t0:t0 + 64, :], w=[gt], eng='pool')
                    gp = gps.get()
                    dc = slice(d * 8, d * 8 + 8)
                    self.tt('dve', gp[:, 0:8], gt[:, dc], self.gd_dt[0:64, j, dc], ALU.add, r=[gt, self.gd_dt], w=[gp])
                    self.act(gp[:, 0:8], gp[:, 0:8], AF.Exp, r=[gp], pw=[gp])
                    self.act(gp[:, 0:8], gp[:, 0:8], AF.Ln, r=[gp], pw=[gp], bias=1.0)
                    self.tt('dve', gp[:, 8:16], gp[:, 0:8], self.gd_nea[0:64, j, dc], ALU.mult, r=[gp, self.gd_nea], pw=[gp])
                    self.act(gp[:, 16:24], gt[:, 16 + d * 8:24 + d * 8], AF.Sigmoid, r=[gt], pw=[gp])
                    self.ts('dve', gp[:, 24:32], gp[:, 16:24], -1.0, None, ALU.mult, r=[gp], pw=[gp])
                    la = gp[:, 8:16]
                    pg = self.pnext()
                    self.mm(pg[0:64, 0:8], Tri[d], la, True, True, r=[cst, gp], w=[pg])
                    self.mm(pg[0:64, 8:16], Str[d], la, True, True, r=[cst, gp], pw=[pg])
                    self.mm(pg[0:128, 16:24], ones64w, la, True, True, r=[cst, gp], pw=[pg])
                    self.act(gp[:, 32:48], pg[0:64, 0:16], AF.Exp, r=[pg], pw=[gp])
                    gl = gls.get()
                    self.act(gl[:], pg[0:128, 16:24], AF.Exp, r=[pg], w=[gl])
                    ctx.append(dict(d=d, t0=t0, qk=qk, kt=kt, vt=vt, gp=gp, gl=gl))
                for cx in ctx:
                    d, gp = cx['d'], cx['gp']
                    tl8 = tl8s.get()
                    self.tt('dve', tl8[:], bh(Tri[d]), bt(gp[:, 8:16], 64), ALU.mult, r=[cst, gp], w=[tl8])
                    pD = self.pnext()
                    self.mm(pD[0:64, 0:512], Str[d], tl8[:].rearrange("p h t -> p (h t)"), True, True, r=[cst, tl8], w=[pD])
                    Er = Er8s.get()
                    self.act(Er[:].rearrange("p h t -> p (h t)"), pD[0:64, 0:512], AF.Exp, r=[pD], w=[Er])
                    cx['Er'] = Er
                for cx in ctx:
                    d, gp, Er = cx['d'], cx['gp'], cx['Er']
                    Ei = Ei8s.get()
                    Es = Es8s.get()
                    self.tt('pool', Ei[:], Er[:], bh(Tri[d]), ALU.mult, r=[Er, cst], w=[Ei])
                    self.tt('pool', Es[:], Er[:], bh(Sm[d]), ALU.mult, r=[Er, cst], w=[Es])
                    self.tt('dve', Es[:], Es[:], bt(gp[:, 24:32], 64), ALU.mult, r=[Es, gp], w=[Es])
                    cx['Ei'], cx['Es'] = Ei, Es
                for cx in ctx:
                    qk = cx['qk']
                    X = X8s.get()
                    qkT = qkT8s.get()
                    for g in range(2):
                        pG = self.pnext()
                        for hh in range(4):
                            h = 4 * g + hh
                            self.mm(pG[0:64, hh * 128:(hh + 1) * 128], qk[:, h, 0, :], qk[:, h, :, :].rearrange("p a t -> p (a t)"), True, True,
                                    r=[qk], **wr(pG, hh == 0))
                        pv = pG[0:64, 0:512].rearrange("p (h a t) -> p h a t", h=4, a=2)
                        self.tt('dve', X[:, 4 * g:4 * g + 4, :], pv[:, :, 0, :], cx['Es'][:, 4 * g:4 * g + 4, :], ALU.mult, r=[pG, cx['Es']], **wr(X, g == 0))
                        self.tt('dve', qkT[:, 4 * g:4 * g + 4, :], pv[:, :, 1, :], cx['Ei'][:, 4 * g:4 * g + 4, :], ALU.mult, r=[pG, cx['Ei']], **wr(qkT, g == 0))
                    cx['X'], cx['qkT'] = X, qkT
                for cx in ctx:
                    X = cx['X']
                    pT = self.pnext()
                    for h in range(8):
                        self.tr_(pT[0:64, h * 64:(h + 1) * 64], X[:, h, :], r=[X], **wr(pT, h == 0))
                    XT = XT8s.get()
                    self.cp('act', XT[:].rearrange("p h t -> p (h t)"), pT[0:64, 0:512], r=[pT], w=[XT])
                    P_ = P8s.get()
                    self.tt('pool', P_[:], X[:], bh(I64), ALU.add, r=[X, cst], w=[P_])
                    cx['XT'], cx['P'] = XT, P_
                for jn in range(1, 6):
                    for cx in ctx:
                        X, XT = cx['X'], cx['XT']
                        Xn = None
                        if jn < 5:
                            pX = self.pnext()
                            for h in range(8):
                                self.mm(pX[0:64, h * 64:(h + 1) * 64], XT[:, h, :], X[:, h, :], True, True, r=[XT, X], **wr(pX, h == 0))
                            Xn = X8s.get()
                            self.cp('dve', Xn[:].rearrange("p h t -> p (h t)"), pX[0:64, 0:512], r=[pX], w=[Xn])
                        pXT = self.pnext()
                        for h in range(8):
                            self.mm(pXT[0:64, h * 64:(h + 1) * 64], X[:, h, :], XT[:, h, :], True, True, r=[XT, X], **wr(pXT, h == 0))
                        XnT = XT8s.get()
                        self.cp('act', XnT[:].rearrange("p h t -> p (h t)"), pXT[0:64, 0:512], r=[pXT], w=[XnT])
                        cx['X'], cx['XT'] = Xn, XnT
                    for cx in ctx:
                        XT, P_ = cx['XT'], cx['P']
                        pP = self.pnext()
                        for h in range(8):
                            self.mm(pP[0:64, h * 64:(h + 1) * 64], XT[:, h, :], P_[:, h, :], True, True, r=[XT, P_], **wr(pP, h == 0))
                        self.tt('dve', P_[:].rearrange("p h t -> p (h t)"), P_[:].rearrange("p h t -> p (h t)"), pP[0:64, 0:512], ALU.add, r=[P_, pP], w=[P_])
                for cx in ctx:
                    gp, kt, vt, P_ = cx['gp'], cx['kt'], cx['vt'], cx['P']
                    keg = keg8s.get()
                    self.tt('pool', keg[:], kt[:], bt(gp[:, 32:40], 128), ALU.mult, r=[kt, gp], w=[keg])
                    kdec = kdec8s.get()
                    self.tt('pool', kdec[:], kt[:], bt(gp[:, 40:48], 128), ALU.mult, r=[kt, gp], w=[kdec])
                    U = U8s.get()
                    for g in range(2):
                        pU = self.pnext()
                        for hh in range(4):
                            h = 4 * g + hh
                            self.mm(pU[0:64, hh * 128:(hh + 1) * 128], P_[:, h, :], vt[:, h, :], True, True, r=[P_, vt], **wr(pU, hh == 0))
                        self.tt('dve', U[:, 4 * g:4 * g + 4, :], pU[0:64, 0:512].rearrange("p (h e) -> p h e", h=4), bt(gp[:, 16 + 4 * g:20 + 4 * g], 128), ALU.mult,
                                r=[pU, gp], **wr(U, g == 0))
                    pW = self.pnext()
                    for h in range(8):
                        self.mm(pW[0:128, h * 64:(h + 1) * 64], keg[:, h, :], P_[:, h, :], True, True, r=[keg, P_], **wr(pW, h == 0))
                    WT = WT8s.get()
                    self.cp('act', WT[:].rearrange("p h t -> p (h t)"), pW[0:128, 0:512], r=[pW], w=[WT])
                    cx['U'], cx['WT'], cx['kdec'] = U, WT, kdec
                for cx in ctx:
                    d, gp = cx['d'], cx['gp']
                    vn = vn8s.get()
                    for g in range(2):
                        pa = self.pnext()
                        for hh in range(4):
                            h = 4 * g + hh
                            self.mm(pa[0:64, hh * 128:(hh + 1) * 128], cx['WT'][:, h, :], S8[d][:, h, :], True, True, r=[cx['WT'], S8[d]], **wr(pa, hh == 0))
                        self.tt('dve', vn[:, 4 * g:4 * g + 4, :], pa[0:64, 0:512].rearrange("p (h e) -> p h e", h=4), bt(gp[:, 24 + 4 * g:28 + 4 * g], 128), ALU.mult,
                                r=[pa, gp], **wr(vn, g == 0))
                    self.tt('pool', vn[:], vn[:], cx['U'][:], ALU.add, r=[vn, cx['U']], w=[vn])
                    cx['vn'] = vn
                for cx in ctx:
                    d, gp, gl, qk, vn = cx['d'], cx['gp'], cx['gl'], cx['qk'], cx['vn']
                    o8 = o8s.get()
                    for g in range(2):
                        po = self.pnext()
                        for hh in range(4):
                            h = 4 * g + hh
                            self.mm(po[0:64, hh * 128:(hh + 1) * 128], qk[:, h, 1, :], S8[d][:, h, :], True, True, r=[qk, S8[d]], **wr(po, hh == 0))
                        self.tt('dve', o8[:, 4 * g:4 * g + 4, :], po[0:64, 0:512].rearrange("p (h e) -> p h e", h=4), bt(gp[:, 32 + 4 * g:36 + 4 * g], 128), ALU.mult,
                                r=[po, gp], **wr(o8, g == 0))
                    for g in range(2):
                        po2 = self.pnext()
                        for hh in range(4):
                            h = 4 * g + hh
                            self.mm(po2[0:64, hh * 128:(hh + 1) * 128], cx['qkT'][:, h, :], vn[:, h, :], True, True, r=[cx['qkT'], vn], **wr(po2, hh == 0))
                        self.tt('dve', o8[:, 4 * g:4 * g + 4, :], o8[:, 4 * g:4 * g + 4, :], po2[0:64, 0:512].rearrange("p (h e) -> p h e", h=4), ALU.add,
                                r=[po2, o8], pw=[o8])
                    self.ld(self.hdir[d][cx['t0']:cx['t0'] + 64, :].rearrange("t (h e) -> t h e", h=8), o8[:], r=[o8], pw=[self.Oout], eng='pool')
                    for g in range(2):
                        pS = self.pnext()
                        for hh in range(4):
                            h = 4 * g + hh
                            self.mm(pS[0:128, hh * 128:(hh + 1) * 128], cx['kdec'][:, h, :], vn[:, h, :], True, True, r=[cx['kdec'], vn], **wr(pS, hh == 0))
                        Sg = S8[d][:, 4 * g:4 * g + 4, :]
                        self.tt('pool', Sg, Sg, bt(gl[:, 4 * g:4 * g + 4], 128, 128), ALU.mult, r=[S8[d], gl], w=[S8[d]])
                        self.tt('dve', Sg, Sg, pS[0:128, 0:512].rearrange("p (h e) -> p h e", h=4), ALU.add, r=[S8[d], pS], w=[S8[d]])
            if pidx >= 0:
                for d in range(2):
                    self.ld(self.newS[pidx, j, d].rearrange("h k e -> k h e"), S8[d][:], r=[S8[d]], pw=[self.Oout], eng='sp' if d else 'pool')

    def diffattn(self, i):
        (self.proj_stage16 if self.bf else self.proj_stage)(i, self.df_w_in, 3072, fm=[],
                        tm=[(0, 2048, self.ktok), (2048, 1024, self.vtok)])
        self.attn_prep()
        self.attn_core()
        self.mixer_post(i, self.df_w_out, self.df_sg, self.df_sg[:], self.hdir, None)

    def attn_prep(self):
        self.areset()
        xs = self.take([128, 32, 64], 2)
        sqs = self.take([128, 32, 64], 1)
        sss = self.take([128, 64], 2)
        css = self.take([128, 64], 2)
        r1 = self.take([128, 32, 2, 16], 1)
        r2 = self.take([128, 32, 2, 16], 1)
        r3 = self.take([128, 32, 2, 16], 1)
        xr = self.take([128, 32, 64], 2)
        vts = self.take([128, 1024], 2)
        xos = self.take([128, 16, 128], 2)
        cks = self.take([128, 16, 64], 2)
        cko = self.take([128, 8, 128], 2)
        gq = self.df_g
        for t in range(TT // 128):
            t0 = t * 128
            x = xs.get()
            self.ld(x[:], self.ktok[t0:t0 + 128, :].rearrange("t (g d) -> t g d", g=32), r=[self.proj_tk[t]], w=[x])
            sq = sqs.get()
            self.tt('pool', sq[:], x[:], x[:], ALU.mult, r=[x], w=[sq])
            ss = sss.get()
            self.S.op('dve', lambda E, ss=ss, sq=sq: E.tensor_reduce(out=ss[:, 0:32], in_=sq[:], axis=AX.X, op=ALU.add), reads=[sq], writes=[ss])
            self.act(ss[:, 0:32], ss[:, 0:32], AF.Sqrt, r=[ss, self.epsb], w=[ss], scale=1.0 / 64, bias=self.epsb[:, 0:1])
            self.recip(ss[:, 32:64], ss[:, 0:32], r=[ss], pw=[ss])
            self.tt('dve', x[:], x[:], ss[:, 32:64].unsqueeze(2).to_broadcast([128, 32, 64]), ALU.mult, r=[x, ss], w=[x])
            self.tt('pool', x[:, 0:16, :], x[:, 0:16, :], gq[:, 0, :].unsqueeze(1).to_broadcast([128, 16, 64]), ALU.mult, r=[x, gq], w=[x])
            self.tt('dve', x[:, 16:32, :], x[:, 16:32, :], gq[:, 1, :].unsqueeze(1).to_broadcast([128, 16, 64]), ALU.mult, r=[x, gq], w=[x])
            if t0 < NP * TP:
                p, tl = t0 // TP, t0 % TP
                self.ld(self.newk[p, :, :, tl:tl + 128, :].rearrange("h m t d -> t (h m) d"), x[:, 16:32, :], r=[x], pw=[self.Oout], eng='pool')
                vt = vts.get()
                self.ld(vt[:], self.vtok[t0:t0 + 128, :], r=[self.proj_tk[t]], w=[vt])
                self.ld(self.newv[p, :, tl:tl + 128, :].rearrange("h t e -> t h e"), vt[:].rearrange("t (h e) -> t h e", h=8), r=[vt], pw=[self.Oout], eng='pool')
                src = x
            else:
                cs = css.get()
                self.ld(cs[:], self.rope_cs[t0 - NP * TP:t0 - NP * TP + 128, :], r=[self.Win], w=[cs])
                X = x[:].rearrange("t g (a f r) -> t g a f r", a=2, f=2)
                xa = X[:, :, :, 0, :]
                xb_ = X[:, :, :, 1, :]
                cosb = cs[:, 0:32].rearrange("t (a r) -> t a r", a=2).unsqueeze(1).to_broadcast([128, 32, 2, 16])
                sinb = cs[:, 32:64].rearrange("t (a r) -> t a r", a=2).unsqueeze(1).to_broadcast([128, 32, 2, 16])
                o_ = xr.get()
                O = o_[:].rearrange("t g (a f r) -> t g a f r", a=2, f=2)
                a1, a2, a3 = r1.get(), r2.get(), r3.get()
                self.tt('dve', a1[:], xa, cosb, ALU.mult, r=[x, cs], w=[a1])
                self.tt('pool', a2[:], xb_, sinb, ALU.mult, r=[x, cs], w=[a2])
                self.tt('dve', O[:, :, :, 0, :], a1[:], a2[:], ALU.subtract, r=[a1, a2], w=[o_])
                self.tt('pool', a3[:], xa, sinb, ALU.mult, r=[x, cs], w=[a3])
                self.tt('dve', a1[:], xb_, cosb, ALU.mult, r=[x, cs], w=[a1])
                self.tt('pool', O[:, :, :, 1, :], a3[:], a1[:], ALU.add, r=[a3, a1], pw=[o_])
                src = o_
            xo = xos.get()
            for g in range(4):
                pp = self.pnext()
                for k in range(4):
                    ch = g * 4 + k
                    self.tr_(pp[:, k * 128:(k + 1) * 128], src[:, 2 * ch:2 * ch + 2, :].rearrange("t a d -> t (a d)"), r=[src], **wr(pp, k == 0))
                self.cp('act' if g % 2 else 'dve', xo[:, g * 4:(g + 1) * 4, :], pp[:, :].rearrange("p (a b) -> p a b", a=4), r=[pp], **wr(xo, g == 0))
            self.ld(self.qkT_v[:, 0:16, t0:t0 + 128], xo[:], r=[xo], w=[self.prep_tk[t]], eng='pool')
        for kt in range(2):
            ck = cks.get()
            self.ld(ck[:], self.ctx_k[:, :, kt * 128:(kt + 1) * 128, :].rearrange("h m t d -> t (h m) d"), r=[self.Win], w=[ck])
            co = cko.get()
            for g in range(2):
                pp = self.pnext()
                for k in range(4):
                    ch = g * 4 + k
                    self.tr_(pp[:, k * 128:(k + 1) * 128], ck[:, 2 * ch:2 * ch + 2, :].rearrange("t a d -> t (a d)"), r=[ck], **wr(pp, k == 0))
                self.cp('act' if g % 2 else 'dve', co[:, g * 4:(g + 1) * 4, :], pp[:, :].rearrange("p (a b) -> p a b", a=4), r=[pp], **wr(co, g == 0))
            self.ld(self.ctxkT.rearrange("c p t -> p c t")[:, :, kt * 128:(kt + 1) * 128], co[:], r=[co], **wr(self.ctx_tk, kt == 0), eng='pool')

    def attn_core(self):
        self.areset()
        NKT = (TS + 256) // 128
        bf = self.bf
        MD = BF16 if bf else F32
        qTs = self.take([128, TS], 2, MD)
        kTs = self.take([128, TS + 256], 2, MD)
        V1s = self.take([128, NKT, 129], 2, MD)
        for V1 in V1s.t:
            self.memset('pool', V1[:, :, 128:129], 1.0, pw=[V1])
        PTs = self.take([128, NKT, 512], 2, MD)
        if bf:
            q32 = self.take([128, TS], 2)
            k32 = self.take([128, TS + 256], 2)
            v32 = self.take([128, NKT, 128], 2)
        obs = self.take([128, 4, 128], 2)
        rvs = self.take([128, 2], 4)
        seqs = [(p * TP, TP, False) for p in range(NP)] + [(NP * TP, TS, True)]
        for (tok0, T, is_s) in seqs:
            nk = T + (256 if is_s else 0)
            nkt = nk // 128
            QB = min(512, T)
            tks = self.prep_tk[tok0 // 128:(tok0 + T) // 128]
            ptk = self.proj_tk[tok0 // 128:(tok0 + T) // 128]
            for h in range(8):
                qT = qTs.get()
                kT = kTs.get()
                V1 = V1s.get()
                if bf:
                    qd, kd, vd = q32.get(), k32.get(), v32.get()
                else:
                    qd, kd, vd = qT, kT, V1
                self.ld(qd[:, 0:T], self.qkT_v[:, h, tok0:tok0 + T], r=tks, w=[qd])
                self.ld(kd[:, 0:T], self.qkT_v[:, 8 + h, tok0:tok0 + T], r=tks, w=[kd], eng='pool')
                self.ld(vd[:, 0:T // 128, 0:128], self.vtok[tok0:tok0 + T, h * 128:(h + 1) * 128].rearrange("(n p) e -> p n e", p=128), r=ptk, **wr(vd, bf))
                if is_s:
                    self.ld(kd[:, T:T + 256], self.ctxkT[h], r=[self.ctx_tk], pw=[kd], eng='pool')
                    self.ld(vd[:, T // 128:nkt, 0:128], self.ctx_v[h].rearrange("(n p) e -> p n e", p=128), r=[self.Win], pw=[vd])
                if bf:
                    self.cp('dve', qT[:, 0:T], qd[:, 0:T], r=[qd], w=[qT])
                    self.cp('act', kT[:, 0:nk], kd[:, 0:nk], r=[kd], w=[kT])
                    self.cp('dve', V1[:, 0:nkt, 0:128], vd[:, 0:nkt, :], r=[vd], pw=[V1])
                for qb in range(T // QB):
                    ob = obs.get()
                    for m in range(2):
                        PT = PTs.get()
                        for kt in range(nkt):
                            pS = self.pnext()
                            self.mm(pS[:, 0:QB], kT[m * 64:(m + 1) * 64, kt * 128:(kt + 1) * 128], qT[m * 64:(m + 1) * 64, qb * QB:(qb + 1) * QB],
                                    True, True, r=[kT, qT], w=[pS])
                            self.act(PT[:, kt, 0:QB], pS[:, 0:QB], AF.Exp, r=[pS], **wr(PT, kt == 0), scale=0.125)
                        for qs in range(QB // 128):
                            pO = self.pnext()
                            for kt in range(nkt):
                                self.mm(pO[:, 0:129], PT[:, kt, qs * 128:(qs + 1) * 128], V1[:, kt, :], kt == 0, kt == nkt - 1, r=[PT, V1], **wr(pO, kt == 0))
                            rv = rvs.get()
                            self.recip(rv[:, 0:1], pO[:, 128:129], r=[pO], w=[rv])
                            if m == 0:
                                self.ts('dve', ob[:, qs, :], pO[:, 0:128], rv[:, 0:1], None, ALU.mult, r=[pO, rv], **wr(ob, qs == 0))
                            else:
                                self.tt('dve', rv[:, 1:2], rv[:, 0:1], self.nlam[:, 3:4], ALU.mult, r=[rv, self.nlam], pw=[rv])
                                self.stt('dve', ob[:, qs, :], pO[:, 0:128], rv[:, 1:2], ob[:, qs, :], ALU.mult, ALU.add, r=[pO, rv, ob], pw=[ob])
                    q0 = tok0 + qb * QB
                    nq = QB // 128
                    htk = self.hdir_tk[0][q0 // 64:(q0 + QB) // 64]
                    self.ld(self.hdir[0][q0:q0 + QB, h * 128:(h + 1) * 128].rearrange("(n p) e -> p n e", p=128), ob[:, 0:nq, :], r=[ob], pw=htk, eng='pool')

    def mixer_post(self, i, w_out, ng_tk, ng_bc, hdir, gate):
        self.areset()
        NT = 512
        if self.bf:
            W = self.take([128, 8, D], None, BF16)
            self.wstage = self.take([128, 8, 512], 2)
            self.load_w16(W, w_out.rearrange("(kc p) n -> p kc n", p=128), D)
        else:
            W = self.take([128, 8, D])
            self.ld(W[:], w_out.rearrange("(kc p) n -> p kc n", p=128), r=[self.Win], w=[W])
        hfs = self.take([128, 8, 128], 2)
        hbs = self.take([128, 8, 128], 2)
        ogs = self.take([128, 8, 128], 2)
        sqs = self.take([128, 8, 128], 2)
        sss = self.take([128, 16], 2)
        yTs = self.take([128, 8, NT], 2, BF16 if self.bf else F32)
        xbs = self.take([128, 8, NT], 2)
        for blk in range(TT // NT):
            c = 0 if blk < (NP * TP) // NT else 1
            tks = self.xT_tk[blk * 4:(blk + 1) * 4]
            xb = xbs.get()
            self.ld(xb[:], self.xT_v[:, :, blk * NT:(blk + 1) * NT], r=tks, w=[xb], eng='pool')
            yT = yTs.get()
            for q in range(4):
                t0 = blk * NT + q * 128
                hf = hfs.get()
                hb = hbs.get()
                og = ogs.get()
                self.ld(hf[:], hdir[0][t0:t0 + 128, :].rearrange("t (h e) -> t h e", h=8), r=self.hdir_tk[0][t0 // 64:t0 // 64 + 2], w=[hf])
                if gate is not None:
                    self.ld(hb[:], hdir[1][t0:t0 + 128, :].rearrange("t (h e) -> t h e", h=8), r=self.hdir_tk[1][t0 // 64:t0 // 64 + 2], w=[hb], eng='pool')
                    self.ld(og[:], self.otok[t0:t0 + 128, :].rearrange("t (h e) -> t h e", h=8), r=[self.proj_tk[t0 // 128]], w=[og])
                    self.tt('dve', hf[:], hf[:], hb[:], ALU.add, r=[hf, hb], w=[hf])
                sq = sqs.get()
                self.tt('pool', sq[:], hf[:], hf[:], ALU.mult, r=[hf], w=[sq])
                ss = sss.get()
                self.S.op('dve', lambda E, ss=ss, sq=sq: E.tensor_reduce(out=ss[:, 0:8], in_=sq[:], axis=AX.X, op=ALU.add), reads=[sq], writes=[ss])
                self.act(ss[:, 0:8], ss[:, 0:8], AF.Sqrt, r=[ss, self.epsb], w=[ss], scale=1.0 / 128, bias=self.epsb[:, 0:1])
                self.recip(ss[:, 8:16], ss[:, 0:8], r=[ss], pw=[ss])
                ngb = ng_bc.unsqueeze(1).to_broadcast([128, 8, 128])
                if gate == 'sigmoid':
                    self.act(og[:], og[:], AF.Sigmoid, r=[og], w=[og])
                    self.tt('pool', og[:], og[:], ngb, ALU.mult, r=[og, ng_tk], w=[og])
                elif gate == 'silu':
                    self.act(og[:], og[:], AF.Silu, r=[og], w=[og])
                    self.tt('pool', og[:], og[:], ngb, ALU.mult, r=[og, ng_tk], w=[og])
                else:
                    self.cp('pool', og[:], ngb, r=[ng_tk], w=[og])
                self.tt('dve', hf[:], hf[:], ss[:, 8:16].unsqueeze(2).to_broadcast([128, 8, 128]), ALU.mult, r=[hf, ss], w=[hf])
                self.tt('dve', hf[:], hf[:], og[:], ALU.mult, r=[hf, og], w=[hf])
                for hh in range(2):
                    p = self.pnext()
                    for k in range(4):
                        self.tr_(p[:, k * 128:(k + 1) * 128], hf[:, hh * 4 + k, :], r=[hf], **wr(p, k == 0))
                    dst = yT[:, hh * 4:(hh + 1) * 4, q * 128:(q + 1) * 128]
                    src = p[:, :].rearrange("p (a b) -> p a b", a=4)
                    self.cp('act' if hh else 'dve', dst, src, r=[p], **wr(yT, q == 0 and hh == 0))
            for oc in range(8):
                p = self.pnext()
                for kc in range(8):
                    self.mm(p[:, :], W[:, kc, oc * 128:(oc + 1) * 128], yT[:, kc, :], kc == 0, kc == 7, r=[W, yT], **wr(p, kc == 0))
                self.stt('dve', xb[:, oc, :], p[:, :], self.mod[:, 16 + oc, c:c + 1], xb[:, oc, :], ALU.mult, ALU.add,
                         r=[p, self.mod, xb], pw=[xb])
            for q in range(4):
                self.ld(self.xT_v[:, :, blk * NT + q * 128: blk * NT + (q + 1) * 128], xb[:, :, q * 128:(q + 1) * 128],
                        r=[xb], w=[tks[q]], eng='pool')

    def tr_(self, out, in_, r=(), w=(), pw=()):
        n = in_.shape[0]
        idn = self.cst.ap[0:n, 0, 0:n]
        self.S.op('pe', lambda E: E.transpose(out, in_, idn), reads=list(r) + [self.cst], writes=w, pw=pw)

    def stage_in(self):
        self.areset()
        xin = self.take([128, D], 2)
        xo = self.take([128, 8, 128], 2)
        for t in range(TT // 128):
            a = xin.get()
            self.ld(a[:], self.x_tok[t * 128:(t + 1) * 128, :], r=[self.Xtok], w=[a])
            b = xo.get()
            for h in range(2):
                p = self.pnext()
                for k in range(4):
                    kc = h * 4 + k
                    self.tr_(p[:, k * 128:(k + 1) * 128], a[:, kc * 128:(kc + 1) * 128], r=[a], w=[p] if k == 0 else (), pw=() if k == 0 else [p])
                dst = b[:, h * 4:(h + 1) * 4, :]
                src = p[:, :].rearrange("p (a b) -> p a b", a=4)
                if h == 0:
                    self.cp('dve', dst, src, r=[p], w=[b])
                else:
                    self.cp('act', dst, src, r=[p], pw=[b])
            self.ld(self.xT_v[:, :, t * 128:(t + 1) * 128], b[:], r=[b], w=[self.xT_tk[t]], eng='pool')

    def stage_out(self):
        self.areset()
        xi = self.take([128, 8, 128], 2)
        yo = self.take([128, D], 2)
        for t in range(TT // 128):
            a = xi.get()
            self.ld(a[:], self.xT_v[:, :, t * 128:(t + 1) * 128], r=[self.xT_tk[t]], w=[a])
            b = yo.get()
            for h in range(2):
                p = self.pnext()
                for k in range(4):
                    kc = h * 4 + k
                    self.tr_(p[:, k * 128:(k + 1) * 128], a[:, kc, :], r=[a], w=[p] if k == 0 else (), pw=() if k == 0 else [p])
                if h == 0:
                    self.cp('dve', b[:, 0:512], p[:, :], r=[p], w=[b])
                else:
                    self.cp('act', b[:, 512:1024], p[:, :], r=[p], pw=[b])
            self.ld(self.y_tok[t * 128:(t + 1) * 128, :], b[:], r=[b], w=[self.Ytok], eng='pool')

    def stage_mod(self, i):
        self.areset()
        wt = self.take([128, 8, 512], 2)
        wv = self.ada_w[i].rearrange("(kc p) n -> p kc n", p=128)
        mp = self.pnext()
        for n in range(12):
            w = wt.get()
            self.ld(w[:], wv[:, :, n * 512:(n + 1) * 512], r=[self.Win], w=[w])
            for jj in range(4):
                j = n * 4 + jj
                for kc in range(8):
                    self.mm(mp[:, 2 * j:2 * j + 2], w[:, kc, jj * 128:(jj + 1) * 128], self.sc[:, kc, :], kc == 0, kc == 7,
                            r=[w, self.sc], **wr(mp, j == 0 and kc == 0))
        mpv = mp[:, 0:96].rearrange("p (j c) -> p j c", c=2)
        for c in range(2):
            self.tt('dve', self.mod[:, :, c], mpv[:, :, c], self.adab[:, i, :], ALU.add, r=[mp, self.adab],
                    w=[self.mod] if c == 0 else (), pw=() if c == 0 else [self.mod])
        for wi in range(2):
            sj = 8 + 24 * wi
            for c in range(2):
                first = (wi == 0 and c == 0)
                self.stt('dve', self.modA[:, wi, :, c], self.mod[:, sj:sj + 8, c], 1.0, self.normg[:, i, wi, :], ALU.add, ALU.mult,
                         r=[self.mod, self.normg], w=[self.modA] if first else (), pw=() if first else [self.modA])

    def norm_mod(self, xb, hb, wi, c, nt, sq=None):
        sj = 24 * wi
        self.act(hb[:, :, :], xb[:, :, :], AF.Square, r=[xb], w=[hb])
        p = self.pnext()
        for kc in range(8):
            self.mm(p[:, 0:nt], self.cst.ap[:, 1, :], hb[:, kc, :], kc == 0, kc == 7, r=[self.cst, hb], **wr(p, kc == 0))
        rs = self.rstd.get()
        self.act(rs[:, 0:nt], p[:, 0:nt], AF.Sqrt, r=[p, self.epsb], w=[rs], scale=1.0 / D, bias=self.epsb[:, 0:1])
        self.recip(rs[:, 0:nt], rs[:, 0:nt], r=[rs], w=[rs])
        for kc in range(8):
            self.tt('dve' if kc % 2 == 0 else 'pool', hb[:, kc, :], xb[:, kc, :], rs[:, 0:nt], ALU.mult, r=[xb, rs],
                    w=[hb] if kc == 0 else (), pw=() if kc == 0 else [hb])
        for kc in range(8):
            self.act(hb[:, kc, :], hb[:, kc, :], AF.Identity, r=[hb, self.modA, self.mod], pw=[hb],
                     scale=self.modA[:, wi, kc, c:c + 1], bias=self.mod[:, sj + kc, c:c + 1])

    def norm_mod2(self, xtk, xap, htk, hap, tmp, wi, c, nt, first):
        sj = 24 * wi
        self.act(tmp[:, :, 0:nt], xap, AF.Square, r=[xtk], w=[tmp])
        p = self.pnext()
        for kc in range(8):
            self.mm(p[:, 0:nt], self.cst.ap[:, 1, :], tmp[:, kc, 0:nt], kc == 0, kc == 7, r=[self.cst, tmp], **wr(p, kc == 0))
        rs = self.rstd.get()
        self.act(rs[:, 0:nt], p[:, 0:nt], AF.Sqrt, r=[p, self.epsb], w=[rs], scale=1.0 / D, bias=self.epsb[:, 0:1])
        self.recip(rs[:, 0:nt], rs[:, 0:nt], r=[rs], w=[rs])
        for kc in range(8):
            self.tt('dve' if kc % 2 == 0 else 'pool', tmp[:, kc, 0:nt], xap[:, kc, :], rs[:, 0:nt], ALU.mult, r=[xtk, rs], **wr(tmp, kc == 0))
        for kc in range(8):
            self.act(hap[:, kc, :], tmp[:, kc, 0:nt], AF.Identity, r=[tmp, self.modA, self.mod], **wr(htk, first and kc == 0),
                     scale=self.modA[:, wi, kc, c:c + 1], bias=self.mod[:, sj + kc, c:c + 1])

    def stage_ffn16(self, i):
        self.areset()
        SB = 1024
        NH = SB // 512
        xbs = self.take([128, 8, SB], 1)
        hbs = self.take([128, 8, SB], 1, BF16)
        sq = self.take([128, 8, 512])
        self.rstd = self.take([128, 512], 2)
        acts = self.take([128, 22, SB], 1, BF16)
        wst = self.take([128, 8, 2, 128], 3)
        w16 = self.take([128, 8, 2, 128], 3, BF16)
        wost = self.take([128, 22, 128], 2)
        wo16 = self.take([128, 22, 128], 2, BF16)
        sg = self.take([128, 512], 2)
        wiv = self.ffn_w_in[i].rearrange("(kc p) n -> p kc n", p=128)
        wov = self.ffn_w_out[i].rearrange("(kc p) n -> p kc n", p=128)
        for sb in range(TT // SB):
            c = 0 if sb * SB < NP * TP else 1
            tks = self.xT_tk[sb * 8:(sb + 1) * 8]
            xb = xbs.get()
            self.ld(xb[:], self.xT_v[:, :, sb * SB:(sb + 1) * SB], r=tks, w=[xb], eng='pool')
            hb = hbs.get()
            for hf in range(NH):
                hs = slice(hf * 512, (hf + 1) * 512)
                self.norm_mod2(xb, xb[:, :, hs], hb, hb[:, :, hs], sq, 1, c, 512, hf == 0)
            at = acts.get()
            for j in range(22):
                ws = wst.get()
                self.ld(ws[:, :, 0, :], wiv[:, :, j * 128:(j + 1) * 128], r=[self.Win], w=[ws])
                self.ld(ws[:, :, 1, :], wiv[:, :, DFF + j * 128:DFF + (j + 1) * 128], r=[self.Win], pw=[ws])
                w = w16.get()
                self.cp('dve' if j % 2 else 'act', w[:], ws[:], r=[ws], w=[w])
                for hf in range(NH):
                    hs = slice(hf * 512, (hf + 1) * 512)
                    pg = self.pnext()
                    pu = self.pnext()
                    for kc in range(8):
                        self.mm(pg[:, :], w[:, kc, 0, :], hb[:, kc, hs], kc == 0, kc == 7, r=[w, hb], **wr(pg, kc == 0))
                    for kc in range(8):
                        self.mm(pu[:, :], w[:, kc, 1, :], hb[:, kc, hs], kc == 0, kc == 7, r=[w, hb], **wr(pu, kc == 0))
                    s_ = sg.get()
                    self.act(s_[:], pg[:, :], AF.Silu, r=[pg], w=[s_])
                    self.tt('dve', at[:, j, hs], s_[:], pu[:, :], ALU.mult, r=[s_, pu], **wr(at, j == 0 and hf == 0))
            for oc in range(8):
                ws = wost.get()
                self.ld(ws[:], wov[:, :, oc * 128:(oc + 1) * 128], r=[self.Win], w=[ws])
                w = wo16.get()
                self.cp('dve' if oc % 2 else 'act', w[:], ws[:], r=[ws], w=[w])
                for hf in range(NH):
                    hs = slice(hf * 512, (hf + 1) * 512)
                    p = self.pnext()
                    for k2 in range(22):
                        self.mm(p[:, :], w[:, k2, :], at[:, k2, hs], k2 == 0, k2 == 21, r=[w, at], **wr(p, k2 == 0))
                    self.stt('dve', xb[:, oc, hs], p[:, :], self.mod[:, 40 + oc, c:c + 1], xb[:, oc, hs], ALU.mult, ALU.add,
                             r=[p, self.mod, xb], pw=[xb])
            for q in range(SB // 128):
                self.ld(self.xT_v[:, :, sb * SB + q * 128: sb * SB + (q + 1) * 128], xb[:, :, q * 128:(q + 1) * 128],
                        r=[xb], w=[tks[q]], eng='pool')

    def stage_ffn(self, i):
        self.areset()
        NT = 512
        xbs = self.take([128, 8, NT], 2)
        hbs = self.take([128, 8, NT], 1)
        self.rstd = self.take([128, NT], 2)
        acts = self.take([128, 22, NT], 1)
        sg = self.take([128, NT], 2)
        wins = self.take([128, 8, 2, 128], 3)
        wouts = self.take([128, 22, 128], 2)
        wiv = self.ffn_w_in[i].rearrange("(kc p) n -> p kc n", p=128)
        wov = self.ffn_w_out[i].rearrange("(kc p) n -> p kc n", p=128)
        for blk in range(TT // NT):
            c = 0 if blk < (NP * TP) // NT else 1
            tks = self.xT_tk[blk * 4:(blk + 1) * 4]
            xb = xbs.get()
            self.ld(xb[:], self.xT_v[:, :, blk * NT:(blk + 1) * NT], r=tks, w=[xb], eng='pool')
            hb = hbs.get()
            self.norm_mod(xb, hb, 1, c, NT)
            at = acts.get()
            for j in range(22):
                w = wins.get()
                self.ld(w[:, :, 0, :], wiv[:, :, j * 128:(j + 1) * 128], r=[self.Win], w=[w])
                self.ld(w[:, :, 1, :], wiv[:, :, DFF + j * 128:DFF + (j + 1) * 128], r=[self.Win], pw=[w])
                pg = self.pnext()
                pu = self.pnext()
                for kc in range(8):
                    self.mm(pg[:, :], w[:, kc, 0, :], hb[:, kc, :], kc == 0, kc == 7, r=[w, hb], **wr(pg, kc == 0))
                for kc in range(8):
                    self.mm(pu[:, :], w[:, kc, 1, :], hb[:, kc, :], kc == 0, kc == 7, r=[w, hb], **wr(pu, kc == 0))
                s = sg.get()
                self.act(s[:], pg[:, :], AF.Silu, r=[pg], w=[s])
                self.tt('dve', at[:, j, :], s[:], pu[:, :], ALU.mult, r=[s, pu], w=[at] if j == 0 else (), pw=() if j == 0 else [at])
            for oc in range(8):
                w = wouts.get()
                self.ld(w[:], wov[:, :, oc * 128:(oc + 1) * 128], r=[self.Win], w=[w])
                p = self.pnext()
                for k2 in range(22):
                    self.mm(p[:, :], w[:, k2, :], at[:, k2, :], k2 == 0, k2 == 21, r=[w, at], **wr(p, k2 == 0))
                self.stt('dve', xb[:, oc, :], p[:, :], self.mod[:, 40 + oc, c:c + 1], xb[:, oc, :], ALU.mult, ALU.add,
                         r=[p, self.mod, xb], pw=[xb])
            for q in range(4):
                self.ld(self.xT_v[:, :, blk * NT + q * 128: blk * NT + (q + 1) * 128], xb[:, :, q * 128:(q + 1) * 128],
                        r=[xb], w=[tks[q]], eng='pool')


def host_consts():
    c = np.zeros((128, 8, 128), np.float32)
    c[:, 0, :] = np.eye(128)
    c[:, 1, :] = 1.0
    k = np.arange(128)[:, None]
    t = np.arange(128)[None, :]
    c[:, 2, :] = (k <= t)
    c[:, 3, :] = (k >= t)
    c[:, 4, :] = (k > t)
    c[:, 5, :] = (k < t)
    return c.reshape(128, 1024)


def rope_tables():
    rows = TS // 64
    row = np.broadcast_to(np.arange(rows)[:, None], (rows, 64)).reshape(-1)
    col = np.broadcast_to(np.arange(64)[None, :], (rows, 64)).reshape(-1)
    inv = (np.float32(10000.0) ** (-np.arange(16, dtype=np.float32) / np.float32(16))).astype(np.float32)
    ang = np.stack([row, col], axis=-1).astype(np.float32)[:, :, None] * inv
    return np.concatenate([np.cos(ang).reshape(TS, 32), np.sin(ang).reshape(TS, 32)], axis=1).astype(np.float32)


_CACHE = {}


def kernel(**inp):
    opts = inp.pop('_opts', {})
    key = repr(sorted(opts.items()))
    if key not in _CACHE:
        P = Prog(opts)
        P.build()
        _CACHE[key] = P
    P = _CACHE[key]
    f = lambda a: np.ascontiguousarray(np.asarray(a, dtype=np.float32))
    xp = f(inp['x_prompt'])
    xs = f(inp['x_sample'])
    c = f(inp['c'])
    c_ctx = f(inp['c_ctx'])
    ada_b = f(inp['ada_b'])
    norm_g = f(inp['norm_g'])
    shared = {
        'consts': host_consts(),
        'ada_w': f(inp['ada_w']),
        'ada_bT': f(ada_b.reshape(4, 48, 128).transpose(2, 0, 1)),
        'normgT': f(norm_g.reshape(4, 2, 8, 128).transpose(3, 0, 1, 2)),
        'ffn_w_in': f(inp['ffn_w_in']),
        'ffn_w_out': f(inp['ffn_w_out']),
        'mlstm_w_in': f(inp['mlstm_w_in'][0]),
        'mlstm_w_out': f(inp['mlstm_w_out'][0]),
        'mlstm_gate_b': f(inp['mlstm_gate_b'].reshape(1, 32)),
        'mlstm_norm_g': f(inp['mlstm_norm_g'].reshape(1, 128)),
    }
    shared.update({
        'diff_w_in': f(inp['diff_w_in'][0]),
        'diff_w_out': f(inp['diff_w_out'][0]),
        'diff_qkg': f(np.concatenate([inp['diff_q_norm_g'][0], inp['diff_k_norm_g'][0]]).reshape(1, 128)),
        'diff_lambda': f(inp['diff_lambda'][0].reshape(1, 256)),
        'diff_subln_g': f(inp['diff_subln_g'][0].reshape(1, 128)),
        'rope_cs': rope_tables(),
    })
    cw = f(inp['gdn_conv_w'])
    shared.update({
        'gdn_w_in': f(inp['gdn_w_in']),
        'gdn_w_out': f(inp['gdn_w_out']),
        'gdn_convT': f(cw.reshape(2, 5, 24, 128).transpose(3, 0, 2, 1)),
        'gdn_a_log': f(inp['gdn_a_log'].reshape(2, 1, 16)),
        'gdn_dt_bias': f(inp['gdn_dt_bias'].reshape(2, 1, 16)),
        'gdn_norm_g': f(inp['gdn_norm_g'].reshape(2, 1, 128)),
    })
    stS = f(inp['state_delta'])
    ck = f(inp['cache_diff_k'])
    cv = f(inp['cache_diff_v'])
    stC = f(inp['state_mlstm_C'])
    stn = f(inp['state_mlstm_n'])
    stm = f(inp['state_mlstm_m'])
    in_maps = []
    for k in range(NCORE):
        m = dict(shared)
        m['x_tok'] = f(np.concatenate([xp[NP * k:NP * (k + 1)].reshape(NP * TP, D), xs[k]], axis=0))
        cond = np.stack([c_ctx, c[k]], axis=-1)
        m['condT'] = f(cond.reshape(8, 128, 2).transpose(1, 0, 2))
        m['st_S'] = f(stS[k])
        m['ctx_k'] = f(ck[k, 0])
        m['ctx_v'] = f(cv[k, 0])
        m['st_C'] = f(stC[k, 0])
        m['st_n'] = f(stn[k, 0].reshape(2, 8, 64, 1))
        m['st_m'] = f(stm[k, 0].reshape(1, 16))
        in_maps.append({n: m[n] for n in P.din})
    res = run_bass_kernel_spmd(P.nc, in_maps, core_ids=list(range(NCORE)))
    R = res.results
    y = np.stack([r['y_tok'] for r in R])
    y_prompt = y[:, :NP * TP].reshape(NCORE * NP, TP, D)
    y_sample = y[:, NP * TP:]
    outs = [y_prompt, y_sample]
    if 'newS' in P.dout:
        outs.append(np.stack([r['newS'] for r in R]).reshape(NCORE * NP, 2, 2, 8, 128, 128))
    if 'newC' in P.dout:
        outs.append(np.stack([r['newC'] for r in R]).reshape(NCORE * NP, 1, 2, 8, 64, 128))
        outs.append(np.stack([r['newn'] for r in R]).reshape(NCORE * NP, 1, 2, 8, 64))
        outs.append(np.stack([r['newm'] for r in R]).reshape(NCORE * NP, 1, 2, 8))
    if 'newk' in P.dout:
        outs.append(np.stack([r['newk'] for r in R]).reshape(NCORE * NP, 1, 8, 2, TP, 64))
        outs.append(np.stack([r['newv'] for r in R]).reshape(NCORE * NP, 1, 8, TP, 128))
    return tuple(outs)
```

```python
import numpy as np
from contextlib import ExitStack
import concourse.bass as bass
import concourse.mybir as mybir
from concourse.bass_utils import run_bass_kernel_spmd

F32 = mybir.dt.float32
BF16 = mybir.dt.bfloat16
AF = mybir.ActivationFunctionType
ALU = mybir.AluOpType
AX = mybir.AxisListType

ENGS = ('pe', 'act', 'dve', 'pool', 'sp')
NDS = 40

D = 1024
NCORE = 8
NP = 4
TP = 256
TS = 2048
TT = NP * TP + TS
DFF = 2816
EPS = 1e-6


class Tk:
    __slots__ = ('ap', 'lw', 'rd', 'rp', 'name')

    def __init__(self, ap, name=''):
        self.ap = ap
        self.lw = {}
        self.rd = {}
        self.rp = {}
        self.name = name

    def __getitem__(self, idx):
        return self.ap[idx]


class Sched:
    def __init__(self, nc, es):
        self.nc = nc
        self.es = es
        self.q = {e: [] for e in ENGS}
        self.sem = {e: es.enter_context(nc.semaphore("s_" + e)) for e in ENGS}
        self.cnt = {e: 0 for e in ENGS}
        self.seen = {e: {} for e in ENGS}
        self.dsem = [es.enter_context(nc.semaphore("d%d" % i)) for i in range(NDS)]
        self.dcnt = [0] * NDS
        self.dnext = 0
        self.nins = 0
        self.uid = 0

    def sb(self, shape, dt=F32, name=None):
        self.uid += 1
        name = name or "t%d" % self.uid
        t = self.es.enter_context(self.nc.sbuf_tensor(name, list(shape), dt))
        return Tk(t, name)

    def ps(self, shape, dt=F32, name=None):
        self.uid += 1
        name = name or "p%d" % self.uid
        t = self.es.enter_context(self.nc.psum_tensor(name, list(shape), dt))
        return Tk(t, name)

    def _wait(self, eng, d):
        k = d[0]
        if eng == 'pe' and k == ('e', 'pe'):
            return
        seen = self.seen[eng]
        if seen.get(k, 0) >= d[2]:
            return
        seen[k] = d[2]
        self.q[eng].append(lambda E, d=d: E.wait_ge(d[1], d[2]))
        self.nins += 1

    def _deps(self, eng, reads, writes, pw):
        deps = {}

        def add(d):
            k = d[0]
            if k not in deps or deps[k][2] < d[2]:
                deps[k] = d
        for t in reads:
            for d in t.lw.values():
                add(d)
        for t in writes:
            for d in t.lw.values():
                add(d)
            for d in t.rd.values():
                add(d)
        for t in pw:
            for d in t.rd.values():
                add(d)
            for d in t.rp.values():
                add(d)
        for d in deps.values():
            self._wait(eng, d)

    def _mark(self, me, reads, writes, pw):
        for t in reads:
            t.rd[me[0]] = me
        for t in writes:
            t.lw = {me[0]: me}
            t.rp = t.rd
            t.rd = {}
        for t in pw:
            t.lw[me[0]] = me

    def op(self, eng, fn, reads=(), writes=(), pw=()):
        self._deps(eng, reads, writes, pw)
        self.cnt[eng] += 1
        sem = self.sem[eng]
        me = (('e', eng), sem, self.cnt[eng])
        self.q[eng].append(lambda E: fn(E).then_inc(sem, 1))
        self.nins += 1
        self._mark(me, reads, writes, pw)

    def dma(self, eng, out_ap, in_ap, reads=(), writes=(), pw=()):
        slot = self.dnext
        self.dnext = (slot + 1) % NDS
        ds = self.dsem[slot]
        self._deps(eng, reads, writes, pw)
        if self.dcnt[slot] > 0:
            self._wait(eng, (('d', slot), ds, 16 * self.dcnt[slot]))
        self.dcnt[slot] += 1
        me = (('d', slot), ds, 16 * self.dcnt[slot])
        self.q[eng].append(lambda E: E.dma_start(out=out_ap, in_=in_ap).then_inc(ds, 16))
        self.nins += 1
        self._mark(me, reads, writes, pw)

    def barrier(self):
        for e in ENGS:
            for o in ENGS:
                if o != e and self.cnt[o] > 0:
                    self._wait(e, (('e', o), self.sem[o], self.cnt[o]))
            for i in range(NDS):
                if self.dcnt[i] > 0:
                    self._wait(e, (('d', i), self.dsem[i], 16 * self.dcnt[i]))

    def emit(self):
        self.barrier()
        q = self.q
        with self.nc.Block() as block:
            @block.tensor
            def _(E):
                for f in q['pe']:
                    f(E)

            @block.scalar
            def _(E):
                for f in q['act']:
                    f(E)

            @block.vector
            def _(E):
                for f in q['dve']:
                    f(E)

            @block.gpsimd
            def _(E):
                for f in q['pool']:
                    f(E)

            @block.sync
            def _(E):
                for f in q['sp']:
                    f(E)


def wr(t, first):
    return {'w': [t]} if first else {'pw': [t]}


class Rot:
    def __init__(self, tiles):
        self.t = tiles
        self.i = 0

    def get(self):
        t = self.t[self.i % len(self.t)]
        self.i += 1
        return t


ARENA_COLS = 50000


class Prog:
    def __init__(self, opts):
        self.opts = opts
        self.nc = bass.Bass("TRN2", target_bir_lowering=False)
        self.es = ExitStack()
        self.din = {}
        self.dout = {}

    def inp(self, name, shape):
        t = self.nc.dram_tensor(name, list(shape), F32, kind="ExternalInput").ap()
        self.din[name] = t
        return t

    def outp(self, name, shape):
        t = self.nc.dram_tensor(name, list(shape), F32, kind="ExternalOutput").ap()
        self.dout[name] = t
        return t

    def scratch(self, name, shape):
        return self.nc.dram_tensor(name, list(shape), F32, kind="Internal").ap()

    def areset(self):
        import inspect
        self.S.barrier()
        self.apos = 0
        if not hasattr(self, 'stages'):
            self.stages = []
        self.stages.append((inspect.stack()[1].function, self.S.cnt['pe']))
        if hasattr(self, 'mark'):
            mk = self.mark
            self.act(mk[:, 0:1], mk[:, 0:1], AF.Sign, r=[mk], w=[mk])

    def take(self, shape, n=None, dt=F32):
        cols = int(np.prod(shape[1:]))
        c32 = cols if dt == F32 else (cols + 1) // 2
        out = []
        for _ in range(n or 1):
            assert self.apos + c32 <= ARENA_COLS, ("arena overflow", self.apos, c32)
            ap = self.arena[0:shape[0], self.apos:self.apos + c32]
            if dt != F32:
                ap = ap.bitcast(dt)[:, 0:cols]
            if len(shape) == 3:
                ap = ap.rearrange("p (a b) -> p a b", a=shape[1])
            elif len(shape) == 4:
                ap = ap.rearrange("p (a b c) -> p a b c", a=shape[1], b=shape[2])
            self.apos += c32
            out.append(Tk(ap))
        return out[0] if n is None else Rot(out)

    def pnext(self):
        p = self.psum[self.pi % 8]
        self.pi += 1
        return p

    def pns(self):
        return self.pnext()

    def mm(self, out, lhsT, rhs, start, stop, r=(), w=(), pw=()):
        self.S.op('pe', lambda E: E.matmul(out, lhsT=lhsT, rhs=rhs, start=start, stop=stop), reads=r, writes=w, pw=pw)

    def tr(self, out, in_, r=(), w=(), pw=()):
        ident = self.ident
        n = in_.shape[0]
        self.S.op('pe', lambda E: E.transpose(out, in_, ident[0:n, 0:n]), reads=list(r) + [ident], writes=w, pw=pw)

    def act(self, out, in_, func, r=(), w=(), pw=(), bias=None, scale=None, accum=None):
        kw = {}
        if bias is not None:
            kw['bias'] = bias
        if scale is not None:
            kw['scale'] = scale
        if accum is not None:
            kw['accum_out'] = accum
        self.S.op('act', lambda E: E.activation(out=out, in_=in_, func=func, **kw), reads=r, writes=w, pw=pw)

    def tt(self, eng, out, a, b, op, r=(), w=(), pw=()):
        self.S.op(eng, lambda E: E.tensor_tensor(out=out, in0=a, in1=b, op=op), reads=r, writes=w, pw=pw)

    def ts(self, eng, out, a, s1, s2, op0, op1=None, r=(), w=(), pw=()):
        if op1 is None:
            self.S.op(eng, lambda E: E.tensor_scalar(out=out, in0=a, scalar1=s1, scalar2=None, op0=op0), reads=r, writes=w, pw=pw)
        else:
            self.S.op(eng, lambda E: E.tensor_scalar(out=out, in0=a, scalar1=s1, scalar2=s2, op0=op0, op1=op1), reads=r, writes=w, pw=pw)

    def stt(self, eng, out, a, s, b, op0, op1, r=(), w=(), pw=()):
        self.S.op(eng, lambda E: E.scalar_tensor_tensor(out=out, in0=a, scalar=s, in1=b, op0=op0, op1=op1), reads=r, writes=w, pw=pw)

    def cp(self, eng, out, in_, r=(), w=(), pw=()):
        if eng == 'act':
            self.S.op('act', lambda E: E.copy(out=out, in_=in_), reads=r, writes=w, pw=pw)
        else:
            self.S.op(eng, lambda E: E.tensor_copy(out=out, in_=in_), reads=r, writes=w, pw=pw)

    def recip(self, out, in_, r=(), w=(), pw=()):
        self.S.op('dve', lambda E: E.reciprocal(out=out, in_=in_), reads=r, writes=w, pw=pw)

    def memset(self, eng, ap, val, w=(), pw=()):
        self.S.op(eng, lambda E: E.memset(ap, val), writes=w, pw=pw)

    def ld(self, out, in_, r=(), w=(), pw=(), eng='sp'):
        self.S.dma(eng, out, in_, reads=r, writes=w, pw=pw)

    def build(self):
        nc = self.nc
        o = self.opts
        with self.es:
            S = self.S = Sched(nc, self.es)
            self.arena = self.es.enter_context(nc.sbuf_tensor("arena", [128, ARENA_COLS], F32))
            self.psum = [S.ps([128, 512], name="ps%d" % i) for i in range(8)]
            self.pi = 0
            self.apos = 0
            self.psmall = [Tk(self.psum[i // 2].ap[:, (i % 2) * 256:(i % 2) * 256 + 256]) for i in range(16)]
            self.psi = 0
            self.x_tok = self.inp("x_tok", [TT, D])
            self.y_tok = self.outp("y_tok", [TT, D])
            self.xT = self.scratch("xT", [8, 128, TT])
            self.xT_v = self.xT.rearrange("c p t -> p c t")
            self.xT_tk = [Tk(None, "xT%d" % i) for i in range(TT // 128)]
            self.Xtok = Tk(None)
            self.Ytok = Tk(None)
            self.Win = Tk(None)
            consts = self.inp("consts", [128, 8 * 128])
            condT = self.inp("condT", [128, 8, 2])
            self.ada_w = self.inp("ada_w", [4, D, 6 * D])
            ada_bT = self.inp("ada_bT", [128, 4, 48])
            normgT = self.inp("normgT", [128, 4, 2, 8])
            self.ffn_w_in = self.inp("ffn_w_in", [4, D, 2 * DFF])
            self.ffn_w_out = self.inp("ffn_w_out", [4, DFF, D])
            self.cst = S.sb([128, 8, 128], name="cst")
            self.ld(self.cst[:], consts.rearrange("p (a b) -> p a b", a=8), r=[self.Win], w=[self.cst])
            self.ones = self.cst.ap[:, 1, :]
            self.sc = S.sb([128, 8, 2], name="sc")
            self.ld(self.sc[:], condT, r=[self.Win], w=[self.sc])
            self.act(self.sc[:], self.sc[:], AF.Silu, r=[self.sc], w=[self.sc])
            self.adab = S.sb([128, 4, 48], name="adab")
            self.ld(self.adab[:], ada_bT, r=[self.Win], w=[self.adab])
            self.normg = S.sb([128, 4, 2, 8], name="normg")
            self.ld(self.normg[:], normgT, r=[self.Win], w=[self.normg])
            self.mod = S.sb([128, 48, 2], name="mod")
            self.modA = S.sb([128, 2, 8, 2], name="modA")
            self.epsb = S.sb([128, 1], name="epsb")
            self.memset('pool', self.epsb[:], EPS, w=[self.epsb])
            self.mark = S.sb([128, 1], name="mark")
            self.memset('pool', self.mark[:], 1.0, w=[self.mark])

            self.stage_in()
            self.bf = o.get('bf16', True)
            mixers = o.get('mixers', (0, 1, 2))
            self.setup_mixers(mixers)
            for i in range(o.get('depth', 4)):
                self.stage_mod(i)
                if i % 3 == 0 and 0 in mixers:
                    self.gdn(i)
                if i % 3 == 1 and 1 in mixers:
                    self.mlstm(i)
                if i % 3 == 2 and 2 in mixers:
                    self.diffattn(i)
                if o.get('ffn', True):
                    if self.bf:
                        self.stage_ffn16(i)
                    else:
                        self.stage_ffn(i)
            self.stage_out()
            S.emit()
        return nc

    def setup_mixers(self, mixers):
        S = self.S
        if 1 in mixers:
            self.ml_w_in = self.inp("mlstm_w_in", [D, 3104])
            self.ml_w_out = self.inp("mlstm_w_out", [D, D])
            ml_gb = self.inp("mlstm_gate_b", [1, 32])
            ml_ng = self.inp("mlstm_norm_g", [1, 128])
            self.st_C = self.inp("st_C", [2, 8, 64, 128])
            self.st_n = self.inp("st_n", [2, 64, 8])
            self.st_m = self.inp("st_m", [1, 16])
            self.newC = self.outp("newC", [NP, 2, 8, 64, 128])
            self.newn = self.outp("newn", [NP, 2, 64, 8])
            self.newm = self.outp("newm", [NP, 2, 8, 1])
            self.ml_gb = S.sb([128, 32], name="ml_gb")
            self.ld(self.ml_gb[:], ml_gb.partition_broadcast(128), r=[self.Win], w=[self.ml_gb])
            self.ml_ng = S.sb([128, 128], name="ml_ng")
            self.ld(self.ml_ng[:], ml_ng.partition_broadcast(128), r=[self.Win], w=[self.ml_ng])
        self.Oout = Tk(None)
        self.qkT = self.scratch("qkT", [24, 128, TT])
        self.qkT_v = self.qkT.rearrange("c p t -> p c t")
        self.qkT_h = self.qkT[0:8].rearrange("c (two p) t -> p (c two) t", two=2)
        self.ktok = self.scratch("ktok", [TT, 2048])
        self.vtok = self.scratch("vtok", [TT, 1024])
        self.otok = self.scratch("otok", [TT, 1024])
        self.gtok = self.scratch("gtok", [TT, 32])
        self.hdir = [self.scratch("hdir%d" % d, [TT, 1024]) for d in range(2)]
        self.proj_tk = [Tk(None) for _ in range(TT // 128)]
        self.prep_tk = [Tk(None) for _ in range(TT // 128)]
        self.hdir_tk = [[Tk(None) for _ in range(TT // 64)] for d in range(2)]
        if 2 in mixers:
            self.df_w_in = self.inp("diff_w_in", [D, 3072])
            self.df_w_out = self.inp("diff_w_out", [D, D])
            df_g = self.inp("diff_qkg", [1, 128])
            df_lam = self.inp("diff_lambda", [1, 256])
            df_sg = self.inp("diff_subln_g", [1, 128])
            self.rope_cs = self.inp("rope_cs", [TS, 64])
            self.ctx_k = self.inp("ctx_k", [8, 2, 256, 64])
            self.ctx_v = self.inp("ctx_v", [8, 256, 128])
            self.newk = self.outp("newk", [NP, 8, 2, TP, 64])
            self.newv = self.outp("newv", [NP, 8, TP, 128])
            self.df_g = S.sb([128, 2, 64], name="df_g")
            self.ld(self.df_g[:], df_g.rearrange("o (a b) -> o a b", a=2).partition_broadcast(128), r=[self.Win], w=[self.df_g])
            self.df_sg = S.sb([128, 128], name="df_sg")
            self.ld(self.df_sg[:], df_sg.partition_broadcast(128), r=[self.Win], w=[self.df_sg])
            lam_init = 0.8 - 0.6 * float(np.exp(-0.3 * 2))
            self.ts('dve', self.df_sg[:], self.df_sg[:], 1.0 - lam_init, None, ALU.mult, r=[self.df_sg], w=[self.df_sg])
            lm = S.sb([128, 4, 64], name="df_lm")
            self.ld(lm[:], df_lam.rearrange("o (a b) -> o a b", a=4).partition_broadcast(128), r=[self.Win], w=[lm])
            l2 = S.sb([128, 2, 64], name="df_l2")
            self.tt('dve', l2[:, 0, :], lm[:, 0, :], lm[:, 1, :], ALU.mult, r=[lm], w=[l2])
            self.tt('dve', l2[:, 1, :], lm[:, 2, :], lm[:, 3, :], ALU.mult, r=[lm], pw=[l2])
            self.nlam = S.sb([128, 4], name="nlam")
            nl = self.nlam
            S.op('dve', lambda E: E.tensor_reduce(out=nl[:, 0:2], in_=l2[:], axis=AX.X, op=ALU.add), reads=[l2], writes=[nl])
            self.act(nl[:, 0:2], nl[:, 0:2], AF.Exp, r=[nl], w=[nl])
            self.tt('dve', nl[:, 2:3], nl[:, 1:2], nl[:, 0:1], ALU.subtract, r=[nl], pw=[nl])
            self.ts('dve', nl[:, 3:4], nl[:, 2:3], -lam_init, None, ALU.add, r=[nl], pw=[nl])
            self.ctxkT = self.scratch("ctxkT", [8, 128, 256])
            self.ctx_tk = Tk(None)
        if 0 in mixers:
            self.gd_w_in = self.inp("gdn_w_in", [2, D, 4128])
            self.gd_w_out = self.inp("gdn_w_out", [2, D, D])
            gd_cw = self.inp("gdn_convT", [128, 2, 24, 5])
            gd_al = self.inp("gdn_a_log", [2, 1, 16])
            gd_dt = self.inp("gdn_dt_bias", [2, 1, 16])
            gd_ng = self.inp("gdn_norm_g", [2, 1, 128])
            self.st_S = self.inp("st_S", [2, 2, 8, 128, 128])
            self.newS = self.outp("newS", [NP, 2, 2, 8, 128, 128])
            self.gd_cw = S.sb([128, 2, 24, 5], name="gd_cw")
            self.ld(self.gd_cw[:], gd_cw, r=[self.Win], w=[self.gd_cw])
            self.gd_nea = S.sb([128, 2, 16], name="gd_nea")
            self.gd_dt = S.sb([128, 2, 16], name="gd_dt")
            self.gd_ng = S.sb([128, 2, 128], name="gd_ng")
            for j in range(2):
                self.ld(self.gd_nea[:, j, :], gd_al[j].partition_broadcast(128), r=[self.Win], **wr(self.gd_nea, j == 0))
                self.ld(self.gd_dt[:, j, :], gd_dt[j].partition_broadcast(128), r=[self.Win], **wr(self.gd_dt, j == 0))
                self.ld(self.gd_ng[:, j, :], gd_ng[j].partition_broadcast(128), r=[self.Win], **wr(self.gd_ng, j == 0))
            self.act(self.gd_nea[:], self.gd_nea[:], AF.Exp, r=[self.gd_nea], w=[self.gd_nea])
            self.ts('dve', self.gd_nea[:], self.gd_nea[:], -1.0, None, ALU.mult, r=[self.gd_nea], w=[self.gd_nea])

    def proj_stage(self, i, w_in, ncol, fm, tm, col_lo=0):
        self.areset()
        NT = 512
        xbs = self.take([128, 8, NT], 1)
        hbs = self.take([128, 8, NT], 1)
        self.rstd = self.take([128, NT], 2)
        W = self.take([128, 8, ncol])
        wv_ = w_in.rearrange("(kc p) n -> p kc n", p=128)
        self.ld(W[:, 0:4, :], wv_[:, 0:4, col_lo:col_lo + ncol], r=[self.Win], w=[W])
        self.ld(W[:, 4:8, :], wv_[:, 4:8, col_lo:col_lo + ncol], r=[self.Win], pw=[W], eng='pool')
        fm = [(a - col_lo, b, c_, d_, e_) for (a, b, c_, d_, e_) in fm]
        tm = [(a - col_lo, b, c_) for (a, b, c_) in tm]
        ofm = self.take([128, NT], 3)
        otm = self.take([128, 512], 3)
        for blk in range(TT // NT):
            c = 0 if blk < (NP * TP) // NT else 1
            tks = self.xT_tk[blk * 4:(blk + 1) * 4]
            ptk = self.proj_tk[blk * 4:(blk + 1) * 4]
            xb = xbs.get()
            self.ld(xb[:], self.xT_v[:, :, blk * NT:(blk + 1) * NT], r=tks, w=[xb], eng='pool')
            hb = hbs.get()
            self.norm_mod(xb, hb, 0, c, NT)
            n = 0
            for (col0, nch, dstv, ch0, scale) in fm:
                for oc in range(nch):
                    p = self.pnext()
                    for kc in range(8):
                        self.mm(p[:, :], W[:, kc, col0 + oc * 128: col0 + (oc + 1) * 128], hb[:, kc, :], kc == 0, kc == 7, r=[W, hb], **wr(p, kc == 0))
                    ot = ofm.get()
                    if n % 2 == 0:
                        self.act(ot[:], p[:, :], AF.Copy, r=[p], w=[ot], scale=scale)
                    else:
                        self.ts('dve', ot[:], p[:, :], scale, None, ALU.mult, r=[p], w=[ot])
                    n += 1
                    self.ld(dstv[:, ch0 + oc, blk * NT:(blk + 1) * NT], ot[:], r=[ot], pw=ptk, eng='pool')
            for q in range(4):
                t0 = blk * NT + q * 128
                for (col0, ncols, dst) in tm:
                    for g0 in range(0, ncols, 512):
                        gw = min(512, ncols - g0)
                        p = self.pnext()
                        for kc in range(8):
                            self.mm(p[:, 0:gw], hb[:, kc, q * 128:(q + 1) * 128], W[:, kc, col0 + g0: col0 + g0 + gw], kc == 0, kc == 7, r=[W, hb], **wr(p, kc == 0))
                        ot = otm.get()
                        if n % 2 == 0:
                            self.cp('act', ot[:, 0:gw], p[:, 0:gw], r=[p], w=[ot])
                        else:
                            self.cp('dve', ot[:, 0:gw], p[:, 0:gw], r=[p], w=[ot])
                        n += 1
                        self.ld(dst[t0:t0 + 128, g0:g0 + gw], ot[:, 0:gw], r=[ot], pw=[ptk[q]], eng='sp')

    def load_w16(self, W16, w_view, ncol, col_lo=0, piece=512, eng2='pool'):
        n = 0
        for c0 in range(0, ncol, piece):
            cw = min(piece, ncol - c0)
            st = self.wstage.get()
            self.ld(st[:, :, 0:cw], w_view[:, :, col_lo + c0:col_lo + c0 + cw], r=[self.Win], w=[st], eng='sp' if n % 2 == 0 else eng2)
            if n % 2 == 0:
                self.cp('dve', W16[:, :, c0:c0 + cw], st[:, :, 0:cw], r=[st], **wr(W16, c0 == 0))
            else:
                self.cp('act', W16[:, :, c0:c0 + cw], st[:, :, 0:cw], r=[st], **wr(W16, c0 == 0))
            n += 1

    def proj_stage16(self, i, w_in, ncol, fm, tm):
        self.areset()
        NT = 512
        xbs = self.take([128, 8, NT], 2)
        hbs = self.take([128, 8, NT], 2, BF16)
        sq = self.take([128, 8, NT])
        self.rstd = self.take([128, NT], 2)
        W = self.take([128, 8, ncol], None, BF16)
        self.wstage = self.take([128, 8, 512], 2)
        self.load_w16(W, w_in.rearrange("(kc p) n -> p kc n", p=128), ncol)
        ofm = self.take([128, NT], 3)
        otm = self.take([128, 512], 3)
        for blk in range(TT // NT):
            c = 0 if blk < (NP * TP) // NT else 1
            tks = self.xT_tk[blk * 4:(blk + 1) * 4]
            xb = xbs.get()
            self.ld(xb[:], self.xT_v[:, :, blk * NT:(blk + 1) * NT], r=tks, w=[xb], eng='pool')
            hb = hbs.get()
            self.norm_mod2(xb, xb[:, :, :], hb, hb[:, :, :], sq, 0, c, NT, True)
            n = 0
            for (col0, nch, dstv, ch0, scale) in fm:
                for oc in range(nch):
                    p = self.pnext()
                    for kc in range(8):
                        self.mm(p[:, :], W[:, kc, col0 + oc * 128: col0 + (oc + 1) * 128], hb[:, kc, :], kc == 0, kc == 7, r=[W, hb], **wr(p, kc == 0))
                    ot = ofm.get()
                    if n % 2 == 0:
                        self.act(ot[:], p[:, :], AF.Copy, r=[p], w=[ot], scale=scale)
                    else:
                        self.ts('dve', ot[:], p[:, :], scale, None, ALU.mult, r=[p], w=[ot])
                    n += 1
                    self.ld(dstv[:, ch0 + oc, blk * NT:(blk + 1) * NT], ot[:], r=[ot], pw=[self.Oout], eng='pool')
            for q in range(4):
                t0 = blk * NT + q * 128
                for (col0, ncols, dst) in tm:
                    for g0 in range(0, ncols, 512):
                        gw = min(512, ncols - g0)
                        p = self.pnext()
                        for kc in range(8):
                            self.mm(p[:, 0:gw], hb[:, kc, q * 128:(q + 1) * 128], W[:, kc, col0 + g0: col0 + g0 + gw], kc == 0, kc == 7, r=[W, hb], **wr(p, kc == 0))
                        ot = otm.get()
                        if n % 2 == 0:
                            self.cp('act', ot[:, 0:gw], p[:, 0:gw], r=[p], w=[ot])
                        else:
                            self.cp('dve', ot[:, 0:gw], p[:, 0:gw], r=[p], w=[ot])
                        n += 1
                        self.ld(dst[t0:t0 + 128, g0:g0 + gw], ot[:, 0:gw], r=[ot], pw=[self.Oout], eng='sp')

    def mlstm(self, i):
        o = self.opts
        (self.proj_stage16 if self.bf else self.proj_stage)(i, self.ml_w_in, 3104,
                        fm=[(0, 4, self.qkT_v, 0, 0.125), (512, 4, self.qkT_v, 4, 1.0)],
                        tm=[(512, 512, self.ktok), (1024, 1024, self.vtok), (2048, 1024, self.otok), (3072, 32, self.gtok)])
        self.mlstm_scan()
        self.mixer_post(i, self.ml_w_out, self.ml_ng, self.ml_ng[:], self.hdir, 'sigmoid')

    def mlstm_scan(self):
        self.areset()
        cst = self.cst
        Tri = [cst.ap[0:64, 2, 0:64], cst.ap[0:64, 3, 0:64]]
        Str = [cst.ap[0:64, 4, 0:64], cst.ap[0:64, 5, 0:64]]
        ones64 = cst.ap[0:64, 1, 0:64]
        Cn8 = [self.take([64, 8, 129]) for d in range(2)]
        qks = self.take([64, 16, 64], 4)
        kts = self.take([64, 512], 4)
        v1s = self.take([64, 8, 129], 4)
        for v1 in v1s.t:
            self.memset('pool', v1[:, :, 128:129], 1.0, pw=[v1])
        gts = self.take([64, 32], 4)
        gps = self.take([64, 64], 4)
        tls = self.take([64, 64], 6)
        tot8s = self.take([64, 8, 129], 4)
        tl8s = self.take([64, 8, 64], 2)
        E8s = self.take([64, 8, 64], 4)
        kw8s = self.take([64, 8, 64], 4)
        dns = self.take([64, 16], 4)
        houts = self.take([64, 8, 128], 4)
        mst = [self.take([8, 1]) for d in range(2)]
        msm = self.take([8, 8], 2)
        emf = self.take([64, 8], 2)
        GBs = self.take([8, 2], 4)
        em0 = self.take([64, 16])
        co8s = self.take([64, 8, 129], 2)
        n0s = self.take([64, 8], 4)
        ia8s = self.take([64, 8, 129], 3)
        seqs = [(p * TP, TP // 64, p) for p in range(NP)] + [(NP * TP, TS // 64, -1)]
        for (tok0, nch, pidx) in seqs:
            if pidx >= 0:
                for d in range(2):
                    self.memset('pool', Cn8[d][:], 0.0, w=[Cn8[d]])
                    self.memset('pool', mst[d][:], 0.0, w=[mst[d]])
            else:
                self.ld(em0[:], self.st_m.partition_broadcast(64), r=[self.Win], w=[em0])
                self.act(em0[:], em0[:], AF.Exp, r=[em0], w=[em0])
                for d in range(2):
                    T_ = Cn8[d]
                    self.ld(T_[:, :, 0:128], self.st_C[d].rearrange("h k e -> k h e"), r=[self.Win], w=[T_])
                    n0 = n0s.get()
                    self.ld(n0[:], self.st_n[d], r=[self.Win], w=[n0], eng='pool')
                    self.cp('dve', T_[:, :, 128], n0[:], r=[n0], pw=[T_])
                    self.tt('pool', T_[:], T_[:], em0[:, d * 8:d * 8 + 8].unsqueeze(2).to_broadcast([64, 8, 129]), ALU.mult, r=[T_, em0], w=[T_])
            for step in range(nch):
                ctxs = []
                for d in range(2):
                    c = step if d == 0 else nch - 1 - step
                    t0 = tok0 + c * 64
                    ptk = [self.proj_tk[t0 // 128]]
                    qk = qks.get()
                    self.ld(qk[:], self.qkT_h[:, :, t0:t0 + 64], r=ptk, w=[qk])
                    kt = kts.get()
                    self.ld(kt[:], self.ktok[t0:t0 + 64, 0:512], r=ptk, w=[kt], eng="pool")
                    v1 = v1s.get()
                    self.ld(v1[:, :, 0:128], self.vtok[t0:t0 + 64, :].rearrange("t (h e) -> t h e", h=8), r=ptk, pw=[v1])
                    gt = gts.get()
                    self.ld(gt[:], self.gtok[t0:t0 + 64, :], r=ptk, w=[gt], eng='pool')
                    gp = gps.get()
                    dc = slice(d * 8, d * 8 + 8)
                    self.tt('dve', gp[:, 0:8], gt[:, dc], self.ml_gb[0:64, dc], ALU.add, r=[gt, self.ml_gb], w=[gp])
                    self.tt('dve', gp[:, 16:24], gt[:, 16 + d * 8:24 + d * 8], self.ml_gb[0:64, 16 + d * 8:24 + d * 8], ALU.add, r=[gt, self.ml_gb], pw=[gp])
                    self.act(gp[:, 16:24], gp[:, 16:24], AF.Exp, r=[gp], pw=[gp], scale=-1.0)
                    self.act(gp[:, 16:24], gp[:, 16:24], AF.Ln, r=[gp], pw=[gp], bias=1.0)
                    self.ts('dve', gp[:, 16:24], gp[:, 16:24], -1.0, None, ALU.mult, r=[gp], pw=[gp])
                    lf = gp[:, 16:24]
                    pg = self.pnext()
                    self.mm(pg[0:64, 0:8], Tri[d], lf, True, True, r=[cst, gp], w=[pg])
                    self.mm(pg[0:64, 8:16], Str[d], lf, True, True, r=[cst, gp], pw=[pg])
                    self.mm(pg[0:64, 16:24], ones64, lf, True, True, r=[cst, gp], pw=[pg])
                    self.act(gp[:, 32:40], pg[0:64, 0:8], AF.Exp, r=[pg], pw=[gp])
                    self.tt('dve', gp[:, 56:64], pg[0:64, 8:16], gp[:, 0:8], ALU.add, r=[pg, gp], pw=[gp])
                    self.act(gp[:, 40:48], gp[:, 56:64], AF.Exp, r=[gp], pw=[gp])
                    self.act(gp[:, 48:56], pg[0:64, 16:24], AF.Exp, r=[pg], pw=[gp])
                    if pidx >= 0:
                        pt = self.pnext()
                        self.tr_(pt[0:8, 0:64], gp[:, 56:64], r=[gp], w=[pt])
                        self.tr_(pt[0:8, 64:128], gp[:, 24:32] if False else pg[0:64, 16:24], r=[pg], pw=[pt]) if False else None
                        GB = GBs.get()
                        self.S.op('dve', lambda E, GB=GB, pt=pt: E.tensor_reduce(out=GB[:, 0:1], in_=pt[0:8, 0:64], axis=AX.X, op=ALU.max), reads=[pt], writes=[GB])
                        pb = self.pnext()
                        self.mm(pb[0:8, 0:1], lf, cst.ap[0:64, 1, 0:1], True, True, r=[gp, cst], w=[pb])
                        self.stt('dve', mst[d][:], mst[d][:], pb[0:8, 0:1], GB[:, 0:1], ALU.add, ALU.max, r=[mst[d], pb, GB], w=[mst[d]])
                    ctxs.append((d, t0, qk, kt, v1, gp, houts.get(), tot8s.get()))
                units = [(cx, h) for cx in ctxs for h in range(8)]
                stE = {}
                for cx in ctxs:
                    d, t0, qk, kt, v1, gp, ho, t8 = cx
                    tl8 = tl8s.get()
                    self.tt('dve', tl8[:], Tri[d].unsqueeze(1).to_broadcast([64, 8, 64]), gp[:, 16:24].unsqueeze(2).to_broadcast([64, 8, 64]), ALU.mult, r=[cst, gp], w=[tl8])
                    pD = self.pnext()
                    self.mm(pD[0:64, 0:512], Str[d], tl8[:].rearrange("p h t -> p (h t)"), True, True, r=[cst, tl8], w=[pD])
                    E8 = E8s.get()
                    self.act(E8[:].rearrange("p h t -> p (h t)"), pD[0:64, 0:512], AF.Exp, r=[pD], w=[E8])
                    self.act(gp[:, 8:16], gp[:, 0:8], AF.Exp, r=[gp], pw=[gp])
                    stE[d] = E8
                stK = {}
                for cx in ctxs:
                    d, t0, qk, kt, v1, gp, ho, t8 = cx
                    E8 = stE[d]
                    self.tt('pool', E8[:], E8[:], Tri[d].unsqueeze(1).to_broadcast([64, 8, 64]), ALU.mult, r=[E8, cst], w=[E8])
                    self.tt('dve', E8[:], E8[:], gp[:, 8:16].unsqueeze(2).to_broadcast([64, 8, 64]), ALU.mult, r=[E8, gp], w=[E8])
                    kw8 = kw8s.get()
                    self.tt('pool', kw8[:], kt[:].rearrange("t (h e) -> t h e", h=8), gp[:, 40:48].unsqueeze(2).to_broadcast([64, 8, 64]), ALU.mult, r=[kt, gp], w=[kw8])
                    stK[d] = kw8
                for cx in ctxs:
                    d, t0, qk, kt, v1, gp, ho, t8 = cx
                    pK = self.pnext()
                    for h in range(8):
                        self.mm(pK[0:64, h * 64:(h + 1) * 64], qk[:, 8 + h, :], qk[:, h, :], True, True, r=[qk], **wr(pK, h == 0))
                    E8 = stE[d]
                    self.tt('dve', E8[:].rearrange("p h t -> p (h t)"), E8[:].rearrange("p h t -> p (h t)"), pK[0:64, 0:512], ALU.mult, r=[E8, pK], w=[E8])
                HG = [(0, 3), (3, 3), (6, 2)]
                for cx in ctxs:
                    d, t0, qk, kt, v1, gp, ho, t8 = cx
                    E8 = stE[d]
                    ia8 = ia8s.get()
                    for bi, (h0, nh) in enumerate(HG):
                        pI = self.pnext()
                        for hh in range(nh):
                            h = h0 + hh
                            self.mm(pI[0:64, hh * 129:(hh + 1) * 129], E8[:, h, :], v1[:, h, :], True, True, r=[E8, v1], **wr(pI, hh == 0))
                        self.cp('act', ia8[:, h0:h0 + nh, :], pI[0:64, 0:nh * 129].rearrange("p (h e) -> p h e", h=nh), r=[pI], **wr(ia8, bi == 0))
                    for bi, (h0, nh) in enumerate(HG):
                        pN = self.pnext()
                        for hh in range(nh):
                            h = h0 + hh
                            self.mm(pN[0:64, hh * 129:(hh + 1) * 129], qk[:, h, :], Cn8[d][:, h, :], True, True, r=[qk, Cn8[d]], **wr(pN, hh == 0))
                        self.tt('dve', t8[:, h0:h0 + nh, :], pN[0:64, 0:nh * 129].rearrange("p (h e) -> p h e", h=nh),
                                gp[:, 32 + h0:32 + h0 + nh].unsqueeze(2).to_broadcast([64, nh, 129]), ALU.mult, r=[pN, gp], **wr(t8, bi == 0))
                    self.tt('dve', t8[:], t8[:], ia8[:], ALU.add, r=[t8, ia8], w=[t8])
                for cx in ctxs:
                    d, t0, qk, kt, v1, gp, ho, t8 = cx
                    dn = dns.get()
                    den = t8[:, :, 128]
                    self.ts('dve', dn[:, 0:8], den, -1.0, None, ALU.mult, r=[t8], w=[dn])
                    self.tt('dve', dn[:, 0:8], dn[:, 0:8], den, ALU.max, r=[dn, t8], w=[dn])
                    self.ts('dve', dn[:, 0:8], dn[:, 0:8], 1.0, None, ALU.max, r=[dn], w=[dn])
                    self.recip(dn[:, 8:16], dn[:, 0:8], r=[dn], pw=[dn])
                    self.tt('pool', ho[:], t8[:, :, 0:128], dn[:, 8:16].unsqueeze(2).to_broadcast([64, 8, 128]), ALU.mult, r=[t8, dn], w=[ho])
                    self.ld(self.hdir[d][t0:t0 + 64, :].rearrange("t (h e) -> t h e", h=8), ho[:], r=[ho], w=[self.hdir_tk[d][t0 // 64]], eng='pool')
                for cx in ctxs:
                    d, t0, qk, kt, v1, gp, ho, t8 = cx
                    C_ = Cn8[d]
                    kw8 = stK[d]
                    pUs = []
                    for bi, (h0, nh) in enumerate(HG):
                        pU = self.pnext()
                        for hh in range(nh):
                            h = h0 + hh
                            self.mm(pU[0:64, hh * 129:(hh + 1) * 129], kw8[:, h, :], v1[:, h, :], True, True, r=[kw8, v1], **wr(pU, hh == 0))
                        pUs.append(pU)
                    self.tt('pool', C_[:], C_[:], gp[:, 48:56].unsqueeze(2).to_broadcast([64, 8, 129]), ALU.mult, r=[C_, gp], w=[C_])
                    for bi, (h0, nh) in enumerate(HG):
                        self.tt('dve', C_[:, h0:h0 + nh, :], C_[:, h0:h0 + nh, :], pUs[bi][0:64, 0:nh * 129].rearrange("p (h e) -> p h e", h=nh), ALU.add,
                                r=[C_, pUs[bi]], w=[C_])
            if pidx >= 0:
                for d in range(2):
                    dm = msm.get()
                    self.ts('dve', dm[:], cst.ap[0:8, 0, 0:8], mst[d][:, 0:1], None, ALU.mult, r=[cst, mst[d]], w=[dm])
                    pm = self.pnext()
                    self.mm(pm[0:64, 0:8], cst.ap[0:8, 1, 0:64], dm[:], True, True, r=[cst, dm], w=[pm])
                    ef = emf.get()
                    self.act(ef[:], pm[0:64, 0:8], AF.Exp, r=[pm], w=[ef], scale=-1.0)
                    self.ld(self.newm[pidx, d], mst[d][:], r=[mst[d]], w=[self.Oout], eng='pool')
                    co = co8s.get()
                    self.tt('dve', co[:], Cn8[d][:], ef[:].unsqueeze(2).to_broadcast([64, 8, 129]), ALU.mult, r=[Cn8[d], ef], w=[co])
                    self.ld(self.newC[pidx, d].rearrange("h k e -> k h e"), co[:, :, 0:128], r=[co], pw=[self.Oout], eng='sp')
                    n1 = n0s.get()
                    self.cp('dve', n1[:], co[:, :, 128], r=[co], w=[n1])
                    self.ld(self.newn[pidx, d], n1[:], r=[n1], pw=[self.Oout], eng='pool')

    def gdn(self, i):
        j = i // 3
        w_in = self.gd_w_in[j]
        if self.bf:
            self.proj_stage16(i, w_in, 4128, fm=[(0, 24, self.qkT_v, 0, 1.0)],
                              tm=[(3072, 1024, self.otok), (4096, 32, self.gtok)])
        else:
            self.proj_stage(i, w_in, 2048, fm=[(0, 16, self.qkT_v, 0, 1.0)], tm=[], col_lo=0)
            self.proj_stage(i, w_in, 2080, fm=[(2048, 8, self.qkT_v, 16, 1.0)],
                            tm=[(3072, 1024, self.otok), (4096, 32, self.gtok)], col_lo=2048)
        stop = self.opts.get('gdn_stop', 9)
        if stop >= 2:
            self.gdn_conv(j)
        if stop >= 3:
            self.gdn_scan(j)
        if stop >= 4:
            self.mixer_post(i, self.gd_w_out[j], self.gd_ng, self.gd_ng[:, j, :], self.hdir, 'silu')

    def gdn_conv(self, j):
        self.areset()
        NBUF = 6
        xins = self.take([128, TS + 16], NBUF)
        tmps = self.take([128, TS], 3)
        sqs = self.take([128, 512], 3)
        rss = self.take([128, 512], 3)
        tos = self.take([128, 4, 128], 4)
        accs = []
        for _ in range(NBUF):
            base = self.take([128, TS])
            accs.append((base.ap, [Tk(base.ap[:, b * 512:(b + 1) * 512]) for b in range(4)]))
        cw = self.gd_cw
        n = 0
        items = [(0, NP, TP), (NP * TP, 1, TS)]
        for (tok0, ns, T) in items:
            W = T + 4
            for ch in range(24):
                on_dve = (n % 2 == 0)
                xin = xins.get()
                acc_ap, accb = accs[n % NBUF]
                n += 1
                xv = xin[:, 0:ns * W].rearrange("p (s w) -> p s w", s=ns)
                av = acc_ap[:, 0:ns * T].rearrange("p (s t) -> p s t", s=ns)
                self.memset('pool', xv[:, :, 0:2], 0.0, w=[xin])
                self.memset('pool', xv[:, :, T + 2:T + 4], 0.0, pw=[xin])
                self.ld(xv[:, :, 2:T + 2], self.qkT_v[:, ch, tok0:tok0 + ns * T].rearrange("p (s t) -> p s t", s=ns), pw=[xin])
                if on_dve:
                    self.ts('dve', av, xv[:, :, 0:T], cw[:, j, ch, 0:1], None, ALU.mult, r=[xin, cw], w=accb)
                    for k in range(1, 5):
                        self.stt('dve', av, xv[:, :, k:k + T], cw[:, j, ch, k:k + 1], av, ALU.mult, ALU.add, r=[xin, cw] + accb, w=accb)
                else:
                    self.act(av, xv[:, :, 0:T], AF.Copy, r=[xin, cw], w=accb, scale=cw[:, j, ch, 0:1])
                    for k in range(1, 5):
                        tm_ = tmps.get()
                        tv = tm_[:, 0:ns * T].rearrange("p (s t) -> p s t", s=ns)
                        self.act(tv, xv[:, :, k:k + T], AF.Copy, r=[xin, cw], w=[tm_], scale=cw[:, j, ch, k:k + 1])
                        self.tt('pool', av, av, tv, ALU.add, r=accb + [tm_], w=accb)
                self.act(acc_ap[:, 0:ns * T], acc_ap[:, 0:ns * T], AF.Silu, r=accb, w=accb)
                NTOK = ns * T
                nb = NTOK // 512
                if ch < 16:
                    scale = (128.0 ** -0.5) if ch < 8 else 1.0
                    for b in range(nb):
                        bs = slice(b * 512, (b + 1) * 512)
                        sq = sqs.get()
                        self.tt('pool', sq[:], acc_ap[:, bs], acc_ap[:, bs], ALU.mult, r=[accb[b]], w=[sq])
                        p = self.pnext()
                        self.mm(p[:, :], self.cst.ap[:, 1, :], sq[:], True, True, r=[self.cst, sq], w=[p])
                        rs = rss.get()
                        self.act(rs[:], p[:, :], AF.Sqrt, r=[p, self.epsb], w=[rs], bias=self.epsb[:, 0:1])
                        self.recip(rs[:], rs[:], r=[rs], w=[rs])
                        self.stt('dve', acc_ap[:, bs], acc_ap[:, bs], scale, rs[:], ALU.mult, ALU.mult, r=[accb[b], rs], w=[accb[b]])
                    self.ld(self.qkT_v[:, ch, tok0:tok0 + NTOK], acc_ap[:, 0:NTOK], r=accb[0:nb], pw=[self.Oout], eng='pool')
                if ch >= 8:
                    dst = self.ktok if ch < 16 else self.vtok
                    c0 = (ch - 8) * 128 if ch < 16 else (ch - 16) * 128
                    for b in range(nb):
                        p = self.pnext()
                        for k in range(4):
                            self.tr_(p[:, k * 128:(k + 1) * 128], acc_ap[:, b * 512 + k * 128:b * 512 + (k + 1) * 128], r=[accb[b]], **wr(p, k == 0))
                        to = tos.get()
                        self.cp('act' if b % 2 else 'dve', to[:], p[:, :].rearrange("p (a b) -> p a b", a=4), r=[p], w=[to])
                        self.ld(dst[tok0 + b * 512:tok0 + (b + 1) * 512, c0:c0 + 128].rearrange("(n p) e -> p n e", p=128), to[:], r=[to], pw=[self.Oout], eng='sp')

    def gdn_scan(self, j):
        self.areset()
        cst = self.cst
        Tri = [cst.ap[0:64, 2, 0:64], cst.ap[0:64, 3, 0:64]]
        Str = [cst.ap[0:64, 4, 0:64], cst.ap[0:64, 5, 0:64]]
        Sm = [cst.ap[0:64, 5, 0:64], cst.ap[0:64, 4, 0:64]]
        I64 = cst.ap[0:64, 0, 0:64]
        ones64w = cst.ap[0:64, 1, 0:128]

        def bh(m):
            return m.unsqueeze(1).to_broadcast([64, 8, 64])

        def bt(v, n, np_=64):
            return v.unsqueeze(2).to_broadcast([np_, v.shape[1], n])

        S8 = [self.take([128, 8, 128]) for d in range(2)]
        qks = self.take([128, 8, 2, 64], 3)
        kts = self.take([64, 8, 128], 3)
        vts = self.take([64, 8, 128], 3)
        gts = self.take([64, 32], 3)
        gps = self.take([64, 48], 3)
        gls = self.take([128, 8], 3)
        tl8s = self.take([64, 8, 64], 2)
        Er8s = self.take([64, 8, 64], 2)
        Ei8s = self.take([64, 8, 64], 2)
        Es8s = self.take([64, 8, 64], 2)
        qkT8s = self.take([64, 8, 64], 3)
        P8s = self.take([64, 8, 64], 3)
        X8s = self.take([64, 8, 64], 5)
        XT8s = self.take([64, 8, 64], 5)
        U8s = self.take([64, 8, 128], 3)
        keg8s = self.take([64, 8, 128], 2)
        kdec8s = self.take([64, 8, 128], 3)
        vn8s = self.take([64, 8, 128], 3)
        o8s = self.take([64, 8, 128], 3)
        WT8s = self.take([128, 8, 64], 3)
        seqs = [(p * TP, TP // 64, p) for p in range(NP)] + [(NP * TP, TS // 64, -1)]
        seqs = seqs[self.opts.get('gdn_seq0', 0):self.opts.get('gdn_seq1', 5)]
        for (tok0, nch, pidx) in seqs:
            for d in range(2):
                if pidx >= 0:
                    self.memset('pool', S8[d][:], 0.0, w=[S8[d]])
                else:
                    self.ld(S8[d][:], self.st_S[j, d].rearrange("h k e -> k h e"), r=[self.Win], w=[S8[d]], eng='sp' if d else 'pool')
            for step in range(nch):
                ctx = []
                for d in range(2):
                    c = step if d == 0 else nch - 1 - step
                    t0 = tok0 + c * 64
                    qk = qks.get()
                    self.ld(qk[:, :, 0, :], self.qkT_v[:, 8:16, t0:t0 + 64], w=[qk])
                    self.ld(qk[:, :, 1, :], self.qkT_v[:, 0:8, t0:t0 + 64], pw=[qk], eng='pool')
                    kt = kts.get()
                    self.ld(kt[:], self.ktok[t0:t0 + 64, 0:1024].rearrange("t (h e) -> t h e", h=8), w=[kt], eng='pool')
                    vt = vts.get()
                    self.ld(vt[:], self.vtok[t0:t0 + 64, :].rearrange("t (h e) -> t h e", h=8), w=[vt])
                    gt = gts.get()
                    self.ld(gt[:], self.gtok[t0:t0 + 64, :], w=[gt], eng='pool')
                    gp = gps.get()
                    dc = slice(d * 8, d * 8 + 8)
                    self.tt('dve', gp[:, 0:8], gt[:, dc], self.gd_dt[0:64, j, dc], ALU.add, r=[gt, self.gd_dt], w=[gp])
                    self.act(gp[:, 0:8], gp[:, 0:8], AF.Exp, r=[gp], pw=[gp])
                    self.act(gp[:, 0:8], gp[:, 0:8], AF.Ln, r=[gp], pw=[gp], bias=1.0)
                    self.tt('dve', gp[:, 8:16], gp[:, 0:8], self.gd_nea[0:64, j, dc], ALU.mult, r=[gp, self.gd_nea], pw=[gp])
                    self.act(gp[:, 16:24], gt[:, 16 + d * 8:24 + d * 8], AF.Sigmoid, r=[gt], pw=[gp])
                    self.ts('dve', gp[:, 24:32], gp[:, 16:24], -1.0, None, ALU.mult, r=[gp], pw=[gp])
                    la = gp[:, 8:16]
                    pg = self.pnext()
                    self.mm(pg[0:64, 0:8], Tri[d], la, True, True, r=[cst, gp], w=[pg])
                    self.mm(pg[0:64, 8:16], Str[d], la, True, True, r=[cst, gp], pw=[pg])
                    self.mm(pg[0:128, 16:24], ones64w, la, True, True, r=[cst, gp], pw=[pg])
                    self.act(gp[:, 32:48], pg[0:64, 0:16], AF.Exp, r=[pg], pw=[gp])
                    gl = gls.get()
                    self.act(gl[:], pg[0:128, 16:24], AF.Exp, r=[pg], w=[gl])
                    ctx.append(dict(d=d, t0=t0, qk=qk, kt=kt, vt=vt, gp=gp, gl=gl))
                for cx in ctx:
                    d, gp = cx['d'], cx['gp']
                    tl8 = tl8s.get()
                    self.tt('dve', tl8[:], bh(Tri[d]), bt(gp[:, 8:16], 64), ALU.mult, r=[cst, gp], w=[tl8])
                    pD = self.pnext()
                    self.mm(pD[0:64, 0:512], Str[d], tl8[:].rearrange("p h t -> p (h t)"), True, True, r=[cst, tl8], w=[pD])
                    Er = Er8s.get()
                    self.act(Er[:].rearrange("p h t -> p (h t)"), pD[0:64, 0:512], AF.Exp, r=[pD], w=[Er])
                    cx['Er'] = Er
                for cx in ctx:
                    d, gp, Er = cx['d'], cx['gp'], cx['Er']
                    Ei = Ei8s.get()
                    Es = Es8s.get()
                    self.tt('pool', Ei[:], Er[:], bh(Tri[d]), ALU.mult, r=[Er, cst], w=[Ei])
                    self.tt('pool', Es[:], Er[:], bh(Sm[d]), ALU.mult, r=[Er, cst], w=[Es])
                    self.tt('dve', Es[:], Es[:], bt(gp[:, 24:32], 64), ALU.mult, r=[Es, gp], w=[Es])
                    cx['Ei'], cx['Es'] = Ei, Es
                for cx in ctx:
                    qk = cx['qk']
                    X = X8s.get()
                    qkT = qkT8s.get()
                    for g in range(2):
                        pG = self.pnext()
                        for hh in range(4):
                            h = 4 * g + hh
                            self.mm(pG[0:64, hh * 128:(hh + 1) * 128], qk[:, h, 0, :], qk[:, h, :, :].rearrange("p a t -> p (a t)"), True, True,
                                    r=[qk], **wr(pG, hh == 0))
                        pv = pG[0:64, 0:512].rearrange("p (h a t) -> p h a t", h=4, a=2)
                        self.tt('dve', X[:, 4 * g:4 * g + 4, :], pv[:, :, 0, :], cx['Es'][:, 4 * g:4 * g + 4, :], ALU.mult, r=[pG, cx['Es']], **wr(X, g == 0))
                        self.tt('dve', qkT[:, 4 * g:4 * g + 4, :], pv[:, :, 1, :], cx['Ei'][:, 4 * g:4 * g + 4, :], ALU.mult, r=[pG, cx['Ei']], **wr(qkT, g == 0))
                    cx['X'], cx['qkT'] = X, qkT
                for cx in ctx:
                    X = cx['X']
                    pT = self.pnext()
                    for h in range(8):
                        self.tr_(pT[0:64, h * 64:(h + 1) * 64], X[:, h, :], r=[X], **wr(pT, h == 0))
                    XT = XT8s.get()
                    self.cp('act', XT[:].rearrange("p h t -> p (h t)"), pT[0:64, 0:512], r=[pT], w=[XT])
                    P_ = P8s.get()
                    self.tt('pool', P_[:], X[:], bh(I64), ALU.add, r=[X, cst], w=[P_])
                    cx['XT'], cx['P'] = XT, P_
                for jn in range(1, 6):
                    for cx in ctx:
                        X, XT = cx['X'], cx['XT']
                        Xn = None
                        if jn < 5:
                            pX = self.pnext()
                            for h in range(8):
                                self.mm(pX[0:64, h * 64:(h + 1) * 64], XT[:, h, :], X[:, h, :], True, True, r=[XT, X], **wr(pX, h == 0))
                            Xn = X8s.get()
                            self.cp('dve', Xn[:].rearrange("p h t -> p (h t)"), pX[0:64, 0:512], r=[pX], w=[Xn])
                        pXT = self.pnext()
                        for h in range(8):
                            self.mm(pXT[0:64, h * 64:(h + 1) * 64], X[:, h, :], XT[:, h, :], True, True, r=[XT, X], **wr(pXT, h == 0))
                        XnT = XT8s.get()
                        self.cp('act', XnT[:].rearrange("p h t -> p (h t)"), pXT[0:64, 0:512], r=[pXT], w=[XnT])
                        cx['X'], cx['XT'] = Xn, XnT
                    for cx in ctx:
                        XT, P_ = cx['XT'], cx['P']
                        pP = self.pnext()
                        for h in range(8):
                            self.mm(pP[0:64, h * 64:(h + 1) * 64], XT[:, h, :], P_[:, h, :], True, True, r=[XT, P_], **wr(pP, h == 0))
                        self.tt('dve', P_[:].rearrange("p h t -> p (h t)"), P_[:].rearrange("p h t -> p (h t)"), pP[0:64, 0:512], ALU.add, r=[P_, pP], w=[P_])
                for cx in ctx:
                    gp, kt, vt, P_ = cx['gp'], cx['kt'], cx['vt'], cx['P']
                    keg = keg8s.get()
                    self.tt('pool', keg[:], kt[:], bt(gp[:, 32:40], 128), ALU.mult, r=[kt, gp], w=[keg])
                    kdec = kdec8s.get()
                    self.tt('pool', kdec[:], kt[:], bt(gp[:, 40:48], 128), ALU.mult, r=[kt, gp], w=[kdec])
                    U = U8s.get()
                    for g in range(2):
                        pU = self.pnext()
                        for hh in range(4):
                            h = 4 * g + hh
                            self.mm(pU[0:64, hh * 128:(hh + 1) * 128], P_[:, h, :], vt[:, h, :], True, True, r=[P_, vt], **wr(pU, hh == 0))
                        self.tt('dve', U[:, 4 * g:4 * g + 4, :], pU[0:64, 0:512].rearrange("p (h e) -> p h e", h=4), bt(gp[:, 16 + 4 * g:20 + 4 * g], 128), ALU.mult,
                                r=[pU, gp], **wr(U, g == 0))
                    pW = self.pnext()
                    for h in range(8):
                        self.mm(pW[0:128, h * 64:(h + 1) * 64], keg[:, h, :], P_[:, h, :], True, True, r=[keg, P_], **wr(pW, h == 0))
                    WT = WT8s.get()
                    self.cp('act', WT[:].rearrange("p h t -> p (h t)"), pW[0:128, 0:512], r=[pW], w=[WT])
                    cx['U'], cx['WT'], cx['kdec'] = U, WT, kdec
                for cx in ctx:
                    d, gp = cx['d'], cx['gp']
                    vn = vn8s.get()
                    for g in range(2):
                        pa = self.pnext()
                        for hh in range(4):
                            h = 4 * g + hh
                            self.mm(pa[0:64, hh * 128:(hh + 1) * 128], cx['WT'][:, h, :], S8[d][:, h, :], True, True, r=[cx['WT'], S8[d]], **wr(pa, hh == 0))
                        self.tt('dve', vn[:, 4 * g:4 * g + 4, :], pa[0:64, 0:512].rearrange("p (h e) -> p h e", h=4), bt(gp[:, 24 + 4 * g:28 + 4 * g], 128), ALU.mult,
                                r=[pa, gp], **wr(vn, g == 0))
                    self.tt('pool', vn[:], vn[:], cx['U'][:], ALU.add, r=[vn, cx['U']], w=[vn])
                    cx['vn'] = vn
                for cx in ctx:
                    d, gp, gl, qk, vn = cx['d'], cx['gp'], cx['gl'], cx['qk'], cx['vn']
                    o8 = o8s.get()
                    for g in range(2):
                        po = self.pnext()
                        for hh in range(4):
                            h = 4 * g + hh
                            self.mm(po[0:64, hh * 128:(hh + 1) * 128], qk[:, h, 1, :], S8[d][:, h, :], True, True, r=[qk, S8[d]], **wr(po, hh == 0))
                        self.tt('dve', o8[:, 4 * g:4 * g + 4, :], po[0:64, 0:512].rearrange("p (h e) -> p h e", h=4), bt(gp[:, 32 + 4 * g:36 + 4 * g], 128), ALU.mult,
                                r=[po, gp], **wr(o8, g == 0))
                    for g in range(2):
                        po2 = self.pnext()
                        for hh in range(4):
                            h = 4 * g + hh
                            self.mm(po2[0:64, hh * 128:(hh + 1) * 128], cx['qkT'][:, h, :], vn[:, h, :], True, True, r=[cx['qkT'], vn], **wr(po2, hh == 0))
                        self.tt('dve', o8[:, 4 * g:4 * g + 4, :], o8[:, 4 * g:4 * g + 4, :], po2[0:64, 0:512].rearrange("p (h e) -> p h e", h=4), ALU.add,
                                r=[po2, o8], pw=[o8])
                    self.ld(self.hdir[d][cx['t0']:cx['t0'] + 64, :].rearrange("t (h e) -> t h e", h=8), o8[:], r=[o8], pw=[self.Oout], eng='pool')
                    for g in range(2):
                        pS = self.pnext()
                        for hh in range(4):
                            h = 4 * g + hh
                            self.mm(pS[0:128, hh * 128:(hh + 1) * 128], cx['kdec'][:, h, :], vn[:, h, :], True, True, r=[cx['kdec'], vn], **wr(pS, hh == 0))
                        Sg = S8[d][:, 4 * g:4 * g + 4, :]
                        self.tt('pool', Sg, Sg, bt(gl[:, 4 * g:4 * g + 4], 128, 128), ALU.mult, r=[S8[d], gl], w=[S8[d]])
                        self.tt('dve', Sg, Sg, pS[0:128, 0:512].rearrange("p (h e) -> p h e", h=4), ALU.add, r=[S8[d], pS], w=[S8[d]])
            if pidx >= 0:
                for d in range(2):
                    self.ld(self.newS[pidx, j, d].rearrange("h k e -> k h e"), S8[d][:], r=[S8[d]], pw=[self.Oout], eng='sp' if d else 'pool')

    def diffattn(self, i):
        (self.proj_stage16 if self.bf else self.proj_stage)(i, self.df_w_in, 3072, fm=[],
                        tm=[(0, 2048, self.ktok), (2048, 1024, self.vtok)])
        self.attn_prep()
        self.attn_core()
        self.mixer_post(i, self.df_w_out, self.df_sg, self.df_sg[:], self.hdir, None)

    def attn_prep(self):
        self.areset()
        xs = self.take([128, 32, 64], 3)
        sqs = self.take([128, 32, 64], 2)
        sss = self.take([128, 64], 3)
        css = self.take([128, 64], 3)
        r1 = self.take([128, 32, 2, 16], 1)
        r2 = self.take([128, 32, 2, 16], 1)
        r3 = self.take([128, 32, 2, 16], 1)
        xr = self.take([128, 32, 64], 2)
        vts = self.take([128, 1024], 2)
        xos = self.take([128, 16, 128], 2)
        cks = self.take([128, 16, 64], 2)
        cko = self.take([128, 8, 128], 2)
        gq = self.df_g
        for t in range(TT // 128):
            t0 = t * 128
            x = xs.get()
            self.ld(x[:], self.ktok[t0:t0 + 128, :].rearrange("t (g d) -> t g d", g=32), r=[self.proj_tk[t]], w=[x])
            sq = sqs.get()
            self.tt('pool', sq[:], x[:], x[:], ALU.mult, r=[x], w=[sq])
            ss = sss.get()
            self.S.op('dve', lambda E, ss=ss, sq=sq: E.tensor_reduce(out=ss[:, 0:32], in_=sq[:], axis=AX.X, op=ALU.add), reads=[sq], writes=[ss])
            self.act(ss[:, 0:32], ss[:, 0:32], AF.Sqrt, r=[ss, self.epsb], w=[ss], scale=1.0 / 64, bias=self.epsb[:, 0:1])
            self.recip(ss[:, 32:64], ss[:, 0:32], r=[ss], pw=[ss])
            self.tt('dve', x[:], x[:], ss[:, 32:64].unsqueeze(2).to_broadcast([128, 32, 64]), ALU.mult, r=[x, ss], w=[x])
            self.tt('pool', x[:, 0:16, :], x[:, 0:16, :], gq[:, 0, :].unsqueeze(1).to_broadcast([128, 16, 64]), ALU.mult, r=[x, gq], w=[x])
            self.tt('dve', x[:, 16:32, :], x[:, 16:32, :], gq[:, 1, :].unsqueeze(1).to_broadcast([128, 16, 64]), ALU.mult, r=[x, gq], w=[x])
            if t0 < NP * TP:
                p, tl = t0 // TP, t0 % TP
                self.ld(self.newk[p, :, :, tl:tl + 128, :].rearrange("h m t d -> t (h m) d"), x[:, 16:32, :], r=[x], pw=[self.Oout], eng='pool')
                vt = vts.get()
                self.ld(vt[:], self.vtok[t0:t0 + 128, :], r=[self.proj_tk[t]], w=[vt])
                self.ld(self.newv[p, :, tl:tl + 128, :].rearrange("h t e -> t h e"), vt[:].rearrange("t (h e) -> t h e", h=8), r=[vt], pw=[self.Oout], eng='pool')
                src = x
            else:
                cs = css.get()
                self.ld(cs[:], self.rope_cs[t0 - NP * TP:t0 - NP * TP + 128, :], r=[self.Win], w=[cs])
                X = x[:].rearrange("t g (a f r) -> t g a f r", a=2, f=2)
                xa = X[:, :, :, 0, :]
                xb_ = X[:, :, :, 1, :]
                cosb = cs[:, 0:32].rearrange("t (a r) -> t a r", a=2).unsqueeze(1).to_broadcast([128, 32, 2, 16])
                sinb = cs[:, 32:64].rearrange("t (a r) -> t a r", a=2).unsqueeze(1).to_broadcast([128, 32, 2, 16])
                o_ = xr.get()
                O = o_[:].rearrange("t g (a f r) -> t g a f r", a=2, f=2)
                a1, a2, a3 = r1.get(), r2.get(), r3.get()
                self.tt('dve', a1[:], xa, cosb, ALU.mult, r=[x, cs], w=[a1])
                self.tt('pool', a2[:], xb_, sinb, ALU.mult, r=[x, cs], w=[a2])
                self.tt('dve', O[:, :, :, 0, :], a1[:], a2[:], ALU.subtract, r=[a1, a2], w=[o_])
                self.tt('pool', a3[:], xa, sinb, ALU.mult, r=[x, cs], w=[a3])
                self.tt('dve', a1[:], xb_, cosb, ALU.mult, r=[x, cs], w=[a1])
                self.tt('pool', O[:, :, :, 1, :], a3[:], a1[:], ALU.add, r=[a3, a1], pw=[o_])
                src = o_
            xo = xos.get()
            for g in range(4):
                pp = self.pnext()
                for k in range(4):
                    ch = g * 4 + k
                    self.tr_(pp[:, k * 128:(k + 1) * 128], src[:, 2 * ch:2 * ch + 2, :].rearrange("t a d -> t (a d)"), r=[src], **wr(pp, k == 0))
                self.cp('act' if g % 2 else 'dve', xo[:, g * 4:(g + 1) * 4, :], pp[:, :].rearrange("p (a b) -> p a b", a=4), r=[pp], **wr(xo, g == 0))
            self.ld(self.qkT_v[:, 0:16, t0:t0 + 128], xo[:], r=[xo], w=[self.prep_tk[t]], eng='pool')
        for kt in range(2):
            ck = cks.get()
            self.ld(ck[:], self.ctx_k[:, :, kt * 128:(kt + 1) * 128, :].rearrange("h m t d -> t (h m) d"), r=[self.Win], w=[ck])
            co = cko.get()
            for g in range(2):
                pp = self.pnext()
                for k in range(4):
                    ch = g * 4 + k
                    self.tr_(pp[:, k * 128:(k + 1) * 128], ck[:, 2 * ch:2 * ch + 2, :].rearrange("t a d -> t (a d)"), r=[ck], **wr(pp, k == 0))
                self.cp('act' if g % 2 else 'dve', co[:, g * 4:(g + 1) * 4, :], pp[:, :].rearrange("p (a b) -> p a b", a=4), r=[pp], **wr(co, g == 0))
            self.ld(self.ctxkT.rearrange("c p t -> p c t")[:, :, kt * 128:(kt + 1) * 128], co[:], r=[co], **wr(self.ctx_tk, kt == 0), eng='pool')

    def attn_core(self):
        self.areset()
        NKT = (TS + 256) // 128
        bf = self.bf
        MD = BF16 if bf else F32
        qTs = self.take([128, TS], 2, MD)
        kTs = self.take([128, TS + 256], 2, MD)
        V1s = self.take([128, NKT, 129], 2, MD)
        for V1 in V1s.t:
            self.memset('pool', V1[:, :, 128:129], 1.0, pw=[V1])
        PTs = self.take([128, NKT, 512], 2, MD)
        if bf:
            q32 = self.take([128, TS], 2)
            k32 = self.take([128, TS + 256], 2)
            v32 = self.take([128, NKT, 128], 2)
        obs = self.take([128, 4, 128], 2)
        rvs = self.take([128, 2], 4)
        seqs = [(p * TP, TP, False) for p in range(NP)] + [(NP * TP, TS, True)]
        for (tok0, T, is_s) in seqs:
            nk = T + (256 if is_s else 0)
            nkt = nk // 128
            QB = min(512, T)
            tks = self.prep_tk[tok0 // 128:(tok0 + T) // 128]
            ptk = self.proj_tk[tok0 // 128:(tok0 + T) // 128]
            for h in range(8):
                qT = qTs.get()
                kT = kTs.get()
                V1 = V1s.get()
                if bf:
                    qd, kd, vd = q32.get(), k32.get(), v32.get()
                else:
                    qd, kd, vd = qT, kT, V1
                self.ld(qd[:, 0:T], self.qkT_v[:, h, tok0:tok0 + T], r=tks, w=[qd])
                self.ld(kd[:, 0:T], self.qkT_v[:, 8 + h, tok0:tok0 + T], r=tks, w=[kd], eng='pool')
                self.ld(vd[:, 0:T // 128, 0:128], self.vtok[tok0:tok0 + T, h * 128:(h + 1) * 128].rearrange("(n p) e -> p n e", p=128), r=ptk, **wr(vd, bf))
                if is_s:
                    self.ld(kd[:, T:T + 256], self.ctxkT[h], r=[self.ctx_tk], pw=[kd], eng='pool')
                    self.ld(vd[:, T // 128:nkt, 0:128], self.ctx_v[h].rearrange("(n p) e -> p n e", p=128), r=[self.Win], pw=[vd])
                if bf:
                    self.cp('dve', qT[:, 0:T], qd[:, 0:T], r=[qd], w=[qT])
                    self.cp('act', kT[:, 0:nk], kd[:, 0:nk], r=[kd], w=[kT])
                    self.cp('dve', V1[:, 0:nkt, 0:128], vd[:, 0:nkt, :], r=[vd], pw=[V1])
                for qb in range(T // QB):
                    ob = obs.get()
                    for m in range(2):
                        PT = PTs.get()
                        for kt in range(nkt):
                            pS = self.pnext()
                            self.mm(pS[:, 0:QB], kT[m * 64:(m + 1) * 64, kt * 128:(kt + 1) * 128], qT[m * 64:(m + 1) * 64, qb * QB:(qb + 1) * QB],
                                    True, True, r=[kT, qT], w=[pS])
                            self.act(PT[:, kt, 0:QB], pS[:, 0:QB], AF.Exp, r=[pS], **wr(PT, kt == 0), scale=0.125)
                        for qs in range(QB // 128):
                            pO = self.pnext()
                            for kt in range(nkt):
                                self.mm(pO[:, 0:129], PT[:, kt, qs * 128:(qs + 1) * 128], V1[:, kt, :], kt == 0, kt == nkt - 1, r=[PT, V1], **wr(pO, kt == 0))
                            rv = rvs.get()
                            self.recip(rv[:, 0:1], pO[:, 128:129], r=[pO], w=[rv])
                            if m == 0:
                                self.ts('dve', ob[:, qs, :], pO[:, 0:128], rv[:, 0:1], None, ALU.mult, r=[pO, rv], **wr(ob, qs == 0))
                            else:
                                self.tt('dve', rv[:, 1:2], rv[:, 0:1], self.nlam[:, 3:4], ALU.mult, r=[rv, self.nlam], pw=[rv])
                                self.stt('dve', ob[:, qs, :], pO[:, 0:128], rv[:, 1:2], ob[:, qs, :], ALU.mult, ALU.add, r=[pO, rv, ob], pw=[ob])
                    q0 = tok0 + qb * QB
                    nq = QB // 128
                    htk = self.hdir_tk[0][q0 // 64:(q0 + QB) // 64]
                    self.ld(self.hdir[0][q0:q0 + QB, h * 128:(h + 1) * 128].rearrange("(n p) e -> p n e", p=128), ob[:, 0:nq, :], r=[ob], pw=htk, eng='pool')

    def mixer_post(self, i, w_out, ng_tk, ng_bc, hdir, gate):
        self.areset()
        NT = 512
        if self.bf:
            W = self.take([128, 8, D], None, BF16)
            self.wstage = self.take([128, 8, 512], 2)
            self.load_w16(W, w_out.rearrange("(kc p) n -> p kc n", p=128), D)
        else:
            W = self.take([128, 8, D])
            self.ld(W[:], w_out.rearrange("(kc p) n -> p kc n", p=128), r=[self.Win], w=[W])
        hfs = self.take([128, 8, 128], 4)
        hbs = self.take([128, 8, 128], 4)
        ogs = self.take([128, 8, 128], 4)
        sqs = self.take([128, 8, 128], 3)
        sss = self.take([128, 16], 4)
        yTs = self.take([128, 8, NT], 2, BF16 if self.bf else F32)
        xbs = self.take([128, 8, NT], 2)
        for blk in range(TT // NT):
            c = 0 if blk < (NP * TP) // NT else 1
            tks = self.xT_tk[blk * 4:(blk + 1) * 4]
            xb = xbs.get()
            self.ld(xb[:], self.xT_v[:, :, blk * NT:(blk + 1) * NT], r=tks, w=[xb], eng='pool')
            yT = yTs.get()
            for q in range(4):
                t0 = blk * NT + q * 128
                hf = hfs.get()
                hb = hbs.get()
                og = ogs.get()
                self.ld(hf[:], hdir[0][t0:t0 + 128, :].rearrange("t (h e) -> t h e", h=8), r=self.hdir_tk[0][t0 // 64:t0 // 64 + 2], w=[hf])
                if gate is not None:
                    self.ld(hb[:], hdir[1][t0:t0 + 128, :].rearrange("t (h e) -> t h e", h=8), r=self.hdir_tk[1][t0 // 64:t0 // 64 + 2], w=[hb], eng='pool')
                    self.ld(og[:], self.otok[t0:t0 + 128, :].rearrange("t (h e) -> t h e", h=8), r=[self.proj_tk[t0 // 128]], w=[og])
                    self.tt('dve', hf[:], hf[:], hb[:], ALU.add, r=[hf, hb], w=[hf])
                sq = sqs.get()
                self.tt('pool', sq[:], hf[:], hf[:], ALU.mult, r=[hf], w=[sq])
                ss = sss.get()
                self.S.op('dve', lambda E, ss=ss, sq=sq: E.tensor_reduce(out=ss[:, 0:8], in_=sq[:], axis=AX.X, op=ALU.add), reads=[sq], writes=[ss])
                self.act(ss[:, 0:8], ss[:, 0:8], AF.Sqrt, r=[ss, self.epsb], w=[ss], scale=1.0 / 128, bias=self.epsb[:, 0:1])
                self.recip(ss[:, 8:16], ss[:, 0:8], r=[ss], pw=[ss])
                ngb = ng_bc.unsqueeze(1).to_broadcast([128, 8, 128])
                if gate == 'sigmoid':
                    self.act(og[:], og[:], AF.Sigmoid, r=[og], w=[og])
                    self.tt('pool', og[:], og[:], ngb, ALU.mult, r=[og, ng_tk], w=[og])
                elif gate == 'silu':
                    self.act(og[:], og[:], AF.Silu, r=[og], w=[og])
                    self.tt('pool', og[:], og[:], ngb, ALU.mult, r=[og, ng_tk], w=[og])
                else:
                    self.cp('pool', og[:], ngb, r=[ng_tk], w=[og])
                self.tt('dve', hf[:], hf[:], ss[:, 8:16].unsqueeze(2).to_broadcast([128, 8, 128]), ALU.mult, r=[hf, ss], w=[hf])
                self.tt('dve', hf[:], hf[:], og[:], ALU.mult, r=[hf, og], w=[hf])
                for hh in range(2):
                    p = self.pnext()
                    for k in range(4):
                        self.tr_(p[:, k * 128:(k + 1) * 128], hf[:, hh * 4 + k, :], r=[hf], **wr(p, k == 0))
                    dst = yT[:, hh * 4:(hh + 1) * 4, q * 128:(q + 1) * 128]
                    src = p[:, :].rearrange("p (a b) -> p a b", a=4)
                    self.cp('act' if hh else 'dve', dst, src, r=[p], **wr(yT, q == 0 and hh == 0))
            for oc in range(8):
                p = self.pnext()
                for kc in range(8):
                    self.mm(p[:, :], W[:, kc, oc * 128:(oc + 1) * 128], yT[:, kc, :], kc == 0, kc == 7, r=[W, yT], **wr(p, kc == 0))
                self.stt('dve', xb[:, oc, :], p[:, :], self.mod[:, 16 + oc, c:c + 1], xb[:, oc, :], ALU.mult, ALU.add,
                         r=[p, self.mod, xb], pw=[xb])
            for q in range(4):
                self.ld(self.xT_v[:, :, blk * NT + q * 128: blk * NT + (q + 1) * 128], xb[:, :, q * 128:(q + 1) * 128],
                        r=[xb], w=[tks[q]], eng='pool')

    def tr_(self, out, in_, r=(), w=(), pw=()):
        n = in_.shape[0]
        idn = self.cst.ap[0:n, 0, 0:n]
        self.S.op('pe', lambda E: E.transpose(out, in_, idn), reads=list(r) + [self.cst], writes=w, pw=pw)

    def stage_in(self):
        self.areset()
        xin = self.take([128, D], 4)
        xo = self.take([128, 8, 128], 4)
        for t in range(TT // 128):
            a = xin.get()
            self.ld(a[:], self.x_tok[t * 128:(t + 1) * 128, :], r=[self.Xtok], w=[a])
            b = xo.get()
            for h in range(2):
                p = self.pnext()
                for k in range(4):
                    kc = h * 4 + k
                    self.tr_(p[:, k * 128:(k + 1) * 128], a[:, kc * 128:(kc + 1) * 128], r=[a], w=[p] if k == 0 else (), pw=() if k == 0 else [p])
                dst = b[:, h * 4:(h + 1) * 4, :]
                src = p[:, :].rearrange("p (a b) -> p a b", a=4)
                if h == 0:
                    self.cp('dve', dst, src, r=[p], w=[b])
                else:
                    self.cp('act', dst, src, r=[p], pw=[b])
            self.ld(self.xT_v[:, :, t * 128:(t + 1) * 128], b[:], r=[b], w=[self.xT_tk[t]], eng='pool')

    def stage_out(self):
        self.areset()
        xi = self.take([128, 8, 128], 4)
        yo = self.take([128, D], 4)
        for t in range(TT // 128):
            a = xi.get()
            self.ld(a[:], self.xT_v[:, :, t * 128:(t + 1) * 128], r=[self.xT_tk[t]], w=[a])
            b = yo.get()
            for h in range(2):
                p = self.pnext()
                for k in range(4):
                    kc = h * 4 + k
                    self.tr_(p[:, k * 128:(k + 1) * 128], a[:, kc, :], r=[a], w=[p] if k == 0 else (), pw=() if k == 0 else [p])
                if h == 0:
                    self.cp('dve', b[:, 0:512], p[:, :], r=[p], w=[b])
                else:
                    self.cp('act', b[:, 512:1024], p[:, :], r=[p], pw=[b])
            self.ld(self.y_tok[t * 128:(t + 1) * 128, :], b[:], r=[b], w=[self.Ytok], eng='pool')

    def stage_mod(self, i):
        self.areset()
        wt = self.take([128, 8, 512], 2)
        wv = self.ada_w[i].rearrange("(kc p) n -> p kc n", p=128)
        mp = self.pnext()
        for n in range(12):
            w = wt.get()
            self.ld(w[:], wv[:, :, n * 512:(n + 1) * 512], r=[self.Win], w=[w])
            for jj in range(4):
                j = n * 4 + jj
                for kc in range(8):
                    self.mm(mp[:, 2 * j:2 * j + 2], w[:, kc, jj * 128:(jj + 1) * 128], self.sc[:, kc, :], kc == 0, kc == 7,
                            r=[w, self.sc], **wr(mp, j == 0 and kc == 0))
        mpv = mp[:, 0:96].rearrange("p (j c) -> p j c", c=2)
        for c in range(2):
            self.tt('dve', self.mod[:, :, c], mpv[:, :, c], self.adab[:, i, :], ALU.add, r=[mp, self.adab],
                    w=[self.mod] if c == 0 else (), pw=() if c == 0 else [self.mod])
        for wi in range(2):
            sj = 8 + 24 * wi
            for c in range(2):
                first = (wi == 0 and c == 0)
                self.stt('dve', self.modA[:, wi, :, c], self.mod[:, sj:sj + 8, c], 1.0, self.normg[:, i, wi, :], ALU.add, ALU.mult,
                         r=[self.mod, self.normg], w=[self.modA] if first else (), pw=() if first else [self.modA])

    def norm_mod(self, xb, hb, wi, c, nt, sq=None):
        sj = 24 * wi
        self.act(hb[:, :, :], xb[:, :, :], AF.Square, r=[xb], w=[hb])
        p = self.pnext()
        for kc in range(8):
            self.mm(p[:, 0:nt], self.cst.ap[:, 1, :], hb[:, kc, :], kc == 0, kc == 7, r=[self.cst, hb], **wr(p, kc == 0))
        rs = self.rstd.get()
        self.act(rs[:, 0:nt], p[:, 0:nt], AF.Sqrt, r=[p, self.epsb], w=[rs], scale=1.0 / D, bias=self.epsb[:, 0:1])
        self.recip(rs[:, 0:nt], rs[:, 0:nt], r=[rs], w=[rs])
        for kc in range(8):
            self.tt('dve' if kc % 2 == 0 else 'pool', hb[:, kc, :], xb[:, kc, :], rs[:, 0:nt], ALU.mult, r=[xb, rs],
                    w=[hb] if kc == 0 else (), pw=() if kc == 0 else [hb])
        for kc in range(8):
            self.act(hb[:, kc, :], hb[:, kc, :], AF.Identity, r=[hb, self.modA, self.mod], pw=[hb],
                     scale=self.modA[:, wi, kc, c:c + 1], bias=self.mod[:, sj + kc, c:c + 1])

    def norm_mod2(self, xtk, xap, htk, hap, tmp, wi, c, nt, first):
        sj = 24 * wi
        self.act(tmp[:, :, 0:nt], xap, AF.Square, r=[xtk], w=[tmp])
        p = self.pnext()
        for kc in range(8):
            self.mm(p[:, 0:nt], self.cst.ap[:, 1, :], tmp[:, kc, 0:nt], kc == 0, kc == 7, r=[self.cst, tmp], **wr(p, kc == 0))
        rs = self.rstd.get()
        self.act(rs[:, 0:nt], p[:, 0:nt], AF.Sqrt, r=[p, self.epsb], w=[rs], scale=1.0 / D, bias=self.epsb[:, 0:1])
        self.recip(rs[:, 0:nt], rs[:, 0:nt], r=[rs], w=[rs])
        for kc in range(8):
            self.tt('dve' if kc % 2 == 0 else 'pool', tmp[:, kc, 0:nt], xap[:, kc, :], rs[:, 0:nt], ALU.mult, r=[xtk, rs], **wr(tmp, kc == 0))
        for kc in range(8):
            self.act(hap[:, kc, :], tmp[:, kc, 0:nt], AF.Identity, r=[tmp, self.modA, self.mod], **wr(htk, first and kc == 0),
                     scale=self.modA[:, wi, kc, c:c + 1], bias=self.mod[:, sj + kc, c:c + 1])

    def stage_ffn16(self, i):
        self.areset()
        SB = 1024
        NH = SB // 512
        xbs = self.take([128, 8, SB], 1)
        hbs = self.take([128, 8, SB], 1, BF16)
        sq = self.take([128, 8, 512])
        self.rstd = self.take([128, 512], 2)
        acts = self.take([128, 22, SB], 1, BF16)
        wst = self.take([128, 8, 2, 128], 3)
        w16 = self.take([128, 8, 2, 128], 3, BF16)
        wost = self.take([128, 22, 128], 2)
        wo16 = self.take([128, 22, 128], 2, BF16)
        sg = self.take([128, 512], 2)
        wiv = self.ffn_w_in[i].rearrange("(kc p) n -> p kc n", p=128)
        wov = self.ffn_w_out[i].rearrange("(kc p) n -> p kc n", p=128)
        for sb in range(TT // SB):
            c = 0 if sb * SB < NP * TP else 1
            tks = self.xT_tk[sb * 8:(sb + 1) * 8]
            xb = xbs.get()
            self.ld(xb[:], self.xT_v[:, :, sb * SB:(sb + 1) * SB], r=tks, w=[xb], eng='pool')
            hb = hbs.get()
            for hf in range(NH):
                hs = slice(hf * 512, (hf + 1) * 512)
                self.norm_mod2(xb, xb[:, :, hs], hb, hb[:, :, hs], sq, 1, c, 512, hf == 0)
            at = acts.get()
            for j in range(22):
                ws = wst.get()
                self.ld(ws[:, :, 0, :], wiv[:, :, j * 128:(j + 1) * 128], r=[self.Win], w=[ws])
                self.ld(ws[:, :, 1, :], wiv[:, :, DFF + j * 128:DFF + (j + 1) * 128], r=[self.Win], pw=[ws])
                w = w16.get()
                self.cp('dve' if j % 2 else 'act', w[:], ws[:], r=[ws], w=[w])
                for hf in range(NH):
                    hs = slice(hf * 512, (hf + 1) * 512)
                    pg = self.pnext()
                    pu = self.pnext()
                    for kc in range(8):
                        self.mm(pg[:, :], w[:, kc, 0, :], hb[:, kc, hs], kc == 0, kc == 7, r=[w, hb], **wr(pg, kc == 0))
                    for kc in range(8):
                        self.mm(pu[:, :], w[:, kc, 1, :], hb[:, kc, hs], kc == 0, kc == 7, r=[w, hb], **wr(pu, kc == 0))
                    s_ = sg.get()
                    self.act(s_[:], pg[:, :], AF.Silu, r=[pg], w=[s_])
                    self.tt('dve', at[:, j, hs], s_[:], pu[:, :], ALU.mult, r=[s_, pu], **wr(at, j == 0 and hf == 0))
            for oc in range(8):
                ws = wost.get()
                self.ld(ws[:], wov[:, :, oc * 128:(oc + 1) * 128], r=[self.Win], w=[ws])
                w = wo16.get()
                self.cp('dve' if oc % 2 else 'act', w[:], ws[:], r=[ws], w=[w])
                for hf in range(NH):
                    hs = slice(hf * 512, (hf + 1) * 512)
                    p = self.pnext()
                    for k2 in range(22):
                        self.mm(p[:, :], w[:, k2, :], at[:, k2, hs], k2 == 0, k2 == 21, r=[w, at], **wr(p, k2 == 0))
                    self.stt('dve', xb[:, oc, hs], p[:, :], self.mod[:, 40 + oc, c:c + 1], xb[:, oc, hs], ALU.mult, ALU.add,
                             r=[p, self.mod, xb], pw=[xb])
            for q in range(SB // 128):
                self.ld(self.xT_v[:, :, sb * SB + q * 128: sb * SB + (q + 1) * 128], xb[:, :, q * 128:(q + 1) * 128],
                        r=[xb], w=[tks[q]], eng='pool')

    def stage_ffn(self, i):
        self.areset()
        NT = 512
        xbs = self.take([128, 8, NT], 2)
        hbs = self.take([128, 8, NT], 1)
        self.rstd = self.take([128, NT], 2)
        acts = self.take([128, 22, NT], 1)
        sg = self.take([128, NT], 2)
        wins = self.take([128, 8, 2, 128], 3)
        wouts = self.take([128, 22, 128], 2)
        wiv = self.ffn_w_in[i].rearrange("(kc p) n -> p kc n", p=128)
        wov = self.ffn_w_out[i].rearrange("(kc p) n -> p kc n", p=128)
        for blk in range(TT // NT):
            c = 0 if blk < (NP * TP) // NT else 1
            tks = self.xT_tk[blk * 4:(blk + 1) * 4]
            xb = xbs.get()
            self.ld(xb[:], self.xT_v[:, :, blk * NT:(blk + 1) * NT], r=tks, w=[xb], eng='pool')
            hb = hbs.get()
            self.norm_mod(xb, hb, 1, c, NT)
            at = acts.get()
            for j in range(22):
                w = wins.get()
                self.ld(w[:, :, 0, :], wiv[:, :, j * 128:(j + 1) * 128], r=[self.Win], w=[w])
                self.ld(w[:, :, 1, :], wiv[:, :, DFF + j * 128:DFF + (j + 1) * 128], r=[self.Win], pw=[w])
                pg = self.pnext()
                pu = self.pnext()
                for kc in range(8):
                    self.mm(pg[:, :], w[:, kc, 0, :], hb[:, kc, :], kc == 0, kc == 7, r=[w, hb], **wr(pg, kc == 0))
                for kc in range(8):
                    self.mm(pu[:, :], w[:, kc, 1, :], hb[:, kc, :], kc == 0, kc == 7, r=[w, hb], **wr(pu, kc == 0))
                s = sg.get()
                self.act(s[:], pg[:, :], AF.Silu, r=[pg], w=[s])
                self.tt('dve', at[:, j, :], s[:], pu[:, :], ALU.mult, r=[s, pu], w=[at] if j == 0 else (), pw=() if j == 0 else [at])
            for oc in range(8):
                w = wouts.get()
                self.ld(w[:], wov[:, :, oc * 128:(oc + 1) * 128], r=[self.Win], w=[w])
                p = self.pnext()
                for k2 in range(22):
                    self.mm(p[:, :], w[:, k2, :], at[:, k2, :], k2 == 0, k2 == 21, r=[w, at], **wr(p, k2 == 0))
                self.stt('dve', xb[:, oc, :], p[:, :], self.mod[:, 40 + oc, c:c + 1], xb[:, oc, :], ALU.mult, ALU.add,
                         r=[p, self.mod, xb], pw=[xb])
            for q in range(4):
                self.ld(self.xT_v[:, :, blk * NT + q * 128: blk * NT + (q + 1) * 128], xb[:, :, q * 128:(q + 1) * 128],
                        r=[xb], w=[tks[q]], eng='pool')


def host_consts():
    c = np.zeros((128, 8, 128), np.float32)
    c[:, 0, :] = np.eye(128)
    c[:, 1, :] = 1.0
    k = np.arange(128)[:, None]
    t = np.arange(128)[None, :]
    c[:, 2, :] = (k <= t)
    c[:, 3, :] = (k >= t)
    c[:, 4, :] = (k > t)
    c[:, 5, :] = (k < t)
    return c.reshape(128, 1024)


def rope_tables():
    rows = TS // 64
    row = np.broadcast_to(np.arange(rows)[:, None], (rows, 64)).reshape(-1)
    col = np.broadcast_to(np.arange(64)[None, :], (rows, 64)).reshape(-1)
    inv = (np.float32(10000.0) ** (-np.arange(16, dtype=np.float32) / np.float32(16))).astype(np.float32)
    ang = np.stack([row, col], axis=-1).astype(np.float32)[:, :, None] * inv
    return np.concatenate([np.cos(ang).reshape(TS, 32), np.sin(ang).reshape(TS, 32)], axis=1).astype(np.float32)


_CACHE = {}


def kernel(**inp):
    opts = inp.pop('_opts', {})
    key = repr(sorted(opts.items()))
    if key not in _CACHE:
        P = Prog(opts)
        P.build()
        _CACHE[key] = P
    P = _CACHE[key]
    f = lambda a: np.ascontiguousarray(np.asarray(a, dtype=np.float32))
    xp = f(inp['x_prompt'])
    xs = f(inp['x_sample'])
    c = f(inp['c'])
    c_ctx = f(inp['c_ctx'])
    ada_b = f(inp['ada_b'])
    norm_g = f(inp['norm_g'])
    shared = {
        'consts': host_consts(),
        'ada_w': f(inp['ada_w']),
        'ada_bT': f(ada_b.reshape(4, 48, 128).transpose(2, 0, 1)),
        'normgT': f(norm_g.reshape(4, 2, 8, 128).transpose(3, 0, 1, 2)),
        'ffn_w_in': f(inp['ffn_w_in']),
        'ffn_w_out': f(inp['ffn_w_out']),
        'mlstm_w_in': f(inp['mlstm_w_in'][0]),
        'mlstm_w_out': f(inp['mlstm_w_out'][0]),
        'mlstm_gate_b': f(inp['mlstm_gate_b'].reshape(1, 32)),
        'mlstm_norm_g': f(inp['mlstm_norm_g'].reshape(1, 128)),
    }
    shared.update({
        'diff_w_in': f(inp['diff_w_in'][0]),
        'diff_w_out': f(inp['diff_w_out'][0]),
        'diff_qkg': f(np.concatenate([inp['diff_q_norm_g'][0], inp['diff_k_norm_g'][0]]).reshape(1, 128)),
        'diff_lambda': f(inp['diff_lambda'][0].reshape(1, 256)),
        'diff_subln_g': f(inp['diff_subln_g'][0].reshape(1, 128)),
        'rope_cs': rope_tables(),
    })
    cw = f(inp['gdn_conv_w'])
    shared.update({
        'gdn_w_in': f(inp['gdn_w_in']),
        'gdn_w_out': f(inp['gdn_w_out']),
        'gdn_convT': f(cw.reshape(2, 5, 24, 128).transpose(3, 0, 2, 1)),
        'gdn_a_log': f(inp['gdn_a_log'].reshape(2, 1, 16)),
        'gdn_dt_bias': f(inp['gdn_dt_bias'].reshape(2, 1, 16)),
        'gdn_norm_g': f(inp['gdn_norm_g'].reshape(2, 1, 128)),
    })
    stS = f(inp['state_delta'])
    ck = f(inp['cache_diff_k'])
    cv = f(inp['cache_diff_v'])
    stC = f(inp['state_mlstm_C'])
    stn = f(inp['state_mlstm_n'])
    stm = f(inp['state_mlstm_m'])
    in_maps = []
    for k in range(NCORE):
        m = dict(shared)
        m['x_tok'] = f(np.concatenate([xp[NP * k:NP * (k + 1)].reshape(NP * TP, D), xs[k]], axis=0))
        cond = np.stack([c_ctx, c[k]], axis=-1)
        m['condT'] = f(cond.reshape(8, 128, 2).transpose(1, 0, 2))
        m['st_S'] = f(stS[k])
        m['ctx_k'] = f(ck[k, 0])
        m['ctx_v'] = f(cv[k, 0])
        m['st_C'] = f(stC[k, 0])
        m['st_n'] = f(stn[k, 0].transpose(0, 2, 1))
        m['st_m'] = f(stm[k, 0].reshape(1, 16))
        in_maps.append({n: m[n] for n in P.din})
    res = run_bass_kernel_spmd(P.nc, in_maps, core_ids=list(range(NCORE)))
    R = res.results
    y = np.stack([r['y_tok'] for r in R])
    y_prompt = y[:, :NP * TP].reshape(NCORE * NP, TP, D)
    y_sample = y[:, NP * TP:]
    outs = [y_prompt, y_sample]
    if 'newS' in P.dout:
        outs.append(np.stack([r['newS'] for r in R]).reshape(NCORE * NP, 2, 2, 8, 128, 128))
    if 'newC' in P.dout:
        outs.append(np.stack([r['newC'] for r in R]).reshape(NCORE * NP, 1, 2, 8, 64, 128))
        outs.append(np.ascontiguousarray(np.stack([r['newn'] for r in R]).reshape(NCORE * NP, 1, 2, 64, 8).transpose(0, 1, 2, 4, 3)))
        outs.append(np.stack([r['newm'] for r in R]).reshape(NCORE * NP, 1, 2, 8))
    if 'newk' in P.dout:
        outs.append(np.stack([r['newk'] for r in R]).reshape(NCORE * NP, 1, 8, 2, TP, 64))
        outs.append(np.stack([r['newv'] for r in R]).reshape(NCORE * NP, 1, 8, TP, 128))
    return tuple(outs)
```

```python
import numpy as np
from contextlib import ExitStack
import concourse.bass as bass
import concourse.mybir as mybir
from concourse.bass_utils import run_bass_kernel_spmd

F32 = mybir.dt.float32
BF16 = mybir.dt.bfloat16
AF = mybir.ActivationFunctionType
ALU = mybir.AluOpType
AX = mybir.AxisListType

ENGS = ('pe', 'act', 'dve', 'pool', 'sp')
NDS = 40

D = 1024
NCORE = 8
NP = 4
TP = 256
TS = 2048
TT = NP * TP + TS
DFF = 2816
EPS = 1e-6


class Tk:
    __slots__ = ('ap', 'lw', 'rd', 'rp', 'name')

    def __init__(self, ap, name=''):
        self.ap = ap
        self.lw = {}
        self.rd = {}
        self.rp = {}
        self.name = name

    def __getitem__(self, idx):
        return self.ap[idx]


class Sched:
    def __init__(self, nc, es):
        self.nc = nc
        self.es = es
        self.q = {e: [] for e in ENGS}
        self.sem = {e: es.enter_context(nc.semaphore("s_" + e)) for e in ENGS}
        self.cnt = {e: 0 for e in ENGS}
        self.seen = {e: {} for e in ENGS}
        self.dsem = [es.enter_context(nc.semaphore("d%d" % i)) for i in range(NDS)]
        self.dcnt = [0] * NDS
        self.dnext = 0
        self.nins = 0
        self.uid = 0

    def sb(self, shape, dt=F32, name=None):
        self.uid += 1
        name = name or "t%d" % self.uid
        t = self.es.enter_context(self.nc.sbuf_tensor(name, list(shape), dt))
        return Tk(t, name)

    def ps(self, shape, dt=F32, name=None):
        self.uid += 1
        name = name or "p%d" % self.uid
        t = self.es.enter_context(self.nc.psum_tensor(name, list(shape), dt))
        return Tk(t, name)

    def _wait(self, eng, d):
        k = d[0]
        if eng == 'pe' and k == ('e', 'pe'):
            return
        seen = self.seen[eng]
        if seen.get(k, 0) >= d[2]:
            return
        seen[k] = d[2]
        self.q[eng].append(lambda E, d=d: E.wait_ge(d[1], d[2]))
        self.nins += 1

    def _deps(self, eng, reads, writes, pw):
        deps = {}

        def add(d):
            k = d[0]
            if k not in deps or deps[k][2] < d[2]:
                deps[k] = d
        for t in reads:
            for d in t.lw.values():
                add(d)
        for t in writes:
            for d in t.lw.values():
                add(d)
            for d in t.rd.values():
                add(d)
        for t in pw:
            for d in t.rd.values():
                add(d)
            for d in t.rp.values():
                add(d)
        for d in deps.values():
            self._wait(eng, d)

    def _mark(self, me, reads, writes, pw):
        for t in reads:
            t.rd[me[0]] = me
        for t in writes:
            t.lw = {me[0]: me}
            t.rp = t.rd
            t.rd = {}
        for t in pw:
            t.lw[me[0]] = me

    def op(self, eng, fn, reads=(), writes=(), pw=()):
        self._deps(eng, reads, writes, pw)
        self.cnt[eng] += 1
        sem = self.sem[eng]
        me = (('e', eng), sem, self.cnt[eng])
        self.q[eng].append(lambda E: fn(E).then_inc(sem, 1))
        self.nins += 1
        self._mark(me, reads, writes, pw)

    def dma(self, eng, out_ap, in_ap, reads=(), writes=(), pw=()):
        slot = self.dnext
        self.dnext = (slot + 1) % NDS
        ds = self.dsem[slot]
        self._deps(eng, reads, writes, pw)
        if self.dcnt[slot] > 0:
            self._wait(eng, (('d', slot), ds, 16 * self.dcnt[slot]))
        self.dcnt[slot] += 1
        me = (('d', slot), ds, 16 * self.dcnt[slot])
        self.q[eng].append(lambda E: E.dma_start(out=out_ap, in_=in_ap).then_inc(ds, 16))
        self.nins += 1
        self._mark(me, reads, writes, pw)

    def barrier(self):
        for e in ENGS:
            for o in ENGS:
                if o != e and self.cnt[o] > 0:
                    self._wait(e, (('e', o), self.sem[o], self.cnt[o]))
            for i in range(NDS):
                if self.dcnt[i] > 0:
                    self._wait(e, (('d', i), self.dsem[i], 16 * self.dcnt[i]))

    def emit(self):
        self.barrier()
        q = self.q
        with self.nc.Block() as block:
            @block.tensor
            def _(E):
                for f in q['pe']:
                    f(E)

            @block.scalar
            def _(E):
                for f in q['act']:
                    f(E)

            @block.vector
            def _(E):
                for f in q['dve']:
                    f(E)

            @block.gpsimd
            def _(E):
                for f in q['pool']:
                    f(E)

            @block.sync
            def _(E):
                for f in q['sp']:
                    f(E)


def wr(t, first):
    return {'w': [t]} if first else {'pw': [t]}


class Rot:
    def __init__(self, tiles):
        self.t = tiles
        self.i = 0

    def get(self):
        t = self.t[self.i % len(self.t)]
        self.i += 1
        return t


ARENA_COLS = 50000


class Prog:
    def __init__(self, opts):
        self.opts = opts
        self.nc = bass.Bass("TRN2", target_bir_lowering=False)
        self.es = ExitStack()
        self.din = {}
        self.dout = {}

    def inp(self, name, shape):
        t = self.nc.dram_tensor(name, list(shape), F32, kind="ExternalInput").ap()
        self.din[name] = t
        return t

    def outp(self, name, shape):
        t = self.nc.dram_tensor(name, list(shape), F32, kind="ExternalOutput").ap()
        self.dout[name] = t
        return t

    def scratch(self, name, shape):
        return self.nc.dram_tensor(name, list(shape), F32, kind="Internal").ap()

    def areset(self):
        import inspect
        self.S.barrier()
        self.apos = 0
        if not hasattr(self, 'stages'):
            self.stages = []
        self.stages.append((inspect.stack()[1].function, self.S.cnt['pe']))
        if hasattr(self, 'mark'):
            mk = self.mark
            self.act(mk[:, 0:1], mk[:, 0:1], AF.Sign, r=[mk], w=[mk])

    def take(self, shape, n=None, dt=F32):
        cols = int(np.prod(shape[1:]))
        c32 = cols if dt == F32 else (cols + 1) // 2
        out = []
        for _ in range(n or 1):
            assert self.apos + c32 <= ARENA_COLS, ("arena overflow", self.apos, c32)
            ap = self.arena[0:shape[0], self.apos:self.apos + c32]
            if dt != F32:
                ap = ap.bitcast(dt)[:, 0:cols]
            if len(shape) == 3:
                ap = ap.rearrange("p (a b) -> p a b", a=shape[1])
            elif len(shape) == 4:
                ap = ap.rearrange("p (a b c) -> p a b c", a=shape[1], b=shape[2])
            self.apos += c32
            out.append(Tk(ap))
        return out[0] if n is None else Rot(out)

    def pnext(self):
        p = self.psum[self.pi % 8]
        self.pi += 1
        return p

    def pns(self):
        return self.pnext()

    def mm(self, out, lhsT, rhs, start, stop, r=(), w=(), pw=()):
        self.S.op('pe', lambda E: E.matmul(out, lhsT=lhsT, rhs=rhs, start=start, stop=stop), reads=r, writes=w, pw=pw)

    def tr(self, out, in_, r=(), w=(), pw=()):
        ident = self.ident
        n = in_.shape[0]
        self.S.op('pe', lambda E: E.transpose(out, in_, ident[0:n, 0:n]), reads=list(r) + [ident], writes=w, pw=pw)

    def act(self, out, in_, func, r=(), w=(), pw=(), bias=None, scale=None, accum=None):
        kw = {}
        if bias is not None:
            kw['bias'] = bias
        if scale is not None:
            kw['scale'] = scale
        if accum is not None:
            kw['accum_out'] = accum
        self.S.op('act', lambda E: E.activation(out=out, in_=in_, func=func, **kw), reads=r, writes=w, pw=pw)

    def tt(self, eng, out, a, b, op, r=(), w=(), pw=()):
        self.S.op(eng, lambda E: E.tensor_tensor(out=out, in0=a, in1=b, op=op), reads=r, writes=w, pw=pw)

    def ts(self, eng, out, a, s1, s2, op0, op1=None, r=(), w=(), pw=()):
        if op1 is None:
            self.S.op(eng, lambda E: E.tensor_scalar(out=out, in0=a, scalar1=s1, scalar2=None, op0=op0), reads=r, writes=w, pw=pw)
        else:
            self.S.op(eng, lambda E: E.tensor_scalar(out=out, in0=a, scalar1=s1, scalar2=s2, op0=op0, op1=op1), reads=r, writes=w, pw=pw)

    def stt(self, eng, out, a, s, b, op0, op1, r=(), w=(), pw=()):
        self.S.op(eng, lambda E: E.scalar_tensor_tensor(out=out, in0=a, scalar=s, in1=b, op0=op0, op1=op1), reads=r, writes=w, pw=pw)

    def cp(self, eng, out, in_, r=(), w=(), pw=()):
        if eng == 'act':
            self.S.op('act', lambda E: E.copy(out=out, in_=in_), reads=r, writes=w, pw=pw)
        else:
            self.S.op(eng, lambda E: E.tensor_copy(out=out, in_=in_), reads=r, writes=w, pw=pw)

    def recip(self, out, in_, r=(), w=(), pw=()):
        self.S.op('dve', lambda E: E.reciprocal(out=out, in_=in_), reads=r, writes=w, pw=pw)

    def memset(self, eng, ap, val, w=(), pw=()):
        self.S.op(eng, lambda E: E.memset(ap, val), writes=w, pw=pw)

    def ld(self, out, in_, r=(), w=(), pw=(), eng='sp'):
        self.S.dma(eng, out, in_, reads=r, writes=w, pw=pw)

    def build(self):
        nc = self.nc
        o = self.opts
        with self.es:
            S = self.S = Sched(nc, self.es)
            self.arena = self.es.enter_context(nc.sbuf_tensor("arena", [128, ARENA_COLS], F32))
            self.psum = [S.ps([128, 512], name="ps%d" % i) for i in range(8)]
            self.pi = 0
            self.apos = 0
            self.psmall = [Tk(self.psum[i // 2].ap[:, (i % 2) * 256:(i % 2) * 256 + 256]) for i in range(16)]
            self.psi = 0
            self.x_tok = self.inp("x_tok", [TT, D])
            self.y_tok = self.outp("y_tok", [TT, D])
            self.xT = self.scratch("xT", [8, 128, TT])
            self.xT_v = self.xT.rearrange("c p t -> p c t")
            self.xT_tk = [Tk(None, "xT%d" % i) for i in range(TT // 128)]
            self.Xtok = Tk(None)
            self.Ytok = Tk(None)
            self.Win = Tk(None)
            consts = self.inp("consts", [128, 8 * 128])
            condT = self.inp("condT", [128, 8, 2])
            self.ada_w = self.inp("ada_w", [4, D, 6 * D])
            ada_bT = self.inp("ada_bT", [128, 4, 48])
            normgT = self.inp("normgT", [128, 4, 2, 8])
            self.ffn_w_in = self.inp("ffn_w_in", [4, D, 2 * DFF])
            self.ffn_w_out = self.inp("ffn_w_out", [4, DFF, D])
            self.cst = S.sb([128, 8, 128], name="cst")
            self.ld(self.cst[:], consts.rearrange("p (a b) -> p a b", a=8), r=[self.Win], w=[self.cst])
            self.ones = self.cst.ap[:, 1, :]
            self.sc = S.sb([128, 8, 2], name="sc")
            self.ld(self.sc[:], condT, r=[self.Win], w=[self.sc])
            self.act(self.sc[:], self.sc[:], AF.Silu, r=[self.sc], w=[self.sc])
            self.adab = S.sb([128, 4, 48], name="adab")
            self.ld(self.adab[:], ada_bT, r=[self.Win], w=[self.adab])
            self.normg = S.sb([128, 4, 2, 8], name="normg")
            self.ld(self.normg[:], normgT, r=[self.Win], w=[self.normg])
            self.mod = S.sb([128, 48, 2], name="mod")
            self.modA = S.sb([128, 2, 8, 2], name="modA")
            self.epsb = S.sb([128, 1], name="epsb")
            self.memset('pool', self.epsb[:], EPS, w=[self.epsb])
            self.mark = S.sb([128, 1], name="mark")
            self.memset('pool', self.mark[:], 1.0, w=[self.mark])

            self.stage_in()
            self.bf = o.get('bf16', True)
            mixers = o.get('mixers', (0, 1, 2))
            self.setup_mixers(mixers)
            for i in range(o.get('depth', 4)):
                self.stage_mod(i)
                if i % 3 == 0 and 0 in mixers:
                    self.gdn(i)
                if i % 3 == 1 and 1 in mixers:
                    self.mlstm(i)
                if i % 3 == 2 and 2 in mixers:
                    self.diffattn(i)
                if o.get('ffn', True):
                    if self.bf:
                        self.stage_ffn16(i)
                    else:
                        self.stage_ffn(i)
            self.stage_out()
            S.emit()
        return nc

    def setup_mixers(self, mixers):
        S = self.S
        if 1 in mixers:
            self.ml_w_in = self.inp("mlstm_w_in", [D, 3104])
            self.ml_w_out = self.inp("mlstm_w_out", [D, D])
            ml_gb = self.inp("mlstm_gate_b", [1, 32])
            ml_ng = self.inp("mlstm_norm_g", [1, 128])
            self.st_C = self.inp("st_C", [2, 8, 64, 128])
            self.st_n = self.inp("st_n", [2, 64, 8])
            self.st_m = self.inp("st_m", [1, 16])
            self.newC = self.outp("newC", [NP, 2, 8, 64, 128])
            self.newn = self.outp("newn", [NP, 2, 64, 8])
            self.newm = self.outp("newm", [NP, 2, 8, 1])
            self.ml_gb = S.sb([128, 32], name="ml_gb")
            self.ld(self.ml_gb[:], ml_gb.partition_broadcast(128), r=[self.Win], w=[self.ml_gb])
            self.ml_ng = S.sb([128, 128], name="ml_ng")
            self.ld(self.ml_ng[:], ml_ng.partition_broadcast(128), r=[self.Win], w=[self.ml_ng])
        self.Oout = Tk(None)
        self.qkT = self.scratch("qkT", [24, 128, TT])
        self.qkT_v = self.qkT.rearrange("c p t -> p c t")
        self.qkT_h = self.qkT[0:8].rearrange("c (two p) t -> p (c two) t", two=2)
        self.ktok = self.scratch("ktok", [TT, 2048])
        self.vtok = self.scratch("vtok", [TT, 1024])
        self.otok = self.scratch("otok", [TT, 1024])
        self.gtok = self.scratch("gtok", [TT, 32])
        self.hdir = [self.scratch("hdir%d" % d, [TT, 1024]) for d in range(2)]
        self.proj_tk = [Tk(None) for _ in range(TT // 128)]
        self.prep_tk = [Tk(None) for _ in range(TT // 128)]
        self.hdir_tk = [[Tk(None) for _ in range(TT // 64)] for d in range(2)]
        if 2 in mixers:
            self.df_w_in = self.inp("diff_w_in", [D, 3072])
            self.df_w_out = self.inp("diff_w_out", [D, D])
            df_g = self.inp("diff_qkg", [1, 128])
            df_lam = self.inp("diff_lambda", [1, 256])
            df_sg = self.inp("diff_subln_g", [1, 128])
            self.rope_cs = self.inp("rope_cs", [TS, 64])
            self.ctx_k = self.inp("ctx_k", [8, 2, 256, 64])
            self.ctx_v = self.inp("ctx_v", [8, 256, 128])
            self.newk = self.outp("newk", [NP, 8, 2, TP, 64])
            self.newv = self.outp("newv", [NP, 8, TP, 128])
            self.df_g = S.sb([128, 2, 64], name="df_g")
            self.ld(self.df_g[:], df_g.rearrange("o (a b) -> o a b", a=2).partition_broadcast(128), r=[self.Win], w=[self.df_g])
            self.df_sg = S.sb([128, 128], name="df_sg")
            self.ld(self.df_sg[:], df_sg.partition_broadcast(128), r=[self.Win], w=[self.df_sg])
            lam_init = 0.8 - 0.6 * float(np.exp(-0.3 * 2))
            self.ts('dve', self.df_sg[:], self.df_sg[:], 1.0 - lam_init, None, ALU.mult, r=[self.df_sg], w=[self.df_sg])
            lm = S.sb([128, 4, 64], name="df_lm")
            self.ld(lm[:], df_lam.rearrange("o (a b) -> o a b", a=4).partition_broadcast(128), r=[self.Win], w=[lm])
            l2 = S.sb([128, 2, 64], name="df_l2")
            self.tt('dve', l2[:, 0, :], lm[:, 0, :], lm[:, 1, :], ALU.mult, r=[lm], w=[l2])
            self.tt('dve', l2[:, 1, :], lm[:, 2, :], lm[:, 3, :], ALU.mult, r=[lm], pw=[l2])
            self.nlam = S.sb([128, 4], name="nlam")
            nl = self.nlam
            S.op('dve', lambda E: E.tensor_reduce(out=nl[:, 0:2], in_=l2[:], axis=AX.X, op=ALU.add), reads=[l2], writes=[nl])
            self.act(nl[:, 0:2], nl[:, 0:2], AF.Exp, r=[nl], w=[nl])
            self.tt('dve', nl[:, 2:3], nl[:, 1:2], nl[:, 0:1], ALU.subtract, r=[nl], pw=[nl])
            self.ts('dve', nl[:, 3:4], nl[:, 2:3], -lam_init, None, ALU.add, r=[nl], pw=[nl])
            self.ctxkT = self.scratch("ctxkT", [8, 128, 256])
            self.ctx_tk = Tk(None)
        if 0 in mixers:
            self.gd_w_in = self.inp("gdn_w_in", [2, D, 4128])
            self.gd_w_out = self.inp("gdn_w_out", [2, D, D])
            gd_cw = self.inp("gdn_convT", [128, 2, 24, 5])
            gd_al = self.inp("gdn_a_log", [2, 1, 16])
            gd_dt = self.inp("gdn_dt_bias", [2, 1, 16])
            gd_ng = self.inp("gdn_norm_g", [2, 1, 128])
            self.st_S = self.inp("st_S", [2, 2, 8, 128, 128])
            self.newS = self.outp("newS", [NP, 2, 2, 8, 128, 128])
            self.gd_cw = S.sb([128, 2, 24, 5], name="gd_cw")
            self.ld(self.gd_cw[:], gd_cw, r=[self.Win], w=[self.gd_cw])
            self.gd_nea = S.sb([128, 2, 16], name="gd_nea")
            self.gd_dt = S.sb([128, 2, 16], name="gd_dt")
            self.gd_ng = S.sb([128, 2, 128], name="gd_ng")
            for j in range(2):
                self.ld(self.gd_nea[:, j, :], gd_al[j].partition_broadcast(128), r=[self.Win], **wr(self.gd_nea, j == 0))
                self.ld(self.gd_dt[:, j, :], gd_dt[j].partition_broadcast(128), r=[self.Win], **wr(self.gd_dt, j == 0))
                self.ld(self.gd_ng[:, j, :], gd_ng[j].partition_broadcast(128), r=[self.Win], **wr(self.gd_ng, j == 0))
            self.act(self.gd_nea[:], self.gd_nea[:], AF.Exp, r=[self.gd_nea], w=[self.gd_nea])
            self.ts('dve', self.gd_nea[:], self.gd_nea[:], -1.0, None, ALU.mult, r=[self.gd_nea], w=[self.gd_nea])

    def proj_stage(self, i, w_in, ncol, fm, tm, col_lo=0):
        self.areset()
        NT = 512
        xbs = self.take([128, 8, NT], 1)
        hbs = self.take([128, 8, NT], 1)
        self.rstd = self.take([128, NT], 2)
        W = self.take([128, 8, ncol])
        wv_ = w_in.rearrange("(kc p) n -> p kc n", p=128)
        self.ld(W[:, 0:4, :], wv_[:, 0:4, col_lo:col_lo + ncol], r=[self.Win], w=[W])
        self.ld(W[:, 4:8, :], wv_[:, 4:8, col_lo:col_lo + ncol], r=[self.Win], pw=[W], eng='pool')
        fm = [(a - col_lo, b, c_, d_, e_) for (a, b, c_, d_, e_) in fm]
        tm = [(a - col_lo, b, c_) for (a, b, c_) in tm]
        ofm = self.take([128, NT], 3)
        otm = self.take([128, 512], 3)
        for blk in range(TT // NT):
            c = 0 if blk < (NP * TP) // NT else 1
            tks = self.xT_tk[blk * 4:(blk + 1) * 4]
            ptk = self.proj_tk[blk * 4:(blk + 1) * 4]
            xb = xbs.get()
            self.ld(xb[:], self.xT_v[:, :, blk * NT:(blk + 1) * NT], r=tks, w=[xb], eng='pool')
            hb = hbs.get()
            self.norm_mod(xb, hb, 0, c, NT)
            n = 0
            for (col0, nch, dstv, ch0, scale) in fm:
                for oc in range(nch):
                    p = self.pnext()
                    for kc in range(8):
                        self.mm(p[:, :], W[:, kc, col0 + oc * 128: col0 + (oc + 1) * 128], hb[:, kc, :], kc == 0, kc == 7, r=[W, hb], **wr(p, kc == 0))
                    ot = ofm.get()
                    if n % 2 == 0:
                        self.act(ot[:], p[:, :], AF.Copy, r=[p], w=[ot], scale=scale)
                    else:
                        self.ts('dve', ot[:], p[:, :], scale, None, ALU.mult, r=[p], w=[ot])
                    n += 1
                    self.ld(dstv[:, ch0 + oc, blk * NT:(blk + 1) * NT], ot[:], r=[ot], pw=ptk, eng='pool')
            for q in range(4):
                t0 = blk * NT + q * 128
                for (col0, ncols, dst) in tm:
                    for g0 in range(0, ncols, 512):
                        gw = min(512, ncols - g0)
                        p = self.pnext()
                        for kc in range(8):
                            self.mm(p[:, 0:gw], hb[:, kc, q * 128:(q + 1) * 128], W[:, kc, col0 + g0: col0 + g0 + gw], kc == 0, kc == 7, r=[W, hb], **wr(p, kc == 0))
                        ot = otm.get()
                        if n % 2 == 0:
                            self.cp('act', ot[:, 0:gw], p[:, 0:gw], r=[p], w=[ot])
                        else:
                            self.cp('dve', ot[:, 0:gw], p[:, 0:gw], r=[p], w=[ot])
                        n += 1
                        self.ld(dst[t0:t0 + 128, g0:g0 + gw], ot[:, 0:gw], r=[ot], pw=[ptk[q]], eng='sp')

    def load_w16(self, W16, w_view, ncol, col_lo=0, piece=512, eng2='pool'):
        n = 0
        for c0 in range(0, ncol, piece):
            cw = min(piece, ncol - c0)
            st = self.wstage.get()
            self.ld(st[:, :, 0:cw], w_view[:, :, col_lo + c0:col_lo + c0 + cw], r=[self.Win], w=[st], eng='sp' if n % 2 == 0 else eng2)
            if n % 2 == 0:
                self.cp('dve', W16[:, :, c0:c0 + cw], st[:, :, 0:cw], r=[st], **wr(W16, c0 == 0))
            else:
                self.cp('act', W16[:, :, c0:c0 + cw], st[:, :, 0:cw], r=[st], **wr(W16, c0 == 0))
            n += 1

    def proj_stage16(self, i, w_in, ncol, fm, tm):
        self.areset()
        NT = 512
        xbs = self.take([128, 8, NT], 2)
        hbs = self.take([128, 8, NT], 2, BF16)
        sq = self.take([128, 8, NT])
        self.rstd = self.take([128, NT], 2)
        W = self.take([128, 8, ncol], None, BF16)
        self.wstage = self.take([128, 8, 512], 2)
        self.load_w16(W, w_in.rearrange("(kc p) n -> p kc n", p=128), ncol)
        ofm = self.take([128, NT], 3)
        otm = self.take([128, 512], 3)
        def pA(blk):
            c = 0 if blk < (NP * TP) // NT else 1
            tks = self.xT_tk[blk * 4:(blk + 1) * 4]
            xb = xbs.get()
            self.ld(xb[:], self.xT_v[:, :, blk * NT:(blk + 1) * NT], r=tks, w=[xb], eng='pool')
            hb = hbs.get()
            self.norm_mod2(xb, xb[:, :, :], hb, hb[:, :, :], sq, 0, c, NT, True)
            return hb

        def pB(blk, hb):
            n = 0
            for (col0, nch, dstv, ch0, scale) in fm:
                for oc in range(nch):
                    p = self.pnext()
                    for kc in range(8):
                        self.mm(p[:, :], W[:, kc, col0 + oc * 128: col0 + (oc + 1) * 128], hb[:, kc, :], kc == 0, kc == 7, r=[W, hb], **wr(p, kc == 0))
                    ot = ofm.get()
                    if n % 2 == 0:
                        self.act(ot[:], p[:, :], AF.Copy, r=[p], w=[ot], scale=scale)
                    else:
                        self.ts('dve', ot[:], p[:, :], scale, None, ALU.mult, r=[p], w=[ot])
                    n += 1
                    self.ld(dstv[:, ch0 + oc, blk * NT:(blk + 1) * NT], ot[:], r=[ot], pw=[self.Oout], eng='pool')
            for q in range(4):
                t0 = blk * NT + q * 128
                for (col0, ncols, dst) in tm:
                    for g0 in range(0, ncols, 512):
                        gw = min(512, ncols - g0)
                        p = self.pnext()
                        for kc in range(8):
                            self.mm(p[:, 0:gw], hb[:, kc, q * 128:(q + 1) * 128], W[:, kc, col0 + g0: col0 + g0 + gw], kc == 0, kc == 7, r=[W, hb], **wr(p, kc == 0))
                        ot = otm.get()
                        if n % 2 == 0:
                            self.cp('act', ot[:, 0:gw], p[:, 0:gw], r=[p], w=[ot])
                        else:
                            self.cp('dve', ot[:, 0:gw], p[:, 0:gw], r=[p], w=[ot])
                        n += 1
                        self.ld(dst[t0:t0 + 128, g0:g0 + gw], ot[:, 0:gw], r=[ot], pw=[self.Oout], eng='sp')


        NBK = TT // NT
        hb_next = pA(0)
        for blk in range(NBK):
            hb_cur = hb_next
            if blk + 1 < NBK:
                hb_next = pA(blk + 1)
            pB(blk, hb_cur)

    def mlstm(self, i):
        o = self.opts
        (self.proj_stage16 if self.bf else self.proj_stage)(i, self.ml_w_in, 3104,
                        fm=[(0, 4, self.qkT_v, 0, 0.125), (512, 4, self.qkT_v, 4, 1.0)],
                        tm=[(512, 512, self.ktok), (1024, 1024, self.vtok), (2048, 1024, self.otok), (3072, 32, self.gtok)])
        self.mlstm_scan()
        self.mixer_post(i, self.ml_w_out, self.ml_ng, self.ml_ng[:], self.hdir, 'sigmoid')

    def mlstm_scan(self):
        self.areset()
        cst = self.cst
        Tri = [cst.ap[0:64, 2, 0:64], cst.ap[0:64, 3, 0:64]]
        Str = [cst.ap[0:64, 4, 0:64], cst.ap[0:64, 5, 0:64]]
        ones64 = cst.ap[0:64, 1, 0:64]
        Cn8 = [self.take([64, 8, 129]) for d in range(2)]
        qks = self.take([64, 16, 64], 4)
        kts = self.take([64, 512], 4)
        v1s = self.take([64, 8, 129], 4)
        for v1 in v1s.t:
            self.memset('pool', v1[:, :, 128:129], 1.0, pw=[v1])
        gts = self.take([64, 32], 4)
        gps = self.take([64, 64], 4)
        tls = self.take([64, 64], 6)
        tot8s = self.take([64, 8, 129], 4)
        tl8s = self.take([64, 8, 64], 2)
        E8s = self.take([64, 8, 64], 4)
        kw8s = self.take([64, 8, 64], 4)
        dns = self.take([64, 16], 4)
        houts = self.take([64, 8, 128], 4)
        mst = [self.take([8, 1]) for d in range(2)]
        msm = self.take([8, 8], 2)
        emf = self.take([64, 8], 2)
        GBs = self.take([8, 2], 4)
        em0 = self.take([64, 16])
        co8s = self.take([64, 8, 129], 2)
        n0s = self.take([64, 8], 4)
        ia8s = self.take([64, 8, 129], 3)
        seqs = [(p * TP, TP // 64, p) for p in range(NP)] + [(NP * TP, TS // 64, -1)]
        for (tok0, nch, pidx) in seqs:
            if pidx >= 0:
                for d in range(2):
                    self.memset('pool', Cn8[d][:], 0.0, w=[Cn8[d]])
                    self.memset('pool', mst[d][:], 0.0, w=[mst[d]])
            else:
                self.ld(em0[:], self.st_m.partition_broadcast(64), r=[self.Win], w=[em0])
                self.act(em0[:], em0[:], AF.Exp, r=[em0], w=[em0])
                for d in range(2):
                    T_ = Cn8[d]
                    self.ld(T_[:, :, 0:128], self.st_C[d].rearrange("h k e -> k h e"), r=[self.Win], w=[T_])
                    n0 = n0s.get()
                    self.ld(n0[:], self.st_n[d], r=[self.Win], w=[n0], eng='pool')
                    self.cp('dve', T_[:, :, 128], n0[:], r=[n0], pw=[T_])
                    self.tt('pool', T_[:], T_[:], em0[:, d * 8:d * 8 + 8].unsqueeze(2).to_broadcast([64, 8, 129]), ALU.mult, r=[T_, em0], w=[T_])
            for step in range(nch):
                ctxs = []
                for d in range(2):
                    c = step if d == 0 else nch - 1 - step
                    t0 = tok0 + c * 64
                    ptk = [self.proj_tk[t0 // 128]]
                    qk = qks.get()
                    self.ld(qk[:], self.qkT_h[:, :, t0:t0 + 64], r=ptk, w=[qk])
                    kt = kts.get()
                    self.ld(kt[:], self.ktok[t0:t0 + 64, 0:512], r=ptk, w=[kt], eng="pool")
                    v1 = v1s.get()
                    self.ld(v1[:, :, 0:128], self.vtok[t0:t0 + 64, :].rearrange("t (h e) -> t h e", h=8), r=ptk, pw=[v1])
                    gt = gts.get()
                    self.ld(gt[:], self.gtok[t0:t0 + 64, :], r=ptk, w=[gt], eng='pool')
                    gp = gps.get()
                    dc = slice(d * 8, d * 8 + 8)
                    self.tt('dve', gp[:, 0:8], gt[:, dc], self.ml_gb[0:64, dc], ALU.add, r=[gt, self.ml_gb], w=[gp])
                    self.tt('dve', gp[:, 16:24], gt[:, 16 + d * 8:24 + d * 8], self.ml_gb[0:64, 16 + d * 8:24 + d * 8], ALU.add, r=[gt, self.ml_gb], pw=[gp])
                    self.act(gp[:, 16:24], gp[:, 16:24], AF.Exp, r=[gp], pw=[gp], scale=-1.0)
                    self.act(gp[:, 16:24], gp[:, 16:24], AF.Ln, r=[gp], pw=[gp], bias=1.0)
                    self.ts('dve', gp[:, 16:24], gp[:, 16:24], -1.0, None, ALU.mult, r=[gp], pw=[gp])
                    lf = gp[:, 16:24]
                    pg = self.pnext()
                    self.mm(pg[0:64, 0:8], Tri[d], lf, True, True, r=[cst, gp], w=[pg])
                    self.mm(pg[0:64, 8:16], Str[d], lf, True, True, r=[cst, gp], pw=[pg])
                    self.mm(pg[0:64, 16:24], ones64, lf, True, True, r=[cst, gp], pw=[pg])
                    self.act(gp[:, 32:40], pg[0:64, 0:8], AF.Exp, r=[pg], pw=[gp])
                    self.tt('dve', gp[:, 56:64], pg[0:64, 8:16], gp[:, 0:8], ALU.add, r=[pg, gp], pw=[gp])
                    self.act(gp[:, 40:48], gp[:, 56:64], AF.Exp, r=[gp], pw=[gp])
                    self.act(gp[:, 48:56], pg[0:64, 16:24], AF.Exp, r=[pg], pw=[gp])
                    if pidx >= 0:
                        pt = self.pnext()
                        self.tr_(pt[0:8, 0:64], gp[:, 56:64], r=[gp], w=[pt])
                        self.tr_(pt[0:8, 64:128], gp[:, 24:32] if False else pg[0:64, 16:24], r=[pg], pw=[pt]) if False else None
                        GB = GBs.get()
                        self.S.op('dve', lambda E, GB=GB, pt=pt: E.tensor_reduce(out=GB[:, 0:1], in_=pt[0:8, 0:64], axis=AX.X, op=ALU.max), reads=[pt], writes=[GB])
                        pb = self.pnext()
                        self.mm(pb[0:8, 0:1], lf, cst.ap[0:64, 1, 0:1], True, True, r=[gp, cst], w=[pb])
                        self.stt('dve', mst[d][:], mst[d][:], pb[0:8, 0:1], GB[:, 0:1], ALU.add, ALU.max, r=[mst[d], pb, GB], w=[mst[d]])
                    ctxs.append((d, t0, qk, kt, v1, gp, houts.get(), tot8s.get()))
                units = [(cx, h) for cx in ctxs for h in range(8)]
                stE = {}
                for cx in ctxs:
                    d, t0, qk, kt, v1, gp, ho, t8 = cx
                    tl8 = tl8s.get()
                    self.tt('dve', tl8[:], Tri[d].unsqueeze(1).to_broadcast([64, 8, 64]), gp[:, 16:24].unsqueeze(2).to_broadcast([64, 8, 64]), ALU.mult, r=[cst, gp], w=[tl8])
                    pD = self.pnext()
                    self.mm(pD[0:64, 0:512], Str[d], tl8[:].rearrange("p h t -> p (h t)"), True, True, r=[cst, tl8], w=[pD])
                    E8 = E8s.get()
                    self.act(E8[:].rearrange("p h t -> p (h t)"), pD[0:64, 0:512], AF.Exp, r=[pD], w=[E8])
                    self.act(gp[:, 8:16], gp[:, 0:8], AF.Exp, r=[gp], pw=[gp])
                    stE[d] = E8
                stK = {}
                for cx in ctxs:
                    d, t0, qk, kt, v1, gp, ho, t8 = cx
                    E8 = stE[d]
                    self.tt('pool', E8[:], E8[:], Tri[d].unsqueeze(1).to_broadcast([64, 8, 64]), ALU.mult, r=[E8, cst], w=[E8])
                    self.tt('dve', E8[:], E8[:], gp[:, 8:16].unsqueeze(2).to_broadcast([64, 8, 64]), ALU.mult, r=[E8, gp], w=[E8])
                    kw8 = kw8s.get()
                    self.tt('pool', kw8[:], kt[:].rearrange("t (h e) -> t h e", h=8), gp[:, 40:48].unsqueeze(2).to_broadcast([64, 8, 64]), ALU.mult, r=[kt, gp], w=[kw8])
                    stK[d] = kw8
                for cx in ctxs:
                    d, t0, qk, kt, v1, gp, ho, t8 = cx
                    pK = self.pnext()
                    for h in range(8):
                        self.mm(pK[0:64, h * 64:(h + 1) * 64], qk[:, 8 + h, :], qk[:, h, :], True, True, r=[qk], **wr(pK, h == 0))
                    E8 = stE[d]
                    self.tt('dve', E8[:].rearrange("p h t -> p (h t)"), E8[:].rearrange("p h t -> p (h t)"), pK[0:64, 0:512], ALU.mult, r=[E8, pK], w=[E8])
                HG = [(0, 3), (3, 3), (6, 2)]
                for cx in ctxs:
                    d, t0, qk, kt, v1, gp, ho, t8 = cx
                    E8 = stE[d]
                    ia8 = ia8s.get()
                    for bi, (h0, nh) in enumerate(HG):
                        pI = self.pnext()
                        for hh in range(nh):
                            h = h0 + hh
                            self.mm(pI[0:64, hh * 129:(hh + 1) * 129], E8[:, h, :], v1[:, h, :], True, True, r=[E8, v1], **wr(pI, hh == 0))
                        self.cp('act', ia8[:, h0:h0 + nh, :], pI[0:64, 0:nh * 129].rearrange("p (h e) -> p h e", h=nh), r=[pI], **wr(ia8, bi == 0))
                    for bi, (h0, nh) in enumerate(HG):
                        pN = self.pnext()
                        for hh in range(nh):
                            h = h0 + hh
                            self.mm(pN[0:64, hh * 129:(hh + 1) * 129], qk[:, h, :], Cn8[d][:, h, :], True, True, r=[qk, Cn8[d]], **wr(pN, hh == 0))
                        self.tt('dve', t8[:, h0:h0 + nh, :], pN[0:64, 0:nh * 129].rearrange("p (h e) -> p h e", h=nh),
                                gp[:, 32 + h0:32 + h0 + nh].unsqueeze(2).to_broadcast([64, nh, 129]), ALU.mult, r=[pN, gp], **wr(t8, bi == 0))
                    self.tt('dve', t8[:], t8[:], ia8[:], ALU.add, r=[t8, ia8], w=[t8])
                for cx in ctxs:
                    d, t0, qk, kt, v1, gp, ho, t8 = cx
                    dn = dns.get()
                    den = t8[:, :, 128]
                    self.ts('dve', dn[:, 0:8], den, -1.0, None, ALU.mult, r=[t8], w=[dn])
                    self.tt('dve', dn[:, 0:8], dn[:, 0:8], den, ALU.max, r=[dn, t8], w=[dn])
                    self.ts('dve', dn[:, 0:8], dn[:, 0:8], 1.0, None, ALU.max, r=[dn], w=[dn])
                    self.recip(dn[:, 8:16], dn[:, 0:8], r=[dn], pw=[dn])
                    self.tt('pool', ho[:], t8[:, :, 0:128], dn[:, 8:16].unsqueeze(2).to_broadcast([64, 8, 128]), ALU.mult, r=[t8, dn], w=[ho])
                    self.ld(self.hdir[d][t0:t0 + 64, :].rearrange("t (h e) -> t h e", h=8), ho[:], r=[ho], w=[self.hdir_tk[d][t0 // 64]], eng='pool')
                for cx in ctxs:
                    d, t0, qk, kt, v1, gp, ho, t8 = cx
                    C_ = Cn8[d]
                    kw8 = stK[d]
                    pUs = []
                    for bi, (h0, nh) in enumerate(HG):
                        pU = self.pnext()
                        for hh in range(nh):
                            h = h0 + hh
                            self.mm(pU[0:64, hh * 129:(hh + 1) * 129], kw8[:, h, :], v1[:, h, :], True, True, r=[kw8, v1], **wr(pU, hh == 0))
                        pUs.append(pU)
                    self.tt('pool', C_[:], C_[:], gp[:, 48:56].unsqueeze(2).to_broadcast([64, 8, 129]), ALU.mult, r=[C_, gp], w=[C_])
                    for bi, (h0, nh) in enumerate(HG):
                        self.tt('dve', C_[:, h0:h0 + nh, :], C_[:, h0:h0 + nh, :], pUs[bi][0:64, 0:nh * 129].rearrange("p (h e) -> p h e", h=nh), ALU.add,
                                r=[C_, pUs[bi]], w=[C_])
            if pidx >= 0:
                for d in range(2):
                    dm = msm.get()
                    self.ts('dve', dm[:], cst.ap[0:8, 0, 0:8], mst[d][:, 0:1], None, ALU.mult, r=[cst, mst[d]], w=[dm])
                    pm = self.pnext()
                    self.mm(pm[0:64, 0:8], cst.ap[0:8, 1, 0:64], dm[:], True, True, r=[cst, dm], w=[pm])
                    ef = emf.get()
                    self.act(ef[:], pm[0:64, 0:8], AF.Exp, r=[pm], w=[ef], scale=-1.0)
                    self.ld(self.newm[pidx, d], mst[d][:], r=[mst[d]], w=[self.Oout], eng='pool')
                    co = co8s.get()
                    self.tt('dve', co[:], Cn8[d][:], ef[:].unsqueeze(2).to_broadcast([64, 8, 129]), ALU.mult, r=[Cn8[d], ef], w=[co])
                    self.ld(self.newC[pidx, d].rearrange("h k e -> k h e"), co[:, :, 0:128], r=[co], pw=[self.Oout], eng='sp')
                    n1 = n0s.get()
                    self.cp('dve', n1[:], co[:, :, 128], r=[co], w=[n1])
                    self.ld(self.newn[pidx, d], n1[:], r=[n1], pw=[self.Oout], eng='pool')

    def gdn(self, i):
        j = i // 3
        w_in = self.gd_w_in[j]
        if self.bf:
            self.proj_stage16(i, w_in, 4128, fm=[(0, 24, self.qkT_v, 0, 1.0)],
                              tm=[(3072, 1024, self.otok), (4096, 32, self.gtok)])
        else:
            self.proj_stage(i, w_in, 2048, fm=[(0, 16, self.qkT_v, 0, 1.0)], tm=[], col_lo=0)
            self.proj_stage(i, w_in, 2080, fm=[(2048, 8, self.qkT_v, 16, 1.0)],
                            tm=[(3072, 1024, self.otok), (4096, 32, self.gtok)], col_lo=2048)
        stop = self.opts.get('gdn_stop', 9)
        if stop >= 2:
            self.gdn_conv(j)
        if stop >= 3:
            self.gdn_scan(j)
        if stop >= 4:
            self.mixer_post(i, self.gd_w_out[j], self.gd_ng, self.gd_ng[:, j, :], self.hdir, 'silu')

    def gdn_conv(self, j):
        self.areset()
        NBUF = 6
        xins = self.take([128, TS + 16], NBUF)
        tmps = self.take([128, TS], 3)
        sqs = self.take([128, 512], 4)
        rss = self.take([128, 512], 6)
        tos = self.take([128, 4, 128], 4)
        accs = []
        for _ in range(NBUF):
            base = self.take([128, TS])
            accs.append((base.ap, [Tk(base.ap[:, b * 512:(b + 1) * 512]) for b in range(4)]))
        cw = self.gd_cw
        n = 0
        items = [(tok0, ns, T, ch) for (tok0, ns, T) in [(0, NP, TP), (NP * TP, 1, TS)] for ch in range(24)]

        def phA(idx):
            tok0, ns, T, ch = items[idx]
            W = T + 4
            on_dve = (idx % 2 == 0)
            xin = xins.get()
            acc_ap, accb = accs[idx % NBUF]
            xv = xin[:, 0:ns * W].rearrange("p (s w) -> p s w", s=ns)
            av = acc_ap[:, 0:ns * T].rearrange("p (s t) -> p s t", s=ns)
            self.memset('pool', xv[:, :, 0:2], 0.0, w=[xin])
            self.memset('pool', xv[:, :, T + 2:T + 4], 0.0, pw=[xin])
            self.ld(xv[:, :, 2:T + 2], self.qkT_v[:, ch, tok0:tok0 + ns * T].rearrange("p (s t) -> p s t", s=ns), pw=[xin])
            if on_dve:
                self.ts('dve', av, xv[:, :, 0:T], cw[:, j, ch, 0:1], None, ALU.mult, r=[xin, cw], w=accb)
                for k in range(1, 5):
                    self.stt('dve', av, xv[:, :, k:k + T], cw[:, j, ch, k:k + 1], av, ALU.mult, ALU.add, r=[xin, cw] + accb, w=accb)
            else:
                self.act(av, xv[:, :, 0:T], AF.Copy, r=[xin, cw], w=accb, scale=cw[:, j, ch, 0:1])
                for k in range(1, 5):
                    tm_ = tmps.get()
                    tv = tm_[:, 0:ns * T].rearrange("p (s t) -> p s t", s=ns)
                    self.act(tv, xv[:, :, k:k + T], AF.Copy, r=[xin, cw], w=[tm_], scale=cw[:, j, ch, k:k + 1])
                    self.tt('pool', av, av, tv, ALU.add, r=accb + [tm_], w=accb)
            self.act(acc_ap[:, 0:ns * T], acc_ap[:, 0:ns * T], AF.Silu, r=accb, w=accb)
            return (tok0, ns * T, ch, acc_ap, accb)

        def phB(cx):
            tok0, NTOK, ch, acc_ap, accb = cx
            nb = NTOK // 512
            if ch < 16:
                scale = (128.0 ** -0.5) if ch < 8 else 1.0
                ps_, rs_ = [], []
                for b in range(nb):
                    bs = slice(b * 512, (b + 1) * 512)
                    sq = sqs.get()
                    self.tt('pool', sq[:], acc_ap[:, bs], acc_ap[:, bs], ALU.mult, r=[accb[b]], w=[sq])
                    p = self.pnext()
                    self.mm(p[:, :], self.cst.ap[:, 1, :], sq[:], True, True, r=[self.cst, sq], w=[p])
                    ps_.append(p)
                for b in range(nb):
                    rs = rss.get()
                    self.act(rs[:], ps_[b][:, :], AF.Sqrt, r=[ps_[b], self.epsb], w=[rs], bias=self.epsb[:, 0:1])
                    rs_.append(rs)
                for b in range(nb):
                    bs = slice(b * 512, (b + 1) * 512)
                    rs = rs_[b]
                    self.recip(rs[:], rs[:], r=[rs], w=[rs])
                    self.stt('dve', acc_ap[:, bs], acc_ap[:, bs], scale, rs[:], ALU.mult, ALU.mult, r=[accb[b], rs], w=[accb[b]])

        def phC(cx):
            tok0, NTOK, ch, acc_ap, accb = cx
            nb = NTOK // 512
            if ch < 16:
                self.ld(self.qkT_v[:, ch, tok0:tok0 + NTOK], acc_ap[:, 0:NTOK], r=accb[0:nb], pw=[self.Oout], eng='pool')
            if ch >= 8:
                dst = self.ktok if ch < 16 else self.vtok
                c0 = (ch - 8) * 128 if ch < 16 else (ch - 16) * 128
                for b in range(nb):
                    p = self.pnext()
                    for k in range(4):
                        self.tr_(p[:, k * 128:(k + 1) * 128], acc_ap[:, b * 512 + k * 128:b * 512 + (k + 1) * 128], r=[accb[b]], **wr(p, k == 0))
                    to = tos.get()
                    self.cp('act' if b % 2 else 'dve', to[:], p[:, :].rearrange("p (a b) -> p a b", a=4), r=[p], w=[to])
                    self.ld(dst[tok0 + b * 512:tok0 + (b + 1) * 512, c0:c0 + 128].rearrange("(n p) e -> p n e", p=128), to[:], r=[to], pw=[self.Oout], eng='sp')

        cxs = {}
        NI = len(items)
        for it in range(NI + 2):
            if it < NI:
                cxs[it] = phA(it)
            if 0 <= it - 1 < NI:
                phB(cxs[it - 1])
            if 0 <= it - 2 < NI:
                phC(cxs.pop(it - 2))

    def gdn_scan(self, j):
        self.areset()
        cst = self.cst
        Tri = [cst.ap[0:64, 2, 0:64], cst.ap[0:64, 3, 0:64]]
        Str = [cst.ap[0:64, 4, 0:64], cst.ap[0:64, 5, 0:64]]
        Sm = [cst.ap[0:64, 5, 0:64], cst.ap[0:64, 4, 0:64]]
        I64 = cst.ap[0:64, 0, 0:64]
        ones64w = cst.ap[0:64, 1, 0:128]

        def bh(m):
            return m.unsqueeze(1).to_broadcast([64, 8, 64])

        def bt(v, n, np_=64):
            return v.unsqueeze(2).to_broadcast([np_, v.shape[1], n])

        S8 = [self.take([128, 8, 128]) for d in range(2)]
        qks = self.take([128, 8, 2, 64], 3)
        kts = self.take([64, 8, 128], 3)
        vts = self.take([64, 8, 128], 3)
        gts = self.take([64, 32], 3)
        gps = self.take([64, 48], 3)
        gls = self.take([128, 8], 3)
        tl8s = self.take([64, 8, 64], 2)
        Er8s = self.take([64, 8, 64], 2)
        Ei8s = self.take([64, 8, 64], 2)
        Es8s = self.take([64, 8, 64], 2)
        qkT8s = self.take([64, 8, 64], 3)
        P8s = self.take([64, 8, 64], 3)
        X8s = self.take([64, 8, 64], 5)
        XT8s = self.take([64, 8, 64], 5)
        U8s = self.take([64, 8, 128], 3)
        keg8s = self.take([64, 8, 128], 2)
        kdec8s = self.take([64, 8, 128], 3)
        vn8s = self.take([64, 8, 128], 3)
        o8s = self.take([64, 8, 128], 3)
        WT8s = self.take([128, 8, 64], 3)
        seqs = [(p * TP, TP // 64, p) for p in range(NP)] + [(NP * TP, TS // 64, -1)]
        seqs = seqs[self.opts.get('gdn_seq0', 0):self.opts.get('gdn_seq1', 5)]
        for (tok0, nch, pidx) in seqs:
            for d in range(2):
                if pidx >= 0:
                    self.memset('pool', S8[d][:], 0.0, w=[S8[d]])
                else:
                    self.ld(S8[d][:], self.st_S[j, d].rearrange("h k e -> k h e"), r=[self.Win], w=[S8[d]], eng='sp' if d else 'pool')
            for step in range(nch):
                ctx = []
                for d in range(2):
                    c = step if d == 0 else nch - 1 - step
                    t0 = tok0 + c * 64
                    qk = qks.get()
                    self.ld(qk[:, :, 0, :], self.qkT_v[:, 8:16, t0:t0 + 64], w=[qk])
                    self.ld(qk[:, :, 1, :], self.qkT_v[:, 0:8, t0:t0 + 64], pw=[qk], eng='pool')
                    kt = kts.get()
                    self.ld(kt[:], self.ktok[t0:t0 + 64, 0:1024].rearrange("t (h e) -> t h e", h=8), w=[kt], eng='pool')
                    vt = vts.get()
                    self.ld(vt[:], self.vtok[t0:t0 + 64, :].rearrange("t (h e) -> t h e", h=8), w=[vt])
                    gt = gts.get()
                    self.ld(gt[:], self.gtok[t0:t0 + 64, :], w=[gt], eng='pool')
                    gp = gps.get()
                    dc = slice(d * 8, d * 8 + 8)
                    self.tt('dve', gp[:, 0:8], gt[:, dc], self.gd_dt[0:64, j, dc], ALU.add, r=[gt, self.gd_dt], w=[gp])
                    self.act(gp[:, 0:8], gp[:, 0:8], AF.Exp, r=[gp], pw=[gp])
                    self.act(gp[:, 0:8], gp[:, 0:8], AF.Ln, r=[gp], pw=[gp], bias=1.0)
                    self.tt('dve', gp[:, 8:16], gp[:, 0:8], self.gd_nea[0:64, j, dc], ALU.mult, r=[gp, self.gd_nea], pw=[gp])
                    self.act(gp[:, 16:24], gt[:, 16 + d * 8:24 + d * 8], AF.Sigmoid, r=[gt], pw=[gp])
                    self.ts('dve', gp[:, 24:32], gp[:, 16:24], -1.0, None, ALU.mult, r=[gp], pw=[gp])
                    la = gp[:, 8:16]
                    pg = self.pnext()
                    self.mm(pg[0:64, 0:8], Tri[d], la, True, True, r=[cst, gp], w=[pg])
                    self.mm(pg[0:64, 8:16], Str[d], la, True, True, r=[cst, gp], pw=[pg])
                    self.mm(pg[0:128, 16:24], ones64w, la, True, True, r=[cst, gp], pw=[pg])
                    self.act(gp[:, 32:48], pg[0:64, 0:16], AF.Exp, r=[pg], pw=[gp])
                    gl = gls.get()
                    self.act(gl[:], pg[0:128, 16:24], AF.Exp, r=[pg], w=[gl])
                    ctx.append(dict(d=d, t0=t0, qk=qk, kt=kt, vt=vt, gp=gp, gl=gl))
                for cx in ctx:
                    d, gp = cx['d'], cx['gp']
                    tl8 = tl8s.get()
                    self.tt('dve', tl8[:], bh(Tri[d]), bt(gp[:, 8:16], 64), ALU.mult, r=[cst, gp], w=[tl8])
                    pD = self.pnext()
                    self.mm(pD[0:64, 0:512], Str[d], tl8[:].rearrange("p h t -> p (h t)"), True, True, r=[cst, tl8], w=[pD])
                    Er = Er8s.get()
                    self.act(Er[:].rearrange("p h t -> p (h t)"), pD[0:64, 0:512], AF.Exp, r=[pD], w=[Er])
                    cx['Er'] = Er
                for cx in ctx:
                    d, gp, Er = cx['d'], cx['gp'], cx['Er']
                    Ei = Ei8s.get()
                    Es = Es8s.get()
                    self.tt('pool', Ei[:], Er[:], bh(Tri[d]), ALU.mult, r=[Er, cst], w=[Ei])
                    self.tt('pool', Es[:], Er[:], bh(Sm[d]), ALU.mult, r=[Er, cst], w=[Es])
                    self.tt('dve', Es[:], Es[:], bt(gp[:, 24:32], 64), ALU.mult, r=[Es, gp], w=[Es])
                    cx['Ei'], cx['Es'] = Ei, Es
                for cx in ctx:
                    qk = cx['qk']
                    X = X8s.get()
                    qkT = qkT8s.get()
                    for g in range(2):
                        pG = self.pnext()
                        for hh in range(4):
                            h = 4 * g + hh
                            self.mm(pG[0:64, hh * 128:(hh + 1) * 128], qk[:, h, 0, :], qk[:, h, :, :].rearrange("p a t -> p (a t)"), True, True,
                                    r=[qk], **wr(pG, hh == 0))
                        pv = pG[0:64, 0:512].rearrange("p (h a t) -> p h a t", h=4, a=2)
                        self.tt('dve', X[:, 4 * g:4 * g + 4, :], pv[:, :, 0, :], cx['Es'][:, 4 * g:4 * g + 4, :], ALU.mult, r=[pG, cx['Es']], **wr(X, g == 0))
                        self.tt('dve', qkT[:, 4 * g:4 * g + 4, :], pv[:, :, 1, :], cx['Ei'][:, 4 * g:4 * g + 4, :], ALU.mult, r=[pG, cx['Ei']], **wr(qkT, g == 0))
                    cx['X'], cx['qkT'] = X, qkT
                for cx in ctx:
                    X = cx['X']
                    pT = self.pnext()
                    for h in range(8):
                        self.tr_(pT[0:64, h * 64:(h + 1) * 64], X[:, h, :], r=[X], **wr(pT, h == 0))
                    XT = XT8s.get()
                    self.cp('act', XT[:].rearrange("p h t -> p (h t)"), pT[0:64, 0:512], r=[pT], w=[XT])
                    P_ = P8s.get()
                    self.tt('pool', P_[:], X[:], bh(I64), ALU.add, r=[X, cst], w=[P_])
                    cx['XT'], cx['P'] = XT, P_
                for jn in range(1, 6):
                    for cx in ctx:
                        X, XT = cx['X'], cx['XT']
                        Xn = None
                        if jn < 5:
                            pX = self.pnext()
                            for h in range(8):
                                self.mm(pX[0:64, h * 64:(h + 1) * 64], XT[:, h, :], X[:, h, :], True, True, r=[XT, X], **wr(pX, h == 0))
                            Xn = X8s.get()
                            self.cp('dve', Xn[:].rearrange("p h t -> p (h t)"), pX[0:64, 0:512], r=[pX], w=[Xn])
                        pXT = self.pnext()
                        for h in range(8):
                            self.mm(pXT[0:64, h * 64:(h + 1) * 64], X[:, h, :], XT[:, h, :], True, True, r=[XT, X], **wr(pXT, h == 0))
                        XnT = XT8s.get()
                        self.cp('act', XnT[:].rearrange("p h t -> p (h t)"), pXT[0:64, 0:512], r=[pXT], w=[XnT])
                        cx['X'], cx['XT'] = Xn, XnT
                    for cx in ctx:
                        XT, P_ = cx['XT'], cx['P']
                        pP = self.pnext()
                        for h in range(8):
                            self.mm(pP[0:64, h * 64:(h + 1) * 64], XT[:, h, :], P_[:, h, :], True, True, r=[XT, P_], **wr(pP, h == 0))
                        self.tt('dve', P_[:].rearrange("p h t -> p (h t)"), P_[:].rearrange("p h t -> p (h t)"), pP[0:64, 0:512], ALU.add, r=[P_, pP], w=[P_])
                for cx in ctx:
                    gp, kt, vt, P_ = cx['gp'], cx['kt'], cx['vt'], cx['P']
                    keg = keg8s.get()
                    self.tt('pool', keg[:], kt[:], bt(gp[:, 32:40], 128), ALU.mult, r=[kt, gp], w=[keg])
                    kdec = kdec8s.get()
                    self.tt('pool', kdec[:], kt[:], bt(gp[:, 40:48], 128), ALU.mult, r=[kt, gp], w=[kdec])
                    U = U8s.get()
                    for g in range(2):
                        pU = self.pnext()
                        for hh in range(4):
                            h = 4 * g + hh
                            self.mm(pU[0:64, hh * 128:(hh + 1) * 128], P_[:, h, :], vt[:, h, :], True, True, r=[P_, vt], **wr(pU, hh == 0))
                        self.tt('dve', U[:, 4 * g:4 * g + 4, :], pU[0:64, 0:512].rearrange("p (h e) -> p h e", h=4), bt(gp[:, 16 + 4 * g:20 + 4 * g], 128), ALU.mult,
                                r=[pU, gp], **wr(U, g == 0))
                    pW = self.pnext()
                    for h in range(8):
                        self.mm(pW[0:128, h * 64:(h + 1) * 64], keg[:, h, :], P_[:, h, :], True, True, r=[keg, P_], **wr(pW, h == 0))
                    WT = WT8s.get()
                    self.cp('act', WT[:].rearrange("p h t -> p (h t)"), pW[0:128, 0:512], r=[pW], w=[WT])
                    cx['U'], cx['WT'], cx['kdec'] = U, WT, kdec
                for cx in ctx:
                    d, gp = cx['d'], cx['gp']
                    vn = vn8s.get()
                    for g in range(2):
                        pa = self.pnext()
                        for hh in range(4):
                            h = 4 * g + hh
                            self.mm(pa[0:64, hh * 128:(hh + 1) * 128], cx['WT'][:, h, :], S8[d][:, h, :], True, True, r=[cx['WT'], S8[d]], **wr(pa, hh == 0))
                        self.tt('dve', vn[:, 4 * g:4 * g + 4, :], pa[0:64, 0:512].rearrange("p (h e) -> p h e", h=4), bt(gp[:, 24 + 4 * g:28 + 4 * g], 128), ALU.mult,
                                r=[pa, gp], **wr(vn, g == 0))
                    self.tt('pool', vn[:], vn[:], cx['U'][:], ALU.add, r=[vn, cx['U']], w=[vn])
                    cx['vn'] = vn
                for cx in ctx:
                    d, gp, gl, qk, vn = cx['d'], cx['gp'], cx['gl'], cx['qk'], cx['vn']
                    o8 = o8s.get()
                    for g in range(2):
                        po = self.pnext()
                        for hh in range(4):
                            h = 4 * g + hh
                            self.mm(po[0:64, hh * 128:(hh + 1) * 128], qk[:, h, 1, :], S8[d][:, h, :], True, True, r=[qk, S8[d]], **wr(po, hh == 0))
                        self.tt('dve', o8[:, 4 * g:4 * g + 4, :], po[0:64, 0:512].rearrange("p (h e) -> p h e", h=4), bt(gp[:, 32 + 4 * g:36 + 4 * g], 128), ALU.mult,
                                r=[po, gp], **wr(o8, g == 0))
                    for g in range(2):
                        po2 = self.pnext()
                        for hh in range(4):
                            h = 4 * g + hh
                            self.mm(po2[0:64, hh * 128:(hh + 1) * 128], cx['qkT'][:, h, :], vn[:, h, :], True, True, r=[cx['qkT'], vn], **wr(po2, hh == 0))
                        self.tt('dve', o8[:, 4 * g:4 * g + 4, :], o8[:, 4 * g:4 * g + 4, :], po2[0:64, 0:512].rearrange("p (h e) -> p h e", h=4), ALU.add,
                                r=[po2, o8], pw=[o8])
                    self.ld(self.hdir[d][cx['t0']:cx['t0'] + 64, :].rearrange("t (h e) -> t h e", h=8), o8[:], r=[o8], pw=[self.Oout], eng='pool')
                    for g in range(2):
                        pS = self.pnext()
                        for hh in range(4):
                            h = 4 * g + hh
                            self.mm(pS[0:128, hh * 128:(hh + 1) * 128], cx['kdec'][:, h, :], vn[:, h, :], True, True, r=[cx['kdec'], vn], **wr(pS, hh == 0))
                        Sg = S8[d][:, 4 * g:4 * g + 4, :]
                        self.tt('pool', Sg, Sg, bt(gl[:, 4 * g:4 * g + 4], 128, 128), ALU.mult, r=[S8[d], gl], w=[S8[d]])
                        self.tt('dve', Sg, Sg, pS[0:128, 0:512].rearrange("p (h e) -> p h e", h=4), ALU.add, r=[S8[d], pS], w=[S8[d]])
            if pidx >= 0:
                for d in range(2):
                    self.ld(self.newS[pidx, j, d].rearrange("h k e -> k h e"), S8[d][:], r=[S8[d]], pw=[self.Oout], eng='sp' if d else 'pool')

    def diffattn(self, i):
        (self.proj_stage16 if self.bf else self.proj_stage)(i, self.df_w_in, 3072, fm=[],
                        tm=[(0, 2048, self.ktok), (2048, 1024, self.vtok)])
        self.attn_prep()
        self.attn_core()
        self.mixer_post(i, self.df_w_out, self.df_sg, self.df_sg[:], self.hdir, None)

    def attn_prep(self):
        self.areset()
        xs = self.take([128, 32, 64], 3)
        sqs = self.take([128, 32, 64], 2)
        sss = self.take([128, 64], 3)
        css = self.take([128, 64], 3)
        r1 = self.take([128, 32, 2, 16], 1)
        r2 = self.take([128, 32, 2, 16], 1)
        r3 = self.take([128, 32, 2, 16], 1)
        xr = self.take([128, 32, 64], 2)
        vts = self.take([128, 1024], 2)
        xos = self.take([128, 16, 128], 2)
        cks = self.take([128, 16, 64], 2)
        cko = self.take([128, 8, 128], 2)
        gq = self.df_g
        def pA(t):
            t0 = t * 128
            x = xs.get()
            self.ld(x[:], self.ktok[t0:t0 + 128, :].rearrange("t (g d) -> t g d", g=32), r=[self.proj_tk[t]], w=[x])
            sq = sqs.get()
            self.tt('pool', sq[:], x[:], x[:], ALU.mult, r=[x], w=[sq])
            ss = sss.get()
            self.S.op('dve', lambda E, ss=ss, sq=sq: E.tensor_reduce(out=ss[:, 0:32], in_=sq[:], axis=AX.X, op=ALU.add), reads=[sq], writes=[ss])
            self.act(ss[:, 0:32], ss[:, 0:32], AF.Sqrt, r=[ss, self.epsb], w=[ss], scale=1.0 / 64, bias=self.epsb[:, 0:1])
            self.recip(ss[:, 32:64], ss[:, 0:32], r=[ss], pw=[ss])
            self.tt('dve', x[:], x[:], ss[:, 32:64].unsqueeze(2).to_broadcast([128, 32, 64]), ALU.mult, r=[x, ss], w=[x])
            self.tt('pool', x[:, 0:16, :], x[:, 0:16, :], gq[:, 0, :].unsqueeze(1).to_broadcast([128, 16, 64]), ALU.mult, r=[x, gq], w=[x])
            self.tt('dve', x[:, 16:32, :], x[:, 16:32, :], gq[:, 1, :].unsqueeze(1).to_broadcast([128, 16, 64]), ALU.mult, r=[x, gq], w=[x])
            if t0 < NP * TP:
                p, tl = t0 // TP, t0 % TP
                self.ld(self.newk[p, :, :, tl:tl + 128, :].rearrange("h m t d -> t (h m) d"), x[:, 16:32, :], r=[x], pw=[self.Oout], eng='pool')
                vt = vts.get()
                self.ld(vt[:], self.vtok[t0:t0 + 128, :], r=[self.proj_tk[t]], w=[vt])
                self.ld(self.newv[p, :, tl:tl + 128, :].rearrange("h t e -> t h e"), vt[:].rearrange("t (h e) -> t h e", h=8), r=[vt], pw=[self.Oout], eng='pool')
                src = x
            else:
                cs = css.get()
                self.ld(cs[:], self.rope_cs[t0 - NP * TP:t0 - NP * TP + 128, :], r=[self.Win], w=[cs])
                X = x[:].rearrange("t g (a f r) -> t g a f r", a=2, f=2)
                xa = X[:, :, :, 0, :]
                xb_ = X[:, :, :, 1, :]
                cosb = cs[:, 0:32].rearrange("t (a r) -> t a r", a=2).unsqueeze(1).to_broadcast([128, 32, 2, 16])
                sinb = cs[:, 32:64].rearrange("t (a r) -> t a r", a=2).unsqueeze(1).to_broadcast([128, 32, 2, 16])
                o_ = xr.get()
                O = o_[:].rearrange("t g (a f r) -> t g a f r", a=2, f=2)
                a1, a2, a3 = r1.get(), r2.get(), r3.get()
                self.tt('dve', a1[:], xa, cosb, ALU.mult, r=[x, cs], w=[a1])
                self.tt('pool', a2[:], xb_, sinb, ALU.mult, r=[x, cs], w=[a2])
                self.tt('dve', O[:, :, :, 0, :], a1[:], a2[:], ALU.subtract, r=[a1, a2], w=[o_])
                self.tt('pool', a3[:], xa, sinb, ALU.mult, r=[x, cs], w=[a3])
                self.tt('dve', a1[:], xb_, cosb, ALU.mult, r=[x, cs], w=[a1])
                self.tt('pool', O[:, :, :, 1, :], a3[:], a1[:], ALU.add, r=[a3, a1], pw=[o_])
                src = o_
            return src

        def pB(t, src):
            t0 = t * 128
            xo = xos.get()
            for g in range(4):
                pp = self.pnext()
                for k in range(4):
                    ch = g * 4 + k
                    self.tr_(pp[:, k * 128:(k + 1) * 128], src[:, 2 * ch:2 * ch + 2, :].rearrange("t a d -> t (a d)"), r=[src], **wr(pp, k == 0))
                self.cp('act' if g % 2 else 'dve', xo[:, g * 4:(g + 1) * 4, :], pp[:, :].rearrange("p (a b) -> p a b", a=4), r=[pp], **wr(xo, g == 0))
            self.ld(self.qkT_v[:, 0:16, t0:t0 + 128], xo[:], r=[xo], w=[self.prep_tk[t]], eng='pool')

        NTL = TT // 128
        srcq = {}
        for t in range(NTL + 1):
            if t < NTL:
                srcq[t] = pA(t)
            if t >= 1:
                pB(t - 1, srcq.pop(t - 1))
        for kt in range(2):
            ck = cks.get()
            self.ld(ck[:], self.ctx_k[:, :, kt * 128:(kt + 1) * 128, :].rearrange("h m t d -> t (h m) d"), r=[self.Win], w=[ck])
            co = cko.get()
            for g in range(2):
                pp = self.pnext()
                for k in range(4):
                    ch = g * 4 + k
                    self.tr_(pp[:, k * 128:(k + 1) * 128], ck[:, 2 * ch:2 * ch + 2, :].rearrange("t a d -> t (a d)"), r=[ck], **wr(pp, k == 0))
                self.cp('act' if g % 2 else 'dve', co[:, g * 4:(g + 1) * 4, :], pp[:, :].rearrange("p (a b) -> p a b", a=4), r=[pp], **wr(co, g == 0))
            self.ld(self.ctxkT.rearrange("c p t -> p c t")[:, :, kt * 128:(kt + 1) * 128], co[:], r=[co], **wr(self.ctx_tk, kt == 0), eng='pool')

    def attn_core(self):
        self.areset()
        NKT = (TS + 256) // 128
        bf = self.bf
        MD = BF16 if bf else F32
        qTs = self.take([128, TS], 2, MD)
        kTs = self.take([128, TS + 256], 2, MD)
        V1s = self.take([128, NKT, 129], 2, MD)
        for V1 in V1s.t:
            self.memset('pool', V1[:, :, 128:129], 1.0, pw=[V1])
        PTs = self.take([128, NKT, 512], 2, MD)
        if bf:
            q32 = self.take([128, TS], 2)
            k32 = self.take([128, TS + 256], 2)
            v32 = self.take([128, NKT, 128], 2)
        obs = self.take([128, 4, 128], 2)
        rvs = self.take([128, 2], 4)
        seqs = [(p * TP, TP, False) for p in range(NP)] + [(NP * TP, TS, True)]
        for (tok0, T, is_s) in seqs:
            nk = T + (256 if is_s else 0)
            nkt = nk // 128
            QB = min(512, T)
            tks = self.prep_tk[tok0 // 128:(tok0 + T) // 128]
            ptk = self.proj_tk[tok0 // 128:(tok0 + T) // 128]
            for h in range(8):
                qT = qTs.get()
                kT = kTs.get()
                V1 = V1s.get()
                if bf:
                    qd, kd, vd = q32.get(), k32.get(), v32.get()
                else:
                    qd, kd, vd = qT, kT, V1
                self.ld(qd[:, 0:T], self.qkT_v[:, h, tok0:tok0 + T], r=tks, w=[qd])
                self.ld(kd[:, 0:T], self.qkT_v[:, 8 + h, tok0:tok0 + T], r=tks, w=[kd], eng='pool')
                self.ld(vd[:, 0:T // 128, 0:128], self.vtok[tok0:tok0 + T, h * 128:(h + 1) * 128].rearrange("(n p) e -> p n e", p=128), r=ptk, **wr(vd, bf))
                if is_s:
                    self.ld(kd[:, T:T + 256], self.ctxkT[h], r=[self.ctx_tk], pw=[kd], eng='pool')
                    self.ld(vd[:, T // 128:nkt, 0:128], self.ctx_v[h].rearrange("(n p) e -> p n e", p=128), r=[self.Win], pw=[vd])
                if bf:
                    self.cp('dve', qT[:, 0:T], qd[:, 0:T], r=[qd], w=[qT])
                    self.cp('act', kT[:, 0:nk], kd[:, 0:nk], r=[kd], w=[kT])
                    self.cp('dve', V1[:, 0:nkt, 0:128], vd[:, 0:nkt, :], r=[vd], pw=[V1])
                for qb in range(T // QB):
                    ob = obs.get()
                    for m in range(2):
                        PT = PTs.get()
                        for kt in range(nkt):
                            pS = self.pnext()
                            self.mm(pS[:, 0:QB], kT[m * 64:(m + 1) * 64, kt * 128:(kt + 1) * 128], qT[m * 64:(m + 1) * 64, qb * QB:(qb + 1) * QB],
                                    True, True, r=[kT, qT], w=[pS])
                            self.act(PT[:, kt, 0:QB], pS[:, 0:QB], AF.Exp, r=[pS], **wr(PT, kt == 0), scale=0.125)
                        for qs in range(QB // 128):
                            pO = self.pnext()
                            for kt in range(nkt):
                                self.mm(pO[:, 0:129], PT[:, kt, qs * 128:(qs + 1) * 128], V1[:, kt, :], kt == 0, kt == nkt - 1, r=[PT, V1], **wr(pO, kt == 0))
                            rv = rvs.get()
                            self.recip(rv[:, 0:1], pO[:, 128:129], r=[pO], w=[rv])
                            if m == 0:
                                self.ts('dve', ob[:, qs, :], pO[:, 0:128], rv[:, 0:1], None, ALU.mult, r=[pO, rv], **wr(ob, qs == 0))
                            else:
                                self.tt('dve', rv[:, 1:2], rv[:, 0:1], self.nlam[:, 3:4], ALU.mult, r=[rv, self.nlam], pw=[rv])
                                self.stt('dve', ob[:, qs, :], pO[:, 0:128], rv[:, 1:2], ob[:, qs, :], ALU.mult, ALU.add, r=[pO, rv, ob], pw=[ob])
                    q0 = tok0 + qb * QB
                    nq = QB // 128
                    htk = self.hdir_tk[0][q0 // 64:(q0 + QB) // 64]
                    self.ld(self.hdir[0][q0:q0 + QB, h * 128:(h + 1) * 128].rearrange("(n p) e -> p n e", p=128), ob[:, 0:nq, :], r=[ob], pw=htk, eng='pool')

    def mixer_post(self, i, w_out, ng_tk, ng_bc, hdir, gate):
        self.areset()
        NT = 512
        if self.bf:
            W = self.take([128, 8, D], None, BF16)
            self.wstage = self.take([128, 8, 512], 2)
            self.load_w16(W, w_out.rearrange("(kc p) n -> p kc n", p=128), D)
        else:
            W = self.take([128, 8, D])
            self.ld(W[:], w_out.rearrange("(kc p) n -> p kc n", p=128), r=[self.Win], w=[W])
        hfs = self.take([128, 8, 128], 4)
        hbs = self.take([128, 8, 128], 4)
        ogs = self.take([128, 8, 128], 4)
        sqs = self.take([128, 8, 128], 3)
        sss = self.take([128, 16], 4)
        yTs = self.take([128, 8, NT], 2, BF16 if self.bf else F32)
        xbs = self.take([128, 8, NT], 2)
        ngb = ng_bc.unsqueeze(1).to_broadcast([128, 8, 128])
        blkst = {}

        def pA(t):
            blk, q = t // 4, t % 4
            if q == 0:
                xb = xbs.get()
                self.ld(xb[:], self.xT_v[:, :, blk * NT:(blk + 1) * NT], r=self.xT_tk[blk * 4:(blk + 1) * 4], w=[xb], eng='pool')
                blkst[blk] = [xb, yTs.get()]
            t0 = t * 128
            hf = hfs.get()
            og = ogs.get()
            self.ld(hf[:], hdir[0][t0:t0 + 128, :].rearrange("t (h e) -> t h e", h=8), r=self.hdir_tk[0][t0 // 64:t0 // 64 + 2], w=[hf])
            if gate is not None:
                hb = hbs.get()
                self.ld(hb[:], hdir[1][t0:t0 + 128, :].rearrange("t (h e) -> t h e", h=8), r=self.hdir_tk[1][t0 // 64:t0 // 64 + 2], w=[hb], eng='pool')
                self.ld(og[:], self.otok[t0:t0 + 128, :].rearrange("t (h e) -> t h e", h=8), r=[self.proj_tk[t0 // 128]], w=[og])
                self.tt('dve', hf[:], hf[:], hb[:], ALU.add, r=[hf, hb], w=[hf])
            sq = sqs.get()
            self.tt('pool', sq[:], hf[:], hf[:], ALU.mult, r=[hf], w=[sq])
            ss = sss.get()
            self.S.op('dve', lambda E, ss=ss, sq=sq: E.tensor_reduce(out=ss[:, 0:8], in_=sq[:], axis=AX.X, op=ALU.add), reads=[sq], writes=[ss])
            self.act(ss[:, 0:8], ss[:, 0:8], AF.Sqrt, r=[ss, self.epsb], w=[ss], scale=1.0 / 128, bias=self.epsb[:, 0:1])
            self.recip(ss[:, 8:16], ss[:, 0:8], r=[ss], pw=[ss])
            if gate == 'sigmoid':
                self.act(og[:], og[:], AF.Sigmoid, r=[og], w=[og])
                self.tt('pool', og[:], og[:], ngb, ALU.mult, r=[og, ng_tk], w=[og])
            elif gate == 'silu':
                self.act(og[:], og[:], AF.Silu, r=[og], w=[og])
                self.tt('pool', og[:], og[:], ngb, ALU.mult, r=[og, ng_tk], w=[og])
            else:
                self.cp('pool', og[:], ngb, r=[ng_tk], w=[og])
            self.tt('dve', hf[:], hf[:], ss[:, 8:16].unsqueeze(2).to_broadcast([128, 8, 128]), ALU.mult, r=[hf, ss], w=[hf])
            self.tt('dve', hf[:], hf[:], og[:], ALU.mult, r=[hf, og], w=[hf])
            return hf

        def pB(t, hf):
            blk, q = t // 4, t % 4
            yT = blkst[blk][1]
            for hh in range(2):
                p = self.pnext()
                for k in range(4):
                    self.tr_(p[:, k * 128:(k + 1) * 128], hf[:, hh * 4 + k, :], r=[hf], **wr(p, k == 0))
                dst = yT[:, hh * 4:(hh + 1) * 4, q * 128:(q + 1) * 128]
                src = p[:, :].rearrange("p (a b) -> p a b", a=4)
                self.cp('act' if hh else 'dve', dst, src, r=[p], **wr(yT, q == 0 and hh == 0))

        def pC(blk):
            c = 0 if blk < (NP * TP) // NT else 1
            xb, yT = blkst.pop(blk)
            tks = self.xT_tk[blk * 4:(blk + 1) * 4]
            for oc in range(8):
                p = self.pnext()
                for kc in range(8):
                    self.mm(p[:, :], W[:, kc, oc * 128:(oc + 1) * 128], yT[:, kc, :], kc == 0, kc == 7, r=[W, yT], **wr(p, kc == 0))
                self.stt('dve', xb[:, oc, :], p[:, :], self.mod[:, 16 + oc, c:c + 1], xb[:, oc, :], ALU.mult, ALU.add,
                         r=[p, self.mod, xb], pw=[xb])
            for q in range(4):
                self.ld(self.xT_v[:, :, blk * NT + q * 128: blk * NT + (q + 1) * 128], xb[:, :, q * 128:(q + 1) * 128],
                        r=[xb], w=[tks[q]], eng='pool')

        NTL = TT // 128
        hfq = {}
        for t in range(NTL + 2):
            if t < NTL:
                hfq[t] = pA(t)
            if 0 <= t - 1 < NTL:
                pB(t - 1, hfq.pop(t - 1))
                if (t - 1) % 4 == 3:
                    pass
            if 0 <= t - 2 < NTL and (t - 2) % 4 == 3:
                pC((t - 2) // 4)

    def tr_(self, out, in_, r=(), w=(), pw=()):
        n = in_.shape[0]
        idn = self.cst.ap[0:n, 0, 0:n]
        self.S.op('pe', lambda E: E.transpose(out, in_, idn), reads=list(r) + [self.cst], writes=w, pw=pw)

    def stage_in(self):
        self.areset()
        xin = self.take([128, D], 4)
        xo = self.take([128, 8, 128], 4)
        for t in range(TT // 128):
            a = xin.get()
            self.ld(a[:], self.x_tok[t * 128:(t + 1) * 128, :], r=[self.Xtok], w=[a])
            b = xo.get()
            for h in range(2):
                p = self.pnext()
                for k in range(4):
                    kc = h * 4 + k
                    self.tr_(p[:, k * 128:(k + 1) * 128], a[:, kc * 128:(kc + 1) * 128], r=[a], w=[p] if k == 0 else (), pw=() if k == 0 else [p])
                dst = b[:, h * 4:(h + 1) * 4, :]
                src = p[:, :].rearrange("p (a b) -> p a b", a=4)
                if h == 0:
                    self.cp('dve', dst, src, r=[p], w=[b])
                else:
                    self.cp('act', dst, src, r=[p], pw=[b])
            self.ld(self.xT_v[:, :, t * 128:(t + 1) * 128], b[:], r=[b], w=[self.xT_tk[t]], eng='pool')

    def stage_out(self):
        self.areset()
        xi = self.take([128, 8, 128], 4)
        yo = self.take([128, D], 4)
        for t in range(TT // 128):
            a = xi.get()
            self.ld(a[:], self.xT_v[:, :, t * 128:(t + 1) * 128], r=[self.xT_tk[t]], w=[a])
            b = yo.get()
            for h in range(2):
                p = self.pnext()
                for k in range(4):
                    kc = h * 4 + k
                    self.tr_(p[:, k * 128:(k + 1) * 128], a[:, kc, :], r=[a], w=[p] if k == 0 else (), pw=() if k == 0 else [p])
                if h == 0:
                    self.cp('dve', b[:, 0:512], p[:, :], r=[p], w=[b])
                else:
                    self.cp('act', b[:, 512:1024], p[:, :], r=[p], pw=[b])
            self.ld(self.y_tok[t * 128:(t + 1) * 128, :], b[:], r=[b], w=[self.Ytok], eng='pool')

    def stage_mod(self, i):
        self.areset()
        wt = self.take([128, 8, 512], 2)
        wv = self.ada_w[i].rearrange("(kc p) n -> p kc n", p=128)
        mp = self.pnext()
        for n in range(12):
            w = wt.get()
            self.ld(w[:], wv[:, :, n * 512:(n + 1) * 512], r=[self.Win], w=[w])
            for jj in range(4):
                j = n * 4 + jj
                for kc in range(8):
                    self.mm(mp[:, 2 * j:2 * j + 2], w[:, kc, jj * 128:(jj + 1) * 128], self.sc[:, kc, :], kc == 0, kc == 7,
                            r=[w, self.sc], **wr(mp, j == 0 and kc == 0))
        mpv = mp[:, 0:96].rearrange("p (j c) -> p j c", c=2)
        for c in range(2):
            self.tt('dve', self.mod[:, :, c], mpv[:, :, c], self.adab[:, i, :], ALU.add, r=[mp, self.adab],
                    w=[self.mod] if c == 0 else (), pw=() if c == 0 else [self.mod])
        for wi in range(2):
            sj = 8 + 24 * wi
            for c in range(2):
                first = (wi == 0 and c == 0)
                self.stt('dve', self.modA[:, wi, :, c], self.mod[:, sj:sj + 8, c], 1.0, self.normg[:, i, wi, :], ALU.add, ALU.mult,
                         r=[self.mod, self.normg], w=[self.modA] if first else (), pw=() if first else [self.modA])

    def norm_mod(self, xb, hb, wi, c, nt, sq=None):
        sj = 24 * wi
        self.act(hb[:, :, :], xb[:, :, :], AF.Square, r=[xb], w=[hb])
        p = self.pnext()
        for kc in range(8):
            self.mm(p[:, 0:nt], self.cst.ap[:, 1, :], hb[:, kc, :], kc == 0, kc == 7, r=[self.cst, hb], **wr(p, kc == 0))
        rs = self.rstd.get()
        self.act(rs[:, 0:nt], p[:, 0:nt], AF.Sqrt, r=[p, self.epsb], w=[rs], scale=1.0 / D, bias=self.epsb[:, 0:1])
        self.recip(rs[:, 0:nt], rs[:, 0:nt], r=[rs], w=[rs])
        for kc in range(8):
            self.tt('dve' if kc % 2 == 0 else 'pool', hb[:, kc, :], xb[:, kc, :], rs[:, 0:nt], ALU.mult, r=[xb, rs],
                    w=[hb] if kc == 0 else (), pw=() if kc == 0 else [hb])
        for kc in range(8):
            self.act(hb[:, kc, :], hb[:, kc, :], AF.Identity, r=[hb, self.modA, self.mod], pw=[hb],
                     scale=self.modA[:, wi, kc, c:c + 1], bias=self.mod[:, sj + kc, c:c + 1])

    def norm_mod2(self, xtk, xap, htk, hap, tmp, wi, c, nt, first):
        sj = 24 * wi
        self.act(tmp[:, :, 0:nt], xap, AF.Square, r=[xtk], w=[tmp])
        p = self.pnext()
        for kc in range(8):
            self.mm(p[:, 0:nt], self.cst.ap[:, 1, :], tmp[:, kc, 0:nt], kc == 0, kc == 7, r=[self.cst, tmp], **wr(p, kc == 0))
        rs = self.rstd.get()
        self.act(rs[:, 0:nt], p[:, 0:nt], AF.Sqrt, r=[p, self.epsb], w=[rs], scale=1.0 / D, bias=self.epsb[:, 0:1])
        self.recip(rs[:, 0:nt], rs[:, 0:nt], r=[rs], w=[rs])
        for kc in range(8):
            self.tt('dve' if kc % 2 == 0 else 'pool', tmp[:, kc, 0:nt], xap[:, kc, :], rs[:, 0:nt], ALU.mult, r=[xtk, rs], **wr(tmp, kc == 0))
        for kc in range(8):
            self.act(hap[:, kc, :], tmp[:, kc, 0:nt], AF.Identity, r=[tmp, self.modA, self.mod], **wr(htk, first and kc == 0),
                     scale=self.modA[:, wi, kc, c:c + 1], bias=self.mod[:, sj + kc, c:c + 1])

    def stage_ffn16(self, i):
        self.areset()
        SB = 1024
        NH = SB // 512
        xbs = self.take([128, 8, SB], 1)
        hbs = self.take([128, 8, SB], 1, BF16)
        sq = self.take([128, 8, 512])
        self.rstd = self.take([128, 512], 2)
        acts = self.take([128, 22, SB], 1, BF16)
        wst = self.take([128, 8, 2, 128], 3)
        w16 = self.take([128, 8, 2, 128], 3, BF16)
        wost = self.take([128, 22, 128], 2)
        wo16 = self.take([128, 22, 128], 2, BF16)
        sg = self.take([128, 512], 2)
        wiv = self.ffn_w_in[i].rearrange("(kc p) n -> p kc n", p=128)
        wov = self.ffn_w_out[i].rearrange("(kc p) n -> p kc n", p=128)
        for sb in range(TT // SB):
            c = 0 if sb * SB < NP * TP else 1
            tks = self.xT_tk[sb * 8:(sb + 1) * 8]
            xb = xbs.get()
            self.ld(xb[:], self.xT_v[:, :, sb * SB:(sb + 1) * SB], r=tks, w=[xb], eng='pool')
            hb = hbs.get()
            for hf in range(NH):
                hs = slice(hf * 512, (hf + 1) * 512)
                self.norm_mod2(xb, xb[:, :, hs], hb, hb[:, :, hs], sq, 1, c, 512, hf == 0)
            at = acts.get()
            for j in range(22):
                ws = wst.get()
                self.ld(ws[:, :, 0, :], wiv[:, :, j * 128:(j + 1) * 128], r=[self.Win], w=[ws])
                self.ld(ws[:, :, 1, :], wiv[:, :, DFF + j * 128:DFF + (j + 1) * 128], r=[self.Win], pw=[ws])
                w = w16.get()
                self.cp('dve' if j % 2 else 'act', w[:], ws[:], r=[ws], w=[w])
                for hf in range(NH):
                    hs = slice(hf * 512, (hf + 1) * 512)
                    pg = self.pnext()
                    pu = self.pnext()
                    for kc in range(8):
                        self.mm(pg[:, :], w[:, kc, 0, :], hb[:, kc, hs], kc == 0, kc == 7, r=[w, hb], **wr(pg, kc == 0))
                    for kc in range(8):
                        self.mm(pu[:, :], w[:, kc, 1, :], hb[:, kc, hs], kc == 0, kc == 7, r=[w, hb], **wr(pu, kc == 0))
                    s_ = sg.get()
                    self.act(s_[:], pg[:, :], AF.Silu, r=[pg], w=[s_])
                    self.tt('dve', at[:, j, hs], s_[:], pu[:, :], ALU.mult, r=[s_, pu], **wr(at, j == 0 and hf == 0))
            for oc in range(8):
                ws = wost.get()
                self.ld(ws[:], wov[:, :, oc * 128:(oc + 1) * 128], r=[self.Win], w=[ws])
                w = wo16.get()
                self.cp('dve' if oc % 2 else 'act', w[:], ws[:], r=[ws], w=[w])
                for hf in range(NH):
                    hs = slice(hf * 512, (hf + 1) * 512)
                    p = self.pnext()
                    for k2 in range(22):
                        self.mm(p[:, :], w[:, k2, :], at[:, k2, hs], k2 == 0, k2 == 21, r=[w, at], **wr(p, k2 == 0))
                    self.stt('dve', xb[:, oc, hs], p[:, :], self.mod[:, 40 + oc, c:c + 1], xb[:, oc, hs], ALU.mult, ALU.add,
                             r=[p, self.mod, xb], pw=[xb])
            for q in range(SB // 128):
                self.ld(self.xT_v[:, :, sb * SB + q * 128: sb * SB + (q + 1) * 128], xb[:, :, q * 128:(q + 1) * 128],
                        r=[xb], w=[tks[q]], eng='pool')

    def stage_ffn(self, i):
        self.areset()
        NT = 512
        xbs = self.take([128, 8, NT], 2)
        hbs = self.take([128, 8, NT], 1)
        self.rstd = self.take([128, NT], 2)
        acts = self.take([128, 22, NT], 1)
        sg = self.take([128, NT], 2)
        wins = self.take([128, 8, 2, 128], 3)
        wouts = self.take([128, 22, 128], 2)
        wiv = self.ffn_w_in[i].rearrange("(kc p) n -> p kc n", p=128)
        wov = self.ffn_w_out[i].rearrange("(kc p) n -> p kc n", p=128)
        for blk in range(TT // NT):
            c = 0 if blk < (NP * TP) // NT else 1
            tks = self.xT_tk[blk * 4:(blk + 1) * 4]
            xb = xbs.get()
            self.ld(xb[:], self.xT_v[:, :, blk * NT:(blk + 1) * NT], r=tks, w=[xb], eng='pool')
            hb = hbs.get()
            self.norm_mod(xb, hb, 1, c, NT)
            at = acts.get()
            for j in range(22):
                w = wins.get()
                self.ld(w[:, :, 0, :], wiv[:, :, j * 128:(j + 1) * 128], r=[self.Win], w=[w])
                self.ld(w[:, :, 1, :], wiv[:, :, DFF + j * 128:DFF + (j + 1) * 128], r=[self.Win], pw=[w])
                pg = self.pnext()
                pu = self.pnext()
                for kc in range(8):
                    self.mm(pg[:, :], w[:, kc, 0, :], hb[:, kc, :], kc == 0, kc == 7, r=[w, hb], **wr(pg, kc == 0))
                for kc in range(8):
                    self.mm(pu[:, :], w[:, kc, 1, :], hb[:, kc, :], kc == 0, kc == 7, r=[w, hb], **wr(pu, kc == 0))
                s = sg.get()
                self.act(s[:], pg[:, :], AF.Silu, r=[pg], w=[s])
                self.tt('dve', at[:, j, :], s[:], pu[:, :], ALU.mult, r=[s, pu], w=[at] if j == 0 else (), pw=() if j == 0 else [at])
            for oc in range(8):
                w = wouts.get()
                self.ld(w[:], wov[:, :, oc * 128:(oc + 1) * 128], r=[self.Win], w=[w])
                p = self.pnext()
                for k2 in range(22):
                    self.mm(p[:, :], w[:, k2, :], at[:, k2, :], k2 == 0, k2 == 21, r=[w, at], **wr(p, k2 == 0))
                self.stt('dve', xb[:, oc, :], p[:, :], self.mod[:, 40 + oc, c:c + 1], xb[:, oc, :], ALU.mult, ALU.add,
                         r=[p, self.mod, xb], pw=[xb])
            for q in range(4):
                self.ld(self.xT_v[:, :, blk * NT + q * 128: blk * NT + (q + 1) * 128], xb[:, :, q * 128:(q + 1) * 128],
                        r=[xb], w=[tks[q]], eng='pool')


def host_consts():
    c = np.zeros((128, 8, 128), np.float32)
    c[:, 0, :] = np.eye(128)
    c[:, 1, :] = 1.0
    k = np.arange(128)[:, None]
    t = np.arange(128)[None, :]
    c[:, 2, :] = (k <= t)
    c[:, 3, :] = (k >= t)
    c[:, 4, :] = (k > t)
    c[:, 5, :] = (k < t)
    return c.reshape(128, 1024)


def rope_tables():
    rows = TS // 64
    row = np.broadcast_to(np.arange(rows)[:, None], (rows, 64)).reshape(-1)
    col = np.broadcast_to(np.arange(64)[None, :], (rows, 64)).reshape(-1)
    inv = (np.float32(10000.0) ** (-np.arange(16, dtype=np.float32) / np.float32(16))).astype(np.float32)
    ang = np.stack([row, col], axis=-1).astype(np.float32)[:, :, None] * inv
    return np.concatenate([np.cos(ang).reshape(TS, 32), np.sin(ang).reshape(TS, 32)], axis=1).astype(np.float32)


_CACHE = {}


def kernel(**inp):
    opts = inp.pop('_opts', {})
    key = repr(sorted(opts.items()))
    if key not in _CACHE:
        P = Prog(opts)
        P.build()
        _CACHE[key] = P
    P = _CACHE[key]
    f = lambda a: np.ascontiguousarray(np.asarray(a, dtype=np.float32))
    xp = f(inp['x_prompt'])
    xs = f(inp['x_sample'])
    c = f(inp['c'])
    c_ctx = f(inp['c_ctx'])
    ada_b = f(inp['ada_b'])
    norm_g = f(inp['norm_g'])
    shared = {
        'consts': host_consts(),
        'ada_w': f(inp['ada_w']),
        'ada_bT': f(ada_b.reshape(4, 48, 128).transpose(2, 0, 1)),
        'normgT': f(norm_g.reshape(4, 2, 8, 128).transpose(3, 0, 1, 2)),
        'ffn_w_in': f(inp['ffn_w_in']),
        'ffn_w_out': f(inp['ffn_w_out']),
        'mlstm_w_in': f(inp['mlstm_w_in'][0]),
        'mlstm_w_out': f(inp['mlstm_w_out'][0]),
        'mlstm_gate_b': f(inp['mlstm_gate_b'].reshape(1, 32)),
        'mlstm_norm_g': f(inp['mlstm_norm_g'].reshape(1, 128)),
    }
    shared.update({
        'diff_w_in': f(inp['diff_w_in'][0]),
        'diff_w_out': f(inp['diff_w_out'][0]),
        'diff_qkg': f(np.concatenate([inp['diff_q_norm_g'][0], inp['diff_k_norm_g'][0]]).reshape(1, 128)),
        'diff_lambda': f(inp['diff_lambda'][0].reshape(1, 256)),
        'diff_subln_g': f(inp['diff_subln_g'][0].reshape(1, 128)),
        'rope_cs': rope_tables(),
    })
    cw = f(inp['gdn_conv_w'])
    shared.update({
        'gdn_w_in': f(inp['gdn_w_in']),
        'gdn_w_out': f(inp['gdn_w_out']),
        'gdn_convT': f(cw.reshape(2, 5, 24, 128).transpose(3, 0, 2, 1)),
        'gdn_a_log': f(inp['gdn_a_log'].reshape(2, 1, 16)),
        'gdn_dt_bias': f(inp['gdn_dt_bias'].reshape(2, 1, 16)),
        'gdn_norm_g': f(inp['gdn_norm_g'].reshape(2, 1, 128)),
    })
    stS = f(inp['state_delta'])
    ck = f(inp['cache_diff_k'])
    cv = f(inp['cache_diff_v'])
    stC = f(inp['state_mlstm_C'])
    stn = f(inp['state_mlstm_n'])
    stm = f(inp['state_mlstm_m'])
    in_maps = []
    for k in range(NCORE):
        m = dict(shared)
        m['x_tok'] = f(np.concatenate([xp[NP * k:NP * (k + 1)].reshape(NP * TP, D), xs[k]], axis=0))
        cond = np.stack([c_ctx, c[k]], axis=-1)
        m['condT'] = f(cond.reshape(8, 128, 2).transpose(1, 0, 2))
        m['st_S'] = f(stS[k])
        m['ctx_k'] = f(ck[k, 0])
        m['ctx_v'] = f(cv[k, 0])
        m['st_C'] = f(stC[k, 0])
        m['st_n'] = f(stn[k, 0].transpose(0, 2, 1))
        m['st_m'] = f(stm[k, 0].reshape(1, 16))
        in_maps.append({n: m[n] for n in P.din})
    res = run_bass_kernel_spmd(P.nc, in_maps, core_ids=list(range(NCORE)))
    R = res.results
    y = np.stack([r['y_tok'] for r in R])
    y_prompt = y[:, :NP * TP].reshape(NCORE * NP, TP, D)
    y_sample = y[:, NP * TP:]
    outs = [y_prompt, y_sample]
    if 'newS' in P.dout:
        outs.append(np.stack([r['newS'] for r in R]).reshape(NCORE * NP, 2, 2, 8, 128, 128))
    if 'newC' in P.dout:
        outs.append(np.stack([r['newC'] for r in R]).reshape(NCORE * NP, 1, 2, 8, 64, 128))
        outs.append(np.ascontiguousarray(np.stack([r['newn'] for r in R]).reshape(NCORE * NP, 1, 2, 64, 8).transpose(0, 1, 2, 4, 3)))
        outs.append(np.stack([r['newm'] for r in R]).reshape(NCORE * NP, 1, 2, 8))
    if 'newk' in P.dout:
        outs.append(np.stack([r['newk'] for r in R]).reshape(NCORE * NP, 1, 8, 2, TP, 64))
        outs.append(np.stack([r['newv'] for r in R]).reshape(NCORE * NP, 1, 8, TP, 128))
    return tuple(outs)
```

```python
import numpy as np
from contextlib import ExitStack
import concourse.bass as bass
import concourse.mybir as mybir
from concourse.bass_utils import run_bass_kernel_spmd

F32 = mybir.dt.float32
BF16 = mybir.dt.bfloat16
AF = mybir.ActivationFunctionType
ALU = mybir.AluOpType
AX = mybir.AxisListType

ENGS = ('pe', 'act', 'dve', 'pool', 'sp')
NDS = 40

D = 1024
NCORE = 8
NP = 4
TP = 256
TS = 2048
TT = NP * TP + TS
DFF = 2816
EPS = 1e-6


class Tk:
    __slots__ = ('ap', 'lw', 'rd', 'rp', 'name')

    def __init__(self, ap, name=''):
        self.ap = ap
        self.lw = {}
        self.rd = {}
        self.rp = {}
        self.name = name

    def __getitem__(self, idx):
        return self.ap[idx]


class Sched:
    def __init__(self, nc, es):
        self.nc = nc
        self.es = es
        self.q = {e: [] for e in ENGS}
        self.sem = {e: es.enter_context(nc.semaphore("s_" + e)) for e in ENGS}
        self.cnt = {e: 0 for e in ENGS}
        self.seen = {e: {} for e in ENGS}
        self.dsem = [es.enter_context(nc.semaphore("d%d" % i)) for i in range(NDS)]
        self.dcnt = [0] * NDS
        self.dnext = 0
        self.nins = 0
        self.uid = 0

    def sb(self, shape, dt=F32, name=None):
        self.uid += 1
        name = name or "t%d" % self.uid
        t = self.es.enter_context(self.nc.sbuf_tensor(name, list(shape), dt))
        return Tk(t, name)

    def ps(self, shape, dt=F32, name=None):
        self.uid += 1
        name = name or "p%d" % self.uid
        t = self.es.enter_context(self.nc.psum_tensor(name, list(shape), dt))
        return Tk(t, name)

    def _wait(self, eng, d):
        k = d[0]
        if eng == 'pe' and k == ('e', 'pe'):
            return
        seen = self.seen[eng]
        if seen.get(k, 0) >= d[2]:
            return
        seen[k] = d[2]
        self.q[eng].append(lambda E, d=d: E.wait_ge(d[1], d[2]))
        self.nins += 1

    def _deps(self, eng, reads, writes, pw):
        deps = {}

        def add(d):
            k = d[0]
            if k not in deps or deps[k][2] < d[2]:
                deps[k] = d
        for t in reads:
            for d in t.lw.values():
                add(d)
        for t in writes:
            for d in t.lw.values():
                add(d)
            for d in t.rd.values():
                add(d)
        for t in pw:
            for d in t.rd.values():
                add(d)
            for d in t.rp.values():
                add(d)
        for d in deps.values():
            self._wait(eng, d)

    def _mark(self, me, reads, writes, pw):
        for t in reads:
            t.rd[me[0]] = me
        for t in writes:
            t.lw = {me[0]: me}
            t.rp = t.rd
            t.rd = {}
        for t in pw:
            t.lw[me[0]] = me

    def op(self, eng, fn, reads=(), writes=(), pw=()):
        self._deps(eng, reads, writes, pw)
        self.cnt[eng] += 1
        sem = self.sem[eng]
        me = (('e', eng), sem, self.cnt[eng])
        self.q[eng].append(lambda E: fn(E).then_inc(sem, 1))
        self.nins += 1
        self._mark(me, reads, writes, pw)

    def dma(self, eng, out_ap, in_ap, reads=(), writes=(), pw=()):
        slot = self.dnext
        self.dnext = (slot + 1) % NDS
        ds = self.dsem[slot]
        self._deps(eng, reads, writes, pw)
        if self.dcnt[slot] > 0:
            self._wait(eng, (('d', slot), ds, 16 * self.dcnt[slot]))
        self.dcnt[slot] += 1
        me = (('d', slot), ds, 16 * self.dcnt[slot])
        self.q[eng].append(lambda E: E.dma_start(out=out_ap, in_=in_ap).then_inc(ds, 16))
        self.nins += 1
        self._mark(me, reads, writes, pw)

    def barrier(self):
        for e in ENGS:
            for o in ENGS:
                if o != e and self.cnt[o] > 0:
                    self._wait(e, (('e', o), self.sem[o], self.cnt[o]))
            for i in range(NDS):
                if self.dcnt[i] > 0:
                    self._wait(e, (('d', i), self.dsem[i], 16 * self.dcnt[i]))

    def emit(self):
        self.barrier()
        q = self.q
        with self.nc.Block() as block:
            @block.tensor
            def _(E):
                for f in q['pe']:
                    f(E)

            @block.scalar
            def _(E):
                for f in q['act']:
                    f(E)

            @block.vector
            def _(E):
                for f in q['dve']:
                    f(E)

            @block.gpsimd
            def _(E):
                for f in q['pool']:
                    f(E)

            @block.sync
            def _(E):
                for f in q['sp']:
                    f(E)


def wr(t, first):
    return {'w': [t]} if first else {'pw': [t]}


class Rot:
    def __init__(self, tiles):
        self.t = tiles
        self.i = 0

    def get(self):
        t = self.t[self.i % len(self.t)]
        self.i += 1
        return t


ARENA_COLS = 50000


class Prog:
    def __init__(self, opts):
        self.opts = opts
        self.nc = bass.Bass("TRN2", target_bir_lowering=False)
        self.es = ExitStack()
        self.din = {}
        self.dout = {}

    def inp(self, name, shape):
        t = self.nc.dram_tensor(name, list(shape), F32, kind="ExternalInput").ap()
        self.din[name] = t
        return t

    def outp(self, name, shape):
        t = self.nc.dram_tensor(name, list(shape), F32, kind="ExternalOutput").ap()
        self.dout[name] = t
        return t

    def scratch(self, name, shape):
        return self.nc.dram_tensor(name, list(shape), F32, kind="Internal").ap()

    def areset(self):
        import inspect
        self.S.barrier()
        self.apos = 0
        if not hasattr(self, 'stages'):
            self.stages = []
        self.stages.append((inspect.stack()[1].function, self.S.cnt['pe']))
        if hasattr(self, 'mark'):
            mk = self.mark
            self.act(mk[:, 0:1], mk[:, 0:1], AF.Sign, r=[mk], w=[mk])

    def take(self, shape, n=None, dt=F32):
        cols = int(np.prod(shape[1:]))
        c32 = cols if dt == F32 else (cols + 1) // 2
        out = []
        for _ in range(n or 1):
            assert self.apos + c32 <= ARENA_COLS, ("arena overflow", self.apos, c32)
            ap = self.arena[0:shape[0], self.apos:self.apos + c32]
            if dt != F32:
                ap = ap.bitcast(dt)[:, 0:cols]
            if len(shape) == 3:
                ap = ap.rearrange("p (a b) -> p a b", a=shape[1])
            elif len(shape) == 4:
                ap = ap.rearrange("p (a b c) -> p a b c", a=shape[1], b=shape[2])
            self.apos += c32
            out.append(Tk(ap))
        return out[0] if n is None else Rot(out)

    def pnext(self):
        p = self.psum[self.pi % 8]
        self.pi += 1
        return p

    def pns(self):
        return self.pnext()

    def mm(self, out, lhsT, rhs, start, stop, r=(), w=(), pw=()):
        self.S.op('pe', lambda E: E.matmul(out, lhsT=lhsT, rhs=rhs, start=start, stop=stop), reads=r, writes=w, pw=pw)

    def tr(self, out, in_, r=(), w=(), pw=()):
        ident = self.ident
        n = in_.shape[0]
        self.S.op('pe', lambda E: E.transpose(out, in_, ident[0:n, 0:n]), reads=list(r) + [ident], writes=w, pw=pw)

    def act(self, out, in_, func, r=(), w=(), pw=(), bias=None, scale=None, accum=None):
        kw = {}
        if bias is not None:
            kw['bias'] = bias
        if scale is not None:
            kw['scale'] = scale
        if accum is not None:
            kw['accum_out'] = accum
        self.S.op('act', lambda E: E.activation(out=out, in_=in_, func=func, **kw), reads=r, writes=w, pw=pw)

    def tt(self, eng, out, a, b, op, r=(), w=(), pw=()):
        self.S.op(eng, lambda E: E.tensor_tensor(out=out, in0=a, in1=b, op=op), reads=r, writes=w, pw=pw)

    def ts(self, eng, out, a, s1, s2, op0, op1=None, r=(), w=(), pw=()):
        if op1 is None:
            self.S.op(eng, lambda E: E.tensor_scalar(out=out, in0=a, scalar1=s1, scalar2=None, op0=op0), reads=r, writes=w, pw=pw)
        else:
            self.S.op(eng, lambda E: E.tensor_scalar(out=out, in0=a, scalar1=s1, scalar2=s2, op0=op0, op1=op1), reads=r, writes=w, pw=pw)

    def stt(self, eng, out, a, s, b, op0, op1, r=(), w=(), pw=()):
        self.S.op(eng, lambda E: E.scalar_tensor_tensor(out=out, in0=a, scalar=s, in1=b, op0=op0, op1=op1), reads=r, writes=w, pw=pw)

    def cp(self, eng, out, in_, r=(), w=(), pw=()):
        if eng == 'act':
            self.S.op('act', lambda E: E.copy(out=out, in_=in_), reads=r, writes=w, pw=pw)
        else:
            self.S.op(eng, lambda E: E.tensor_copy(out=out, in_=in_), reads=r, writes=w, pw=pw)

    def recip(self, out, in_, r=(), w=(), pw=()):
        self.S.op('dve', lambda E: E.reciprocal(out=out, in_=in_), reads=r, writes=w, pw=pw)

    def memset(self, eng, ap, val, w=(), pw=()):
        self.S.op(eng, lambda E: E.memset(ap, val), writes=w, pw=pw)

    def ld(self, out, in_, r=(), w=(), pw=(), eng='sp'):
        self.S.dma(eng, out, in_, reads=r, writes=w, pw=pw)

    def build(self):
        nc = self.nc
        o = self.opts
        with self.es:
            S = self.S = Sched(nc, self.es)
            self.arena = self.es.enter_context(nc.sbuf_tensor("arena", [128, ARENA_COLS], F32))
            self.psum = [S.ps([128, 512], name="ps%d" % i) for i in range(8)]
            self.pi = 0
            self.apos = 0
            self.psmall = [Tk(self.psum[i // 2].ap[:, (i % 2) * 256:(i % 2) * 256 + 256]) for i in range(16)]
            self.psi = 0
            self.x_tok = self.inp("x_tok", [TT, D])
            self.y_tok = self.outp("y_tok", [TT, D])
            self.xT = self.scratch("xT", [8, 128, TT])
            self.xT_v = self.xT.rearrange("c p t -> p c t")
            self.xT_tk = [Tk(None, "xT%d" % i) for i in range(TT // 128)]
            self.Xtok = Tk(None)
            self.Ytok = Tk(None)
            self.Win = Tk(None)
            consts = self.inp("consts", [128, 8 * 128])
            condT = self.inp("condT", [128, 8, 2])
            self.ada_w = self.inp("ada_w", [4, D, 6 * D])
            ada_bT = self.inp("ada_bT", [128, 4, 48])
            normgT = self.inp("normgT", [128, 4, 2, 8])
            self.ffn_w_in = self.inp("ffn_w_in", [4, D, 2 * DFF])
            self.ffn_w_out = self.inp("ffn_w_out", [4, DFF, D])
            self.cst = S.sb([128, 8, 128], name="cst")
            self.ld(self.cst[:], consts.rearrange("p (a b) -> p a b", a=8), r=[self.Win], w=[self.cst])
            self.ones = self.cst.ap[:, 1, :]
            self.sc = S.sb([128, 8, 2], name="sc")
            self.ld(self.sc[:], condT, r=[self.Win], w=[self.sc])
            self.act(self.sc[:], self.sc[:], AF.Silu, r=[self.sc], w=[self.sc])
            self.adab = S.sb([128, 4, 48], name="adab")
            self.ld(self.adab[:], ada_bT, r=[self.Win], w=[self.adab])
            self.normg = S.sb([128, 4, 2, 8], name="normg")
            self.ld(self.normg[:], normgT, r=[self.Win], w=[self.normg])
            self.mod = S.sb([128, 48, 2], name="mod")
            self.modA = S.sb([128, 2, 8, 2], name="modA")
            self.epsb = S.sb([128, 1], name="epsb")
            self.memset('pool', self.epsb[:], EPS, w=[self.epsb])
            self.mark = S.sb([128, 1], name="mark")
            self.memset('pool', self.mark[:], 1.0, w=[self.mark])

            self.stage_in()
            self.bf = o.get('bf16', True)
            mixers = o.get('mixers', (0, 1, 2))
            self.setup_mixers(mixers)
            for i in range(o.get('depth', 4)):
                self.stage_mod(i)
                if i % 3 == 0 and 0 in mixers:
                    self.gdn(i)
                if i % 3 == 1 and 1 in mixers:
                    self.mlstm(i)
                if i % 3 == 2 and 2 in mixers:
                    self.diffattn(i)
                if o.get('ffn', True):
                    if self.bf:
                        self.stage_ffn16(i)
                    else:
                        self.stage_ffn(i)
            self.stage_out()
            S.emit()
        return nc

    def setup_mixers(self, mixers):
        S = self.S
        if 1 in mixers:
            self.ml_w_in = self.inp("mlstm_w_in", [D, 3104])
            self.ml_w_out = self.inp("mlstm_w_out", [D, D])
            ml_gb = self.inp("mlstm_gate_b", [1, 32])
            ml_ng = self.inp("mlstm_norm_g", [1, 128])
            self.st_C = self.inp("st_C", [2, 8, 64, 128])
            self.st_n = self.inp("st_n", [2, 64, 8])
            self.st_m = self.inp("st_m", [1, 16])
            self.newC = self.outp("newC", [NP, 2, 8, 64, 128])
            self.newn = self.outp("newn", [NP, 2, 64, 8])
            self.newm = self.outp("newm", [NP, 2, 8, 1])
            self.ml_gb = S.sb([128, 32], name="ml_gb")
            self.ld(self.ml_gb[:], ml_gb.partition_broadcast(128), r=[self.Win], w=[self.ml_gb])
            self.ml_ng = S.sb([128, 128], name="ml_ng")
            self.ld(self.ml_ng[:], ml_ng.partition_broadcast(128), r=[self.Win], w=[self.ml_ng])
        self.Oout = Tk(None)
        self.qkT = self.scratch("qkT", [24, 128, TT])
        self.qkT_v = self.qkT.rearrange("c p t -> p c t")
        self.qkT_h = self.qkT[0:8].rearrange("c (two p) t -> p (c two) t", two=2)
        self.ktok = self.scratch("ktok", [TT, 2048])
        self.vtok = self.scratch("vtok", [TT, 1024])
        self.otok = self.scratch("otok", [TT, 1024])
        self.gtok = self.scratch("gtok", [TT, 32])
        self.hdir = [self.scratch("hdir%d" % d, [TT, 1024]) for d in range(2)]
        self.proj_tk = [Tk(None) for _ in range(TT // 128)]
        self.prep_tk = [Tk(None) for _ in range(TT // 128)]
        self.hdir_tk = [[Tk(None) for _ in range(TT // 64)] for d in range(2)]
        if 2 in mixers:
            self.df_w_in = self.inp("diff_w_in", [D, 3072])
            self.df_w_out = self.inp("diff_w_out", [D, D])
            df_g = self.inp("diff_qkg", [1, 128])
            df_lam = self.inp("diff_lambda", [1, 256])
            df_sg = self.inp("diff_subln_g", [1, 128])
            self.rope_cs = self.inp("rope_cs", [TS, 64])
            self.ctx_k = self.inp("ctx_k", [8, 2, 256, 64])
            self.ctx_v = self.inp("ctx_v", [8, 256, 128])
            self.newk = self.outp("newk", [NP, 8, 2, TP, 64])
            self.newv = self.outp("newv", [NP, 8, TP, 128])
            self.df_g = S.sb([128, 2, 64], name="df_g")
            self.ld(self.df_g[:], df_g.rearrange("o (a b) -> o a b", a=2).partition_broadcast(128), r=[self.Win], w=[self.df_g])
            self.df_sg = S.sb([128, 128], name="df_sg")
            self.ld(self.df_sg[:], df_sg.partition_broadcast(128), r=[self.Win], w=[self.df_sg])
            lam_init = 0.8 - 0.6 * float(np.exp(-0.3 * 2))
            self.ts('dve', self.df_sg[:], self.df_sg[:], 1.0 - lam_init, None, ALU.mult, r=[self.df_sg], w=[self.df_sg])
            lm = S.sb([128, 4, 64], name="df_lm")
            self.ld(lm[:], df_lam.rearrange("o (a b) -> o a b", a=4).partition_broadcast(128), r=[self.Win], w=[lm])
            l2 = S.sb([128, 2, 64], name="df_l2")
            self.tt('dve', l2[:, 0, :], lm[:, 0, :], lm[:, 1, :], ALU.mult, r=[lm], w=[l2])
            self.tt('dve', l2[:, 1, :], lm[:, 2, :], lm[:, 3, :], ALU.mult, r=[lm], pw=[l2])
            self.nlam = S.sb([128, 4], name="nlam")
            nl = self.nlam
            S.op('dve', lambda E: E.tensor_reduce(out=nl[:, 0:2], in_=l2[:], axis=AX.X, op=ALU.add), reads=[l2], writes=[nl])
            self.act(nl[:, 0:2], nl[:, 0:2], AF.Exp, r=[nl], w=[nl])
            self.tt('dve', nl[:, 2:3], nl[:, 1:2], nl[:, 0:1], ALU.subtract, r=[nl], pw=[nl])
            self.ts('dve', nl[:, 3:4], nl[:, 2:3], -lam_init, None, ALU.add, r=[nl], pw=[nl])
            self.ctxkT = self.scratch("ctxkT", [8, 128, 256])
            self.ctx_tk = Tk(None)
        if 0 in mixers:
            self.gd_w_in = self.inp("gdn_w_in", [2, D, 4128])
            self.gd_w_out = self.inp("gdn_w_out", [2, D, D])
            gd_cw = self.inp("gdn_convT", [128, 2, 24, 5])
            gd_al = self.inp("gdn_a_log", [2, 1, 16])
            gd_dt = self.inp("gdn_dt_bias", [2, 1, 16])
            gd_ng = self.inp("gdn_norm_g", [2, 1, 128])
            self.st_S = self.inp("st_S", [2, 2, 8, 128, 128])
            self.newS = self.outp("newS", [NP, 2, 2, 8, 128, 128])
            self.gd_cw = S.sb([128, 2, 24, 5], name="gd_cw")
            self.ld(self.gd_cw[:], gd_cw, r=[self.Win], w=[self.gd_cw])
            self.gd_nea = S.sb([128, 2, 16], name="gd_nea")
            self.gd_dt = S.sb([128, 2, 16], name="gd_dt")
            self.gd_ng = S.sb([128, 2, 128], name="gd_ng")
            for j in range(2):
                self.ld(self.gd_nea[:, j, :], gd_al[j].partition_broadcast(128), r=[self.Win], **wr(self.gd_nea, j == 0))
                self.ld(self.gd_dt[:, j, :], gd_dt[j].partition_broadcast(128), r=[self.Win], **wr(self.gd_dt, j == 0))
                self.ld(self.gd_ng[:, j, :], gd_ng[j].partition_broadcast(128), r=[self.Win], **wr(self.gd_ng, j == 0))
            self.act(self.gd_nea[:], self.gd_nea[:], AF.Exp, r=[self.gd_nea], w=[self.gd_nea])
            self.ts('dve', self.gd_nea[:], self.gd_nea[:], -1.0, None, ALU.mult, r=[self.gd_nea], w=[self.gd_nea])

    def proj_stage(self, i, w_in, ncol, fm, tm, col_lo=0):
        self.areset()
        NT = 512
        xbs = self.take([128, 8, NT], 1)
        hbs = self.take([128, 8, NT], 1)
        self.rstd = self.take([128, NT], 2)
        W = self.take([128, 8, ncol])
        wv_ = w_in.rearrange("(kc p) n -> p kc n", p=128)
        self.ld(W[:, 0:4, :], wv_[:, 0:4, col_lo:col_lo + ncol], r=[self.Win], w=[W])
        self.ld(W[:, 4:8, :], wv_[:, 4:8, col_lo:col_lo + ncol], r=[self.Win], pw=[W], eng='pool')
        fm = [(a - col_lo, b, c_, d_, e_) for (a, b, c_, d_, e_) in fm]
        tm = [(a - col_lo, b, c_) for (a, b, c_) in tm]
        ofm = self.take([128, NT], 3)
        otm = self.take([128, 512], 3)
        for blk in range(TT // NT):
            c = 0 if blk < (NP * TP) // NT else 1
            tks = self.xT_tk[blk * 4:(blk + 1) * 4]
            ptk = self.proj_tk[blk * 4:(blk + 1) * 4]
            xb = xbs.get()
            self.ld(xb[:], self.xT_v[:, :, blk * NT:(blk + 1) * NT], r=tks, w=[xb], eng='pool')
            hb = hbs.get()
            self.norm_mod(xb, hb, 0, c, NT)
            n = 0
            for (col0, nch, dstv, ch0, scale) in fm:
                for oc in range(nch):
                    p = self.pnext()
                    for kc in range(8):
                        self.mm(p[:, :], W[:, kc, col0 + oc * 128: col0 + (oc + 1) * 128], hb[:, kc, :], kc == 0, kc == 7, r=[W, hb], **wr(p, kc == 0))
                    ot = ofm.get()
                    if n % 2 == 0:
                        self.act(ot[:], p[:, :], AF.Copy, r=[p], w=[ot], scale=scale)
                    else:
                        self.ts('dve', ot[:], p[:, :], scale, None, ALU.mult, r=[p], w=[ot])
                    n += 1
                    self.ld(dstv[:, ch0 + oc, blk * NT:(blk + 1) * NT], ot[:], r=[ot], pw=ptk, eng='pool')
            for q in range(4):
                t0 = blk * NT + q * 128
                for (col0, ncols, dst) in tm:
                    for g0 in range(0, ncols, 512):
                        gw = min(512, ncols - g0)
                        p = self.pnext()
                        for kc in range(8):
                            self.mm(p[:, 0:gw], hb[:, kc, q * 128:(q + 1) * 128], W[:, kc, col0 + g0: col0 + g0 + gw], kc == 0, kc == 7, r=[W, hb], **wr(p, kc == 0))
                        ot = otm.get()
                        if n % 2 == 0:
                            self.cp('act', ot[:, 0:gw], p[:, 0:gw], r=[p], w=[ot])
                        else:
                            self.cp('dve', ot[:, 0:gw], p[:, 0:gw], r=[p], w=[ot])
                        n += 1
                        self.ld(dst[t0:t0 + 128, g0:g0 + gw], ot[:, 0:gw], r=[ot], pw=[ptk[q]], eng='sp')

    def load_w16(self, W16, w_view, ncol, col_lo=0, piece=512, eng2='pool'):
        n = 0
        for c0 in range(0, ncol, piece):
            cw = min(piece, ncol - c0)
            st = self.wstage.get()
            self.ld(st[:, :, 0:cw], w_view[:, :, col_lo + c0:col_lo + c0 + cw], r=[self.Win], w=[st], eng='sp' if n % 2 == 0 else eng2)
            if n % 2 == 0:
                self.cp('dve', W16[:, :, c0:c0 + cw], st[:, :, 0:cw], r=[st], **wr(W16, c0 == 0))
            else:
                self.cp('act', W16[:, :, c0:c0 + cw], st[:, :, 0:cw], r=[st], **wr(W16, c0 == 0))
            n += 1

    def proj_stage16(self, i, w_in, ncol, fm, tm):
        self.areset()
        NT = 512
        xbs = self.take([128, 8, NT], 2)
        hbs = self.take([128, 8, NT], 2, BF16)
        sq = self.take([128, 8, NT])
        self.rstd = self.take([128, NT], 2)
        W = self.take([128, 8, ncol], None, BF16)
        self.wstage = self.take([128, 8, 512], 2)
        self.load_w16(W, w_in.rearrange("(kc p) n -> p kc n", p=128), ncol)
        ofm = self.take([128, NT], 3)
        otm = self.take([128, 512], 3)
        def pA(blk):
            c = 0 if blk < (NP * TP) // NT else 1
            tks = self.xT_tk[blk * 4:(blk + 1) * 4]
            xb = xbs.get()
            self.ld(xb[:], self.xT_v[:, :, blk * NT:(blk + 1) * NT], r=tks, w=[xb], eng='pool')
            hb = hbs.get()
            self.norm_mod2(xb, xb[:, :, :], hb, hb[:, :, :], sq, 0, c, NT, True)
            return hb

        def pB(blk, hb):
            n = 0
            for (col0, nch, dstv, ch0, scale) in fm:
                for oc in range(nch):
                    p = self.pnext()
                    for kc in range(8):
                        self.mm(p[:, :], W[:, kc, col0 + oc * 128: col0 + (oc + 1) * 128], hb[:, kc, :], kc == 0, kc == 7, r=[W, hb], **wr(p, kc == 0))
                    ot = ofm.get()
                    if n % 2 == 0:
                        self.act(ot[:], p[:, :], AF.Copy, r=[p], w=[ot], scale=scale)
                    else:
                        self.ts('dve', ot[:], p[:, :], scale, None, ALU.mult, r=[p], w=[ot])
                    n += 1
                    self.ld(dstv[:, ch0 + oc, blk * NT:(blk + 1) * NT], ot[:], r=[ot], pw=[self.Oout], eng='pool')
            for q in range(4):
                t0 = blk * NT + q * 128
                for (col0, ncols, dst) in tm:
                    for g0 in range(0, ncols, 512):
                        gw = min(512, ncols - g0)
                        p = self.pnext()
                        for kc in range(8):
                            self.mm(p[:, 0:gw], hb[:, kc, q * 128:(q + 1) * 128], W[:, kc, col0 + g0: col0 + g0 + gw], kc == 0, kc == 7, r=[W, hb], **wr(p, kc == 0))
                        ot = otm.get()
                        if n % 2 == 0:
                            self.cp('act', ot[:, 0:gw], p[:, 0:gw], r=[p], w=[ot])
                        else:
                            self.cp('dve', ot[:, 0:gw], p[:, 0:gw], r=[p], w=[ot])
                        n += 1
                        self.ld(dst[t0:t0 + 128, g0:g0 + gw], ot[:, 0:gw], r=[ot], pw=[self.Oout], eng='sp')


        NBK = TT // NT
        hb_next = pA(0)
        for blk in range(NBK):
            hb_cur = hb_next
            if blk + 1 < NBK:
                hb_next = pA(blk + 1)
            pB(blk, hb_cur)

    def mlstm(self, i):
        o = self.opts
        (self.proj_stage16 if self.bf else self.proj_stage)(i, self.ml_w_in, 3104,
                        fm=[(0, 4, self.qkT_v, 0, 0.125), (512, 4, self.qkT_v, 4, 1.0)],
                        tm=[(512, 512, self.ktok), (1024, 1024, self.vtok), (2048, 1024, self.otok), (3072, 32, self.gtok)])
        self.mlstm_scan()
        self.mixer_post(i, self.ml_w_out, self.ml_ng, self.ml_ng[:], self.hdir, 'sigmoid')

    def mlstm_scan(self):
        self.areset()
        cst = self.cst
        Tri = [cst.ap[0:64, 2, 0:64], cst.ap[0:64, 3, 0:64]]
        Str = [cst.ap[0:64, 4, 0:64], cst.ap[0:64, 5, 0:64]]
        ones64 = cst.ap[0:64, 1, 0:64]
        HG = [(0, 3), (3, 3), (6, 2)]
        qks = self.take([64, 16, 64], 6)
        kts = self.take([64, 512], 6)
        v1s = self.take([64, 8, 129], 6)
        for v1 in v1s.t:
            self.memset('pool', v1[:, :, 128:129], 1.0, pw=[v1])
        gts = self.take([64, 32], 6)
        gps = self.take([64, 64], 6)
        tot8s = self.take([64, 8, 129], 5)
        tl8s = self.take([64, 8, 64], 3)
        E8s = self.take([64, 8, 64], 6)
        kw8s = self.take([64, 8, 64], 6)
        dns = self.take([64, 16], 6)
        houts = self.take([64, 8, 128], 5)
        msm = self.take([8, 8], 2)
        emf = self.take([64, 8], 2)
        GBs = self.take([8, 2], 6)
        em0 = self.take([64, 16])
        co8s = self.take([64, 8, 129], 2)
        n0s = self.take([64, 8], 4)
        ia8s = self.take([64, 8, 129], 3)
        Cst = [[self.take([64, 8, 129]) for d in range(2)] for k in range(2)]
        Mst = [self.take([8, 1]) for d in range(2)]
        streams = [dict(tok0=NP * TP, nch=TS // 64, pidx=-1, start=0, C=Cst[0])]
        for p in range(NP):
            streams.append(dict(tok0=p * TP, nch=TP // 64, pidx=p, start=8 * p, C=Cst[1]))

        def init(st):
            if st['pidx'] >= 0:
                for d in range(2):
                    self.memset('pool', st['C'][d][:], 0.0, w=[st['C'][d]])
                    self.memset('pool', Mst[d][:], 0.0, w=[Mst[d]])
            else:
                self.ld(em0[:], self.st_m.partition_broadcast(64), r=[self.Win], w=[em0])
                self.act(em0[:], em0[:], AF.Exp, r=[em0], w=[em0])
                for d in range(2):
                    T_ = st['C'][d]
                    self.ld(T_[:, :, 0:128], self.st_C[d].rearrange("h k e -> k h e"), r=[self.Win], w=[T_])
                    n0 = n0s.get()
                    self.ld(n0[:], self.st_n[d], r=[self.Win], w=[n0], eng='pool')
                    self.cp('dve', T_[:, :, 128], n0[:], r=[n0], pw=[T_])
                    self.tt('pool', T_[:], T_[:], em0[:, d * 8:d * 8 + 8].unsqueeze(2).to_broadcast([64, 8, 129]), ALU.mult, r=[T_, em0], w=[T_])

        def final(st):
            pidx = st['pidx']
            if pidx < 0:
                return
            for d in range(2):
                dm = msm.get()
                self.ts('dve', dm[:], cst.ap[0:8, 0, 0:8], Mst[d][:, 0:1], None, ALU.mult, r=[cst, Mst[d]], w=[dm])
                pm = self.pnext()
                self.mm(pm[0:64, 0:8], cst.ap[0:8, 1, 0:64], dm[:], True, True, r=[cst, dm], w=[pm])
                ef = emf.get()
                self.act(ef[:], pm[0:64, 0:8], AF.Exp, r=[pm], w=[ef], scale=-1.0)
                self.ld(self.newm[pidx, d], Mst[d][:], r=[Mst[d]], w=[self.Oout], eng='pool')
                co = co8s.get()
                self.tt('dve', co[:], st['C'][d][:], ef[:].unsqueeze(2).to_broadcast([64, 8, 129]), ALU.mult, r=[st['C'][d], ef], w=[co])
                self.ld(self.newC[pidx, d].rearrange("h k e -> k h e"), co[:, :, 0:128], r=[co], pw=[self.Oout], eng='sp')
                n1 = n0s.get()
                self.cp('dve', n1[:], co[:, :, 128], r=[co], w=[n1])
                self.ld(self.newn[pidx, d], n1[:], r=[n1], pw=[self.Oout], eng='pool')

        def build_ctx(st, d, t0):
            pidx = st['pidx']
            ptk = [self.proj_tk[t0 // 128]]
            qk = qks.get()
            self.ld(qk[:], self.qkT_h[:, :, t0:t0 + 64], r=ptk, w=[qk])
            kt = kts.get()
            self.ld(kt[:], self.ktok[t0:t0 + 64, 0:512], r=ptk, w=[kt], eng="pool")
            v1 = v1s.get()
            self.ld(v1[:, :, 0:128], self.vtok[t0:t0 + 64, :].rearrange("t (h e) -> t h e", h=8), r=ptk, pw=[v1])
            gt = gts.get()
            self.ld(gt[:], self.gtok[t0:t0 + 64, :], r=ptk, w=[gt], eng='pool')
            gp = gps.get()
            dc = slice(d * 8, d * 8 + 8)
            self.tt('dve', gp[:, 0:8], gt[:, dc], self.ml_gb[0:64, dc], ALU.add, r=[gt, self.ml_gb], w=[gp])
            self.tt('dve', gp[:, 16:24], gt[:, 16 + d * 8:24 + d * 8], self.ml_gb[0:64, 16 + d * 8:24 + d * 8], ALU.add, r=[gt, self.ml_gb], pw=[gp])
            self.act(gp[:, 16:24], gp[:, 16:24], AF.Exp, r=[gp], pw=[gp], scale=-1.0)
            self.act(gp[:, 16:24], gp[:, 16:24], AF.Ln, r=[gp], pw=[gp], bias=1.0)
            self.ts('dve', gp[:, 16:24], gp[:, 16:24], -1.0, None, ALU.mult, r=[gp], pw=[gp])
            lf = gp[:, 16:24]
            pg = self.pnext()
            self.mm(pg[0:64, 0:8], Tri[d], lf, True, True, r=[cst, gp], w=[pg])
            self.mm(pg[0:64, 8:16], Str[d], lf, True, True, r=[cst, gp], pw=[pg])
            self.mm(pg[0:64, 16:24], ones64, lf, True, True, r=[cst, gp], pw=[pg])
            self.act(gp[:, 32:40], pg[0:64, 0:8], AF.Exp, r=[pg], pw=[gp])
            self.tt('dve', gp[:, 56:64], pg[0:64, 8:16], gp[:, 0:8], ALU.add, r=[pg, gp], pw=[gp])
            self.act(gp[:, 40:48], gp[:, 56:64], AF.Exp, r=[gp], pw=[gp])
            self.act(gp[:, 48:56], pg[0:64, 16:24], AF.Exp, r=[pg], pw=[gp])
            self.act(gp[:, 8:16], gp[:, 0:8], AF.Exp, r=[gp], pw=[gp])
            if pidx >= 0:
                pt = self.pnext()
                self.tr_(pt[0:8, 0:64], gp[:, 56:64], r=[gp], w=[pt])
                GB = GBs.get()
                self.S.op('dve', lambda E, GB=GB, pt=pt: E.tensor_reduce(out=GB[:, 0:1], in_=pt[0:8, 0:64], axis=AX.X, op=ALU.max), reads=[pt], writes=[GB])
                pb = self.pnext()
                self.mm(pb[0:8, 0:1], lf, cst.ap[0:64, 1, 0:1], True, True, r=[gp, cst], w=[pb])
                self.stt('dve', Mst[d][:], Mst[d][:], pb[0:8, 0:1], GB[:, 0:1], ALU.add, ALU.max, r=[Mst[d], pb, GB], w=[Mst[d]])
            return dict(d=d, t0=t0, qk=qk, kt=kt, v1=v1, gp=gp, ho=houts.get(), t8=tot8s.get(), C=st['C'][d])

        def phases(ctxs):
            for cx in ctxs:
                d, gp = cx['d'], cx['gp']
                tl8 = tl8s.get()
                self.tt('dve', tl8[:], Tri[d].unsqueeze(1).to_broadcast([64, 8, 64]), gp[:, 16:24].unsqueeze(2).to_broadcast([64, 8, 64]), ALU.mult, r=[cst, gp], w=[tl8])
                pD = self.pnext()
                self.mm(pD[0:64, 0:512], Str[d], tl8[:].rearrange("p h t -> p (h t)"), True, True, r=[cst, tl8], w=[pD])
                E8 = E8s.get()
                self.act(E8[:].rearrange("p h t -> p (h t)"), pD[0:64, 0:512], AF.Exp, r=[pD], w=[E8])
                cx['E8'] = E8
            for cx in ctxs:
                d, gp, kt, E8 = cx['d'], cx['gp'], cx['kt'], cx['E8']
                self.tt('pool', E8[:], E8[:], Tri[d].unsqueeze(1).to_broadcast([64, 8, 64]), ALU.mult, r=[E8, cst], w=[E8])
                self.tt('dve', E8[:], E8[:], gp[:, 8:16].unsqueeze(2).to_broadcast([64, 8, 64]), ALU.mult, r=[E8, gp], w=[E8])
                kw8 = kw8s.get()
                self.tt('pool', kw8[:], kt[:].rearrange("t (h e) -> t h e", h=8), gp[:, 40:48].unsqueeze(2).to_broadcast([64, 8, 64]), ALU.mult, r=[kt, gp], w=[kw8])
                cx['kw8'] = kw8
            for cx in ctxs:
                qk, E8 = cx['qk'], cx['E8']
                pK = self.pnext()
                for h in range(8):
                    self.mm(pK[0:64, h * 64:(h + 1) * 64], qk[:, 8 + h, :], qk[:, h, :], True, True, r=[qk], **wr(pK, h == 0))
                self.tt('dve', E8[:].rearrange("p h t -> p (h t)"), E8[:].rearrange("p h t -> p (h t)"), pK[0:64, 0:512], ALU.mult, r=[E8, pK], w=[E8])
            for cx in ctxs:
                qk, v1, gp, t8, E8, C_ = cx['qk'], cx['v1'], cx['gp'], cx['t8'], cx['E8'], cx['C']
                ia8 = ia8s.get()
                for bi, (h0, nh) in enumerate(HG):
                    pI = self.pnext()
                    for hh in range(nh):
                        h = h0 + hh
                        self.mm(pI[0:64, hh * 129:(hh + 1) * 129], E8[:, h, :], v1[:, h, :], True, True, r=[E8, v1], **wr(pI, hh == 0))
                    self.cp('act', ia8[:, h0:h0 + nh, :], pI[0:64, 0:nh * 129].rearrange("p (h e) -> p h e", h=nh), r=[pI], **wr(ia8, bi == 0))
                for bi, (h0, nh) in enumerate(HG):
                    pN = self.pnext()
                    for hh in range(nh):
                        h = h0 + hh
                        self.mm(pN[0:64, hh * 129:(hh + 1) * 129], qk[:, h, :], C_[:, h, :], True, True, r=[qk, C_], **wr(pN, hh == 0))
                    self.tt('dve', t8[:, h0:h0 + nh, :], pN[0:64, 0:nh * 129].rearrange("p (h e) -> p h e", h=nh),
                            gp[:, 32 + h0:32 + h0 + nh].unsqueeze(2).to_broadcast([64, nh, 129]), ALU.mult, r=[pN, gp], **wr(t8, bi == 0))
                self.tt('dve', t8[:], t8[:], ia8[:], ALU.add, r=[t8, ia8], w=[t8])
            for cx in ctxs:
                d, t0, ho, t8 = cx['d'], cx['t0'], cx['ho'], cx['t8']
                dn = dns.get()
                den = t8[:, :, 128]
                self.ts('dve', dn[:, 0:8], den, -1.0, None, ALU.mult, r=[t8], w=[dn])
                self.tt('dve', dn[:, 0:8], dn[:, 0:8], den, ALU.max, r=[dn, t8], w=[dn])
                self.ts('dve', dn[:, 0:8], dn[:, 0:8], 1.0, None, ALU.max, r=[dn], w=[dn])
                self.recip(dn[:, 8:16], dn[:, 0:8], r=[dn], pw=[dn])
                self.tt('pool', ho[:], t8[:, :, 0:128], dn[:, 8:16].unsqueeze(2).to_broadcast([64, 8, 128]), ALU.mult, r=[t8, dn], w=[ho])
                self.ld(self.hdir[d][t0:t0 + 64, :].rearrange("t (h e) -> t h e", h=8), ho[:], r=[ho], w=[self.hdir_tk[d][t0 // 64]], eng='pool')
            for cx in ctxs:
                v1, gp, C_, kw8 = cx['v1'], cx['gp'], cx['C'], cx['kw8']
                pUs = []
                for bi, (h0, nh) in enumerate(HG):
                    pU = self.pnext()
                    for hh in range(nh):
                        h = h0 + hh
                        self.mm(pU[0:64, hh * 129:(hh + 1) * 129], kw8[:, h, :], v1[:, h, :], True, True, r=[kw8, v1], **wr(pU, hh == 0))
                    pUs.append(pU)
                self.tt('pool', C_[:], C_[:], gp[:, 48:56].unsqueeze(2).to_broadcast([64, 8, 129]), ALU.mult, r=[C_, gp], w=[C_])
                for bi, (h0, nh) in enumerate(HG):
                    self.tt('dve', C_[:, h0:h0 + nh, :], C_[:, h0:h0 + nh, :], pUs[bi][0:64, 0:nh * 129].rearrange("p (h e) -> p h e", h=nh), ALU.add,
                            r=[C_, pUs[bi]], w=[C_])

        nsteps = max(st['start'] + st['nch'] for st in streams)
        for g in range(nsteps):
            ctxs = []
            for st in streams:
                ls = g - st['start']
                if 0 <= ls < st['nch']:
                    if ls == 0:
                        init(st)
                    for d in range(2):
                        c = ls if d == 0 else st['nch'] - 1 - ls
                        ctxs.append(build_ctx(st, d, st['tok0'] + c * 64))
            phases(ctxs)
            for st in streams:
                if g == st['start'] + st['nch'] - 1:
                    final(st)

    def gdn(self, i):
        j = i // 3
        w_in = self.gd_w_in[j]
        if self.bf:
            self.proj_stage16(i, w_in, 4128, fm=[(0, 24, self.qkT_v, 0, 1.0)],
                              tm=[(3072, 1024, self.otok), (4096, 32, self.gtok)])
        else:
            self.proj_stage(i, w_in, 2048, fm=[(0, 16, self.qkT_v, 0, 1.0)], tm=[], col_lo=0)
            self.proj_stage(i, w_in, 2080, fm=[(2048, 8, self.qkT_v, 16, 1.0)],
                            tm=[(3072, 1024, self.otok), (4096, 32, self.gtok)], col_lo=2048)
        stop = self.opts.get('gdn_stop', 9)
        if stop >= 2:
            self.gdn_conv(j)
        if stop >= 3:
            self.gdn_scan(j)
        if stop >= 4:
            self.mixer_post(i, self.gd_w_out[j], self.gd_ng, self.gd_ng[:, j, :], self.hdir, 'silu')

    def gdn_conv(self, j):
        self.areset()
        NBUF = 6
        xins = self.take([128, TS + 16], NBUF)
        tmps = self.take([128, TS], 3)
        sqs = self.take([128, 512], 4)
        rss = self.take([128, 512], 6)
        tos = self.take([128, 4, 128], 4)
        accs = []
        for _ in range(NBUF):
            base = self.take([128, TS])
            accs.append((base.ap, [Tk(base.ap[:, b * 512:(b + 1) * 512]) for b in range(4)]))
        cw = self.gd_cw
        n = 0
        items = [(tok0, ns, T, ch) for (tok0, ns, T) in [(0, NP, TP), (NP * TP, 1, TS)] for ch in range(24)]

        def phA(idx):
            tok0, ns, T, ch = items[idx]
            W = T + 4
            on_dve = (idx % 2 == 0)
            xin = xins.get()
            acc_ap, accb = accs[idx % NBUF]
            xv = xin[:, 0:ns * W].rearrange("p (s w) -> p s w", s=ns)
            av = acc_ap[:, 0:ns * T].rearrange("p (s t) -> p s t", s=ns)
            self.memset('pool', xv[:, :, 0:2], 0.0, w=[xin])
            self.memset('pool', xv[:, :, T + 2:T + 4], 0.0, pw=[xin])
            self.ld(xv[:, :, 2:T + 2], self.qkT_v[:, ch, tok0:tok0 + ns * T].rearrange("p (s t) -> p s t", s=ns), pw=[xin])
            if on_dve:
                self.ts('dve', av, xv[:, :, 0:T], cw[:, j, ch, 0:1], None, ALU.mult, r=[xin, cw], w=accb)
                for k in range(1, 5):
                    self.stt('dve', av, xv[:, :, k:k + T], cw[:, j, ch, k:k + 1], av, ALU.mult, ALU.add, r=[xin, cw] + accb, w=accb)
            else:
                self.act(av, xv[:, :, 0:T], AF.Copy, r=[xin, cw], w=accb, scale=cw[:, j, ch, 0:1])
                for k in range(1, 5):
                    tm_ = tmps.get()
                    tv = tm_[:, 0:ns * T].rearrange("p (s t) -> p s t", s=ns)
                    self.act(tv, xv[:, :, k:k + T], AF.Copy, r=[xin, cw], w=[tm_], scale=cw[:, j, ch, k:k + 1])
                    self.tt('pool', av, av, tv, ALU.add, r=accb + [tm_], w=accb)
            self.act(acc_ap[:, 0:ns * T], acc_ap[:, 0:ns * T], AF.Silu, r=accb, w=accb)
            return (tok0, ns * T, ch, acc_ap, accb)

        def phB(cx):
            tok0, NTOK, ch, acc_ap, accb = cx
            nb = NTOK // 512
            if ch < 16:
                scale = (128.0 ** -0.5) if ch < 8 else 1.0
                ps_, rs_ = [], []
                for b in range(nb):
                    bs = slice(b * 512, (b + 1) * 512)
                    sq = sqs.get()
                    self.tt('pool', sq[:], acc_ap[:, bs], acc_ap[:, bs], ALU.mult, r=[accb[b]], w=[sq])
                    p = self.pnext()
                    self.mm(p[:, :], self.cst.ap[:, 1, :], sq[:], True, True, r=[self.cst, sq], w=[p])
                    ps_.append(p)
                for b in range(nb):
                    rs = rss.get()
                    self.act(rs[:], ps_[b][:, :], AF.Sqrt, r=[ps_[b], self.epsb], w=[rs], bias=self.epsb[:, 0:1])
                    rs_.append(rs)
                for b in range(nb):
                    bs = slice(b * 512, (b + 1) * 512)
                    rs = rs_[b]
                    self.recip(rs[:], rs[:], r=[rs], w=[rs])
                    self.stt('dve', acc_ap[:, bs], acc_ap[:, bs], scale, rs[:], ALU.mult, ALU.mult, r=[accb[b], rs], w=[accb[b]])

        def phC(cx):
            tok0, NTOK, ch, acc_ap, accb = cx
            nb = NTOK // 512
            if ch < 16:
                self.ld(self.qkT_v[:, ch, tok0:tok0 + NTOK], acc_ap[:, 0:NTOK], r=accb[0:nb], pw=[self.Oout], eng='pool')
            if ch >= 8:
                dst = self.ktok if ch < 16 else self.vtok
                c0 = (ch - 8) * 128 if ch < 16 else (ch - 16) * 128
                for b in range(nb):
                    p = self.pnext()
                    for k in range(4):
                        self.tr_(p[:, k * 128:(k + 1) * 128], acc_ap[:, b * 512 + k * 128:b * 512 + (k + 1) * 128], r=[accb[b]], **wr(p, k == 0))
                    to = tos.get()
                    self.cp('act' if b % 2 else 'dve', to[:], p[:, :].rearrange("p (a b) -> p a b", a=4), r=[p], w=[to])
                    self.ld(dst[tok0 + b * 512:tok0 + (b + 1) * 512, c0:c0 + 128].rearrange("(n p) e -> p n e", p=128), to[:], r=[to], pw=[self.Oout], eng='sp')

        cxs = {}
        NI = len(items)
        for it in range(NI + 2):
            if it < NI:
                cxs[it] = phA(it)
            if 0 <= it - 1 < NI:
                phB(cxs[it - 1])
            if 0 <= it - 2 < NI:
                phC(cxs.pop(it - 2))

    def gdn_scan(self, j):
        self.areset()
        cst = self.cst
        Tri = [cst.ap[0:64, 2, 0:64], cst.ap[0:64, 3, 0:64]]
        Str = [cst.ap[0:64, 4, 0:64], cst.ap[0:64, 5, 0:64]]
        Sm = [cst.ap[0:64, 5, 0:64], cst.ap[0:64, 4, 0:64]]
        I64 = cst.ap[0:64, 0, 0:64]
        ones64w = cst.ap[0:64, 1, 0:128]

        def bh(m):
            return m.unsqueeze(1).to_broadcast([64, 8, 64])

        def bt(v, n, np_=64):
            return v.unsqueeze(2).to_broadcast([np_, v.shape[1], n])

        S8 = [self.take([128, 8, 128]) for d in range(2)]
        qks = self.take([128, 8, 2, 64], 3)
        kts = self.take([64, 8, 128], 3)
        vts = self.take([64, 8, 128], 3)
        gts = self.take([64, 32], 3)
        gps = self.take([64, 48], 3)
        gls = self.take([128, 8], 3)
        tl8s = self.take([64, 8, 64], 2)
        Er8s = self.take([64, 8, 64], 2)
        Ei8s = self.take([64, 8, 64], 2)
        Es8s = self.take([64, 8, 64], 2)
        qkT8s = self.take([64, 8, 64], 3)
        P8s = self.take([64, 8, 64], 3)
        X8s = self.take([64, 8, 64], 5)
        XT8s = self.take([64, 8, 64], 5)
        U8s = self.take([64, 8, 128], 3)
        keg8s = self.take([64, 8, 128], 2)
        kdec8s = self.take([64, 8, 128], 3)
        vn8s = self.take([64, 8, 128], 3)
        o8s = self.take([64, 8, 128], 3)
        WT8s = self.take([128, 8, 64], 3)
        seqs = [(p * TP, TP // 64, p) for p in range(NP)] + [(NP * TP, TS // 64, -1)]
        seqs = seqs[self.opts.get('gdn_seq0', 0):self.opts.get('gdn_seq1', 5)]
        for (tok0, nch, pidx) in seqs:
            for d in range(2):
                if pidx >= 0:
                    self.memset('pool', S8[d][:], 0.0, w=[S8[d]])
                else:
                    self.ld(S8[d][:], self.st_S[j, d].rearrange("h k e -> k h e"), r=[self.Win], w=[S8[d]], eng='sp' if d else 'pool')
            for step in range(nch):
                ctx = []
                for d in range(2):
                    c = step if d == 0 else nch - 1 - step
                    t0 = tok0 + c * 64
                    qk = qks.get()
                    self.ld(qk[:, :, 0, :], self.qkT_v[:, 8:16, t0:t0 + 64], w=[qk])
                    self.ld(qk[:, :, 1, :], self.qkT_v[:, 0:8, t0:t0 + 64], pw=[qk], eng='pool')
                    kt = kts.get()
                    self.ld(kt[:], self.ktok[t0:t0 + 64, 0:1024].rearrange("t (h e) -> t h e", h=8), w=[kt], eng='pool')
                    vt = vts.get()
                    self.ld(vt[:], self.vtok[t0:t0 + 64, :].rearrange("t (h e) -> t h e", h=8), w=[vt])
                    gt = gts.get()
                    self.ld(gt[:], self.gtok[t0:t0 + 64, :], w=[gt], eng='pool')
                    gp = gps.get()
                    dc = slice(d * 8, d * 8 + 8)
                    self.tt('dve', gp[:, 0:8], gt[:, dc], self.gd_dt[0:64, j, dc], ALU.add, r=[gt, self.gd_dt], w=[gp])
                    self.act(gp[:, 0:8], gp[:, 0:8], AF.Exp, r=[gp], pw=[gp])
                    self.act(gp[:, 0:8], gp[:, 0:8], AF.Ln, r=[gp], pw=[gp], bias=1.0)
                    self.tt('dve', gp[:, 8:16], gp[:, 0:8], self.gd_nea[0:64, j, dc], ALU.mult, r=[gp, self.gd_nea], pw=[gp])
                    self.act(gp[:, 16:24], gt[:, 16 + d * 8:24 + d * 8], AF.Sigmoid, r=[gt], pw=[gp])
                    self.ts('dve', gp[:, 24:32], gp[:, 16:24], -1.0, None, ALU.mult, r=[gp], pw=[gp])
                    la = gp[:, 8:16]
                    pg = self.pnext()
                    self.mm(pg[0:64, 0:8], Tri[d], la, True, True, r=[cst, gp], w=[pg])
                    self.mm(pg[0:64, 8:16], Str[d], la, True, True, r=[cst, gp], pw=[pg])
                    self.mm(pg[0:128, 16:24], ones64w, la, True, True, r=[cst, gp], pw=[pg])
                    self.act(gp[:, 32:48], pg[0:64, 0:16], AF.Exp, r=[pg], pw=[gp])
                    gl = gls.get()
                    self.act(gl[:], pg[0:128, 16:24], AF.Exp, r=[pg], w=[gl])
                    ctx.append(dict(d=d, t0=t0, qk=qk, kt=kt, vt=vt, gp=gp, gl=gl))
                for cx in ctx:
                    d, gp = cx['d'], cx['gp']
                    tl8 = tl8s.get()
                    self.tt('dve', tl8[:], bh(Tri[d]), bt(gp[:, 8:16], 64), ALU.mult, r=[cst, gp], w=[tl8])
                    pD = self.pnext()
                    self.mm(pD[0:64, 0:512], Str[d], tl8[:].rearrange("p h t -> p (h t)"), True, True, r=[cst, tl8], w=[pD])
                    Er = Er8s.get()
                    self.act(Er[:].rearrange("p h t -> p (h t)"), pD[0:64, 0:512], AF.Exp, r=[pD], w=[Er])
                    cx['Er'] = Er
                for cx in ctx:
                    d, gp, Er = cx['d'], cx['gp'], cx['Er']
                    Ei = Ei8s.get()
                    Es = Es8s.get()
                    self.tt('pool', Ei[:], Er[:], bh(Tri[d]), ALU.mult, r=[Er, cst], w=[Ei])
                    self.tt('pool', Es[:], Er[:], bh(Sm[d]), ALU.mult, r=[Er, cst], w=[Es])
                    self.tt('dve', Es[:], Es[:], bt(gp[:, 24:32], 64), ALU.mult, r=[Es, gp], w=[Es])
                    cx['Ei'], cx['Es'] = Ei, Es
                for cx in ctx:
                    qk = cx['qk']
                    X = X8s.get()
                    qkT = qkT8s.get()
                    for g in range(2):
                        pG = self.pnext()
                        for hh in range(4):
                            h = 4 * g + hh
                            self.mm(pG[0:64, hh * 128:(hh + 1) * 128], qk[:, h, 0, :], qk[:, h, :, :].rearrange("p a t -> p (a t)"), True, True,
                                    r=[qk], **wr(pG, hh == 0))
                        pv = pG[0:64, 0:512].rearrange("p (h a t) -> p h a t", h=4, a=2)
                        self.tt('dve', X[:, 4 * g:4 * g + 4, :], pv[:, :, 0, :], cx['Es'][:, 4 * g:4 * g + 4, :], ALU.mult, r=[pG, cx['Es']], **wr(X, g == 0))
                        self.tt('dve', qkT[:, 4 * g:4 * g + 4, :], pv[:, :, 1, :], cx['Ei'][:, 4 * g:4 * g + 4, :], ALU.mult, r=[pG, cx['Ei']], **wr(qkT, g == 0))
                    cx['X'], cx['qkT'] = X, qkT
                for cx in ctx:
                    X = cx['X']
                    pT = self.pnext()
                    for h in range(8):
                        self.tr_(pT[0:64, h * 64:(h + 1) * 64], X[:, h, :], r=[X], **wr(pT, h == 0))
                    XT = XT8s.get()
                    self.cp('act', XT[:].rearrange("p h t -> p (h t)"), pT[0:64, 0:512], r=[pT], w=[XT])
                    P_ = P8s.get()
                    self.tt('pool', P_[:], X[:], bh(I64), ALU.add, r=[X, cst], w=[P_])
                    cx['XT'], cx['P'] = XT, P_
                for jn in range(1, 6):
                    for cx in ctx:
                        X, XT = cx['X'], cx['XT']
                        Xn = None
                        if jn < 5:
                            pX = self.pnext()
                            for h in range(8):
                                self.mm(pX[0:64, h * 64:(h + 1) * 64], XT[:, h, :], X[:, h, :], True, True, r=[XT, X], **wr(pX, h == 0))
                            Xn = X8s.get()
                            self.cp('dve', Xn[:].rearrange("p h t -> p (h t)"), pX[0:64, 0:512], r=[pX], w=[Xn])
                        pXT = self.pnext()
                        for h in range(8):
                            self.mm(pXT[0:64, h * 64:(h + 1) * 64], X[:, h, :], XT[:, h, :], True, True, r=[XT, X], **wr(pXT, h == 0))
                        XnT = XT8s.get()
                        self.cp('act', XnT[:].rearrange("p h t -> p (h t)"), pXT[0:64, 0:512], r=[pXT], w=[XnT])
                        cx['X'], cx['XT'] = Xn, XnT
                    for cx in ctx:
                        XT, P_ = cx['XT'], cx['P']
                        pP = self.pnext()
                        for h in range(8):
                            self.mm(pP[0:64, h * 64:(h + 1) * 64], XT[:, h, :], P_[:, h, :], True, True, r=[XT, P_], **wr(pP, h == 0))
                        self.tt('dve', P_[:].rearrange("p h t -> p (h t)"), P_[:].rearrange("p h t -> p (h t)"), pP[0:64, 0:512], ALU.add, r=[P_, pP], w=[P_])
                for cx in ctx:
                    gp, kt, vt, P_ = cx['gp'], cx['kt'], cx['vt'], cx['P']
                    keg = keg8s.get()
                    self.tt('pool', keg[:], kt[:], bt(gp[:, 32:40], 128), ALU.mult, r=[kt, gp], w=[keg])
                    kdec = kdec8s.get()
                    self.tt('pool', kdec[:], kt[:], bt(gp[:, 40:48], 128), ALU.mult, r=[kt, gp], w=[kdec])
                    U = U8s.get()
                    for g in range(2):
                        pU = self.pnext()
                        for hh in range(4):
                            h = 4 * g + hh
                            self.mm(pU[0:64, hh * 128:(hh + 1) * 128], P_[:, h, :], vt[:, h, :], True, True, r=[P_, vt], **wr(pU, hh == 0))
                        self.tt('dve', U[:, 4 * g:4 * g + 4, :], pU[0:64, 0:512].rearrange("p (h e) -> p h e", h=4), bt(gp[:, 16 + 4 * g:20 + 4 * g], 128), ALU.mult,
                                r=[pU, gp], **wr(U, g == 0))
                    pW = self.pnext()
                    for h in range(8):
                        self.mm(pW[0:128, h * 64:(h + 1) * 64], keg[:, h, :], P_[:, h, :], True, True, r=[keg, P_], **wr(pW, h == 0))
                    WT = WT8s.get()
                    self.cp('act', WT[:].rearrange("p h t -> p (h t)"), pW[0:128, 0:512], r=[pW], w=[WT])
                    cx['U'], cx['WT'], cx['kdec'] = U, WT, kdec
                for cx in ctx:
                    d, gp = cx['d'], cx['gp']
                    vn = vn8s.get()
                    for g in range(2):
                        pa = self.pnext()
                        for hh in range(4):
                            h = 4 * g + hh
                            self.mm(pa[0:64, hh * 128:(hh + 1) * 128], cx['WT'][:, h, :], S8[d][:, h, :], True, True, r=[cx['WT'], S8[d]], **wr(pa, hh == 0))
                        self.tt('dve', vn[:, 4 * g:4 * g + 4, :], pa[0:64, 0:512].rearrange("p (h e) -> p h e", h=4), bt(gp[:, 24 + 4 * g:28 + 4 * g], 128), ALU.mult,
                                r=[pa, gp], **wr(vn, g == 0))
                    self.tt('pool', vn[:], vn[:], cx['U'][:], ALU.add, r=[vn, cx['U']], w=[vn])
                    cx['vn'] = vn
                for cx in ctx:
                    d, gp, gl, qk, vn = cx['d'], cx['gp'], cx['gl'], cx['qk'], cx['vn']
                    o8 = o8s.get()
                    for g in range(2):
                        po = self.pnext()
                        for hh in range(4):
                            h = 4 * g + hh
                            self.mm(po[0:64, hh * 128:(hh + 1) * 128], qk[:, h, 1, :], S8[d][:, h, :], True, True, r=[qk, S8[d]], **wr(po, hh == 0))
                        self.tt('dve', o8[:, 4 * g:4 * g + 4, :], po[0:64, 0:512].rearrange("p (h e) -> p h e", h=4), bt(gp[:, 32 + 4 * g:36 + 4 * g], 128), ALU.mult,
                                r=[po, gp], **wr(o8, g == 0))
                    for g in range(2):
                        po2 = self.pnext()
                        for hh in range(4):
                            h = 4 * g + hh
                            self.mm(po2[0:64, hh * 128:(hh + 1) * 128], cx['qkT'][:, h, :], vn[:, h, :], True, True, r=[cx['qkT'], vn], **wr(po2, hh == 0))
                        self.tt('dve', o8[:, 4 * g:4 * g + 4, :], o8[:, 4 * g:4 * g + 4, :], po2[0:64, 0:512].rearrange("p (h e) -> p h e", h=4), ALU.add,
                                r=[po2, o8], pw=[o8])
                    self.ld(self.hdir[d][cx['t0']:cx['t0'] + 64, :].rearrange("t (h e) -> t h e", h=8), o8[:], r=[o8], pw=[self.Oout], eng='pool')
                    for g in range(2):
                        pS = self.pnext()
                        for hh in range(4):
                            h = 4 * g + hh
                            self.mm(pS[0:128, hh * 128:(hh + 1) * 128], cx['kdec'][:, h, :], vn[:, h, :], True, True, r=[cx['kdec'], vn], **wr(pS, hh == 0))
                        Sg = S8[d][:, 4 * g:4 * g + 4, :]
                        self.tt('pool', Sg, Sg, bt(gl[:, 4 * g:4 * g + 4], 128, 128), ALU.mult, r=[S8[d], gl], w=[S8[d]])
                        self.tt('dve', Sg, Sg, pS[0:128, 0:512].rearrange("p (h e) -> p h e", h=4), ALU.add, r=[S8[d], pS], w=[S8[d]])
            if pidx >= 0:
                for d in range(2):
                    self.ld(self.newS[pidx, j, d].rearrange("h k e -> k h e"), S8[d][:], r=[S8[d]], pw=[self.Oout], eng='sp' if d else 'pool')

    def diffattn(self, i):
        (self.proj_stage16 if self.bf else self.proj_stage)(i, self.df_w_in, 3072, fm=[],
                        tm=[(0, 2048, self.ktok), (2048, 1024, self.vtok)])
        self.attn_prep()
        self.attn_core()
        self.mixer_post(i, self.df_w_out, self.df_sg, self.df_sg[:], self.hdir, None)

    def attn_prep(self):
        self.areset()
        xs = self.take([128, 32, 64], 3)
        sqs = self.take([128, 32, 64], 2)
        sss = self.take([128, 64], 3)
        css = self.take([128, 64], 3)
        r1 = self.take([128, 32, 2, 16], 1)
        r2 = self.take([128, 32, 2, 16], 1)
        r3 = self.take([128, 32, 2, 16], 1)
        xr = self.take([128, 32, 64], 2)
        vts = self.take([128, 1024], 2)
        xos = self.take([128, 16, 128], 2)
        cks = self.take([128, 16, 64], 2)
        cko = self.take([128, 8, 128], 2)
        gq = self.df_g
        def pA(t):
            t0 = t * 128
            x = xs.get()
            self.ld(x[:], self.ktok[t0:t0 + 128, :].rearrange("t (g d) -> t g d", g=32), r=[self.proj_tk[t]], w=[x])
            sq = sqs.get()
            self.tt('pool', sq[:], x[:], x[:], ALU.mult, r=[x], w=[sq])
            ss = sss.get()
            self.S.op('dve', lambda E, ss=ss, sq=sq: E.tensor_reduce(out=ss[:, 0:32], in_=sq[:], axis=AX.X, op=ALU.add), reads=[sq], writes=[ss])
            self.act(ss[:, 0:32], ss[:, 0:32], AF.Sqrt, r=[ss, self.epsb], w=[ss], scale=1.0 / 64, bias=self.epsb[:, 0:1])
            self.recip(ss[:, 32:64], ss[:, 0:32], r=[ss], pw=[ss])
            self.tt('dve', x[:], x[:], ss[:, 32:64].unsqueeze(2).to_broadcast([128, 32, 64]), ALU.mult, r=[x, ss], w=[x])
            self.tt('pool', x[:, 0:16, :], x[:, 0:16, :], gq[:, 0, :].unsqueeze(1).to_broadcast([128, 16, 64]), ALU.mult, r=[x, gq], w=[x])
            self.tt('dve', x[:, 16:32, :], x[:, 16:32, :], gq[:, 1, :].unsqueeze(1).to_broadcast([128, 16, 64]), ALU.mult, r=[x, gq], w=[x])
            if t0 < NP * TP:
                p, tl = t0 // TP, t0 % TP
                self.ld(self.newk[p, :, :, tl:tl + 128, :].rearrange("h m t d -> t (h m) d"), x[:, 16:32, :], r=[x], pw=[self.Oout], eng='pool')
                vt = vts.get()
                self.ld(vt[:], self.vtok[t0:t0 + 128, :], r=[self.proj_tk[t]], w=[vt])
                self.ld(self.newv[p, :, tl:tl + 128, :].rearrange("h t e -> t h e"), vt[:].rearrange("t (h e) -> t h e", h=8), r=[vt], pw=[self.Oout], eng='pool')
                src = x
            else:
                cs = css.get()
                self.ld(cs[:], self.rope_cs[t0 - NP * TP:t0 - NP * TP + 128, :], r=[self.Win], w=[cs])
                X = x[:].rearrange("t g (a f r) -> t g a f r", a=2, f=2)
                xa = X[:, :, :, 0, :]
                xb_ = X[:, :, :, 1, :]
                cosb = cs[:, 0:32].rearrange("t (a r) -> t a r", a=2).unsqueeze(1).to_broadcast([128, 32, 2, 16])
                sinb = cs[:, 32:64].rearrange("t (a r) -> t a r", a=2).unsqueeze(1).to_broadcast([128, 32, 2, 16])
                o_ = xr.get()
                O = o_[:].rearrange("t g (a f r) -> t g a f r", a=2, f=2)
                a1, a2, a3 = r1.get(), r2.get(), r3.get()
                self.tt('dve', a1[:], xa, cosb, ALU.mult, r=[x, cs], w=[a1])
                self.tt('pool', a2[:], xb_, sinb, ALU.mult, r=[x, cs], w=[a2])
                self.tt('dve', O[:, :, :, 0, :], a1[:], a2[:], ALU.subtract, r=[a1, a2], w=[o_])
                self.tt('pool', a3[:], xa, sinb, ALU.mult, r=[x, cs], w=[a3])
                self.tt('dve', a1[:], xb_, cosb, ALU.mult, r=[x, cs], w=[a1])
                self.tt('pool', O[:, :, :, 1, :], a3[:], a1[:], ALU.add, r=[a3, a1], pw=[o_])
                src = o_
            return src

        def pB(t, src):
            t0 = t * 128
            xo = xos.get()
            for g in range(4):
                pp = self.pnext()
                for k in range(4):
                    ch = g * 4 + k
                    self.tr_(pp[:, k * 128:(k + 1) * 128], src[:, 2 * ch:2 * ch + 2, :].rearrange("t a d -> t (a d)"), r=[src], **wr(pp, k == 0))
                self.cp('act' if g % 2 else 'dve', xo[:, g * 4:(g + 1) * 4, :], pp[:, :].rearrange("p (a b) -> p a b", a=4), r=[pp], **wr(xo, g == 0))
            self.ld(self.qkT_v[:, 0:16, t0:t0 + 128], xo[:], r=[xo], w=[self.prep_tk[t]], eng='pool')

        NTL = TT // 128
        srcq = {}
        for t in range(NTL + 1):
            if t < NTL:
                srcq[t] = pA(t)
            if t >= 1:
                pB(t - 1, srcq.pop(t - 1))
        for kt in range(2):
            ck = cks.get()
            self.ld(ck[:], self.ctx_k[:, :, kt * 128:(kt + 1) * 128, :].rearrange("h m t d -> t (h m) d"), r=[self.Win], w=[ck])
            co = cko.get()
            for g in range(2):
                pp = self.pnext()
                for k in range(4):
                    ch = g * 4 + k
                    self.tr_(pp[:, k * 128:(k + 1) * 128], ck[:, 2 * ch:2 * ch + 2, :].rearrange("t a d -> t (a d)"), r=[ck], **wr(pp, k == 0))
                self.cp('act' if g % 2 else 'dve', co[:, g * 4:(g + 1) * 4, :], pp[:, :].rearrange("p (a b) -> p a b", a=4), r=[pp], **wr(co, g == 0))
            self.ld(self.ctxkT.rearrange("c p t -> p c t")[:, :, kt * 128:(kt + 1) * 128], co[:], r=[co], **wr(self.ctx_tk, kt == 0), eng='pool')

    def attn_core(self):
        self.areset()
        NKT = (TS + 256) // 128
        bf = self.bf
        MD = BF16 if bf else F32
        qTs = self.take([128, TS], 2, MD)
        kTs = self.take([128, TS + 256], 2, MD)
        V1s = self.take([128, NKT, 129], 2, MD)
        for V1 in V1s.t:
            self.memset('pool', V1[:, :, 128:129], 1.0, pw=[V1])
        PTs = self.take([128, NKT, 512], 2, MD)
        if bf:
            q32 = self.take([128, TS], 2)
            k32 = self.take([128, TS + 256], 2)
            v32 = self.take([128, NKT, 128], 2)
        obs = self.take([128, 4, 128], 2)
        rvs = self.take([128, 2], 4)
        seqs = [(p * TP, TP, False) for p in range(NP)] + [(NP * TP, TS, True)]
        for (tok0, T, is_s) in seqs:
            nk = T + (256 if is_s else 0)
            nkt = nk // 128
            QB = min(512, T)
            tks = self.prep_tk[tok0 // 128:(tok0 + T) // 128]
            ptk = self.proj_tk[tok0 // 128:(tok0 + T) // 128]
            for h in range(8):
                qT = qTs.get()
                kT = kTs.get()
                V1 = V1s.get()
                if bf:
                    qd, kd, vd = q32.get(), k32.get(), v32.get()
                else:
                    qd, kd, vd = qT, kT, V1
                self.ld(qd[:, 0:T], self.qkT_v[:, h, tok0:tok0 + T], r=tks, w=[qd])
                self.ld(kd[:, 0:T], self.qkT_v[:, 8 + h, tok0:tok0 + T], r=tks, w=[kd], eng='pool')
                self.ld(vd[:, 0:T // 128, 0:128], self.vtok[tok0:tok0 + T, h * 128:(h + 1) * 128].rearrange("(n p) e -> p n e", p=128), r=ptk, **wr(vd, bf))
                if is_s:
                    self.ld(kd[:, T:T + 256], self.ctxkT[h], r=[self.ctx_tk], pw=[kd], eng='pool')
                    self.ld(vd[:, T // 128:nkt, 0:128], self.ctx_v[h].rearrange("(n p) e -> p n e", p=128), r=[self.Win], pw=[vd])
                if bf:
                    self.cp('dve', qT[:, 0:T], qd[:, 0:T], r=[qd], w=[qT])
                    self.cp('act', kT[:, 0:nk], kd[:, 0:nk], r=[kd], w=[kT])
                    self.cp('dve', V1[:, 0:nkt, 0:128], vd[:, 0:nkt, :], r=[vd], pw=[V1])
                for qb in range(T // QB):
                    ob = obs.get()
                    for m in range(2):
                        PT = PTs.get()
                        for kt in range(nkt):
                            pS = self.pnext()
                            self.mm(pS[:, 0:QB], kT[m * 64:(m + 1) * 64, kt * 128:(kt + 1) * 128], qT[m * 64:(m + 1) * 64, qb * QB:(qb + 1) * QB],
                                    True, True, r=[kT, qT], w=[pS])
                            self.act(PT[:, kt, 0:QB], pS[:, 0:QB], AF.Exp, r=[pS], **wr(PT, kt == 0), scale=0.125)
                        for qs in range(QB // 128):
                            pO = self.pnext()
                            for kt in range(nkt):
                                self.mm(pO[:, 0:129], PT[:, kt, qs * 128:(qs + 1) * 128], V1[:, kt, :], kt == 0, kt == nkt - 1, r=[PT, V1], **wr(pO, kt == 0))
                            rv = rvs.get()
                            self.recip(rv[:, 0:1], pO[:, 128:129], r=[pO], w=[rv])
                            if m == 0:
                                self.ts('dve', ob[:, qs, :], pO[:, 0:128], rv[:, 0:1], None, ALU.mult, r=[pO, rv], **wr(ob, qs == 0))
                            else:
                                self.tt('dve', rv[:, 1:2], rv[:, 0:1], self.nlam[:, 3:4], ALU.mult, r=[rv, self.nlam], pw=[rv])
                                self.stt('dve', ob[:, qs, :], pO[:, 0:128], rv[:, 1:2], ob[:, qs, :], ALU.mult, ALU.add, r=[pO, rv, ob], pw=[ob])
                    q0 = tok0 + qb * QB
                    nq = QB // 128
                    htk = self.hdir_tk[0][q0 // 64:(q0 + QB) // 64]
                    self.ld(self.hdir[0][q0:q0 + QB, h * 128:(h + 1) * 128].rearrange("(n p) e -> p n e", p=128), ob[:, 0:nq, :], r=[ob], pw=htk, eng='pool')

    def mixer_post(self, i, w_out, ng_tk, ng_bc, hdir, gate):
        self.areset()
        NT = 512
        if self.bf:
            W = self.take([128, 8, D], None, BF16)
            self.wstage = self.take([128, 8, 512], 2)
            self.load_w16(W, w_out.rearrange("(kc p) n -> p kc n", p=128), D)
        else:
            W = self.take([128, 8, D])
            self.ld(W[:], w_out.rearrange("(kc p) n -> p kc n", p=128), r=[self.Win], w=[W])
        hfs = self.take([128, 8, 128], 4)
        hbs = self.take([128, 8, 128], 4)
        ogs = self.take([128, 8, 128], 4)
        sqs = self.take([128, 8, 128], 3)
        sss = self.take([128, 16], 4)
        yTs = self.take([128, 8, NT], 2, BF16 if self.bf else F32)
        xbs = self.take([128, 8, NT], 2)
        ngb = ng_bc.unsqueeze(1).to_broadcast([128, 8, 128])
        blkst = {}

        def pA(t):
            blk, q = t // 4, t % 4
            if q == 0:
                xb = xbs.get()
                self.ld(xb[:], self.xT_v[:, :, blk * NT:(blk + 1) * NT], r=self.xT_tk[blk * 4:(blk + 1) * 4], w=[xb], eng='pool')
                blkst[blk] = [xb, yTs.get()]
            t0 = t * 128
            hf = hfs.get()
            og = ogs.get()
            self.ld(hf[:], hdir[0][t0:t0 + 128, :].rearrange("t (h e) -> t h e", h=8), r=self.hdir_tk[0][t0 // 64:t0 // 64 + 2], w=[hf])
            if gate is not None:
                hb = hbs.get()
                self.ld(hb[:], hdir[1][t0:t0 + 128, :].rearrange("t (h e) -> t h e", h=8), r=self.hdir_tk[1][t0 // 64:t0 // 64 + 2], w=[hb], eng='pool')
                self.ld(og[:], self.otok[t0:t0 + 128, :].rearrange("t (h e) -> t h e", h=8), r=[self.proj_tk[t0 // 128]], w=[og])
                self.tt('dve', hf[:], hf[:], hb[:], ALU.add, r=[hf, hb], w=[hf])
            sq = sqs.get()
            self.tt('pool', sq[:], hf[:], hf[:], ALU.mult, r=[hf], w=[sq])
            ss = sss.get()
            self.S.op('dve', lambda E, ss=ss, sq=sq: E.tensor_reduce(out=ss[:, 0:8], in_=sq[:], axis=AX.X, op=ALU.add), reads=[sq], writes=[ss])
            self.act(ss[:, 0:8], ss[:, 0:8], AF.Sqrt, r=[ss, self.epsb], w=[ss], scale=1.0 / 128, bias=self.epsb[:, 0:1])
            self.recip(ss[:, 8:16], ss[:, 0:8], r=[ss], pw=[ss])
            if gate == 'sigmoid':
                self.act(og[:], og[:], AF.Sigmoid, r=[og], w=[og])
                self.tt('pool', og[:], og[:], ngb, ALU.mult, r=[og, ng_tk], w=[og])
            elif gate == 'silu':
                self.act(og[:], og[:], AF.Silu, r=[og], w=[og])
                self.tt('pool', og[:], og[:], ngb, ALU.mult, r=[og, ng_tk], w=[og])
            else:
                self.cp('pool', og[:], ngb, r=[ng_tk], w=[og])
            self.tt('dve', hf[:], hf[:], ss[:, 8:16].unsqueeze(2).to_broadcast([128, 8, 128]), ALU.mult, r=[hf, ss], w=[hf])
            self.tt('dve', hf[:], hf[:], og[:], ALU.mult, r=[hf, og], w=[hf])
            return hf

        def pB(t, hf):
            blk, q = t // 4, t % 4
            yT = blkst[blk][1]
            for hh in range(2):
                p = self.pnext()
                for k in range(4):
                    self.tr_(p[:, k * 128:(k + 1) * 128], hf[:, hh * 4 + k, :], r=[hf], **wr(p, k == 0))
                dst = yT[:, hh * 4:(hh + 1) * 4, q * 128:(q + 1) * 128]
                src = p[:, :].rearrange("p (a b) -> p a b", a=4)
                self.cp('act' if hh else 'dve', dst, src, r=[p], **wr(yT, q == 0 and hh == 0))

        def pC(blk):
            c = 0 if blk < (NP * TP) // NT else 1
            xb, yT = blkst.pop(blk)
            tks = self.xT_tk[blk * 4:(blk + 1) * 4]
            for oc in range(8):
                p = self.pnext()
                for kc in range(8):
                    self.mm(p[:, :], W[:, kc, oc * 128:(oc + 1) * 128], yT[:, kc, :], kc == 0, kc == 7, r=[W, yT], **wr(p, kc == 0))
                self.stt('dve', xb[:, oc, :], p[:, :], self.mod[:, 16 + oc, c:c + 1], xb[:, oc, :], ALU.mult, ALU.add,
                         r=[p, self.mod, xb], pw=[xb])
            for q in range(4):
                self.ld(self.xT_v[:, :, blk * NT + q * 128: blk * NT + (q + 1) * 128], xb[:, :, q * 128:(q + 1) * 128],
                        r=[xb], w=[tks[q]], eng='pool')

        NTL = TT // 128
        hfq = {}
        for t in range(NTL + 2):
            if t < NTL:
                hfq[t] = pA(t)
            if 0 <= t - 1 < NTL:
                pB(t - 1, hfq.pop(t - 1))
                if (t - 1) % 4 == 3:
                    pass
            if 0 <= t - 2 < NTL and (t - 2) % 4 == 3:
                pC((t - 2) // 4)

    def tr_(self, out, in_, r=(), w=(), pw=()):
        n = in_.shape[0]
        idn = self.cst.ap[0:n, 0, 0:n]
        self.S.op('pe', lambda E: E.transpose(out, in_, idn), reads=list(r) + [self.cst], writes=w, pw=pw)

    def stage_in(self):
        self.areset()
        xin = self.take([128, D], 4)
        xo = self.take([128, 8, 128], 4)
        for t in range(TT // 128):
            a = xin.get()
            self.ld(a[:], self.x_tok[t * 128:(t + 1) * 128, :], r=[self.Xtok], w=[a])
            b = xo.get()
            for h in range(2):
                p = self.pnext()
                for k in range(4):
                    kc = h * 4 + k
                    self.tr_(p[:, k * 128:(k + 1) * 128], a[:, kc * 128:(kc + 1) * 128], r=[a], w=[p] if k == 0 else (), pw=() if k == 0 else [p])
                dst = b[:, h * 4:(h + 1) * 4, :]
                src = p[:, :].rearrange("p (a b) -> p a b", a=4)
                if h == 0:
                    self.cp('dve', dst, src, r=[p], w=[b])
                else:
                    self.cp('act', dst, src, r=[p], pw=[b])
            self.ld(self.xT_v[:, :, t * 128:(t + 1) * 128], b[:], r=[b], w=[self.xT_tk[t]], eng='pool')

    def stage_out(self):
        self.areset()
        xi = self.take([128, 8, 128], 4)
        yo = self.take([128, D], 4)
        for t in range(TT // 128):
            a = xi.get()
            self.ld(a[:], self.xT_v[:, :, t * 128:(t + 1) * 128], r=[self.xT_tk[t]], w=[a])
            b = yo.get()
            for h in range(2):
                p = self.pnext()
                for k in range(4):
                    kc = h * 4 + k
                    self.tr_(p[:, k * 128:(k + 1) * 128], a[:, kc, :], r=[a], w=[p] if k == 0 else (), pw=() if k == 0 else [p])
                if h == 0:
                    self.cp('dve', b[:, 0:512], p[:, :], r=[p], w=[b])
                else:
                    self.cp('act', b[:, 512:1024], p[:, :], r=[p], pw=[b])
            self.ld(self.y_tok[t * 128:(t + 1) * 128, :], b[:], r=[b], w=[self.Ytok], eng='pool')

    def stage_mod(self, i):
        self.areset()
        wt = self.take([128, 8, 512], 2)
        wv = self.ada_w[i].rearrange("(kc p) n -> p kc n", p=128)
        mp = self.pnext()
        for n in range(12):
            w = wt.get()
            self.ld(w[:], wv[:, :, n * 512:(n + 1) * 512], r=[self.Win], w=[w])
            for jj in range(4):
                j = n * 4 + jj
                for kc in range(8):
                    self.mm(mp[:, 2 * j:2 * j + 2], w[:, kc, jj * 128:(jj + 1) * 128], self.sc[:, kc, :], kc == 0, kc == 7,
                            r=[w, self.sc], **wr(mp, j == 0 and kc == 0))
        mpv = mp[:, 0:96].rearrange("p (j c) -> p j c", c=2)
        for c in range(2):
            self.tt('dve', self.mod[:, :, c], mpv[:, :, c], self.adab[:, i, :], ALU.add, r=[mp, self.adab],
                    w=[self.mod] if c == 0 else (), pw=() if c == 0 else [self.mod])
        for wi in range(2):
            sj = 8 + 24 * wi
            for c in range(2):
                first = (wi == 0 and c == 0)
                self.stt('dve', self.modA[:, wi, :, c], self.mod[:, sj:sj + 8, c], 1.0, self.normg[:, i, wi, :], ALU.add, ALU.mult,
                         r=[self.mod, self.normg], w=[self.modA] if first else (), pw=() if first else [self.modA])

    def norm_mod(self, xb, hb, wi, c, nt, sq=None):
        sj = 24 * wi
        self.act(hb[:, :, :], xb[:, :, :], AF.Square, r=[xb], w=[hb])
        p = self.pnext()
        for kc in range(8):
            self.mm(p[:, 0:nt], self.cst.ap[:, 1, :], hb[:, kc, :], kc == 0, kc == 7, r=[self.cst, hb], **wr(p, kc == 0))
        rs = self.rstd.get()
        self.act(rs[:, 0:nt], p[:, 0:nt], AF.Sqrt, r=[p, self.epsb], w=[rs], scale=1.0 / D, bias=self.epsb[:, 0:1])
        self.recip(rs[:, 0:nt], rs[:, 0:nt], r=[rs], w=[rs])
        for kc in range(8):
            self.tt('dve' if kc % 2 == 0 else 'pool', hb[:, kc, :], xb[:, kc, :], rs[:, 0:nt], ALU.mult, r=[xb, rs],
                    w=[hb] if kc == 0 else (), pw=() if kc == 0 else [hb])
        for kc in range(8):
            self.act(hb[:, kc, :], hb[:, kc, :], AF.Identity, r=[hb, self.modA, self.mod], pw=[hb],
                     scale=self.modA[:, wi, kc, c:c + 1], bias=self.mod[:, sj + kc, c:c + 1])

    def norm_mod2(self, xtk, xap, htk, hap, tmp, wi, c, nt, first):
        sj = 24 * wi
        self.act(tmp[:, :, 0:nt], xap, AF.Square, r=[xtk], w=[tmp])
        p = self.pnext()
        for kc in range(8):
            self.mm(p[:, 0:nt], self.cst.ap[:, 1, :], tmp[:, kc, 0:nt], kc == 0, kc == 7, r=[self.cst, tmp], **wr(p, kc == 0))
        rs = self.rstd.get()
        self.act(rs[:, 0:nt], p[:, 0:nt], AF.Sqrt, r=[p, self.epsb], w=[rs], scale=1.0 / D, bias=self.epsb[:, 0:1])
        self.recip(rs[:, 0:nt], rs[:, 0:nt], r=[rs], w=[rs])
        for kc in range(8):
            self.tt('dve' if kc % 2 == 0 else 'pool', tmp[:, kc, 0:nt], xap[:, kc, :], rs[:, 0:nt], ALU.mult, r=[xtk, rs], **wr(tmp, kc == 0))
        for kc in range(8):
            self.act(hap[:, kc, :], tmp[:, kc, 0:nt], AF.Identity, r=[tmp, self.modA, self.mod], **wr(htk, first and kc == 0),
                     scale=self.modA[:, wi, kc, c:c + 1], bias=self.mod[:, sj + kc, c:c + 1])

    def stage_ffn16(self, i):
        self.areset()
        SB = 1024
        NH = SB // 512
        xbs = self.take([128, 8, SB], 1)
        hbs = self.take([128, 8, SB], 1, BF16)
        sq = self.take([128, 8, 512])
        self.rstd = self.take([128, 512], 2)
        acts = self.take([128, 22, SB], 1, BF16)
        wst = self.take([128, 8, 2, 128], 3)
        w16 = self.take([128, 8, 2, 128], 3, BF16)
        wost = self.take([128, 22, 128], 2)
        wo16 = self.take([128, 22, 128], 2, BF16)
        sg = self.take([128, 512], 2)
        wiv = self.ffn_w_in[i].rearrange("(kc p) n -> p kc n", p=128)
        wov = self.ffn_w_out[i].rearrange("(kc p) n -> p kc n", p=128)
        for sb in range(TT // SB):
            c = 0 if sb * SB < NP * TP else 1
            tks = self.xT_tk[sb * 8:(sb + 1) * 8]
            xb = xbs.get()
            self.ld(xb[:], self.xT_v[:, :, sb * SB:(sb + 1) * SB], r=tks, w=[xb], eng='pool')
            hb = hbs.get()
            for hf in range(NH):
                hs = slice(hf * 512, (hf + 1) * 512)
                self.norm_mod2(xb, xb[:, :, hs], hb, hb[:, :, hs], sq, 1, c, 512, hf == 0)
            at = acts.get()
            for j in range(22):
                ws = wst.get()
                self.ld(ws[:, :, 0, :], wiv[:, :, j * 128:(j + 1) * 128], r=[self.Win], w=[ws])
                self.ld(ws[:, :, 1, :], wiv[:, :, DFF + j * 128:DFF + (j + 1) * 128], r=[self.Win], pw=[ws], eng='pool')
                w = w16.get()
                self.cp('dve' if j % 2 else 'act', w[:], ws[:], r=[ws], w=[w])
                for hf in range(NH):
                    hs = slice(hf * 512, (hf + 1) * 512)
                    pg = self.pnext()
                    pu = self.pnext()
                    for kc in range(8):
                        self.mm(pg[:, :], w[:, kc, 0, :], hb[:, kc, hs], kc == 0, kc == 7, r=[w, hb], **wr(pg, kc == 0))
                    for kc in range(8):
                        self.mm(pu[:, :], w[:, kc, 1, :], hb[:, kc, hs], kc == 0, kc == 7, r=[w, hb], **wr(pu, kc == 0))
                    s_ = sg.get()
                    self.act(s_[:], pg[:, :], AF.Silu, r=[pg], w=[s_])
                    self.tt('dve', at[:, j, hs], s_[:], pu[:, :], ALU.mult, r=[s_, pu], **wr(at, j == 0 and hf == 0))
            for oc in range(8):
                ws = wost.get()
                self.ld(ws[:], wov[:, :, oc * 128:(oc + 1) * 128], r=[self.Win], w=[ws])
                w = wo16.get()
                self.cp('dve' if oc % 2 else 'act', w[:], ws[:], r=[ws], w=[w])
                for hf in range(NH):
                    hs = slice(hf * 512, (hf + 1) * 512)
                    p = self.pnext()
                    for k2 in range(22):
                        self.mm(p[:, :], w[:, k2, :], at[:, k2, hs], k2 == 0, k2 == 21, r=[w, at], **wr(p, k2 == 0))
                    self.stt('dve', xb[:, oc, hs], p[:, :], self.mod[:, 40 + oc, c:c + 1], xb[:, oc, hs], ALU.mult, ALU.add,
                             r=[p, self.mod, xb], pw=[xb])
            for q in range(SB // 128):
                self.ld(self.xT_v[:, :, sb * SB + q * 128: sb * SB + (q + 1) * 128], xb[:, :, q * 128:(q + 1) * 128],
                        r=[xb], w=[tks[q]], eng='pool')

    def stage_ffn(self, i):
        self.areset()
        NT = 512
        xbs = self.take([128, 8, NT], 2)
        hbs = self.take([128, 8, NT], 1)
        self.rstd = self.take([128, NT], 2)
        acts = self.take([128, 22, NT], 1)
        sg = self.take([128, NT], 2)
        wins = self.take([128, 8, 2, 128], 3)
        wouts = self.take([128, 22, 128], 2)
        wiv = self.ffn_w_in[i].rearrange("(kc p) n -> p kc n", p=128)
        wov = self.ffn_w_out[i].rearrange("(kc p) n -> p kc n", p=128)
        for blk in range(TT // NT):
            c = 0 if blk < (NP * TP) // NT else 1
            tks = self.xT_tk[blk * 4:(blk + 1) * 4]
            xb = xbs.get()
            self.ld(xb[:], self.xT_v[:, :, blk * NT:(blk + 1) * NT], r=tks, w=[xb], eng='pool')
            hb = hbs.get()
            self.norm_mod(xb, hb, 1, c, NT)
            at = acts.get()
            for j in range(22):
                w = wins.get()
                self.ld(w[:, :, 0, :], wiv[:, :, j * 128:(j + 1) * 128], r=[self.Win], w=[w])
                self.ld(w[:, :, 1, :], wiv[:, :, DFF + j * 128:DFF + (j + 1) * 128], r=[self.Win], pw=[w])
                pg = self.pnext()
                pu = self.pnext()
                for kc in range(8):
                    self.mm(pg[:, :], w[:, kc, 0, :], hb[:, kc, :], kc == 0, kc == 7, r=[w, hb], **wr(pg, kc == 0))
                for kc in range(8):
                    self.mm(pu[:, :], w[:, kc, 1, :], hb[:, kc, :], kc == 0, kc == 7, r=[w, hb], **wr(pu, kc == 0))
                s = sg.get()
                self.act(s[:], pg[:, :], AF.Silu, r=[pg], w=[s])
                self.tt('dve', at[:, j, :], s[:], pu[:, :], ALU.mult, r=[s, pu], w=[at] if j == 0 else (), pw=() if j == 0 else [at])
            for oc in range(8):
                w = wouts.get()
                self.ld(w[:], wov[:, :, oc * 128:(oc + 1) * 128], r=[self.Win], w=[w])
                p = self.pnext()
                for k2 in range(22):
                    self.mm(p[:, :], w[:, k2, :], at[:, k2, :], k2 == 0, k2 == 21, r=[w, at], **wr(p, k2 == 0))
                self.stt('dve', xb[:, oc, :], p[:, :], self.mod[:, 40 + oc, c:c + 1], xb[:, oc, :], ALU.mult, ALU.add,
                         r=[p, self.mod, xb], pw=[xb])
            for q in range(4):
                self.ld(self.xT_v[:, :, blk * NT + q * 128: blk * NT + (q + 1) * 128], xb[:, :, q * 128:(q + 1) * 128],
                        r=[xb], w=[tks[q]], eng='pool')


def host_consts():
    c = np.zeros((128, 8, 128), np.float32)
    c[:, 0, :] = np.eye(128)
    c[:, 1, :] = 1.0
    k = np.arange(128)[:, None]
    t = np.arange(128)[None, :]
    c[:, 2, :] = (k <= t)
    c[:, 3, :] = (k >= t)
    c[:, 4, :] = (k > t)
    c[:, 5, :] = (k < t)
    return c.reshape(128, 1024)


def rope_tables():
    rows = TS // 64
    row = np.broadcast_to(np.arange(rows)[:, None], (rows, 64)).reshape(-1)
    col = np.broadcast_to(np.arange(64)[None, :], (rows, 64)).reshape(-1)
    inv = (np.float32(10000.0) ** (-np.arange(16, dtype=np.float32) / np.float32(16))).astype(np.float32)
    ang = np.stack([row, col], axis=-1).astype(np.float32)[:, :, None] * inv
    return np.concatenate([np.cos(ang).reshape(TS, 32), np.sin(ang).reshape(TS, 32)], axis=1).astype(np.float32)


_CACHE = {}


def kernel(**inp):
    opts = inp.pop('_opts', {})
    key = repr(sorted(opts.items()))
    if key not in _CACHE:
        P = Prog(opts)
        P.build()
        _CACHE[key] = P
    P = _CACHE[key]
    f = lambda a: np.ascontiguousarray(np.asarray(a, dtype=np.float32))
    xp = f(inp['x_prompt'])
    xs = f(inp['x_sample'])
    c = f(inp['c'])
    c_ctx = f(inp['c_ctx'])
    ada_b = f(inp['ada_b'])
    norm_g = f(inp['norm_g'])
    shared = {
        'consts': host_consts(),
        'ada_w': f(inp['ada_w']),
        'ada_bT': f(ada_b.reshape(4, 48, 128).transpose(2, 0, 1)),
        'normgT': f(norm_g.reshape(4, 2, 8, 128).transpose(3, 0, 1, 2)),
        'ffn_w_in': f(inp['ffn_w_in']),
        'ffn_w_out': f(inp['ffn_w_out']),
        'mlstm_w_in': f(inp['mlstm_w_in'][0]),
        'mlstm_w_out': f(inp['mlstm_w_out'][0]),
        'mlstm_gate_b': f(inp['mlstm_gate_b'].reshape(1, 32)),
        'mlstm_norm_g': f(inp['mlstm_norm_g'].reshape(1, 128)),
    }
    shared.update({
        'diff_w_in': f(inp['diff_w_in'][0]),
        'diff_w_out': f(inp['diff_w_out'][0]),
        'diff_qkg': f(np.concatenate([inp['diff_q_norm_g'][0], inp['diff_k_norm_g'][0]]).reshape(1, 128)),
        'diff_lambda': f(inp['diff_lambda'][0].reshape(1, 256)),
        'diff_subln_g': f(inp['diff_subln_g'][0].reshape(1, 128)),
        'rope_cs': rope_tables(),
    })
    cw = f(inp['gdn_conv_w'])
    shared.update({
        'gdn_w_in': f(inp['gdn_w_in']),
        'gdn_w_out': f(inp['gdn_w_out']),
        'gdn_convT': f(cw.reshape(2, 5, 24, 128).transpose(3, 0, 2, 1)),
        'gdn_a_log': f(inp['gdn_a_log'].reshape(2, 1, 16)),
        'gdn_dt_bias': f(inp['gdn_dt_bias'].reshape(2, 1, 16)),
        'gdn_norm_g': f(inp['gdn_norm_g'].reshape(2, 1, 128)),
    })
    stS = f(inp['state_delta'])
    ck = f(inp['cache_diff_k'])
    cv = f(inp['cache_diff_v'])
    stC = f(inp['state_mlstm_C'])
    stn = f(inp['state_mlstm_n'])
    stm = f(inp['state_mlstm_m'])
    in_maps = []
    for k in range(NCORE):
        m = dict(shared)
        m['x_tok'] = f(np.concatenate([xp[NP * k:NP * (k + 1)].reshape(NP * TP, D), xs[k]], axis=0))
        cond = np.stack([c_ctx, c[k]], axis=-1)
        m['condT'] = f(cond.reshape(8, 128, 2).transpose(1, 0, 2))
        m['st_S'] = f(stS[k])
        m['ctx_k'] = f(ck[k, 0])
        m['ctx_v'] = f(cv[k, 0])
        m['st_C'] = f(stC[k, 0])
        m['st_n'] = f(stn[k, 0].transpose(0, 2, 1))
        m['st_m'] = f(stm[k, 0].reshape(1, 16))
        in_maps.append({n: m[n] for n in P.din})
    res = run_bass_kernel_spmd(P.nc, in_maps, core_ids=list(range(NCORE)))
    R = res.results
    y = np.stack([r['y_tok'] for r in R])
    y_prompt = y[:, :NP * TP].reshape(NCORE * NP, TP, D)
    y_sample = y[:, NP * TP:]
    outs = [y_prompt, y_sample]
    if 'newS' in P.dout:
        outs.append(np.stack([r['newS'] for r in R]).reshape(NCORE * NP, 2, 2, 8, 128, 128))
    if 'newC' in P.dout:
        outs.append(np.stack([r['newC'] for r in R]).reshape(NCORE * NP, 1, 2, 8, 64, 128))
        outs.append(np.ascontiguousarray(np.stack([r['newn'] for r in R]).reshape(NCORE * NP, 1, 2, 64, 8).transpose(0, 1, 2, 4, 3)))
        outs.append(np.stack([r['newm'] for r in R]).reshape(NCORE * NP, 1, 2, 8))
    if 'newk' in P.dout:
        outs.append(np.stack([r['newk'] for r in R]).reshape(NCORE * NP, 1, 8, 2, TP, 64))
        outs.append(np.stack([r['newv'] for r in R]).reshape(NCORE * NP, 1, 8, TP, 128))
    return tuple(outs)
```

```python
import numpy as np
from contextlib import ExitStack
import concourse.bass as bass
import concourse.mybir as mybir
from concourse.bass_utils import run_bass_kernel_spmd

F32 = mybir.dt.float32
BF16 = mybir.dt.bfloat16
AF = mybir.ActivationFunctionType
ALU = mybir.AluOpType
AX = mybir.AxisListType

ENGS = ('pe', 'act', 'dve', 'pool', 'sp')
NDS = 40

D = 1024
NCORE = 8
NP = 4
TP = 256
TS = 2048
TT = NP * TP + TS
DFF = 2816
EPS = 1e-6


class Tk:
    __slots__ = ('ap', 'lw', 'rd', 'rp', 'name')

    def __init__(self, ap, name=''):
        self.ap = ap
        self.lw = {}
        self.rd = {}
        self.rp = {}
        self.name = name

    def __getitem__(self, idx):
        return self.ap[idx]


class Sched:
    def __init__(self, nc, es):
        self.nc = nc
        self.es = es
        self.q = {e: [] for e in ENGS}
        self.sem = {e: es.enter_context(nc.semaphore("s_" + e)) for e in ENGS}
        self.cnt = {e: 0 for e in ENGS}
        self.seen = {e: {} for e in ENGS}
        self.dsem = [es.enter_context(nc.semaphore("d%d" % i)) for i in range(NDS)]
        self.dcnt = [0] * NDS
        self.dnext = 0
        self.nins = 0
        self.uid = 0

    def sb(self, shape, dt=F32, name=None):
        self.uid += 1
        name = name or "t%d" % self.uid
        t = self.es.enter_context(self.nc.sbuf_tensor(name, list(shape), dt))
        return Tk(t, name)

    def ps(self, shape, dt=F32, name=None):
        self.uid += 1
        name = name or "p%d" % self.uid
        t = self.es.enter_context(self.nc.psum_tensor(name, list(shape), dt))
        return Tk(t, name)

    def _wait(self, eng, d):
        k = d[0]
        if eng == 'pe' and k == ('e', 'pe'):
            return
        seen = self.seen[eng]
        if seen.get(k, 0) >= d[2]:
            return
        seen[k] = d[2]
        self.q[eng].append(lambda E, d=d: E.wait_ge(d[1], d[2]))
        self.nins += 1

    def _deps(self, eng, reads, writes, pw):
        deps = {}

        def add(d):
            k = d[0]
            if k not in deps or deps[k][2] < d[2]:
                deps[k] = d
        for t in reads:
            for d in t.lw.values():
                add(d)
        for t in writes:
            for d in t.lw.values():
                add(d)
            for d in t.rd.values():
                add(d)
        for t in pw:
            for d in t.rd.values():
                add(d)
            for d in t.rp.values():
                add(d)
        for d in deps.values():
            self._wait(eng, d)

    def _mark(self, me, reads, writes, pw):
        for t in reads:
            t.rd[me[0]] = me
        for t in writes:
            t.lw = {me[0]: me}
            t.rp = t.rd
            t.rd = {}
        for t in pw:
            t.lw[me[0]] = me

    def op(self, eng, fn, reads=(), writes=(), pw=()):
        self._deps(eng, reads, writes, pw)
        self.cnt[eng] += 1
        sem = self.sem[eng]
        me = (('e', eng), sem, self.cnt[eng])
        self.q[eng].append(lambda E: fn(E).then_inc(sem, 1))
        self.nins += 1
        self._mark(me, reads, writes, pw)

    def dma(self, eng, out_ap, in_ap, reads=(), writes=(), pw=()):
        slot = self.dnext
        self.dnext = (slot + 1) % NDS
        ds = self.dsem[slot]
        self._deps(eng, reads, writes, pw)
        if self.dcnt[slot] > 0:
            self._wait(eng, (('d', slot), ds, 16 * self.dcnt[slot]))
        self.dcnt[slot] += 1
        me = (('d', slot), ds, 16 * self.dcnt[slot])
        self.q[eng].append(lambda E: E.dma_start(out=out_ap, in_=in_ap).then_inc(ds, 16))
        self.nins += 1
        self._mark(me, reads, writes, pw)

    def barrier(self):
        for e in ENGS:
            for o in ENGS:
                if o != e and self.cnt[o] > 0:
                    self._wait(e, (('e', o), self.sem[o], self.cnt[o]))
            for i in range(NDS):
                if self.dcnt[i] > 0:
                    self._wait(e, (('d', i), self.dsem[i], 16 * self.dcnt[i]))

    def emit(self):
        self.barrier()
        q = self.q
        with self.nc.Block() as block:
            @block.tensor
            def _(E):
                for f in q['pe']:
                    f(E)

            @block.scalar
            def _(E):
                for f in q['act']:
                    f(E)

            @block.vector
            def _(E):
                for f in q['dve']:
                    f(E)

            @block.gpsimd
            def _(E):
                for f in q['pool']:
                    f(E)

            @block.sync
            def _(E):
                for f in q['sp']:
                    f(E)


def wr(t, first):
    return {'w': [t]} if first else {'pw': [t]}


class Rot:
    def __init__(self, tiles):
        self.t = tiles
        self.i = 0

    def get(self):
        t = self.t[self.i % len(self.t)]
        self.i += 1
        return t


ARENA_COLS = 50000


class Prog:
    def __init__(self, opts):
        self.opts = opts
        self.nc = bass.Bass("TRN2", target_bir_lowering=False)
        self.es = ExitStack()
        self.din = {}
        self.dout = {}

    def inp(self, name, shape):
        t = self.nc.dram_tensor(name, list(shape), F32, kind="ExternalInput").ap()
        self.din[name] = t
        return t

    def outp(self, name, shape):
        t = self.nc.dram_tensor(name, list(shape), F32, kind="ExternalOutput").ap()
        self.dout[name] = t
        return t

    def scratch(self, name, shape):
        return self.nc.dram_tensor(name, list(shape), F32, kind="Internal").ap()

    def areset(self):
        import inspect
        self.S.barrier()
        self.apos = 0
        if not hasattr(self, 'stages'):
            self.stages = []
        self.stages.append((inspect.stack()[1].function, self.S.cnt['pe']))
        if hasattr(self, 'mark'):
            mk = self.mark
            self.act(mk[:, 0:1], mk[:, 0:1], AF.Sign, r=[mk], w=[mk])

    def take(self, shape, n=None, dt=F32):
        cols = int(np.prod(shape[1:]))
        c32 = cols if dt == F32 else (cols + 1) // 2
        out = []
        for _ in range(n or 1):
            assert self.apos + c32 <= ARENA_COLS, ("arena overflow", self.apos, c32)
            ap = self.arena[0:shape[0], self.apos:self.apos + c32]
            if dt != F32:
                ap = ap.bitcast(dt)[:, 0:cols]
            if len(shape) == 3:
                ap = ap.rearrange("p (a b) -> p a b", a=shape[1])
            elif len(shape) == 4:
                ap = ap.rearrange("p (a b c) -> p a b c", a=shape[1], b=shape[2])
            self.apos += c32
            out.append(Tk(ap))
        return out[0] if n is None else Rot(out)

    def pnext(self):
        p = self.psum[self.pi % 8]
        self.pi += 1
        return p

    def pns(self):
        return self.pnext()

    def mm(self, out, lhsT, rhs, start, stop, r=(), w=(), pw=()):
        self.S.op('pe', lambda E: E.matmul(out, lhsT=lhsT, rhs=rhs, start=start, stop=stop), reads=r, writes=w, pw=pw)

    def tr(self, out, in_, r=(), w=(), pw=()):
        ident = self.ident
        n = in_.shape[0]
        self.S.op('pe', lambda E: E.transpose(out, in_, ident[0:n, 0:n]), reads=list(r) + [ident], writes=w, pw=pw)

    def act(self, out, in_, func, r=(), w=(), pw=(), bias=None, scale=None, accum=None):
        kw = {}
        if bias is not None:
            kw['bias'] = bias
        if scale is not None:
            kw['scale'] = scale
        if accum is not None:
            kw['accum_out'] = accum
        self.S.op('act', lambda E: E.activation(out=out, in_=in_, func=func, **kw), reads=r, writes=w, pw=pw)

    def tt(self, eng, out, a, b, op, r=(), w=(), pw=()):
        self.S.op(eng, lambda E: E.tensor_tensor(out=out, in0=a, in1=b, op=op), reads=r, writes=w, pw=pw)

    def ts(self, eng, out, a, s1, s2, op0, op1=None, r=(), w=(), pw=()):
        if op1 is None:
            self.S.op(eng, lambda E: E.tensor_scalar(out=out, in0=a, scalar1=s1, scalar2=None, op0=op0), reads=r, writes=w, pw=pw)
        else:
            self.S.op(eng, lambda E: E.tensor_scalar(out=out, in0=a, scalar1=s1, scalar2=s2, op0=op0, op1=op1), reads=r, writes=w, pw=pw)

    def stt(self, eng, out, a, s, b, op0, op1, r=(), w=(), pw=()):
        self.S.op(eng, lambda E: E.scalar_tensor_tensor(out=out, in0=a, scalar=s, in1=b, op0=op0, op1=op1), reads=r, writes=w, pw=pw)

    def cp(self, eng, out, in_, r=(), w=(), pw=()):
        if eng == 'act':
            self.S.op('act', lambda E: E.copy(out=out, in_=in_), reads=r, writes=w, pw=pw)
        else:
            self.S.op(eng, lambda E: E.tensor_copy(out=out, in_=in_), reads=r, writes=w, pw=pw)

    def recip(self, out, in_, r=(), w=(), pw=()):
        self.S.op('dve', lambda E: E.reciprocal(out=out, in_=in_), reads=r, writes=w, pw=pw)

    def memset(self, eng, ap, val, w=(), pw=()):
        self.S.op(eng, lambda E: E.memset(ap, val), writes=w, pw=pw)

    def ld(self, out, in_, r=(), w=(), pw=(), eng='sp'):
        self.S.dma(eng, out, in_, reads=r, writes=w, pw=pw)

    def build(self):
        nc = self.nc
        o = self.opts
        with self.es:
            S = self.S = Sched(nc, self.es)
            self.arena = self.es.enter_context(nc.sbuf_tensor("arena", [128, ARENA_COLS], F32))
            self.psum = [S.ps([128, 512], name="ps%d" % i) for i in range(8)]
            self.pi = 0
            self.apos = 0
            self.psmall = [Tk(self.psum[i // 2].ap[:, (i % 2) * 256:(i % 2) * 256 + 256]) for i in range(16)]
            self.psi = 0
            self.x_tok = self.inp("x_tok", [TT, D])
            self.y_tok = self.outp("y_tok", [TT, D])
            self.xT = self.scratch("xT", [8, 128, TT])
            self.xT_v = self.xT.rearrange("c p t -> p c t")
            self.xT_tk = [Tk(None, "xT%d" % i) for i in range(TT // 128)]
            self.Xtok = Tk(None)
            self.Ytok = Tk(None)
            self.Win = Tk(None)
            consts = self.inp("consts", [128, 8 * 128])
            condT = self.inp("condT", [128, 8, 2])
            self.ada_w = self.inp("ada_w", [4, D, 6 * D])
            ada_bT = self.inp("ada_bT", [128, 4, 48])
            normgT = self.inp("normgT", [128, 4, 2, 8])
            self.ffn_w_in = self.inp("ffn_w_in", [4, D, 2 * DFF])
            self.ffn_w_out = self.inp("ffn_w_out", [4, DFF, D])
            self.cst = S.sb([128, 8, 128], name="cst")
            self.ld(self.cst[:], consts.rearrange("p (a b) -> p a b", a=8), r=[self.Win], w=[self.cst])
            self.ones = self.cst.ap[:, 1, :]
            self.sc = S.sb([128, 8, 2], name="sc")
            self.ld(self.sc[:], condT, r=[self.Win], w=[self.sc])
            self.act(self.sc[:], self.sc[:], AF.Silu, r=[self.sc], w=[self.sc])
            self.adab = S.sb([128, 4, 48], name="adab")
            self.ld(self.adab[:], ada_bT, r=[self.Win], w=[self.adab])
            self.normg = S.sb([128, 4, 2, 8], name="normg")
            self.ld(self.normg[:], normgT, r=[self.Win], w=[self.normg])
            self.mod = S.sb([128, 48, 2], name="mod")
            self.modA = S.sb([128, 2, 8, 2], name="modA")
            self.epsb = S.sb([128, 1], name="epsb")
            self.memset('pool', self.epsb[:], EPS, w=[self.epsb])
            self.mark = S.sb([128, 1], name="mark")
            self.memset('pool', self.mark[:], 1.0, w=[self.mark])

            self.stage_in()
            self.bf = o.get('bf16', True)
            mixers = o.get('mixers', (0, 1, 2))
            self.setup_mixers(mixers)
            for i in range(o.get('depth', 4)):
                self.stage_mod(i)
                if i % 3 == 0 and 0 in mixers:
                    self.gdn(i)
                if i % 3 == 1 and 1 in mixers:
                    self.mlstm(i)
                if i % 3 == 2 and 2 in mixers:
                    self.diffattn(i)
                if o.get('ffn', True):
                    if self.bf:
                        self.stage_ffn16(i)
                    else:
                        self.stage_ffn(i)
            self.stage_out()
            S.emit()
        return nc

    def setup_mixers(self, mixers):
        S = self.S
        if 1 in mixers:
            self.ml_w_in = self.inp("mlstm_w_in", [D, 3104])
            self.ml_w_out = self.inp("mlstm_w_out", [D, D])
            ml_gb = self.inp("mlstm_gate_b", [1, 32])
            ml_ng = self.inp("mlstm_norm_g", [1, 128])
            self.st_C = self.inp("st_C", [2, 8, 64, 128])
            self.st_n = self.inp("st_n", [2, 64, 8])
            self.st_m = self.inp("st_m", [1, 16])
            self.newC = self.outp("newC", [NP, 2, 8, 64, 128])
            self.newn = self.outp("newn", [NP, 2, 64, 8])
            self.newm = self.outp("newm", [NP, 2, 8, 1])
            self.ml_gb = S.sb([128, 32], name="ml_gb")
            self.ld(self.ml_gb[:], ml_gb.partition_broadcast(128), r=[self.Win], w=[self.ml_gb])
            self.ml_ng = S.sb([128, 128], name="ml_ng")
            self.ld(self.ml_ng[:], ml_ng.partition_broadcast(128), r=[self.Win], w=[self.ml_ng])
        self.Oout = Tk(None)
        self.qkT = self.scratch("qkT", [24, 128, TT])
        self.qkT_v = self.qkT.rearrange("c p t -> p c t")
        self.qkT_h = self.qkT[0:8].rearrange("c (two p) t -> p (c two) t", two=2)
        self.ktok = self.scratch("ktok", [TT, 2048])
        self.vtok = self.scratch("vtok", [TT, 1024])
        self.otok = self.scratch("otok", [TT, 1024])
        self.gtok = self.scratch("gtok", [TT, 32])
        self.hdir = [self.scratch("hdir%d" % d, [TT, 1024]) for d in range(2)]
        self.proj_tk = [Tk(None) for _ in range(TT // 128)]
        self.prep_tk = [Tk(None) for _ in range(TT // 128)]
        self.hdir_tk = [[Tk(None) for _ in range(TT // 64)] for d in range(2)]
        if 2 in mixers:
            self.df_w_in = self.inp("diff_w_in", [D, 3072])
            self.df_w_out = self.inp("diff_w_out", [D, D])
            df_g = self.inp("diff_qkg", [1, 128])
            df_lam = self.inp("diff_lambda", [1, 256])
            df_sg = self.inp("diff_subln_g", [1, 128])
            self.rope_cs = self.inp("rope_cs", [TS, 64])
            self.ctx_k = self.inp("ctx_k", [8, 2, 256, 64])
            self.ctx_v = self.inp("ctx_v", [8, 256, 128])
            self.newk = self.outp("newk", [NP, 8, 2, TP, 64])
            self.newv = self.outp("newv", [NP, 8, TP, 128])
            self.df_g = S.sb([128, 2, 64], name="df_g")
            self.ld(self.df_g[:], df_g.rearrange("o (a b) -> o a b", a=2).partition_broadcast(128), r=[self.Win], w=[self.df_g])
            self.df_sg = S.sb([128, 128], name="df_sg")
            self.ld(self.df_sg[:], df_sg.partition_broadcast(128), r=[self.Win], w=[self.df_sg])
            lam_init = 0.8 - 0.6 * float(np.exp(-0.3 * 2))
            self.ts('dve', self.df_sg[:], self.df_sg[:], 1.0 - lam_init, None, ALU.mult, r=[self.df_sg], w=[self.df_sg])
            lm = S.sb([128, 4, 64], name="df_lm")
            self.ld(lm[:], df_lam.rearrange("o (a b) -> o a b", a=4).partition_broadcast(128), r=[self.Win], w=[lm])
            l2 = S.sb([128, 2, 64], name="df_l2")
            self.tt('dve', l2[:, 0, :], lm[:, 0, :], lm[:, 1, :], ALU.mult, r=[lm], w=[l2])
            self.tt('dve', l2[:, 1, :], lm[:, 2, :], lm[:, 3, :], ALU.mult, r=[lm], pw=[l2])
            self.nlam = S.sb([128, 4], name="nlam")
            nl = self.nlam
            S.op('dve', lambda E: E.tensor_reduce(out=nl[:, 0:2], in_=l2[:], axis=AX.X, op=ALU.add), reads=[l2], writes=[nl])
            self.act(nl[:, 0:2], nl[:, 0:2], AF.Exp, r=[nl], w=[nl])
            self.tt('dve', nl[:, 2:3], nl[:, 1:2], nl[:, 0:1], ALU.subtract, r=[nl], pw=[nl])
            self.ts('dve', nl[:, 3:4], nl[:, 2:3], -lam_init, None, ALU.add, r=[nl], pw=[nl])
            self.ctxkT = self.scratch("ctxkT", [8, 128, 256])
            self.ctx_tk = Tk(None)
        if 0 in mixers:
            self.gd_w_in = self.inp("gdn_w_in", [2, D, 4128])
            self.gd_w_out = self.inp("gdn_w_out", [2, D, D])
            gd_cw = self.inp("gdn_convT", [128, 2, 24, 5])
            gd_al = self.inp("gdn_a_log", [2, 1, 16])
            gd_dt = self.inp("gdn_dt_bias", [2, 1, 16])
            gd_ng = self.inp("gdn_norm_g", [2, 1, 128])
            self.st_S = self.inp("st_S", [2, 2, 8, 128, 128])
            self.newS = self.outp("newS", [NP, 2, 2, 8, 128, 128])
            self.gd_cw = S.sb([128, 2, 24, 5], name="gd_cw")
            self.ld(self.gd_cw[:], gd_cw, r=[self.Win], w=[self.gd_cw])
            self.gd_nea = S.sb([128, 2, 16], name="gd_nea")
            self.gd_dt = S.sb([128, 2, 16], name="gd_dt")
            self.gd_ng = S.sb([128, 2, 128], name="gd_ng")
            for j in range(2):
                self.ld(self.gd_nea[:, j, :], gd_al[j].partition_broadcast(128), r=[self.Win], **wr(self.gd_nea, j == 0))
                self.ld(self.gd_dt[:, j, :], gd_dt[j].partition_broadcast(128), r=[self.Win], **wr(self.gd_dt, j == 0))
                self.ld(self.gd_ng[:, j, :], gd_ng[j].partition_broadcast(128), r=[self.Win], **wr(self.gd_ng, j == 0))
            self.act(self.gd_nea[:], self.gd_nea[:], AF.Exp, r=[self.gd_nea], w=[self.gd_nea])
            self.ts('dve', self.gd_nea[:], self.gd_nea[:], -1.0, None, ALU.mult, r=[self.gd_nea], w=[self.gd_nea])

    def proj_stage(self, i, w_in, ncol, fm, tm, col_lo=0):
        self.areset()
        NT = 512
        xbs = self.take([128, 8, NT], 1)
        hbs = self.take([128, 8, NT], 1)
        self.rstd = self.take([128, NT], 2)
        W = self.take([128, 8, ncol])
        wv_ = w_in.rearrange("(kc p) n -> p kc n", p=128)
        self.ld(W[:, 0:4, :], wv_[:, 0:4, col_lo:col_lo + ncol], r=[self.Win], w=[W])
        self.ld(W[:, 4:8, :], wv_[:, 4:8, col_lo:col_lo + ncol], r=[self.Win], pw=[W], eng='pool')
        fm = [(a - col_lo, b, c_, d_, e_) for (a, b, c_, d_, e_) in fm]
        tm = [(a - col_lo, b, c_) for (a, b, c_) in tm]
        ofm = self.take([128, NT], 3)
        otm = self.take([128, 512], 3)
        for blk in range(TT // NT):
            c = 0 if blk < (NP * TP) // NT else 1
            tks = self.xT_tk[blk * 4:(blk + 1) * 4]
            ptk = self.proj_tk[blk * 4:(blk + 1) * 4]
            xb = xbs.get()
            self.ld(xb[:], self.xT_v[:, :, blk * NT:(blk + 1) * NT], r=tks, w=[xb], eng='pool')
            hb = hbs.get()
            self.norm_mod(xb, hb, 0, c, NT)
            n = 0
            for (col0, nch, dstv, ch0, scale) in fm:
                for oc in range(nch):
                    p = self.pnext()
                    for kc in range(8):
                        self.mm(p[:, :], W[:, kc, col0 + oc * 128: col0 + (oc + 1) * 128], hb[:, kc, :], kc == 0, kc == 7, r=[W, hb], **wr(p, kc == 0))
                    ot = ofm.get()
                    if n % 2 == 0:
                        self.act(ot[:], p[:, :], AF.Copy, r=[p], w=[ot], scale=scale)
                    else:
                        self.ts('dve', ot[:], p[:, :], scale, None, ALU.mult, r=[p], w=[ot])
                    n += 1
                    self.ld(dstv[:, ch0 + oc, blk * NT:(blk + 1) * NT], ot[:], r=[ot], pw=ptk, eng='pool')
            for q in range(4):
                t0 = blk * NT + q * 128
                for (col0, ncols, dst) in tm:
                    for g0 in range(0, ncols, 512):
                        gw = min(512, ncols - g0)
                        p = self.pnext()
                        for kc in range(8):
                            self.mm(p[:, 0:gw], hb[:, kc, q * 128:(q + 1) * 128], W[:, kc, col0 + g0: col0 + g0 + gw], kc == 0, kc == 7, r=[W, hb], **wr(p, kc == 0))
                        ot = otm.get()
                        if n % 2 == 0:
                            self.cp('act', ot[:, 0:gw], p[:, 0:gw], r=[p], w=[ot])
                        else:
                            self.cp('dve', ot[:, 0:gw], p[:, 0:gw], r=[p], w=[ot])
                        n += 1
                        self.ld(dst[t0:t0 + 128, g0:g0 + gw], ot[:, 0:gw], r=[ot], pw=[ptk[q]], eng='sp')

    def load_w16(self, W16, w_view, ncol, col_lo=0, piece=512, eng2='pool'):
        n = 0
        for c0 in range(0, ncol, piece):
            cw = min(piece, ncol - c0)
            st = self.wstage.get()
            self.ld(st[:, :, 0:cw], w_view[:, :, col_lo + c0:col_lo + c0 + cw], r=[self.Win], w=[st], eng='sp' if n % 2 == 0 else eng2)
            if n % 2 == 0:
                self.cp('dve', W16[:, :, c0:c0 + cw], st[:, :, 0:cw], r=[st], **wr(W16, c0 == 0))
            else:
                self.cp('act', W16[:, :, c0:c0 + cw], st[:, :, 0:cw], r=[st], **wr(W16, c0 == 0))
            n += 1

    def proj_stage16(self, i, w_in, ncol, fm, tm):
        self.areset()
        NT = 512
        xbs = self.take([128, 8, NT], 2)
        hbs = self.take([128, 8, NT], 2, BF16)
        sq = self.take([128, 8, NT])
        self.rstd = self.take([128, NT], 2)
        W = self.take([128, 8, ncol], None, BF16)
        self.wstage = self.take([128, 8, 512], 2)
        self.load_w16(W, w_in.rearrange("(kc p) n -> p kc n", p=128), ncol)
        ofm = self.take([128, NT], 3)
        otm = self.take([128, 512], 3)
        def pA(blk):
            c = 0 if blk < (NP * TP) // NT else 1
            tks = self.xT_tk[blk * 4:(blk + 1) * 4]
            xb = xbs.get()
            self.ld(xb[:], self.xT_v[:, :, blk * NT:(blk + 1) * NT], r=tks, w=[xb], eng='pool')
            hb = hbs.get()
            self.norm_mod2(xb, xb[:, :, :], hb, hb[:, :, :], sq, 0, c, NT, True)
            return hb

        def pB(blk, hb):
            n = 0
            for (col0, nch, dstv, ch0, scale) in fm:
                for oc in range(nch):
                    p = self.pnext()
                    for kc in range(8):
                        self.mm(p[:, :], W[:, kc, col0 + oc * 128: col0 + (oc + 1) * 128], hb[:, kc, :], kc == 0, kc == 7, r=[W, hb], **wr(p, kc == 0))
                    ot = ofm.get()
                    if n % 2 == 0:
                        self.act(ot[:], p[:, :], AF.Copy, r=[p], w=[ot], scale=scale)
                    else:
                        self.ts('dve', ot[:], p[:, :], scale, None, ALU.mult, r=[p], w=[ot])
                    n += 1
                    self.ld(dstv[:, ch0 + oc, blk * NT:(blk + 1) * NT], ot[:], r=[ot], pw=[self.Oout], eng='pool')
            for q in range(4):
                t0 = blk * NT + q * 128
                for (col0, ncols, dst) in tm:
                    for g0 in range(0, ncols, 512):
                        gw = min(512, ncols - g0)
                        p = self.pnext()
                        for kc in range(8):
                            self.mm(p[:, 0:gw], hb[:, kc, q * 128:(q + 1) * 128], W[:, kc, col0 + g0: col0 + g0 + gw], kc == 0, kc == 7, r=[W, hb], **wr(p, kc == 0))
                        ot = otm.get()
                        if n % 2 == 0:
                            self.cp('act', ot[:, 0:gw], p[:, 0:gw], r=[p], w=[ot])
                        else:
                            self.cp('dve', ot[:, 0:gw], p[:, 0:gw], r=[p], w=[ot])
                        n += 1
                        self.ld(dst[t0:t0 + 128, g0:g0 + gw], ot[:, 0:gw], r=[ot], pw=[self.Oout], eng='sp')


        NBK = TT // NT
        hb_next = pA(0)
        for blk in range(NBK):
            hb_cur = hb_next
            if blk + 1 < NBK:
                hb_next = pA(blk + 1)
            pB(blk, hb_cur)

    def mlstm(self, i):
        o = self.opts
        (self.proj_stage16 if self.bf else self.proj_stage)(i, self.ml_w_in, 3104,
                        fm=[(0, 4, self.qkT_v, 0, 0.125), (512, 4, self.qkT_v, 4, 1.0)],
                        tm=[(512, 512, self.ktok), (1024, 1024, self.vtok), (2048, 1024, self.otok), (3072, 32, self.gtok)])
        self.mlstm_scan()
        self.mixer_post(i, self.ml_w_out, self.ml_ng, self.ml_ng[:], self.hdir, 'sigmoid')

    def mlstm_scan(self):
        self.areset()
        cst = self.cst
        Tri = [cst.ap[0:64, 2, 0:64], cst.ap[0:64, 3, 0:64]]
        Str = [cst.ap[0:64, 4, 0:64], cst.ap[0:64, 5, 0:64]]
        ones64 = cst.ap[0:64, 1, 0:64]
        HG = [(0, 3), (3, 3), (6, 2)]
        qks = self.take([64, 16, 64], 6)
        kts = self.take([64, 512], 6)
        v1s = self.take([64, 8, 129], 6)
        for v1 in v1s.t:
            self.memset('pool', v1[:, :, 128:129], 1.0, pw=[v1])
        gts = self.take([64, 32], 6)
        gps = self.take([64, 64], 6)
        tot8s = self.take([64, 8, 129], 5)
        tl8s = self.take([64, 8, 64], 3)
        E8s = self.take([64, 8, 64], 6)
        kw8s = self.take([64, 8, 64], 6)
        dns = self.take([64, 16], 6)
        houts = self.take([64, 8, 128], 5)
        msm = self.take([8, 8], 2)
        emf = self.take([64, 8], 2)
        GBs = self.take([8, 2], 6)
        em0 = self.take([64, 16])
        co8s = self.take([64, 8, 129], 2)
        n0s = self.take([64, 8], 4)
        ia8s = self.take([64, 8, 129], 3)
        Cst = [[self.take([64, 8, 129]) for d in range(2)] for k in range(2)]
        Mst = [self.take([8, 1]) for d in range(2)]
        streams = [dict(tok0=NP * TP, nch=TS // 64, pidx=-1, start=0, C=Cst[0])]
        for p in range(NP):
            streams.append(dict(tok0=p * TP, nch=TP // 64, pidx=p, start=8 * p, C=Cst[1]))

        def init(st):
            if st['pidx'] >= 0:
                for d in range(2):
                    self.memset('pool', st['C'][d][:], 0.0, w=[st['C'][d]])
                    self.memset('pool', Mst[d][:], 0.0, w=[Mst[d]])
            else:
                self.ld(em0[:], self.st_m.partition_broadcast(64), r=[self.Win], w=[em0])
                self.act(em0[:], em0[:], AF.Exp, r=[em0], w=[em0])
                for d in range(2):
                    T_ = st['C'][d]
                    self.ld(T_[:, :, 0:128], self.st_C[d].rearrange("h k e -> k h e"), r=[self.Win], w=[T_])
                    n0 = n0s.get()
                    self.ld(n0[:], self.st_n[d], r=[self.Win], w=[n0], eng='pool')
                    self.cp('dve', T_[:, :, 128], n0[:], r=[n0], pw=[T_])
                    self.tt('pool', T_[:], T_[:], em0[:, d * 8:d * 8 + 8].unsqueeze(2).to_broadcast([64, 8, 129]), ALU.mult, r=[T_, em0], w=[T_])

        def final(st):
            pidx = st['pidx']
            if pidx < 0:
                return
            for d in range(2):
                dm = msm.get()
                self.ts('dve', dm[:], cst.ap[0:8, 0, 0:8], Mst[d][:, 0:1], None, ALU.mult, r=[cst, Mst[d]], w=[dm])
                pm = self.pnext()
                self.mm(pm[0:64, 0:8], cst.ap[0:8, 1, 0:64], dm[:], True, True, r=[cst, dm], w=[pm])
                ef = emf.get()
                self.act(ef[:], pm[0:64, 0:8], AF.Exp, r=[pm], w=[ef], scale=-1.0)
                self.ld(self.newm[pidx, d], Mst[d][:], r=[Mst[d]], w=[self.Oout], eng='pool')
                co = co8s.get()
                self.tt('dve', co[:], st['C'][d][:], ef[:].unsqueeze(2).to_broadcast([64, 8, 129]), ALU.mult, r=[st['C'][d], ef], w=[co])
                self.ld(self.newC[pidx, d].rearrange("h k e -> k h e"), co[:, :, 0:128], r=[co], pw=[self.Oout], eng='sp')
                n1 = n0s.get()
                self.cp('dve', n1[:], co[:, :, 128], r=[co], w=[n1])
                self.ld(self.newn[pidx, d], n1[:], r=[n1], pw=[self.Oout], eng='pool')

        def build_ctx(st, d, t0):
            pidx = st['pidx']
            ptk = [self.proj_tk[t0 // 128]]
            qk = qks.get()
            self.ld(qk[:], self.qkT_h[:, :, t0:t0 + 64], r=ptk, w=[qk])
            kt = kts.get()
            self.ld(kt[:], self.ktok[t0:t0 + 64, 0:512], r=ptk, w=[kt], eng="pool")
            v1 = v1s.get()
            self.ld(v1[:, :, 0:128], self.vtok[t0:t0 + 64, :].rearrange("t (h e) -> t h e", h=8), r=ptk, pw=[v1])
            gt = gts.get()
            self.ld(gt[:], self.gtok[t0:t0 + 64, :], r=ptk, w=[gt], eng='pool')
            gp = gps.get()
            dc = slice(d * 8, d * 8 + 8)
            self.tt('dve', gp[:, 0:8], gt[:, dc], self.ml_gb[0:64, dc], ALU.add, r=[gt, self.ml_gb], w=[gp])
            self.tt('dve', gp[:, 16:24], gt[:, 16 + d * 8:24 + d * 8], self.ml_gb[0:64, 16 + d * 8:24 + d * 8], ALU.add, r=[gt, self.ml_gb], pw=[gp])
            self.act(gp[:, 16:24], gp[:, 16:24], AF.Exp, r=[gp], pw=[gp], scale=-1.0)
            self.act(gp[:, 16:24], gp[:, 16:24], AF.Ln, r=[gp], pw=[gp], bias=1.0)
            self.ts('dve', gp[:, 16:24], gp[:, 16:24], -1.0, None, ALU.mult, r=[gp], pw=[gp])
            lf = gp[:, 16:24]
            pg = self.pnext()
            self.mm(pg[0:64, 0:8], Tri[d], lf, True, True, r=[cst, gp], w=[pg])
            self.mm(pg[0:64, 8:16], Str[d], lf, True, True, r=[cst, gp], pw=[pg])
            self.mm(pg[0:64, 16:24], ones64, lf, True, True, r=[cst, gp], pw=[pg])
            self.act(gp[:, 32:40], pg[0:64, 0:8], AF.Exp, r=[pg], pw=[gp])
            self.tt('dve', gp[:, 56:64], pg[0:64, 8:16], gp[:, 0:8], ALU.add, r=[pg, gp], pw=[gp])
            self.act(gp[:, 40:48], gp[:, 56:64], AF.Exp, r=[gp], pw=[gp])
            self.act(gp[:, 48:56], pg[0:64, 16:24], AF.Exp, r=[pg], pw=[gp])
            self.act(gp[:, 8:16], gp[:, 0:8], AF.Exp, r=[gp], pw=[gp])
            if pidx >= 0:
                pt = self.pnext()
                self.tr_(pt[0:8, 0:64], gp[:, 56:64], r=[gp], w=[pt])
                GB = GBs.get()
                self.S.op('dve', lambda E, GB=GB, pt=pt: E.tensor_reduce(out=GB[:, 0:1], in_=pt[0:8, 0:64], axis=AX.X, op=ALU.max), reads=[pt], writes=[GB])
                pb = self.pnext()
                self.mm(pb[0:8, 0:1], lf, cst.ap[0:64, 1, 0:1], True, True, r=[gp, cst], w=[pb])
                self.stt('dve', Mst[d][:], Mst[d][:], pb[0:8, 0:1], GB[:, 0:1], ALU.add, ALU.max, r=[Mst[d], pb, GB], w=[Mst[d]])
            return dict(d=d, t0=t0, qk=qk, kt=kt, v1=v1, gp=gp, ho=houts.get(), t8=tot8s.get(), C=st['C'][d])

        def phases(ctxs):
            for cx in ctxs:
                d, gp = cx['d'], cx['gp']
                tl8 = tl8s.get()
                self.tt('dve', tl8[:], Tri[d].unsqueeze(1).to_broadcast([64, 8, 64]), gp[:, 16:24].unsqueeze(2).to_broadcast([64, 8, 64]), ALU.mult, r=[cst, gp], w=[tl8])
                pD = self.pnext()
                self.mm(pD[0:64, 0:512], Str[d], tl8[:].rearrange("p h t -> p (h t)"), True, True, r=[cst, tl8], w=[pD])
                E8 = E8s.get()
                self.act(E8[:].rearrange("p h t -> p (h t)"), pD[0:64, 0:512], AF.Exp, r=[pD], w=[E8])
                cx['E8'] = E8
            for cx in ctxs:
                d, gp, kt, E8 = cx['d'], cx['gp'], cx['kt'], cx['E8']
                self.tt('pool', E8[:], E8[:], Tri[d].unsqueeze(1).to_broadcast([64, 8, 64]), ALU.mult, r=[E8, cst], w=[E8])
                self.tt('dve', E8[:], E8[:], gp[:, 8:16].unsqueeze(2).to_broadcast([64, 8, 64]), ALU.mult, r=[E8, gp], w=[E8])
                kw8 = kw8s.get()
                self.tt('pool', kw8[:], kt[:].rearrange("t (h e) -> t h e", h=8), gp[:, 40:48].unsqueeze(2).to_broadcast([64, 8, 64]), ALU.mult, r=[kt, gp], w=[kw8])
                cx['kw8'] = kw8
            for cx in ctxs:
                qk, E8 = cx['qk'], cx['E8']
                pK = self.pnext()
                for h in range(8):
                    self.mm(pK[0:64, h * 64:(h + 1) * 64], qk[:, 8 + h, :], qk[:, h, :], True, True, r=[qk], **wr(pK, h == 0))
                self.tt('dve', E8[:].rearrange("p h t -> p (h t)"), E8[:].rearrange("p h t -> p (h t)"), pK[0:64, 0:512], ALU.mult, r=[E8, pK], w=[E8])
            for cx in ctxs:
                qk, v1, gp, t8, E8, C_ = cx['qk'], cx['v1'], cx['gp'], cx['t8'], cx['E8'], cx['C']
                ia8 = ia8s.get()
                for bi, (h0, nh) in enumerate(HG):
                    pI = self.pnext()
                    for hh in range(nh):
                        h = h0 + hh
                        self.mm(pI[0:64, hh * 129:(hh + 1) * 129], E8[:, h, :], v1[:, h, :], True, True, r=[E8, v1], **wr(pI, hh == 0))
                    self.cp('act', ia8[:, h0:h0 + nh, :], pI[0:64, 0:nh * 129].rearrange("p (h e) -> p h e", h=nh), r=[pI], **wr(ia8, bi == 0))
                for bi, (h0, nh) in enumerate(HG):
                    pN = self.pnext()
                    for hh in range(nh):
                        h = h0 + hh
                        self.mm(pN[0:64, hh * 129:(hh + 1) * 129], qk[:, h, :], C_[:, h, :], True, True, r=[qk, C_], **wr(pN, hh == 0))
                    self.tt('dve', t8[:, h0:h0 + nh, :], pN[0:64, 0:nh * 129].rearrange("p (h e) -> p h e", h=nh),
                            gp[:, 32 + h0:32 + h0 + nh].unsqueeze(2).to_broadcast([64, nh, 129]), ALU.mult, r=[pN, gp], **wr(t8, bi == 0))
                self.tt('dve', t8[:], t8[:], ia8[:], ALU.add, r=[t8, ia8], w=[t8])
            for cx in ctxs:
                d, t0, ho, t8 = cx['d'], cx['t0'], cx['ho'], cx['t8']
                dn = dns.get()
                den = t8[:, :, 128]
                self.ts('dve', dn[:, 0:8], den, -1.0, None, ALU.mult, r=[t8], w=[dn])
                self.tt('dve', dn[:, 0:8], dn[:, 0:8], den, ALU.max, r=[dn, t8], w=[dn])
                self.ts('dve', dn[:, 0:8], dn[:, 0:8], 1.0, None, ALU.max, r=[dn], w=[dn])
                self.recip(dn[:, 8:16], dn[:, 0:8], r=[dn], pw=[dn])
                self.tt('pool', ho[:], t8[:, :, 0:128], dn[:, 8:16].unsqueeze(2).to_broadcast([64, 8, 128]), ALU.mult, r=[t8, dn], w=[ho])
                self.ld(self.hdir[d][t0:t0 + 64, :].rearrange("t (h e) -> t h e", h=8), ho[:], r=[ho], w=[self.hdir_tk[d][t0 // 64]], eng='pool')
            for cx in ctxs:
                v1, gp, C_, kw8 = cx['v1'], cx['gp'], cx['C'], cx['kw8']
                pUs = []
                for bi, (h0, nh) in enumerate(HG):
                    pU = self.pnext()
                    for hh in range(nh):
                        h = h0 + hh
                        self.mm(pU[0:64, hh * 129:(hh + 1) * 129], kw8[:, h, :], v1[:, h, :], True, True, r=[kw8, v1], **wr(pU, hh == 0))
                    pUs.append(pU)
                self.tt('pool', C_[:], C_[:], gp[:, 48:56].unsqueeze(2).to_broadcast([64, 8, 129]), ALU.mult, r=[C_, gp], w=[C_])
                for bi, (h0, nh) in enumerate(HG):
                    self.tt('dve', C_[:, h0:h0 + nh, :], C_[:, h0:h0 + nh, :], pUs[bi][0:64, 0:nh * 129].rearrange("p (h e) -> p h e", h=nh), ALU.add,
                            r=[C_, pUs[bi]], w=[C_])

        nsteps = max(st['start'] + st['nch'] for st in streams)
        for g in range(nsteps):
            ctxs = []
            for st in streams:
                ls = g - st['start']
                if 0 <= ls < st['nch']:
                    if ls == 0:
                        init(st)
                    for d in range(2):
                        c = ls if d == 0 else st['nch'] - 1 - ls
                        ctxs.append(build_ctx(st, d, st['tok0'] + c * 64))
            phases(ctxs)
            for st in streams:
                if g == st['start'] + st['nch'] - 1:
                    final(st)

    def gdn(self, i):
        j = i // 3
        w_in = self.gd_w_in[j]
        if self.bf:
            self.proj_stage16(i, w_in, 4128, fm=[(0, 24, self.qkT_v, 0, 1.0)],
                              tm=[(3072, 1024, self.otok), (4096, 32, self.gtok)])
        else:
            self.proj_stage(i, w_in, 2048, fm=[(0, 16, self.qkT_v, 0, 1.0)], tm=[], col_lo=0)
            self.proj_stage(i, w_in, 2080, fm=[(2048, 8, self.qkT_v, 16, 1.0)],
                            tm=[(3072, 1024, self.otok), (4096, 32, self.gtok)], col_lo=2048)
        stop = self.opts.get('gdn_stop', 9)
        if stop >= 2:
            self.gdn_conv(j)
        if stop >= 3:
            self.gdn_scan(j)
        if stop >= 4:
            self.mixer_post(i, self.gd_w_out[j], self.gd_ng, self.gd_ng[:, j, :], self.hdir, 'silu')

    def gdn_conv(self, j):
        self.areset()
        NBUF = 6
        xins = self.take([128, TS + 16], NBUF)
        tmps = self.take([128, TS], 3)
        sqs = self.take([128, 512], 4)
        rss = self.take([128, 512], 6)
        tos = self.take([128, 4, 128], 4)
        accs = []
        for _ in range(NBUF):
            base = self.take([128, TS])
            accs.append((base.ap, [Tk(base.ap[:, b * 512:(b + 1) * 512]) for b in range(4)]))
        cw = self.gd_cw
        n = 0
        items = [(tok0, ns, T, ch) for (tok0, ns, T) in [(0, NP, TP), (NP * TP, 1, TS)] for ch in range(24)]

        def phA(idx):
            tok0, ns, T, ch = items[idx]
            W = T + 4
            on_dve = (idx % 2 == 0)
            xin = xins.get()
            acc_ap, accb = accs[idx % NBUF]
            xv = xin[:, 0:ns * W].rearrange("p (s w) -> p s w", s=ns)
            av = acc_ap[:, 0:ns * T].rearrange("p (s t) -> p s t", s=ns)
            self.memset('pool', xv[:, :, 0:2], 0.0, w=[xin])
            self.memset('pool', xv[:, :, T + 2:T + 4], 0.0, pw=[xin])
            self.ld(xv[:, :, 2:T + 2], self.qkT_v[:, ch, tok0:tok0 + ns * T].rearrange("p (s t) -> p s t", s=ns), pw=[xin])
            if on_dve:
                self.ts('dve', av, xv[:, :, 0:T], cw[:, j, ch, 0:1], None, ALU.mult, r=[xin, cw], w=accb)
                for k in range(1, 5):
                    self.stt('dve', av, xv[:, :, k:k + T], cw[:, j, ch, k:k + 1], av, ALU.mult, ALU.add, r=[xin, cw] + accb, w=accb)
            else:
                self.act(av, xv[:, :, 0:T], AF.Copy, r=[xin, cw], w=accb, scale=cw[:, j, ch, 0:1])
                for k in range(1, 5):
                    tm_ = tmps.get()
                    tv = tm_[:, 0:ns * T].rearrange("p (s t) -> p s t", s=ns)
                    self.act(tv, xv[:, :, k:k + T], AF.Copy, r=[xin, cw], w=[tm_], scale=cw[:, j, ch, k:k + 1])
                    self.tt('pool', av, av, tv, ALU.add, r=accb + [tm_], w=accb)
            self.act(acc_ap[:, 0:ns * T], acc_ap[:, 0:ns * T], AF.Silu, r=accb, w=accb)
            return (tok0, ns * T, ch, acc_ap, accb)

        def phB(cx):
            tok0, NTOK, ch, acc_ap, accb = cx
            nb = NTOK // 512
            if ch < 16:
                scale = (128.0 ** -0.5) if ch < 8 else 1.0
                ps_, rs_ = [], []
                for b in range(nb):
                    bs = slice(b * 512, (b + 1) * 512)
                    sq = sqs.get()
                    self.tt('pool', sq[:], acc_ap[:, bs], acc_ap[:, bs], ALU.mult, r=[accb[b]], w=[sq])
                    p = self.pnext()
                    self.mm(p[:, :], self.cst.ap[:, 1, :], sq[:], True, True, r=[self.cst, sq], w=[p])
                    ps_.append(p)
                for b in range(nb):
                    rs = rss.get()
                    self.act(rs[:], ps_[b][:, :], AF.Sqrt, r=[ps_[b], self.epsb], w=[rs], bias=self.epsb[:, 0:1])
                    rs_.append(rs)
                for b in range(nb):
                    bs = slice(b * 512, (b + 1) * 512)
                    rs = rs_[b]
                    self.recip(rs[:], rs[:], r=[rs], w=[rs])
                    self.stt('dve', acc_ap[:, bs], acc_ap[:, bs], scale, rs[:], ALU.mult, ALU.mult, r=[accb[b], rs], w=[accb[b]])

        def phC(cx):
            tok0, NTOK, ch, acc_ap, accb = cx
            nb = NTOK // 512
            if ch < 16:
                self.ld(self.qkT_v[:, ch, tok0:tok0 + NTOK], acc_ap[:, 0:NTOK], r=accb[0:nb], pw=[self.Oout], eng='pool')
            if ch >= 8:
                dst = self.ktok if ch < 16 else self.vtok
                c0 = (ch - 8) * 128 if ch < 16 else (ch - 16) * 128
                for b in range(nb):
                    p = self.pnext()
                    for k in range(4):
                        self.tr_(p[:, k * 128:(k + 1) * 128], acc_ap[:, b * 512 + k * 128:b * 512 + (k + 1) * 128], r=[accb[b]], **wr(p, k == 0))
                    to = tos.get()
                    self.cp('act' if b % 2 else 'dve', to[:], p[:, :].rearrange("p (a b) -> p a b", a=4), r=[p], w=[to])
                    self.ld(dst[tok0 + b * 512:tok0 + (b + 1) * 512, c0:c0 + 128].rearrange("(n p) e -> p n e", p=128), to[:], r=[to], pw=[self.Oout], eng='sp')

        cxs = {}
        NI = len(items)
        for it in range(NI + 2):
            if it < NI:
                cxs[it] = phA(it)
            if 0 <= it - 1 < NI:
                phB(cxs[it - 1])
            if 0 <= it - 2 < NI:
                phC(cxs.pop(it - 2))

    def gdn_scan(self, j):
        self.areset()
        cst = self.cst
        Tri = [cst.ap[0:64, 2, 0:64], cst.ap[0:64, 3, 0:64]]
        Str = [cst.ap[0:64, 4, 0:64], cst.ap[0:64, 5, 0:64]]
        Sm = [cst.ap[0:64, 5, 0:64], cst.ap[0:64, 4, 0:64]]
        I64 = cst.ap[0:64, 0, 0:64]
        ones64w = cst.ap[0:64, 1, 0:128]

        def bh(m):
            return m.unsqueeze(1).to_broadcast([64, 8, 64])

        def bt(v, n, np_=64):
            return v.unsqueeze(2).to_broadcast([np_, v.shape[1], n])

        S8 = [self.take([128, 8, 128]) for d in range(2)]
        qks = self.take([128, 8, 2, 64], 3)
        kts = self.take([64, 8, 128], 3)
        vts = self.take([64, 8, 128], 3)
        gts = self.take([64, 32], 3)
        gps = self.take([64, 48], 3)
        gls = self.take([128, 8], 3)
        tl8s = self.take([64, 8, 64], 2)
        Er8s = self.take([64, 8, 64], 2)
        Ei8s = self.take([64, 8, 64], 2)
        Es8s = self.take([64, 8, 64], 2)
        qkT8s = self.take([64, 8, 64], 3)
        P8s = self.take([64, 8, 64], 3)
        X8s = self.take([64, 8, 64], 5)
        XT8s = self.take([64, 8, 64], 5)
        U8s = self.take([64, 8, 128], 3)
        keg8s = self.take([64, 8, 128], 2)
        kdec8s = self.take([64, 8, 128], 3)
        vn8s = self.take([64, 8, 128], 3)
        o8s = self.take([64, 8, 128], 3)
        WT8s = self.take([128, 8, 64], 3)
        seqs = [(p * TP, TP // 64, p) for p in range(NP)] + [(NP * TP, TS // 64, -1)]
        seqs = seqs[self.opts.get('gdn_seq0', 0):self.opts.get('gdn_seq1', 5)]
        for (tok0, nch, pidx) in seqs:
            for d in range(2):
                if pidx >= 0:
                    self.memset('pool', S8[d][:], 0.0, w=[S8[d]])
                else:
                    self.ld(S8[d][:], self.st_S[j, d].rearrange("h k e -> k h e"), r=[self.Win], w=[S8[d]], eng='sp' if d else 'pool')
            for step in range(nch):
                ctx = []
                for d in range(2):
                    c = step if d == 0 else nch - 1 - step
                    t0 = tok0 + c * 64
                    qk = qks.get()
                    self.ld(qk[:, :, 0, :], self.qkT_v[:, 8:16, t0:t0 + 64], w=[qk])
                    self.ld(qk[:, :, 1, :], self.qkT_v[:, 0:8, t0:t0 + 64], pw=[qk], eng='pool')
                    kt = kts.get()
                    self.ld(kt[:], self.ktok[t0:t0 + 64, 0:1024].rearrange("t (h e) -> t h e", h=8), w=[kt], eng='pool')
                    vt = vts.get()
                    self.ld(vt[:], self.vtok[t0:t0 + 64, :].rearrange("t (h e) -> t h e", h=8), w=[vt])
                    gt = gts.get()
                    self.ld(gt[:], self.gtok[t0:t0 + 64, :], w=[gt], eng='pool')
                    gp = gps.get()
                    dc = slice(d * 8, d * 8 + 8)
                    self.tt('dve', gp[:, 0:8], gt[:, dc], self.gd_dt[0:64, j, dc], ALU.add, r=[gt, self.gd_dt], w=[gp])
                    self.act(gp[:, 0:8], gp[:, 0:8], AF.Exp, r=[gp], pw=[gp])
                    self.act(gp[:, 0:8], gp[:, 0:8], AF.Ln, r=[gp], pw=[gp], bias=1.0)
                    self.tt('dve', gp[:, 8:16], gp[:, 0:8], self.gd_nea[0:64, j, dc], ALU.mult, r=[gp, self.gd_nea], pw=[gp])
                    self.act(gp[:, 16:24], gt[:, 16 + d * 8:24 + d * 8], AF.Sigmoid, r=[gt], pw=[gp])
                    self.ts('dve', gp[:, 24:32], gp[:, 16:24], -1.0, None, ALU.mult, r=[gp], pw=[gp])
                    la = gp[:, 8:16]
                    pg = self.pnext()
                    self.mm(pg[0:64, 0:8], Tri[d], la, True, True, r=[cst, gp], w=[pg])
                    self.mm(pg[0:64, 8:16], Str[d], la, True, True, r=[cst, gp], pw=[pg])
                    self.mm(pg[0:128, 16:24], ones64w, la, True, True, r=[cst, gp], pw=[pg])
                    self.act(gp[:, 32:48], pg[0:64, 0:16], AF.Exp, r=[pg], pw=[gp])
                    gl = gls.get()
                    self.act(gl[:], pg[0:128, 16:24], AF.Exp, r=[pg], w=[gl])
                    ctx.append(dict(d=d, t0=t0, qk=qk, kt=kt, vt=vt, gp=gp, gl=gl))
                for cx in ctx:
                    d, gp = cx['d'], cx['gp']
                    tl8 = tl8s.get()
                    self.tt('dve', tl8[:], bh(Tri[d]), bt(gp[:, 8:16], 64), ALU.mult, r=[cst, gp], w=[tl8])
                    pD = self.pnext()
                    self.mm(pD[0:64, 0:512], Str[d], tl8[:].rearrange("p h t -> p (h t)"), True, True, r=[cst, tl8], w=[pD])
                    Er = Er8s.get()
                    self.act(Er[:].rearrange("p h t -> p (h t)"), pD[0:64, 0:512], AF.Exp, r=[pD], w=[Er])
                    cx['Er'] = Er
                for cx in ctx:
                    d, gp, Er = cx['d'], cx['gp'], cx['Er']
                    Ei = Ei8s.get()
                    Es = Es8s.get()
                    self.tt('pool', Ei[:], Er[:], bh(Tri[d]), ALU.mult, r=[Er, cst], w=[Ei])
                    self.tt('pool', Es[:], Er[:], bh(Sm[d]), ALU.mult, r=[Er, cst], w=[Es])
                    self.tt('dve', Es[:], Es[:], bt(gp[:, 24:32], 64), ALU.mult, r=[Es, gp], w=[Es])
                    cx['Ei'], cx['Es'] = Ei, Es
                for cx in ctx:
                    qk = cx['qk']
                    X = X8s.get()
                    qkT = qkT8s.get()
                    for g in range(2):
                        pG = self.pnext()
                        for hh in range(4):
                            h = 4 * g + hh
                            self.mm(pG[0:64, hh * 128:(hh + 1) * 128], qk[:, h, 0, :], qk[:, h, :, :].rearrange("p a t -> p (a t)"), True, True,
                                    r=[qk], **wr(pG, hh == 0))
                        pv = pG[0:64, 0:512].rearrange("p (h a t) -> p h a t", h=4, a=2)
                        self.tt('dve', X[:, 4 * g:4 * g + 4, :], pv[:, :, 0, :], cx['Es'][:, 4 * g:4 * g + 4, :], ALU.mult, r=[pG, cx['Es']], **wr(X, g == 0))
                        self.tt('dve', qkT[:, 4 * g:4 * g + 4, :], pv[:, :, 1, :], cx['Ei'][:, 4 * g:4 * g + 4, :], ALU.mult, r=[pG, cx['Ei']], **wr(qkT, g == 0))
                    cx['X'], cx['qkT'] = X, qkT
                for cx in ctx:
                    X = cx['X']
                    pT = self.pnext()
                    for h in range(8):
                        self.tr_(pT[0:64, h * 64:(h + 1) * 64], X[:, h, :], r=[X], **wr(pT, h == 0))
                    XT = XT8s.get()
                    self.cp('act', XT[:].rearrange("p h t -> p (h t)"), pT[0:64, 0:512], r=[pT], w=[XT])
                    P_ = P8s.get()
                    self.tt('pool', P_[:], X[:], bh(I64), ALU.add, r=[X, cst], w=[P_])
                    cx['XT'], cx['P'] = XT, P_
                for jn in range(1, 6):
                    for cx in ctx:
                        X, XT = cx['X'], cx['XT']
                        Xn = None
                        if jn < 5:
                            pX = self.pnext()
                            for h in range(8):
                                self.mm(pX[0:64, h * 64:(h + 1) * 64], XT[:, h, :], X[:, h, :], True, True, r=[XT, X], **wr(pX, h == 0))
                            Xn = X8s.get()
                            self.cp('dve', Xn[:].rearrange("p h t -> p (h t)"), pX[0:64, 0:512], r=[pX], w=[Xn])
                        pXT = self.pnext()
                        for h in range(8):
                            self.mm(pXT[0:64, h * 64:(h + 1) * 64], X[:, h, :], XT[:, h, :], True, True, r=[XT, X], **wr(pXT, h == 0))
                        XnT = XT8s.get()
                        self.cp('act', XnT[:].rearrange("p h t -> p (h t)"), pXT[0:64, 0:512], r=[pXT], w=[XnT])
                        cx['X'], cx['XT'] = Xn, XnT
                    for cx in ctx:
                        XT, P_ = cx['XT'], cx['P']
                        pP = self.pnext()
                        for h in range(8):
                            self.mm(pP[0:64, h * 64:(h + 1) * 64], XT[:, h, :], P_[:, h, :], True, True, r=[XT, P_], **wr(pP, h == 0))
                        self.tt('dve', P_[:].rearrange("p h t -> p (h t)"), P_[:].rearrange("p h t -> p (h t)"), pP[0:64, 0:512], ALU.add, r=[P_, pP], w=[P_])
                for cx in ctx:
                    gp, kt, vt, P_ = cx['gp'], cx['kt'], cx['vt'], cx['P']
                    keg = keg8s.get()
                    self.tt('pool', keg[:], kt[:], bt(gp[:, 32:40], 128), ALU.mult, r=[kt, gp], w=[keg])
                    kdec = kdec8s.get()
                    self.tt('pool', kdec[:], kt[:], bt(gp[:, 40:48], 128), ALU.mult, r=[kt, gp], w=[kdec])
                    U = U8s.get()
                    for g in range(2):
                        pU = self.pnext()
                        for hh in range(4):
                            h = 4 * g + hh
                            self.mm(pU[0:64, hh * 128:(hh + 1) * 128], P_[:, h, :], vt[:, h, :], True, True, r=[P_, vt], **wr(pU, hh == 0))
                        self.tt('dve', U[:, 4 * g:4 * g + 4, :], pU[0:64, 0:512].rearrange("p (h e) -> p h e", h=4), bt(gp[:, 16 + 4 * g:20 + 4 * g], 128), ALU.mult,
                                r=[pU, gp], **wr(U, g == 0))
                    pW = self.pnext()
                    for h in range(8):
                        self.mm(pW[0:128, h * 64:(h + 1) * 64], keg[:, h, :], P_[:, h, :], True, True, r=[keg, P_], **wr(pW, h == 0))
                    WT = WT8s.get()
                    self.cp('act', WT[:].rearrange("p h t -> p (h t)"), pW[0:128, 0:512], r=[pW], w=[WT])
                    cx['U'], cx['WT'], cx['kdec'] = U, WT, kdec
                for cx in ctx:
                    d, gp = cx['d'], cx['gp']
                    vn = vn8s.get()
                    for g in range(2):
                        pa = self.pnext()
                        for hh in range(4):
                            h = 4 * g + hh
                            self.mm(pa[0:64, hh * 128:(hh + 1) * 128], cx['WT'][:, h, :], S8[d][:, h, :], True, True, r=[cx['WT'], S8[d]], **wr(pa, hh == 0))
                        self.tt('dve', vn[:, 4 * g:4 * g + 4, :], pa[0:64, 0:512].rearrange("p (h e) -> p h e", h=4), bt(gp[:, 24 + 4 * g:28 + 4 * g], 128), ALU.mult,
                                r=[pa, gp], **wr(vn, g == 0))
                    self.tt('pool', vn[:], vn[:], cx['U'][:], ALU.add, r=[vn, cx['U']], w=[vn])
                    cx['vn'] = vn
                for cx in ctx:
                    d, gp, gl, qk, vn = cx['d'], cx['gp'], cx['gl'], cx['qk'], cx['vn']
                    o8 = o8s.get()
                    for g in range(2):
                        po = self.pnext()
                        for hh in range(4):
                            h = 4 * g + hh
                            self.mm(po[0:64, hh * 128:(hh + 1) * 128], qk[:, h, 1, :], S8[d][:, h, :], True, True, r=[qk, S8[d]], **wr(po, hh == 0))
                        self.tt('dve', o8[:, 4 * g:4 * g + 4, :], po[0:64, 0:512].rearrange("p (h e) -> p h e", h=4), bt(gp[:, 32 + 4 * g:36 + 4 * g], 128), ALU.mult,
                                r=[po, gp], **wr(o8, g == 0))
                    for g in range(2):
                        po2 = self.pnext()
                        for hh in range(4):
                            h = 4 * g + hh
                            self.mm(po2[0:64, hh * 128:(hh + 1) * 128], cx['qkT'][:, h, :], vn[:, h, :], True, True, r=[cx['qkT'], vn], **wr(po2, hh == 0))
                        self.tt('dve', o8[:, 4 * g:4 * g + 4, :], o8[:, 4 * g:4 * g + 4, :], po2[0:64, 0:512].rearrange("p (h e) -> p h e", h=4), ALU.add,
                                r=[po2, o8], pw=[o8])
                    self.ld(self.hdir[d][cx['t0']:cx['t0'] + 64, :].rearrange("t (h e) -> t h e", h=8), o8[:], r=[o8], pw=[self.Oout], eng='pool')
                    for g in range(2):
                        pS = self.pnext()
                        for hh in range(4):
                            h = 4 * g + hh
                            self.mm(pS[0:128, hh * 128:(hh + 1) * 128], cx['kdec'][:, h, :], vn[:, h, :], True, True, r=[cx['kdec'], vn], **wr(pS, hh == 0))
                        Sg = S8[d][:, 4 * g:4 * g + 4, :]
                        self.tt('pool', Sg, Sg, bt(gl[:, 4 * g:4 * g + 4], 128, 128), ALU.mult, r=[S8[d], gl], w=[S8[d]])
                        self.tt('dve', Sg, Sg, pS[0:128, 0:512].rearrange("p (h e) -> p h e", h=4), ALU.add, r=[S8[d], pS], w=[S8[d]])
            if pidx >= 0:
                for d in range(2):
                    self.ld(self.newS[pidx, j, d].rearrange("h k e -> k h e"), S8[d][:], r=[S8[d]], pw=[self.Oout], eng='sp' if d else 'pool')

    def diffattn(self, i):
        (self.proj_stage16 if self.bf else self.proj_stage)(i, self.df_w_in, 3072, fm=[],
                        tm=[(0, 2048, self.ktok), (2048, 1024, self.vtok)])
        self.attn_prep()
        self.attn_core()
        self.mixer_post(i, self.df_w_out, self.df_sg, self.df_sg[:], self.hdir, None)

    def attn_prep(self):
        self.areset()
        xs = self.take([128, 32, 64], 3)
        sqs = self.take([128, 32, 64], 2)
        sss = self.take([128, 64], 3)
        css = self.take([128, 64], 3)
        r1 = self.take([128, 32, 2, 16], 1)
        r2 = self.take([128, 32, 2, 16], 1)
        r3 = self.take([128, 32, 2, 16], 1)
        xr = self.take([128, 32, 64], 2)
        vts = self.take([128, 1024], 2)
        xos = self.take([128, 16, 128], 2)
        cks = self.take([128, 16, 64], 2)
        cko = self.take([128, 8, 128], 2)
        gq = self.df_g
        def pA(t):
            t0 = t * 128
            x = xs.get()
            self.ld(x[:], self.ktok[t0:t0 + 128, :].rearrange("t (g d) -> t g d", g=32), r=[self.proj_tk[t]], w=[x])
            sq = sqs.get()
            self.tt('pool', sq[:], x[:], x[:], ALU.mult, r=[x], w=[sq])
            ss = sss.get()
            self.S.op('dve', lambda E, ss=ss, sq=sq: E.tensor_reduce(out=ss[:, 0:32], in_=sq[:], axis=AX.X, op=ALU.add), reads=[sq], writes=[ss])
            self.act(ss[:, 0:32], ss[:, 0:32], AF.Sqrt, r=[ss, self.epsb], w=[ss], scale=1.0 / 64, bias=self.epsb[:, 0:1])
            self.recip(ss[:, 32:64], ss[:, 0:32], r=[ss], pw=[ss])
            self.tt('dve', x[:], x[:], ss[:, 32:64].unsqueeze(2).to_broadcast([128, 32, 64]), ALU.mult, r=[x, ss], w=[x])
            self.tt('pool', x[:, 0:16, :], x[:, 0:16, :], gq[:, 0, :].unsqueeze(1).to_broadcast([128, 16, 64]), ALU.mult, r=[x, gq], w=[x])
            self.tt('dve', x[:, 16:32, :], x[:, 16:32, :], gq[:, 1, :].unsqueeze(1).to_broadcast([128, 16, 64]), ALU.mult, r=[x, gq], w=[x])
            if t0 < NP * TP:
                p, tl = t0 // TP, t0 % TP
                self.ld(self.newk[p, :, :, tl:tl + 128, :].rearrange("h m t d -> t (h m) d"), x[:, 16:32, :], r=[x], pw=[self.Oout], eng='pool')
                vt = vts.get()
                self.ld(vt[:], self.vtok[t0:t0 + 128, :], r=[self.proj_tk[t]], w=[vt])
                self.ld(self.newv[p, :, tl:tl + 128, :].rearrange("h t e -> t h e"), vt[:].rearrange("t (h e) -> t h e", h=8), r=[vt], pw=[self.Oout], eng='pool')
                src = x
            else:
                cs = css.get()
                self.ld(cs[:], self.rope_cs[t0 - NP * TP:t0 - NP * TP + 128, :], r=[self.Win], w=[cs])
                X = x[:].rearrange("t g (a f r) -> t g a f r", a=2, f=2)
                xa = X[:, :, :, 0, :]
                xb_ = X[:, :, :, 1, :]
                cosb = cs[:, 0:32].rearrange("t (a r) -> t a r", a=2).unsqueeze(1).to_broadcast([128, 32, 2, 16])
                sinb = cs[:, 32:64].rearrange("t (a r) -> t a r", a=2).unsqueeze(1).to_broadcast([128, 32, 2, 16])
                o_ = xr.get()
                O = o_[:].rearrange("t g (a f r) -> t g a f r", a=2, f=2)
                a1, a2, a3 = r1.get(), r2.get(), r3.get()
                self.tt('dve', a1[:], xa, cosb, ALU.mult, r=[x, cs], w=[a1])
                self.tt('pool', a2[:], xb_, sinb, ALU.mult, r=[x, cs], w=[a2])
                self.tt('dve', O[:, :, :, 0, :], a1[:], a2[:], ALU.subtract, r=[a1, a2], w=[o_])
                self.tt('pool', a3[:], xa, sinb, ALU.mult, r=[x, cs], w=[a3])
                self.tt('dve', a1[:], xb_, cosb, ALU.mult, r=[x, cs], w=[a1])
                self.tt('pool', O[:, :, :, 1, :], a3[:], a1[:], ALU.add, r=[a3, a1], pw=[o_])
                src = o_
            return src

        def pB(t, src):
            t0 = t * 128
            xo = xos.get()
            for g in range(4):
                pp = self.pnext()
                for k in range(4):
                    ch = g * 4 + k
                    self.tr_(pp[:, k * 128:(k + 1) * 128], src[:, 2 * ch:2 * ch + 2, :].rearrange("t a d -> t (a d)"), r=[src], **wr(pp, k == 0))
                self.cp('act' if g % 2 else 'dve', xo[:, g * 4:(g + 1) * 4, :], pp[:, :].rearrange("p (a b) -> p a b", a=4), r=[pp], **wr(xo, g == 0))
            self.ld(self.qkT_v[:, 0:16, t0:t0 + 128], xo[:], r=[xo], w=[self.prep_tk[t]], eng='pool')

        NTL = TT // 128
        srcq = {}
        for t in range(NTL + 1):
            if t < NTL:
                srcq[t] = pA(t)
            if t >= 1:
                pB(t - 1, srcq.pop(t - 1))
        for kt in range(2):
            ck = cks.get()
            self.ld(ck[:], self.ctx_k[:, :, kt * 128:(kt + 1) * 128, :].rearrange("h m t d -> t (h m) d"), r=[self.Win], w=[ck])
            co = cko.get()
            for g in range(2):
                pp = self.pnext()
                for k in range(4):
                    ch = g * 4 + k
                    self.tr_(pp[:, k * 128:(k + 1) * 128], ck[:, 2 * ch:2 * ch + 2, :].rearrange("t a d -> t (a d)"), r=[ck], **wr(pp, k == 0))
                self.cp('act' if g % 2 else 'dve', co[:, g * 4:(g + 1) * 4, :], pp[:, :].rearrange("p (a b) -> p a b", a=4), r=[pp], **wr(co, g == 0))
            self.ld(self.ctxkT.rearrange("c p t -> p c t")[:, :, kt * 128:(kt + 1) * 128], co[:], r=[co], **wr(self.ctx_tk, kt == 0), eng='pool')

    def attn_core(self):
        self.areset()
        NKT = (TS + 256) // 128
        bf = self.bf
        MD = BF16 if bf else F32
        qTs = self.take([128, TS], 2, MD)
        kTs = self.take([128, TS + 256], 2, MD)
        V1s = self.take([128, NKT, 129], 2, MD)
        for V1 in V1s.t:
            self.memset('pool', V1[:, :, 128:129], 1.0, pw=[V1])
        PTs = self.take([128, NKT, 512], 2, MD)
        if bf:
            q32 = self.take([128, TS], 2)
            k32 = self.take([128, TS + 256], 2)
            v32 = self.take([128, NKT, 128], 2)
        obs = self.take([128, 4, 128], 2)
        rvs = self.take([128, 2], 4)
        seqs = [(p * TP, TP, False) for p in range(NP)] + [(NP * TP, TS, True)]
        for (tok0, T, is_s) in seqs:
            nk = T + (256 if is_s else 0)
            nkt = nk // 128
            QB = min(512, T)
            tks = self.prep_tk[tok0 // 128:(tok0 + T) // 128]
            ptk = self.proj_tk[tok0 // 128:(tok0 + T) // 128]
            for h in range(8):
                qT = qTs.get()
                kT = kTs.get()
                V1 = V1s.get()
                if bf:
                    qd, kd, vd = q32.get(), k32.get(), v32.get()
                else:
                    qd, kd, vd = qT, kT, V1
                self.ld(qd[:, 0:T], self.qkT_v[:, h, tok0:tok0 + T], r=tks, w=[qd])
                self.ld(kd[:, 0:T], self.qkT_v[:, 8 + h, tok0:tok0 + T], r=tks, w=[kd], eng='pool')
                self.ld(vd[:, 0:T // 128, 0:128], self.vtok[tok0:tok0 + T, h * 128:(h + 1) * 128].rearrange("(n p) e -> p n e", p=128), r=ptk, **wr(vd, bf))
                if is_s:
                    self.ld(kd[:, T:T + 256], self.ctxkT[h], r=[self.ctx_tk], pw=[kd], eng='pool')
                    self.ld(vd[:, T // 128:nkt, 0:128], self.ctx_v[h].rearrange("(n p) e -> p n e", p=128), r=[self.Win], pw=[vd])
                if bf:
                    self.cp('dve', qT[:, 0:T], qd[:, 0:T], r=[qd], w=[qT])
                    self.cp('act', kT[:, 0:nk], kd[:, 0:nk], r=[kd], w=[kT])
                    self.cp('dve', V1[:, 0:nkt, 0:128], vd[:, 0:nkt, :], r=[vd], pw=[V1])
                for qb in range(T // QB):
                    ob = obs.get()
                    for m in range(2):
                        PT = PTs.get()
                        for kt in range(nkt):
                            pS = self.pnext()
                            self.mm(pS[:, 0:QB], kT[m * 64:(m + 1) * 64, kt * 128:(kt + 1) * 128], qT[m * 64:(m + 1) * 64, qb * QB:(qb + 1) * QB],
                                    True, True, r=[kT, qT], w=[pS])
                            self.act(PT[:, kt, 0:QB], pS[:, 0:QB], AF.Exp, r=[pS], **wr(PT, kt == 0), scale=0.125)
                        for qs in range(QB // 128):
                            pO = self.pnext()
                            for kt in range(nkt):
                                self.mm(pO[:, 0:129], PT[:, kt, qs * 128:(qs + 1) * 128], V1[:, kt, :], kt == 0, kt == nkt - 1, r=[PT, V1], **wr(pO, kt == 0))
                            rv = rvs.get()
                            self.recip(rv[:, 0:1], pO[:, 128:129], r=[pO], w=[rv])
                            if m == 0:
                                self.ts('dve', ob[:, qs, :], pO[:, 0:128], rv[:, 0:1], None, ALU.mult, r=[pO, rv], **wr(ob, qs == 0))
                            else:
                                self.tt('dve', rv[:, 1:2], rv[:, 0:1], self.nlam[:, 3:4], ALU.mult, r=[rv, self.nlam], pw=[rv])
                                self.stt('dve', ob[:, qs, :], pO[:, 0:128], rv[:, 1:2], ob[:, qs, :], ALU.mult, ALU.add, r=[pO, rv, ob], pw=[ob])
                    q0 = tok0 + qb * QB
                    nq = QB // 128
                    htk = self.hdir_tk[0][q0 // 64:(q0 + QB) // 64]
                    self.ld(self.hdir[0][q0:q0 + QB, h * 128:(h + 1) * 128].rearrange("(n p) e -> p n e", p=128), ob[:, 0:nq, :], r=[ob], pw=htk, eng='pool')

    def mixer_post(self, i, w_out, ng_tk, ng_bc, hdir, gate):
        self.areset()
        NT = 512
        if self.bf:
            W = self.take([128, 8, D], None, BF16)
            self.wstage = self.take([128, 8, 512], 2)
            self.load_w16(W, w_out.rearrange("(kc p) n -> p kc n", p=128), D)
        else:
            W = self.take([128, 8, D])
            self.ld(W[:], w_out.rearrange("(kc p) n -> p kc n", p=128), r=[self.Win], w=[W])
        hfs = self.take([128, 8, 128], 4)
        hbs = self.take([128, 8, 128], 4)
        ogs = self.take([128, 8, 128], 4)
        sqs = self.take([128, 8, 128], 3)
        sss = self.take([128, 16], 4)
        yTs = self.take([128, 8, NT], 2, BF16 if self.bf else F32)
        xbs = self.take([128, 8, NT], 2)
        ngb = ng_bc.unsqueeze(1).to_broadcast([128, 8, 128])
        blkst = {}

        def pA(t):
            blk, q = t // 4, t % 4
            if q == 0:
                xb = xbs.get()
                self.ld(xb[:], self.xT_v[:, :, blk * NT:(blk + 1) * NT], r=self.xT_tk[blk * 4:(blk + 1) * 4], w=[xb], eng='pool')
                blkst[blk] = [xb, yTs.get()]
            t0 = t * 128
            hf = hfs.get()
            og = ogs.get()
            self.ld(hf[:], hdir[0][t0:t0 + 128, :].rearrange("t (h e) -> t h e", h=8), r=self.hdir_tk[0][t0 // 64:t0 // 64 + 2], w=[hf])
            if gate is not None:
                hb = hbs.get()
                self.ld(hb[:], hdir[1][t0:t0 + 128, :].rearrange("t (h e) -> t h e", h=8), r=self.hdir_tk[1][t0 // 64:t0 // 64 + 2], w=[hb], eng='pool')
                self.ld(og[:], self.otok[t0:t0 + 128, :].rearrange("t (h e) -> t h e", h=8), r=[self.proj_tk[t0 // 128]], w=[og])
                self.tt('dve', hf[:], hf[:], hb[:], ALU.add, r=[hf, hb], w=[hf])
            sq = sqs.get()
            self.tt('pool', sq[:], hf[:], hf[:], ALU.mult, r=[hf], w=[sq])
            ss = sss.get()
            self.S.op('dve', lambda E, ss=ss, sq=sq: E.tensor_reduce(out=ss[:, 0:8], in_=sq[:], axis=AX.X, op=ALU.add), reads=[sq], writes=[ss])
            self.act(ss[:, 0:8], ss[:, 0:8], AF.Sqrt, r=[ss, self.epsb], w=[ss], scale=1.0 / 128, bias=self.epsb[:, 0:1])
            self.recip(ss[:, 8:16], ss[:, 0:8], r=[ss], pw=[ss])
            if gate == 'sigmoid':
                self.act(og[:], og[:], AF.Sigmoid, r=[og], w=[og])
                self.tt('pool', og[:], og[:], ngb, ALU.mult, r=[og, ng_tk], w=[og])
            elif gate == 'silu':
                self.act(og[:], og[:], AF.Silu, r=[og], w=[og])
                self.tt('pool', og[:], og[:], ngb, ALU.mult, r=[og, ng_tk], w=[og])
            else:
                self.cp('pool', og[:], ngb, r=[ng_tk], w=[og])
            self.tt('dve', hf[:], hf[:], ss[:, 8:16].unsqueeze(2).to_broadcast([128, 8, 128]), ALU.mult, r=[hf, ss], w=[hf])
            self.tt('dve', hf[:], hf[:], og[:], ALU.mult, r=[hf, og], w=[hf])
            return hf

        def pB(t, hf):
            blk, q = t // 4, t % 4
            yT = blkst[blk][1]
            for hh in range(2):
                p = self.pnext()
                for k in range(4):
                    self.tr_(p[:, k * 128:(k + 1) * 128], hf[:, hh * 4 + k, :], r=[hf], **wr(p, k == 0))
                dst = yT[:, hh * 4:(hh + 1) * 4, q * 128:(q + 1) * 128]
                src = p[:, :].rearrange("p (a b) -> p a b", a=4)
                self.cp('act' if hh else 'dve', dst, src, r=[p], **wr(yT, q == 0 and hh == 0))

        def pC(blk):
            c = 0 if blk < (NP * TP) // NT else 1
            xb, yT = blkst.pop(blk)
            tks = self.xT_tk[blk * 4:(blk + 1) * 4]
            for oc in range(8):
                p = self.pnext()
                for kc in range(8):
                    self.mm(p[:, :], W[:, kc, oc * 128:(oc + 1) * 128], yT[:, kc, :], kc == 0, kc == 7, r=[W, yT], **wr(p, kc == 0))
                self.stt('dve', xb[:, oc, :], p[:, :], self.mod[:, 16 + oc, c:c + 1], xb[:, oc, :], ALU.mult, ALU.add,
                         r=[p, self.mod, xb], pw=[xb])
            for q in range(4):
                self.ld(self.xT_v[:, :, blk * NT + q * 128: blk * NT + (q + 1) * 128], xb[:, :, q * 128:(q + 1) * 128],
                        r=[xb], w=[tks[q]], eng='pool')

        NTL = TT // 128
        hfq = {}
        for t in range(NTL + 2):
            if t < NTL:
                hfq[t] = pA(t)
            if 0 <= t - 1 < NTL:
                pB(t - 1, hfq.pop(t - 1))
                if (t - 1) % 4 == 3:
                    pass
            if 0 <= t - 2 < NTL and (t - 2) % 4 == 3:
                pC((t - 2) // 4)

    def tr_(self, out, in_, r=(), w=(), pw=()):
        n = in_.shape[0]
        idn = self.cst.ap[0:n, 0, 0:n]
        self.S.op('pe', lambda E: E.transpose(out, in_, idn), reads=list(r) + [self.cst], writes=w, pw=pw)

    def stage_in(self):
        self.areset()
        xin = self.take([128, D], 4)
        xo = self.take([128, 8, 128], 4)
        for t in range(TT // 128):
            a = xin.get()
            self.ld(a[:], self.x_tok[t * 128:(t + 1) * 128, :], r=[self.Xtok], w=[a])
            b = xo.get()
            for h in range(2):
                p = self.pnext()
                for k in range(4):
                    kc = h * 4 + k
                    self.tr_(p[:, k * 128:(k + 1) * 128], a[:, kc * 128:(kc + 1) * 128], r=[a], w=[p] if k == 0 else (), pw=() if k == 0 else [p])
                dst = b[:, h * 4:(h + 1) * 4, :]
                src = p[:, :].rearrange("p (a b) -> p a b", a=4)
                if h == 0:
                    self.cp('dve', dst, src, r=[p], w=[b])
                else:
                    self.cp('act', dst, src, r=[p], pw=[b])
            self.ld(self.xT_v[:, :, t * 128:(t + 1) * 128], b[:], r=[b], w=[self.xT_tk[t]], eng='pool')

    def stage_out(self):
        self.areset()
        xi = self.take([128, 8, 128], 4)
        yo = self.take([128, D], 4)
        for t in range(TT // 128):
            a = xi.get()
            self.ld(a[:], self.xT_v[:, :, t * 128:(t + 1) * 128], r=[self.xT_tk[t]], w=[a])
            b = yo.get()
            for h in range(2):
                p = self.pnext()
                for k in range(4):
                    kc = h * 4 + k
                    self.tr_(p[:, k * 128:(k + 1) * 128], a[:, kc, :], r=[a], w=[p] if k == 0 else (), pw=() if k == 0 else [p])
                if h == 0:
                    self.cp('dve', b[:, 0:512], p[:, :], r=[p], w=[b])
                else:
                    self.cp('act', b[:, 512:1024], p[:, :], r=[p], pw=[b])
            self.ld(self.y_tok[t * 128:(t + 1) * 128, :], b[:], r=[b], w=[self.Ytok], eng='pool')

    def stage_mod(self, i):
        self.areset()
        wt = self.take([128, 8, 512], 2)
        wv = self.ada_w[i].rearrange("(kc p) n -> p kc n", p=128)
        mp = self.pnext()
        for n in range(12):
            w = wt.get()
            self.ld(w[:], wv[:, :, n * 512:(n + 1) * 512], r=[self.Win], w=[w])
            for jj in range(4):
                j = n * 4 + jj
                for kc in range(8):
                    self.mm(mp[:, 2 * j:2 * j + 2], w[:, kc, jj * 128:(jj + 1) * 128], self.sc[:, kc, :], kc == 0, kc == 7,
                            r=[w, self.sc], **wr(mp, j == 0 and kc == 0))
        mpv = mp[:, 0:96].rearrange("p (j c) -> p j c", c=2)
        for c in range(2):
            self.tt('dve', self.mod[:, :, c], mpv[:, :, c], self.adab[:, i, :], ALU.add, r=[mp, self.adab],
                    w=[self.mod] if c == 0 else (), pw=() if c == 0 else [self.mod])
        for wi in range(2):
            sj = 8 + 24 * wi
            for c in range(2):
                first = (wi == 0 and c == 0)
                self.stt('dve', self.modA[:, wi, :, c], self.mod[:, sj:sj + 8, c], 1.0, self.normg[:, i, wi, :], ALU.add, ALU.mult,
                         r=[self.mod, self.normg], w=[self.modA] if first else (), pw=() if first else [self.modA])

    def norm_mod(self, xb, hb, wi, c, nt, sq=None):
        sj = 24 * wi
        self.act(hb[:, :, :], xb[:, :, :], AF.Square, r=[xb], w=[hb])
        p = self.pnext()
        for kc in range(8):
            self.mm(p[:, 0:nt], self.cst.ap[:, 1, :], hb[:, kc, :], kc == 0, kc == 7, r=[self.cst, hb], **wr(p, kc == 0))
        rs = self.rstd.get()
        self.act(rs[:, 0:nt], p[:, 0:nt], AF.Sqrt, r=[p, self.epsb], w=[rs], scale=1.0 / D, bias=self.epsb[:, 0:1])
        self.recip(rs[:, 0:nt], rs[:, 0:nt], r=[rs], w=[rs])
        for kc in range(8):
            self.tt('dve' if kc % 2 == 0 else 'pool', hb[:, kc, :], xb[:, kc, :], rs[:, 0:nt], ALU.mult, r=[xb, rs],
                    w=[hb] if kc == 0 else (), pw=() if kc == 0 else [hb])
        for kc in range(8):
            self.act(hb[:, kc, :], hb[:, kc, :], AF.Identity, r=[hb, self.modA, self.mod], pw=[hb],
                     scale=self.modA[:, wi, kc, c:c + 1], bias=self.mod[:, sj + kc, c:c + 1])

    def norm_mod2(self, xtk, xap, htk, hap, tmp, wi, c, nt, first):
        sj = 24 * wi
        self.act(tmp[:, :, 0:nt], xap, AF.Square, r=[xtk], w=[tmp])
        p = self.pnext()
        for kc in range(8):
            self.mm(p[:, 0:nt], self.cst.ap[:, 1, :], tmp[:, kc, 0:nt], kc == 0, kc == 7, r=[self.cst, tmp], **wr(p, kc == 0))
        rs = self.rstd.get()
        self.act(rs[:, 0:nt], p[:, 0:nt], AF.Sqrt, r=[p, self.epsb], w=[rs], scale=1.0 / D, bias=self.epsb[:, 0:1])
        self.recip(rs[:, 0:nt], rs[:, 0:nt], r=[rs], w=[rs])
        for kc in range(8):
            self.tt('dve' if kc % 2 == 0 else 'pool', tmp[:, kc, 0:nt], xap[:, kc, :], rs[:, 0:nt], ALU.mult, r=[xtk, rs], **wr(tmp, kc == 0))
        for kc in range(8):
            self.act(hap[:, kc, :], tmp[:, kc, 0:nt], AF.Identity, r=[tmp, self.modA, self.mod], **wr(htk, first and kc == 0),
                     scale=self.modA[:, wi, kc, c:c + 1], bias=self.mod[:, sj + kc, c:c + 1])

    def stage_ffn16(self, i):
        self.areset()
        SB = 1024
        NH = SB // 512
        NSB = TT // SB
        xbs = self.take([128, 8, SB], 1)
        hbs = self.take([128, 8, SB], 2, BF16)
        xtmp = self.take([128, 8, 512])
        sq = self.take([128, 8, 512])
        self.rstd = self.take([128, 512], 2)
        acts = self.take([128, 22, SB], 1, BF16)
        wst = self.take([128, 8, 2, 128], 2)
        w16 = self.take([128, 8, 2, 128], 2, BF16)
        wost = self.take([128, 11, 128], 2)
        wo16 = self.take([128, 22, 128], 2, BF16)
        sg = self.take([128, 512], 2)
        wiv = self.ffn_w_in[i].rearrange("(kc p) n -> p kc n", p=128)
        wov = self.ffn_w_out[i].rearrange("(kc p) n -> p kc n", p=128)

        def cond(sb):
            return 0 if sb * SB < NP * TP else 1

        def norm_half(sb, hf, hb):
            t0 = sb * SB + hf * 512
            self.ld(xtmp[:], self.xT_v[:, :, t0:t0 + 512], r=self.xT_tk[t0 // 128:t0 // 128 + 4], w=[xtmp], eng='pool')
            self.norm_mod2(xtmp, xtmp[:, :, :], hb, hb[:, :, hf * 512:(hf + 1) * 512], sq, 1, cond(sb), 512, hf == 0)

        hb_cur = hbs.get()
        for hf in range(NH):
            norm_half(0, hf, hb_cur)
        for sb in range(NSB):
            c = cond(sb)
            tks = self.xT_tk[sb * 8:(sb + 1) * 8]
            xb = xbs.get()
            self.ld(xb[:], self.xT_v[:, :, sb * SB:(sb + 1) * SB], r=tks, w=[xb], eng='pool')
            hb = hb_cur
            hb_next = hbs.get() if sb + 1 < NSB else None
            at = acts.get()
            for j in range(22):
                ws = wst.get()
                self.ld(ws[:, :, 0, :], wiv[:, :, j * 128:(j + 1) * 128], r=[self.Win], w=[ws])
                self.ld(ws[:, :, 1, :], wiv[:, :, DFF + j * 128:DFF + (j + 1) * 128], r=[self.Win], pw=[ws], eng='pool')
                w = w16.get()
                self.cp('dve' if j % 2 else 'act', w[:], ws[:], r=[ws], w=[w])
                for hf in range(NH):
                    hs = slice(hf * 512, (hf + 1) * 512)
                    pg = self.pnext()
                    pu = self.pnext()
                    for kc in range(8):
                        self.mm(pg[:, :], w[:, kc, 0, :], hb[:, kc, hs], kc == 0, kc == 7, r=[w, hb], **wr(pg, kc == 0))
                    for kc in range(8):
                        self.mm(pu[:, :], w[:, kc, 1, :], hb[:, kc, hs], kc == 0, kc == 7, r=[w, hb], **wr(pu, kc == 0))
                    s_ = sg.get()
                    self.act(s_[:], pg[:, :], AF.Silu, r=[pg], w=[s_])
                    self.tt('dve', at[:, j, hs], s_[:], pu[:, :], ALU.mult, r=[s_, pu], **wr(at, j == 0 and hf == 0))
                if hb_next is not None and j in (7, 15):
                    norm_half(sb + 1, 0 if j == 7 else 1, hb_next)
            for oc in range(8):
                w = wo16.get()
                for kh in range(2):
                    ws = wost.get()
                    self.ld(ws[:], wov[:, kh * 11:(kh + 1) * 11, oc * 128:(oc + 1) * 128], r=[self.Win], w=[ws], eng='sp' if kh == 0 else 'pool')
                    self.cp('dve' if kh else 'act', w[:, kh * 11:(kh + 1) * 11, :], ws[:], r=[ws], **wr(w, kh == 0))
                for hf in range(NH):
                    hs = slice(hf * 512, (hf + 1) * 512)
                    p = self.pnext()
                    for k2 in range(22):
                        self.mm(p[:, :], w[:, k2, :], at[:, k2, hs], k2 == 0, k2 == 21, r=[w, at], **wr(p, k2 == 0))
                    self.stt('dve', xb[:, oc, hs], p[:, :], self.mod[:, 40 + oc, c:c + 1], xb[:, oc, hs], ALU.mult, ALU.add,
                             r=[p, self.mod, xb], pw=[xb])
            for q in range(SB // 128):
                self.ld(self.xT_v[:, :, sb * SB + q * 128: sb * SB + (q + 1) * 128], xb[:, :, q * 128:(q + 1) * 128],
                        r=[xb], w=[tks[q]], eng='pool')
            hb_cur = hb_next

    def stage_ffn(self, i):
        self.areset()
        NT = 512
        xbs = self.take([128, 8, NT], 2)
        hbs = self.take([128, 8, NT], 1)
        self.rstd = self.take([128, NT], 2)
        acts = self.take([128, 22, NT], 1)
        sg = self.take([128, NT], 2)
        wins = self.take([128, 8, 2, 128], 3)
        wouts = self.take([128, 22, 128], 2)
        wiv = self.ffn_w_in[i].rearrange("(kc p) n -> p kc n", p=128)
        wov = self.ffn_w_out[i].rearrange("(kc p) n -> p kc n", p=128)
        for blk in range(TT // NT):
            c = 0 if blk < (NP * TP) // NT else 1
            tks = self.xT_tk[blk * 4:(blk + 1) * 4]
            xb = xbs.get()
            self.ld(xb[:], self.xT_v[:, :, blk * NT:(blk + 1) * NT], r=tks, w=[xb], eng='pool')
            hb = hbs.get()
            self.norm_mod(xb, hb, 1, c, NT)
            at = acts.get()
            for j in range(22):
                w = wins.get()
                self.ld(w[:, :, 0, :], wiv[:, :, j * 128:(j + 1) * 128], r=[self.Win], w=[w])
                self.ld(w[:, :, 1, :], wiv[:, :, DFF + j * 128:DFF + (j + 1) * 128], r=[self.Win], pw=[w])
                pg = self.pnext()
                pu = self.pnext()
                for kc in range(8):
                    self.mm(pg[:, :], w[:, kc, 0, :], hb[:, kc, :], kc == 0, kc == 7, r=[w, hb], **wr(pg, kc == 0))
                for kc in range(8):
                    self.mm(pu[:, :], w[:, kc, 1, :], hb[:, kc, :], kc == 0, kc == 7, r=[w, hb], **wr(pu, kc == 0))
                s = sg.get()
                self.act(s[:], pg[:, :], AF.Silu, r=[pg], w=[s])
                self.tt('dve', at[:, j, :], s[:], pu[:, :], ALU.mult, r=[s, pu], w=[at] if j == 0 else (), pw=() if j == 0 else [at])
            for oc in range(8):
                w = wouts.get()
                self.ld(w[:], wov[:, :, oc * 128:(oc + 1) * 128], r=[self.Win], w=[w])
                p = self.pnext()
                for k2 in range(22):
                    self.mm(p[:, :], w[:, k2, :], at[:, k2, :], k2 == 0, k2 == 21, r=[w, at], **wr(p, k2 == 0))
                self.stt('dve', xb[:, oc, :], p[:, :], self.mod[:, 40 + oc, c:c + 1], xb[:, oc, :], ALU.mult, ALU.add,
                         r=[p, self.mod, xb], pw=[xb])
            for q in range(4):
                self.ld(self.xT_v[:, :, blk * NT + q * 128: blk * NT + (q + 1) * 128], xb[:, :, q * 128:(q + 1) * 128],
                        r=[xb], w=[tks[q]], eng='pool')


def host_consts():
    c = np.zeros((128, 8, 128), np.float32)
    c[:, 0, :] = np.eye(128)
    c[:, 1, :] = 1.0
    k = np.arange(128)[:, None]
    t = np.arange(128)[None, :]
    c[:, 2, :] = (k <= t)
    c[:, 3, :] = (k >= t)
    c[:, 4, :] = (k > t)
    c[:, 5, :] = (k < t)
    return c.reshape(128, 1024)


def rope_tables():
    rows = TS // 64
    row = np.broadcast_to(np.arange(rows)[:, None], (rows, 64)).reshape(-1)
    col = np.broadcast_to(np.arange(64)[None, :], (rows, 64)).reshape(-1)
    inv = (np.float32(10000.0) ** (-np.arange(16, dtype=np.float32) / np.float32(16))).astype(np.float32)
    ang = np.stack([row, col], axis=-1).astype(np.float32)[:, :, None] * inv
    return np.concatenate([np.cos(ang).reshape(TS, 32), np.sin(ang).reshape(TS, 32)], axis=1).astype(np.float32)


_CACHE = {}


def kernel(**inp):
    opts = inp.pop('_opts', {})
    key = repr(sorted(opts.items()))
    if key not in _CACHE:
        P = Prog(opts)
        P.build()
        _CACHE[key] = P
    P = _CACHE[key]
    f = lambda a: np.ascontiguousarray(np.asarray(a, dtype=np.float32))
    xp = f(inp['x_prompt'])
    xs = f(inp['x_sample'])
    c = f(inp['c'])
    c_ctx = f(inp['c_ctx'])
    ada_b = f(inp['ada_b'])
    norm_g = f(inp['norm_g'])
    shared = {
        'consts': host_consts(),
        'ada_w': f(inp['ada_w']),
        'ada_bT': f(ada_b.reshape(4, 48, 128).transpose(2, 0, 1)),
        'normgT': f(norm_g.reshape(4, 2, 8, 128).transpose(3, 0, 1, 2)),
        'ffn_w_in': f(inp['ffn_w_in']),
        'ffn_w_out': f(inp['ffn_w_out']),
        'mlstm_w_in': f(inp['mlstm_w_in'][0]),
        'mlstm_w_out': f(inp['mlstm_w_out'][0]),
        'mlstm_gate_b': f(inp['mlstm_gate_b'].reshape(1, 32)),
        'mlstm_norm_g': f(inp['mlstm_norm_g'].reshape(1, 128)),
    }
    shared.update({
        'diff_w_in': f(inp['diff_w_in'][0]),
        'diff_w_out': f(inp['diff_w_out'][0]),
        'diff_qkg': f(np.concatenate([inp['diff_q_norm_g'][0], inp['diff_k_norm_g'][0]]).reshape(1, 128)),
        'diff_lambda': f(inp['diff_lambda'][0].reshape(1, 256)),
        'diff_subln_g': f(inp['diff_subln_g'][0].reshape(1, 128)),
        'rope_cs': rope_tables(),
    })
    cw = f(inp['gdn_conv_w'])
    shared.update({
        'gdn_w_in': f(inp['gdn_w_in']),
        'gdn_w_out': f(inp['gdn_w_out']),
        'gdn_convT': f(cw.reshape(2, 5, 24, 128).transpose(3, 0, 2, 1)),
        'gdn_a_log': f(inp['gdn_a_log'].reshape(2, 1, 16)),
        'gdn_dt_bias': f(inp['gdn_dt_bias'].reshape(2, 1, 16)),
        'gdn_norm_g': f(inp['gdn_norm_g'].reshape(2, 1, 128)),
    })
    stS = f(inp['state_delta'])
    ck = f(inp['cache_diff_k'])
    cv = f(inp['cache_diff_v'])
    stC = f(inp['state_mlstm_C'])
    stn = f(inp['state_mlstm_n'])
    stm = f(inp['state_mlstm_m'])
    in_maps = []
    for k in range(NCORE):
        m = dict(shared)
        m['x_tok'] = f(np.concatenate([xp[NP * k:NP * (k + 1)].reshape(NP * TP, D), xs[k]], axis=0))
        cond = np.stack([c_ctx, c[k]], axis=-1)
        m['condT'] = f(cond.reshape(8, 128, 2).transpose(1, 0, 2))
        m['st_S'] = f(stS[k])
        m['ctx_k'] = f(ck[k, 0])
        m['ctx_v'] = f(cv[k, 0])
        m['st_C'] = f(stC[k, 0])
        m['st_n'] = f(stn[k, 0].transpose(0, 2, 1))
        m['st_m'] = f(stm[k, 0].reshape(1, 16))
        in_maps.append({n: m[n] for n in P.din})
    res = run_bass_kernel_spmd(P.nc, in_maps, core_ids=list(range(NCORE)))
    R = res.results
    y = np.stack([r['y_tok'] for r in R])
    y_prompt = y[:, :NP * TP].reshape(NCORE * NP, TP, D)
    y_sample = y[:, NP * TP:]
    outs = [y_prompt, y_sample]
    if 'newS' in P.dout:
        outs.append(np.stack([r['newS'] for r in R]).reshape(NCORE * NP, 2, 2, 8, 128, 128))
    if 'newC' in P.dout:
        outs.append(np.stack([r['newC'] for r in R]).reshape(NCORE * NP, 1, 2, 8, 64, 128))
        outs.append(np.ascontiguousarray(np.stack([r['newn'] for r in R]).reshape(NCORE * NP, 1, 2, 64, 8).transpose(0, 1, 2, 4, 3)))
        outs.append(np.stack([r['newm'] for r in R]).reshape(NCORE * NP, 1, 2, 8))
    if 'newk' in P.dout:
        outs.append(np.stack([r['newk'] for r in R]).reshape(NCORE * NP, 1, 8, 2, TP, 64))
        outs.append(np.stack([r['newv'] for r in R]).reshape(NCORE * NP, 1, 8, TP, 128))
    return tuple(outs)
```
